# Optimizing a Trainium2 kernel written in Bass

```python
import math
import jax, jax.numpy as jnp
from jax import lax
import numpy as np

D_MODEL = 1024
BATCH = 1
SEQ = 16384
DEPTH = 1

SSM_EXPAND = 2
SSM_D_INNER = SSM_EXPAND * D_MODEL
SSM_HEADDIM = 64
SSM_HEADS = SSM_D_INNER // SSM_HEADDIM
SSM_GROUPS = 4
SSM_HEADS_PER_GROUP = SSM_HEADS // SSM_GROUPS
SSM_STATE = 128
SSM_CONV = 4
SSM_CHUNK = 128
SSM_CONV_DIM = SSM_D_INNER + 2 * SSM_GROUPS * SSM_STATE
SSM_DT_MIN = 0.001
SSM_DT_MAX = 0.1

ATTN_HEADS = 16
ATTN_KV_HEADS = 2
ATTN_HEADDIM = 64
ATTN_GROUP = ATTN_HEADS // ATTN_KV_HEADS
WINDOW = 128
REL_BUCKETS = 32
REL_MAX_DIST = 128

D_FF = 2816
FFN_CONV = 3

DEEPNORM_ALPHA = (2.0 * DEPTH) ** 0.25
DEEPNORM_BETA = (8.0 * DEPTH) ** -0.25
LN_EPS = 1e-5
RMS_EPS = 1e-5

Z_COLS = SSM_D_INNER
XBC_COLS = SSM_CONV_DIM
DT_COLS = SSM_HEADS
Q_COLS = ATTN_HEADS * ATTN_HEADDIM
KV_COLS = ATTN_KV_HEADS * ATTN_HEADDIM
GATE_COLS = 2 * D_MODEL
IN_COLS = Z_COLS + XBC_COLS + DT_COLS + Q_COLS + 2 * KV_COLS + GATE_COLS
SPLIT_POINTS = [Z_COLS,
                Z_COLS + XBC_COLS,
                Z_COLS + XBC_COLS + DT_COLS,
                Z_COLS + XBC_COLS + DT_COLS + Q_COLS,
                Z_COLS + XBC_COLS + DT_COLS + Q_COLS + KV_COLS,
                Z_COLS + XBC_COLS + DT_COLS + Q_COLS + 2 * KV_COLS]

kernel_name = 'hybrid_ssd_swa_sink_convffn_deepnorm'


def layer_norm(x, g, b):
    xf = x.astype(jnp.float32)
    mu = jnp.mean(xf, axis=-1, keepdims=True)
    xc = xf - mu
    var = jnp.mean(xc * xc, axis=-1, keepdims=True)
    y = xc * lax.rsqrt(var + LN_EPS) * g.astype(jnp.float32) + b.astype(jnp.float32)
    return y.astype(x.dtype)


def causal_depthwise_conv(u, w, b):
    k, c = w.shape
    out = lax.conv_general_dilated(u, w[:, None, :].astype(u.dtype), window_strides=(1,),
                                   padding=[(k - 1, 0)],
                                   dimension_numbers=('NWC', 'WIO', 'NWC'),
                                   feature_group_count=c)
    return out + b.astype(u.dtype)


def ssd_chunked(xs, dt, a, bm, cm):
    b, s = xs.shape[:2]
    nc, lc = s // SSM_CHUNK, SSM_CHUNK
    G, E, P, N = SSM_GROUPS, SSM_HEADS_PER_GROUP, SSM_HEADDIM, SSM_STATE
    x = xs.reshape(b, nc, lc, G, E, P)
    dt_c = dt.reshape(b, nc, lc, G, E)
    bm = bm.reshape(b, nc, lc, G, N)
    cm = cm.reshape(b, nc, lc, G, N)
    a_dt = jnp.moveaxis(dt_c * a.reshape(G, E), 2, -1)
    a_cs = jnp.cumsum(a_dt, axis=-1)
    xdt = x * dt_c[..., None]
    seg = a_cs[..., :, None] - a_cs[..., None, :]
    causal = jnp.tril(jnp.ones((lc, lc), dtype=bool))
    decay = jnp.exp(jnp.where(causal, seg, -jnp.inf))
    cb = jnp.einsum('bclgn,bcsgn->bcgls', cm, bm)
    y_diag = jnp.einsum('bcgels,bcsgep->bclgep', cb[:, :, :, None] * decay, xdt)
    decay_states = jnp.moveaxis(jnp.exp(a_cs[..., -1:] - a_cs), -1, 2)
    states = jnp.einsum('bclgn,bclgep->bcgepn', bm, xdt * decay_states[..., None])
    chunk_decay = jnp.exp(a_cs[..., -1])

    def step(h, inp):
        st, dec = inp
        return h * dec[..., None, None] + st, h

    h0 = jnp.zeros((b, G, E, P, N), jnp.float32)
    _, prev = lax.scan(step, h0, (jnp.moveaxis(states, 1, 0), jnp.moveaxis(chunk_decay, 1, 0)))
    prev = jnp.moveaxis(prev, 0, 1)
    state_decay_out = jnp.moveaxis(jnp.exp(a_cs), -1, 2)
    y_off = jnp.einsum('bclgn,bcgepn->bclgep', cm, prev) * state_decay_out[..., None]
    return (y_diag + y_off).reshape(b, s, SSM_HEADS, P)


def mamba2_branch(z, xbc, dt_raw, conv_w, conv_b, dt_bias, a_log, d_skip, norm_w):
    b, s, _ = z.shape
    xbc = jax.nn.silu(causal_depthwise_conv(xbc, conv_w, conv_b))
    xs, bm, cm = jnp.split(xbc, [SSM_D_INNER, SSM_D_INNER + SSM_GROUPS * SSM_STATE], axis=-1)
    xs = xs.reshape(b, s, SSM_HEADS, SSM_HEADDIM).astype(jnp.float32)
    bm = bm.reshape(b, s, SSM_GROUPS, SSM_STATE).astype(jnp.float32)
    cm = cm.reshape(b, s, SSM_GROUPS, SSM_STATE).astype(jnp.float32)
    dt = jax.nn.softplus(dt_raw.astype(jnp.float32) + dt_bias.astype(jnp.float32))
    a = -jnp.exp(a_log.astype(jnp.float32))
    y = ssd_chunked(xs, dt, a, bm, cm) + xs * d_skip.astype(jnp.float32)[:, None]
    y = y.reshape(b, s, SSM_D_INNER) * jax.nn.silu(z.astype(jnp.float32))
    yg = y.reshape(b, s, SSM_GROUPS, SSM_D_INNER // SSM_GROUPS)
    yg = yg * lax.rsqrt(jnp.mean(yg * yg, axis=-1, keepdims=True) + RMS_EPS)
    return (yg.reshape(b, s, SSM_D_INNER) * norm_w.astype(jnp.float32)).astype(z.dtype)


def rel_bucket(rel):
    n = jnp.maximum(rel, 0)
    max_exact = REL_BUCKETS // 2
    nf = jnp.maximum(n, 1).astype(jnp.float32)
    large = max_exact + (jnp.log(nf / max_exact) / math.log(REL_MAX_DIST / max_exact)
                         * (REL_BUCKETS - max_exact)).astype(jnp.int32)
    large = jnp.minimum(large, REL_BUCKETS - 1)
    return jnp.where(n < max_exact, n, large)


def swa_sink_attention(q, k, v, sinks, rel_bias):
    b, s, _ = q.shape
    W = WINDOW
    nb = s // W
    KV, G, Dh = ATTN_KV_HEADS, ATTN_GROUP, ATTN_HEADDIM
    qb = q.reshape(b, nb, W, KV, G, Dh).astype(jnp.float32)
    kb = k.reshape(b, nb, W, KV, Dh).astype(jnp.float32)
    vb = v.reshape(b, nb, W, KV, Dh).astype(jnp.float32)
    pad = jnp.zeros_like(kb[:, :1])
    k_band = jnp.concatenate([jnp.concatenate([pad, kb[:, :-1]], axis=1), kb], axis=2)
    v_band = jnp.concatenate([jnp.concatenate([pad, vb[:, :-1]], axis=1), vb], axis=2)
    logits = jnp.einsum('bnqkgd,bnskd->bnkgqs', qb, k_band) * (Dh ** -0.5)
    qi = jnp.arange(W)[:, None] + W
    kj = jnp.arange(2 * W)[None, :]
    rel = qi - kj
    bias = rel_bias.astype(jnp.float32)[rel_bucket(rel)]
    bias = jnp.transpose(bias, (2, 0, 1)).reshape(KV, G, W, 2 * W)
    in_window = (rel >= 0) & (rel < W)
    block_idx = jnp.arange(nb)[:, None, None]
    valid = in_window[None] & ((block_idx > 0) | (kj >= W)[None])
    logits = jnp.where(valid[None, :, None, None], logits + bias, -jnp.inf)
    sink = sinks.astype(jnp.float32).reshape(1, 1, KV, G, 1, 1)
    m = jnp.maximum(jnp.max(logits, axis=-1, keepdims=True), sink)
    p = jnp.exp(logits - m)
    probs = p / (jnp.sum(p, axis=-1, keepdims=True) + jnp.exp(sink - m))
    out = jnp.einsum('bnkgqs,bnskd->bnqkgd', probs, v_band)
    return out.reshape(b, s, ATTN_HEADS * Dh).astype(q.dtype)


def conv_ffn(h, w_up, conv_w, conv_b, w_down):
    u = causal_depthwise_conv(jnp.einsum('bsd,df->bsf', h, w_up), conv_w, conv_b)
    gate, val = jnp.split(u, 2, axis=-1)
    return jnp.einsum('bsf,fd->bsd', jax.nn.silu(gate) * val, w_down)


def setup_inputs(seed: int = 0) -> dict:
    key = jax.random.key(seed)
    ks = jax.random.split(key, 24)
    L = DEPTH
    f32 = jnp.float32
    nrm = lambda k, shape: jax.random.normal(k, shape, f32)
    x = nrm(ks[0], (BATCH, SEQ, D_MODEL))
    rel_bias = 0.1 * nrm(ks[1], (REL_BUCKETS, ATTN_HEADS))
    w_in = nrm(ks[2], (L, D_MODEL, IN_COLS)) * D_MODEL ** -0.5
    b_gate = 0.01 * nrm(ks[3], (L, GATE_COLS))
    ssm_conv_w = 0.5 * nrm(ks[4], (L, SSM_CONV, SSM_CONV_DIM))
    ssm_conv_b = 0.01 * nrm(ks[5], (L, SSM_CONV_DIM))
    u = jax.random.uniform(ks[6], (L, SSM_HEADS), f32)
    dt0 = jnp.exp(u * (math.log(SSM_DT_MAX) - math.log(SSM_DT_MIN)) + math.log(SSM_DT_MIN))
    ssm_dt_bias = dt0 + jnp.log(-jnp.expm1(-dt0))
    ssm_a_log = jnp.log(jax.random.uniform(ks[7], (L, SSM_HEADS), f32, 1.0, 16.0))
    ssm_d = 1.0 + 0.01 * nrm(ks[8], (L, SSM_HEADS))
    ssm_norm_w = 1.0 + 0.01 * nrm(ks[9], (L, SSM_D_INNER))
    attn_sinks = 0.1 * nrm(ks[10], (L, ATTN_HEADS))
    w_branch_ssm = nrm(ks[11], (L, SSM_D_INNER, D_MODEL)) * SSM_D_INNER ** -0.5 * DEEPNORM_BETA
    w_branch_attn = nrm(ks[12], (L, Q_COLS, D_MODEL)) * Q_COLS ** -0.5 * DEEPNORM_BETA
    w_mix_out = nrm(ks[13], (L, D_MODEL, D_MODEL)) * D_MODEL ** -0.5 * DEEPNORM_BETA
    ln1_g = 1.0 + 0.01 * nrm(ks[14], (L, D_MODEL))
    ln1_b = 0.01 * nrm(ks[15], (L, D_MODEL))
    w_up = nrm(ks[16], (L, D_MODEL, 2 * D_FF)) * D_MODEL ** -0.5 * DEEPNORM_BETA
    ffn_conv_w = nrm(ks[17], (L, FFN_CONV, 2 * D_FF)) * FFN_CONV ** -0.5
    ffn_conv_b = 0.01 * nrm(ks[18], (L, 2 * D_FF))
    w_down = nrm(ks[19], (L, D_FF, D_MODEL)) * D_FF ** -0.5 * DEEPNORM_BETA
    ln2_g = 1.0 + 0.01 * nrm(ks[20], (L, D_MODEL))
    ln2_b = 0.01 * nrm(ks[21], (L, D_MODEL))
    return {'x': x, 'rel_bias': rel_bias, 'w_in': w_in, 'b_gate': b_gate,
            'ssm_conv_w': ssm_conv_w, 'ssm_conv_b': ssm_conv_b, 'ssm_dt_bias': ssm_dt_bias,
            'ssm_a_log': ssm_a_log, 'ssm_d': ssm_d, 'ssm_norm_w': ssm_norm_w,
            'attn_sinks': attn_sinks, 'w_branch_ssm': w_branch_ssm, 'w_branch_attn': w_branch_attn,
            'w_mix_out': w_mix_out, 'ln1_g': ln1_g, 'ln1_b': ln1_b, 'w_up': w_up,
            'ffn_conv_w': ffn_conv_w, 'ffn_conv_b': ffn_conv_b, 'w_down': w_down,
            'ln2_g': ln2_g, 'ln2_b': ln2_b}


def reference(x, rel_bias, w_in, b_gate, ssm_conv_w, ssm_conv_b, ssm_dt_bias, ssm_a_log, ssm_d,
              ssm_norm_w, attn_sinks, w_branch_ssm, w_branch_attn, w_mix_out, ln1_g, ln1_b,
              w_up, ffn_conv_w, ffn_conv_b, w_down, ln2_g, ln2_b):
    h = x
    for l in range(DEPTH):
        proj = jnp.einsum('bsd,dc->bsc', h, w_in[l])
        z, xbc, dt_raw, q, k, v, gates = jnp.split(proj, SPLIT_POINTS, axis=-1)
        y_ssm = mamba2_branch(z, xbc, dt_raw, ssm_conv_w[l], ssm_conv_b[l], ssm_dt_bias[l],
                              ssm_a_log[l], ssm_d[l], ssm_norm_w[l])
        y_attn = swa_sink_attention(q, k, v, attn_sinks[l], rel_bias)
        g_ssm, g_attn = jnp.split(jax.nn.sigmoid(gates + b_gate[l]), 2, axis=-1)
        merged = (g_ssm * jnp.einsum('bsi,id->bsd', y_ssm, w_branch_ssm[l])
                  + g_attn * jnp.einsum('bsi,id->bsd', y_attn, w_branch_attn[l]))
        mix_out = jnp.einsum('bsd,de->bse', merged, w_mix_out[l])
        h = layer_norm(DEEPNORM_ALPHA * h + mix_out, ln1_g[l], ln1_b[l])
        ffn_out = conv_ffn(h, w_up[l], ffn_conv_w[l], ffn_conv_b[l], w_down[l])
        h = layer_norm(DEEPNORM_ALPHA * h + ffn_out, ln2_g[l], ln2_b[l])
    return h
```

```python
import numpy as np
import concourse.bass as bass
import concourse.mybir as mybir

ENGS = ["pe", "act", "dve", "pool", "sp"]
N_DMA_SEM = 16
SAME_ENGINE_SYNC = True


class Buf:
    __slots__ = ("name", "writer", "readers", "dma_readers", "excl")

    def __init__(self, name, excl=False):
        self.name = name
        self.excl = excl
        self.writer = None
        self.readers = {}
        self.dma_readers = []


class Op:
    __slots__ = ("eng", "fn", "idx", "waits", "signal", "is_dma", "dsem", "dtarget", "clock", "sigcount", "uid")


class Prog:
    def __init__(self):
        self.ops = {e: [] for e in ENGS}
        self.clock = {e: {f: 0 for f in ENGS} for e in ENGS}
        self.known_dma = {e: set() for e in ENGS}
        self.dma_sem_count = [0] * N_DMA_SEM
        self.dma_sem_last = [None] * N_DMA_SEM
        self.dma_rr = 0
        self.n_dma = 0
        self.uid = 0
        self.bar = {e: [] for e in ENGS}

    def barrier(self):
        lasts = [self.ops[e][-1] for e in ENGS if self.ops[e] and not self.ops[e][-1].is_dma]
        for e in ENGS:
            pass
        lasts = []
        for e in ENGS:
            for op in reversed(self.ops[e]):
                if not op.is_dma:
                    lasts.append(op)
                    break
        dmas = [op for op in self.dma_sem_last if op is not None]
        for e in ENGS:
            self.bar[e] = lasts + dmas

    def add(self, eng, fn, reads=(), writes=(), dma=False):
        op = Op()
        op.eng = eng
        op.fn = fn
        op.idx = len(self.ops[eng])
        op.waits = []
        op.signal = False
        op.is_dma = dma
        op.uid = self.uid
        self.uid += 1
        deps = []
        for b in reads:
            if b.writer is not None:
                deps.append(b.writer)
            if b.excl:
                for e2, r in b.readers.items():
                    if e2 != eng:
                        deps.append(r)
        for b in writes:
            if b.writer is not None:
                deps.append(b.writer)
            deps.extend(b.readers.values())
            deps.extend(b.dma_readers)
        if self.bar[eng]:
            deps.extend(self.bar[eng])
            self.bar[eng] = []
        clk = self.clock[eng]
        seen = set()
        for d in deps:
            if d.uid in seen:
                continue
            seen.add(d.uid)
            if d.is_dma:
                if d.uid not in self.known_dma[eng]:
                    op.waits.append(("dma", d.dsem, d.dtarget))
                    self.known_dma[eng].add(d.uid)
            else:
                if d.eng == eng and (eng == "pe" or not SAME_ENGINE_SYNC):
                    continue
                if clk[d.eng] < d.idx + 1:
                    op.waits.append(("eng", d))
                    d.signal = True
                    for f in ENGS:
                        if d.clock[f] > clk[f]:
                            clk[f] = d.clock[f]
                    if clk[d.eng] < d.idx + 1:
                        clk[d.eng] = d.idx + 1
        if dma and eng == "pool":
            pd = self.__dict__.setdefault("pool_dmas", [])
            if len(pd) >= 4:
                d = pd[-4]
                if d.uid not in self.known_dma[eng]:
                    op.waits.append(("dma", d.dsem, d.dtarget))
                    self.known_dma[eng].add(d.uid)
            pd.append(op)
        if dma:
            k = self.dma_rr
            self.dma_rr = (self.dma_rr + 1) % N_DMA_SEM
            prev = self.dma_sem_last[k]
            if prev is not None and prev.uid not in self.known_dma[eng]:
                op.waits.append(("dma", k, prev.dtarget))
                self.known_dma[eng].add(prev.uid)
            self.dma_sem_count[k] += 16
            op.dsem = k
            op.dtarget = self.dma_sem_count[k]
            self.dma_sem_last[k] = op
            self.n_dma += 1
        op.clock = dict(clk)
        if not SAME_ENGINE_SYNC or eng == "pe":
            pass
        self.ops[eng].append(op)
        for b in writes:
            b.writer = op
            b.readers = {}
            b.dma_readers = []
        for b in reads:
            if dma:
                b.dma_readers.append(op)
            else:
                b.readers[eng] = op
        return op

    def emit(self, nc, final_waits=True):
        for e in ENGS:
            c = 0
            for op in self.ops[e]:
                if op.signal:
                    c += 1
                op.sigcount = c
        from contextlib import ExitStack
        with ExitStack() as es:
            esem = {e: es.enter_context(nc.semaphore("s_" + e)) for e in ENGS}
            dsem = [es.enter_context(nc.semaphore("d%d" % i)) for i in range(N_DMA_SEM)]
            block = es.enter_context(nc.Block())
            last_dma = [op for op in self.dma_sem_last if op is not None]

            def run(e, h):
                for op in self.ops[e]:
                    for w in op.waits:
                        if w[0] == "dma":
                            h.wait_ge(dsem[w[1]], w[2])
                        else:
                            h.wait_ge(esem[w[1].eng], w[1].sigcount)
                    ins = op.fn(h)
                    if op.is_dma:
                        ins.then_inc(dsem[op.dsem], 16)
                    elif op.signal:
                        ins.then_inc(esem[e], 1)
                if e == "sp" and final_waits:
                    for k in range(N_DMA_SEM):
                        if self.dma_sem_count[k] > 0:
                            h.wait_ge(dsem[k], self.dma_sem_count[k])

            @block.tensor
            def _(h):
                run("pe", h)

            @block.scalar
            def _(h):
                run("act", h)

            @block.vector
            def _(h):
                run("dve", h)

            @block.gpsimd
            def _(h):
                run("pool", h)

            @block.sync
            def _(h):
                run("sp", h)

from contextlib import ExitStack
import ml_dtypes
from concourse.bass_utils import run_bass_kernel_spmd

F32 = mybir.dt.float32
BF16 = mybir.dt.bfloat16
AF = mybir.ActivationFunctionType
ALU = mybir.AluOpType

NCORE = 8
TOK = 2048
NPRE = 111
NM1 = 17
NM2 = 18
ALPHA = 2.0 ** 0.25
COL_Z, COL_X, COL_B, COL_C, COL_DT, COL_Q, COL_K, COL_V, COL_G = 0, 2048, 4096, 4608, 5120, 5152, 6176, 6304, 6432


class TT:
    def __init__(self, ap, name, excl=False):
        self.ap = ap
        self.b = Buf(name, excl)


class Arena:
    def __init__(self, t, n):
        self.t, self.n, self.off = t, n, 0

    def get(self, name, *fs):
        n = int(np.prod(fs))
        ap = self.t[:, self.off:self.off + n]
        self.off += n
        assert self.off <= self.n, (name, self.off, self.n)
        if len(fs) == 2:
            ap = ap.rearrange("p (a b) -> p a b", a=fs[0])
        elif len(fs) == 3:
            ap = ap.rearrange("p (a b c) -> p a b c", a=fs[0], b=fs[1])
        return TT(ap, name)


def bc(ap2, n):
    return ap2.unsqueeze(2).to_broadcast([ap2.shape[0], ap2.shape[1], n])


def build(stop=4, npre=NPRE, dbg=False):
    nc = bass.Bass("TRN2", target_bir_lowering=False)
    dt_in = lambda n, s: nc.dram_tensor(n, s, F32, kind="ExternalInput").ap()
    xpre = dt_in("xpre", [NPRE * 128, 1024])
    xmain = dt_in("xmain", [NM2 * 128, 1024])
    pflag = dt_in("pflag", [128, NPRE + NM1])
    w_in = dt_in("w_in", [1024, 8480])
    b_gate = dt_in("b_gate", [1, 2048])
    scw = dt_in("scw", [96, 128])
    scb = dt_in("scb", [24, 128])
    dtb = dt_in("dtb", [1, 32])
    alog = dt_in("alog", [1, 32])
    dsk = dt_in("dsk", [1, 32])
    normw = dt_in("normw", [1, 2048])
    sinks = dt_in("sinks", [1, 16])
    w_bs = dt_in("w_bs", [2048, 1024])
    w_ba = dt_in("w_ba", [1024, 1024])
    w_mix = dt_in("w_mix", [1024, 1024])
    ln1g = dt_in("ln1g", [1, 1024]); ln1b = dt_in("ln1b", [1, 1024])
    ln2g = dt_in("ln2g", [1, 1024]); ln2b = dt_in("ln2b", [1, 1024])
    w_up = dt_in("w_up", [1024, 5632])
    fcw = dt_in("fcw", [132, 128])
    fcb = dt_in("fcb", [44, 128])
    w_dn = dt_in("w_dn", [2816, 1024])
    cst = dt_in("cst", [128, 4 * 128])
    biasg = dt_in("biasg", [128, 2 * 16 * 128])
    maskg = dt_in("maskg", [128, 2 * 16 * 128])
    out = nc.dram_tensor("out", [TOK, 1024], F32, kind="ExternalOutput").ap()
    skind = "ExternalOutput" if dbg else "Internal"
    ynd = nc.dram_tensor("ynd", [NM1 * 128, 2048], BF16, kind=skind).ap()
    yad = nc.dram_tensor("yad", [NM1 * 128, 1024], BF16, kind=skind).ap()
    h1d = nc.dram_tensor("h1d", [NM1 * 128, 1024], F32, kind=skind).ap()

    P = Prog()
    es = ExitStack()
    NB, NF = 148 * 512, 48 * 256
    ABt = es.enter_context(nc.sbuf_tensor("AB", [128, NB], BF16))
    AFt = es.enter_context(nc.sbuf_tensor("AF", [128, NF], F32))
    pf = [TT(es.enter_context(nc.psum_tensor("pf%d" % i, [128, 512], F32))[:], "pf%d" % i, True) for i in range(6)]
    pb = [TT(es.enter_context(nc.psum_tensor("pb%d" % i, [128, 1024], BF16))[:], "pb%d" % i, True) for i in range(2)]
    AB = Arena(ABt, NB)
    AFa = Arena(AFt, NF)

    def bcast_row(src, n):
        return bass.AP(src.tensor, 0, [[0, 128], [1, n]])

    def dma(eng, o, i, reads, writes):
        P.add(eng, lambda e, o=o, i=i: e.dma_start(out=o, in_=i), reads=reads, writes=writes, dma=True)

    cf = AFa.get("cf", 4, 128)
    dma("sp", cf.ap, cst.rearrange("p (a b) -> p a b", a=4), [], [cf.b])
    identf, triU, Ustr, onesf = cf.ap[:, 0, :], cf.ap[:, 1, :], cf.ap[:, 2, :], cf.ap[:, 3, :]
    cb16 = AB.get("cb16", 4, 128)
    dma("pool", cb16.ap, cst.rearrange("p (a b) -> p a b", a=4), [], [cb16.b])
    identb, maskb = cb16.ap[:, 0, :], cb16.ap[:, 1, :]
    flg = AFa.get("flg", NPRE + NM1)
    dma("sp", flg.ap, pflag, [], [flg.b])
    smallp = AFa.get("smallp", 6, 32)
    dma("sp", smallp.ap[:, 0, :], bcast_row(dtb, 32), [], [smallp.b])
    dma("sp", smallp.ap[:, 1, :], bcast_row(alog, 32), [], [smallp.b])
    dma("sp", smallp.ap[:, 2, :], bcast_row(dsk, 32), [], [smallp.b])
    dma("sp", smallp.ap[:, 3, 0:16], bcast_row(sinks, 16), [], [smallp.b])
    P.add("act", lambda e: e.activation(out=smallp.ap[:, 1, :], in_=smallp.ap[:, 1, :], func=AF.Exp), reads=[smallp.b], writes=[smallp.b])
    P.add("dve", lambda e: e.tensor_scalar(out=smallp.ap[:, 1, :], in0=smallp.ap[:, 1, :], scalar1=-1.0, scalar2=None, op0=ALU.mult), reads=[smallp.b], writes=[smallp.b])
    P.add("act", lambda e: e.activation(out=smallp.ap[:, 3, 0:16], in_=smallp.ap[:, 3, 0:16], func=AF.Exp), reads=[smallp.b], writes=[smallp.b])
    dtb_bc, a_bc, D_bc, esink = smallp.ap[:, 0, :], smallp.ap[:, 1, :], smallp.ap[:, 2, :], smallp.ap[:, 3, 0:16]
    onecol = onesf[:, 0:1]
    markB, markF = AB.off, AFa.off

    def load_w(dst, src, ncols, nk=8):
        nk = src.shape[0] // 128
        for c0 in range(0, ncols, 512):
            c1 = min(c0 + 512, ncols)
            for k in range(nk):
                dma("pool", dst.ap[:, k, c0:c1], src[k * 128:(k + 1) * 128, c0:c1], [], [dst.b])

    def load_xT(xrow_ap, xb, xT):
        dma("pool", xb.ap, xrow_ap, [], [xb.b])
        for k in range(8):
            P.add("pe", lambda e, k=k: e.transpose(pb[0].ap[:, k * 128:(k + 1) * 128], xb.ap[:, k * 128:(k + 1) * 128], identb), reads=[xb.b, cb16.b], writes=[pb[0].b])
        P.add("act", lambda e: e.copy(out=xT.ap.rearrange("p a b -> p (a b)"), in_=pb[0].ap), reads=[pb[0].b], writes=[xT.b])

    def chain(bank, out_ap, pairs, reads):
        n = len(pairs)
        for i, (l, r) in enumerate(pairs):
            P.add("pe", lambda e, l=l, r=r, i=i: e.matmul(out_ap, l, r, start=(i == 0), stop=(i == n - 1)), reads=reads, writes=[bank.b])

    def per_channel(dst, src_dram, rows):
        tmp = AFa.get("pc_tmp", 128)
        dma("sp", tmp.ap[0:rows, :], src_dram, [], [tmp.b])
        P.add("pe", lambda e: e.transpose(pf[5].ap[:, 0:rows], tmp.ap[0:rows, :], identf[0:rows, 0:rows]), reads=[tmp.b, cf.b], writes=[pf[5].b])
        P.add("dve", lambda e: e.tensor_copy(out=dst, in_=pf[5].ap[:, 0:rows]), reads=[pf[5].b], writes=[])

    Wx = AB.get("Wx", 8, 3072); load_w(Wx, w_in[:, COL_X:COL_X + 3072], 3072)
    Wdt = AB.get("Wdt", 8, 32); load_w(Wdt, w_in[:, COL_DT:COL_DT + 32], 32)
    Wz = AB.get("Wz", 8, 2048); load_w(Wz, w_in[:, COL_Z:COL_Z + 2048], 2048)
    cwt = AFa.get("cwt", 96); cbt = AFa.get("cbt", 24)
    per_channel(cwt.ap, scw, 96)
    per_channel(cbt.ap, scb, 24)
    cwt.b.writer = P.ops["dve"][-2]; cbt.b.writer = P.ops["dve"][-1]
    diag = AB.get("diag", 24, 4, 128)
    for j in range(24):
        for tp in range(4):
            P.add("dve", lambda e, j=j, tp=tp: e.tensor_scalar(out=diag.ap[:, j, tp, :], in0=identf, scalar1=cwt.ap[:, tp * 24 + j:tp * 24 + j + 1], scalar2=None, op0=ALU.mult), reads=[cf.b, cwt.b], writes=[diag.b])
    nwb = AFa.get("nwb", 2048)
    dma("sp", nwb.ap, bcast_row(normw, 2048), [], [nwb.b])
    xb = AB.get("xb", 1024); xT = AB.get("xT", 8, 128)
    raw = AB.get("raw", 24, 132); xc = AB.get("xc", 24, 128)
    xtok = AB.get("xtok", 2560)
    xdt = AB.get("xdt", 512); xdts = AB.get("xdts", 512)
    Hb = AB.get("Hb", 2048); cbm = AB.get("cbm", 4, 128)
    LT = AB.get("LT", 4, 128); MT = AB.get("MT", 8, 128); yn = AB.get("yn", 2048)
    H = AFa.get("H", 2048)
    adtU = [AFa.get("adtU%d" % i, 128) for i in range(2)]
    sm = AFa.get("sm", 12, 32)
    yacc = AFa.get("yacc", 512); ytmp = AFa.get("ytmp", 512); sz = AFa.get("sz", 512)
    ssq = AFa.get("ssq", 4)
    P.add("dve", lambda e: e.memset(H.ap, 0.0), writes=[H.b])
    P.add("pool", lambda e: e.memset(raw.ap, 0.0), writes=[raw.b])

    import os
    CUT = int(os.environ.get('KCUT', '99')); NCH1 = int(os.environ.get('KNCH', str(NM1)))
    def ssd_chunk(xrow, fcol, main, ci):
        ntile = 24 if (main or ci == NPRE - 1) else 20
        load_xT(xrow, xb, xT)
        if CUT <= 1: return
        for g in range(ntile // 4):
            bank = pf[g % 2]
            for jj in range(4):
                j = 4 * g + jj
                chain(bank, bank.ap[:, jj * 128:(jj + 1) * 128], [(Wx.ap[:, k, j * 128:(j + 1) * 128], xT.ap[:, k, :]) for k in range(8)], [Wx.b, xT.b])
            P.add("act", lambda e, g=g, bank=bank: e.copy(out=raw.ap[:, 4 * g:4 * g + 4, 3:131], in_=bank.ap.rearrange("p (a b) -> p a b", a=4)), reads=[bank.b], writes=[raw.b])
        if CUT <= 2: return
        for g in range(ntile // 4):
            bank = pf[2 + g % 2]
            for jj in range(4):
                j = 4 * g + jj
                chain(bank, bank.ap[:, jj * 128:(jj + 1) * 128], [(diag.ap[:, j, tp, :], raw.ap[:, j, tp:tp + 128]) for tp in range(4)], [diag.b, raw.b])
                P.add("act", lambda e, j=j, jj=jj, bank=bank: e.activation(out=xc.ap[:, j, :], in_=bank.ap[:, jj * 128:(jj + 1) * 128], func=AF.Silu, bias=cbt.ap[:, j:j + 1], scale=1.0), reads=[bank.b, cbt.b], writes=[xc.b])
        if CUT <= 3: return
        P.add("pool", lambda e: e.tensor_copy(out=raw.ap[:, :, 0:3], in_=raw.ap[:, :, 128:131]), reads=[raw.b], writes=[raw.b])
        for bt in range(3):
            nt = 8 if bt < 2 else 4
            for i in range(nt):
                j = bt * 8 + i
                P.add("pe", lambda e, i=i, j=j: e.transpose(pb[1].ap[:, i * 128:(i + 1) * 128], xc.ap[:, j, :], identb), reads=[xc.b, cb16.b], writes=[pb[1].b])
            P.add("dve", lambda e, bt=bt, nt=nt: e.tensor_copy(out=xtok.ap[:, bt * 1024:bt * 1024 + nt * 128], in_=pb[1].ap[:, 0:nt * 128]), reads=[pb[1].b], writes=[xtok.b])
        if CUT <= 4: return
        chain(pf[4], pf[4].ap[:, 0:32], [(xT.ap[:, k, :], Wdt.ap[:, k, :]) for k in range(8)], [xT.b, Wdt.b])
        S = lambda i: sm.ap[:, i, :]
        P.add("dve", lambda e: e.tensor_tensor(out=S(0), in0=pf[4].ap[:, 0:32], in1=dtb_bc, op=ALU.add), reads=[pf[4].b, smallp.b], writes=[sm.b])
        P.add("act", lambda e: e.activation(out=S(0), in_=S(0), func=AF.Exp), reads=[sm.b], writes=[sm.b])
        P.add("act", lambda e: e.activation(out=S(1), in_=S(0), func=AF.Ln, bias=onecol, scale=1.0), reads=[sm.b, cf.b], writes=[sm.b])
        P.add("dve", lambda e: e.tensor_scalar(out=S(1), in0=S(1), scalar1=fcol, scalar2=None, op0=ALU.mult), reads=[sm.b, flg.b], writes=[sm.b])
        P.add("dve", lambda e: e.tensor_tensor(out=S(2), in0=S(1), in1=a_bc, op=ALU.mult), reads=[sm.b, smallp.b], writes=[sm.b])
        if CUT <= 5: return
        P.add("pe", lambda e: e.matmul(pf[4].ap[:, 32:64], triU, S(2), start=True, stop=True), reads=[cf.b, sm.b], writes=[pf[4].b])
        P.add("pe", lambda e: e.matmul(pf[4].ap[:, 64:96], onesf, S(2), start=True, stop=True), reads=[cf.b, sm.b], writes=[pf[4].b])
        P.add("dve", lambda e: e.tensor_copy(out=sm.ap[:, 3:5, :], in_=pf[4].ap[:, 32:96].rearrange("p (a b) -> p a b", a=2)), reads=[pf[4].b], writes=[sm.b])
        if CUT <= 6: return
        if main and CUT > 7:
            for g in range(4):
                P.add("pe", lambda e, g=g: e.matmul(pf[5].ap[:, g * 128:(g + 1) * 128], xc.ap[:, 16 + g, :], xc.ap[:, 20 + g, :], start=True, stop=True), reads=[xc.b], writes=[pf[5].b])
            P.add("dve", lambda e: e.tensor_tensor(out=cbm.ap, in0=pf[5].ap.rearrange("p (a b) -> p a b", a=4), in1=maskb.unsqueeze(1).to_broadcast([128, 4, 128]), op=ALU.mult), reads=[pf[5].b, cb16.b], writes=[cbm.b])
            P.add("pool", lambda e: e.tensor_copy(out=Hb.ap, in_=H.ap), reads=[H.b], writes=[Hb.b])
            P.add("act", lambda e: e.activation(out=S(5), in_=S(3), func=AF.Exp), reads=[sm.b], writes=[sm.b])
            for g in range(4):
                xg = xtok.ap[:, g * 512:(g + 1) * 512].rearrange("p (a b) -> p a b", a=8)
                P.add("dve", lambda e, g=g, xg=xg: e.tensor_tensor(out=xdt.ap.rearrange("p (a b) -> p a b", a=8), in0=xg, in1=bc(sm.ap[:, 1, 8 * g:8 * g + 8], 64), op=ALU.mult), reads=[xtok.b, sm.b], writes=[xdt.b])
                for hh in range(2):
                    bank = pf[hh]
                    for h4 in range(4):
                        h = 8 * g + 4 * hh + h4
                        au = adtU[h4 % 2]
                        P.add("dve", lambda e, h=h, au=au: e.tensor_scalar(out=au.ap, in0=Ustr, scalar1=sm.ap[:, 2, h:h + 1], scalar2=None, op0=ALU.mult), reads=[cf.b, sm.b], writes=[au.b])
                        P.add("pe", lambda e, h4=h4, au=au, bank=bank: e.matmul(bank.ap[:, h4 * 128:(h4 + 1) * 128], au.ap, triU, start=True, stop=True), reads=[au.b, cf.b], writes=[bank.b])
                    P.add("act", lambda e, bank=bank: e.activation(out=LT.ap.rearrange("p a b -> p (a b)"), in_=bank.ap, func=AF.Exp), reads=[bank.b], writes=[LT.b])
                    P.add("dve", lambda e, g=g, hh=hh: e.tensor_tensor(out=MT.ap[:, 4 * hh:4 * hh + 4, :], in0=LT.ap, in1=cbm.ap[:, g, :].unsqueeze(1).to_broadcast([128, 4, 128]), op=ALU.mult), reads=[LT.b, cbm.b], writes=[MT.b])
                for h8 in range(8):
                    P.add("pe", lambda e, h8=h8: e.matmul(pf[2].ap[:, h8 * 64:(h8 + 1) * 64], MT.ap[:, h8, :], xdt.ap[:, h8 * 64:(h8 + 1) * 64], start=True, stop=True), reads=[MT.b, xdt.b], writes=[pf[2].b])
                P.add("pe", lambda e, g=g: e.matmul(pf[3].ap, xc.ap[:, 20 + g, :], Hb.ap[:, g * 512:(g + 1) * 512], start=True, stop=True), reads=[xc.b, Hb.b], writes=[pf[3].b])
                v3 = lambda ap: ap.rearrange("p (a b) -> p a b", a=8)
                P.add("dve", lambda e, g=g: e.tensor_tensor(out=v3(yacc.ap), in0=v3(pf[3].ap), in1=bc(sm.ap[:, 5, 8 * g:8 * g + 8], 64), op=ALU.mult), reads=[pf[3].b, sm.b], writes=[yacc.b])
                P.add("dve", lambda e: e.tensor_tensor(out=yacc.ap, in0=yacc.ap, in1=pf[2].ap, op=ALU.add), reads=[pf[2].b, yacc.b], writes=[yacc.b])
                P.add("dve", lambda e, g=g, xg=xg: e.tensor_tensor(out=v3(ytmp.ap), in0=xg, in1=bc(D_bc[:, 8 * g:8 * g + 8], 64), op=ALU.mult), reads=[xtok.b, smallp.b], writes=[ytmp.b])
                P.add("dve", lambda e: e.tensor_tensor(out=yacc.ap, in0=yacc.ap, in1=ytmp.ap, op=ALU.add), reads=[ytmp.b, yacc.b], writes=[yacc.b])
                chain(pf[5], pf[5].ap, [(xT.ap[:, k, :], Wz.ap[:, k, g * 512:(g + 1) * 512]) for k in range(8)], [xT.b, Wz.b])
                P.add("act", lambda e: e.activation(out=sz.ap, in_=pf[5].ap, func=AF.Silu), reads=[pf[5].b], writes=[sz.b])
                P.add("dve", lambda e: e.tensor_tensor(out=yacc.ap, in0=yacc.ap, in1=sz.ap, op=ALU.mult), reads=[sz.b, yacc.b], writes=[yacc.b])
                P.add("act", lambda e, g=g: e.activation(out=ytmp.ap, in_=yacc.ap, func=AF.Square, accum_out=ssq.ap[:, g:g + 1]), reads=[yacc.b], writes=[ytmp.b, ssq.b])
                P.add("dve", lambda e, g=g: e.tensor_scalar(out=ssq.ap[:, g:g + 1], in0=ssq.ap[:, g:g + 1], scalar1=1.0 / 512, scalar2=1e-5, op0=ALU.mult, op1=ALU.add), reads=[ssq.b], writes=[ssq.b])
                P.add("act", lambda e, g=g: e.activation(out=ssq.ap[:, g:g + 1], in_=ssq.ap[:, g:g + 1], func=AF.Sqrt), reads=[ssq.b], writes=[ssq.b])
                P.add("dve", lambda e, g=g: e.reciprocal(out=ssq.ap[:, g:g + 1], in_=ssq.ap[:, g:g + 1]), reads=[ssq.b], writes=[ssq.b])
                P.add("dve", lambda e, g=g: e.scalar_tensor_tensor(out=yn.ap[:, g * 512:(g + 1) * 512], in0=yacc.ap, scalar=ssq.ap[:, g:g + 1], in1=nwb.ap[:, g * 512:(g + 1) * 512], op0=ALU.mult, op1=ALU.mult), reads=[yacc.b, ssq.b, nwb.b], writes=[yn.b])
            dma("sp", ynd[ci * 128:(ci + 1) * 128, :], yn.ap, [yn.b], [])
        if CUT <= 8: return
        P.add("dve", lambda e: e.tensor_tensor(out=S(6), in0=S(4), in1=S(3), op=ALU.subtract), reads=[sm.b], writes=[sm.b])
        P.add("act", lambda e: e.activation(out=S(6), in_=S(6), func=AF.Exp), reads=[sm.b], writes=[sm.b])
        P.add("act", lambda e: e.activation(out=S(7), in_=S(4), func=AF.Exp), reads=[sm.b], writes=[sm.b])
        P.add("dve", lambda e: e.tensor_tensor(out=S(6), in0=S(6), in1=S(1), op=ALU.mult), reads=[sm.b], writes=[sm.b])
        for g in range(4):
            xg = xtok.ap[:, g * 512:(g + 1) * 512].rearrange("p (a b) -> p a b", a=8)
            v3 = lambda ap: ap.rearrange("p (a b) -> p a b", a=8)
            P.add("dve", lambda e, g=g, xg=xg: e.tensor_tensor(out=v3(xdts.ap), in0=xg, in1=bc(sm.ap[:, 6, 8 * g:8 * g + 8], 64), op=ALU.mult), reads=[xtok.b, sm.b], writes=[xdts.b])
            bank = pf[2 + g % 2]
            P.add("pe", lambda e, g=g, bank=bank: e.matmul(bank.ap, xtok.ap[:, 2048 + g * 128:2048 + (g + 1) * 128], xdts.ap, start=True, stop=True), reads=[xtok.b, xdts.b], writes=[bank.b])
            Hg = H.ap[:, g * 512:(g + 1) * 512]
            P.add("dve", lambda e, g=g, Hg=Hg: e.tensor_tensor(out=v3(Hg), in0=v3(Hg), in1=bc(sm.ap[:, 7, 8 * g:8 * g + 8], 64), op=ALU.mult), reads=[H.b, sm.b], writes=[H.b])
            P.add("dve", lambda e, Hg=Hg, bank=bank: e.tensor_tensor(out=Hg, in0=Hg, in1=bank.ap, op=ALU.add), reads=[H.b, bank.b], writes=[H.b])

    SKIP1 = int(os.environ.get('KSKIP1', '0'))
    for c in range(NPRE - npre, NPRE):
        if SKIP1: break
        ssd_chunk(xpre[c * 128:(c + 1) * 128, :], flg.ap[:, c:c + 1], False, c)
    for ci in range(0 if SKIP1 else NCH1):
        ssd_chunk(xmain[(ci + 1) * 128:(ci + 2) * 128, :], flg.ap[:, NPRE + ci:NPRE + ci + 1], True, ci)

    if stop < 2:
        P.emit(nc); es.close(); return nc
    P.barrier()
    AB.off, AFa.off = markB, markF
    Wq = AB.get("Wq", 8, 1024); load_w(Wq, w_in[:, COL_Q:COL_Q + 1024], 1024)
    Wk2 = AB.get("Wk2", 8, 128); load_w(Wk2, w_in[:, COL_K:COL_K + 128], 128)
    Wv = AB.get("Wv", 8, 128); load_w(Wv, w_in[:, COL_V:COL_V + 128], 128)
    EB = AB.get("EB", 2, 16, 128)
    ebf = AFa.get("ebf", 2048); mkf = AFa.get("mkf", 2048)
    for kt in range(2):
        dma("sp", ebf.ap, biasg[:, kt * 2048:(kt + 1) * 2048], [], [ebf.b])
        dma("sp", mkf.ap, maskg[:, kt * 2048:(kt + 1) * 2048], [], [mkf.b])
        P.add("act", lambda e: e.activation(out=ebf.ap, in_=ebf.ap, func=AF.Exp), reads=[ebf.b], writes=[ebf.b])
        P.add("dve", lambda e, kt=kt: e.tensor_tensor(out=EB.ap[:, kt, :, :].rearrange("p a b -> p (a b)"), in0=ebf.ap, in1=mkf.ap, op=ALU.mult), reads=[ebf.b, mkf.b], writes=[EB.b])
    xb = AB.get("xb2", 1024); xT = AB.get("xT2", 8, 128)
    kT = [AB.get("kT%d" % i, 2, 128) for i in range(2)]
    vx = [AB.get("vx%d" % i, 2, 65) for i in range(2)]
    for i in range(2):
        P.add("pool", lambda e, i=i: e.memset(vx[i].ap, 1.0), writes=[vx[i].b])
    qT = AB.get("qT", 16, 128); et = AB.get("et", 4, 128)
    PT = [AB.get("PT%d" % i, 4, 128) for i in range(2)]
    ya = AB.get("ya", 1024)
    den = AFa.get("den", 4)
    flag0 = flg.ap[:, NPRE:NPRE + 1]
    CUT2 = int(os.environ.get('KCUT2', '99')); NCH2 = int(os.environ.get('KNCH2', str(NM2)))
    for ci in range(NCH2):
        sl = ci % 2
        load_xT(xmain[ci * 128:(ci + 1) * 128, :], xb, xT)
        for kv in range(2):
            chain(pf[0], pf[0].ap[0:64, kv * 128:(kv + 1) * 128], [(Wk2.ap[:, k, kv * 64:(kv + 1) * 64], xT.ap[:, k, :]) for k in range(8)], [Wk2.b, xT.b])
        P.add("act", lambda e, sl=sl: e.copy(out=kT[sl].ap[0:64, :, :], in_=pf[0].ap[0:64, 0:256].rearrange("p (a b) -> p a b", a=2)), reads=[pf[0].b], writes=[kT[sl].b])
        chain(pf[1], pf[1].ap[:, 0:128], [(xT.ap[:, k, :], Wv.ap[:, k, :]) for k in range(8)], [xT.b, Wv.b])
        P.add("dve", lambda e, sl=sl: e.tensor_copy(out=vx[sl].ap[:, :, 0:64], in_=pf[1].ap[:, 0:128].rearrange("p (a b) -> p a b", a=2)), reads=[pf[1].b], writes=[vx[sl].b])
        if ci == 0 or CUT2 <= 1:
            continue
        for q4 in range(4):
            bank = pf[2 + q4 % 2]
            for tt in range(4):
                j = q4 * 4 + tt
                chain(bank, bank.ap[0:64, tt * 128:(tt + 1) * 128], [(Wq.ap[:, k, j * 64:(j + 1) * 64], xT.ap[:, k, :]) for k in range(8)], [Wq.b, xT.b])
            P.add("act", lambda e, q4=q4, bank=bank: e.copy(out=qT.ap[0:64, q4 * 4:q4 * 4 + 4, :], in_=bank.ap[0:64, :].rearrange("p (a b) -> p a b", a=4)), reads=[bank.b], writes=[qT.b])
        for kvh in range(2):
            if CUT2 <= 2: break
            for hb in range(2):
                j0 = kvh * 8 + hb * 4
                for kt in range(2):
                    slk = (ci + 1 + kt) % 2
                    bank = pf[4 + kt]
                    for i in range(4):
                        j = j0 + i
                        base = (j % 2) * 64 * int(os.environ.get("KB64", "1"))
                        P.add("pe", lambda e, i=i, j=j, base=base, slk=slk, bank=bank, kvh=kvh: e.matmul(bank.ap[:, i * 128:(i + 1) * 128], kT[slk].ap[0:64, kvh, :], qT.ap[0:64, j, :], start=True, stop=True), reads=[kT[slk].b, qT.b], writes=[bank.b])
                    P.add("act", lambda e, bank=bank: e.activation(out=et.ap.rearrange("p a b -> p (a b)"), in_=bank.ap, func=AF.Exp, scale=0.125), reads=[bank.b], writes=[et.b])
                    if ci == 2 and kt == 0:
                        P.add("dve", lambda e, kt=kt, j0=j0: e.scalar_tensor_tensor(out=PT[kt].ap, in0=et.ap, scalar=flag0, in1=EB.ap[:, kt, j0:j0 + 4, :], op0=ALU.mult, op1=ALU.mult), reads=[et.b, EB.b, flg.b], writes=[PT[kt].b])
                    else:
                        P.add("dve", lambda e, kt=kt, j0=j0: e.tensor_tensor(out=PT[kt].ap, in0=et.ap, in1=EB.ap[:, kt, j0:j0 + 4, :], op=ALU.mult), reads=[et.b, EB.b], writes=[PT[kt].b])
                if CUT2 <= 3: continue
                bank = pf[hb]
                for i in range(4):
                    for kt in range(2):
                        slk = (ci + 1 + kt) % 2
                        P.add("pe", lambda e, i=i, kt=kt, slk=slk, bank=bank, kvh=kvh: e.matmul(bank.ap[:, i * 65:(i + 1) * 65], PT[kt].ap[:, i, :], vx[slk].ap[:, kvh, :], start=(kt == 0), stop=(kt == 1)), reads=[PT[kt].b, vx[slk].b], writes=[bank.b])
                if CUT2 <= 4: continue
                pv = bank.ap[:, 0:260].rearrange("p (a b) -> p a b", a=4)
                P.add("dve", lambda e, pv=pv, j0=j0: e.tensor_tensor(out=den.ap, in0=pv[:, :, 64], in1=esink[:, j0:j0 + 4], op=ALU.add), reads=[bank.b, smallp.b], writes=[den.b])
                P.add("dve", lambda e: e.reciprocal(out=den.ap, in_=den.ap), reads=[den.b], writes=[den.b])
                P.add("dve", lambda e, pv=pv, j0=j0: e.tensor_tensor(out=ya.ap[:, j0 * 64:(j0 + 4) * 64].rearrange("p (a b) -> p a b", a=4), in0=pv[:, :, 0:64], in1=bc(den.ap, 64), op=ALU.mult), reads=[bank.b, den.b], writes=[ya.b])
        dma("sp", yad[(ci - 1) * 128:ci * 128, :], ya.ap, [ya.b], [])

    if stop < 3:
        P.emit(nc); es.close(); return nc
    P.barrier()
    AB.off, AFa.off = markB, markF
    Wg = AB.get("Wg", 8, 2048); load_w(Wg, w_in[:, COL_G:COL_G + 2048], 2048)
    Wbs = AB.get("Wbs", 16, 1024); load_w(Wbs, w_bs, 1024)
    Wba = AB.get("Wba", 8, 1024); load_w(Wba, w_ba, 1024)
    Wmx = AB.get("Wmx", 8, 1024); load_w(Wmx, w_mix, 1024)
    bgb = AFa.get("bgb", 2048); dma("sp", bgb.ap, bcast_row(b_gate, 2048), [], [bgb.b])
    lng = AFa.get("lng", 2, 1024)
    dma("sp", lng.ap[:, 0, :], bcast_row(ln1g, 1024), [], [lng.b]); dma("sp", lng.ap[:, 1, :], bcast_row(ln1b, 1024), [], [lng.b])
    xb = AB.get("xb3", 1024); xT = AB.get("xT3", 8, 128)
    ynb = AB.get("ynb", 2048); yab = AB.get("yab", 1024)
    ynT = AB.get("ynT", 16, 128); yaT = AB.get("yaT", 8, 128)
    mg = AB.get("mg", 1024); mT = AB.get("mT", 8, 128)
    xf = AFa.get("xf", 1024); gt = AFa.get("gt", 2048); m1 = AFa.get("m1", 512); r = AFa.get("r", 1024)
    st = AFa.get("st", 2, 6); mv = AFa.get("mv", 2)

    def transp(src, dst, ntl):
        for bt in range(ntl // 8):
            for i in range(8):
                j = bt * 8 + i
                P.add("pe", lambda e, i=i, j=j: e.transpose(pb[1].ap[:, i * 128:(i + 1) * 128], src.ap[:, j * 128:(j + 1) * 128], identb), reads=[src.b, cb16.b], writes=[pb[1].b])
            P.add("act", lambda e, bt=bt: e.copy(out=dst.ap[:, bt * 8:bt * 8 + 8, :].rearrange("p a b -> p (a b)"), in_=pb[1].ap), reads=[pb[1].b], writes=[dst.b])

    def layer_norm(r, g_ap, b_ap, gb, st, mv):
        for i in range(2):
            P.add("dve", lambda e, i=i: e.bn_stats(out=st.ap[:, i, :], in_=r.ap[:, i * 512:(i + 1) * 512]), reads=[r.b], writes=[st.b])
        P.add("dve", lambda e: e.bn_aggr(out=mv.ap, in_=st.ap.rearrange("p a b -> p (a b)")), reads=[st.b], writes=[mv.b])
        P.add("dve", lambda e: e.tensor_scalar(out=mv.ap[:, 1:2], in0=mv.ap[:, 1:2], scalar1=1e-5, scalar2=None, op0=ALU.add), reads=[mv.b], writes=[mv.b])
        P.add("act", lambda e: e.activation(out=mv.ap[:, 1:2], in_=mv.ap[:, 1:2], func=AF.Sqrt), reads=[mv.b], writes=[mv.b])
        P.add("dve", lambda e: e.reciprocal(out=mv.ap[:, 1:2], in_=mv.ap[:, 1:2]), reads=[mv.b], writes=[mv.b])
        P.add("dve", lambda e: e.tensor_scalar(out=r.ap, in0=r.ap, scalar1=mv.ap[:, 0:1], scalar2=mv.ap[:, 1:2], op0=ALU.subtract, op1=ALU.mult), reads=[r.b, mv.b], writes=[r.b])
        P.add("dve", lambda e: e.tensor_tensor(out=r.ap, in0=r.ap, in1=g_ap, op=ALU.mult), reads=[r.b, gb], writes=[r.b])
        P.add("dve", lambda e: e.tensor_tensor(out=r.ap, in0=r.ap, in1=b_ap, op=ALU.add), reads=[r.b, gb], writes=[r.b])

    for ci in range(NM1):
        xrow = xmain[(ci + 1) * 128:(ci + 2) * 128, :]
        load_xT(xrow, xb, xT)
        dma("sp", xf.ap, xrow, [], [xf.b])
        dma("sp", ynb.ap, ynd[ci * 128:(ci + 1) * 128, :], [], [ynb.b])
        dma("sp", yab.ap, yad[ci * 128:(ci + 1) * 128, :], [], [yab.b])
        transp(ynb, ynT, 16)
        transp(yab, yaT, 8)
        for s4 in range(4):
            bank = pf[s4 % 2]
            chain(bank, bank.ap, [(xT.ap[:, k, :], Wg.ap[:, k, s4 * 512:(s4 + 1) * 512]) for k in range(8)], [xT.b, Wg.b])
            P.add("dve", lambda e, s4=s4, bank=bank: e.tensor_tensor(out=gt.ap[:, s4 * 512:(s4 + 1) * 512], in0=bank.ap, in1=bgb.ap[:, s4 * 512:(s4 + 1) * 512], op=ALU.add), reads=[bank.b, bgb.b], writes=[gt.b])
        P.add("act", lambda e: e.activation(out=gt.ap, in_=gt.ap, func=AF.Sigmoid), reads=[gt.b], writes=[gt.b])
        for hf in range(2):
            chain(pf[2], pf[2].ap, [(ynT.ap[:, i, :], Wbs.ap[:, i, hf * 512:(hf + 1) * 512]) for i in range(16)], [ynT.b, Wbs.b])
            chain(pf[3], pf[3].ap, [(yaT.ap[:, i, :], Wba.ap[:, i, hf * 512:(hf + 1) * 512]) for i in range(8)], [yaT.b, Wba.b])
            P.add("dve", lambda e, hf=hf: e.tensor_tensor(out=m1.ap, in0=pf[2].ap, in1=gt.ap[:, hf * 512:(hf + 1) * 512], op=ALU.mult), reads=[pf[2].b, gt.b], writes=[m1.b])
            P.add("dve", lambda e, hf=hf: e.tensor_tensor(out=r.ap[:, hf * 512:(hf + 1) * 512], in0=pf[3].ap, in1=gt.ap[:, 1024 + hf * 512:1024 + (hf + 1) * 512], op=ALU.mult), reads=[pf[3].b, gt.b], writes=[r.b])
            P.add("dve", lambda e, hf=hf: e.tensor_tensor(out=mg.ap[:, hf * 512:(hf + 1) * 512], in0=m1.ap, in1=r.ap[:, hf * 512:(hf + 1) * 512], op=ALU.add), reads=[m1.b, r.b], writes=[mg.b])
        transp(mg, mT, 8)
        for hf in range(2):
            bank = pf[4 + hf]
            chain(bank, bank.ap, [(mT.ap[:, i, :], Wmx.ap[:, i, hf * 512:(hf + 1) * 512]) for i in range(8)], [mT.b, Wmx.b])
            P.add("dve", lambda e, hf=hf, bank=bank: e.scalar_tensor_tensor(out=r.ap[:, hf * 512:(hf + 1) * 512], in0=xf.ap[:, hf * 512:(hf + 1) * 512], scalar=ALPHA, in1=bank.ap, op0=ALU.mult, op1=ALU.add), reads=[xf.b, bank.b], writes=[r.b])
        layer_norm(r, lng.ap[:, 0, :], lng.ap[:, 1, :], lng.b, st, mv)
        if ci == 0:
            P.add("dve", lambda e: e.tensor_scalar(out=r.ap, in0=r.ap, scalar1=flag0, scalar2=None, op0=ALU.mult), reads=[r.b, flg.b], writes=[r.b])
        dma("sp", h1d[ci * 128:(ci + 1) * 128, :], r.ap, [r.b], [])

    if stop < 4:
        P.emit(nc); es.close(); return nc
    P.barrier()
    AB.off, AFa.off = markB, markF
    Wup = AB.get("Wup", 8, 5632); load_w(Wup, w_up, 5632)
    Wdn = AB.get("Wdn", 22, 1024); load_w(Wdn, w_dn, 1024)
    fw = AFa.get("fw", 132); fb = AFa.get("fb", 44)
    per_channel(fw.ap[:, 0:88], fcw[0:88, :], 88); fw.b.writer = P.ops["dve"][-1]
    per_channel(fw.ap[:, 88:132], fcw[88:132, :], 44); fw.b.writer = P.ops["dve"][-1]
    per_channel(fb.ap, fcb, 44); fb.b.writer = P.ops["dve"][-1]
    lng2 = AFa.get("lng2", 2, 1024)
    dma("sp", lng2.ap[:, 0, :], bcast_row(ln2g, 1024), [], [lng2.b]); dma("sp", lng2.ap[:, 1, :], bcast_row(ln2b, 1024), [], [lng2.b])
    hA = AB.get("hA", 1024); hB = AB.get("hB", 1024); h1T = AB.get("h1T", 8, 132)
    aT = AB.get("aT", 22, 128)
    cv = AFa.get("cv", 44, 128); t0 = AFa.get("t0", 128); sg = AFa.get("sg", 128)
    hr = AFa.get("hr", 1024); r4 = AFa.get("r4", 1024)
    st4 = AFa.get("st4", 2, 6); mv4 = AFa.get("mv4", 2)
    for c in range(16):
        r0 = 128 + c * 128
        dma("pool", hA.ap, h1d[r0 - 2:r0 + 126, :], [], [hA.b])
        dma("pool", hB.ap[0:2, :], h1d[r0 + 126:r0 + 128, :], [], [hB.b])
        dma("sp", hr.ap, h1d[r0:r0 + 128, :], [], [hr.b])
        for k in range(8):
            P.add("pe", lambda e, k=k: e.transpose(pb[0].ap[:, k * 128:(k + 1) * 128], hA.ap[:, k * 128:(k + 1) * 128], identb), reads=[hA.b, cb16.b], writes=[pb[0].b])
        P.add("act", lambda e: e.copy(out=h1T.ap[:, :, 0:128], in_=pb[0].ap.rearrange("p (a b) -> p a b", a=8)), reads=[pb[0].b], writes=[h1T.b])
        for k in range(8):
            P.add("pe", lambda e, k=k: e.transpose(pb[1].ap[:, k * 2:(k + 1) * 2], hB.ap[0:2, k * 128:(k + 1) * 128], identb[0:2, 0:2]), reads=[hB.b, cb16.b], writes=[pb[1].b])
        P.add("act", lambda e: e.copy(out=h1T.ap[:, :, 128:130], in_=pb[1].ap[:, 0:16].rearrange("p (a b) -> p a b", a=8)), reads=[pb[1].b], writes=[h1T.b])
        for j in range(44):
            bank = pf[j % 4]
            chain(bank, bank.ap[:, 0:130], [(Wup.ap[:, k, j * 128:(j + 1) * 128], h1T.ap[:, k, 0:130]) for k in range(8)], [Wup.b, h1T.b])
            P.add("act", lambda e, j=j, bank=bank: e.activation(out=t0.ap, in_=bank.ap[:, 2:130], func=AF.Identity, bias=fb.ap[:, j:j + 1], scale=fw.ap[:, 88 + j:89 + j]), reads=[bank.b, fw.b, fb.b], writes=[t0.b])
            P.add("dve", lambda e, j=j, bank=bank: e.scalar_tensor_tensor(out=t0.ap, in0=bank.ap[:, 1:129], scalar=fw.ap[:, 44 + j:45 + j], in1=t0.ap, op0=ALU.mult, op1=ALU.add), reads=[bank.b, fw.b, t0.b], writes=[t0.b])
            P.add("dve", lambda e, j=j, bank=bank: e.scalar_tensor_tensor(out=cv.ap[:, j, :], in0=bank.ap[:, 0:128], scalar=fw.ap[:, j:j + 1], in1=t0.ap, op0=ALU.mult, op1=ALU.add), reads=[bank.b, fw.b, t0.b], writes=[cv.b])
        for j in range(22):
            P.add("act", lambda e, j=j: e.activation(out=sg.ap, in_=cv.ap[:, j, :], func=AF.Silu), reads=[cv.b], writes=[sg.b])
            P.add("dve", lambda e, j=j: e.tensor_tensor(out=aT.ap[:, j, :], in0=sg.ap, in1=cv.ap[:, 22 + j, :], op=ALU.mult), reads=[sg.b, cv.b], writes=[aT.b])
        for hf in range(2):
            bank = pf[4 + hf]
            chain(bank, bank.ap, [(aT.ap[:, j, :], Wdn.ap[:, j, hf * 512:(hf + 1) * 512]) for j in range(22)], [aT.b, Wdn.b])
            P.add("dve", lambda e, hf=hf, bank=bank: e.scalar_tensor_tensor(out=r4.ap[:, hf * 512:(hf + 1) * 512], in0=hr.ap[:, hf * 512:(hf + 1) * 512], scalar=ALPHA, in1=bank.ap, op0=ALU.mult, op1=ALU.add), reads=[hr.b, bank.b], writes=[r4.b])
        layer_norm(r4, lng2.ap[:, 0, :], lng2.ap[:, 1, :], lng2.b, st4, mv4)
        dma("sp", out[c * 128:(c + 1) * 128, :], r4.ap, [r4.b], [])

    P.emit(nc)
    es.close()
    return nc


def rel_bucket_np(rel):
    n = np.maximum(rel, 0)
    nf = np.maximum(n, 1).astype(np.float32)
    large = 16 + (np.log(nf / np.float32(16)) / np.float32(np.log(128 / 16)) * np.float32(16)).astype(np.int32)
    large = np.minimum(large, 31)
    return np.where(n < 16, n, large)


_NC = None


def kernel(_dbg=None, **inp):
    global _NC
    x = np.asarray(inp["x"], np.float32)[0]
    f = lambda k: np.ascontiguousarray(np.asarray(inp[k], np.float32)[0])
    common = {
        "w_in": f("w_in"), "b_gate": f("b_gate")[None], "dtb": f("ssm_dt_bias")[None], "alog": f("ssm_a_log")[None],
        "dsk": f("ssm_d")[None], "normw": f("ssm_norm_w")[None], "sinks": f("attn_sinks")[None],
        "w_bs": f("w_branch_ssm"), "w_ba": f("w_branch_attn"), "w_mix": f("w_mix_out"),
        "ln1g": f("ln1_g")[None], "ln1b": f("ln1_b")[None], "ln2g": f("ln2_g")[None], "ln2b": f("ln2_b")[None],
        "w_up": f("w_up"), "w_dn": f("w_down"),
    }
    scw = f("ssm_conv_w")
    common["scw"] = np.ascontiguousarray(scw.reshape(4 * 24, 128))
    common["scb"] = np.ascontiguousarray(f("ssm_conv_b").reshape(24, 128))
    common["fcw"] = np.ascontiguousarray(f("ffn_conv_w").reshape(3 * 44, 128))
    common["fcb"] = np.ascontiguousarray(f("ffn_conv_b").reshape(44, 128))
    s = np.arange(128)
    ident = np.eye(128, dtype=np.float32)
    triU = (s[:, None] <= s[None, :]).astype(np.float32)
    ustr = (s[:, None] > s[None, :]).astype(np.float32)
    common["cst"] = np.ascontiguousarray(np.concatenate([ident, triU, ustr, np.ones((128, 128), np.float32)], axis=1))
    rb = np.asarray(inp["rel_bias"], np.float32)
    bg = np.zeros((128, 2, 16, 128), np.float32); mk = np.zeros((128, 2, 16, 128), np.float32)
    for kt in range(2):
        rel = (s[None, :] + 128) - (s[:, None] + 128 * kt)
        valid = (rel >= 0) & (rel < 128)
        bidx = rel_bucket_np(rel)
        g = rb[bidx]
        bg[:, kt] = np.transpose(g, (0, 2, 1))
        mk[:, kt] = np.broadcast_to(valid[:, None, :], (128, 16, 128))
    common["biasg"] = np.ascontiguousarray(bg.reshape(128, -1)); common["maskg"] = np.ascontiguousarray(mk.reshape(128, -1))
    in_maps = []
    for c in range(NCORE):
        S = c * TOK
        lo = S - 128 - NPRE * 128
        xp = np.zeros((NPRE * 128, 1024), np.float32)
        if S - 128 > 0:
            src_lo = max(lo, 0)
            xp[src_lo - lo:] = x[src_lo:S - 128]
        xm = np.zeros((NM2 * 128, 1024), np.float32)
        lo2 = S - 256
        src_lo = max(lo2, 0)
        xm[src_lo - lo2:] = x[src_lo:S + TOK]
        fl = np.zeros((128, NPRE + NM1), np.float32)
        for i in range(NPRE):
            fl[:, i] = 1.0 if lo + i * 128 >= 0 else 0.0
        fl[:, NPRE] = 1.0 if c > 0 else 0.0
        fl[:, NPRE + 1:] = 1.0
        m = dict(common); m["xpre"] = xp; m["xmain"] = xm; m["pflag"] = fl
        in_maps.append(m)
    if _dbg is not None:
        return in_maps
    if _NC is None:
        _NC = build()
    res = run_bass_kernel_spmd(_NC, in_maps, core_ids=list(range(NCORE)))
    o = np.concatenate([res.results[c]["out"] for c in range(NCORE)], axis=0)
    return o[None].astype(np.float32)
```

```python
import numpy as np
import concourse.bass as bass
import concourse.mybir as mybir

ENGS = ["pe", "act", "dve", "pool", "sp"]
N_DMA_SEM = 16
import os as _os
SAME_ENGINE_SYNC = _os.environ.get("KSES", "1") == "1"


class Buf:
    __slots__ = ("name", "writer", "readers", "dma_readers", "excl")

    def __init__(self, name, excl=False):
        self.name = name
        self.excl = excl
        self.writer = None
        self.readers = {}
        self.dma_readers = []


class Op:
    __slots__ = ("eng", "fn", "idx", "waits", "signal", "is_dma", "dsem", "dtarget", "clock", "sigcount", "uid")


class Prog:
    def __init__(self):
        self.ops = {e: [] for e in ENGS}
        self.clock = {e: {f: 0 for f in ENGS} for e in ENGS}
        self.known_dma = {e: set() for e in ENGS}
        self.dma_sem_count = [0] * N_DMA_SEM
        self.dma_sem_last = [None] * N_DMA_SEM
        self.dma_rr = 0
        self.n_dma = 0
        self.uid = 0
        self.bar = {e: [] for e in ENGS}

    def barrier(self):
        lasts = [self.ops[e][-1] for e in ENGS if self.ops[e] and not self.ops[e][-1].is_dma]
        for e in ENGS:
            pass
        lasts = []
        for e in ENGS:
            for op in reversed(self.ops[e]):
                if not op.is_dma:
                    lasts.append(op)
                    break
        dmas = [op for op in self.dma_sem_last if op is not None]
        for e in ENGS:
            self.bar[e] = lasts + dmas

    def add(self, eng, fn, reads=(), writes=(), dma=False):
        op = Op()
        op.eng = eng
        op.fn = fn
        op.idx = len(self.ops[eng])
        op.waits = []
        op.signal = False
        op.is_dma = dma
        op.uid = self.uid
        self.uid += 1
        deps = []
        for b in reads:
            if b.writer is not None:
                deps.append(b.writer)
            if b.excl:
                for e2, r in b.readers.items():
                    if e2 != eng:
                        deps.append(r)
        for b in writes:
            if b.writer is not None:
                deps.append(b.writer)
            deps.extend(b.readers.values())
            deps.extend(b.dma_readers)
        if self.bar[eng]:
            deps.extend(self.bar[eng])
            self.bar[eng] = []
        clk = self.clock[eng]
        seen = set()
        for d in deps:
            if d.uid in seen:
                continue
            seen.add(d.uid)
            if d.is_dma:
                if d.uid not in self.known_dma[eng]:
                    op.waits.append(("dma", d.dsem, d.dtarget))
                    self.known_dma[eng].add(d.uid)
            else:
                if d.eng == eng and (eng == "pe" or not SAME_ENGINE_SYNC):
                    continue
                if clk[d.eng] < d.idx + 1:
                    op.waits.append(("eng", d))
                    d.signal = True
                    for f in ENGS:
                        if d.clock[f] > clk[f]:
                            clk[f] = d.clock[f]
                    if clk[d.eng] < d.idx + 1:
                        clk[d.eng] = d.idx + 1
        if dma and eng == "pool":
            pd = self.__dict__.setdefault("pool_dmas", [])
            if len(pd) >= 4:
                d = pd[-4]
                if d.uid not in self.known_dma[eng]:
                    op.waits.append(("dma", d.dsem, d.dtarget))
                    self.known_dma[eng].add(d.uid)
            pd.append(op)
        if dma:
            k = self.dma_rr
            self.dma_rr = (self.dma_rr + 1) % N_DMA_SEM
            prev = self.dma_sem_last[k]
            if prev is not None and prev.uid not in self.known_dma[eng]:
                op.waits.append(("dma", k, prev.dtarget))
                self.known_dma[eng].add(prev.uid)
            self.dma_sem_count[k] += 16
            op.dsem = k
            op.dtarget = self.dma_sem_count[k]
            self.dma_sem_last[k] = op
            self.n_dma += 1
        op.clock = dict(clk)
        if not SAME_ENGINE_SYNC or eng == "pe":
            pass
        self.ops[eng].append(op)
        for b in writes:
            b.writer = op
            b.readers = {}
            b.dma_readers = []
        for b in reads:
            if dma:
                b.dma_readers.append(op)
            else:
                b.readers[eng] = op
        return op

    def emit(self, nc, final_waits=True):
        for e in ENGS:
            c = 0
            for op in self.ops[e]:
                if op.signal:
                    c += 1
                op.sigcount = c
        from contextlib import ExitStack
        with ExitStack() as es:
            esem = {e: es.enter_context(nc.semaphore("s_" + e)) for e in ENGS}
            dsem = [es.enter_context(nc.semaphore("d%d" % i)) for i in range(N_DMA_SEM)]
            block = es.enter_context(nc.Block())
            last_dma = [op for op in self.dma_sem_last if op is not None]

            def run(e, h):
                for op in self.ops[e]:
                    for w in op.waits:
                        if w[0] == "dma":
                            h.wait_ge(dsem[w[1]], w[2])
                        else:
                            h.wait_ge(esem[w[1].eng], w[1].sigcount)
                    ins = op.fn(h)
                    if op.is_dma:
                        ins.then_inc(dsem[op.dsem], 16)
                    elif op.signal:
                        ins.then_inc(esem[e], 1)
                if e == "sp" and final_waits:
                    for k in range(N_DMA_SEM):
                        if self.dma_sem_count[k] > 0:
                            h.wait_ge(dsem[k], self.dma_sem_count[k])

            @block.tensor
            def _(h):
                run("pe", h)

            @block.scalar
            def _(h):
                run("act", h)

            @block.vector
            def _(h):
                run("dve", h)

            @block.gpsimd
            def _(h):
                run("pool", h)

            @block.sync
            def _(h):
                run("sp", h)

from contextlib import ExitStack
import ml_dtypes
from concourse.bass_utils import run_bass_kernel_spmd

F32 = mybir.dt.float32
BF16 = mybir.dt.bfloat16
AF = mybir.ActivationFunctionType
ALU = mybir.AluOpType

NCORE = 8
TOK = 2048
NPRE = 112
NM1 = 17
NM2 = 18
ALPHA = 2.0 ** 0.25
COL_Z, COL_X, COL_B, COL_C, COL_DT, COL_Q, COL_K, COL_V, COL_G = 0, 2048, 4096, 4608, 5120, 5152, 6176, 6304, 6432


class TT:
    def __init__(self, ap, name, excl=False):
        self.ap = ap
        self.b = Buf(name, excl)


class Arena:
    def __init__(self, t, n):
        self.t, self.n, self.off = t, n, 0

    def get(self, name, *fs):
        n = int(np.prod(fs))
        ap = self.t[:, self.off:self.off + n]
        self.off += n
        assert self.off <= self.n, (name, self.off, self.n)
        if len(fs) == 2:
            ap = ap.rearrange("p (a b) -> p a b", a=fs[0])
        elif len(fs) == 3:
            ap = ap.rearrange("p (a b c) -> p a b c", a=fs[0], b=fs[1])
        return TT(ap, name)


def bc(ap2, n):
    return ap2.unsqueeze(2).to_broadcast([ap2.shape[0], ap2.shape[1], n])


def build(stop=4, npre=NPRE, dbg=False):
    nc = bass.Bass("TRN2", target_bir_lowering=False)
    dt_in = lambda n, s: nc.dram_tensor(n, s, F32, kind="ExternalInput").ap()
    xpre = dt_in("xpre", [NPRE * 128, 1024])
    xmain = dt_in("xmain", [NM2 * 128, 1024])
    pflag = dt_in("pflag", [128, NPRE + NM1])
    w_in = dt_in("w_in", [1024, 8480])
    b_gate = dt_in("b_gate", [1, 2048])
    scw = dt_in("scw", [96, 128])
    scb = dt_in("scb", [24, 128])
    dtb = dt_in("dtb", [1, 32])
    alog = dt_in("alog", [1, 32])
    dsk = dt_in("dsk", [1, 32])
    normw = dt_in("normw", [1, 2048])
    sinks = dt_in("sinks", [1, 16])
    w_bs = dt_in("w_bs", [2048, 1024])
    w_ba = dt_in("w_ba", [1024, 1024])
    w_mix = dt_in("w_mix", [1024, 1024])
    ln1g = dt_in("ln1g", [1, 1024]); ln1b = dt_in("ln1b", [1, 1024])
    ln2g = dt_in("ln2g", [1, 1024]); ln2b = dt_in("ln2b", [1, 1024])
    w_up = dt_in("w_up", [1024, 5632])
    fcw = dt_in("fcw", [132, 128])
    fcb = dt_in("fcb", [44, 128])
    w_dn = dt_in("w_dn", [2816, 1024])
    cst = dt_in("cst", [128, 4 * 128])
    biasg = dt_in("biasg", [128, 2 * 16 * 128])
    maskg = dt_in("maskg", [128, 2 * 16 * 128])
    out = nc.dram_tensor("out", [TOK, 1024], F32, kind="ExternalOutput").ap()
    skind = "ExternalOutput" if dbg else "Internal"
    ynd = nc.dram_tensor("ynd", [NM1 * 128, 2048], BF16, kind=skind).ap()
    yad = nc.dram_tensor("yad", [NM1 * 128, 1024], BF16, kind=skind).ap()
    h1d = nc.dram_tensor("h1d", [NM1 * 128, 1024], F32, kind=skind).ap()

    P = Prog()
    es = ExitStack()
    NB, NF = 153 * 512, 47 * 256
    ABt = es.enter_context(nc.sbuf_tensor("AB", [128, NB], BF16))
    AFt = es.enter_context(nc.sbuf_tensor("AF", [128, NF], F32))
    pf = [TT(es.enter_context(nc.psum_tensor("pf%d" % i, [128, 512], F32))[:], "pf%d" % i, True) for i in range(6)]
    pb = [TT(es.enter_context(nc.psum_tensor("pb%d" % i, [128, 1024], BF16))[:], "pb%d" % i, True) for i in range(2)]
    AB = Arena(ABt, NB)
    AFa = Arena(AFt, NF)

    def bcast_row(src, n):
        return bass.AP(src.tensor, 0, [[0, 128], [1, n]])

    def dma(eng, o, i, reads, writes):
        P.add(eng, lambda e, o=o, i=i: e.dma_start(out=o, in_=i), reads=reads, writes=writes, dma=True)

    cf = AFa.get("cf", 4, 128)
    dma("sp", cf.ap, cst.rearrange("p (a b) -> p a b", a=4), [], [cf.b])
    identf, triU, Ustr, onesf = cf.ap[:, 0, :], cf.ap[:, 1, :], cf.ap[:, 2, :], cf.ap[:, 3, :]
    cb16 = AB.get("cb16", 4, 128)
    dma("pool", cb16.ap, cst.rearrange("p (a b) -> p a b", a=4), [], [cb16.b])
    identb, maskb = cb16.ap[:, 0, :], cb16.ap[:, 1, :]
    flg = AFa.get("flg", NPRE + NM1)
    dma("sp", flg.ap, pflag, [], [flg.b])
    smallp = AFa.get("smallp", 6, 32)
    dma("sp", smallp.ap[:, 0, :], bcast_row(dtb, 32), [], [smallp.b])
    dma("sp", smallp.ap[:, 1, :], bcast_row(alog, 32), [], [smallp.b])
    dma("sp", smallp.ap[:, 2, :], bcast_row(dsk, 32), [], [smallp.b])
    dma("sp", smallp.ap[:, 3, 0:16], bcast_row(sinks, 16), [], [smallp.b])
    P.add("act", lambda e: e.activation(out=smallp.ap[:, 1, :], in_=smallp.ap[:, 1, :], func=AF.Exp), reads=[smallp.b], writes=[smallp.b])
    P.add("dve", lambda e: e.tensor_scalar(out=smallp.ap[:, 1, :], in0=smallp.ap[:, 1, :], scalar1=-1.0, scalar2=None, op0=ALU.mult), reads=[smallp.b], writes=[smallp.b])
    P.add("act", lambda e: e.activation(out=smallp.ap[:, 3, 0:16], in_=smallp.ap[:, 3, 0:16], func=AF.Exp), reads=[smallp.b], writes=[smallp.b])
    dtb_bc, a_bc, D_bc, esink = smallp.ap[:, 0, :], smallp.ap[:, 1, :], smallp.ap[:, 2, :], smallp.ap[:, 3, 0:16]
    onecol = onesf[:, 0:1]
    rawh = AB.get("rawh", 24, 4)
    markB, markF = AB.off, AFa.off
    H = AFa.get("H", 2048)

    def load_w(dst, src, ncols, nk=8):
        nk = src.shape[0] // 128
        for c0 in range(0, ncols, 512):
            c1 = min(c0 + 512, ncols)
            for k in range(nk):
                dma("pool", dst.ap[:, k, c0:c1], src[k * 128:(k + 1) * 128, c0:c1], [], [dst.b])

    def load_xT(xrow_ap, xb, xT):
        dma("pool", xb.ap, xrow_ap, [], [xb.b])
        for k in range(8):
            P.add("pe", lambda e, k=k: e.transpose(pb[0].ap[:, k * 128:(k + 1) * 128], xb.ap[:, k * 128:(k + 1) * 128], identb), reads=[xb.b, cb16.b], writes=[pb[0].b])
        P.add("act", lambda e: e.copy(out=xT.ap.rearrange("p a b -> p (a b)"), in_=pb[0].ap), reads=[pb[0].b], writes=[xT.b])

    def chain(bank, out_ap, pairs, reads):
        n = len(pairs)
        for i, (l, r) in enumerate(pairs):
            P.add("pe", lambda e, l=l, r=r, i=i: e.matmul(out_ap, l, r, start=(i == 0), stop=(i == n - 1)), reads=reads, writes=[bank.b])

    def per_channel(dst, src_dram, rows):
        tmp = AFa.get("pc_tmp", 128)
        dma("sp", tmp.ap[0:rows, :], src_dram, [], [tmp.b])
        P.add("pe", lambda e: e.transpose(pf[5].ap[:, 0:rows], tmp.ap[0:rows, :], identf[0:rows, 0:rows]), reads=[tmp.b, cf.b], writes=[pf[5].b])
        P.add("dve", lambda e: e.tensor_copy(out=dst, in_=pf[5].ap[:, 0:rows]), reads=[pf[5].b], writes=[])

    import os
    SKIP1 = int(os.environ.get('KSKIP1', '0'))
    Wx = AB.get("Wx", 8, 3072); load_w(Wx, w_in[:, COL_X:COL_X + 3072], 3072)
    Wdt = AB.get("Wdt", 8, 32); load_w(Wdt, w_in[:, COL_DT:COL_DT + 32], 32)
    cwt = AFa.get("cwt", 96); cbt = AFa.get("cbt", 24)
    per_channel(cwt.ap, scw, 96)
    per_channel(cbt.ap, scb, 24)
    cwt.b.writer = P.ops["dve"][-2]; cbt.b.writer = P.ops["dve"][-1]
    diag = AB.get("diag", 24, 4, 128)
    for j in range(24):
        for tp in range(4):
            P.add("dve", lambda e, j=j, tp=tp: e.tensor_scalar(out=diag.ap[:, j, tp, :], in0=identf, scalar1=cwt.ap[:, tp * 24 + j:tp * 24 + j + 1], scalar2=None, op0=ALU.mult), reads=[cf.b, cwt.b], writes=[diag.b])
    P.add("dve", lambda e: e.memset(H.ap, 0.0), writes=[H.b])
    mark1B, mark1F = AB.off, AFa.off
    xbA = [AB.get("xbA%d" % i, 1024) for i in range(2)]
    xT4 = [AB.get("xT4_%d" % i, 8, 512) for i in range(2)]
    raw4 = AB.get("raw4", 24, 516); xc4 = AB.get("xc4", 24, 512)
    rawb = [Buf("rawb%d" % j) for j in range(24)]; xcb = [Buf("xcb%d" % j) for j in range(24)]
    xtk = [AB.get("xtk%d" % i, 2560) for i in range(2)]
    xdtsA = AB.get("xdtsA", 512)
    smA = [AFa.get("smA%d" % i, 8, 32) for i in range(2)]
    P.add("pool", lambda e: e.memset(raw4.ap, 0.0), writes=rawb)

    def load_group(gi, buf):
        for q in range(4):
            c = gi * 4 + q
            xb_ = xbA[q % 2]
            dma("pool", xb_.ap, xpre[c * 128:(c + 1) * 128, :], [], [xb_.b])
            for k in range(8):
                P.add("pe", lambda e, k=k, xb_=xb_: e.transpose(pb[0].ap[:, k * 128:(k + 1) * 128], xb_.ap[:, k * 128:(k + 1) * 128], identb), reads=[xb_.b, cb16.b], writes=[pb[0].b])
            P.add("act", lambda e, q=q, buf=buf: e.copy(out=xT4[buf].ap[:, :, q * 128:(q + 1) * 128], in_=pb[0].ap.rearrange("p (a b) -> p a b", a=8)), reads=[pb[0].b], writes=[xT4[buf].b])

    def group_proj(gi, buf, last):
        ntile = 24 if last else 20
        for j in range(ntile):
            bank = pf[j % 2]
            chain(bank, bank.ap, [(Wx.ap[:, k, j * 128:(j + 1) * 128], xT4[buf].ap[:, k, :]) for k in range(8)], [Wx.b, xT4[buf].b])
            P.add("act", lambda e, j=j, bank=bank: e.copy(out=raw4.ap[:, j, 3:515], in_=bank.ap), reads=[bank.b], writes=[rawb[j]])
        for j in range(ntile):
            bank = pf[2 + j % 2]
            chain(bank, bank.ap, [(diag.ap[:, j, tp, :], raw4.ap[:, j, tp:tp + 512]) for tp in range(4)], [diag.b, rawb[j]])
            P.add("act", lambda e, j=j, bank=bank: e.activation(out=xc4.ap[:, j, :], in_=bank.ap, func=AF.Silu, bias=cbt.ap[:, j:j + 1], scale=1.0), reads=[bank.b, cbt.b], writes=[xcb[j]])
        P.add("pool", lambda e: e.tensor_copy(out=raw4.ap[:, :, 0:3], in_=raw4.ap[:, :, 512:515]), reads=rawb, writes=rawb)

    def group_chunks(gi, buf):
        for q in range(4):
            c = gi * 4 + q
            fcol = flg.ap[:, c:c + 1]
            xt = xtk[q % 2]; smq = smA[q % 2]
            for bt in range(3):
                nt = 8 if bt < 2 else 4
                for i in range(nt):
                    j = bt * 8 + i
                    P.add("pe", lambda e, i=i, j=j, q=q: e.transpose(pb[1].ap[:, i * 128:(i + 1) * 128], xc4.ap[:, j, q * 128:(q + 1) * 128], identb), reads=[xcb[j], cb16.b], writes=[pb[1].b])
                P.add("dve", lambda e, bt=bt, nt=nt, xt=xt: e.tensor_copy(out=xt.ap[:, bt * 1024:bt * 1024 + nt * 128], in_=pb[1].ap[:, 0:nt * 128]), reads=[pb[1].b], writes=[xt.b])
            chain(pf[4], pf[4].ap[:, 0:32], [(xT4[buf].ap[:, k, q * 128:(q + 1) * 128], Wdt.ap[:, k, :]) for k in range(8)], [xT4[buf].b, Wdt.b])
            S = lambda i, smq=smq: smq.ap[:, i, :]
            P.add("dve", lambda e, S=S: e.tensor_tensor(out=S(0), in0=pf[4].ap[:, 0:32], in1=dtb_bc, op=ALU.add), reads=[pf[4].b, smallp.b], writes=[smq.b])
            P.add("act", lambda e, S=S: e.activation(out=S(0), in_=S(0), func=AF.Exp), reads=[smq.b], writes=[smq.b])
            P.add("act", lambda e, S=S: e.activation(out=S(1), in_=S(0), func=AF.Ln, bias=onecol, scale=1.0), reads=[smq.b, cf.b], writes=[smq.b])
            P.add("dve", lambda e, S=S, fcol=fcol: e.tensor_scalar(out=S(1), in0=S(1), scalar1=fcol, scalar2=None, op0=ALU.mult), reads=[smq.b, flg.b], writes=[smq.b])
            P.add("dve", lambda e, S=S: e.tensor_tensor(out=S(2), in0=S(1), in1=a_bc, op=ALU.mult), reads=[smq.b, smallp.b], writes=[smq.b])
            P.add("pe", lambda e, S=S: e.matmul(pf[4].ap[:, 32:64], triU, S(2), start=True, stop=True), reads=[cf.b, smq.b], writes=[pf[4].b])
            P.add("pe", lambda e, S=S: e.matmul(pf[4].ap[:, 64:96], onesf, S(2), start=True, stop=True), reads=[cf.b, smq.b], writes=[pf[4].b])
            P.add("dve", lambda e, smq=smq: e.tensor_copy(out=smq.ap[:, 3:5, :], in_=pf[4].ap[:, 32:96].rearrange("p (a b) -> p a b", a=2)), reads=[pf[4].b], writes=[smq.b])
            P.add("dve", lambda e, S=S: e.tensor_tensor(out=S(6), in0=S(4), in1=S(3), op=ALU.subtract), reads=[smq.b], writes=[smq.b])
            P.add("act", lambda e, S=S: e.activation(out=S(6), in_=S(6), func=AF.Exp), reads=[smq.b], writes=[smq.b])
            P.add("act", lambda e, S=S: e.activation(out=S(7), in_=S(4), func=AF.Exp), reads=[smq.b], writes=[smq.b])
            P.add("dve", lambda e, S=S: e.tensor_tensor(out=S(6), in0=S(6), in1=S(1), op=ALU.mult), reads=[smq.b], writes=[smq.b])
            v3 = lambda ap: ap.rearrange("p (a b) -> p a b", a=8)
            for g in range(4):
                xg = xt.ap[:, g * 512:(g + 1) * 512].rearrange("p (a b) -> p a b", a=8)
                P.add("dve", lambda e, g=g, xg=xg, smq=smq: e.tensor_tensor(out=v3(xdtsA.ap), in0=xg, in1=bc(smq.ap[:, 6, 8 * g:8 * g + 8], 64), op=ALU.mult), reads=[xt.b, smq.b], writes=[xdtsA.b])
                P.add("pe", lambda e, g=g, xt=xt: e.matmul(pf[5].ap, xt.ap[:, 2048 + g * 128:2048 + (g + 1) * 128], xdtsA.ap, start=True, stop=True), reads=[xt.b, xdtsA.b], writes=[pf[5].b])
                Hg = H.ap[:, g * 512:(g + 1) * 512]
                P.add("dve", lambda e, g=g, Hg=Hg, smq=smq: e.tensor_tensor(out=v3(Hg), in0=v3(Hg), in1=bc(smq.ap[:, 7, 8 * g:8 * g + 8], 64), op=ALU.mult), reads=[H.b, smq.b], writes=[H.b])
                P.add("dve", lambda e, Hg=Hg: e.tensor_tensor(out=Hg, in0=Hg, in1=pf[5].ap, op=ALU.add), reads=[H.b, pf[5].b], writes=[H.b])

    NG = NPRE // 4
    g0 = NG - (npre + 3) // 4
    if not SKIP1 and g0 < NG:
        load_group(g0, g0 % 2)
        for gi in range(g0, NG):
            group_proj(gi, gi % 2, gi == NG - 1)
            if gi + 1 < NG:
                load_group(gi + 1, (gi + 1) % 2)
            group_chunks(gi, gi % 2)
        P.add("pool", lambda e: e.tensor_copy(out=rawh.ap[:, :, 0:3], in_=raw4.ap[:, :, 0:3]), reads=rawb, writes=[rawh.b])
    else:
        P.add("pool", lambda e: e.memset(rawh.ap, 0.0), writes=[rawh.b])

    P.barrier()
    AB.off, AFa.off = mark1B, mark1F
    Wz = AB.get("Wz", 8, 2048); load_w(Wz, w_in[:, COL_Z:COL_Z + 2048], 2048)
    nwb = AFa.get("nwb", 2048)
    dma("sp", nwb.ap, bcast_row(normw, 2048), [], [nwb.b])
    xb = AB.get("xb", 1024); xT = AB.get("xT", 8, 128)
    raw = AB.get("raw", 24, 132); xc = AB.get("xc", 24, 128)
    xtok = AB.get("xtok", 2560)
    xdt = AB.get("xdt", 512); xdts = AB.get("xdts", 512)
    Hb = AB.get("Hb", 2048); cbm = AB.get("cbm", 4, 128)
    LT = AB.get("LT", 4, 128); MT = AB.get("MT", 8, 128); yn = AB.get("yn", 2048)
    adtU = [AFa.get("adtU%d" % i, 128) for i in range(2)]
    sm = AFa.get("sm", 12, 32)
    yacc = AFa.get("yacc", 512); ytmp = AFa.get("ytmp", 512); sz = AFa.get("sz", 512)
    ssq = AFa.get("ssq", 4)
    P.add("pool", lambda e: e.memset(raw.ap, 0.0), writes=[raw.b])
    P.add("pool", lambda e: e.tensor_copy(out=raw.ap[:, :, 0:3], in_=rawh.ap[:, :, 0:3]), reads=[rawh.b], writes=[raw.b])

    CUT = int(os.environ.get('KCUT', '99')); NCH1 = int(os.environ.get('KNCH', str(NM1)))
    def ssd_chunk(xrow, fcol, main, ci):
        ntile = 24
        load_xT(xrow, xb, xT)
        if CUT <= 1: return
        for g in range(ntile // 4):
            bank = pf[g % 2]
            for jj in range(4):
                j = 4 * g + jj
                chain(bank, bank.ap[:, jj * 128:(jj + 1) * 128], [(Wx.ap[:, k, j * 128:(j + 1) * 128], xT.ap[:, k, :]) for k in range(8)], [Wx.b, xT.b])
            P.add("act", lambda e, g=g, bank=bank: e.copy(out=raw.ap[:, 4 * g:4 * g + 4, 3:131], in_=bank.ap.rearrange("p (a b) -> p a b", a=4)), reads=[bank.b], writes=[raw.b])
        if CUT <= 2: return
        for g in range(ntile // 4):
            bank = pf[2 + g % 2]
            for jj in range(4):
                j = 4 * g + jj
                chain(bank, bank.ap[:, jj * 128:(jj + 1) * 128], [(diag.ap[:, j, tp, :], raw.ap[:, j, tp:tp + 128]) for tp in range(4)], [diag.b, raw.b])
                P.add("act", lambda e, j=j, jj=jj, bank=bank: e.activation(out=xc.ap[:, j, :], in_=bank.ap[:, jj * 128:(jj + 1) * 128], func=AF.Silu, bias=cbt.ap[:, j:j + 1], scale=1.0), reads=[bank.b, cbt.b], writes=[xc.b])
        if CUT <= 3: return
        P.add("pool", lambda e: e.tensor_copy(out=raw.ap[:, :, 0:3], in_=raw.ap[:, :, 128:131]), reads=[raw.b], writes=[raw.b])
        for bt in range(3):
            nt = 8 if bt < 2 else 4
            for i in range(nt):
                j = bt * 8 + i
                P.add("pe", lambda e, i=i, j=j: e.transpose(pb[1].ap[:, i * 128:(i + 1) * 128], xc.ap[:, j, :], identb), reads=[xc.b, cb16.b], writes=[pb[1].b])
            P.add("dve", lambda e, bt=bt, nt=nt: e.tensor_copy(out=xtok.ap[:, bt * 1024:bt * 1024 + nt * 128], in_=pb[1].ap[:, 0:nt * 128]), reads=[pb[1].b], writes=[xtok.b])
        if CUT <= 4: return
        chain(pf[4], pf[4].ap[:, 0:32], [(xT.ap[:, k, :], Wdt.ap[:, k, :]) for k in range(8)], [xT.b, Wdt.b])
        S = lambda i: sm.ap[:, i, :]
        P.add("dve", lambda e: e.tensor_tensor(out=S(0), in0=pf[4].ap[:, 0:32], in1=dtb_bc, op=ALU.add), reads=[pf[4].b, smallp.b], writes=[sm.b])
        P.add("act", lambda e: e.activation(out=S(0), in_=S(0), func=AF.Exp), reads=[sm.b], writes=[sm.b])
        P.add("act", lambda e: e.activation(out=S(1), in_=S(0), func=AF.Ln, bias=onecol, scale=1.0), reads=[sm.b, cf.b], writes=[sm.b])
        P.add("dve", lambda e: e.tensor_scalar(out=S(1), in0=S(1), scalar1=fcol, scalar2=None, op0=ALU.mult), reads=[sm.b, flg.b], writes=[sm.b])
        P.add("dve", lambda e: e.tensor_tensor(out=S(2), in0=S(1), in1=a_bc, op=ALU.mult), reads=[sm.b, smallp.b], writes=[sm.b])
        if CUT <= 5: return
        P.add("pe", lambda e: e.matmul(pf[4].ap[:, 32:64], triU, S(2), start=True, stop=True), reads=[cf.b, sm.b], writes=[pf[4].b])
        P.add("pe", lambda e: e.matmul(pf[4].ap[:, 64:96], onesf, S(2), start=True, stop=True), reads=[cf.b, sm.b], writes=[pf[4].b])
        P.add("dve", lambda e: e.tensor_copy(out=sm.ap[:, 3:5, :], in_=pf[4].ap[:, 32:96].rearrange("p (a b) -> p a b", a=2)), reads=[pf[4].b], writes=[sm.b])
        if CUT <= 6: return
        if main and CUT > 7:
            for g in range(4):
                P.add("pe", lambda e, g=g: e.matmul(pf[5].ap[:, g * 128:(g + 1) * 128], xc.ap[:, 16 + g, :], xc.ap[:, 20 + g, :], start=True, stop=True), reads=[xc.b], writes=[pf[5].b])
            P.add("dve", lambda e: e.tensor_tensor(out=cbm.ap, in0=pf[5].ap.rearrange("p (a b) -> p a b", a=4), in1=maskb.unsqueeze(1).to_broadcast([128, 4, 128]), op=ALU.mult), reads=[pf[5].b, cb16.b], writes=[cbm.b])
            P.add("pool", lambda e: e.tensor_copy(out=Hb.ap, in_=H.ap), reads=[H.b], writes=[Hb.b])
            P.add("act", lambda e: e.activation(out=S(5), in_=S(3), func=AF.Exp), reads=[sm.b], writes=[sm.b])
            for g in range(4):
                xg = xtok.ap[:, g * 512:(g + 1) * 512].rearrange("p (a b) -> p a b", a=8)
                P.add("dve", lambda e, g=g, xg=xg: e.tensor_tensor(out=xdt.ap.rearrange("p (a b) -> p a b", a=8), in0=xg, in1=bc(sm.ap[:, 1, 8 * g:8 * g + 8], 64), op=ALU.mult), reads=[xtok.b, sm.b], writes=[xdt.b])
                for hh in range(2):
                    bank = pf[hh]
                    for h4 in range(4):
                        h = 8 * g + 4 * hh + h4
                        au = adtU[h4 % 2]
                        P.add("dve", lambda e, h=h, au=au: e.tensor_scalar(out=au.ap, in0=Ustr, scalar1=sm.ap[:, 2, h:h + 1], scalar2=None, op0=ALU.mult), reads=[cf.b, sm.b], writes=[au.b])
                        P.add("pe", lambda e, h4=h4, au=au, bank=bank: e.matmul(bank.ap[:, h4 * 128:(h4 + 1) * 128], au.ap, triU, start=True, stop=True), reads=[au.b, cf.b], writes=[bank.b])
                    P.add("act", lambda e, bank=bank: e.activation(out=LT.ap.rearrange("p a b -> p (a b)"), in_=bank.ap, func=AF.Exp), reads=[bank.b], writes=[LT.b])
                    P.add("dve", lambda e, g=g, hh=hh: e.tensor_tensor(out=MT.ap[:, 4 * hh:4 * hh + 4, :], in0=LT.ap, in1=cbm.ap[:, g, :].unsqueeze(1).to_broadcast([128, 4, 128]), op=ALU.mult), reads=[LT.b, cbm.b], writes=[MT.b])
                for h8 in range(8):
                    P.add("pe", lambda e, h8=h8: e.matmul(pf[2].ap[:, h8 * 64:(h8 + 1) * 64], MT.ap[:, h8, :], xdt.ap[:, h8 * 64:(h8 + 1) * 64], start=True, stop=True), reads=[MT.b, xdt.b], writes=[pf[2].b])
                P.add("pe", lambda e, g=g: e.matmul(pf[3].ap, xc.ap[:, 20 + g, :], Hb.ap[:, g * 512:(g + 1) * 512], start=True, stop=True), reads=[xc.b, Hb.b], writes=[pf[3].b])
                v3 = lambda ap: ap.rearrange("p (a b) -> p a b", a=8)
                P.add("dve", lambda e, g=g: e.tensor_tensor(out=v3(yacc.ap), in0=v3(pf[3].ap), in1=bc(sm.ap[:, 5, 8 * g:8 * g + 8], 64), op=ALU.mult), reads=[pf[3].b, sm.b], writes=[yacc.b])
                P.add("dve", lambda e: e.tensor_tensor(out=yacc.ap, in0=yacc.ap, in1=pf[2].ap, op=ALU.add), reads=[pf[2].b, yacc.b], writes=[yacc.b])
                P.add("dve", lambda e, g=g, xg=xg: e.tensor_tensor(out=v3(ytmp.ap), in0=xg, in1=bc(D_bc[:, 8 * g:8 * g + 8], 64), op=ALU.mult), reads=[xtok.b, smallp.b], writes=[ytmp.b])
                P.add("dve", lambda e: e.tensor_tensor(out=yacc.ap, in0=yacc.ap, in1=ytmp.ap, op=ALU.add), reads=[ytmp.b, yacc.b], writes=[yacc.b])
                chain(pf[5], pf[5].ap, [(xT.ap[:, k, :], Wz.ap[:, k, g * 512:(g + 1) * 512]) for k in range(8)], [xT.b, Wz.b])
                P.add("act", lambda e: e.activation(out=sz.ap, in_=pf[5].ap, func=AF.Silu), reads=[pf[5].b], writes=[sz.b])
                P.add("dve", lambda e: e.tensor_tensor(out=yacc.ap, in0=yacc.ap, in1=sz.ap, op=ALU.mult), reads=[sz.b, yacc.b], writes=[yacc.b])
                P.add("act", lambda e, g=g: e.activation(out=ytmp.ap, in_=yacc.ap, func=AF.Square, accum_out=ssq.ap[:, g:g + 1]), reads=[yacc.b], writes=[ytmp.b, ssq.b])
                P.add("dve", lambda e, g=g: e.tensor_scalar(out=ssq.ap[:, g:g + 1], in0=ssq.ap[:, g:g + 1], scalar1=1.0 / 512, scalar2=1e-5, op0=ALU.mult, op1=ALU.add), reads=[ssq.b], writes=[ssq.b])
                P.add("act", lambda e, g=g: e.activation(out=ssq.ap[:, g:g + 1], in_=ssq.ap[:, g:g + 1], func=AF.Sqrt), reads=[ssq.b], writes=[ssq.b])
                P.add("dve", lambda e, g=g: e.reciprocal(out=ssq.ap[:, g:g + 1], in_=ssq.ap[:, g:g + 1]), reads=[ssq.b], writes=[ssq.b])
                P.add("dve", lambda e, g=g: e.scalar_tensor_tensor(out=yn.ap[:, g * 512:(g + 1) * 512], in0=yacc.ap, scalar=ssq.ap[:, g:g + 1], in1=nwb.ap[:, g * 512:(g + 1) * 512], op0=ALU.mult, op1=ALU.mult), reads=[yacc.b, ssq.b, nwb.b], writes=[yn.b])
            dma("sp", ynd[ci * 128:(ci + 1) * 128, :], yn.ap, [yn.b], [])
        if CUT <= 8: return
        P.add("dve", lambda e: e.tensor_tensor(out=S(6), in0=S(4), in1=S(3), op=ALU.subtract), reads=[sm.b], writes=[sm.b])
        P.add("act", lambda e: e.activation(out=S(6), in_=S(6), func=AF.Exp), reads=[sm.b], writes=[sm.b])
        P.add("act", lambda e: e.activation(out=S(7), in_=S(4), func=AF.Exp), reads=[sm.b], writes=[sm.b])
        P.add("dve", lambda e: e.tensor_tensor(out=S(6), in0=S(6), in1=S(1), op=ALU.mult), reads=[sm.b], writes=[sm.b])
        for g in range(4):
            xg = xtok.ap[:, g * 512:(g + 1) * 512].rearrange("p (a b) -> p a b", a=8)
            v3 = lambda ap: ap.rearrange("p (a b) -> p a b", a=8)
            P.add("dve", lambda e, g=g, xg=xg: e.tensor_tensor(out=v3(xdts.ap), in0=xg, in1=bc(sm.ap[:, 6, 8 * g:8 * g + 8], 64), op=ALU.mult), reads=[xtok.b, sm.b], writes=[xdts.b])
            bank = pf[2 + g % 2]
            P.add("pe", lambda e, g=g, bank=bank: e.matmul(bank.ap, xtok.ap[:, 2048 + g * 128:2048 + (g + 1) * 128], xdts.ap, start=True, stop=True), reads=[xtok.b, xdts.b], writes=[bank.b])
            Hg = H.ap[:, g * 512:(g + 1) * 512]
            P.add("dve", lambda e, g=g, Hg=Hg: e.tensor_tensor(out=v3(Hg), in0=v3(Hg), in1=bc(sm.ap[:, 7, 8 * g:8 * g + 8], 64), op=ALU.mult), reads=[H.b, sm.b], writes=[H.b])
            P.add("dve", lambda e, Hg=Hg, bank=bank: e.tensor_tensor(out=Hg, in0=Hg, in1=bank.ap, op=ALU.add), reads=[H.b, bank.b], writes=[H.b])

    for ci in range(0 if SKIP1 else NCH1):
        ssd_chunk(xmain[(ci + 1) * 128:(ci + 2) * 128, :], flg.ap[:, NPRE + ci:NPRE + ci + 1], True, ci)

    if stop < 2:
        P.emit(nc); es.close(); return nc
    P.barrier()
    AB.off, AFa.off = markB, markF
    Wq = AB.get("Wq", 8, 1024); load_w(Wq, w_in[:, COL_Q:COL_Q + 1024], 1024)
    Wk2 = AB.get("Wk2", 8, 128); load_w(Wk2, w_in[:, COL_K:COL_K + 128], 128)
    Wv = AB.get("Wv", 8, 128); load_w(Wv, w_in[:, COL_V:COL_V + 128], 128)
    EB = AB.get("EB", 2, 16, 128)
    ebf = AFa.get("ebf", 2048); mkf = AFa.get("mkf", 2048)
    for kt in range(2):
        dma("sp", ebf.ap, biasg[:, kt * 2048:(kt + 1) * 2048], [], [ebf.b])
        dma("sp", mkf.ap, maskg[:, kt * 2048:(kt + 1) * 2048], [], [mkf.b])
        P.add("act", lambda e: e.activation(out=ebf.ap, in_=ebf.ap, func=AF.Exp), reads=[ebf.b], writes=[ebf.b])
        P.add("dve", lambda e, kt=kt: e.tensor_tensor(out=EB.ap[:, kt, :, :].rearrange("p a b -> p (a b)"), in0=ebf.ap, in1=mkf.ap, op=ALU.mult), reads=[ebf.b, mkf.b], writes=[EB.b])
    xb = AB.get("xb2", 1024); xT = AB.get("xT2", 8, 128)
    kT = [AB.get("kT%d" % i, 2, 128) for i in range(2)]
    vx = [AB.get("vx%d" % i, 2, 65) for i in range(2)]
    for i in range(2):
        P.add("pool", lambda e, i=i: e.memset(vx[i].ap, 1.0), writes=[vx[i].b])
    qT = AB.get("qT", 16, 128); et = AB.get("et", 4, 128)
    PT = [AB.get("PT%d" % i, 4, 128) for i in range(2)]
    ya = AB.get("ya", 1024)
    den = AFa.get("den", 4)
    flag0 = flg.ap[:, NPRE:NPRE + 1]
    CUT2 = int(os.environ.get('KCUT2', '99')); NCH2 = int(os.environ.get('KNCH2', str(NM2)))
    for ci in range(NCH2):
        sl = ci % 2
        load_xT(xmain[ci * 128:(ci + 1) * 128, :], xb, xT)
        for kv in range(2):
            chain(pf[0], pf[0].ap[0:64, kv * 128:(kv + 1) * 128], [(Wk2.ap[:, k, kv * 64:(kv + 1) * 64], xT.ap[:, k, :]) for k in range(8)], [Wk2.b, xT.b])
        P.add("act", lambda e, sl=sl: e.copy(out=kT[sl].ap[0:64, :, :], in_=pf[0].ap[0:64, 0:256].rearrange("p (a b) -> p a b", a=2)), reads=[pf[0].b], writes=[kT[sl].b])
        chain(pf[1], pf[1].ap[:, 0:128], [(xT.ap[:, k, :], Wv.ap[:, k, :]) for k in range(8)], [xT.b, Wv.b])
        P.add("dve", lambda e, sl=sl: e.tensor_copy(out=vx[sl].ap[:, :, 0:64], in_=pf[1].ap[:, 0:128].rearrange("p (a b) -> p a b", a=2)), reads=[pf[1].b], writes=[vx[sl].b])
        if ci == 0 or CUT2 <= 1:
            continue
        for q4 in range(4):
            bank = pf[2 + q4 % 2]
            for tt in range(4):
                j = q4 * 4 + tt
                chain(bank, bank.ap[0:64, tt * 128:(tt + 1) * 128], [(Wq.ap[:, k, j * 64:(j + 1) * 64], xT.ap[:, k, :]) for k in range(8)], [Wq.b, xT.b])
            P.add("act", lambda e, q4=q4, bank=bank: e.copy(out=qT.ap[0:64, q4 * 4:q4 * 4 + 4, :], in_=bank.ap[0:64, :].rearrange("p (a b) -> p a b", a=4)), reads=[bank.b], writes=[qT.b])
        for kvh in range(2):
            if CUT2 <= 2: break
            for hb in range(2):
                j0 = kvh * 8 + hb * 4
                for kt in range(2):
                    slk = (ci + 1 + kt) % 2
                    bank = pf[4 + kt]
                    for i in range(4):
                        j = j0 + i
                        base = (j % 2) * 64 * int(os.environ.get("KB64", "1"))
                        P.add("pe", lambda e, i=i, j=j, base=base, slk=slk, bank=bank, kvh=kvh: e.matmul(bank.ap[:, i * 128:(i + 1) * 128], kT[slk].ap[0:64, kvh, :], qT.ap[0:64, j, :], start=True, stop=True), reads=[kT[slk].b, qT.b], writes=[bank.b])
                    P.add("act", lambda e, bank=bank: e.activation(out=et.ap.rearrange("p a b -> p (a b)"), in_=bank.ap, func=AF.Exp, scale=0.125), reads=[bank.b], writes=[et.b])
                    if ci == 2 and kt == 0:
                        P.add("dve", lambda e, kt=kt, j0=j0: e.scalar_tensor_tensor(out=PT[kt].ap, in0=et.ap, scalar=flag0, in1=EB.ap[:, kt, j0:j0 + 4, :], op0=ALU.mult, op1=ALU.mult), reads=[et.b, EB.b, flg.b], writes=[PT[kt].b])
                    else:
                        P.add("dve", lambda e, kt=kt, j0=j0: e.tensor_tensor(out=PT[kt].ap, in0=et.ap, in1=EB.ap[:, kt, j0:j0 + 4, :], op=ALU.mult), reads=[et.b, EB.b], writes=[PT[kt].b])
                if CUT2 <= 3: continue
                bank = pf[hb]
                for i in range(4):
                    for kt in range(2):
                        slk = (ci + 1 + kt) % 2
                        P.add("pe", lambda e, i=i, kt=kt, slk=slk, bank=bank, kvh=kvh: e.matmul(bank.ap[:, i * 65:(i + 1) * 65], PT[kt].ap[:, i, :], vx[slk].ap[:, kvh, :], start=(kt == 0), stop=(kt == 1)), reads=[PT[kt].b, vx[slk].b], writes=[bank.b])
                if CUT2 <= 4: continue
                pv = bank.ap[:, 0:260].rearrange("p (a b) -> p a b", a=4)
                P.add("dve", lambda e, pv=pv, j0=j0: e.tensor_tensor(out=den.ap, in0=pv[:, :, 64], in1=esink[:, j0:j0 + 4], op=ALU.add), reads=[bank.b, smallp.b], writes=[den.b])
                P.add("dve", lambda e: e.reciprocal(out=den.ap, in_=den.ap), reads=[den.b], writes=[den.b])
                P.add("dve", lambda e, pv=pv, j0=j0: e.tensor_tensor(out=ya.ap[:, j0 * 64:(j0 + 4) * 64].rearrange("p (a b) -> p a b", a=4), in0=pv[:, :, 0:64], in1=bc(den.ap, 64), op=ALU.mult), reads=[bank.b, den.b], writes=[ya.b])
        dma("sp", yad[(ci - 1) * 128:ci * 128, :], ya.ap, [ya.b], [])

    if stop < 3:
        P.emit(nc); es.close(); return nc
    P.barrier()
    AB.off, AFa.off = markB, markF
    Wg = AB.get("Wg", 8, 2048); load_w(Wg, w_in[:, COL_G:COL_G + 2048], 2048)
    Wbs = AB.get("Wbs", 16, 1024); load_w(Wbs, w_bs, 1024)
    Wba = AB.get("Wba", 8, 1024); load_w(Wba, w_ba, 1024)
    Wmx = AB.get("Wmx", 8, 1024); load_w(Wmx, w_mix, 1024)
    bgb = AFa.get("bgb", 2048); dma("sp", bgb.ap, bcast_row(b_gate, 2048), [], [bgb.b])
    lng = AFa.get("lng", 2, 1024)
    dma("sp", lng.ap[:, 0, :], bcast_row(ln1g, 1024), [], [lng.b]); dma("sp", lng.ap[:, 1, :], bcast_row(ln1b, 1024), [], [lng.b])
    xb = AB.get("xb3", 1024); xT = AB.get("xT3", 8, 128)
    ynb = AB.get("ynb", 2048); yab = AB.get("yab", 1024)
    ynT = AB.get("ynT", 16, 128); yaT = AB.get("yaT", 8, 128)
    mg = AB.get("mg", 1024); mT = AB.get("mT", 8, 128)
    xf = AFa.get("xf", 1024); gt = AFa.get("gt", 2048); m1 = AFa.get("m1", 512); r = AFa.get("r", 1024)
    st = AFa.get("st", 2, 6); mv = AFa.get("mv", 2)

    def transp(src, dst, ntl):
        for bt in range(ntl // 8):
            for i in range(8):
                j = bt * 8 + i
                P.add("pe", lambda e, i=i, j=j: e.transpose(pb[1].ap[:, i * 128:(i + 1) * 128], src.ap[:, j * 128:(j + 1) * 128], identb), reads=[src.b, cb16.b], writes=[pb[1].b])
            P.add("act", lambda e, bt=bt: e.copy(out=dst.ap[:, bt * 8:bt * 8 + 8, :].rearrange("p a b -> p (a b)"), in_=pb[1].ap), reads=[pb[1].b], writes=[dst.b])

    def layer_norm(r, g_ap, b_ap, gb, st, mv):
        for i in range(2):
            P.add("dve", lambda e, i=i: e.bn_stats(out=st.ap[:, i, :], in_=r.ap[:, i * 512:(i + 1) * 512]), reads=[r.b], writes=[st.b])
        P.add("dve", lambda e: e.bn_aggr(out=mv.ap, in_=st.ap.rearrange("p a b -> p (a b)")), reads=[st.b], writes=[mv.b])
        P.add("dve", lambda e: e.tensor_scalar(out=mv.ap[:, 1:2], in0=mv.ap[:, 1:2], scalar1=1e-5, scalar2=None, op0=ALU.add), reads=[mv.b], writes=[mv.b])
        P.add("act", lambda e: e.activation(out=mv.ap[:, 1:2], in_=mv.ap[:, 1:2], func=AF.Sqrt), reads=[mv.b], writes=[mv.b])
        P.add("dve", lambda e: e.reciprocal(out=mv.ap[:, 1:2], in_=mv.ap[:, 1:2]), reads=[mv.b], writes=[mv.b])
        P.add("dve", lambda e: e.tensor_scalar(out=r.ap, in0=r.ap, scalar1=mv.ap[:, 0:1], scalar2=mv.ap[:, 1:2], op0=ALU.subtract, op1=ALU.mult), reads=[r.b, mv.b], writes=[r.b])
        P.add("dve", lambda e: e.tensor_tensor(out=r.ap, in0=r.ap, in1=g_ap, op=ALU.mult), reads=[r.b, gb], writes=[r.b])
        P.add("dve", lambda e: e.tensor_tensor(out=r.ap, in0=r.ap, in1=b_ap, op=ALU.add), reads=[r.b, gb], writes=[r.b])

    for ci in range(NM1):
        xrow = xmain[(ci + 1) * 128:(ci + 2) * 128, :]
        load_xT(xrow, xb, xT)
        dma("sp", xf.ap, xrow, [], [xf.b])
        dma("sp", ynb.ap, ynd[ci * 128:(ci + 1) * 128, :], [], [ynb.b])
        dma("sp", yab.ap, yad[ci * 128:(ci + 1) * 128, :], [], [yab.b])
        transp(ynb, ynT, 16)
        transp(yab, yaT, 8)
        for s4 in range(4):
            bank = pf[s4 % 2]
            chain(bank, bank.ap, [(xT.ap[:, k, :], Wg.ap[:, k, s4 * 512:(s4 + 1) * 512]) for k in range(8)], [xT.b, Wg.b])
            P.add("dve", lambda e, s4=s4, bank=bank: e.tensor_tensor(out=gt.ap[:, s4 * 512:(s4 + 1) * 512], in0=bank.ap, in1=bgb.ap[:, s4 * 512:(s4 + 1) * 512], op=ALU.add), reads=[bank.b, bgb.b], writes=[gt.b])
        P.add("act", lambda e: e.activation(out=gt.ap, in_=gt.ap, func=AF.Sigmoid), reads=[gt.b], writes=[gt.b])
        for hf in range(2):
            chain(pf[2], pf[2].ap, [(ynT.ap[:, i, :], Wbs.ap[:, i, hf * 512:(hf + 1) * 512]) for i in range(16)], [ynT.b, Wbs.b])
            chain(pf[3], pf[3].ap, [(yaT.ap[:, i, :], Wba.ap[:, i, hf * 512:(hf + 1) * 512]) for i in range(8)], [yaT.b, Wba.b])
            P.add("dve", lambda e, hf=hf: e.tensor_tensor(out=m1.ap, in0=pf[2].ap, in1=gt.ap[:, hf * 512:(hf + 1) * 512], op=ALU.mult), reads=[pf[2].b, gt.b], writes=[m1.b])
            P.add("dve", lambda e, hf=hf: e.tensor_tensor(out=r.ap[:, hf * 512:(hf + 1) * 512], in0=pf[3].ap, in1=gt.ap[:, 1024 + hf * 512:1024 + (hf + 1) * 512], op=ALU.mult), reads=[pf[3].b, gt.b], writes=[r.b])
            P.add("dve", lambda e, hf=hf: e.tensor_tensor(out=mg.ap[:, hf * 512:(hf + 1) * 512], in0=m1.ap, in1=r.ap[:, hf * 512:(hf + 1) * 512], op=ALU.add), reads=[m1.b, r.b], writes=[mg.b])
        transp(mg, mT, 8)
        for hf in range(2):
            bank = pf[4 + hf]
            chain(bank, bank.ap, [(mT.ap[:, i, :], Wmx.ap[:, i, hf * 512:(hf + 1) * 512]) for i in range(8)], [mT.b, Wmx.b])
            P.add("dve", lambda e, hf=hf, bank=bank: e.scalar_tensor_tensor(out=r.ap[:, hf * 512:(hf + 1) * 512], in0=xf.ap[:, hf * 512:(hf + 1) * 512], scalar=ALPHA, in1=bank.ap, op0=ALU.mult, op1=ALU.add), reads=[xf.b, bank.b], writes=[r.b])
        layer_norm(r, lng.ap[:, 0, :], lng.ap[:, 1, :], lng.b, st, mv)
        if ci == 0:
            P.add("dve", lambda e: e.tensor_scalar(out=r.ap, in0=r.ap, scalar1=flag0, scalar2=None, op0=ALU.mult), reads=[r.b, flg.b], writes=[r.b])
        dma("sp", h1d[ci * 128:(ci + 1) * 128, :], r.ap, [r.b], [])

    if stop < 4:
        P.emit(nc); es.close(); return nc
    P.barrier()
    AB.off, AFa.off = markB, markF
    Wup = AB.get("Wup", 8, 5632); load_w(Wup, w_up, 5632)
    Wdn = AB.get("Wdn", 22, 1024); load_w(Wdn, w_dn, 1024)
    fw = AFa.get("fw", 132); fb = AFa.get("fb", 44)
    per_channel(fw.ap[:, 0:88], fcw[0:88, :], 88); fw.b.writer = P.ops["dve"][-1]
    per_channel(fw.ap[:, 88:132], fcw[88:132, :], 44); fw.b.writer = P.ops["dve"][-1]
    per_channel(fb.ap, fcb, 44); fb.b.writer = P.ops["dve"][-1]
    lng2 = AFa.get("lng2", 2, 1024)
    dma("sp", lng2.ap[:, 0, :], bcast_row(ln2g, 1024), [], [lng2.b]); dma("sp", lng2.ap[:, 1, :], bcast_row(ln2b, 1024), [], [lng2.b])
    hA = AB.get("hA", 1024); hB = AB.get("hB", 1024); h1T = AB.get("h1T", 8, 132)
    aT = AB.get("aT", 22, 128)
    cv = AFa.get("cv", 44, 128); t0 = AFa.get("t0", 128); sg = AFa.get("sg", 128)
    hr = AFa.get("hr", 1024); r4 = AFa.get("r4", 1024)
    st4 = AFa.get("st4", 2, 6); mv4 = AFa.get("mv4", 2)
    for c in range(16):
        r0 = 128 + c * 128
        dma("pool", hA.ap, h1d[r0 - 2:r0 + 126, :], [], [hA.b])
        dma("pool", hB.ap[0:2, :], h1d[r0 + 126:r0 + 128, :], [], [hB.b])
        dma("sp", hr.ap, h1d[r0:r0 + 128, :], [], [hr.b])
        for k in range(8):
            P.add("pe", lambda e, k=k: e.transpose(pb[0].ap[:, k * 128:(k + 1) * 128], hA.ap[:, k * 128:(k + 1) * 128], identb), reads=[hA.b, cb16.b], writes=[pb[0].b])
        P.add("act", lambda e: e.copy(out=h1T.ap[:, :, 0:128], in_=pb[0].ap.rearrange("p (a b) -> p a b", a=8)), reads=[pb[0].b], writes=[h1T.b])
        for k in range(8):
            P.add("pe", lambda e, k=k: e.transpose(pb[1].ap[:, k * 2:(k + 1) * 2], hB.ap[0:2, k * 128:(k + 1) * 128], identb[0:2, 0:2]), reads=[hB.b, cb16.b], writes=[pb[1].b])
        P.add("act", lambda e: e.copy(out=h1T.ap[:, :, 128:130], in_=pb[1].ap[:, 0:16].rearrange("p (a b) -> p a b", a=8)), reads=[pb[1].b], writes=[h1T.b])
        for j in range(44):
            bank = pf[j % 4]
            chain(bank, bank.ap[:, 0:130], [(Wup.ap[:, k, j * 128:(j + 1) * 128], h1T.ap[:, k, 0:130]) for k in range(8)], [Wup.b, h1T.b])
            P.add("act", lambda e, j=j, bank=bank: e.activation(out=t0.ap, in_=bank.ap[:, 2:130], func=AF.Identity, bias=fb.ap[:, j:j + 1], scale=fw.ap[:, 88 + j:89 + j]), reads=[bank.b, fw.b, fb.b], writes=[t0.b])
            P.add("dve", lambda e, j=j, bank=bank: e.scalar_tensor_tensor(out=t0.ap, in0=bank.ap[:, 1:129], scalar=fw.ap[:, 44 + j:45 + j], in1=t0.ap, op0=ALU.mult, op1=ALU.add), reads=[bank.b, fw.b, t0.b], writes=[t0.b])
            P.add("dve", lambda e, j=j, bank=bank: e.scalar_tensor_tensor(out=cv.ap[:, j, :], in0=bank.ap[:, 0:128], scalar=fw.ap[:, j:j + 1], in1=t0.ap, op0=ALU.mult, op1=ALU.add), reads=[bank.b, fw.b, t0.b], writes=[cv.b])
        for j in range(22):
            P.add("act", lambda e, j=j: e.activation(out=sg.ap, in_=cv.ap[:, j, :], func=AF.Silu), reads=[cv.b], writes=[sg.b])
            P.add("dve", lambda e, j=j: e.tensor_tensor(out=aT.ap[:, j, :], in0=sg.ap, in1=cv.ap[:, 22 + j, :], op=ALU.mult), reads=[sg.b, cv.b], writes=[aT.b])
        for hf in range(2):
            bank = pf[4 + hf]
            chain(bank, bank.ap, [(aT.ap[:, j, :], Wdn.ap[:, j, hf * 512:(hf + 1) * 512]) for j in range(22)], [aT.b, Wdn.b])
            P.add("dve", lambda e, hf=hf, bank=bank: e.scalar_tensor_tensor(out=r4.ap[:, hf * 512:(hf + 1) * 512], in0=hr.ap[:, hf * 512:(hf + 1) * 512], scalar=ALPHA, in1=bank.ap, op0=ALU.mult, op1=ALU.add), reads=[hr.b, bank.b], writes=[r4.b])
        layer_norm(r4, lng2.ap[:, 0, :], lng2.ap[:, 1, :], lng2.b, st4, mv4)
        dma("sp", out[c * 128:(c + 1) * 128, :], r4.ap, [r4.b], [])

    P.emit(nc)
    es.close()
    return nc


def rel_bucket_np(rel):
    n = np.maximum(rel, 0)
    nf = np.maximum(n, 1).astype(np.float32)
    large = 16 + (np.log(nf / np.float32(16)) / np.float32(np.log(128 / 16)) * np.float32(16)).astype(np.int32)
    large = np.minimum(large, 31)
    return np.where(n < 16, n, large)


_NC = None


def kernel(_dbg=None, **inp):
    global _NC
    x = np.asarray(inp["x"], np.float32)[0]
    f = lambda k: np.ascontiguousarray(np.asarray(inp[k], np.float32)[0])
    common = {
        "w_in": f("w_in"), "b_gate": f("b_gate")[None], "dtb": f("ssm_dt_bias")[None], "alog": f("ssm_a_log")[None],
        "dsk": f("ssm_d")[None], "normw": f("ssm_norm_w")[None], "sinks": f("attn_sinks")[None],
        "w_bs": f("w_branch_ssm"), "w_ba": f("w_branch_attn"), "w_mix": f("w_mix_out"),
        "ln1g": f("ln1_g")[None], "ln1b": f("ln1_b")[None], "ln2g": f("ln2_g")[None], "ln2b": f("ln2_b")[None],
        "w_up": f("w_up"), "w_dn": f("w_down"),
    }
    scw = f("ssm_conv_w")
    common["scw"] = np.ascontiguousarray(scw.reshape(4 * 24, 128))
    common["scb"] = np.ascontiguousarray(f("ssm_conv_b").reshape(24, 128))
    common["fcw"] = np.ascontiguousarray(f("ffn_conv_w").reshape(3 * 44, 128))
    common["fcb"] = np.ascontiguousarray(f("ffn_conv_b").reshape(44, 128))
    s = np.arange(128)
    ident = np.eye(128, dtype=np.float32)
    triU = (s[:, None] <= s[None, :]).astype(np.float32)
    ustr = (s[:, None] > s[None, :]).astype(np.float32)
    common["cst"] = np.ascontiguousarray(np.concatenate([ident, triU, ustr, np.ones((128, 128), np.float32)], axis=1))
    rb = np.asarray(inp["rel_bias"], np.float32)
    bg = np.zeros((128, 2, 16, 128), np.float32); mk = np.zeros((128, 2, 16, 128), np.float32)
    for kt in range(2):
        rel = (s[None, :] + 128) - (s[:, None] + 128 * kt)
        valid = (rel >= 0) & (rel < 128)
        bidx = rel_bucket_np(rel)
        g = rb[bidx]
        bg[:, kt] = np.transpose(g, (0, 2, 1))
        mk[:, kt] = np.broadcast_to(valid[:, None, :], (128, 16, 128))
    common["biasg"] = np.ascontiguousarray(bg.reshape(128, -1)); common["maskg"] = np.ascontiguousarray(mk.reshape(128, -1))
    in_maps = []
    for c in range(NCORE):
        S = c * TOK
        lo = S - 128 - NPRE * 128
        xp = np.zeros((NPRE * 128, 1024), np.float32)
        if S - 128 > 0:
            src_lo = max(lo, 0)
            xp[src_lo - lo:] = x[src_lo:S - 128]
        xm = np.zeros((NM2 * 128, 1024), np.float32)
        lo2 = S - 256
        src_lo = max(lo2, 0)
        xm[src_lo - lo2:] = x[src_lo:S + TOK]
        fl = np.zeros((128, NPRE + NM1), np.float32)
        for i in range(NPRE):
            fl[:, i] = 1.0 if lo + i * 128 >= 0 else 0.0
        fl[:, NPRE] = 1.0 if c > 0 else 0.0
        fl[:, NPRE + 1:] = 1.0
        m = dict(common); m["xpre"] = xp; m["xmain"] = xm; m["pflag"] = fl
        in_maps.append(m)
    if _dbg is not None:
        return in_maps
    if _NC is None:
        _NC = build()
    res = run_bass_kernel_spmd(_NC, in_maps, core_ids=list(range(NCORE)))
    o = np.concatenate([res.results[c]["out"] for c in range(NCORE)], axis=0)
    return o[None].astype(np.float32)
```

```python
import numpy as np
import concourse.bass as bass
import concourse.mybir as mybir

ENGS = ["pe", "act", "dve", "pool", "sp"]
N_DMA_SEM = 16
import os as _os
SAME_ENGINE_SYNC = _os.environ.get("KSES", "1") == "1"


class Buf:
    __slots__ = ("name", "writer", "readers", "dma_readers", "excl")

    def __init__(self, name, excl=False):
        self.name = name
        self.excl = excl
        self.writer = None
        self.readers = {}
        self.dma_readers = []


class Op:
    __slots__ = ("eng", "fn", "idx", "waits", "signal", "is_dma", "dsem", "dtarget", "clock", "sigcount", "uid")


class Prog:
    def __init__(self):
        self.ops = {e: [] for e in ENGS}
        self.clock = {e: {f: 0 for f in ENGS} for e in ENGS}
        self.known_dma = {e: set() for e in ENGS}
        self.dma_sem_count = [0] * N_DMA_SEM
        self.dma_sem_last = [None] * N_DMA_SEM
        self.dma_rr = 0
        self.n_dma = 0
        self.uid = 0
        self.bar = {e: [] for e in ENGS}

    def barrier(self):
        lasts = [self.ops[e][-1] for e in ENGS if self.ops[e] and not self.ops[e][-1].is_dma]
        for e in ENGS:
            pass
        lasts = []
        for e in ENGS:
            for op in reversed(self.ops[e]):
                if not op.is_dma:
                    lasts.append(op)
                    break
        dmas = [op for op in self.dma_sem_last if op is not None]
        for e in ENGS:
            self.bar[e] = lasts + dmas

    def add(self, eng, fn, reads=(), writes=(), dma=False):
        op = Op()
        op.eng = eng
        op.fn = fn
        op.idx = len(self.ops[eng])
        op.waits = []
        op.signal = False
        op.is_dma = dma
        op.uid = self.uid
        self.uid += 1
        deps = []
        for b in reads:
            if b.writer is not None:
                deps.append(b.writer)
            if b.excl:
                for e2, r in b.readers.items():
                    if e2 != eng:
                        deps.append(r)
        for b in writes:
            if b.writer is not None:
                deps.append(b.writer)
            deps.extend(b.readers.values())
            deps.extend(b.dma_readers)
        if self.bar[eng]:
            deps.extend(self.bar[eng])
            self.bar[eng] = []
        clk = self.clock[eng]
        seen = set()
        for d in deps:
            if d.uid in seen:
                continue
            seen.add(d.uid)
            if d.is_dma:
                if d.uid not in self.known_dma[eng]:
                    op.waits.append(("dma", d.dsem, d.dtarget))
                    self.known_dma[eng].add(d.uid)
            else:
                if d.eng == eng and (eng == "pe" or not SAME_ENGINE_SYNC):
                    continue
                if clk[d.eng] < d.idx + 1:
                    op.waits.append(("eng", d))
                    d.signal = True
                    for f in ENGS:
                        if d.clock[f] > clk[f]:
                            clk[f] = d.clock[f]
                    if clk[d.eng] < d.idx + 1:
                        clk[d.eng] = d.idx + 1
        if dma and eng == "pool":
            pd = self.__dict__.setdefault("pool_dmas", [])
            if len(pd) >= 4:
                d = pd[-4]
                if d.uid not in self.known_dma[eng]:
                    op.waits.append(("dma", d.dsem, d.dtarget))
                    self.known_dma[eng].add(d.uid)
            pd.append(op)
        if dma:
            k = self.dma_rr
            self.dma_rr = (self.dma_rr + 1) % N_DMA_SEM
            prev = self.dma_sem_last[k]
            if prev is not None and prev.uid not in self.known_dma[eng]:
                op.waits.append(("dma", k, prev.dtarget))
                self.known_dma[eng].add(prev.uid)
            self.dma_sem_count[k] += 16
            op.dsem = k
            op.dtarget = self.dma_sem_count[k]
            self.dma_sem_last[k] = op
            self.n_dma += 1
        op.clock = dict(clk)
        if not SAME_ENGINE_SYNC or eng == "pe":
            pass
        self.ops[eng].append(op)
        for b in writes:
            b.writer = op
            b.readers = {}
            b.dma_readers = []
        for b in reads:
            if dma:
                b.dma_readers.append(op)
            else:
                b.readers[eng] = op
        return op

    def emit(self, nc, final_waits=True):
        for e in ENGS:
            c = 0
            for op in self.ops[e]:
                if op.signal:
                    c += 1
                op.sigcount = c
        from contextlib import ExitStack
        with ExitStack() as es:
            esem = {e: es.enter_context(nc.semaphore("s_" + e)) for e in ENGS}
            dsem = [es.enter_context(nc.semaphore("d%d" % i)) for i in range(N_DMA_SEM)]
            block = es.enter_context(nc.Block())
            last_dma = [op for op in self.dma_sem_last if op is not None]

            def run(e, h):
                for op in self.ops[e]:
                    for w in op.waits:
                        if w[0] == "dma":
                            h.wait_ge(dsem[w[1]], w[2])
                        else:
                            h.wait_ge(esem[w[1].eng], w[1].sigcount)
                    ins = op.fn(h)
                    if op.is_dma:
                        ins.then_inc(dsem[op.dsem], 16)
                    elif op.signal:
                        ins.then_inc(esem[e], 1)
                if e == "sp" and final_waits:
                    for k in range(N_DMA_SEM):
                        if self.dma_sem_count[k] > 0:
                            h.wait_ge(dsem[k], self.dma_sem_count[k])

            @block.tensor
            def _(h):
                run("pe", h)

            @block.scalar
            def _(h):
                run("act", h)

            @block.vector
            def _(h):
                run("dve", h)

            @block.gpsimd
            def _(h):
                run("pool", h)

            @block.sync
            def _(h):
                run("sp", h)

from contextlib import ExitStack
import ml_dtypes
from concourse.bass_utils import run_bass_kernel_spmd

F32 = mybir.dt.float32
BF16 = mybir.dt.bfloat16
AF = mybir.ActivationFunctionType
ALU = mybir.AluOpType

NCORE = 8
TOK = 2048
NPRE = 112
NM1 = 17
NM2 = 18
ALPHA = 2.0 ** 0.25
COL_Z, COL_X, COL_B, COL_C, COL_DT, COL_Q, COL_K, COL_V, COL_G = 0, 2048, 4096, 4608, 5120, 5152, 6176, 6304, 6432


class TT:
    def __init__(self, ap, name, excl=False):
        self.ap = ap
        self.b = Buf(name, excl)


class Arena:
    def __init__(self, t, n):
        self.t, self.n, self.off = t, n, 0

    def get(self, name, *fs):
        n = int(np.prod(fs))
        ap = self.t[:, self.off:self.off + n]
        self.off += n
        assert self.off <= self.n, (name, self.off, self.n)
        if len(fs) == 2:
            ap = ap.rearrange("p (a b) -> p a b", a=fs[0])
        elif len(fs) == 3:
            ap = ap.rearrange("p (a b c) -> p a b c", a=fs[0], b=fs[1])
        return TT(ap, name)


def bc(ap2, n):
    return ap2.unsqueeze(2).to_broadcast([ap2.shape[0], ap2.shape[1], n])


def build(stop=4, npre=NPRE, dbg=False):
    nc = bass.Bass("TRN2", target_bir_lowering=False)
    dt_in = lambda n, s: nc.dram_tensor(n, s, F32, kind="ExternalInput").ap()
    xpre = dt_in("xpre", [NPRE * 128, 1024])
    xmain = dt_in("xmain", [NM2 * 128, 1024])
    pflag = dt_in("pflag", [128, NPRE + NM1])
    w_in = dt_in("w_in", [1024, 8480])
    b_gate = dt_in("b_gate", [1, 2048])
    scw = dt_in("scw", [96, 128])
    scb = dt_in("scb", [24, 128])
    dtb = dt_in("dtb", [1, 32])
    alog = dt_in("alog", [1, 32])
    dsk = dt_in("dsk", [1, 32])
    normw = dt_in("normw", [1, 2048])
    sinks = dt_in("sinks", [1, 16])
    w_bs = dt_in("w_bs", [2048, 1024])
    w_ba = dt_in("w_ba", [1024, 1024])
    w_mix = dt_in("w_mix", [1024, 1024])
    ln1g = dt_in("ln1g", [1, 1024]); ln1b = dt_in("ln1b", [1, 1024])
    ln2g = dt_in("ln2g", [1, 1024]); ln2b = dt_in("ln2b", [1, 1024])
    w_up = dt_in("w_up", [1024, 5632])
    fcw = dt_in("fcw", [132, 128])
    fcb = dt_in("fcb", [44, 128])
    w_dn = dt_in("w_dn", [2816, 1024])
    cst = dt_in("cst", [128, 4 * 128])
    biasg = dt_in("biasg", [128, 2 * 16 * 128])
    maskg = dt_in("maskg", [128, 2 * 16 * 128])
    out = nc.dram_tensor("out", [TOK, 1024], F32, kind="ExternalOutput").ap()
    skind = "ExternalOutput" if dbg else "Internal"
    ynd = nc.dram_tensor("ynd", [NM1 * 128, 2048], BF16, kind=skind).ap()
    yad = nc.dram_tensor("yad", [NM1 * 128, 1024], BF16, kind=skind).ap()
    h1d = nc.dram_tensor("h1d", [NM1 * 128, 1024], F32, kind=skind).ap()

    P = Prog()
    es = ExitStack()
    NB, NF = 153 * 512, 47 * 256
    ABt = es.enter_context(nc.sbuf_tensor("AB", [128, NB], BF16))
    AFt = es.enter_context(nc.sbuf_tensor("AF", [128, NF], F32))
    pf = [TT(es.enter_context(nc.psum_tensor("pf%d" % i, [128, 512], F32))[:], "pf%d" % i, True) for i in range(6)]
    pb = [TT(es.enter_context(nc.psum_tensor("pb%d" % i, [128, 1024], BF16))[:], "pb%d" % i, True) for i in range(2)]
    AB = Arena(ABt, NB)
    AFa = Arena(AFt, NF)

    def bcast_row(src, n):
        return bass.AP(src.tensor, 0, [[0, 128], [1, n]])

    def dma(eng, o, i, reads, writes):
        P.add(eng, lambda e, o=o, i=i: e.dma_start(out=o, in_=i), reads=reads, writes=writes, dma=True)

    cf = AFa.get("cf", 4, 128)
    dma("sp", cf.ap, cst.rearrange("p (a b) -> p a b", a=4), [], [cf.b])
    identf, triU, Ustr, onesf = cf.ap[:, 0, :], cf.ap[:, 1, :], cf.ap[:, 2, :], cf.ap[:, 3, :]
    cb16 = AB.get("cb16", 4, 128)
    dma("pool", cb16.ap, cst.rearrange("p (a b) -> p a b", a=4), [], [cb16.b])
    identb, maskb = cb16.ap[:, 0, :], cb16.ap[:, 1, :]
    flg = AFa.get("flg", NPRE + NM1)
    dma("sp", flg.ap, pflag, [], [flg.b])
    smallp = AFa.get("smallp", 6, 32)
    dma("sp", smallp.ap[:, 0, :], bcast_row(dtb, 32), [], [smallp.b])
    dma("sp", smallp.ap[:, 1, :], bcast_row(alog, 32), [], [smallp.b])
    dma("sp", smallp.ap[:, 2, :], bcast_row(dsk, 32), [], [smallp.b])
    dma("sp", smallp.ap[:, 3, 0:16], bcast_row(sinks, 16), [], [smallp.b])
    P.add("act", lambda e: e.activation(out=smallp.ap[:, 1, :], in_=smallp.ap[:, 1, :], func=AF.Exp), reads=[smallp.b], writes=[smallp.b])
    P.add("dve", lambda e: e.tensor_scalar(out=smallp.ap[:, 1, :], in0=smallp.ap[:, 1, :], scalar1=-1.0, scalar2=None, op0=ALU.mult), reads=[smallp.b], writes=[smallp.b])
    P.add("act", lambda e: e.activation(out=smallp.ap[:, 3, 0:16], in_=smallp.ap[:, 3, 0:16], func=AF.Exp), reads=[smallp.b], writes=[smallp.b])
    dtb_bc, a_bc, D_bc, esink = smallp.ap[:, 0, :], smallp.ap[:, 1, :], smallp.ap[:, 2, :], smallp.ap[:, 3, 0:16]
    onecol = onesf[:, 0:1]
    rawh = AB.get("rawh", 24, 4)
    markB, markF = AB.off, AFa.off
    H = AFa.get("H", 2048)

    def load_w(dst, src, ncols, nk=8):
        nk = src.shape[0] // 128
        for c0 in range(0, ncols, 512):
            c1 = min(c0 + 512, ncols)
            for k in range(nk):
                dma("pool", dst.ap[:, k, c0:c1], src[k * 128:(k + 1) * 128, c0:c1], [], [dst.b])

    def load_xT(xrow_ap, xb, xT):
        dma("pool", xb.ap, xrow_ap, [], [xb.b])
        for k in range(8):
            P.add("pe", lambda e, k=k: e.transpose(pb[0].ap[:, k * 128:(k + 1) * 128], xb.ap[:, k * 128:(k + 1) * 128], identb), reads=[xb.b, cb16.b], writes=[pb[0].b])
        P.add("act", lambda e: e.copy(out=xT.ap.rearrange("p a b -> p (a b)"), in_=pb[0].ap), reads=[pb[0].b], writes=[xT.b])

    def chain(bank, out_ap, pairs, reads):
        n = len(pairs)
        for i, (l, r) in enumerate(pairs):
            P.add("pe", lambda e, l=l, r=r, i=i: e.matmul(out_ap, l, r, start=(i == 0), stop=(i == n - 1)), reads=reads, writes=[bank.b])

    def per_channel(dst, src_dram, rows):
        tmp = AFa.get("pc_tmp", 128)
        dma("sp", tmp.ap[0:rows, :], src_dram, [], [tmp.b])
        P.add("pe", lambda e: e.transpose(pf[5].ap[:, 0:rows], tmp.ap[0:rows, :], identf[0:rows, 0:rows]), reads=[tmp.b, cf.b], writes=[pf[5].b])
        P.add("dve", lambda e: e.tensor_copy(out=dst, in_=pf[5].ap[:, 0:rows]), reads=[pf[5].b], writes=[])

    import os
    SKIP1 = int(os.environ.get('KSKIP1', '0'))
    Wx = AB.get("Wx", 8, 3072); load_w(Wx, w_in[:, COL_X:COL_X + 3072], 3072)
    Wdt = AB.get("Wdt", 8, 32); load_w(Wdt, w_in[:, COL_DT:COL_DT + 32], 32)
    cwt = AFa.get("cwt", 96); cbt = AFa.get("cbt", 24)
    per_channel(cwt.ap, scw, 96)
    per_channel(cbt.ap, scb, 24)
    cwt.b.writer = P.ops["dve"][-2]; cbt.b.writer = P.ops["dve"][-1]
    diag = AB.get("diag", 24, 4, 128)
    for j in range(24):
        for tp in range(4):
            P.add("dve", lambda e, j=j, tp=tp: e.tensor_scalar(out=diag.ap[:, j, tp, :], in0=identf, scalar1=cwt.ap[:, tp * 24 + j:tp * 24 + j + 1], scalar2=None, op0=ALU.mult), reads=[cf.b, cwt.b], writes=[diag.b])
    P.add("dve", lambda e: e.memset(H.ap, 0.0), writes=[H.b])
    mark1B, mark1F = AB.off, AFa.off
    xbA = [AB.get("xbA%d" % i, 1024) for i in range(2)]
    xT4 = [AB.get("xT4_%d" % i, 8, 512) for i in range(2)]
    raw4 = AB.get("raw4", 24, 516); xc4 = AB.get("xc4", 24, 512)
    rawb = [Buf("rawb%d" % j) for j in range(24)]; xcb = [Buf("xcb%d" % j) for j in range(24)]
    xtk = [AB.get("xtk%d" % i, 2560) for i in range(2)]
    xdtsA = AB.get("xdtsA", 512)
    smA = [AFa.get("smA%d" % i, 8, 32) for i in range(2)]
    P.add("pool", lambda e: e.memset(raw4.ap, 0.0), writes=rawb)

    def load_group(gi, buf):
        for q in range(4):
            c = gi * 4 + q
            xb_ = xbA[q % 2]
            dma("pool", xb_.ap, xpre[c * 128:(c + 1) * 128, :], [], [xb_.b])
            for k in range(8):
                P.add("pe", lambda e, k=k, xb_=xb_: e.transpose(pb[0].ap[:, k * 128:(k + 1) * 128], xb_.ap[:, k * 128:(k + 1) * 128], identb), reads=[xb_.b, cb16.b], writes=[pb[0].b])
            P.add("act", lambda e, q=q, buf=buf: e.copy(out=xT4[buf].ap[:, :, q * 128:(q + 1) * 128], in_=pb[0].ap.rearrange("p (a b) -> p a b", a=8)), reads=[pb[0].b], writes=[xT4[buf].b])

    def group_proj(gi, buf, last):
        ntile = 24 if last else 20
        for j in range(ntile):
            bank = pf[j % 2]
            chain(bank, bank.ap, [(Wx.ap[:, k, j * 128:(j + 1) * 128], xT4[buf].ap[:, k, :]) for k in range(8)], [Wx.b, xT4[buf].b])
            P.add("act", lambda e, j=j, bank=bank: e.copy(out=raw4.ap[:, j, 3:515], in_=bank.ap), reads=[bank.b], writes=[rawb[j]])
        for j in range(ntile):
            bank = pf[2 + j % 2]
            chain(bank, bank.ap, [(diag.ap[:, j, tp, :], raw4.ap[:, j, tp:tp + 512]) for tp in range(4)], [diag.b, rawb[j]])
            P.add("act", lambda e, j=j, bank=bank: e.activation(out=xc4.ap[:, j, :], in_=bank.ap, func=AF.Silu, bias=cbt.ap[:, j:j + 1], scale=1.0), reads=[bank.b, cbt.b], writes=[xcb[j]])
        P.add("pool", lambda e: e.tensor_copy(out=raw4.ap[:, :, 0:3], in_=raw4.ap[:, :, 512:515]), reads=rawb, writes=rawb)

    def group_chunks(gi, buf):
        for q in range(4):
            c = gi * 4 + q
            fcol = flg.ap[:, c:c + 1]
            xt = xtk[q % 2]; smq = smA[q % 2]
            for bt in range(3):
                nt = 8 if bt < 2 else 4
                for i in range(nt):
                    j = bt * 8 + i
                    P.add("pe", lambda e, i=i, j=j, q=q: e.transpose(pb[1].ap[:, i * 128:(i + 1) * 128], xc4.ap[:, j, q * 128:(q + 1) * 128], identb), reads=[xcb[j], cb16.b], writes=[pb[1].b])
                P.add("dve", lambda e, bt=bt, nt=nt, xt=xt: e.tensor_copy(out=xt.ap[:, bt * 1024:bt * 1024 + nt * 128], in_=pb[1].ap[:, 0:nt * 128]), reads=[pb[1].b], writes=[xt.b])
            chain(pf[4], pf[4].ap[:, 0:32], [(xT4[buf].ap[:, k, q * 128:(q + 1) * 128], Wdt.ap[:, k, :]) for k in range(8)], [xT4[buf].b, Wdt.b])
            S = lambda i, smq=smq: smq.ap[:, i, :]
            P.add("dve", lambda e, S=S: e.tensor_tensor(out=S(0), in0=pf[4].ap[:, 0:32], in1=dtb_bc, op=ALU.add), reads=[pf[4].b, smallp.b], writes=[smq.b])
            P.add("act", lambda e, S=S: e.activation(out=S(0), in_=S(0), func=AF.Exp), reads=[smq.b], writes=[smq.b])
            P.add("act", lambda e, S=S: e.activation(out=S(1), in_=S(0), func=AF.Ln, bias=onecol, scale=1.0), reads=[smq.b, cf.b], writes=[smq.b])
            P.add("dve", lambda e, S=S, fcol=fcol: e.tensor_scalar(out=S(1), in0=S(1), scalar1=fcol, scalar2=None, op0=ALU.mult), reads=[smq.b, flg.b], writes=[smq.b])
            P.add("dve", lambda e, S=S: e.tensor_tensor(out=S(2), in0=S(1), in1=a_bc, op=ALU.mult), reads=[smq.b, smallp.b], writes=[smq.b])
            P.add("pe", lambda e, S=S: e.matmul(pf[4].ap[:, 32:64], triU, S(2), start=True, stop=True), reads=[cf.b, smq.b], writes=[pf[4].b])
            P.add("pe", lambda e, S=S: e.matmul(pf[4].ap[:, 64:96], onesf, S(2), start=True, stop=True), reads=[cf.b, smq.b], writes=[pf[4].b])
            P.add("dve", lambda e, smq=smq: e.tensor_copy(out=smq.ap[:, 3:5, :], in_=pf[4].ap[:, 32:96].rearrange("p (a b) -> p a b", a=2)), reads=[pf[4].b], writes=[smq.b])
            P.add("dve", lambda e, S=S: e.tensor_tensor(out=S(6), in0=S(4), in1=S(3), op=ALU.subtract), reads=[smq.b], writes=[smq.b])
            P.add("act", lambda e, S=S: e.activation(out=S(6), in_=S(6), func=AF.Exp), reads=[smq.b], writes=[smq.b])
            P.add("act", lambda e, S=S: e.activation(out=S(7), in_=S(4), func=AF.Exp), reads=[smq.b], writes=[smq.b])
            P.add("dve", lambda e, S=S: e.tensor_tensor(out=S(6), in0=S(6), in1=S(1), op=ALU.mult), reads=[smq.b], writes=[smq.b])
            v3 = lambda ap: ap.rearrange("p (a b) -> p a b", a=8)
            for g in range(4):
                xg = xt.ap[:, g * 512:(g + 1) * 512].rearrange("p (a b) -> p a b", a=8)
                P.add("dve", lambda e, g=g, xg=xg, smq=smq: e.tensor_tensor(out=v3(xdtsA.ap), in0=xg, in1=bc(smq.ap[:, 6, 8 * g:8 * g + 8], 64), op=ALU.mult), reads=[xt.b, smq.b], writes=[xdtsA.b])
                P.add("pe", lambda e, g=g, xt=xt: e.matmul(pf[5].ap, xt.ap[:, 2048 + g * 128:2048 + (g + 1) * 128], xdtsA.ap, start=True, stop=True), reads=[xt.b, xdtsA.b], writes=[pf[5].b])
                Hg = H.ap[:, g * 512:(g + 1) * 512]
                P.add("dve", lambda e, g=g, Hg=Hg, smq=smq: e.tensor_tensor(out=v3(Hg), in0=v3(Hg), in1=bc(smq.ap[:, 7, 8 * g:8 * g + 8], 64), op=ALU.mult), reads=[H.b, smq.b], writes=[H.b])
                P.add("dve", lambda e, Hg=Hg: e.tensor_tensor(out=Hg, in0=Hg, in1=pf[5].ap, op=ALU.add), reads=[H.b, pf[5].b], writes=[H.b])

    NG = NPRE // 4
    g0 = NG - (npre + 3) // 4
    if not SKIP1 and g0 < NG:
        load_group(g0, g0 % 2)
        for gi in range(g0, NG):
            group_proj(gi, gi % 2, gi == NG - 1)
            if gi + 1 < NG:
                load_group(gi + 1, (gi + 1) % 2)
            group_chunks(gi, gi % 2)
        P.add("pool", lambda e: e.tensor_copy(out=rawh.ap[:, :, 0:3], in_=raw4.ap[:, :, 0:3]), reads=rawb, writes=[rawh.b])
    else:
        P.add("pool", lambda e: e.memset(rawh.ap, 0.0), writes=[rawh.b])

    P.barrier()
    AB.off, AFa.off = mark1B, mark1F
    Wz = AB.get("Wz", 8, 2048); load_w(Wz, w_in[:, COL_Z:COL_Z + 2048], 2048)
    nwb = AFa.get("nwb", 2048)
    dma("sp", nwb.ap, bcast_row(normw, 2048), [], [nwb.b])
    xb = AB.get("xb", 1024); xT = AB.get("xT", 8, 128)
    raw = AB.get("raw", 24, 132); xc = AB.get("xc", 24, 128)
    xtok = AB.get("xtok", 2560)
    xdt = AB.get("xdt", 512); xdts = AB.get("xdts", 512)
    Hb = AB.get("Hb", 2048); cbm = AB.get("cbm", 4, 128)
    LT = AB.get("LT", 4, 128); MT = AB.get("MT", 8, 128); yn = AB.get("yn", 2048)
    adtU = [AFa.get("adtU%d" % i, 128) for i in range(2)]
    sm = AFa.get("sm", 12, 32)
    yacc = AFa.get("yacc", 512); ytmp = AFa.get("ytmp", 512); sz = AFa.get("sz", 512)
    ssq = AFa.get("ssq", 4)
    P.add("pool", lambda e: e.memset(raw.ap, 0.0), writes=[raw.b])
    P.add("pool", lambda e: e.tensor_copy(out=raw.ap[:, :, 0:3], in_=rawh.ap[:, :, 0:3]), reads=[rawh.b], writes=[raw.b])

    CUT = int(os.environ.get('KCUT', '99')); NCH1 = int(os.environ.get('KNCH', str(NM1)))
    def ssd_chunk(xrow, fcol, main, ci):
        ntile = 24
        load_xT(xrow, xb, xT)
        if CUT <= 1: return
        for g in range(ntile // 4):
            bank = pf[g % 2]
            for jj in range(4):
                j = 4 * g + jj
                chain(bank, bank.ap[:, jj * 128:(jj + 1) * 128], [(Wx.ap[:, k, j * 128:(j + 1) * 128], xT.ap[:, k, :]) for k in range(8)], [Wx.b, xT.b])
            P.add("act", lambda e, g=g, bank=bank: e.copy(out=raw.ap[:, 4 * g:4 * g + 4, 3:131], in_=bank.ap.rearrange("p (a b) -> p a b", a=4)), reads=[bank.b], writes=[raw.b])
        if CUT <= 2: return
        for g in range(ntile // 4):
            bank = pf[2 + g % 2]
            for jj in range(4):
                j = 4 * g + jj
                chain(bank, bank.ap[:, jj * 128:(jj + 1) * 128], [(diag.ap[:, j, tp, :], raw.ap[:, j, tp:tp + 128]) for tp in range(4)], [diag.b, raw.b])
                P.add("act", lambda e, j=j, jj=jj, bank=bank: e.activation(out=xc.ap[:, j, :], in_=bank.ap[:, jj * 128:(jj + 1) * 128], func=AF.Silu, bias=cbt.ap[:, j:j + 1], scale=1.0), reads=[bank.b, cbt.b], writes=[xc.b])
        if CUT <= 3: return
        P.add("pool", lambda e: e.tensor_copy(out=raw.ap[:, :, 0:3], in_=raw.ap[:, :, 128:131]), reads=[raw.b], writes=[raw.b])
        for bt in range(3):
            nt = 8 if bt < 2 else 4
            for i in range(nt):
                j = bt * 8 + i
                P.add("pe", lambda e, i=i, j=j: e.transpose(pb[1].ap[:, i * 128:(i + 1) * 128], xc.ap[:, j, :], identb), reads=[xc.b, cb16.b], writes=[pb[1].b])
            P.add("dve", lambda e, bt=bt, nt=nt: e.tensor_copy(out=xtok.ap[:, bt * 1024:bt * 1024 + nt * 128], in_=pb[1].ap[:, 0:nt * 128]), reads=[pb[1].b], writes=[xtok.b])
        if CUT <= 4: return
        chain(pf[4], pf[4].ap[:, 0:32], [(xT.ap[:, k, :], Wdt.ap[:, k, :]) for k in range(8)], [xT.b, Wdt.b])
        S = lambda i: sm.ap[:, i, :]
        P.add("dve", lambda e: e.tensor_tensor(out=S(0), in0=pf[4].ap[:, 0:32], in1=dtb_bc, op=ALU.add), reads=[pf[4].b, smallp.b], writes=[sm.b])
        P.add("act", lambda e: e.activation(out=S(0), in_=S(0), func=AF.Exp), reads=[sm.b], writes=[sm.b])
        P.add("act", lambda e: e.activation(out=S(1), in_=S(0), func=AF.Ln, bias=onecol, scale=1.0), reads=[sm.b, cf.b], writes=[sm.b])
        P.add("dve", lambda e: e.tensor_scalar(out=S(1), in0=S(1), scalar1=fcol, scalar2=None, op0=ALU.mult), reads=[sm.b, flg.b], writes=[sm.b])
        P.add("dve", lambda e: e.tensor_tensor(out=S(2), in0=S(1), in1=a_bc, op=ALU.mult), reads=[sm.b, smallp.b], writes=[sm.b])
        if CUT <= 5: return
        P.add("pe", lambda e: e.matmul(pf[4].ap[:, 32:64], triU, S(2), start=True, stop=True), reads=[cf.b, sm.b], writes=[pf[4].b])
        P.add("pe", lambda e: e.matmul(pf[4].ap[:, 64:96], onesf, S(2), start=True, stop=True), reads=[cf.b, sm.b], writes=[pf[4].b])
        P.add("dve", lambda e: e.tensor_copy(out=sm.ap[:, 3:5, :], in_=pf[4].ap[:, 32:96].rearrange("p (a b) -> p a b", a=2)), reads=[pf[4].b], writes=[sm.b])
        if CUT <= 6: return
        if main and CUT > 7:
            for g in range(4):
                P.add("pe", lambda e, g=g: e.matmul(pf[5].ap[:, g * 128:(g + 1) * 128], xc.ap[:, 16 + g, :], xc.ap[:, 20 + g, :], start=True, stop=True), reads=[xc.b], writes=[pf[5].b])
            P.add("dve", lambda e: e.tensor_tensor(out=cbm.ap, in0=pf[5].ap.rearrange("p (a b) -> p a b", a=4), in1=maskb.unsqueeze(1).to_broadcast([128, 4, 128]), op=ALU.mult), reads=[pf[5].b, cb16.b], writes=[cbm.b])
            P.add("pool", lambda e: e.tensor_copy(out=Hb.ap, in_=H.ap), reads=[H.b], writes=[Hb.b])
            P.add("act", lambda e: e.activation(out=S(5), in_=S(3), func=AF.Exp), reads=[sm.b], writes=[sm.b])
            for g in range(4):
                xg = xtok.ap[:, g * 512:(g + 1) * 512].rearrange("p (a b) -> p a b", a=8)
                P.add("dve", lambda e, g=g, xg=xg: e.tensor_tensor(out=xdt.ap.rearrange("p (a b) -> p a b", a=8), in0=xg, in1=bc(sm.ap[:, 1, 8 * g:8 * g + 8], 64), op=ALU.mult), reads=[xtok.b, sm.b], writes=[xdt.b])
                for hh in range(2):
                    bank = pf[hh]
                    for h4 in range(4):
                        h = 8 * g + 4 * hh + h4
                        au = adtU[h4 % 2]
                        P.add("dve", lambda e, h=h, au=au: e.tensor_scalar(out=au.ap, in0=Ustr, scalar1=sm.ap[:, 2, h:h + 1], scalar2=None, op0=ALU.mult), reads=[cf.b, sm.b], writes=[au.b])
                        P.add("pe", lambda e, h4=h4, au=au, bank=bank: e.matmul(bank.ap[:, h4 * 128:(h4 + 1) * 128], au.ap, triU, start=True, stop=True), reads=[au.b, cf.b], writes=[bank.b])
                    P.add("act", lambda e, bank=bank: e.activation(out=LT.ap.rearrange("p a b -> p (a b)"), in_=bank.ap, func=AF.Exp), reads=[bank.b], writes=[LT.b])
                    P.add("dve", lambda e, g=g, hh=hh: e.tensor_tensor(out=MT.ap[:, 4 * hh:4 * hh + 4, :], in0=LT.ap, in1=cbm.ap[:, g, :].unsqueeze(1).to_broadcast([128, 4, 128]), op=ALU.mult), reads=[LT.b, cbm.b], writes=[MT.b])
                for h8 in range(8):
                    P.add("pe", lambda e, h8=h8: e.matmul(pf[2].ap[:, h8 * 64:(h8 + 1) * 64], MT.ap[:, h8, :], xdt.ap[:, h8 * 64:(h8 + 1) * 64], start=True, stop=True), reads=[MT.b, xdt.b], writes=[pf[2].b])
                P.add("pe", lambda e, g=g: e.matmul(pf[3].ap, xc.ap[:, 20 + g, :], Hb.ap[:, g * 512:(g + 1) * 512], start=True, stop=True), reads=[xc.b, Hb.b], writes=[pf[3].b])
                v3 = lambda ap: ap.rearrange("p (a b) -> p a b", a=8)
                P.add("dve", lambda e, g=g: e.tensor_tensor(out=v3(yacc.ap), in0=v3(pf[3].ap), in1=bc(sm.ap[:, 5, 8 * g:8 * g + 8], 64), op=ALU.mult), reads=[pf[3].b, sm.b], writes=[yacc.b])
                P.add("dve", lambda e: e.tensor_tensor(out=yacc.ap, in0=yacc.ap, in1=pf[2].ap, op=ALU.add), reads=[pf[2].b, yacc.b], writes=[yacc.b])
                P.add("dve", lambda e, g=g, xg=xg: e.tensor_tensor(out=v3(ytmp.ap), in0=xg, in1=bc(D_bc[:, 8 * g:8 * g + 8], 64), op=ALU.mult), reads=[xtok.b, smallp.b], writes=[ytmp.b])
                P.add("dve", lambda e: e.tensor_tensor(out=yacc.ap, in0=yacc.ap, in1=ytmp.ap, op=ALU.add), reads=[ytmp.b, yacc.b], writes=[yacc.b])
                chain(pf[5], pf[5].ap, [(xT.ap[:, k, :], Wz.ap[:, k, g * 512:(g + 1) * 512]) for k in range(8)], [xT.b, Wz.b])
                P.add("act", lambda e: e.activation(out=sz.ap, in_=pf[5].ap, func=AF.Silu), reads=[pf[5].b], writes=[sz.b])
                P.add("dve", lambda e: e.tensor_tensor(out=yacc.ap, in0=yacc.ap, in1=sz.ap, op=ALU.mult), reads=[sz.b, yacc.b], writes=[yacc.b])
                P.add("act", lambda e, g=g: e.activation(out=ytmp.ap, in_=yacc.ap, func=AF.Square, accum_out=ssq.ap[:, g:g + 1]), reads=[yacc.b], writes=[ytmp.b, ssq.b])
                P.add("dve", lambda e, g=g: e.tensor_scalar(out=ssq.ap[:, g:g + 1], in0=ssq.ap[:, g:g + 1], scalar1=1.0 / 512, scalar2=1e-5, op0=ALU.mult, op1=ALU.add), reads=[ssq.b], writes=[ssq.b])
                P.add("act", lambda e, g=g: e.activation(out=ssq.ap[:, g:g + 1], in_=ssq.ap[:, g:g + 1], func=AF.Sqrt), reads=[ssq.b], writes=[ssq.b])
                P.add("dve", lambda e, g=g: e.reciprocal(out=ssq.ap[:, g:g + 1], in_=ssq.ap[:, g:g + 1]), reads=[ssq.b], writes=[ssq.b])
                P.add("dve", lambda e, g=g: e.scalar_tensor_tensor(out=yn.ap[:, g * 512:(g + 1) * 512], in0=yacc.ap, scalar=ssq.ap[:, g:g + 1], in1=nwb.ap[:, g * 512:(g + 1) * 512], op0=ALU.mult, op1=ALU.mult), reads=[yacc.b, ssq.b, nwb.b], writes=[yn.b])
            dma("sp", ynd[ci * 128:(ci + 1) * 128, :], yn.ap, [yn.b], [])
        if CUT <= 8: return
        P.add("dve", lambda e: e.tensor_tensor(out=S(6), in0=S(4), in1=S(3), op=ALU.subtract), reads=[sm.b], writes=[sm.b])
        P.add("act", lambda e: e.activation(out=S(6), in_=S(6), func=AF.Exp), reads=[sm.b], writes=[sm.b])
        P.add("act", lambda e: e.activation(out=S(7), in_=S(4), func=AF.Exp), reads=[sm.b], writes=[sm.b])
        P.add("dve", lambda e: e.tensor_tensor(out=S(6), in0=S(6), in1=S(1), op=ALU.mult), reads=[sm.b], writes=[sm.b])
        for g in range(4):
            xg = xtok.ap[:, g * 512:(g + 1) * 512].rearrange("p (a b) -> p a b", a=8)
            v3 = lambda ap: ap.rearrange("p (a b) -> p a b", a=8)
            P.add("dve", lambda e, g=g, xg=xg: e.tensor_tensor(out=v3(xdts.ap), in0=xg, in1=bc(sm.ap[:, 6, 8 * g:8 * g + 8], 64), op=ALU.mult), reads=[xtok.b, sm.b], writes=[xdts.b])
            bank = pf[2 + g % 2]
            P.add("pe", lambda e, g=g, bank=bank: e.matmul(bank.ap, xtok.ap[:, 2048 + g * 128:2048 + (g + 1) * 128], xdts.ap, start=True, stop=True), reads=[xtok.b, xdts.b], writes=[bank.b])
            Hg = H.ap[:, g * 512:(g + 1) * 512]
            P.add("dve", lambda e, g=g, Hg=Hg: e.tensor_tensor(out=v3(Hg), in0=v3(Hg), in1=bc(sm.ap[:, 7, 8 * g:8 * g + 8], 64), op=ALU.mult), reads=[H.b, sm.b], writes=[H.b])
            P.add("dve", lambda e, Hg=Hg, bank=bank: e.tensor_tensor(out=Hg, in0=Hg, in1=bank.ap, op=ALU.add), reads=[H.b, bank.b], writes=[H.b])

    for ci in range(0 if SKIP1 else NCH1):
        ssd_chunk(xmain[(ci + 1) * 128:(ci + 2) * 128, :], flg.ap[:, NPRE + ci:NPRE + ci + 1], True, ci)

    if stop < 2:
        P.emit(nc); es.close(); return nc
    P.barrier()
    AB.off, AFa.off = markB, markF
    Wq = AB.get("Wq", 8, 1024); load_w(Wq, w_in[:, COL_Q:COL_Q + 1024], 1024)
    Wk2 = AB.get("Wk2", 8, 128); load_w(Wk2, w_in[:, COL_K:COL_K + 128], 128)
    Wv = AB.get("Wv", 8, 128); load_w(Wv, w_in[:, COL_V:COL_V + 128], 128)
    EB = AB.get("EB", 2, 16, 128)
    ebf = AFa.get("ebf", 2048); mkf = AFa.get("mkf", 2048)
    for kt in range(2):
        dma("sp", ebf.ap, biasg[:, kt * 2048:(kt + 1) * 2048], [], [ebf.b])
        dma("sp", mkf.ap, maskg[:, kt * 2048:(kt + 1) * 2048], [], [mkf.b])
        P.add("act", lambda e: e.activation(out=ebf.ap, in_=ebf.ap, func=AF.Exp), reads=[ebf.b], writes=[ebf.b])
        P.add("dve", lambda e, kt=kt: e.tensor_tensor(out=EB.ap[:, kt, :, :].rearrange("p a b -> p (a b)"), in0=ebf.ap, in1=mkf.ap, op=ALU.mult), reads=[ebf.b, mkf.b], writes=[EB.b])
    xb = AB.get("xb2", 1024); xT = AB.get("xT2", 8, 128)
    kT = [AB.get("kT%d" % i, 2, 128) for i in range(2)]
    vx = [AB.get("vx%d" % i, 2, 65) for i in range(2)]
    for i in range(2):
        P.add("pool", lambda e, i=i: e.memset(vx[i].ap, 1.0), writes=[vx[i].b])
    qT = AB.get("qT", 16, 128); et = AB.get("et", 4, 128)
    PT = [AB.get("PT%d" % i, 4, 128) for i in range(2)]
    ya = AB.get("ya", 1024)
    den = AFa.get("den", 4)
    flag0 = flg.ap[:, NPRE:NPRE + 1]
    CUT2 = int(os.environ.get('KCUT2', '99')); NCH2 = int(os.environ.get('KNCH2', str(NM2)))
    for ci in range(NCH2):
        sl = ci % 2
        load_xT(xmain[ci * 128:(ci + 1) * 128, :], xb, xT)
        for kv in range(2):
            chain(pf[0], pf[0].ap[0:64, kv * 128:(kv + 1) * 128], [(Wk2.ap[:, k, kv * 64:(kv + 1) * 64], xT.ap[:, k, :]) for k in range(8)], [Wk2.b, xT.b])
        P.add("act", lambda e, sl=sl: e.copy(out=kT[sl].ap[0:64, :, :], in_=pf[0].ap[0:64, 0:256].rearrange("p (a b) -> p a b", a=2)), reads=[pf[0].b], writes=[kT[sl].b])
        chain(pf[1], pf[1].ap[:, 0:128], [(xT.ap[:, k, :], Wv.ap[:, k, :]) for k in range(8)], [xT.b, Wv.b])
        P.add("dve", lambda e, sl=sl: e.tensor_copy(out=vx[sl].ap[:, :, 0:64], in_=pf[1].ap[:, 0:128].rearrange("p (a b) -> p a b", a=2)), reads=[pf[1].b], writes=[vx[sl].b])
        if ci == 0 or CUT2 <= 1:
            continue
        for q4 in range(4):
            bank = pf[2 + q4 % 2]
            for tt in range(4):
                j = q4 * 4 + tt
                chain(bank, bank.ap[0:64, tt * 128:(tt + 1) * 128], [(Wq.ap[:, k, j * 64:(j + 1) * 64], xT.ap[:, k, :]) for k in range(8)], [Wq.b, xT.b])
            P.add("act", lambda e, q4=q4, bank=bank: e.copy(out=qT.ap[0:64, q4 * 4:q4 * 4 + 4, :], in_=bank.ap[0:64, :].rearrange("p (a b) -> p a b", a=4)), reads=[bank.b], writes=[qT.b])
        for kvh in range(2):
            if CUT2 <= 2: break
            for hb in range(2):
                j0 = kvh * 8 + hb * 4
                for kt in range(2):
                    slk = (ci + 1 + kt) % 2
                    bank = pf[4 + kt]
                    for i in range(4):
                        j = j0 + i
                        base = (j % 2) * 64 * int(os.environ.get("KB64", "1"))
                        P.add("pe", lambda e, i=i, j=j, base=base, slk=slk, bank=bank, kvh=kvh: e.matmul(bank.ap[:, i * 128:(i + 1) * 128], kT[slk].ap[0:64, kvh, :], qT.ap[0:64, j, :], start=True, stop=True), reads=[kT[slk].b, qT.b], writes=[bank.b])
                    P.add("act", lambda e, bank=bank: e.activation(out=et.ap.rearrange("p a b -> p (a b)"), in_=bank.ap, func=AF.Exp, scale=0.125), reads=[bank.b], writes=[et.b])
                    if ci == 2 and kt == 0:
                        P.add("dve", lambda e, kt=kt, j0=j0: e.scalar_tensor_tensor(out=PT[kt].ap, in0=et.ap, scalar=flag0, in1=EB.ap[:, kt, j0:j0 + 4, :], op0=ALU.mult, op1=ALU.mult), reads=[et.b, EB.b, flg.b], writes=[PT[kt].b])
                    else:
                        P.add("dve", lambda e, kt=kt, j0=j0: e.tensor_tensor(out=PT[kt].ap, in0=et.ap, in1=EB.ap[:, kt, j0:j0 + 4, :], op=ALU.mult), reads=[et.b, EB.b], writes=[PT[kt].b])
                if CUT2 <= 3: continue
                bank = pf[hb]
                for i in range(4):
                    for kt in range(2):
                        slk = (ci + 1 + kt) % 2
                        P.add("pe", lambda e, i=i, kt=kt, slk=slk, bank=bank, kvh=kvh: e.matmul(bank.ap[:, i * 65:(i + 1) * 65], PT[kt].ap[:, i, :], vx[slk].ap[:, kvh, :], start=(kt == 0), stop=(kt == 1)), reads=[PT[kt].b, vx[slk].b], writes=[bank.b])
                if CUT2 <= 4: continue
                pv = bank.ap[:, 0:260].rearrange("p (a b) -> p a b", a=4)
                P.add("dve", lambda e, pv=pv, j0=j0: e.tensor_tensor(out=den.ap, in0=pv[:, :, 64], in1=esink[:, j0:j0 + 4], op=ALU.add), reads=[bank.b, smallp.b], writes=[den.b])
                P.add("dve", lambda e: e.reciprocal(out=den.ap, in_=den.ap), reads=[den.b], writes=[den.b])
                P.add("dve", lambda e, pv=pv, j0=j0: e.tensor_tensor(out=ya.ap[:, j0 * 64:(j0 + 4) * 64].rearrange("p (a b) -> p a b", a=4), in0=pv[:, :, 0:64], in1=bc(den.ap, 64), op=ALU.mult), reads=[bank.b, den.b], writes=[ya.b])
        dma("sp", yad[(ci - 1) * 128:ci * 128, :], ya.ap, [ya.b], [])

    if stop < 3:
        P.emit(nc); es.close(); return nc
    P.barrier()
    AB.off, AFa.off = markB, markF
    Wg = AB.get("Wg", 8, 2048); load_w(Wg, w_in[:, COL_G:COL_G + 2048], 2048)
    Wbs = AB.get("Wbs", 16, 1024); load_w(Wbs, w_bs, 1024)
    Wba = AB.get("Wba", 8, 1024); load_w(Wba, w_ba, 1024)
    Wmx = AB.get("Wmx", 8, 1024); load_w(Wmx, w_mix, 1024)
    bgb = AFa.get("bgb", 2048); dma("sp", bgb.ap, bcast_row(b_gate, 2048), [], [bgb.b])
    lng = AFa.get("lng", 2, 1024)
    dma("sp", lng.ap[:, 0, :], bcast_row(ln1g, 1024), [], [lng.b]); dma("sp", lng.ap[:, 1, :], bcast_row(ln1b, 1024), [], [lng.b])
    xb = AB.get("xb3", 1024); xT = AB.get("xT3", 8, 128)
    ynb = AB.get("ynb", 2048); yab = AB.get("yab", 1024)
    ynT = AB.get("ynT", 16, 128); yaT = AB.get("yaT", 8, 128)
    mg = AB.get("mg", 1024); mT = AB.get("mT", 8, 128)
    xf = AFa.get("xf", 1024); gt = AFa.get("gt", 2048); m1 = AFa.get("m1", 512); r = AFa.get("r", 1024)
    st = AFa.get("st", 2, 6); mv = AFa.get("mv", 2)

    def transp(src, dst, ntl):
        for bt in range(ntl // 8):
            for i in range(8):
                j = bt * 8 + i
                P.add("pe", lambda e, i=i, j=j: e.transpose(pb[1].ap[:, i * 128:(i + 1) * 128], src.ap[:, j * 128:(j + 1) * 128], identb), reads=[src.b, cb16.b], writes=[pb[1].b])
            P.add("act", lambda e, bt=bt: e.copy(out=dst.ap[:, bt * 8:bt * 8 + 8, :].rearrange("p a b -> p (a b)"), in_=pb[1].ap), reads=[pb[1].b], writes=[dst.b])

    def layer_norm(r, g_ap, b_ap, gb, st, mv):
        for i in range(2):
            P.add("dve", lambda e, i=i: e.bn_stats(out=st.ap[:, i, :], in_=r.ap[:, i * 512:(i + 1) * 512]), reads=[r.b], writes=[st.b])
        P.add("dve", lambda e: e.bn_aggr(out=mv.ap, in_=st.ap.rearrange("p a b -> p (a b)")), reads=[st.b], writes=[mv.b])
        P.add("dve", lambda e: e.tensor_scalar(out=mv.ap[:, 1:2], in0=mv.ap[:, 1:2], scalar1=1e-5, scalar2=None, op0=ALU.add), reads=[mv.b], writes=[mv.b])
        P.add("act", lambda e: e.activation(out=mv.ap[:, 1:2], in_=mv.ap[:, 1:2], func=AF.Sqrt), reads=[mv.b], writes=[mv.b])
        P.add("dve", lambda e: e.reciprocal(out=mv.ap[:, 1:2], in_=mv.ap[:, 1:2]), reads=[mv.b], writes=[mv.b])
        P.add("dve", lambda e: e.tensor_scalar(out=r.ap, in0=r.ap, scalar1=mv.ap[:, 0:1], scalar2=mv.ap[:, 1:2], op0=ALU.subtract, op1=ALU.mult), reads=[r.b, mv.b], writes=[r.b])
        P.add("dve", lambda e: e.tensor_tensor(out=r.ap, in0=r.ap, in1=g_ap, op=ALU.mult), reads=[r.b, gb], writes=[r.b])
        P.add("dve", lambda e: e.tensor_tensor(out=r.ap, in0=r.ap, in1=b_ap, op=ALU.add), reads=[r.b, gb], writes=[r.b])

    for ci in range(NM1):
        xrow = xmain[(ci + 1) * 128:(ci + 2) * 128, :]
        load_xT(xrow, xb, xT)
        dma("sp", xf.ap, xrow, [], [xf.b])
        dma("sp", ynb.ap, ynd[ci * 128:(ci + 1) * 128, :], [], [ynb.b])
        dma("sp", yab.ap, yad[ci * 128:(ci + 1) * 128, :], [], [yab.b])
        transp(ynb, ynT, 16)
        transp(yab, yaT, 8)
        for s4 in range(4):
            bank = pf[s4 % 2]
            chain(bank, bank.ap, [(xT.ap[:, k, :], Wg.ap[:, k, s4 * 512:(s4 + 1) * 512]) for k in range(8)], [xT.b, Wg.b])
            P.add("dve", lambda e, s4=s4, bank=bank: e.tensor_tensor(out=gt.ap[:, s4 * 512:(s4 + 1) * 512], in0=bank.ap, in1=bgb.ap[:, s4 * 512:(s4 + 1) * 512], op=ALU.add), reads=[bank.b, bgb.b], writes=[gt.b])
        P.add("act", lambda e: e.activation(out=gt.ap, in_=gt.ap, func=AF.Sigmoid), reads=[gt.b], writes=[gt.b])
        for hf in range(2):
            chain(pf[2], pf[2].ap, [(ynT.ap[:, i, :], Wbs.ap[:, i, hf * 512:(hf + 1) * 512]) for i in range(16)], [ynT.b, Wbs.b])
            chain(pf[3], pf[3].ap, [(yaT.ap[:, i, :], Wba.ap[:, i, hf * 512:(hf + 1) * 512]) for i in range(8)], [yaT.b, Wba.b])
            P.add("dve", lambda e, hf=hf: e.tensor_tensor(out=m1.ap, in0=pf[2].ap, in1=gt.ap[:, hf * 512:(hf + 1) * 512], op=ALU.mult), reads=[pf[2].b, gt.b], writes=[m1.b])
            P.add("dve", lambda e, hf=hf: e.tensor_tensor(out=r.ap[:, hf * 512:(hf + 1) * 512], in0=pf[3].ap, in1=gt.ap[:, 1024 + hf * 512:1024 + (hf + 1) * 512], op=ALU.mult), reads=[pf[3].b, gt.b], writes=[r.b])
            P.add("dve", lambda e, hf=hf: e.tensor_tensor(out=mg.ap[:, hf * 512:(hf + 1) * 512], in0=m1.ap, in1=r.ap[:, hf * 512:(hf + 1) * 512], op=ALU.add), reads=[m1.b, r.b], writes=[mg.b])
        transp(mg, mT, 8)
        for hf in range(2):
            bank = pf[4 + hf]
            chain(bank, bank.ap, [(mT.ap[:, i, :], Wmx.ap[:, i, hf * 512:(hf + 1) * 512]) for i in range(8)], [mT.b, Wmx.b])
            P.add("dve", lambda e, hf=hf, bank=bank: e.scalar_tensor_tensor(out=r.ap[:, hf * 512:(hf + 1) * 512], in0=xf.ap[:, hf * 512:(hf + 1) * 512], scalar=ALPHA, in1=bank.ap, op0=ALU.mult, op1=ALU.add), reads=[xf.b, bank.b], writes=[r.b])
        layer_norm(r, lng.ap[:, 0, :], lng.ap[:, 1, :], lng.b, st, mv)
        if ci == 0:
            P.add("dve", lambda e: e.tensor_scalar(out=r.ap, in0=r.ap, scalar1=flag0, scalar2=None, op0=ALU.mult), reads=[r.b, flg.b], writes=[r.b])
        dma("sp", h1d[ci * 128:(ci + 1) * 128, :], r.ap, [r.b], [])

    if stop < 4:
        P.emit(nc); es.close(); return nc
    P.barrier()
    AB.off, AFa.off = markB, markF
    Wup = AB.get("Wup", 8, 5632); load_w(Wup, w_up, 5632)
    Wdn = AB.get("Wdn", 22, 1024); load_w(Wdn, w_dn, 1024)
    fw = AFa.get("fw", 132); fb = AFa.get("fb", 44)
    per_channel(fw.ap[:, 0:88], fcw[0:88, :], 88); fw.b.writer = P.ops["dve"][-1]
    per_channel(fw.ap[:, 88:132], fcw[88:132, :], 44); fw.b.writer = P.ops["dve"][-1]
    per_channel(fb.ap, fcb, 44); fb.b.writer = P.ops["dve"][-1]
    lng2 = AFa.get("lng2", 2, 1024)
    dma("sp", lng2.ap[:, 0, :], bcast_row(ln2g, 1024), [], [lng2.b]); dma("sp", lng2.ap[:, 1, :], bcast_row(ln2b, 1024), [], [lng2.b])
    SC = 256
    hA = AB.get("hA", 1024); hB = AB.get("hB", 1024); h1T = AB.get("h1T", 8, SC + 2)
    aT = AB.get("aT", 22, SC)
    cvs = [AFa.get("cvs%d" % i, SC) for i in range(4)]
    t0s = [AFa.get("t0s%d" % i, SC) for i in range(2)]
    sg = AFa.get("sg", SC)
    hr = AFa.get("hr", 1024); r4 = AFa.get("r4", 1024)
    st4 = AFa.get("st4", 2, 6); mv4 = AFa.get("mv4", 2)
    NT = SC // 128
    tcount = 0
    for sc in range(TOK // SC):
        r0 = 128 + sc * SC
        for i in range(NT):
            dma("pool", hA.ap, h1d[r0 - 2 + i * 128:r0 + 126 + i * 128, :], [], [hA.b])
            for k in range(8):
                P.add("pe", lambda e, k=k: e.transpose(pb[0].ap[:, k * 128:(k + 1) * 128], hA.ap[:, k * 128:(k + 1) * 128], identb), reads=[hA.b, cb16.b], writes=[pb[0].b])
            P.add("act", lambda e, i=i: e.copy(out=h1T.ap[:, :, i * 128:(i + 1) * 128], in_=pb[0].ap.rearrange("p (a b) -> p a b", a=8)), reads=[pb[0].b], writes=[h1T.b])
        dma("pool", hB.ap[0:2, :], h1d[r0 + SC - 2:r0 + SC, :], [], [hB.b])
        for k in range(8):
            P.add("pe", lambda e, k=k: e.transpose(pb[1].ap[:, k * 2:(k + 1) * 2], hB.ap[0:2, k * 128:(k + 1) * 128], identb[0:2, 0:2]), reads=[hB.b, cb16.b], writes=[pb[1].b])
        P.add("act", lambda e: e.copy(out=h1T.ap[:, :, SC:SC + 2], in_=pb[1].ap[:, 0:16].rearrange("p (a b) -> p a b", a=8)), reads=[pb[1].b], writes=[h1T.b])
        for jp in range(22):
            for gv in range(2):
                j = jp + 22 * gv
                X = pf[tcount % 4]; t0 = t0s[tcount % 2]
                cv = cvs[(jp % 2) * 2 + gv]
                tcount += 1
                chain(X, X.ap[:, 0:SC + 2], [(Wup.ap[:, k, j * 128:(j + 1) * 128], h1T.ap[:, k, 0:SC + 2]) for k in range(8)], [Wup.b, h1T.b])
                w0, w1, w2, bb = fw.ap[:, j:j + 1], fw.ap[:, 44 + j:45 + j], fw.ap[:, 88 + j:89 + j], fb.ap[:, j:j + 1]
                P.add("act", lambda e, X=X, t0=t0, w2=w2, bb=bb: e.activation(out=t0.ap, in_=X.ap[:, 2:SC + 2], func=AF.Identity, bias=bb, scale=w2), reads=[X.b, fw.b, fb.b], writes=[t0.b])
                P.add("dve", lambda e, X=X, t0=t0, w1=w1: e.scalar_tensor_tensor(out=t0.ap, in0=X.ap[:, 1:SC + 1], scalar=w1, in1=t0.ap, op0=ALU.mult, op1=ALU.add), reads=[X.b, fw.b, t0.b], writes=[t0.b])
                P.add("dve", lambda e, X=X, t0=t0, w0=w0, cv=cv: e.scalar_tensor_tensor(out=cv.ap, in0=X.ap[:, 0:SC], scalar=w0, in1=t0.ap, op0=ALU.mult, op1=ALU.add), reads=[X.b, fw.b, t0.b], writes=[cv.b])
            cg, cvv = cvs[(jp % 2) * 2], cvs[(jp % 2) * 2 + 1]
            P.add("act", lambda e, cg=cg: e.activation(out=sg.ap, in_=cg.ap, func=AF.Silu), reads=[cg.b], writes=[sg.b])
            P.add("dve", lambda e, jp=jp, cvv=cvv: e.tensor_tensor(out=aT.ap[:, jp, :], in0=sg.ap, in1=cvv.ap, op=ALU.mult), reads=[sg.b, cvv.b], writes=[aT.b])
        for tt in range(NT):
            rr = r0 + tt * 128
            dma("sp", hr.ap, h1d[rr:rr + 128, :], [], [hr.b])
            for hf in range(2):
                bank = pf[4 + hf]
                chain(bank, bank.ap, [(aT.ap[:, jj, tt * 128:(tt + 1) * 128], Wdn.ap[:, jj, hf * 512:(hf + 1) * 512]) for jj in range(22)], [aT.b, Wdn.b])
                P.add("dve", lambda e, hf=hf, bank=bank: e.scalar_tensor_tensor(out=r4.ap[:, hf * 512:(hf + 1) * 512], in0=hr.ap[:, hf * 512:(hf + 1) * 512], scalar=ALPHA, in1=bank.ap, op0=ALU.mult, op1=ALU.add), reads=[hr.b, bank.b], writes=[r4.b])
            layer_norm(r4, lng2.ap[:, 0, :], lng2.ap[:, 1, :], lng2.b, st4, mv4)
            dma("sp", out[rr - 128:rr, :], r4.ap, [r4.b], [])

    P.emit(nc)
    es.close()
    return nc


def rel_bucket_np(rel):
    n = np.maximum(rel, 0)
    nf = np.maximum(n, 1).astype(np.float32)
    large = 16 + (np.log(nf / np.float32(16)) / np.float32(np.log(128 / 16)) * np.float32(16)).astype(np.int32)
    large = np.minimum(large, 31)
    return np.where(n < 16, n, large)


_NC = None


def kernel(_dbg=None, **inp):
    global _NC
    x = np.asarray(inp["x"], np.float32)[0]
    f = lambda k: np.ascontiguousarray(np.asarray(inp[k], np.float32)[0])
    common = {
        "w_in": f("w_in"), "b_gate": f("b_gate")[None], "dtb": f("ssm_dt_bias")[None], "alog": f("ssm_a_log")[None],
        "dsk": f("ssm_d")[None], "normw": f("ssm_norm_w")[None], "sinks": f("attn_sinks")[None],
        "w_bs": f("w_branch_ssm"), "w_ba": f("w_branch_attn"), "w_mix": f("w_mix_out"),
        "ln1g": f("ln1_g")[None], "ln1b": f("ln1_b")[None], "ln2g": f("ln2_g")[None], "ln2b": f("ln2_b")[None],
        "w_up": f("w_up"), "w_dn": f("w_down"),
    }
    scw = f("ssm_conv_w")
    common["scw"] = np.ascontiguousarray(scw.reshape(4 * 24, 128))
    common["scb"] = np.ascontiguousarray(f("ssm_conv_b").reshape(24, 128))
    common["fcw"] = np.ascontiguousarray(f("ffn_conv_w").reshape(3 * 44, 128))
    common["fcb"] = np.ascontiguousarray(f("ffn_conv_b").reshape(44, 128))
    s = np.arange(128)
    ident = np.eye(128, dtype=np.float32)
    triU = (s[:, None] <= s[None, :]).astype(np.float32)
    ustr = (s[:, None] > s[None, :]).astype(np.float32)
    common["cst"] = np.ascontiguousarray(np.concatenate([ident, triU, ustr, np.ones((128, 128), np.float32)], axis=1))
    rb = np.asarray(inp["rel_bias"], np.float32)
    bg = np.zeros((128, 2, 16, 128), np.float32); mk = np.zeros((128, 2, 16, 128), np.float32)
    for kt in range(2):
        rel = (s[None, :] + 128) - (s[:, None] + 128 * kt)
        valid = (rel >= 0) & (rel < 128)
        bidx = rel_bucket_np(rel)
        g = rb[bidx]
        bg[:, kt] = np.transpose(g, (0, 2, 1))
        mk[:, kt] = np.broadcast_to(valid[:, None, :], (128, 16, 128))
    common["biasg"] = np.ascontiguousarray(bg.reshape(128, -1)); common["maskg"] = np.ascontiguousarray(mk.reshape(128, -1))
    in_maps = []
    for c in range(NCORE):
        S = c * TOK
        lo = S - 128 - NPRE * 128
        xp = np.zeros((NPRE * 128, 1024), np.float32)
        if S - 128 > 0:
            src_lo = max(lo, 0)
            xp[src_lo - lo:] = x[src_lo:S - 128]
        xm = np.zeros((NM2 * 128, 1024), np.float32)
        lo2 = S - 256
        src_lo = max(lo2, 0)
        xm[src_lo - lo2:] = x[src_lo:S + TOK]
        fl = np.zeros((128, NPRE + NM1), np.float32)
        for i in range(NPRE):
            fl[:, i] = 1.0 if lo + i * 128 >= 0 else 0.0
        fl[:, NPRE] = 1.0 if c > 0 else 0.0
        fl[:, NPRE + 1:] = 1.0
        m = dict(common); m["xpre"] = xp; m["xmain"] = xm; m["pflag"] = fl
        in_maps.append(m)
    if _dbg is not None:
        return in_maps
    if _NC is None:
        _NC = build()
    res = run_bass_kernel_spmd(_NC, in_maps, core_ids=list(range(NCORE)))
    o = np.concatenate([res.results[c]["out"] for c in range(NCORE)], axis=0)
    return o[None].astype(np.float32)
```

```python
import numpy as np
import concourse.bass as bass
import concourse.mybir as mybir

ENGS = ["pe", "act", "dve", "pool", "sp"]
N_DMA_SEM = 16
import os as _os
SAME_ENGINE_SYNC = _os.environ.get("KSES", "1") == "1"


class Buf:
    __slots__ = ("name", "writer", "readers", "dma_readers", "excl")

    def __init__(self, name, excl=False):
        self.name = name
        self.excl = excl
        self.writer = None
        self.readers = {}
        self.dma_readers = []


class Op:
    __slots__ = ("eng", "fn", "idx", "waits", "signal", "is_dma", "dsem", "dtarget", "clock", "sigcount", "uid")


class Prog:
    def __init__(self):
        self.ops = {e: [] for e in ENGS}
        self.clock = {e: {f: 0 for f in ENGS} for e in ENGS}
        self.known_dma = {e: set() for e in ENGS}
        self.dma_sem_count = [0] * N_DMA_SEM
        self.dma_sem_last = [None] * N_DMA_SEM
        self.dma_rr = 0
        self.n_dma = 0
        self.uid = 0
        self.bar = {e: [] for e in ENGS}

    def barrier(self):
        lasts = [self.ops[e][-1] for e in ENGS if self.ops[e] and not self.ops[e][-1].is_dma]
        for e in ENGS:
            pass
        lasts = []
        for e in ENGS:
            for op in reversed(self.ops[e]):
                if not op.is_dma:
                    lasts.append(op)
                    break
        dmas = [op for op in self.dma_sem_last if op is not None]
        for e in ENGS:
            self.bar[e] = lasts + dmas

    def add(self, eng, fn, reads=(), writes=(), dma=False):
        op = Op()
        op.eng = eng
        op.fn = fn
        op.idx = len(self.ops[eng])
        op.waits = []
        op.signal = False
        op.is_dma = dma
        op.uid = self.uid
        self.uid += 1
        deps = []
        for b in reads:
            if b.writer is not None:
                deps.append(b.writer)
            if b.excl:
                for e2, r in b.readers.items():
                    if e2 != eng:
                        deps.append(r)
        for b in writes:
            if b.writer is not None:
                deps.append(b.writer)
            deps.extend(b.readers.values())
            deps.extend(b.dma_readers)
        if self.bar[eng]:
            deps.extend(self.bar[eng])
            self.bar[eng] = []
        clk = self.clock[eng]
        seen = set()
        for d in deps:
            if d.uid in seen:
                continue
            seen.add(d.uid)
            if d.is_dma:
                if d.uid not in self.known_dma[eng]:
                    op.waits.append(("dma", d.dsem, d.dtarget))
                    self.known_dma[eng].add(d.uid)
            else:
                if d.eng == eng and (eng == "pe" or not SAME_ENGINE_SYNC):
                    continue
                if clk[d.eng] < d.idx + 1:
                    op.waits.append(("eng", d))
                    d.signal = True
                    for f in ENGS:
                        if d.clock[f] > clk[f]:
                            clk[f] = d.clock[f]
                    if clk[d.eng] < d.idx + 1:
                        clk[d.eng] = d.idx + 1
        if dma and eng == "pool":
            pd = self.__dict__.setdefault("pool_dmas", [])
            if len(pd) >= 4:
                d = pd[-4]
                if d.uid not in self.known_dma[eng]:
                    op.waits.append(("dma", d.dsem, d.dtarget))
                    self.known_dma[eng].add(d.uid)
            pd.append(op)
        if dma:
            k = self.dma_rr
            self.dma_rr = (self.dma_rr + 1) % N_DMA_SEM
            prev = self.dma_sem_last[k]
            if prev is not None and prev.uid not in self.known_dma[eng]:
                op.waits.append(("dma", k, prev.dtarget))
                self.known_dma[eng].add(prev.uid)
            self.dma_sem_count[k] += 16
            op.dsem = k
            op.dtarget = self.dma_sem_count[k]
            self.dma_sem_last[k] = op
            self.n_dma += 1
        op.clock = dict(clk)
        if not SAME_ENGINE_SYNC or eng == "pe":
            pass
        self.ops[eng].append(op)
        for b in writes:
            b.writer = op
            b.readers = {}
            b.dma_readers = []
        for b in reads:
            if dma:
                b.dma_readers.append(op)
            else:
                b.readers[eng] = op
        return op

    def emit(self, nc, final_waits=True):
        for e in ENGS:
            c = 0
            for op in self.ops[e]:
                if op.signal:
                    c += 1
                op.sigcount = c
        from contextlib import ExitStack
        with ExitStack() as es:
            esem = {e: es.enter_context(nc.semaphore("s_" + e)) for e in ENGS}
            dsem = [es.enter_context(nc.semaphore("d%d" % i)) for i in range(N_DMA_SEM)]
            block = es.enter_context(nc.Block())
            last_dma = [op for op in self.dma_sem_last if op is not None]

            def run(e, h):
                for op in self.ops[e]:
                    for w in op.waits:
                        if w[0] == "dma":
                            h.wait_ge(dsem[w[1]], w[2])
                        else:
                            h.wait_ge(esem[w[1].eng], w[1].sigcount)
                    ins = op.fn(h)
                    if op.is_dma:
                        ins.then_inc(dsem[op.dsem], 16)
                    elif op.signal:
                        ins.then_inc(esem[e], 1)
                if e == "sp" and final_waits:
                    for k in range(N_DMA_SEM):
                        if self.dma_sem_count[k] > 0:
                            h.wait_ge(dsem[k], self.dma_sem_count[k])

            @block.tensor
            def _(h):
                run("pe", h)

            @block.scalar
            def _(h):
                run("act", h)

            @block.vector
            def _(h):
                run("dve", h)

            @block.gpsimd
            def _(h):
                run("pool", h)

            @block.sync
            def _(h):
                run("sp", h)

from contextlib import ExitStack
import ml_dtypes
from concourse.bass_utils import run_bass_kernel_spmd

F32 = mybir.dt.float32
BF16 = mybir.dt.bfloat16
AF = mybir.ActivationFunctionType
ALU = mybir.AluOpType

NCORE = 8
TOK = 2048
NPRE = 112
NM1 = 17
NM2 = 18
ALPHA = 2.0 ** 0.25
COL_Z, COL_X, COL_B, COL_C, COL_DT, COL_Q, COL_K, COL_V, COL_G = 0, 2048, 4096, 4608, 5120, 5152, 6176, 6304, 6432


class TT:
    def __init__(self, ap, name, excl=False):
        self.ap = ap
        self.b = Buf(name, excl)


class Arena:
    def __init__(self, t, n):
        self.t, self.n, self.off = t, n, 0

    def get(self, name, *fs):
        n = int(np.prod(fs))
        ap = self.t[:, self.off:self.off + n]
        self.off += n
        assert self.off <= self.n, (name, self.off, self.n)
        if len(fs) == 2:
            ap = ap.rearrange("p (a b) -> p a b", a=fs[0])
        elif len(fs) == 3:
            ap = ap.rearrange("p (a b c) -> p a b c", a=fs[0], b=fs[1])
        return TT(ap, name)


def bc(ap2, n):
    return ap2.unsqueeze(2).to_broadcast([ap2.shape[0], ap2.shape[1], n])


def build(stop=4, npre=NPRE, dbg=False):
    nc = bass.Bass("TRN2", target_bir_lowering=False)
    dt_in = lambda n, s: nc.dram_tensor(n, s, F32, kind="ExternalInput").ap()
    xpre = dt_in("xpre", [NPRE * 128, 1024])
    xmain = dt_in("xmain", [NM2 * 128, 1024])
    pflag = dt_in("pflag", [128, NPRE + NM1])
    w_in = dt_in("w_in", [1024, 8480])
    b_gate = dt_in("b_gate", [1, 2048])
    scw = dt_in("scw", [96, 128])
    scb = dt_in("scb", [24, 128])
    dtb = dt_in("dtb", [1, 32])
    alog = dt_in("alog", [1, 32])
    dsk = dt_in("dsk", [1, 32])
    normw = dt_in("normw", [1, 2048])
    sinks = dt_in("sinks", [1, 16])
    w_bs = dt_in("w_bs", [2048, 1024])
    w_ba = dt_in("w_ba", [1024, 1024])
    w_mix = dt_in("w_mix", [1024, 1024])
    ln1g = dt_in("ln1g", [1, 1024]); ln1b = dt_in("ln1b", [1, 1024])
    ln2g = dt_in("ln2g", [1, 1024]); ln2b = dt_in("ln2b", [1, 1024])
    w_up = dt_in("w_up", [1024, 5632])
    fcw = dt_in("fcw", [132, 128])
    fcb = dt_in("fcb", [44, 128])
    w_dn = dt_in("w_dn", [2816, 1024])
    cst = dt_in("cst", [128, 4 * 128])
    biasg = dt_in("biasg", [128, 2 * 16 * 128])
    maskg = dt_in("maskg", [128, 2 * 16 * 128])
    out = nc.dram_tensor("out", [TOK, 1024], F32, kind="ExternalOutput").ap()
    skind = "ExternalOutput" if dbg else "Internal"
    ynd = nc.dram_tensor("ynd", [NM1 * 128, 2048], BF16, kind=skind).ap()
    yad = nc.dram_tensor("yad", [NM1 * 128, 1024], BF16, kind=skind).ap()
    h1d = nc.dram_tensor("h1d", [NM1 * 128, 1024], F32, kind=skind).ap()

    P = Prog()
    es = ExitStack()
    NB, NF = 153 * 512, 47 * 256
    ABt = es.enter_context(nc.sbuf_tensor("AB", [128, NB], BF16))
    AFt = es.enter_context(nc.sbuf_tensor("AF", [128, NF], F32))
    pf = [TT(es.enter_context(nc.psum_tensor("pf%d" % i, [128, 512], F32))[:], "pf%d" % i, True) for i in range(6)]
    pb = [TT(es.enter_context(nc.psum_tensor("pb%d" % i, [128, 1024], BF16))[:], "pb%d" % i, True) for i in range(2)]
    AB = Arena(ABt, NB)
    AFa = Arena(AFt, NF)

    def bcast_row(src, n):
        return bass.AP(src.tensor, 0, [[0, 128], [1, n]])

    def dma(eng, o, i, reads, writes):
        P.add(eng, lambda e, o=o, i=i: e.dma_start(out=o, in_=i), reads=reads, writes=writes, dma=True)

    cf = AFa.get("cf", 4, 128)
    dma("sp", cf.ap, cst.rearrange("p (a b) -> p a b", a=4), [], [cf.b])
    identf, triU, Ustr, onesf = cf.ap[:, 0, :], cf.ap[:, 1, :], cf.ap[:, 2, :], cf.ap[:, 3, :]
    cb16 = AB.get("cb16", 4, 128)
    dma("pool", cb16.ap, cst.rearrange("p (a b) -> p a b", a=4), [], [cb16.b])
    identb, maskb = cb16.ap[:, 0, :], cb16.ap[:, 1, :]
    flg = AFa.get("flg", NPRE + NM1)
    dma("sp", flg.ap, pflag, [], [flg.b])
    smallp = AFa.get("smallp", 6, 32)
    dma("sp", smallp.ap[:, 0, :], bcast_row(dtb, 32), [], [smallp.b])
    dma("sp", smallp.ap[:, 1, :], bcast_row(alog, 32), [], [smallp.b])
    dma("sp", smallp.ap[:, 2, :], bcast_row(dsk, 32), [], [smallp.b])
    dma("sp", smallp.ap[:, 3, 0:16], bcast_row(sinks, 16), [], [smallp.b])
    P.add("act", lambda e: e.activation(out=smallp.ap[:, 1, :], in_=smallp.ap[:, 1, :], func=AF.Exp), reads=[smallp.b], writes=[smallp.b])
    P.add("dve", lambda e: e.tensor_scalar(out=smallp.ap[:, 1, :], in0=smallp.ap[:, 1, :], scalar1=-1.0, scalar2=None, op0=ALU.mult), reads=[smallp.b], writes=[smallp.b])
    P.add("act", lambda e: e.activation(out=smallp.ap[:, 3, 0:16], in_=smallp.ap[:, 3, 0:16], func=AF.Exp), reads=[smallp.b], writes=[smallp.b])
    dtb_bc, a_bc, D_bc, esink = smallp.ap[:, 0, :], smallp.ap[:, 1, :], smallp.ap[:, 2, :], smallp.ap[:, 3, 0:16]
    onecol = onesf[:, 0:1]
    rawh = AB.get("rawh", 24, 4)
    markB, markF = AB.off, AFa.off
    H = AFa.get("H", 2048)

    def load_w(dst, src, ncols, nk=8):
        nk = src.shape[0] // 128
        for c0 in range(0, ncols, 512):
            c1 = min(c0 + 512, ncols)
            for k in range(nk):
                dma("pool", dst.ap[:, k, c0:c1], src[k * 128:(k + 1) * 128, c0:c1], [], [dst.b])

    def load_xT(xrow_ap, xb, xT):
        dma("pool", xb.ap, xrow_ap, [], [xb.b])
        for k in range(8):
            P.add("pe", lambda e, k=k: e.transpose(pb[0].ap[:, k * 128:(k + 1) * 128], xb.ap[:, k * 128:(k + 1) * 128], identb), reads=[xb.b, cb16.b], writes=[pb[0].b])
        P.add("act", lambda e: e.copy(out=xT.ap.rearrange("p a b -> p (a b)"), in_=pb[0].ap), reads=[pb[0].b], writes=[xT.b])

    def chain(bank, out_ap, pairs, reads):
        n = len(pairs)
        for i, (l, r) in enumerate(pairs):
            P.add("pe", lambda e, l=l, r=r, i=i: e.matmul(out_ap, l, r, start=(i == 0), stop=(i == n - 1)), reads=reads, writes=[bank.b])

    def per_channel(dst, src_dram, rows):
        tmp = AFa.get("pc_tmp", 128)
        dma("sp", tmp.ap[0:rows, :], src_dram, [], [tmp.b])
        P.add("pe", lambda e: e.transpose(pf[5].ap[:, 0:rows], tmp.ap[0:rows, :], identf[0:rows, 0:rows]), reads=[tmp.b, cf.b], writes=[pf[5].b])
        P.add("dve", lambda e: e.tensor_copy(out=dst, in_=pf[5].ap[:, 0:rows]), reads=[pf[5].b], writes=[])

    import os
    SKIP1 = int(os.environ.get('KSKIP1', '0'))
    Wx = AB.get("Wx", 8, 3072); load_w(Wx, w_in[:, COL_X:COL_X + 3072], 3072)
    Wdt = AB.get("Wdt", 8, 32); load_w(Wdt, w_in[:, COL_DT:COL_DT + 32], 32)
    cwt = AFa.get("cwt", 96); cbt = AFa.get("cbt", 24)
    per_channel(cwt.ap, scw, 96)
    per_channel(cbt.ap, scb, 24)
    cwt.b.writer = P.ops["dve"][-2]; cbt.b.writer = P.ops["dve"][-1]
    diag = AB.get("diag", 24, 4, 128)
    for j in range(24):
        for tp in range(4):
            P.add("dve", lambda e, j=j, tp=tp: e.tensor_scalar(out=diag.ap[:, j, tp, :], in0=identf, scalar1=cwt.ap[:, tp * 24 + j:tp * 24 + j + 1], scalar2=None, op0=ALU.mult), reads=[cf.b, cwt.b], writes=[diag.b])
    P.add("dve", lambda e: e.memset(H.ap, 0.0), writes=[H.b])
    mark1B, mark1F = AB.off, AFa.off
    xbA = [AB.get("xbA%d" % i, 1024) for i in range(2)]
    xT4 = [AB.get("xT4_%d" % i, 8, 512) for i in range(2)]
    raw4 = AB.get("raw4", 24, 516); xc4 = AB.get("xc4", 24, 512)
    rawb = [Buf("rawb%d" % j) for j in range(24)]; xcb = [Buf("xcb%d" % j) for j in range(24)]
    xtk = [AB.get("xtk%d" % i, 2560) for i in range(2)]
    xdtsA = AB.get("xdtsA", 512)
    P.add("pool", lambda e: e.memset(raw4.ap, 0.0), writes=rawb)

    def load_group(gi, buf):
        for q in range(4):
            c = gi * 4 + q
            xb_ = xbA[q % 2]
            dma("pool", xb_.ap, xpre[c * 128:(c + 1) * 128, :], [], [xb_.b])
            for k in range(8):
                P.add("pe", lambda e, k=k, xb_=xb_: e.transpose(pb[0].ap[:, k * 128:(k + 1) * 128], xb_.ap[:, k * 128:(k + 1) * 128], identb), reads=[xb_.b, cb16.b], writes=[pb[0].b])
            P.add("act", lambda e, q=q, buf=buf: e.copy(out=xT4[buf].ap[:, :, q * 128:(q + 1) * 128], in_=pb[0].ap.rearrange("p (a b) -> p a b", a=8)), reads=[pb[0].b], writes=[xT4[buf].b])

    def group_proj(gi, buf, last):
        ntile = 24 if last else 20
        for j in range(ntile):
            bank = pf[j % 2]
            chain(bank, bank.ap, [(Wx.ap[:, k, j * 128:(j + 1) * 128], xT4[buf].ap[:, k, :]) for k in range(8)], [Wx.b, xT4[buf].b])
            P.add("act", lambda e, j=j, bank=bank: e.copy(out=raw4.ap[:, j, 3:515], in_=bank.ap), reads=[bank.b], writes=[rawb[j]])
        for j in range(ntile):
            bank = pf[2 + j % 2]
            chain(bank, bank.ap, [(diag.ap[:, j, tp, :], raw4.ap[:, j, tp:tp + 512]) for tp in range(4)], [diag.b, rawb[j]])
            P.add("act", lambda e, j=j, bank=bank: e.activation(out=xc4.ap[:, j, :], in_=bank.ap, func=AF.Silu, bias=cbt.ap[:, j:j + 1], scale=1.0), reads=[bank.b, cbt.b], writes=[xcb[j]])
        P.add("pool", lambda e: e.tensor_copy(out=raw4.ap[:, :, 0:3], in_=raw4.ap[:, :, 512:515]), reads=rawb, writes=rawb)

    smGs = [AFa.get("smG%d" % i, 8, 128) for i in range(2)]

    def group_chunks(gi, buf):
        c0 = gi * 4
        smG = smGs[gi % 2]
        S = lambda i: smG.ap[:, i, :]
        S3 = lambda i: smG.ap[:, i, :].rearrange("p (q h) -> p q h", q=4)
        b4 = lambda ap: ap.unsqueeze(1).to_broadcast([128, 4, 32])
        v3 = lambda ap: ap.rearrange("p (a b) -> p a b", a=8)
        for q in range(4):
            chain(pf[4], pf[4].ap[:, q * 32:(q + 1) * 32], [(xT4[buf].ap[:, k, q * 128:(q + 1) * 128], Wdt.ap[:, k, :]) for k in range(8)], [xT4[buf].b, Wdt.b])
        P.add("dve", lambda e: e.tensor_tensor(out=S3(0), in0=pf[4].ap[:, 0:128].rearrange("p (q h) -> p q h", q=4), in1=b4(dtb_bc), op=ALU.add), reads=[pf[4].b, smallp.b], writes=[smG.b])
        P.add("act", lambda e: e.activation(out=S(0), in_=S(0), func=AF.Exp), reads=[smG.b], writes=[smG.b])
        P.add("act", lambda e: e.activation(out=S(1), in_=S(0), func=AF.Ln, bias=onecol, scale=1.0), reads=[smG.b, cf.b], writes=[smG.b])
        P.add("dve", lambda e: e.tensor_tensor(out=S3(1), in0=S3(1), in1=bc(flg.ap[:, c0:c0 + 4], 32), op=ALU.mult), reads=[smG.b, flg.b], writes=[smG.b])
        P.add("dve", lambda e: e.tensor_tensor(out=S3(2), in0=S3(1), in1=b4(a_bc), op=ALU.mult), reads=[smG.b, smallp.b], writes=[smG.b])

        def tr(q):
            xt = xtk[q % 2]
            for bt in range(3):
                nt = 8 if bt < 2 else 4
                for i in range(nt):
                    jj = bt * 8 + i
                    P.add("pe", lambda e, i=i, jj=jj, q=q: e.transpose(pb[1].ap[:, i * 128:(i + 1) * 128], xc4.ap[:, jj, q * 128:(q + 1) * 128], identb), reads=[xcb[jj], cb16.b], writes=[pb[1].b])
                P.add("dve", lambda e, bt=bt, nt=nt, xt=xt: e.tensor_copy(out=xt.ap[:, bt * 1024:bt * 1024 + nt * 128], in_=pb[1].ap[:, 0:nt * 128]), reads=[pb[1].b], writes=[xt.b])

        def state(q):
            xt = xtk[q % 2]
            for g in range(4):
                xg = xt.ap[:, g * 512:(g + 1) * 512].rearrange("p (a b) -> p a b", a=8)
                o = q * 32 + 8 * g
                P.add("dve", lambda e, xg=xg, o=o: e.tensor_tensor(out=v3(xdtsA.ap), in0=xg, in1=bc(smG.ap[:, 6, o:o + 8], 64), op=ALU.mult), reads=[xt.b, smG.b], writes=[xdtsA.b])
                P.add("pe", lambda e, g=g, xt=xt: e.matmul(pf[5].ap, xt.ap[:, 2048 + g * 128:2048 + (g + 1) * 128], xdtsA.ap, start=True, stop=True), reads=[xt.b, xdtsA.b], writes=[pf[5].b])
                Hg = H.ap[:, g * 512:(g + 1) * 512]
                P.add("dve", lambda e, Hg=Hg, o=o: e.tensor_tensor(out=v3(Hg), in0=v3(Hg), in1=bc(smG.ap[:, 7, o:o + 8], 64), op=ALU.mult), reads=[H.b, smG.b], writes=[H.b])
                P.add("dve", lambda e, Hg=Hg: e.tensor_tensor(out=Hg, in0=Hg, in1=pf[5].ap, op=ALU.add), reads=[H.b, pf[5].b], writes=[H.b])

        tr(0); tr(1)
        P.add("pe", lambda e: e.matmul(pf[4].ap[:, 128:256], triU, S(2), start=True, stop=True), reads=[cf.b, smG.b], writes=[pf[4].b])
        P.add("pe", lambda e: e.matmul(pf[4].ap[:, 256:384], onesf, S(2), start=True, stop=True), reads=[cf.b, smG.b], writes=[pf[4].b])
        P.add("dve", lambda e: e.tensor_copy(out=smG.ap[:, 3:5, :], in_=pf[4].ap[:, 128:384].rearrange("p (a b) -> p a b", a=2)), reads=[pf[4].b], writes=[smG.b])
        P.add("dve", lambda e: e.tensor_tensor(out=S(6), in0=S(4), in1=S(3), op=ALU.subtract), reads=[smG.b], writes=[smG.b])
        P.add("act", lambda e: e.activation(out=S(6), in_=S(6), func=AF.Exp), reads=[smG.b], writes=[smG.b])
        P.add("act", lambda e: e.activation(out=S(7), in_=S(4), func=AF.Exp), reads=[smG.b], writes=[smG.b])
        P.add("dve", lambda e: e.tensor_tensor(out=S(6), in0=S(6), in1=S(1), op=ALU.mult), reads=[smG.b], writes=[smG.b])
        state(0); state(1)
        tr(2); tr(3)
        state(2); state(3)

    NG = NPRE // 4
    g0 = NG - (npre + 3) // 4
    if not SKIP1 and g0 < NG:
        load_group(g0, g0 % 2)
        for gi in range(g0, NG):
            group_proj(gi, gi % 2, gi == NG - 1)
            if gi + 1 < NG:
                load_group(gi + 1, (gi + 1) % 2)
            group_chunks(gi, gi % 2)
        P.add("pool", lambda e: e.tensor_copy(out=rawh.ap[:, :, 0:3], in_=raw4.ap[:, :, 0:3]), reads=rawb, writes=[rawh.b])
    else:
        P.add("pool", lambda e: e.memset(rawh.ap, 0.0), writes=[rawh.b])

    P.barrier()
    AB.off, AFa.off = mark1B, mark1F
    Wz = AB.get("Wz", 8, 2048); load_w(Wz, w_in[:, COL_Z:COL_Z + 2048], 2048)
    nwb = AFa.get("nwb", 2048)
    dma("sp", nwb.ap, bcast_row(normw, 2048), [], [nwb.b])
    xb = AB.get("xb", 1024); xT = AB.get("xT", 8, 128)
    raw = AB.get("raw", 24, 132); xc = AB.get("xc", 24, 128)
    xtok = AB.get("xtok", 2560)
    xdt = AB.get("xdt", 512); xdts = AB.get("xdts", 512)
    Hb = AB.get("Hb", 2048); cbm = AB.get("cbm", 4, 128)
    LT = AB.get("LT", 4, 128); MT = AB.get("MT", 8, 128); yn = AB.get("yn", 2048)
    adtU = [AFa.get("adtU%d" % i, 128) for i in range(2)]
    sm = AFa.get("sm", 12, 32)
    yacc = AFa.get("yacc", 512); ytmp = AFa.get("ytmp", 512); sz = AFa.get("sz", 512)
    ssq = AFa.get("ssq", 4)
    P.add("pool", lambda e: e.memset(raw.ap, 0.0), writes=[raw.b])
    P.add("pool", lambda e: e.tensor_copy(out=raw.ap[:, :, 0:3], in_=rawh.ap[:, :, 0:3]), reads=[rawh.b], writes=[raw.b])

    CUT = int(os.environ.get('KCUT', '99')); NCH1 = int(os.environ.get('KNCH', str(NM1)))
    def ssd_chunk(xrow, fcol, main, ci):
        ntile = 24
        load_xT(xrow, xb, xT)
        if CUT <= 1: return
        for g in range(ntile // 4):
            bank = pf[g % 2]
            for jj in range(4):
                j = 4 * g + jj
                chain(bank, bank.ap[:, jj * 128:(jj + 1) * 128], [(Wx.ap[:, k, j * 128:(j + 1) * 128], xT.ap[:, k, :]) for k in range(8)], [Wx.b, xT.b])
            P.add("act", lambda e, g=g, bank=bank: e.copy(out=raw.ap[:, 4 * g:4 * g + 4, 3:131], in_=bank.ap.rearrange("p (a b) -> p a b", a=4)), reads=[bank.b], writes=[raw.b])
        if CUT <= 2: return
        for g in range(ntile // 4):
            bank = pf[2 + g % 2]
            for jj in range(4):
                j = 4 * g + jj
                chain(bank, bank.ap[:, jj * 128:(jj + 1) * 128], [(diag.ap[:, j, tp, :], raw.ap[:, j, tp:tp + 128]) for tp in range(4)], [diag.b, raw.b])
                P.add("act", lambda e, j=j, jj=jj, bank=bank: e.activation(out=xc.ap[:, j, :], in_=bank.ap[:, jj * 128:(jj + 1) * 128], func=AF.Silu, bias=cbt.ap[:, j:j + 1], scale=1.0), reads=[bank.b, cbt.b], writes=[xc.b])
        if CUT <= 3: return
        P.add("pool", lambda e: e.tensor_copy(out=raw.ap[:, :, 0:3], in_=raw.ap[:, :, 128:131]), reads=[raw.b], writes=[raw.b])
        for bt in range(3):
            nt = 8 if bt < 2 else 4
            for i in range(nt):
                j = bt * 8 + i
                P.add("pe", lambda e, i=i, j=j: e.transpose(pb[1].ap[:, i * 128:(i + 1) * 128], xc.ap[:, j, :], identb), reads=[xc.b, cb16.b], writes=[pb[1].b])
            P.add("dve", lambda e, bt=bt, nt=nt: e.tensor_copy(out=xtok.ap[:, bt * 1024:bt * 1024 + nt * 128], in_=pb[1].ap[:, 0:nt * 128]), reads=[pb[1].b], writes=[xtok.b])
        if CUT <= 4: return
        chain(pf[4], pf[4].ap[:, 0:32], [(xT.ap[:, k, :], Wdt.ap[:, k, :]) for k in range(8)], [xT.b, Wdt.b])
        S = lambda i: sm.ap[:, i, :]
        P.add("dve", lambda e: e.tensor_tensor(out=S(0), in0=pf[4].ap[:, 0:32], in1=dtb_bc, op=ALU.add), reads=[pf[4].b, smallp.b], writes=[sm.b])
        P.add("act", lambda e: e.activation(out=S(0), in_=S(0), func=AF.Exp), reads=[sm.b], writes=[sm.b])
        P.add("act", lambda e: e.activation(out=S(1), in_=S(0), func=AF.Ln, bias=onecol, scale=1.0), reads=[sm.b, cf.b], writes=[sm.b])
        P.add("dve", lambda e: e.tensor_scalar(out=S(1), in0=S(1), scalar1=fcol, scalar2=None, op0=ALU.mult), reads=[sm.b, flg.b], writes=[sm.b])
        P.add("dve", lambda e: e.tensor_tensor(out=S(2), in0=S(1), in1=a_bc, op=ALU.mult), reads=[sm.b, smallp.b], writes=[sm.b])
        if CUT <= 5: return
        P.add("pe", lambda e: e.matmul(pf[4].ap[:, 32:64], triU, S(2), start=True, stop=True), reads=[cf.b, sm.b], writes=[pf[4].b])
        P.add("pe", lambda e: e.matmul(pf[4].ap[:, 64:96], onesf, S(2), start=True, stop=True), reads=[cf.b, sm.b], writes=[pf[4].b])
        P.add("dve", lambda e: e.tensor_copy(out=sm.ap[:, 3:5, :], in_=pf[4].ap[:, 32:96].rearrange("p (a b) -> p a b", a=2)), reads=[pf[4].b], writes=[sm.b])
        if CUT <= 6: return
        if main and CUT > 7:
            for g in range(4):
                P.add("pe", lambda e, g=g: e.matmul(pf[5].ap[:, g * 128:(g + 1) * 128], xc.ap[:, 16 + g, :], xc.ap[:, 20 + g, :], start=True, stop=True), reads=[xc.b], writes=[pf[5].b])
            P.add("dve", lambda e: e.tensor_tensor(out=cbm.ap, in0=pf[5].ap.rearrange("p (a b) -> p a b", a=4), in1=maskb.unsqueeze(1).to_broadcast([128, 4, 128]), op=ALU.mult), reads=[pf[5].b, cb16.b], writes=[cbm.b])
            P.add("pool", lambda e: e.tensor_copy(out=Hb.ap, in_=H.ap), reads=[H.b], writes=[Hb.b])
            P.add("act", lambda e: e.activation(out=S(5), in_=S(3), func=AF.Exp), reads=[sm.b], writes=[sm.b])
            for g in range(4):
                xg = xtok.ap[:, g * 512:(g + 1) * 512].rearrange("p (a b) -> p a b", a=8)
                P.add("dve", lambda e, g=g, xg=xg: e.tensor_tensor(out=xdt.ap.rearrange("p (a b) -> p a b", a=8), in0=xg, in1=bc(sm.ap[:, 1, 8 * g:8 * g + 8], 64), op=ALU.mult), reads=[xtok.b, sm.b], writes=[xdt.b])
                for hh in range(2):
                    bank = pf[hh]
                    for h4 in range(4):
                        h = 8 * g + 4 * hh + h4
                        au = adtU[h4 % 2]
                        P.add("dve", lambda e, h=h, au=au: e.tensor_scalar(out=au.ap, in0=Ustr, scalar1=sm.ap[:, 2, h:h + 1], scalar2=None, op0=ALU.mult), reads=[cf.b, sm.b], writes=[au.b])
                        P.add("pe", lambda e, h4=h4, au=au, bank=bank: e.matmul(bank.ap[:, h4 * 128:(h4 + 1) * 128], au.ap, triU, start=True, stop=True), reads=[au.b, cf.b], writes=[bank.b])
                    P.add("act", lambda e, bank=bank: e.activation(out=LT.ap.rearrange("p a b -> p (a b)"), in_=bank.ap, func=AF.Exp), reads=[bank.b], writes=[LT.b])
                    P.add("dve", lambda e, g=g, hh=hh: e.tensor_tensor(out=MT.ap[:, 4 * hh:4 * hh + 4, :], in0=LT.ap, in1=cbm.ap[:, g, :].unsqueeze(1).to_broadcast([128, 4, 128]), op=ALU.mult), reads=[LT.b, cbm.b], writes=[MT.b])
                for h8 in range(8):
                    P.add("pe", lambda e, h8=h8: e.matmul(pf[2].ap[:, h8 * 64:(h8 + 1) * 64], MT.ap[:, h8, :], xdt.ap[:, h8 * 64:(h8 + 1) * 64], start=True, stop=True), reads=[MT.b, xdt.b], writes=[pf[2].b])
                P.add("pe", lambda e, g=g: e.matmul(pf[3].ap, xc.ap[:, 20 + g, :], Hb.ap[:, g * 512:(g + 1) * 512], start=True, stop=True), reads=[xc.b, Hb.b], writes=[pf[3].b])
                v3 = lambda ap: ap.rearrange("p (a b) -> p a b", a=8)
                P.add("dve", lambda e, g=g: e.tensor_tensor(out=v3(yacc.ap), in0=v3(pf[3].ap), in1=bc(sm.ap[:, 5, 8 * g:8 * g + 8], 64), op=ALU.mult), reads=[pf[3].b, sm.b], writes=[yacc.b])
                P.add("dve", lambda e: e.tensor_tensor(out=yacc.ap, in0=yacc.ap, in1=pf[2].ap, op=ALU.add), reads=[pf[2].b, yacc.b], writes=[yacc.b])
                P.add("dve", lambda e, g=g, xg=xg: e.tensor_tensor(out=v3(ytmp.ap), in0=xg, in1=bc(D_bc[:, 8 * g:8 * g + 8], 64), op=ALU.mult), reads=[xtok.b, smallp.b], writes=[ytmp.b])
                P.add("dve", lambda e: e.tensor_tensor(out=yacc.ap, in0=yacc.ap, in1=ytmp.ap, op=ALU.add), reads=[ytmp.b, yacc.b], writes=[yacc.b])
                chain(pf[5], pf[5].ap, [(xT.ap[:, k, :], Wz.ap[:, k, g * 512:(g + 1) * 512]) for k in range(8)], [xT.b, Wz.b])
                P.add("act", lambda e: e.activation(out=sz.ap, in_=pf[5].ap, func=AF.Silu), reads=[pf[5].b], writes=[sz.b])
                P.add("dve", lambda e: e.tensor_tensor(out=yacc.ap, in0=yacc.ap, in1=sz.ap, op=ALU.mult), reads=[sz.b, yacc.b], writes=[yacc.b])
                P.add("act", lambda e, g=g: e.activation(out=ytmp.ap, in_=yacc.ap, func=AF.Square, accum_out=ssq.ap[:, g:g + 1]), reads=[yacc.b], writes=[ytmp.b, ssq.b])
                P.add("dve", lambda e, g=g: e.tensor_scalar(out=ssq.ap[:, g:g + 1], in0=ssq.ap[:, g:g + 1], scalar1=1.0 / 512, scalar2=1e-5, op0=ALU.mult, op1=ALU.add), reads=[ssq.b], writes=[ssq.b])
                P.add("act", lambda e, g=g: e.activation(out=ssq.ap[:, g:g + 1], in_=ssq.ap[:, g:g + 1], func=AF.Sqrt), reads=[ssq.b], writes=[ssq.b])
                P.add("dve", lambda e, g=g: e.reciprocal(out=ssq.ap[:, g:g + 1], in_=ssq.ap[:, g:g + 1]), reads=[ssq.b], writes=[ssq.b])
                P.add("dve", lambda e, g=g: e.scalar_tensor_tensor(out=yn.ap[:, g * 512:(g + 1) * 512], in0=yacc.ap, scalar=ssq.ap[:, g:g + 1], in1=nwb.ap[:, g * 512:(g + 1) * 512], op0=ALU.mult, op1=ALU.mult), reads=[yacc.b, ssq.b, nwb.b], writes=[yn.b])
            dma("sp", ynd[ci * 128:(ci + 1) * 128, :], yn.ap, [yn.b], [])
        if CUT <= 8: return
        P.add("dve", lambda e: e.tensor_tensor(out=S(6), in0=S(4), in1=S(3), op=ALU.subtract), reads=[sm.b], writes=[sm.b])
        P.add("act", lambda e: e.activation(out=S(6), in_=S(6), func=AF.Exp), reads=[sm.b], writes=[sm.b])
        P.add("act", lambda e: e.activation(out=S(7), in_=S(4), func=AF.Exp), reads=[sm.b], writes=[sm.b])
        P.add("dve", lambda e: e.tensor_tensor(out=S(6), in0=S(6), in1=S(1), op=ALU.mult), reads=[sm.b], writes=[sm.b])
        for g in range(4):
            xg = xtok.ap[:, g * 512:(g + 1) * 512].rearrange("p (a b) -> p a b", a=8)
            v3 = lambda ap: ap.rearrange("p (a b) -> p a b", a=8)
            P.add("dve", lambda e, g=g, xg=xg: e.tensor_tensor(out=v3(xdts.ap), in0=xg, in1=bc(sm.ap[:, 6, 8 * g:8 * g + 8], 64), op=ALU.mult), reads=[xtok.b, sm.b], writes=[xdts.b])
            bank = pf[2 + g % 2]
            P.add("pe", lambda e, g=g, bank=bank: e.matmul(bank.ap, xtok.ap[:, 2048 + g * 128:2048 + (g + 1) * 128], xdts.ap, start=True, stop=True), reads=[xtok.b, xdts.b], writes=[bank.b])
            Hg = H.ap[:, g * 512:(g + 1) * 512]
            P.add("dve", lambda e, g=g, Hg=Hg: e.tensor_tensor(out=v3(Hg), in0=v3(Hg), in1=bc(sm.ap[:, 7, 8 * g:8 * g + 8], 64), op=ALU.mult), reads=[H.b, sm.b], writes=[H.b])
            P.add("dve", lambda e, Hg=Hg, bank=bank: e.tensor_tensor(out=Hg, in0=Hg, in1=bank.ap, op=ALU.add), reads=[H.b, bank.b], writes=[H.b])

    for ci in range(0 if SKIP1 else NCH1):
        ssd_chunk(xmain[(ci + 1) * 128:(ci + 2) * 128, :], flg.ap[:, NPRE + ci:NPRE + ci + 1], True, ci)

    if stop < 2:
        P.emit(nc); es.close(); return nc
    P.barrier()
    AB.off, AFa.off = markB, markF
    Wq = AB.get("Wq", 8, 1024); load_w(Wq, w_in[:, COL_Q:COL_Q + 1024], 1024)
    Wk2 = AB.get("Wk2", 8, 128); load_w(Wk2, w_in[:, COL_K:COL_K + 128], 128)
    Wv = AB.get("Wv", 8, 128); load_w(Wv, w_in[:, COL_V:COL_V + 128], 128)
    EB = AB.get("EB", 2, 16, 128)
    ebf = AFa.get("ebf", 2048); mkf = AFa.get("mkf", 2048)
    for kt in range(2):
        dma("sp", ebf.ap, biasg[:, kt * 2048:(kt + 1) * 2048], [], [ebf.b])
        dma("sp", mkf.ap, maskg[:, kt * 2048:(kt + 1) * 2048], [], [mkf.b])
        P.add("act", lambda e: e.activation(out=ebf.ap, in_=ebf.ap, func=AF.Exp), reads=[ebf.b], writes=[ebf.b])
        P.add("dve", lambda e, kt=kt: e.tensor_tensor(out=EB.ap[:, kt, :, :].rearrange("p a b -> p (a b)"), in0=ebf.ap, in1=mkf.ap, op=ALU.mult), reads=[ebf.b, mkf.b], writes=[EB.b])
    xb = AB.get("xb2", 1024); xT = AB.get("xT2", 8, 128)
    kT = [AB.get("kT%d" % i, 2, 128) for i in range(2)]
    vx = [AB.get("vx%d" % i, 2, 65) for i in range(2)]
    for i in range(2):
        P.add("pool", lambda e, i=i: e.memset(vx[i].ap, 1.0), writes=[vx[i].b])
    qT = AB.get("qT", 16, 128); et = AB.get("et", 4, 128)
    PT = [AB.get("PT%d" % i, 4, 128) for i in range(2)]
    ya = AB.get("ya", 1024)
    den = AFa.get("den", 4)
    flag0 = flg.ap[:, NPRE:NPRE + 1]
    CUT2 = int(os.environ.get('KCUT2', '99')); NCH2 = int(os.environ.get('KNCH2', str(NM2)))
    for ci in range(NCH2):
        sl = ci % 2
        load_xT(xmain[ci * 128:(ci + 1) * 128, :], xb, xT)
        for kv in range(2):
            chain(pf[0], pf[0].ap[0:64, kv * 128:(kv + 1) * 128], [(Wk2.ap[:, k, kv * 64:(kv + 1) * 64], xT.ap[:, k, :]) for k in range(8)], [Wk2.b, xT.b])
        P.add("act", lambda e, sl=sl: e.copy(out=kT[sl].ap[0:64, :, :], in_=pf[0].ap[0:64, 0:256].rearrange("p (a b) -> p a b", a=2)), reads=[pf[0].b], writes=[kT[sl].b])
        chain(pf[1], pf[1].ap[:, 0:128], [(xT.ap[:, k, :], Wv.ap[:, k, :]) for k in range(8)], [xT.b, Wv.b])
        P.add("dve", lambda e, sl=sl: e.tensor_copy(out=vx[sl].ap[:, :, 0:64], in_=pf[1].ap[:, 0:128].rearrange("p (a b) -> p a b", a=2)), reads=[pf[1].b], writes=[vx[sl].b])
        if ci == 0 or CUT2 <= 1:
            continue
        for q4 in range(4):
            bank = pf[2 + q4 % 2]
            for tt in range(4):
                j = q4 * 4 + tt
                chain(bank, bank.ap[0:64, tt * 128:(tt + 1) * 128], [(Wq.ap[:, k, j * 64:(j + 1) * 64], xT.ap[:, k, :]) for k in range(8)], [Wq.b, xT.b])
            P.add("act", lambda e, q4=q4, bank=bank: e.copy(out=qT.ap[0:64, q4 * 4:q4 * 4 + 4, :], in_=bank.ap[0:64, :].rearrange("p (a b) -> p a b", a=4)), reads=[bank.b], writes=[qT.b])
        for kvh in range(2):
            if CUT2 <= 2: break
            for hb in range(2):
                j0 = kvh * 8 + hb * 4
                for kt in range(2):
                    slk = (ci + 1 + kt) % 2
                    bank = pf[4 + kt]
                    for i in range(4):
                        j = j0 + i
                        base = (j % 2) * 64 * int(os.environ.get("KB64", "1"))
                        P.add("pe", lambda e, i=i, j=j, base=base, slk=slk, bank=bank, kvh=kvh: e.matmul(bank.ap[:, i * 128:(i + 1) * 128], kT[slk].ap[0:64, kvh, :], qT.ap[0:64, j, :], start=True, stop=True), reads=[kT[slk].b, qT.b], writes=[bank.b])
                    P.add("act", lambda e, bank=bank: e.activation(out=et.ap.rearrange("p a b -> p (a b)"), in_=bank.ap, func=AF.Exp, scale=0.125), reads=[bank.b], writes=[et.b])
                    if ci == 2 and kt == 0:
                        P.add("dve", lambda e, kt=kt, j0=j0: e.scalar_tensor_tensor(out=PT[kt].ap, in0=et.ap, scalar=flag0, in1=EB.ap[:, kt, j0:j0 + 4, :], op0=ALU.mult, op1=ALU.mult), reads=[et.b, EB.b, flg.b], writes=[PT[kt].b])
                    else:
                        P.add("dve", lambda e, kt=kt, j0=j0: e.tensor_tensor(out=PT[kt].ap, in0=et.ap, in1=EB.ap[:, kt, j0:j0 + 4, :], op=ALU.mult), reads=[et.b, EB.b], writes=[PT[kt].b])
                if CUT2 <= 3: continue
                bank = pf[hb]
                for i in range(4):
                    for kt in range(2):
                        slk = (ci + 1 + kt) % 2
                        P.add("pe", lambda e, i=i, kt=kt, slk=slk, bank=bank, kvh=kvh: e.matmul(bank.ap[:, i * 65:(i + 1) * 65], PT[kt].ap[:, i, :], vx[slk].ap[:, kvh, :], start=(kt == 0), stop=(kt == 1)), reads=[PT[kt].b, vx[slk].b], writes=[bank.b])
                if CUT2 <= 4: continue
                pv = bank.ap[:, 0:260].rearrange("p (a b) -> p a b", a=4)
                P.add("dve", lambda e, pv=pv, j0=j0: e.tensor_tensor(out=den.ap, in0=pv[:, :, 64], in1=esink[:, j0:j0 + 4], op=ALU.add), reads=[bank.b, smallp.b], writes=[den.b])
                P.add("dve", lambda e: e.reciprocal(out=den.ap, in_=den.ap), reads=[den.b], writes=[den.b])
                P.add("dve", lambda e, pv=pv, j0=j0: e.tensor_tensor(out=ya.ap[:, j0 * 64:(j0 + 4) * 64].rearrange("p (a b) -> p a b", a=4), in0=pv[:, :, 0:64], in1=bc(den.ap, 64), op=ALU.mult), reads=[bank.b, den.b], writes=[ya.b])
        dma("sp", yad[(ci - 1) * 128:ci * 128, :], ya.ap, [ya.b], [])

    if stop < 3:
        P.emit(nc); es.close(); return nc
    P.barrier()
    AB.off, AFa.off = markB, markF
    Wg = AB.get("Wg", 8, 2048); load_w(Wg, w_in[:, COL_G:COL_G + 2048], 2048)
    Wbs = AB.get("Wbs", 16, 1024); load_w(Wbs, w_bs, 1024)
    Wba = AB.get("Wba", 8, 1024); load_w(Wba, w_ba, 1024)
    Wmx = AB.get("Wmx", 8, 1024); load_w(Wmx, w_mix, 1024)
    bgb = AFa.get("bgb", 2048); dma("sp", bgb.ap, bcast_row(b_gate, 2048), [], [bgb.b])
    lng = AFa.get("lng", 2, 1024)
    dma("sp", lng.ap[:, 0, :], bcast_row(ln1g, 1024), [], [lng.b]); dma("sp", lng.ap[:, 1, :], bcast_row(ln1b, 1024), [], [lng.b])
    xb = AB.get("xb3", 1024); xT = AB.get("xT3", 8, 128)
    ynb = AB.get("ynb", 2048); yab = AB.get("yab", 1024)
    ynT = AB.get("ynT", 16, 128); yaT = AB.get("yaT", 8, 128)
    mg = AB.get("mg", 1024); mT = AB.get("mT", 8, 128)
    xf = AFa.get("xf", 1024); gt = AFa.get("gt", 2048); m1 = AFa.get("m1", 512); r = AFa.get("r", 1024)
    st = AFa.get("st", 2, 6); mv = AFa.get("mv", 2)

    def transp(src, dst, ntl):
        for bt in range(ntl // 8):
            for i in range(8):
                j = bt * 8 + i
                P.add("pe", lambda e, i=i, j=j: e.transpose(pb[1].ap[:, i * 128:(i + 1) * 128], src.ap[:, j * 128:(j + 1) * 128], identb), reads=[src.b, cb16.b], writes=[pb[1].b])
            P.add("act", lambda e, bt=bt: e.copy(out=dst.ap[:, bt * 8:bt * 8 + 8, :].rearrange("p a b -> p (a b)"), in_=pb[1].ap), reads=[pb[1].b], writes=[dst.b])

    def layer_norm(r, g_ap, b_ap, gb, st, mv):
        for i in range(2):
            P.add("dve", lambda e, i=i: e.bn_stats(out=st.ap[:, i, :], in_=r.ap[:, i * 512:(i + 1) * 512]), reads=[r.b], writes=[st.b])
        P.add("dve", lambda e: e.bn_aggr(out=mv.ap, in_=st.ap.rearrange("p a b -> p (a b)")), reads=[st.b], writes=[mv.b])
        P.add("dve", lambda e: e.tensor_scalar(out=mv.ap[:, 1:2], in0=mv.ap[:, 1:2], scalar1=1e-5, scalar2=None, op0=ALU.add), reads=[mv.b], writes=[mv.b])
        P.add("act", lambda e: e.activation(out=mv.ap[:, 1:2], in_=mv.ap[:, 1:2], func=AF.Sqrt), reads=[mv.b], writes=[mv.b])
        P.add("dve", lambda e: e.reciprocal(out=mv.ap[:, 1:2], in_=mv.ap[:, 1:2]), reads=[mv.b], writes=[mv.b])
        P.add("dve", lambda e: e.tensor_scalar(out=r.ap, in0=r.ap, scalar1=mv.ap[:, 0:1], scalar2=mv.ap[:, 1:2], op0=ALU.subtract, op1=ALU.mult), reads=[r.b, mv.b], writes=[r.b])
        P.add("dve", lambda e: e.tensor_tensor(out=r.ap, in0=r.ap, in1=g_ap, op=ALU.mult), reads=[r.b, gb], writes=[r.b])
        P.add("dve", lambda e: e.tensor_tensor(out=r.ap, in0=r.ap, in1=b_ap, op=ALU.add), reads=[r.b, gb], writes=[r.b])

    for ci in range(NM1):
        xrow = xmain[(ci + 1) * 128:(ci + 2) * 128, :]
        load_xT(xrow, xb, xT)
        dma("sp", xf.ap, xrow, [], [xf.b])
        dma("sp", ynb.ap, ynd[ci * 128:(ci + 1) * 128, :], [], [ynb.b])
        dma("sp", yab.ap, yad[ci * 128:(ci + 1) * 128, :], [], [yab.b])
        transp(ynb, ynT, 16)
        transp(yab, yaT, 8)
        for s4 in range(4):
            bank = pf[s4 % 2]
            chain(bank, bank.ap, [(xT.ap[:, k, :], Wg.ap[:, k, s4 * 512:(s4 + 1) * 512]) for k in range(8)], [xT.b, Wg.b])
            P.add("dve", lambda e, s4=s4, bank=bank: e.tensor_tensor(out=gt.ap[:, s4 * 512:(s4 + 1) * 512], in0=bank.ap, in1=bgb.ap[:, s4 * 512:(s4 + 1) * 512], op=ALU.add), reads=[bank.b, bgb.b], writes=[gt.b])
        P.add("act", lambda e: e.activation(out=gt.ap, in_=gt.ap, func=AF.Sigmoid), reads=[gt.b], writes=[gt.b])
        for hf in range(2):
            chain(pf[2], pf[2].ap, [(ynT.ap[:, i, :], Wbs.ap[:, i, hf * 512:(hf + 1) * 512]) for i in range(16)], [ynT.b, Wbs.b])
            chain(pf[3], pf[3].ap, [(yaT.ap[:, i, :], Wba.ap[:, i, hf * 512:(hf + 1) * 512]) for i in range(8)], [yaT.b, Wba.b])
            P.add("dve", lambda e, hf=hf: e.tensor_tensor(out=m1.ap, in0=pf[2].ap, in1=gt.ap[:, hf * 512:(hf + 1) * 512], op=ALU.mult), reads=[pf[2].b, gt.b], writes=[m1.b])
            P.add("dve", lambda e, hf=hf: e.tensor_tensor(out=r.ap[:, hf * 512:(hf + 1) * 512], in0=pf[3].ap, in1=gt.ap[:, 1024 + hf * 512:1024 + (hf + 1) * 512], op=ALU.mult), reads=[pf[3].b, gt.b], writes=[r.b])
            P.add("dve", lambda e, hf=hf: e.tensor_tensor(out=mg.ap[:, hf * 512:(hf + 1) * 512], in0=m1.ap, in1=r.ap[:, hf * 512:(hf + 1) * 512], op=ALU.add), reads=[m1.b, r.b], writes=[mg.b])
        transp(mg, mT, 8)
        for hf in range(2):
            bank = pf[4 + hf]
            chain(bank, bank.ap, [(mT.ap[:, i, :], Wmx.ap[:, i, hf * 512:(hf + 1) * 512]) for i in range(8)], [mT.b, Wmx.b])
            P.add("dve", lambda e, hf=hf, bank=bank: e.scalar_tensor_tensor(out=r.ap[:, hf * 512:(hf + 1) * 512], in0=xf.ap[:, hf * 512:(hf + 1) * 512], scalar=ALPHA, in1=bank.ap, op0=ALU.mult, op1=ALU.add), reads=[xf.b, bank.b], writes=[r.b])
        layer_norm(r, lng.ap[:, 0, :], lng.ap[:, 1, :], lng.b, st, mv)
        if ci == 0:
            P.add("dve", lambda e: e.tensor_scalar(out=r.ap, in0=r.ap, scalar1=flag0, scalar2=None, op0=ALU.mult), reads=[r.b, flg.b], writes=[r.b])
        dma("sp", h1d[ci * 128:(ci + 1) * 128, :], r.ap, [r.b], [])

    if stop < 4:
        P.emit(nc); es.close(); return nc
    P.barrier()
    AB.off, AFa.off = markB, markF
    Wup = AB.get("Wup", 8, 5632); load_w(Wup, w_up, 5632)
    Wdn = AB.get("Wdn", 22, 1024); load_w(Wdn, w_dn, 1024)
    fw = AFa.get("fw", 132); fb = AFa.get("fb", 44)
    per_channel(fw.ap[:, 0:88], fcw[0:88, :], 88); fw.b.writer = P.ops["dve"][-1]
    per_channel(fw.ap[:, 88:132], fcw[88:132, :], 44); fw.b.writer = P.ops["dve"][-1]
    per_channel(fb.ap, fcb, 44); fb.b.writer = P.ops["dve"][-1]
    lng2 = AFa.get("lng2", 2, 1024)
    dma("sp", lng2.ap[:, 0, :], bcast_row(ln2g, 1024), [], [lng2.b]); dma("sp", lng2.ap[:, 1, :], bcast_row(ln2b, 1024), [], [lng2.b])
    SC = 256
    hA = AB.get("hA", 1024); hB = AB.get("hB", 1024); h1T = AB.get("h1T", 8, SC + 2)
    aT = AB.get("aT", 22, SC)
    cvs = [AFa.get("cvs%d" % i, SC) for i in range(4)]
    t0s = [AFa.get("t0s%d" % i, SC) for i in range(2)]
    sg = AFa.get("sg", SC)
    hr = AFa.get("hr", 1024); r4 = AFa.get("r4", 1024)
    st4 = AFa.get("st4", 2, 6); mv4 = AFa.get("mv4", 2)
    NT = SC // 128
    tcount = 0
    for sc in range(TOK // SC):
        r0 = 128 + sc * SC
        for i in range(NT):
            dma("pool", hA.ap, h1d[r0 - 2 + i * 128:r0 + 126 + i * 128, :], [], [hA.b])
            for k in range(8):
                P.add("pe", lambda e, k=k: e.transpose(pb[0].ap[:, k * 128:(k + 1) * 128], hA.ap[:, k * 128:(k + 1) * 128], identb), reads=[hA.b, cb16.b], writes=[pb[0].b])
            P.add("act", lambda e, i=i: e.copy(out=h1T.ap[:, :, i * 128:(i + 1) * 128], in_=pb[0].ap.rearrange("p (a b) -> p a b", a=8)), reads=[pb[0].b], writes=[h1T.b])
        dma("pool", hB.ap[0:2, :], h1d[r0 + SC - 2:r0 + SC, :], [], [hB.b])
        for k in range(8):
            P.add("pe", lambda e, k=k: e.transpose(pb[1].ap[:, k * 2:(k + 1) * 2], hB.ap[0:2, k * 128:(k + 1) * 128], identb[0:2, 0:2]), reads=[hB.b, cb16.b], writes=[pb[1].b])
        P.add("act", lambda e: e.copy(out=h1T.ap[:, :, SC:SC + 2], in_=pb[1].ap[:, 0:16].rearrange("p (a b) -> p a b", a=8)), reads=[pb[1].b], writes=[h1T.b])
        for jp in range(22):
            for gv in range(2):
                j = jp + 22 * gv
                X = pf[tcount % 4]; t0 = t0s[tcount % 2]
                cv = cvs[(jp % 2) * 2 + gv]
                tcount += 1
                chain(X, X.ap[:, 0:SC + 2], [(Wup.ap[:, k, j * 128:(j + 1) * 128], h1T.ap[:, k, 0:SC + 2]) for k in range(8)], [Wup.b, h1T.b])
                w0, w1, w2, bb = fw.ap[:, j:j + 1], fw.ap[:, 44 + j:45 + j], fw.ap[:, 88 + j:89 + j], fb.ap[:, j:j + 1]
                P.add("act", lambda e, X=X, t0=t0, w2=w2, bb=bb: e.activation(out=t0.ap, in_=X.ap[:, 2:SC + 2], func=AF.Identity, bias=bb, scale=w2), reads=[X.b, fw.b, fb.b], writes=[t0.b])
                P.add("dve", lambda e, X=X, t0=t0, w1=w1: e.scalar_tensor_tensor(out=t0.ap, in0=X.ap[:, 1:SC + 1], scalar=w1, in1=t0.ap, op0=ALU.mult, op1=ALU.add), reads=[X.b, fw.b, t0.b], writes=[t0.b])
                P.add("dve", lambda e, X=X, t0=t0, w0=w0, cv=cv: e.scalar_tensor_tensor(out=cv.ap, in0=X.ap[:, 0:SC], scalar=w0, in1=t0.ap, op0=ALU.mult, op1=ALU.add), reads=[X.b, fw.b, t0.b], writes=[cv.b])
            cg, cvv = cvs[(jp % 2) * 2], cvs[(jp % 2) * 2 + 1]
            P.add("act", lambda e, cg=cg: e.activation(out=sg.ap, in_=cg.ap, func=AF.Silu), reads=[cg.b], writes=[sg.b])
            P.add("dve", lambda e, jp=jp, cvv=cvv: e.tensor_tensor(out=aT.ap[:, jp, :], in0=sg.ap, in1=cvv.ap, op=ALU.mult), reads=[sg.b, cvv.b], writes=[aT.b])
        for tt in range(NT):
            rr = r0 + tt * 128
            dma("sp", hr.ap, h1d[rr:rr + 128, :], [], [hr.b])
            for hf in range(2):
                bank = pf[4 + hf]
                chain(bank, bank.ap, [(aT.ap[:, jj, tt * 128:(tt + 1) * 128], Wdn.ap[:, jj, hf * 512:(hf + 1) * 512]) for jj in range(22)], [aT.b, Wdn.b])
                P.add("dve", lambda e, hf=hf, bank=bank: e.scalar_tensor_tensor(out=r4.ap[:, hf * 512:(hf + 1) * 512], in0=hr.ap[:, hf * 512:(hf + 1) * 512], scalar=ALPHA, in1=bank.ap, op0=ALU.mult, op1=ALU.add), reads=[hr.b, bank.b], writes=[r4.b])
            layer_norm(r4, lng2.ap[:, 0, :], lng2.ap[:, 1, :], lng2.b, st4, mv4)
            dma("sp", out[rr - 128:rr, :], r4.ap, [r4.b], [])

    P.emit(nc)
    es.close()
    return nc


def rel_bucket_np(rel):
    n = np.maximum(rel, 0)
    nf = np.maximum(n, 1).astype(np.float32)
    large = 16 + (np.log(nf / np.float32(16)) / np.float32(np.log(128 / 16)) * np.float32(16)).astype(np.int32)
    large = np.minimum(large, 31)
    return np.where(n < 16, n, large)


_NC = None


def kernel(_dbg=None, **inp):
    global _NC
    x = np.asarray(inp["x"], np.float32)[0]
    f = lambda k: np.ascontiguousarray(np.asarray(inp[k], np.float32)[0])
    common = {
        "w_in": f("w_in"), "b_gate": f("b_gate")[None], "dtb": f("ssm_dt_bias")[None], "alog": f("ssm_a_log")[None],
        "dsk": f("ssm_d")[None], "normw": f("ssm_norm_w")[None], "sinks": f("attn_sinks")[None],
        "w_bs": f("w_branch_ssm"), "w_ba": f("w_branch_attn"), "w_mix": f("w_mix_out"),
        "ln1g": f("ln1_g")[None], "ln1b": f("ln1_b")[None], "ln2g": f("ln2_g")[None], "ln2b": f("ln2_b")[None],
        "w_up": f("w_up"), "w_dn": f("w_down"),
    }
    scw = f("ssm_conv_w")
    common["scw"] = np.ascontiguousarray(scw.reshape(4 * 24, 128))
    common["scb"] = np.ascontiguousarray(f("ssm_conv_b").reshape(24, 128))
    common["fcw"] = np.ascontiguousarray(f("ffn_conv_w").reshape(3 * 44, 128))
    common["fcb"] = np.ascontiguousarray(f("ffn_conv_b").reshape(44, 128))
    s = np.arange(128)
    ident = np.eye(128, dtype=np.float32)
    triU = (s[:, None] <= s[None, :]).astype(np.float32)
    ustr = (s[:, None] > s[None, :]).astype(np.float32)
    common["cst"] = np.ascontiguousarray(np.concatenate([ident, triU, ustr, np.ones((128, 128), np.float32)], axis=1))
    rb = np.asarray(inp["rel_bias"], np.float32)
    bg = np.zeros((128, 2, 16, 128), np.float32); mk = np.zeros((128, 2, 16, 128), np.float32)
    for kt in range(2):
        rel = (s[None, :] + 128) - (s[:, None] + 128 * kt)
        valid = (rel >= 0) & (rel < 128)
        bidx = rel_bucket_np(rel)
        g = rb[bidx]
        bg[:, kt] = np.transpose(g, (0, 2, 1))
        mk[:, kt] = np.broadcast_to(valid[:, None, :], (128, 16, 128))
    common["biasg"] = np.ascontiguousarray(bg.reshape(128, -1)); common["maskg"] = np.ascontiguousarray(mk.reshape(128, -1))
    in_maps = []
    for c in range(NCORE):
        S = c * TOK
        lo = S - 128 - NPRE * 128
        xp = np.zeros((NPRE * 128, 1024), np.float32)
        if S - 128 > 0:
            src_lo = max(lo, 0)
            xp[src_lo - lo:] = x[src_lo:S - 128]
        xm = np.zeros((NM2 * 128, 1024), np.float32)
        lo2 = S - 256
        src_lo = max(lo2, 0)
        xm[src_lo - lo2:] = x[src_lo:S + TOK]
        fl = np.zeros((128, NPRE + NM1), np.float32)
        for i in range(NPRE):
            fl[:, i] = 1.0 if lo + i * 128 >= 0 else 0.0
        fl[:, NPRE] = 1.0 if c > 0 else 0.0
        fl[:, NPRE + 1:] = 1.0
        m = dict(common); m["xpre"] = xp; m["xmain"] = xm; m["pflag"] = fl
        in_maps.append(m)
    if _dbg is not None:
        return in_maps
    if _NC is None:
        _NC = build()
    res = run_bass_kernel_spmd(_NC, in_maps, core_ids=list(range(NCORE)))
    o = np.concatenate([res.results[c]["out"] for c in range(NCORE)], axis=0)
    return o[None].astype(np.float32)
```

```python
import numpy as np
import concourse.bass as bass
import concourse.mybir as mybir

ENGS = ["pe", "act", "dve", "pool", "sp"]
N_DMA_SEM = 16
import os as _os
SAME_ENGINE_SYNC = _os.environ.get("KSES", "1") == "1"


class Buf:
    __slots__ = ("name", "writer", "readers", "dma_readers", "excl")

    def __init__(self, name, excl=False):
        self.name = name
        self.excl = excl
        self.writer = None
        self.readers = {}
        self.dma_readers = []


class Op:
    __slots__ = ("eng", "fn", "idx", "waits", "signal", "is_dma", "dsem", "dtarget", "clock", "sigcount", "uid")


class Prog:
    def __init__(self):
        self.ops = {e: [] for e in ENGS}
        self.clock = {e: {f: 0 for f in ENGS} for e in ENGS}
        self.known_dma = {e: set() for e in ENGS}
        self.dma_sem_count = [0] * N_DMA_SEM
        self.dma_sem_last = [None] * N_DMA_SEM
        self.dma_rr = 0
        self.n_dma = 0
        self.uid = 0
        self.bar = {e: [] for e in ENGS}

    def barrier(self):
        lasts = [self.ops[e][-1] for e in ENGS if self.ops[e] and not self.ops[e][-1].is_dma]
        for e in ENGS:
            pass
        lasts = []
        for e in ENGS:
            for op in reversed(self.ops[e]):
                if not op.is_dma:
                    lasts.append(op)
                    break
        dmas = [op for op in self.dma_sem_last if op is not None]
        for e in ENGS:
            self.bar[e] = lasts + dmas

    def add(self, eng, fn, reads=(), writes=(), dma=False):
        op = Op()
        op.eng = eng
        op.fn = fn
        op.idx = len(self.ops[eng])
        op.waits = []
        op.signal = False
        op.is_dma = dma
        op.uid = self.uid
        self.uid += 1
        deps = []
        for b in reads:
            if b.writer is not None:
                deps.append(b.writer)
            if b.excl:
                for e2, r in b.readers.items():
                    if e2 != eng:
                        deps.append(r)
        for b in writes:
            if b.writer is not None:
                deps.append(b.writer)
            deps.extend(b.readers.values())
            deps.extend(b.dma_readers)
        if self.bar[eng]:
            deps.extend(self.bar[eng])
            self.bar[eng] = []
        clk = self.clock[eng]
        seen = set()
        for d in deps:
            if d.uid in seen:
                continue
            seen.add(d.uid)
            if d.is_dma:
                if d.uid not in self.known_dma[eng]:
                    op.waits.append(("dma", d.dsem, d.dtarget))
                    self.known_dma[eng].add(d.uid)
            else:
                if d.eng == eng and (eng == "pe" or not SAME_ENGINE_SYNC):
                    continue
                if clk[d.eng] < d.idx + 1:
                    op.waits.append(("eng", d))
                    d.signal = True
                    for f in ENGS:
                        if d.clock[f] > clk[f]:
                            clk[f] = d.clock[f]
                    if clk[d.eng] < d.idx + 1:
                        clk[d.eng] = d.idx + 1
        if dma and eng == "pool":
            pd = self.__dict__.setdefault("pool_dmas", [])
            if len(pd) >= 4:
                d = pd[-4]
                if d.uid not in self.known_dma[eng]:
                    op.waits.append(("dma", d.dsem, d.dtarget))
                    self.known_dma[eng].add(d.uid)
            pd.append(op)
        if dma:
            k = self.dma_rr
            self.dma_rr = (self.dma_rr + 1) % N_DMA_SEM
            prev = self.dma_sem_last[k]
            if prev is not None and prev.uid not in self.known_dma[eng]:
                op.waits.append(("dma", k, prev.dtarget))
                self.known_dma[eng].add(prev.uid)
            self.dma_sem_count[k] += 16
            op.dsem = k
            op.dtarget = self.dma_sem_count[k]
            self.dma_sem_last[k] = op
            self.n_dma += 1
        op.clock = dict(clk)
        if not SAME_ENGINE_SYNC or eng == "pe":
            pass
        self.ops[eng].append(op)
        for b in writes:
            b.writer = op
            b.readers = {}
            b.dma_readers = []
        for b in reads:
            if dma:
                b.dma_readers.append(op)
            else:
                b.readers[eng] = op
        return op

    def emit(self, nc, final_waits=True):
        for e in ENGS:
            c = 0
            for op in self.ops[e]:
                if op.signal:
                    c += 1
                op.sigcount = c
        from contextlib import ExitStack
        with ExitStack() as es:
            esem = {e: es.enter_context(nc.semaphore("s_" + e)) for e in ENGS}
            dsem = [es.enter_context(nc.semaphore("d%d" % i)) for i in range(N_DMA_SEM)]
            block = es.enter_context(nc.Block())
            last_dma = [op for op in self.dma_sem_last if op is not None]

            def run(e, h):
                for op in self.ops[e]:
                    for w in op.waits:
                        if w[0] == "dma":
                            h.wait_ge(dsem[w[1]], w[2])
                        else:
                            h.wait_ge(esem[w[1].eng], w[1].sigcount)
                    ins = op.fn(h)
                    if op.is_dma:
                        ins.then_inc(dsem[op.dsem], 16)
                    elif op.signal:
                        ins.then_inc(esem[e], 1)
                if e == "sp" and final_waits:
                    for k in range(N_DMA_SEM):
                        if self.dma_sem_count[k] > 0:
                            h.wait_ge(dsem[k], self.dma_sem_count[k])

            @block.tensor
            def _(h):
                run("pe", h)

            @block.scalar
            def _(h):
                run("act", h)

            @block.vector
            def _(h):
                run("dve", h)

            @block.gpsimd
            def _(h):
                run("pool", h)

            @block.sync
            def _(h):
                run("sp", h)

from contextlib import ExitStack
import ml_dtypes
from concourse.bass_utils import run_bass_kernel_spmd

F32 = mybir.dt.float32
BF16 = mybir.dt.bfloat16
AF = mybir.ActivationFunctionType
ALU = mybir.AluOpType

NCORE = 8
TOK = 2048
NPRE = 112
NM1 = 17
NM2 = 18
ALPHA = 2.0 ** 0.25
COL_Z, COL_X, COL_B, COL_C, COL_DT, COL_Q, COL_K, COL_V, COL_G = 0, 2048, 4096, 4608, 5120, 5152, 6176, 6304, 6432


class TT:
    def __init__(self, ap, name, excl=False):
        self.ap = ap
        self.b = Buf(name, excl)


class Arena:
    def __init__(self, t, n):
        self.t, self.n, self.off = t, n, 0

    def get(self, name, *fs):
        n = int(np.prod(fs))
        ap = self.t[:, self.off:self.off + n]
        self.off += n
        assert self.off <= self.n, (name, self.off, self.n)
        if len(fs) == 2:
            ap = ap.rearrange("p (a b) -> p a b", a=fs[0])
        elif len(fs) == 3:
            ap = ap.rearrange("p (a b c) -> p a b c", a=fs[0], b=fs[1])
        return TT(ap, name)


def bc(ap2, n):
    return ap2.unsqueeze(2).to_broadcast([ap2.shape[0], ap2.shape[1], n])


def build(stop=4, npre=NPRE, dbg=False):
    nc = bass.Bass("TRN2", target_bir_lowering=False)
    dt_in = lambda n, s: nc.dram_tensor(n, s, F32, kind="ExternalInput").ap()
    xpre = dt_in("xpre", [NPRE * 128, 1024])
    xmain = dt_in("xmain", [NM2 * 128, 1024])
    pflag = dt_in("pflag", [128, NPRE + NM1])
    w_in = dt_in("w_in", [1024, 8480])
    b_gate = dt_in("b_gate", [1, 2048])
    scw = dt_in("scw", [96, 128])
    scb = dt_in("scb", [24, 128])
    dtb = dt_in("dtb", [1, 32])
    alog = dt_in("alog", [1, 32])
    dsk = dt_in("dsk", [1, 32])
    normw = dt_in("normw", [1, 2048])
    sinks = dt_in("sinks", [1, 16])
    w_bs = dt_in("w_bs", [2048, 1024])
    w_ba = dt_in("w_ba", [1024, 1024])
    w_mix = dt_in("w_mix", [1024, 1024])
    ln1g = dt_in("ln1g", [1, 1024]); ln1b = dt_in("ln1b", [1, 1024])
    ln2g = dt_in("ln2g", [1, 1024]); ln2b = dt_in("ln2b", [1, 1024])
    w_up = dt_in("w_up", [1024, 5632])
    fcw = dt_in("fcw", [132, 128])
    fcb = dt_in("fcb", [44, 128])
    w_dn = dt_in("w_dn", [2816, 1024])
    cst = dt_in("cst", [128, 4 * 128])
    biasg = dt_in("biasg", [128, 2 * 16 * 128])
    maskg = dt_in("maskg", [128, 2 * 16 * 128])
    out = nc.dram_tensor("out", [TOK, 1024], F32, kind="ExternalOutput").ap()
    skind = "ExternalOutput" if dbg else "Internal"
    ynd = nc.dram_tensor("ynd", [NM1 * 128, 2048], BF16, kind=skind).ap()
    yad = nc.dram_tensor("yad", [NM1 * 128, 1024], BF16, kind=skind).ap()
    h1d = nc.dram_tensor("h1d", [NM1 * 128, 1024], F32, kind=skind).ap()

    P = Prog()
    es = ExitStack()
    NB, NF = 157 * 512, 47 * 256
    ABt = es.enter_context(nc.sbuf_tensor("AB", [128, NB], BF16))
    AFt = es.enter_context(nc.sbuf_tensor("AF", [128, NF], F32))
    pf = [TT(es.enter_context(nc.psum_tensor("pf%d" % i, [128, 512], F32))[:], "pf%d" % i, True) for i in range(6)]
    pb = [TT(es.enter_context(nc.psum_tensor("pb%d" % i, [128, 1024], BF16))[:], "pb%d" % i, True) for i in range(2)]
    AB = Arena(ABt, NB)
    AFa = Arena(AFt, NF)

    def bcast_row(src, n):
        return bass.AP(src.tensor, 0, [[0, 128], [1, n]])

    def dma(eng, o, i, reads, writes):
        P.add(eng, lambda e, o=o, i=i: e.dma_start(out=o, in_=i), reads=reads, writes=writes, dma=True)

    cf = AFa.get("cf", 4, 128)
    dma("sp", cf.ap, cst.rearrange("p (a b) -> p a b", a=4), [], [cf.b])
    identf, triU, Ustr, onesf = cf.ap[:, 0, :], cf.ap[:, 1, :], cf.ap[:, 2, :], cf.ap[:, 3, :]
    cb16 = AB.get("cb16", 4, 128)
    dma("pool", cb16.ap, cst.rearrange("p (a b) -> p a b", a=4), [], [cb16.b])
    identb, maskb = cb16.ap[:, 0, :], cb16.ap[:, 1, :]
    flg = AFa.get("flg", NPRE + NM1)
    dma("sp", flg.ap, pflag, [], [flg.b])
    smallp = AFa.get("smallp", 6, 32)
    dma("sp", smallp.ap[:, 0, :], bcast_row(dtb, 32), [], [smallp.b])
    dma("sp", smallp.ap[:, 1, :], bcast_row(alog, 32), [], [smallp.b])
    dma("sp", smallp.ap[:, 2, :], bcast_row(dsk, 32), [], [smallp.b])
    dma("sp", smallp.ap[:, 3, 0:16], bcast_row(sinks, 16), [], [smallp.b])
    P.add("act", lambda e: e.activation(out=smallp.ap[:, 1, :], in_=smallp.ap[:, 1, :], func=AF.Exp), reads=[smallp.b], writes=[smallp.b])
    P.add("dve", lambda e: e.tensor_scalar(out=smallp.ap[:, 1, :], in0=smallp.ap[:, 1, :], scalar1=-1.0, scalar2=None, op0=ALU.mult), reads=[smallp.b], writes=[smallp.b])
    P.add("act", lambda e: e.activation(out=smallp.ap[:, 3, 0:16], in_=smallp.ap[:, 3, 0:16], func=AF.Exp), reads=[smallp.b], writes=[smallp.b])
    dtb_bc, a_bc, D_bc, esink = smallp.ap[:, 0, :], smallp.ap[:, 1, :], smallp.ap[:, 2, :], smallp.ap[:, 3, 0:16]
    onecol = onesf[:, 0:1]
    rawh = AB.get("rawh", 24, 4)
    markB, markF = AB.off, AFa.off
    H = AFa.get("H", 2048)

    def load_w(dst, src, ncols, nk=8):
        nk = src.shape[0] // 128
        for c0 in range(0, ncols, 512):
            c1 = min(c0 + 512, ncols)
            for k in range(nk):
                dma("pool", dst.ap[:, k, c0:c1], src[k * 128:(k + 1) * 128, c0:c1], [], [dst.b])

    def load_xT(xrow_ap, xb, xT):
        dma("pool", xb.ap, xrow_ap, [], [xb.b])
        for k in range(8):
            P.add("pe", lambda e, k=k: e.transpose(pb[0].ap[:, k * 128:(k + 1) * 128], xb.ap[:, k * 128:(k + 1) * 128], identb), reads=[xb.b, cb16.b], writes=[pb[0].b])
        P.add("act", lambda e: e.copy(out=xT.ap.rearrange("p a b -> p (a b)"), in_=pb[0].ap), reads=[pb[0].b], writes=[xT.b])

    def chain(bank, out_ap, pairs, reads):
        n = len(pairs)
        for i, (l, r) in enumerate(pairs):
            P.add("pe", lambda e, l=l, r=r, i=i: e.matmul(out_ap, l, r, start=(i == 0), stop=(i == n - 1)), reads=reads, writes=[bank.b])

    def per_channel(dst, src_dram, rows):
        tmp = AFa.get("pc_tmp", 128)
        dma("sp", tmp.ap[0:rows, :], src_dram, [], [tmp.b])
        P.add("pe", lambda e: e.transpose(pf[5].ap[:, 0:rows], tmp.ap[0:rows, :], identf[0:rows, 0:rows]), reads=[tmp.b, cf.b], writes=[pf[5].b])
        P.add("dve", lambda e: e.tensor_copy(out=dst, in_=pf[5].ap[:, 0:rows]), reads=[pf[5].b], writes=[])

    import os
    SKIP1 = int(os.environ.get('KSKIP1', '0'))
    Wx = AB.get("Wx", 8, 3072); load_w(Wx, w_in[:, COL_X:COL_X + 3072], 3072)
    Wdt = AB.get("Wdt", 8, 32); load_w(Wdt, w_in[:, COL_DT:COL_DT + 32], 32)
    cwt = AFa.get("cwt", 96); cbt = AFa.get("cbt", 24)
    per_channel(cwt.ap, scw, 96)
    per_channel(cbt.ap, scb, 24)
    cwt.b.writer = P.ops["dve"][-2]; cbt.b.writer = P.ops["dve"][-1]
    diag = AB.get("diag", 24, 4, 128)
    for j in range(24):
        for tp in range(4):
            P.add("dve", lambda e, j=j, tp=tp: e.tensor_scalar(out=diag.ap[:, j, tp, :], in0=identf, scalar1=cwt.ap[:, tp * 24 + j:tp * 24 + j + 1], scalar2=None, op0=ALU.mult), reads=[cf.b, cwt.b], writes=[diag.b])
    P.add("dve", lambda e: e.memset(H.ap, 0.0), writes=[H.b])
    mark1B, mark1F = AB.off, AFa.off
    xbA = [AB.get("xbA%d" % i, 1024) for i in range(2)]
    xT4 = [AB.get("xT4_%d" % i, 8, 512) for i in range(2)]
    raw4 = AB.get("raw4", 24, 516); xc4 = AB.get("xc4", 24, 512)
    rawb = [Buf("rawb%d" % j) for j in range(24)]; xcb = [Buf("xcb%d" % j) for j in range(24)]
    xtk = [AB.get("xtk%d" % i, 2560) for i in range(2)]
    xdtsA4 = [AB.get("xdtsA%d" % i, 512) for i in range(4)]
    P.add("pool", lambda e: e.memset(raw4.ap, 0.0), writes=rawb)

    def load_group(gi, buf):
        for q in range(4):
            c = gi * 4 + q
            xb_ = xbA[q % 2]
            dma("pool", xb_.ap, xpre[c * 128:(c + 1) * 128, :], [], [xb_.b])
            for k in range(8):
                P.add("pe", lambda e, k=k, xb_=xb_: e.transpose(pb[0].ap[:, k * 128:(k + 1) * 128], xb_.ap[:, k * 128:(k + 1) * 128], identb), reads=[xb_.b, cb16.b], writes=[pb[0].b])
            P.add("act", lambda e, q=q, buf=buf: e.copy(out=xT4[buf].ap[:, :, q * 128:(q + 1) * 128], in_=pb[0].ap.rearrange("p (a b) -> p a b", a=8)), reads=[pb[0].b], writes=[xT4[buf].b])

    def group_proj(gi, buf, last):
        ntile = 24 if last else 20
        for j in range(ntile):
            bank = pf[j % 2]
            chain(bank, bank.ap, [(Wx.ap[:, k, j * 128:(j + 1) * 128], xT4[buf].ap[:, k, :]) for k in range(8)], [Wx.b, xT4[buf].b])
            P.add("act", lambda e, j=j, bank=bank: e.copy(out=raw4.ap[:, j, 3:515], in_=bank.ap), reads=[bank.b], writes=[rawb[j]])
        for j in range(ntile):
            bank = pf[2 + j % 2]
            chain(bank, bank.ap, [(diag.ap[:, j, tp, :], raw4.ap[:, j, tp:tp + 512]) for tp in range(4)], [diag.b, rawb[j]])
            P.add("act", lambda e, j=j, bank=bank: e.activation(out=xc4.ap[:, j, :], in_=bank.ap, func=AF.Silu, bias=cbt.ap[:, j:j + 1], scale=1.0), reads=[bank.b, cbt.b], writes=[xcb[j]])
        P.add("pool", lambda e: e.tensor_copy(out=raw4.ap[:, :, 0:3], in_=raw4.ap[:, :, 512:515]), reads=rawb, writes=rawb)

    smGs = [AFa.get("smG%d" % i, 8, 128) for i in range(2)]

    def group_chunks(gi, buf):
        c0 = gi * 4
        smG = smGs[gi % 2]
        S = lambda i: smG.ap[:, i, :]
        S3 = lambda i: smG.ap[:, i, :].rearrange("p (q h) -> p q h", q=4)
        b4 = lambda ap: ap.unsqueeze(1).to_broadcast([128, 4, 32])
        v3 = lambda ap: ap.rearrange("p (a b) -> p a b", a=8)
        for q in range(4):
            chain(pf[4], pf[4].ap[:, q * 32:(q + 1) * 32], [(xT4[buf].ap[:, k, q * 128:(q + 1) * 128], Wdt.ap[:, k, :]) for k in range(8)], [xT4[buf].b, Wdt.b])
        P.add("dve", lambda e: e.tensor_tensor(out=S3(0), in0=pf[4].ap[:, 0:128].rearrange("p (q h) -> p q h", q=4), in1=b4(dtb_bc), op=ALU.add), reads=[pf[4].b, smallp.b], writes=[smG.b])
        P.add("act", lambda e: e.activation(out=S(0), in_=S(0), func=AF.Exp), reads=[smG.b], writes=[smG.b])
        P.add("act", lambda e: e.activation(out=S(1), in_=S(0), func=AF.Ln, bias=onecol, scale=1.0), reads=[smG.b, cf.b], writes=[smG.b])
        P.add("dve", lambda e: e.tensor_tensor(out=S3(1), in0=S3(1), in1=bc(flg.ap[:, c0:c0 + 4], 32), op=ALU.mult), reads=[smG.b, flg.b], writes=[smG.b])
        P.add("dve", lambda e: e.tensor_tensor(out=S3(2), in0=S3(1), in1=b4(a_bc), op=ALU.mult), reads=[smG.b, smallp.b], writes=[smG.b])

        def tr(q):
            xt = xtk[q % 2]
            for bt in range(3):
                nt = 8 if bt < 2 else 4
                pbk = pb[bt % 2]
                for i in range(nt):
                    jj = bt * 8 + i
                    P.add("pe", lambda e, i=i, jj=jj, q=q, pbk=pbk: e.transpose(pbk.ap[:, i * 128:(i + 1) * 128], xc4.ap[:, jj, q * 128:(q + 1) * 128], identb), reads=[xcb[jj], cb16.b], writes=[pbk.b])
                if bt == 1:
                    P.add("act", lambda e, bt=bt, nt=nt, xt=xt, pbk=pbk: e.copy(out=xt.ap[:, bt * 1024:bt * 1024 + nt * 128], in_=pbk.ap[:, 0:nt * 128]), reads=[pbk.b], writes=[xt.b])
                else:
                    P.add("dve", lambda e, bt=bt, nt=nt, xt=xt, pbk=pbk: e.tensor_copy(out=xt.ap[:, bt * 1024:bt * 1024 + nt * 128], in_=pbk.ap[:, 0:nt * 128]), reads=[pbk.b], writes=[xt.b])

        def state(q):
            xt = xtk[q % 2]
            for g in range(4):
                xg = xt.ap[:, g * 512:(g + 1) * 512].rearrange("p (a b) -> p a b", a=8)
                o = q * 32 + 8 * g
                xd = xdtsA4[g]
                P.add("dve", lambda e, xg=xg, o=o, xd=xd: e.tensor_tensor(out=v3(xd.ap), in0=xg, in1=bc(smG.ap[:, 6, o:o + 8], 64), op=ALU.mult), reads=[xt.b, smG.b], writes=[xd.b])
                P.add("pe", lambda e, g=g, xt=xt, xd=xd: e.matmul(pf[g].ap, xt.ap[:, 2048 + g * 128:2048 + (g + 1) * 128], xd.ap, start=True, stop=True), reads=[xt.b, xd.b], writes=[pf[g].b])
            P.add("dve", lambda e, q=q: e.tensor_tensor(out=H.ap.rearrange("p (a b) -> p a b", a=32), in0=H.ap.rearrange("p (a b) -> p a b", a=32), in1=bc(smG.ap[:, 7, q * 32:(q + 1) * 32], 64), op=ALU.mult), reads=[H.b, smG.b], writes=[H.b])
            for g in range(4):
                Hg = H.ap[:, g * 512:(g + 1) * 512]
                P.add("dve", lambda e, Hg=Hg, g=g: e.tensor_tensor(out=Hg, in0=Hg, in1=pf[g].ap, op=ALU.add), reads=[H.b, pf[g].b], writes=[H.b])

        tr(0); tr(1)
        P.add("pe", lambda e: e.matmul(pf[4].ap[:, 128:256], triU, S(2), start=True, stop=True), reads=[cf.b, smG.b], writes=[pf[4].b])
        P.add("pe", lambda e: e.matmul(pf[4].ap[:, 256:384], onesf, S(2), start=True, stop=True), reads=[cf.b, smG.b], writes=[pf[4].b])
        P.add("dve", lambda e: e.tensor_copy(out=smG.ap[:, 3:5, :], in_=pf[4].ap[:, 128:384].rearrange("p (a b) -> p a b", a=2)), reads=[pf[4].b], writes=[smG.b])
        P.add("dve", lambda e: e.tensor_tensor(out=S(6), in0=S(4), in1=S(3), op=ALU.subtract), reads=[smG.b], writes=[smG.b])
        P.add("act", lambda e: e.activation(out=S(6), in_=S(6), func=AF.Exp), reads=[smG.b], writes=[smG.b])
        P.add("act", lambda e: e.activation(out=S(7), in_=S(4), func=AF.Exp), reads=[smG.b], writes=[smG.b])
        P.add("dve", lambda e: e.tensor_tensor(out=S(6), in0=S(6), in1=S(1), op=ALU.mult), reads=[smG.b], writes=[smG.b])
        state(0); state(1)
        tr(2); tr(3)
        state(2); state(3)

    NG = NPRE // 4
    g0 = NG - (npre + 3) // 4
    if not SKIP1 and g0 < NG:
        load_group(g0, g0 % 2)
        for gi in range(g0, NG):
            group_proj(gi, gi % 2, gi == NG - 1)
            if gi + 1 < NG:
                load_group(gi + 1, (gi + 1) % 2)
            group_chunks(gi, gi % 2)
        P.add("pool", lambda e: e.tensor_copy(out=rawh.ap[:, :, 0:3], in_=raw4.ap[:, :, 0:3]), reads=rawb, writes=[rawh.b])
    else:
        P.add("pool", lambda e: e.memset(rawh.ap, 0.0), writes=[rawh.b])

    P.barrier()
    AB.off, AFa.off = mark1B, mark1F
    Wz = AB.get("Wz", 8, 2048); load_w(Wz, w_in[:, COL_Z:COL_Z + 2048], 2048)
    nwb = AFa.get("nwb", 2048)
    dma("sp", nwb.ap, bcast_row(normw, 2048), [], [nwb.b])
    xb = AB.get("xb", 1024); xT = AB.get("xT", 8, 128)
    raw = AB.get("raw", 24, 132); xc = AB.get("xc", 24, 128)
    xtok = AB.get("xtok", 2560)
    xdt = AB.get("xdt", 512); xdts = AB.get("xdts", 512)
    Hb = AB.get("Hb", 2048); cbm = AB.get("cbm", 4, 128)
    LT = AB.get("LT", 4, 128); MT = AB.get("MT", 8, 128); yn = AB.get("yn", 2048)
    adtU = [AFa.get("adtU%d" % i, 128) for i in range(2)]
    sm = AFa.get("sm", 12, 32)
    yacc = AFa.get("yacc", 512); ytmp = AFa.get("ytmp", 512); sz = AFa.get("sz", 512)
    ssq = AFa.get("ssq", 4)
    P.add("pool", lambda e: e.memset(raw.ap, 0.0), writes=[raw.b])
    P.add("pool", lambda e: e.tensor_copy(out=raw.ap[:, :, 0:3], in_=rawh.ap[:, :, 0:3]), reads=[rawh.b], writes=[raw.b])

    CUT = int(os.environ.get('KCUT', '99')); NCH1 = int(os.environ.get('KNCH', str(NM1)))
    def ssd_chunk(xrow, fcol, main, ci):
        ntile = 24
        load_xT(xrow, xb, xT)
        if CUT <= 1: return
        for g in range(ntile // 4):
            bank = pf[g % 2]
            for jj in range(4):
                j = 4 * g + jj
                chain(bank, bank.ap[:, jj * 128:(jj + 1) * 128], [(Wx.ap[:, k, j * 128:(j + 1) * 128], xT.ap[:, k, :]) for k in range(8)], [Wx.b, xT.b])
            P.add("act", lambda e, g=g, bank=bank: e.copy(out=raw.ap[:, 4 * g:4 * g + 4, 3:131], in_=bank.ap.rearrange("p (a b) -> p a b", a=4)), reads=[bank.b], writes=[raw.b])
        if CUT <= 2: return
        for g in range(ntile // 4):
            bank = pf[2 + g % 2]
            for jj in range(4):
                j = 4 * g + jj
                chain(bank, bank.ap[:, jj * 128:(jj + 1) * 128], [(diag.ap[:, j, tp, :], raw.ap[:, j, tp:tp + 128]) for tp in range(4)], [diag.b, raw.b])
                P.add("act", lambda e, j=j, jj=jj, bank=bank: e.activation(out=xc.ap[:, j, :], in_=bank.ap[:, jj * 128:(jj + 1) * 128], func=AF.Silu, bias=cbt.ap[:, j:j + 1], scale=1.0), reads=[bank.b, cbt.b], writes=[xc.b])
        if CUT <= 3: return
        P.add("pool", lambda e: e.tensor_copy(out=raw.ap[:, :, 0:3], in_=raw.ap[:, :, 128:131]), reads=[raw.b], writes=[raw.b])
        for bt in range(3):
            nt = 8 if bt < 2 else 4
            for i in range(nt):
                j = bt * 8 + i
                P.add("pe", lambda e, i=i, j=j: e.transpose(pb[1].ap[:, i * 128:(i + 1) * 128], xc.ap[:, j, :], identb), reads=[xc.b, cb16.b], writes=[pb[1].b])
            P.add("dve", lambda e, bt=bt, nt=nt: e.tensor_copy(out=xtok.ap[:, bt * 1024:bt * 1024 + nt * 128], in_=pb[1].ap[:, 0:nt * 128]), reads=[pb[1].b], writes=[xtok.b])
        if CUT <= 4: return
        chain(pf[4], pf[4].ap[:, 0:32], [(xT.ap[:, k, :], Wdt.ap[:, k, :]) for k in range(8)], [xT.b, Wdt.b])
        S = lambda i: sm.ap[:, i, :]
        P.add("dve", lambda e: e.tensor_tensor(out=S(0), in0=pf[4].ap[:, 0:32], in1=dtb_bc, op=ALU.add), reads=[pf[4].b, smallp.b], writes=[sm.b])
        P.add("act", lambda e: e.activation(out=S(0), in_=S(0), func=AF.Exp), reads=[sm.b], writes=[sm.b])
        P.add("act", lambda e: e.activation(out=S(1), in_=S(0), func=AF.Ln, bias=onecol, scale=1.0), reads=[sm.b, cf.b], writes=[sm.b])
        P.add("dve", lambda e: e.tensor_scalar(out=S(1), in0=S(1), scalar1=fcol, scalar2=None, op0=ALU.mult), reads=[sm.b, flg.b], writes=[sm.b])
        P.add("dve", lambda e: e.tensor_tensor(out=S(2), in0=S(1), in1=a_bc, op=ALU.mult), reads=[sm.b, smallp.b], writes=[sm.b])
        if CUT <= 5: return
        P.add("pe", lambda e: e.matmul(pf[4].ap[:, 32:64], triU, S(2), start=True, stop=True), reads=[cf.b, sm.b], writes=[pf[4].b])
        P.add("pe", lambda e: e.matmul(pf[4].ap[:, 64:96], onesf, S(2), start=True, stop=True), reads=[cf.b, sm.b], writes=[pf[4].b])
        P.add("dve", lambda e: e.tensor_copy(out=sm.ap[:, 3:5, :], in_=pf[4].ap[:, 32:96].rearrange("p (a b) -> p a b", a=2)), reads=[pf[4].b], writes=[sm.b])
        if CUT <= 6: return
        if main and CUT > 7:
            for g in range(4):
                P.add("pe", lambda e, g=g: e.matmul(pf[5].ap[:, g * 128:(g + 1) * 128], xc.ap[:, 16 + g, :], xc.ap[:, 20 + g, :], start=True, stop=True), reads=[xc.b], writes=[pf[5].b])
            P.add("dve", lambda e: e.tensor_tensor(out=cbm.ap, in0=pf[5].ap.rearrange("p (a b) -> p a b", a=4), in1=maskb.unsqueeze(1).to_broadcast([128, 4, 128]), op=ALU.mult), reads=[pf[5].b, cb16.b], writes=[cbm.b])
            P.add("pool", lambda e: e.tensor_copy(out=Hb.ap, in_=H.ap), reads=[H.b], writes=[Hb.b])
            P.add("act", lambda e: e.activation(out=S(5), in_=S(3), func=AF.Exp), reads=[sm.b], writes=[sm.b])
            for g in range(4):
                xg = xtok.ap[:, g * 512:(g + 1) * 512].rearrange("p (a b) -> p a b", a=8)
                P.add("dve", lambda e, g=g, xg=xg: e.tensor_tensor(out=xdt.ap.rearrange("p (a b) -> p a b", a=8), in0=xg, in1=bc(sm.ap[:, 1, 8 * g:8 * g + 8], 64), op=ALU.mult), reads=[xtok.b, sm.b], writes=[xdt.b])
                for hh in range(2):
                    bank = pf[hh]
                    for h4 in range(4):
                        h = 8 * g + 4 * hh + h4
                        au = adtU[h4 % 2]
                        P.add("dve", lambda e, h=h, au=au: e.tensor_scalar(out=au.ap, in0=Ustr, scalar1=sm.ap[:, 2, h:h + 1], scalar2=None, op0=ALU.mult), reads=[cf.b, sm.b], writes=[au.b])
                        P.add("pe", lambda e, h4=h4, au=au, bank=bank: e.matmul(bank.ap[:, h4 * 128:(h4 + 1) * 128], au.ap, triU, start=True, stop=True), reads=[au.b, cf.b], writes=[bank.b])
                    P.add("act", lambda e, bank=bank: e.activation(out=LT.ap.rearrange("p a b -> p (a b)"), in_=bank.ap, func=AF.Exp), reads=[bank.b], writes=[LT.b])
                    P.add("dve", lambda e, g=g, hh=hh: e.tensor_tensor(out=MT.ap[:, 4 * hh:4 * hh + 4, :], in0=LT.ap, in1=cbm.ap[:, g, :].unsqueeze(1).to_broadcast([128, 4, 128]), op=ALU.mult), reads=[LT.b, cbm.b], writes=[MT.b])
                for h8 in range(8):
                    P.add("pe", lambda e, h8=h8: e.matmul(pf[2].ap[:, h8 * 64:(h8 + 1) * 64], MT.ap[:, h8, :], xdt.ap[:, h8 * 64:(h8 + 1) * 64], start=True, stop=True), reads=[MT.b, xdt.b], writes=[pf[2].b])
                P.add("pe", lambda e, g=g: e.matmul(pf[3].ap, xc.ap[:, 20 + g, :], Hb.ap[:, g * 512:(g + 1) * 512], start=True, stop=True), reads=[xc.b, Hb.b], writes=[pf[3].b])
                v3 = lambda ap: ap.rearrange("p (a b) -> p a b", a=8)
                P.add("dve", lambda e, g=g: e.tensor_tensor(out=v3(yacc.ap), in0=v3(pf[3].ap), in1=bc(sm.ap[:, 5, 8 * g:8 * g + 8], 64), op=ALU.mult), reads=[pf[3].b, sm.b], writes=[yacc.b])
                P.add("dve", lambda e: e.tensor_tensor(out=yacc.ap, in0=yacc.ap, in1=pf[2].ap, op=ALU.add), reads=[pf[2].b, yacc.b], writes=[yacc.b])
                P.add("dve", lambda e, g=g, xg=xg: e.tensor_tensor(out=v3(ytmp.ap), in0=xg, in1=bc(D_bc[:, 8 * g:8 * g + 8], 64), op=ALU.mult), reads=[xtok.b, smallp.b], writes=[ytmp.b])
                P.add("dve", lambda e: e.tensor_tensor(out=yacc.ap, in0=yacc.ap, in1=ytmp.ap, op=ALU.add), reads=[ytmp.b, yacc.b], writes=[yacc.b])
                chain(pf[5], pf[5].ap, [(xT.ap[:, k, :], Wz.ap[:, k, g * 512:(g + 1) * 512]) for k in range(8)], [xT.b, Wz.b])
                P.add("act", lambda e: e.activation(out=sz.ap, in_=pf[5].ap, func=AF.Silu), reads=[pf[5].b], writes=[sz.b])
                P.add("dve", lambda e: e.tensor_tensor(out=yacc.ap, in0=yacc.ap, in1=sz.ap, op=ALU.mult), reads=[sz.b, yacc.b], writes=[yacc.b])
                P.add("act", lambda e, g=g: e.activation(out=ytmp.ap, in_=yacc.ap, func=AF.Square, accum_out=ssq.ap[:, g:g + 1]), reads=[yacc.b], writes=[ytmp.b, ssq.b])
                P.add("dve", lambda e, g=g: e.tensor_scalar(out=ssq.ap[:, g:g + 1], in0=ssq.ap[:, g:g + 1], scalar1=1.0 / 512, scalar2=1e-5, op0=ALU.mult, op1=ALU.add), reads=[ssq.b], writes=[ssq.b])
                P.add("act", lambda e, g=g: e.activation(out=ssq.ap[:, g:g + 1], in_=ssq.ap[:, g:g + 1], func=AF.Sqrt), reads=[ssq.b], writes=[ssq.b])
                P.add("dve", lambda e, g=g: e.reciprocal(out=ssq.ap[:, g:g + 1], in_=ssq.ap[:, g:g + 1]), reads=[ssq.b], writes=[ssq.b])
                P.add("dve", lambda e, g=g: e.scalar_tensor_tensor(out=yn.ap[:, g * 512:(g + 1) * 512], in0=yacc.ap, scalar=ssq.ap[:, g:g + 1], in1=nwb.ap[:, g * 512:(g + 1) * 512], op0=ALU.mult, op1=ALU.mult), reads=[yacc.b, ssq.b, nwb.b], writes=[yn.b])
            dma("sp", ynd[ci * 128:(ci + 1) * 128, :], yn.ap, [yn.b], [])
        if CUT <= 8: return
        P.add("dve", lambda e: e.tensor_tensor(out=S(6), in0=S(4), in1=S(3), op=ALU.subtract), reads=[sm.b], writes=[sm.b])
        P.add("act", lambda e: e.activation(out=S(6), in_=S(6), func=AF.Exp), reads=[sm.b], writes=[sm.b])
        P.add("act", lambda e: e.activation(out=S(7), in_=S(4), func=AF.Exp), reads=[sm.b], writes=[sm.b])
        P.add("dve", lambda e: e.tensor_tensor(out=S(6), in0=S(6), in1=S(1), op=ALU.mult), reads=[sm.b], writes=[sm.b])
        for g in range(4):
            xg = xtok.ap[:, g * 512:(g + 1) * 512].rearrange("p (a b) -> p a b", a=8)
            v3 = lambda ap: ap.rearrange("p (a b) -> p a b", a=8)
            P.add("dve", lambda e, g=g, xg=xg: e.tensor_tensor(out=v3(xdts.ap), in0=xg, in1=bc(sm.ap[:, 6, 8 * g:8 * g + 8], 64), op=ALU.mult), reads=[xtok.b, sm.b], writes=[xdts.b])
            bank = pf[2 + g % 2]
            P.add("pe", lambda e, g=g, bank=bank: e.matmul(bank.ap, xtok.ap[:, 2048 + g * 128:2048 + (g + 1) * 128], xdts.ap, start=True, stop=True), reads=[xtok.b, xdts.b], writes=[bank.b])
            Hg = H.ap[:, g * 512:(g + 1) * 512]
            P.add("dve", lambda e, g=g, Hg=Hg: e.tensor_tensor(out=v3(Hg), in0=v3(Hg), in1=bc(sm.ap[:, 7, 8 * g:8 * g + 8], 64), op=ALU.mult), reads=[H.b, sm.b], writes=[H.b])
            P.add("dve", lambda e, Hg=Hg, bank=bank: e.tensor_tensor(out=Hg, in0=Hg, in1=bank.ap, op=ALU.add), reads=[H.b, bank.b], writes=[H.b])

    for ci in range(0 if SKIP1 else NCH1):
        ssd_chunk(xmain[(ci + 1) * 128:(ci + 2) * 128, :], flg.ap[:, NPRE + ci:NPRE + ci + 1], True, ci)

    if stop < 2:
        P.emit(nc); es.close(); return nc
    P.barrier()
    AB.off, AFa.off = markB, markF
    Wq = AB.get("Wq", 8, 1024); load_w(Wq, w_in[:, COL_Q:COL_Q + 1024], 1024)
    Wk2 = AB.get("Wk2", 8, 128); load_w(Wk2, w_in[:, COL_K:COL_K + 128], 128)
    Wv = AB.get("Wv", 8, 128); load_w(Wv, w_in[:, COL_V:COL_V + 128], 128)
    EB = AB.get("EB", 2, 16, 128)
    ebf = AFa.get("ebf", 2048); mkf = AFa.get("mkf", 2048)
    for kt in range(2):
        dma("sp", ebf.ap, biasg[:, kt * 2048:(kt + 1) * 2048], [], [ebf.b])
        dma("sp", mkf.ap, maskg[:, kt * 2048:(kt + 1) * 2048], [], [mkf.b])
        P.add("act", lambda e: e.activation(out=ebf.ap, in_=ebf.ap, func=AF.Exp), reads=[ebf.b], writes=[ebf.b])
        P.add("dve", lambda e, kt=kt: e.tensor_tensor(out=EB.ap[:, kt, :, :].rearrange("p a b -> p (a b)"), in0=ebf.ap, in1=mkf.ap, op=ALU.mult), reads=[ebf.b, mkf.b], writes=[EB.b])
    xb = AB.get("xb2", 1024); xT = AB.get("xT2", 8, 128)
    kT = [AB.get("kT%d" % i, 2, 128) for i in range(2)]
    vx = [AB.get("vx%d" % i, 2, 65) for i in range(2)]
    for i in range(2):
        P.add("pool", lambda e, i=i: e.memset(vx[i].ap, 1.0), writes=[vx[i].b])
    qT = AB.get("qT", 16, 128); et = AB.get("et", 4, 128)
    PT = [AB.get("PT%d" % i, 4, 128) for i in range(2)]
    ya = AB.get("ya", 1024)
    den = AFa.get("den", 4)
    flag0 = flg.ap[:, NPRE:NPRE + 1]
    CUT2 = int(os.environ.get('KCUT2', '99')); NCH2 = int(os.environ.get('KNCH2', str(NM2)))
    for ci in range(NCH2):
        sl = ci % 2
        load_xT(xmain[ci * 128:(ci + 1) * 128, :], xb, xT)
        for kv in range(2):
            chain(pf[0], pf[0].ap[0:64, kv * 128:(kv + 1) * 128], [(Wk2.ap[:, k, kv * 64:(kv + 1) * 64], xT.ap[:, k, :]) for k in range(8)], [Wk2.b, xT.b])
        P.add("act", lambda e, sl=sl: e.copy(out=kT[sl].ap[0:64, :, :], in_=pf[0].ap[0:64, 0:256].rearrange("p (a b) -> p a b", a=2)), reads=[pf[0].b], writes=[kT[sl].b])
        chain(pf[1], pf[1].ap[:, 0:128], [(xT.ap[:, k, :], Wv.ap[:, k, :]) for k in range(8)], [xT.b, Wv.b])
        P.add("dve", lambda e, sl=sl: e.tensor_copy(out=vx[sl].ap[:, :, 0:64], in_=pf[1].ap[:, 0:128].rearrange("p (a b) -> p a b", a=2)), reads=[pf[1].b], writes=[vx[sl].b])
        if ci == 0 or CUT2 <= 1:
            continue
        for q4 in range(4):
            bank = pf[2 + q4 % 2]
            for tt in range(4):
                j = q4 * 4 + tt
                chain(bank, bank.ap[0:64, tt * 128:(tt + 1) * 128], [(Wq.ap[:, k, j * 64:(j + 1) * 64], xT.ap[:, k, :]) for k in range(8)], [Wq.b, xT.b])
            P.add("act", lambda e, q4=q4, bank=bank: e.copy(out=qT.ap[0:64, q4 * 4:q4 * 4 + 4, :], in_=bank.ap[0:64, :].rearrange("p (a b) -> p a b", a=4)), reads=[bank.b], writes=[qT.b])
        for kvh in range(2):
            if CUT2 <= 2: break
            for hb in range(2):
                j0 = kvh * 8 + hb * 4
                for kt in range(2):
                    slk = (ci + 1 + kt) % 2
                    bank = pf[4 + kt]
                    for i in range(4):
                        j = j0 + i
                        base = (j % 2) * 64 * int(os.environ.get("KB64", "1"))
                        P.add("pe", lambda e, i=i, j=j, base=base, slk=slk, bank=bank, kvh=kvh: e.matmul(bank.ap[:, i * 128:(i + 1) * 128], kT[slk].ap[0:64, kvh, :], qT.ap[0:64, j, :], start=True, stop=True), reads=[kT[slk].b, qT.b], writes=[bank.b])
                    P.add("act", lambda e, bank=bank: e.activation(out=et.ap.rearrange("p a b -> p (a b)"), in_=bank.ap, func=AF.Exp, scale=0.125), reads=[bank.b], writes=[et.b])
                    if ci == 2 and kt == 0:
                        P.add("dve", lambda e, kt=kt, j0=j0: e.scalar_tensor_tensor(out=PT[kt].ap, in0=et.ap, scalar=flag0, in1=EB.ap[:, kt, j0:j0 + 4, :], op0=ALU.mult, op1=ALU.mult), reads=[et.b, EB.b, flg.b], writes=[PT[kt].b])
                    else:
                        P.add("dve", lambda e, kt=kt, j0=j0: e.tensor_tensor(out=PT[kt].ap, in0=et.ap, in1=EB.ap[:, kt, j0:j0 + 4, :], op=ALU.mult), reads=[et.b, EB.b], writes=[PT[kt].b])
                if CUT2 <= 3: continue
                bank = pf[hb]
                for i in range(4):
                    for kt in range(2):
                        slk = (ci + 1 + kt) % 2
                        P.add("pe", lambda e, i=i, kt=kt, slk=slk, bank=bank, kvh=kvh: e.matmul(bank.ap[:, i * 65:(i + 1) * 65], PT[kt].ap[:, i, :], vx[slk].ap[:, kvh, :], start=(kt == 0), stop=(kt == 1)), reads=[PT[kt].b, vx[slk].b], writes=[bank.b])
                if CUT2 <= 4: continue
                pv = bank.ap[:, 0:260].rearrange("p (a b) -> p a b", a=4)
                P.add("dve", lambda e, pv=pv, j0=j0: e.tensor_tensor(out=den.ap, in0=pv[:, :, 64], in1=esink[:, j0:j0 + 4], op=ALU.add), reads=[bank.b, smallp.b], writes=[den.b])
                P.add("dve", lambda e: e.reciprocal(out=den.ap, in_=den.ap), reads=[den.b], writes=[den.b])
                P.add("dve", lambda e, pv=pv, j0=j0: e.tensor_tensor(out=ya.ap[:, j0 * 64:(j0 + 4) * 64].rearrange("p (a b) -> p a b", a=4), in0=pv[:, :, 0:64], in1=bc(den.ap, 64), op=ALU.mult), reads=[bank.b, den.b], writes=[ya.b])
        dma("sp", yad[(ci - 1) * 128:ci * 128, :], ya.ap, [ya.b], [])

    if stop < 3:
        P.emit(nc); es.close(); return nc
    P.barrier()
    AB.off, AFa.off = markB, markF
    Wg = AB.get("Wg", 8, 2048); load_w(Wg, w_in[:, COL_G:COL_G + 2048], 2048)
    Wbs = AB.get("Wbs", 16, 1024); load_w(Wbs, w_bs, 1024)
    Wba = AB.get("Wba", 8, 1024); load_w(Wba, w_ba, 1024)
    Wmx = AB.get("Wmx", 8, 1024); load_w(Wmx, w_mix, 1024)
    bgb = AFa.get("bgb", 2048); dma("sp", bgb.ap, bcast_row(b_gate, 2048), [], [bgb.b])
    lng = AFa.get("lng", 2, 1024)
    dma("sp", lng.ap[:, 0, :], bcast_row(ln1g, 1024), [], [lng.b]); dma("sp", lng.ap[:, 1, :], bcast_row(ln1b, 1024), [], [lng.b])
    xb = AB.get("xb3", 1024); xT = AB.get("xT3", 8, 128)
    ynb = AB.get("ynb", 2048); yab = AB.get("yab", 1024)
    ynT = AB.get("ynT", 16, 128); yaT = AB.get("yaT", 8, 128)
    mg = AB.get("mg", 1024); mT = AB.get("mT", 8, 128)
    xf = AFa.get("xf", 1024); gt = AFa.get("gt", 2048); m1 = AFa.get("m1", 512); r = AFa.get("r", 1024)
    st = AFa.get("st", 2, 6); mv = AFa.get("mv", 2)

    def transp(src, dst, ntl):
        for bt in range(ntl // 8):
            for i in range(8):
                j = bt * 8 + i
                P.add("pe", lambda e, i=i, j=j: e.transpose(pb[1].ap[:, i * 128:(i + 1) * 128], src.ap[:, j * 128:(j + 1) * 128], identb), reads=[src.b, cb16.b], writes=[pb[1].b])
            P.add("act", lambda e, bt=bt: e.copy(out=dst.ap[:, bt * 8:bt * 8 + 8, :].rearrange("p a b -> p (a b)"), in_=pb[1].ap), reads=[pb[1].b], writes=[dst.b])

    def layer_norm(r, g_ap, b_ap, gb, st, mv):
        for i in range(2):
            P.add("dve", lambda e, i=i: e.bn_stats(out=st.ap[:, i, :], in_=r.ap[:, i * 512:(i + 1) * 512]), reads=[r.b], writes=[st.b])
        P.add("dve", lambda e: e.bn_aggr(out=mv.ap, in_=st.ap.rearrange("p a b -> p (a b)")), reads=[st.b], writes=[mv.b])
        P.add("dve", lambda e: e.tensor_scalar(out=mv.ap[:, 1:2], in0=mv.ap[:, 1:2], scalar1=1e-5, scalar2=None, op0=ALU.add), reads=[mv.b], writes=[mv.b])
        P.add("act", lambda e: e.activation(out=mv.ap[:, 1:2], in_=mv.ap[:, 1:2], func=AF.Sqrt), reads=[mv.b], writes=[mv.b])
        P.add("dve", lambda e: e.reciprocal(out=mv.ap[:, 1:2], in_=mv.ap[:, 1:2]), reads=[mv.b], writes=[mv.b])
        P.add("dve", lambda e: e.tensor_scalar(out=r.ap, in0=r.ap, scalar1=mv.ap[:, 0:1], scalar2=mv.ap[:, 1:2], op0=ALU.subtract, op1=ALU.mult), reads=[r.b, mv.b], writes=[r.b])
        P.add("dve", lambda e: e.tensor_tensor(out=r.ap, in0=r.ap, in1=g_ap, op=ALU.mult), reads=[r.b, gb], writes=[r.b])
        P.add("dve", lambda e: e.tensor_tensor(out=r.ap, in0=r.ap, in1=b_ap, op=ALU.add), reads=[r.b, gb], writes=[r.b])

    for ci in range(NM1):
        xrow = xmain[(ci + 1) * 128:(ci + 2) * 128, :]
        load_xT(xrow, xb, xT)
        dma("sp", xf.ap, xrow, [], [xf.b])
        dma("sp", ynb.ap, ynd[ci * 128:(ci + 1) * 128, :], [], [ynb.b])
        dma("sp", yab.ap, yad[ci * 128:(ci + 1) * 128, :], [], [yab.b])
        transp(ynb, ynT, 16)
        transp(yab, yaT, 8)
        for s4 in range(4):
            bank = pf[s4 % 2]
            chain(bank, bank.ap, [(xT.ap[:, k, :], Wg.ap[:, k, s4 * 512:(s4 + 1) * 512]) for k in range(8)], [xT.b, Wg.b])
            P.add("dve", lambda e, s4=s4, bank=bank: e.tensor_tensor(out=gt.ap[:, s4 * 512:(s4 + 1) * 512], in0=bank.ap, in1=bgb.ap[:, s4 * 512:(s4 + 1) * 512], op=ALU.add), reads=[bank.b, bgb.b], writes=[gt.b])
        P.add("act", lambda e: e.activation(out=gt.ap, in_=gt.ap, func=AF.Sigmoid), reads=[gt.b], writes=[gt.b])
        for hf in range(2):
            chain(pf[2], pf[2].ap, [(ynT.ap[:, i, :], Wbs.ap[:, i, hf * 512:(hf + 1) * 512]) for i in range(16)], [ynT.b, Wbs.b])
            chain(pf[3], pf[3].ap, [(yaT.ap[:, i, :], Wba.ap[:, i, hf * 512:(hf + 1) * 512]) for i in range(8)], [yaT.b, Wba.b])
            P.add("dve", lambda e, hf=hf: e.tensor_tensor(out=m1.ap, in0=pf[2].ap, in1=gt.ap[:, hf * 512:(hf + 1) * 512], op=ALU.mult), reads=[pf[2].b, gt.b], writes=[m1.b])
            P.add("dve", lambda e, hf=hf: e.tensor_tensor(out=r.ap[:, hf * 512:(hf + 1) * 512], in0=pf[3].ap, in1=gt.ap[:, 1024 + hf * 512:1024 + (hf + 1) * 512], op=ALU.mult), reads=[pf[3].b, gt.b], writes=[r.b])
            P.add("dve", lambda e, hf=hf: e.tensor_tensor(out=mg.ap[:, hf * 512:(hf + 1) * 512], in0=m1.ap, in1=r.ap[:, hf * 512:(hf + 1) * 512], op=ALU.add), reads=[m1.b, r.b], writes=[mg.b])
        transp(mg, mT, 8)
        for hf in range(2):
            bank = pf[4 + hf]
            chain(bank, bank.ap, [(mT.ap[:, i, :], Wmx.ap[:, i, hf * 512:(hf + 1) * 512]) for i in range(8)], [mT.b, Wmx.b])
            P.add("dve", lambda e, hf=hf, bank=bank: e.scalar_tensor_tensor(out=r.ap[:, hf * 512:(hf + 1) * 512], in0=xf.ap[:, hf * 512:(hf + 1) * 512], scalar=ALPHA, in1=bank.ap, op0=ALU.mult, op1=ALU.add), reads=[xf.b, bank.b], writes=[r.b])
        layer_norm(r, lng.ap[:, 0, :], lng.ap[:, 1, :], lng.b, st, mv)
        if ci == 0:
            P.add("dve", lambda e: e.tensor_scalar(out=r.ap, in0=r.ap, scalar1=flag0, scalar2=None, op0=ALU.mult), reads=[r.b, flg.b], writes=[r.b])
        dma("sp", h1d[ci * 128:(ci + 1) * 128, :], r.ap, [r.b], [])

    if stop < 4:
        P.emit(nc); es.close(); return nc
    P.barrier()
    AB.off, AFa.off = markB, markF
    Wup = AB.get("Wup", 8, 5632); load_w(Wup, w_up, 5632)
    Wdn = AB.get("Wdn", 22, 1024); load_w(Wdn, w_dn, 1024)
    fw = AFa.get("fw", 132); fb = AFa.get("fb", 44)
    per_channel(fw.ap[:, 0:88], fcw[0:88, :], 88); fw.b.writer = P.ops["dve"][-1]
    per_channel(fw.ap[:, 88:132], fcw[88:132, :], 44); fw.b.writer = P.ops["dve"][-1]
    per_channel(fb.ap, fcb, 44); fb.b.writer = P.ops["dve"][-1]
    lng2 = AFa.get("lng2", 2, 1024)
    dma("sp", lng2.ap[:, 0, :], bcast_row(ln2g, 1024), [], [lng2.b]); dma("sp", lng2.ap[:, 1, :], bcast_row(ln2b, 1024), [], [lng2.b])
    SC = 256
    hA = AB.get("hA", 1024); hB = AB.get("hB", 1024); h1T = AB.get("h1T", 8, SC + 2)
    aT = AB.get("aT", 22, SC)
    cvs = [AFa.get("cvs%d" % i, SC) for i in range(4)]
    t0s = [AFa.get("t0s%d" % i, SC) for i in range(2)]
    sg = AFa.get("sg", SC)
    hr = AFa.get("hr", 1024); r4 = AFa.get("r4", 1024)
    st4 = AFa.get("st4", 2, 6); mv4 = AFa.get("mv4", 2)
    NT = SC // 128
    tcount = 0
    for sc in range(TOK // SC):
        r0 = 128 + sc * SC
        for i in range(NT):
            dma("pool", hA.ap, h1d[r0 - 2 + i * 128:r0 + 126 + i * 128, :], [], [hA.b])
            for k in range(8):
                P.add("pe", lambda e, k=k: e.transpose(pb[0].ap[:, k * 128:(k + 1) * 128], hA.ap[:, k * 128:(k + 1) * 128], identb), reads=[hA.b, cb16.b], writes=[pb[0].b])
            P.add("act", lambda e, i=i: e.copy(out=h1T.ap[:, :, i * 128:(i + 1) * 128], in_=pb[0].ap.rearrange("p (a b) -> p a b", a=8)), reads=[pb[0].b], writes=[h1T.b])
        dma("pool", hB.ap[0:2, :], h1d[r0 + SC - 2:r0 + SC, :], [], [hB.b])
        for k in range(8):
            P.add("pe", lambda e, k=k: e.transpose(pb[1].ap[:, k * 2:(k + 1) * 2], hB.ap[0:2, k * 128:(k + 1) * 128], identb[0:2, 0:2]), reads=[hB.b, cb16.b], writes=[pb[1].b])
        P.add("act", lambda e: e.copy(out=h1T.ap[:, :, SC:SC + 2], in_=pb[1].ap[:, 0:16].rearrange("p (a b) -> p a b", a=8)), reads=[pb[1].b], writes=[h1T.b])
        for jp in range(22):
            for gv in range(2):
                j = jp + 22 * gv
                X = pf[tcount % 4]; t0 = t0s[tcount % 2]
                cv = cvs[(jp % 2) * 2 + gv]
                tcount += 1
                chain(X, X.ap[:, 0:SC + 2], [(Wup.ap[:, k, j * 128:(j + 1) * 128], h1T.ap[:, k, 0:SC + 2]) for k in range(8)], [Wup.b, h1T.b])
                w0, w1, w2, bb = fw.ap[:, j:j + 1], fw.ap[:, 44 + j:45 + j], fw.ap[:, 88 + j:89 + j], fb.ap[:, j:j + 1]
                P.add("act", lambda e, X=X, t0=t0, w2=w2, bb=bb: e.activation(out=t0.ap, in_=X.ap[:, 2:SC + 2], func=AF.Identity, bias=bb, scale=w2), reads=[X.b, fw.b, fb.b], writes=[t0.b])
                P.add("dve", lambda e, X=X, t0=t0, w1=w1: e.scalar_tensor_tensor(out=t0.ap, in0=X.ap[:, 1:SC + 1], scalar=w1, in1=t0.ap, op0=ALU.mult, op1=ALU.add), reads=[X.b, fw.b, t0.b], writes=[t0.b])
                P.add("dve", lambda e, X=X, t0=t0, w0=w0, cv=cv: e.scalar_tensor_tensor(out=cv.ap, in0=X.ap[:, 0:SC], scalar=w0, in1=t0.ap, op0=ALU.mult, op1=ALU.add), reads=[X.b, fw.b, t0.b], writes=[cv.b])
            cg, cvv = cvs[(jp % 2) * 2], cvs[(jp % 2) * 2 + 1]
            P.add("act", lambda e, cg=cg: e.activation(out=sg.ap, in_=cg.ap, func=AF.Silu), reads=[cg.b], writes=[sg.b])
            P.add("dve", lambda e, jp=jp, cvv=cvv: e.tensor_tensor(out=aT.ap[:, jp, :], in0=sg.ap, in1=cvv.ap, op=ALU.mult), reads=[sg.b, cvv.b], writes=[aT.b])
        for tt in range(NT):
            rr = r0 + tt * 128
            dma("sp", hr.ap, h1d[rr:rr + 128, :], [], [hr.b])
            for hf in range(2):
                bank = pf[4 + hf]
                chain(bank, bank.ap, [(aT.ap[:, jj, tt * 128:(tt + 1) * 128], Wdn.ap[:, jj, hf * 512:(hf + 1) * 512]) for jj in range(22)], [aT.b, Wdn.b])
                P.add("dve", lambda e, hf=hf, bank=bank: e.scalar_tensor_tensor(out=r4.ap[:, hf * 512:(hf + 1) * 512], in0=hr.ap[:, hf * 512:(hf + 1) * 512], scalar=ALPHA, in1=bank.ap, op0=ALU.mult, op1=ALU.add), reads=[hr.b, bank.b], writes=[r4.b])
            layer_norm(r4, lng2.ap[:, 0, :], lng2.ap[:, 1, :], lng2.b, st4, mv4)
            dma("sp", out[rr - 128:rr, :], r4.ap, [r4.b], [])

    P.emit(nc)
    es.close()
    return nc


def rel_bucket_np(rel):
    n = np.maximum(rel, 0)
    nf = np.maximum(n, 1).astype(np.float32)
    large = 16 + (np.log(nf / np.float32(16)) / np.float32(np.log(128 / 16)) * np.float32(16)).astype(np.int32)
    large = np.minimum(large, 31)
    return np.where(n < 16, n, large)


_NC = None


def kernel(_dbg=None, **inp):
    global _NC
    x = np.asarray(inp["x"], np.float32)[0]
    f = lambda k: np.ascontiguousarray(np.asarray(inp[k], np.float32)[0])
    common = {
        "w_in": f("w_in"), "b_gate": f("b_gate")[None], "dtb": f("ssm_dt_bias")[None], "alog": f("ssm_a_log")[None],
        "dsk": f("ssm_d")[None], "normw": f("ssm_norm_w")[None], "sinks": f("attn_sinks")[None],
        "w_bs": f("w_branch_ssm"), "w_ba": f("w_branch_attn"), "w_mix": f("w_mix_out"),
        "ln1g": f("ln1_g")[None], "ln1b": f("ln1_b")[None], "ln2g": f("ln2_g")[None], "ln2b": f("ln2_b")[None],
        "w_up": f("w_up"), "w_dn": f("w_down"),
    }
    scw = f("ssm_conv_w")
    common["scw"] = np.ascontiguousarray(scw.reshape(4 * 24, 128))
    common["scb"] = np.ascontiguousarray(f("ssm_conv_b").reshape(24, 128))
    common["fcw"] = np.ascontiguousarray(f("ffn_conv_w").reshape(3 * 44, 128))
    common["fcb"] = np.ascontiguousarray(f("ffn_conv_b").reshape(44, 128))
    s = np.arange(128)
    ident = np.eye(128, dtype=np.float32)
    triU = (s[:, None] <= s[None, :]).astype(np.float32)
    ustr = (s[:, None] > s[None, :]).astype(np.float32)
    common["cst"] = np.ascontiguousarray(np.concatenate([ident, triU, ustr, np.ones((128, 128), np.float32)], axis=1))
    rb = np.asarray(inp["rel_bias"], np.float32)
    bg = np.zeros((128, 2, 16, 128), np.float32); mk = np.zeros((128, 2, 16, 128), np.float32)
    for kt in range(2):
        rel = (s[None, :] + 128) - (s[:, None] + 128 * kt)
        valid = (rel >= 0) & (rel < 128)
        bidx = rel_bucket_np(rel)
        g = rb[bidx]
        bg[:, kt] = np.transpose(g, (0, 2, 1))
        mk[:, kt] = np.broadcast_to(valid[:, None, :], (128, 16, 128))
    common["biasg"] = np.ascontiguousarray(bg.reshape(128, -1)); common["maskg"] = np.ascontiguousarray(mk.reshape(128, -1))
    in_maps = []
    for c in range(NCORE):
        S = c * TOK
        lo = S - 128 - NPRE * 128
        xp = np.zeros((NPRE * 128, 1024), np.float32)
        if S - 128 > 0:
            src_lo = max(lo, 0)
            xp[src_lo - lo:] = x[src_lo:S - 128]
        xm = np.zeros((NM2 * 128, 1024), np.float32)
        lo2 = S - 256
        src_lo = max(lo2, 0)
        xm[src_lo - lo2:] = x[src_lo:S + TOK]
        fl = np.zeros((128, NPRE + NM1), np.float32)
        for i in range(NPRE):
            fl[:, i] = 1.0 if lo + i * 128 >= 0 else 0.0
        fl[:, NPRE] = 1.0 if c > 0 else 0.0
        fl[:, NPRE + 1:] = 1.0
        m = dict(common); m["xpre"] = xp; m["xmain"] = xm; m["pflag"] = fl
        in_maps.append(m)
    if _dbg is not None:
        return in_maps
    if _NC is None:
        _NC = build()
    res = run_bass_kernel_spmd(_NC, in_maps, core_ids=list(range(NCORE)))
    o = np.concatenate([res.results[c]["out"] for c in range(NCORE)], axis=0)
    return o[None].astype(np.float32)
```

```python
import numpy as np
import concourse.bass as bass
import concourse.mybir as mybir

ENGS = ["pe", "act", "dve", "pool", "sp"]
N_DMA_SEM = 16
import os as _os
SAME_ENGINE_SYNC = _os.environ.get("KSES", "1") == "1"


class Buf:
    __slots__ = ("name", "writer", "readers", "dma_readers", "excl")

    def __init__(self, name, excl=False):
        self.name = name
        self.excl = excl
        self.writer = None
        self.readers = {}
        self.dma_readers = []


class Op:
    __slots__ = ("eng", "fn", "idx", "waits", "signal", "is_dma", "dsem", "dtarget", "clock", "sigcount", "uid")


class Prog:
    def __init__(self):
        self.ops = {e: [] for e in ENGS}
        self.clock = {e: {f: 0 for f in ENGS} for e in ENGS}
        self.known_dma = {e: set() for e in ENGS}
        self.dma_sem_count = [0] * N_DMA_SEM
        self.dma_sem_last = [None] * N_DMA_SEM
        self.dma_rr = 0
        self.n_dma = 0
        self.uid = 0
        self.bar = {e: [] for e in ENGS}

    def barrier(self):
        lasts = [self.ops[e][-1] for e in ENGS if self.ops[e] and not self.ops[e][-1].is_dma]
        for e in ENGS:
            pass
        lasts = []
        for e in ENGS:
            for op in reversed(self.ops[e]):
                if not op.is_dma:
                    lasts.append(op)
                    break
        dmas = [op for op in self.dma_sem_last if op is not None]
        for e in ENGS:
            self.bar[e] = lasts + dmas

    def add(self, eng, fn, reads=(), writes=(), dma=False):
        op = Op()
        op.eng = eng
        op.fn = fn
        op.idx = len(self.ops[eng])
        op.waits = []
        op.signal = False
        op.is_dma = dma
        op.uid = self.uid
        self.uid += 1
        deps = []
        for b in reads:
            if b.writer is not None:
                deps.append(b.writer)
            if b.excl:
                for e2, r in b.readers.items():
                    if e2 != eng:
                        deps.append(r)
        for b in writes:
            if b.writer is not None:
                deps.append(b.writer)
            deps.extend(b.readers.values())
            deps.extend(b.dma_readers)
        if self.bar[eng]:
            deps.extend(self.bar[eng])
            self.bar[eng] = []
        clk = self.clock[eng]
        seen = set()
        for d in deps:
            if d.uid in seen:
                continue
            seen.add(d.uid)
            if d.is_dma:
                if d.uid not in self.known_dma[eng]:
                    op.waits.append(("dma", d.dsem, d.dtarget))
                    self.known_dma[eng].add(d.uid)
            else:
                if d.eng == eng and (eng == "pe" or not SAME_ENGINE_SYNC):
                    continue
                if clk[d.eng] < d.idx + 1:
                    op.waits.append(("eng", d))
                    d.signal = True
                    for f in ENGS:
                        if d.clock[f] > clk[f]:
                            clk[f] = d.clock[f]
                    if clk[d.eng] < d.idx + 1:
                        clk[d.eng] = d.idx + 1
        if dma and eng == "pool":
            pd = self.__dict__.setdefault("pool_dmas", [])
            if len(pd) >= 4:
                d = pd[-4]
                if d.uid not in self.known_dma[eng]:
                    op.waits.append(("dma", d.dsem, d.dtarget))
                    self.known_dma[eng].add(d.uid)
            pd.append(op)
        if dma:
            k = self.dma_rr
            self.dma_rr = (self.dma_rr + 1) % N_DMA_SEM
            prev = self.dma_sem_last[k]
            if prev is not None and prev.uid not in self.known_dma[eng]:
                op.waits.append(("dma", k, prev.dtarget))
                self.known_dma[eng].add(prev.uid)
            self.dma_sem_count[k] += 16
            op.dsem = k
            op.dtarget = self.dma_sem_count[k]
            self.dma_sem_last[k] = op
            self.n_dma += 1
        op.clock = dict(clk)
        if not SAME_ENGINE_SYNC or eng == "pe":
            pass
        self.ops[eng].append(op)
        for b in writes:
            b.writer = op
            b.readers = {}
            b.dma_readers = []
        for b in reads:
            if dma:
                b.dma_readers.append(op)
            else:
                b.readers[eng] = op
        return op

    def emit(self, nc, final_waits=True):
        for e in ENGS:
            c = 0
            for op in self.ops[e]:
                if op.signal:
                    c += 1
                op.sigcount = c
        from contextlib import ExitStack
        with ExitStack() as es:
            esem = {e: es.enter_context(nc.semaphore("s_" + e)) for e in ENGS}
            dsem = [es.enter_context(nc.semaphore("d%d" % i)) for i in range(N_DMA_SEM)]
            block = es.enter_context(nc.Block())
            last_dma = [op for op in self.dma_sem_last if op is not None]

            def run(e, h):
                for op in self.ops[e]:
                    for w in op.waits:
                        if w[0] == "dma":
                            h.wait_ge(dsem[w[1]], w[2])
                        else:
                            h.wait_ge(esem[w[1].eng], w[1].sigcount)
                    ins = op.fn(h)
                    if op.is_dma:
                        ins.then_inc(dsem[op.dsem], 16)
                    elif op.signal:
                        ins.then_inc(esem[e], 1)
                if e == "sp" and final_waits:
                    for k in range(N_DMA_SEM):
                        if self.dma_sem_count[k] > 0:
                            h.wait_ge(dsem[k], self.dma_sem_count[k])

            @block.tensor
            def _(h):
                run("pe", h)

            @block.scalar
            def _(h):
                run("act", h)

            @block.vector
            def _(h):
                run("dve", h)

            @block.gpsimd
            def _(h):
                run("pool", h)

            @block.sync
            def _(h):
                run("sp", h)

from contextlib import ExitStack
import ml_dtypes
from concourse.bass_utils import run_bass_kernel_spmd

F32 = mybir.dt.float32
BF16 = mybir.dt.bfloat16
AF = mybir.ActivationFunctionType
ALU = mybir.AluOpType

NCORE = 8
TOK = 2048
NPRE = 112
NM1 = 17
NM2 = 18
ALPHA = 2.0 ** 0.25
COL_Z, COL_X, COL_B, COL_C, COL_DT, COL_Q, COL_K, COL_V, COL_G = 0, 2048, 4096, 4608, 5120, 5152, 6176, 6304, 6432


class TT:
    def __init__(self, ap, name, excl=False):
        self.ap = ap
        self.b = Buf(name, excl)


class Arena:
    def __init__(self, t, n):
        self.t, self.n, self.off = t, n, 0

    def get(self, name, *fs):
        n = int(np.prod(fs))
        ap = self.t[:, self.off:self.off + n]
        self.off += n
        assert self.off <= self.n, (name, self.off, self.n)
        if len(fs) == 2:
            ap = ap.rearrange("p (a b) -> p a b", a=fs[0])
        elif len(fs) == 3:
            ap = ap.rearrange("p (a b c) -> p a b c", a=fs[0], b=fs[1])
        return TT(ap, name)


def bc(ap2, n):
    return ap2.unsqueeze(2).to_broadcast([ap2.shape[0], ap2.shape[1], n])


def build(stop=4, npre=NPRE, dbg=False):
    nc = bass.Bass("TRN2", target_bir_lowering=False)
    dt_in = lambda n, s: nc.dram_tensor(n, s, F32, kind="ExternalInput").ap()
    xpre = dt_in("xpre", [NPRE * 128, 1024])
    xmain = dt_in("xmain", [NM2 * 128, 1024])
    pflag = dt_in("pflag", [128, NPRE + NM1])
    w_in = dt_in("w_in", [1024, 8480])
    b_gate = dt_in("b_gate", [1, 2048])
    scw = dt_in("scw", [96, 128])
    scb = dt_in("scb", [24, 128])
    dtb = dt_in("dtb", [1, 32])
    alog = dt_in("alog", [1, 32])
    dsk = dt_in("dsk", [1, 32])
    normw = dt_in("normw", [1, 2048])
    sinks = dt_in("sinks", [1, 16])
    w_bs = dt_in("w_bs", [2048, 1024])
    w_ba = dt_in("w_ba", [1024, 1024])
    w_mix = dt_in("w_mix", [1024, 1024])
    ln1g = dt_in("ln1g", [1, 1024]); ln1b = dt_in("ln1b", [1, 1024])
    ln2g = dt_in("ln2g", [1, 1024]); ln2b = dt_in("ln2b", [1, 1024])
    w_up = dt_in("w_up", [1024, 5632])
    fcw = dt_in("fcw", [132, 128])
    fcb = dt_in("fcb", [44, 128])
    w_dn = dt_in("w_dn", [2816, 1024])
    cst = dt_in("cst", [128, 4 * 128])
    biasg = dt_in("biasg", [128, 2 * 16 * 128])
    maskg = dt_in("maskg", [128, 2 * 16 * 128])
    out = nc.dram_tensor("out", [TOK, 1024], F32, kind="ExternalOutput").ap()
    skind = "ExternalOutput" if dbg else "Internal"
    ynd = nc.dram_tensor("ynd", [NM1 * 128, 2048], BF16, kind=skind).ap()
    yad = nc.dram_tensor("yad", [NM1 * 128, 1024], BF16, kind=skind).ap()
    h1d = nc.dram_tensor("h1d", [NM1 * 128, 1024], F32, kind=skind).ap()

    P = Prog()
    es = ExitStack()
    NB, NF = 157 * 512, 47 * 256
    ABt = es.enter_context(nc.sbuf_tensor("AB", [128, NB], BF16))
    AFt = es.enter_context(nc.sbuf_tensor("AF", [128, NF], F32))
    pf = [TT(es.enter_context(nc.psum_tensor("pf%d" % i, [128, 512], F32))[:], "pf%d" % i, True) for i in range(6)]
    pb = [TT(es.enter_context(nc.psum_tensor("pb%d" % i, [128, 1024], BF16))[:], "pb%d" % i, True) for i in range(2)]
    AB = Arena(ABt, NB)
    AFa = Arena(AFt, NF)

    def bcast_row(src, n):
        return bass.AP(src.tensor, 0, [[0, 128], [1, n]])

    def dma(eng, o, i, reads, writes):
        P.add(eng, lambda e, o=o, i=i: e.dma_start(out=o, in_=i), reads=reads, writes=writes, dma=True)

    cf = AFa.get("cf", 4, 128)
    dma("sp", cf.ap, cst.rearrange("p (a b) -> p a b", a=4), [], [cf.b])
    identf, triU, Ustr, onesf = cf.ap[:, 0, :], cf.ap[:, 1, :], cf.ap[:, 2, :], cf.ap[:, 3, :]
    cb16 = AB.get("cb16", 4, 128)
    dma("pool", cb16.ap, cst.rearrange("p (a b) -> p a b", a=4), [], [cb16.b])
    identb, maskb = cb16.ap[:, 0, :], cb16.ap[:, 1, :]
    flg = AFa.get("flg", NPRE + NM1)
    dma("sp", flg.ap, pflag, [], [flg.b])
    smallp = AFa.get("smallp", 6, 32)
    dma("sp", smallp.ap[:, 0, :], bcast_row(dtb, 32), [], [smallp.b])
    dma("sp", smallp.ap[:, 1, :], bcast_row(alog, 32), [], [smallp.b])
    dma("sp", smallp.ap[:, 2, :], bcast_row(dsk, 32), [], [smallp.b])
    dma("sp", smallp.ap[:, 3, 0:16], bcast_row(sinks, 16), [], [smallp.b])
    P.add("act", lambda e: e.activation(out=smallp.ap[:, 1, :], in_=smallp.ap[:, 1, :], func=AF.Exp), reads=[smallp.b], writes=[smallp.b])
    P.add("dve", lambda e: e.tensor_scalar(out=smallp.ap[:, 1, :], in0=smallp.ap[:, 1, :], scalar1=-1.0, scalar2=None, op0=ALU.mult), reads=[smallp.b], writes=[smallp.b])
    P.add("act", lambda e: e.activation(out=smallp.ap[:, 3, 0:16], in_=smallp.ap[:, 3, 0:16], func=AF.Exp), reads=[smallp.b], writes=[smallp.b])
    dtb_bc, a_bc, D_bc, esink = smallp.ap[:, 0, :], smallp.ap[:, 1, :], smallp.ap[:, 2, :], smallp.ap[:, 3, 0:16]
    onecol = onesf[:, 0:1]
    rawh = AB.get("rawh", 24, 4)
    markB, markF = AB.off, AFa.off
    H = AFa.get("H", 2048)

    def load_w(dst, src, ncols, nk=8):
        nk = src.shape[0] // 128
        for c0 in range(0, ncols, 512):
            c1 = min(c0 + 512, ncols)
            for k in range(nk):
                dma("pool", dst.ap[:, k, c0:c1], src[k * 128:(k + 1) * 128, c0:c1], [], [dst.b])

    def load_xT(xrow_ap, xb, xT):
        dma("pool", xb.ap, xrow_ap, [], [xb.b])
        for k in range(8):
            P.add("pe", lambda e, k=k: e.transpose(pb[0].ap[:, k * 128:(k + 1) * 128], xb.ap[:, k * 128:(k + 1) * 128], identb), reads=[xb.b, cb16.b], writes=[pb[0].b])
        P.add("act", lambda e: e.copy(out=xT.ap.rearrange("p a b -> p (a b)"), in_=pb[0].ap), reads=[pb[0].b], writes=[xT.b])

    def chain(bank, out_ap, pairs, reads):
        n = len(pairs)
        for i, (l, r) in enumerate(pairs):
            P.add("pe", lambda e, l=l, r=r, i=i: e.matmul(out_ap, l, r, start=(i == 0), stop=(i == n - 1)), reads=reads, writes=[bank.b])

    def per_channel(dst, src_dram, rows):
        tmp = AFa.get("pc_tmp", 128)
        dma("sp", tmp.ap[0:rows, :], src_dram, [], [tmp.b])
        P.add("pe", lambda e: e.transpose(pf[5].ap[:, 0:rows], tmp.ap[0:rows, :], identf[0:rows, 0:rows]), reads=[tmp.b, cf.b], writes=[pf[5].b])
        P.add("dve", lambda e: e.tensor_copy(out=dst, in_=pf[5].ap[:, 0:rows]), reads=[pf[5].b], writes=[])

    import os
    SKIP1 = int(os.environ.get('KSKIP1', '0'))
    Wx = AB.get("Wx", 8, 3072); load_w(Wx, w_in[:, COL_X:COL_X + 3072], 3072)
    Wdt = AB.get("Wdt", 8, 32); load_w(Wdt, w_in[:, COL_DT:COL_DT + 32], 32)
    cwt = AFa.get("cwt", 96); cbt = AFa.get("cbt", 24)
    per_channel(cwt.ap, scw, 96)
    per_channel(cbt.ap, scb, 24)
    cwt.b.writer = P.ops["dve"][-2]; cbt.b.writer = P.ops["dve"][-1]
    diag = AB.get("diag", 24, 4, 128)
    for j in range(24):
        for tp in range(4):
            P.add("dve", lambda e, j=j, tp=tp: e.tensor_scalar(out=diag.ap[:, j, tp, :], in0=identf, scalar1=cwt.ap[:, tp * 24 + j:tp * 24 + j + 1], scalar2=None, op0=ALU.mult), reads=[cf.b, cwt.b], writes=[diag.b])
    P.add("dve", lambda e: e.memset(H.ap, 0.0), writes=[H.b])
    mark1B, mark1F = AB.off, AFa.off
    xbA = [AB.get("xbA%d" % i, 1024) for i in range(2)]
    xT4 = [AB.get("xT4_%d" % i, 8, 512) for i in range(2)]
    raw4 = AB.get("raw4", 24, 516); xc4 = AB.get("xc4", 24, 512)
    rawb = [Buf("rawb%d" % j) for j in range(24)]; xcb = [Buf("xcb%d" % j) for j in range(24)]
    xtk = [AB.get("xtk%d" % i, 2560) for i in range(2)]
    xdtsA4 = [AB.get("xdtsA%d" % i, 512) for i in range(4)]
    P.add("pool", lambda e: e.memset(raw4.ap, 0.0), writes=rawb)

    def load_group(gi, buf):
        for q in range(4):
            c = gi * 4 + q
            xb_ = xbA[q % 2]
            dma("pool", xb_.ap, xpre[c * 128:(c + 1) * 128, :], [], [xb_.b])
            for k in range(8):
                P.add("pe", lambda e, k=k, xb_=xb_: e.transpose(pb[0].ap[:, k * 128:(k + 1) * 128], xb_.ap[:, k * 128:(k + 1) * 128], identb), reads=[xb_.b, cb16.b], writes=[pb[0].b])
            P.add("act", lambda e, q=q, buf=buf: e.copy(out=xT4[buf].ap[:, :, q * 128:(q + 1) * 128], in_=pb[0].ap.rearrange("p (a b) -> p a b", a=8)), reads=[pb[0].b], writes=[xT4[buf].b])

    def proj_in(gi, buf, last, j0, j1):
        ntile = 24 if last else 20
        for j in range(j0, min(j1, ntile)):
            bank = pf[j % 2]
            chain(bank, bank.ap, [(Wx.ap[:, k, j * 128:(j + 1) * 128], xT4[buf].ap[:, k, :]) for k in range(8)], [Wx.b, xT4[buf].b])
            P.add("act", lambda e, j=j, bank=bank: e.copy(out=raw4.ap[:, j, 3:515], in_=bank.ap), reads=[bank.b], writes=[rawb[j]])

    def proj_conv(gi, last):
        ntile = 24 if last else 20
        for j in range(ntile):
            bank = pf[2 + j % 2]
            chain(bank, bank.ap, [(diag.ap[:, j, tp, :], raw4.ap[:, j, tp:tp + 512]) for tp in range(4)], [diag.b, rawb[j]])
            P.add("act", lambda e, j=j, bank=bank: e.activation(out=xc4.ap[:, j, :], in_=bank.ap, func=AF.Silu, bias=cbt.ap[:, j:j + 1], scale=1.0), reads=[bank.b, cbt.b], writes=[xcb[j]])
        P.add("pool", lambda e: e.tensor_copy(out=raw4.ap[:, :, 0:3], in_=raw4.ap[:, :, 512:515]), reads=rawb, writes=rawb)

    smGs = [AFa.get("smG%d" % i, 8, 128) for i in range(2)]

    def group_chunks(gi, buf, hooks):
        c0 = gi * 4
        smG = smGs[gi % 2]
        S = lambda i: smG.ap[:, i, :]
        S3 = lambda i: smG.ap[:, i, :].rearrange("p (q h) -> p q h", q=4)
        b4 = lambda ap: ap.unsqueeze(1).to_broadcast([128, 4, 32])
        v3 = lambda ap: ap.rearrange("p (a b) -> p a b", a=8)
        for q in range(4):
            chain(pf[4], pf[4].ap[:, q * 32:(q + 1) * 32], [(xT4[buf].ap[:, k, q * 128:(q + 1) * 128], Wdt.ap[:, k, :]) for k in range(8)], [xT4[buf].b, Wdt.b])
        P.add("dve", lambda e: e.tensor_tensor(out=S3(0), in0=pf[4].ap[:, 0:128].rearrange("p (q h) -> p q h", q=4), in1=b4(dtb_bc), op=ALU.add), reads=[pf[4].b, smallp.b], writes=[smG.b])
        P.add("act", lambda e: e.activation(out=S(0), in_=S(0), func=AF.Exp), reads=[smG.b], writes=[smG.b])
        P.add("act", lambda e: e.activation(out=S(1), in_=S(0), func=AF.Ln, bias=onecol, scale=1.0), reads=[smG.b, cf.b], writes=[smG.b])
        P.add("dve", lambda e: e.tensor_tensor(out=S3(1), in0=S3(1), in1=bc(flg.ap[:, c0:c0 + 4], 32), op=ALU.mult), reads=[smG.b, flg.b], writes=[smG.b])
        P.add("dve", lambda e: e.tensor_tensor(out=S3(2), in0=S3(1), in1=b4(a_bc), op=ALU.mult), reads=[smG.b, smallp.b], writes=[smG.b])

        def tr(q):
            xt = xtk[q % 2]
            for bt in range(3):
                nt = 8 if bt < 2 else 4
                pbk = pb[bt % 2]
                for i in range(nt):
                    jj = bt * 8 + i
                    P.add("pe", lambda e, i=i, jj=jj, q=q, pbk=pbk: e.transpose(pbk.ap[:, i * 128:(i + 1) * 128], xc4.ap[:, jj, q * 128:(q + 1) * 128], identb), reads=[xcb[jj], cb16.b], writes=[pbk.b])
                if bt == 1:
                    P.add("act", lambda e, bt=bt, nt=nt, xt=xt, pbk=pbk: e.copy(out=xt.ap[:, bt * 1024:bt * 1024 + nt * 128], in_=pbk.ap[:, 0:nt * 128]), reads=[pbk.b], writes=[xt.b])
                else:
                    P.add("dve", lambda e, bt=bt, nt=nt, xt=xt, pbk=pbk: e.tensor_copy(out=xt.ap[:, bt * 1024:bt * 1024 + nt * 128], in_=pbk.ap[:, 0:nt * 128]), reads=[pbk.b], writes=[xt.b])

        SBK = [pf[2], pf[3], pf[5], pf[4]]

        def state(q):
            xt = xtk[q % 2]
            for g in range(4):
                xg = xt.ap[:, g * 512:(g + 1) * 512].rearrange("p (a b) -> p a b", a=8)
                o = q * 32 + 8 * g
                xd = xdtsA4[g]
                P.add("dve", lambda e, xg=xg, o=o, xd=xd: e.tensor_tensor(out=v3(xd.ap), in0=xg, in1=bc(smG.ap[:, 6, o:o + 8], 64), op=ALU.mult), reads=[xt.b, smG.b], writes=[xd.b])
                P.add("pe", lambda e, g=g, xt=xt, xd=xd: e.matmul(SBK[g].ap, xt.ap[:, 2048 + g * 128:2048 + (g + 1) * 128], xd.ap, start=True, stop=True), reads=[xt.b, xd.b], writes=[SBK[g].b])
            P.add("dve", lambda e, q=q: e.tensor_tensor(out=H.ap.rearrange("p (a b) -> p a b", a=32), in0=H.ap.rearrange("p (a b) -> p a b", a=32), in1=bc(smG.ap[:, 7, q * 32:(q + 1) * 32], 64), op=ALU.mult), reads=[H.b, smG.b], writes=[H.b])
            for g in range(4):
                Hg = H.ap[:, g * 512:(g + 1) * 512]
                P.add("dve", lambda e, Hg=Hg, g=g: e.tensor_tensor(out=Hg, in0=Hg, in1=SBK[g].ap, op=ALU.add), reads=[H.b, SBK[g].b], writes=[H.b])

        tr(0); tr(1)
        hooks[0]()
        P.add("pe", lambda e: e.matmul(pf[4].ap[:, 128:256], triU, S(2), start=True, stop=True), reads=[cf.b, smG.b], writes=[pf[4].b])
        P.add("pe", lambda e: e.matmul(pf[4].ap[:, 256:384], onesf, S(2), start=True, stop=True), reads=[cf.b, smG.b], writes=[pf[4].b])
        P.add("dve", lambda e: e.tensor_copy(out=smG.ap[:, 3:5, :], in_=pf[4].ap[:, 128:384].rearrange("p (a b) -> p a b", a=2)), reads=[pf[4].b], writes=[smG.b])
        P.add("dve", lambda e: e.tensor_tensor(out=S(6), in0=S(4), in1=S(3), op=ALU.subtract), reads=[smG.b], writes=[smG.b])
        P.add("act", lambda e: e.activation(out=S(6), in_=S(6), func=AF.Exp), reads=[smG.b], writes=[smG.b])
        P.add("act", lambda e: e.activation(out=S(7), in_=S(4), func=AF.Exp), reads=[smG.b], writes=[smG.b])
        P.add("dve", lambda e: e.tensor_tensor(out=S(6), in0=S(6), in1=S(1), op=ALU.mult), reads=[smG.b], writes=[smG.b])
        state(0); state(1)
        hooks[1]()
        tr(2); tr(3)
        hooks[2]()
        state(2); state(3)
        hooks[3]()

    NG = NPRE // 4
    g0 = NG - (npre + 3) // 4
    if not SKIP1 and g0 < NG:
        load_group(g0, g0 % 2)
        proj_in(g0, g0 % 2, g0 == NG - 1, 0, 24)
        for gi in range(g0, NG):
            proj_conv(gi, gi == NG - 1)
            if gi + 1 < NG:
                load_group(gi + 1, (gi + 1) % 2)
                hk = [lambda a=a, gi=gi: proj_in(gi + 1, (gi + 1) % 2, gi + 1 == NG - 1, a, a + 6) for a in (0, 6, 12, 18)]
            else:
                hk = [lambda: None] * 4
            group_chunks(gi, gi % 2, hk)
        P.add("pool", lambda e: e.tensor_copy(out=rawh.ap[:, :, 0:3], in_=raw4.ap[:, :, 0:3]), reads=rawb, writes=[rawh.b])
    else:
        P.add("pool", lambda e: e.memset(rawh.ap, 0.0), writes=[rawh.b])

    P.barrier()
    AB.off, AFa.off = mark1B, mark1F
    Wz = AB.get("Wz", 8, 2048); load_w(Wz, w_in[:, COL_Z:COL_Z + 2048], 2048)
    nwb = AFa.get("nwb", 2048)
    dma("sp", nwb.ap, bcast_row(normw, 2048), [], [nwb.b])
    xb = AB.get("xb", 1024); xT = AB.get("xT", 8, 128)
    raw = AB.get("raw", 24, 132); xc = AB.get("xc", 24, 128)
    xtok = AB.get("xtok", 2560)
    xdt = AB.get("xdt", 512); xdts = AB.get("xdts", 512)
    Hb = AB.get("Hb", 2048); cbm = AB.get("cbm", 4, 128)
    LT = AB.get("LT", 4, 128); MT = AB.get("MT", 8, 128); yn = AB.get("yn", 2048)
    adtU = [AFa.get("adtU%d" % i, 128) for i in range(2)]
    sm = AFa.get("sm", 12, 32)
    yacc = AFa.get("yacc", 512); ytmp = AFa.get("ytmp", 512); sz = AFa.get("sz", 512)
    ssq = AFa.get("ssq", 4)
    P.add("pool", lambda e: e.memset(raw.ap, 0.0), writes=[raw.b])
    P.add("pool", lambda e: e.tensor_copy(out=raw.ap[:, :, 0:3], in_=rawh.ap[:, :, 0:3]), reads=[rawh.b], writes=[raw.b])

    CUT = int(os.environ.get('KCUT', '99')); NCH1 = int(os.environ.get('KNCH', str(NM1)))
    def ssd_chunk(xrow, fcol, main, ci):
        ntile = 24
        load_xT(xrow, xb, xT)
        if CUT <= 1: return
        for g in range(ntile // 4):
            bank = pf[g % 2]
            for jj in range(4):
                j = 4 * g + jj
                chain(bank, bank.ap[:, jj * 128:(jj + 1) * 128], [(Wx.ap[:, k, j * 128:(j + 1) * 128], xT.ap[:, k, :]) for k in range(8)], [Wx.b, xT.b])
            P.add("act", lambda e, g=g, bank=bank: e.copy(out=raw.ap[:, 4 * g:4 * g + 4, 3:131], in_=bank.ap.rearrange("p (a b) -> p a b", a=4)), reads=[bank.b], writes=[raw.b])
        if CUT <= 2: return
        for g in range(ntile // 4):
            bank = pf[2 + g % 2]
            for jj in range(4):
                j = 4 * g + jj
                chain(bank, bank.ap[:, jj * 128:(jj + 1) * 128], [(diag.ap[:, j, tp, :], raw.ap[:, j, tp:tp + 128]) for tp in range(4)], [diag.b, raw.b])
                P.add("act", lambda e, j=j, jj=jj, bank=bank: e.activation(out=xc.ap[:, j, :], in_=bank.ap[:, jj * 128:(jj + 1) * 128], func=AF.Silu, bias=cbt.ap[:, j:j + 1], scale=1.0), reads=[bank.b, cbt.b], writes=[xc.b])
        if CUT <= 3: return
        P.add("pool", lambda e: e.tensor_copy(out=raw.ap[:, :, 0:3], in_=raw.ap[:, :, 128:131]), reads=[raw.b], writes=[raw.b])
        for bt in range(3):
            nt = 8 if bt < 2 else 4
            for i in range(nt):
                j = bt * 8 + i
                P.add("pe", lambda e, i=i, j=j: e.transpose(pb[1].ap[:, i * 128:(i + 1) * 128], xc.ap[:, j, :], identb), reads=[xc.b, cb16.b], writes=[pb[1].b])
            P.add("dve", lambda e, bt=bt, nt=nt: e.tensor_copy(out=xtok.ap[:, bt * 1024:bt * 1024 + nt * 128], in_=pb[1].ap[:, 0:nt * 128]), reads=[pb[1].b], writes=[xtok.b])
        if CUT <= 4: return
        chain(pf[4], pf[4].ap[:, 0:32], [(xT.ap[:, k, :], Wdt.ap[:, k, :]) for k in range(8)], [xT.b, Wdt.b])
        S = lambda i: sm.ap[:, i, :]
        P.add("dve", lambda e: e.tensor_tensor(out=S(0), in0=pf[4].ap[:, 0:32], in1=dtb_bc, op=ALU.add), reads=[pf[4].b, smallp.b], writes=[sm.b])
        P.add("act", lambda e: e.activation(out=S(0), in_=S(0), func=AF.Exp), reads=[sm.b], writes=[sm.b])
        P.add("act", lambda e: e.activation(out=S(1), in_=S(0), func=AF.Ln, bias=onecol, scale=1.0), reads=[sm.b, cf.b], writes=[sm.b])
        P.add("dve", lambda e: e.tensor_scalar(out=S(1), in0=S(1), scalar1=fcol, scalar2=None, op0=ALU.mult), reads=[sm.b, flg.b], writes=[sm.b])
        P.add("dve", lambda e: e.tensor_tensor(out=S(2), in0=S(1), in1=a_bc, op=ALU.mult), reads=[sm.b, smallp.b], writes=[sm.b])
        if CUT <= 5: return
        P.add("pe", lambda e: e.matmul(pf[4].ap[:, 32:64], triU, S(2), start=True, stop=True), reads=[cf.b, sm.b], writes=[pf[4].b])
        P.add("pe", lambda e: e.matmul(pf[4].ap[:, 64:96], onesf, S(2), start=True, stop=True), reads=[cf.b, sm.b], writes=[pf[4].b])
        P.add("dve", lambda e: e.tensor_copy(out=sm.ap[:, 3:5, :], in_=pf[4].ap[:, 32:96].rearrange("p (a b) -> p a b", a=2)), reads=[pf[4].b], writes=[sm.b])
        if CUT <= 6: return
        if main and CUT > 7:
            for g in range(4):
                P.add("pe", lambda e, g=g: e.matmul(pf[5].ap[:, g * 128:(g + 1) * 128], xc.ap[:, 16 + g, :], xc.ap[:, 20 + g, :], start=True, stop=True), reads=[xc.b], writes=[pf[5].b])
            P.add("dve", lambda e: e.tensor_tensor(out=cbm.ap, in0=pf[5].ap.rearrange("p (a b) -> p a b", a=4), in1=maskb.unsqueeze(1).to_broadcast([128, 4, 128]), op=ALU.mult), reads=[pf[5].b, cb16.b], writes=[cbm.b])
            P.add("pool", lambda e: e.tensor_copy(out=Hb.ap, in_=H.ap), reads=[H.b], writes=[Hb.b])
            P.add("act", lambda e: e.activation(out=S(5), in_=S(3), func=AF.Exp), reads=[sm.b], writes=[sm.b])
            for g in range(4):
                xg = xtok.ap[:, g * 512:(g + 1) * 512].rearrange("p (a b) -> p a b", a=8)
                P.add("dve", lambda e, g=g, xg=xg: e.tensor_tensor(out=xdt.ap.rearrange("p (a b) -> p a b", a=8), in0=xg, in1=bc(sm.ap[:, 1, 8 * g:8 * g + 8], 64), op=ALU.mult), reads=[xtok.b, sm.b], writes=[xdt.b])
                for hh in range(2):
                    bank = pf[hh]
                    for h4 in range(4):
                        h = 8 * g + 4 * hh + h4
                        au = adtU[h4 % 2]
                        P.add("dve", lambda e, h=h, au=au: e.tensor_scalar(out=au.ap, in0=Ustr, scalar1=sm.ap[:, 2, h:h + 1], scalar2=None, op0=ALU.mult), reads=[cf.b, sm.b], writes=[au.b])
                        P.add("pe", lambda e, h4=h4, au=au, bank=bank: e.matmul(bank.ap[:, h4 * 128:(h4 + 1) * 128], au.ap, triU, start=True, stop=True), reads=[au.b, cf.b], writes=[bank.b])
                    P.add("act", lambda e, bank=bank: e.activation(out=LT.ap.rearrange("p a b -> p (a b)"), in_=bank.ap, func=AF.Exp), reads=[bank.b], writes=[LT.b])
                    P.add("dve", lambda e, g=g, hh=hh: e.tensor_tensor(out=MT.ap[:, 4 * hh:4 * hh + 4, :], in0=LT.ap, in1=cbm.ap[:, g, :].unsqueeze(1).to_broadcast([128, 4, 128]), op=ALU.mult), reads=[LT.b, cbm.b], writes=[MT.b])
                for h8 in range(8):
                    P.add("pe", lambda e, h8=h8: e.matmul(pf[2].ap[:, h8 * 64:(h8 + 1) * 64], MT.ap[:, h8, :], xdt.ap[:, h8 * 64:(h8 + 1) * 64], start=True, stop=True), reads=[MT.b, xdt.b], writes=[pf[2].b])
                P.add("pe", lambda e, g=g: e.matmul(pf[3].ap, xc.ap[:, 20 + g, :], Hb.ap[:, g * 512:(g + 1) * 512], start=True, stop=True), reads=[xc.b, Hb.b], writes=[pf[3].b])
                v3 = lambda ap: ap.rearrange("p (a b) -> p a b", a=8)
                P.add("dve", lambda e, g=g: e.tensor_tensor(out=v3(yacc.ap), in0=v3(pf[3].ap), in1=bc(sm.ap[:, 5, 8 * g:8 * g + 8], 64), op=ALU.mult), reads=[pf[3].b, sm.b], writes=[yacc.b])
                P.add("dve", lambda e: e.tensor_tensor(out=yacc.ap, in0=yacc.ap, in1=pf[2].ap, op=ALU.add), reads=[pf[2].b, yacc.b], writes=[yacc.b])
                P.add("dve", lambda e, g=g, xg=xg: e.tensor_tensor(out=v3(ytmp.ap), in0=xg, in1=bc(D_bc[:, 8 * g:8 * g + 8], 64), op=ALU.mult), reads=[xtok.b, smallp.b], writes=[ytmp.b])
                P.add("dve", lambda e: e.tensor_tensor(out=yacc.ap, in0=yacc.ap, in1=ytmp.ap, op=ALU.add), reads=[ytmp.b, yacc.b], writes=[yacc.b])
                chain(pf[5], pf[5].ap, [(xT.ap[:, k, :], Wz.ap[:, k, g * 512:(g + 1) * 512]) for k in range(8)], [xT.b, Wz.b])
                P.add("act", lambda e: e.activation(out=sz.ap, in_=pf[5].ap, func=AF.Silu), reads=[pf[5].b], writes=[sz.b])
                P.add("dve", lambda e: e.tensor_tensor(out=yacc.ap, in0=yacc.ap, in1=sz.ap, op=ALU.mult), reads=[sz.b, yacc.b], writes=[yacc.b])
                P.add("act", lambda e, g=g: e.activation(out=ytmp.ap, in_=yacc.ap, func=AF.Square, accum_out=ssq.ap[:, g:g + 1]), reads=[yacc.b], writes=[ytmp.b, ssq.b])
                P.add("dve", lambda e, g=g: e.tensor_scalar(out=ssq.ap[:, g:g + 1], in0=ssq.ap[:, g:g + 1], scalar1=1.0 / 512, scalar2=1e-5, op0=ALU.mult, op1=ALU.add), reads=[ssq.b], writes=[ssq.b])
                P.add("act", lambda e, g=g: e.activation(out=ssq.ap[:, g:g + 1], in_=ssq.ap[:, g:g + 1], func=AF.Sqrt), reads=[ssq.b], writes=[ssq.b])
                P.add("dve", lambda e, g=g: e.reciprocal(out=ssq.ap[:, g:g + 1], in_=ssq.ap[:, g:g + 1]), reads=[ssq.b], writes=[ssq.b])
                P.add("dve", lambda e, g=g: e.scalar_tensor_tensor(out=yn.ap[:, g * 512:(g + 1) * 512], in0=yacc.ap, scalar=ssq.ap[:, g:g + 1], in1=nwb.ap[:, g * 512:(g + 1) * 512], op0=ALU.mult, op1=ALU.mult), reads=[yacc.b, ssq.b, nwb.b], writes=[yn.b])
            dma("sp", ynd[ci * 128:(ci + 1) * 128, :], yn.ap, [yn.b], [])
        if CUT <= 8: return
        P.add("dve", lambda e: e.tensor_tensor(out=S(6), in0=S(4), in1=S(3), op=ALU.subtract), reads=[sm.b], writes=[sm.b])
        P.add("act", lambda e: e.activation(out=S(6), in_=S(6), func=AF.Exp), reads=[sm.b], writes=[sm.b])
        P.add("act", lambda e: e.activation(out=S(7), in_=S(4), func=AF.Exp), reads=[sm.b], writes=[sm.b])
        P.add("dve", lambda e: e.tensor_tensor(out=S(6), in0=S(6), in1=S(1), op=ALU.mult), reads=[sm.b], writes=[sm.b])
        for g in range(4):
            xg = xtok.ap[:, g * 512:(g + 1) * 512].rearrange("p (a b) -> p a b", a=8)
            v3 = lambda ap: ap.rearrange("p (a b) -> p a b", a=8)
            P.add("dve", lambda e, g=g, xg=xg: e.tensor_tensor(out=v3(xdts.ap), in0=xg, in1=bc(sm.ap[:, 6, 8 * g:8 * g + 8], 64), op=ALU.mult), reads=[xtok.b, sm.b], writes=[xdts.b])
            bank = pf[2 + g % 2]
            P.add("pe", lambda e, g=g, bank=bank: e.matmul(bank.ap, xtok.ap[:, 2048 + g * 128:2048 + (g + 1) * 128], xdts.ap, start=True, stop=True), reads=[xtok.b, xdts.b], writes=[bank.b])
            Hg = H.ap[:, g * 512:(g + 1) * 512]
            P.add("dve", lambda e, g=g, Hg=Hg: e.tensor_tensor(out=v3(Hg), in0=v3(Hg), in1=bc(sm.ap[:, 7, 8 * g:8 * g + 8], 64), op=ALU.mult), reads=[H.b, sm.b], writes=[H.b])
            P.add("dve", lambda e, Hg=Hg, bank=bank: e.tensor_tensor(out=Hg, in0=Hg, in1=bank.ap, op=ALU.add), reads=[H.b, bank.b], writes=[H.b])

    for ci in range(0 if SKIP1 else NCH1):
        ssd_chunk(xmain[(ci + 1) * 128:(ci + 2) * 128, :], flg.ap[:, NPRE + ci:NPRE + ci + 1], True, ci)

    if stop < 2:
        P.emit(nc); es.close(); return nc
    P.barrier()
    AB.off, AFa.off = markB, markF
    Wq = AB.get("Wq", 8, 1024); load_w(Wq, w_in[:, COL_Q:COL_Q + 1024], 1024)
    Wk2 = AB.get("Wk2", 8, 128); load_w(Wk2, w_in[:, COL_K:COL_K + 128], 128)
    Wv = AB.get("Wv", 8, 128); load_w(Wv, w_in[:, COL_V:COL_V + 128], 128)
    EB = AB.get("EB", 2, 16, 128)
    ebf = AFa.get("ebf", 2048); mkf = AFa.get("mkf", 2048)
    for kt in range(2):
        dma("sp", ebf.ap, biasg[:, kt * 2048:(kt + 1) * 2048], [], [ebf.b])
        dma("sp", mkf.ap, maskg[:, kt * 2048:(kt + 1) * 2048], [], [mkf.b])
        P.add("act", lambda e: e.activation(out=ebf.ap, in_=ebf.ap, func=AF.Exp), reads=[ebf.b], writes=[ebf.b])
        P.add("dve", lambda e, kt=kt: e.tensor_tensor(out=EB.ap[:, kt, :, :].rearrange("p a b -> p (a b)"), in0=ebf.ap, in1=mkf.ap, op=ALU.mult), reads=[ebf.b, mkf.b], writes=[EB.b])
    xb = AB.get("xb2", 1024); xT = AB.get("xT2", 8, 128)
    kT = [AB.get("kT%d" % i, 2, 128) for i in range(2)]
    vx = [AB.get("vx%d" % i, 2, 65) for i in range(2)]
    for i in range(2):
        P.add("pool", lambda e, i=i: e.memset(vx[i].ap, 1.0), writes=[vx[i].b])
    qT = AB.get("qT", 16, 128); et = AB.get("et", 4, 128)
    PT = [AB.get("PT%d" % i, 4, 128) for i in range(2)]
    ya = AB.get("ya", 1024)
    den = AFa.get("den", 4)
    flag0 = flg.ap[:, NPRE:NPRE + 1]
    CUT2 = int(os.environ.get('KCUT2', '99')); NCH2 = int(os.environ.get('KNCH2', str(NM2)))
    for ci in range(NCH2):
        sl = ci % 2
        load_xT(xmain[ci * 128:(ci + 1) * 128, :], xb, xT)
        for kv in range(2):
            chain(pf[0], pf[0].ap[0:64, kv * 128:(kv + 1) * 128], [(Wk2.ap[:, k, kv * 64:(kv + 1) * 64], xT.ap[:, k, :]) for k in range(8)], [Wk2.b, xT.b])
        P.add("act", lambda e, sl=sl: e.copy(out=kT[sl].ap[0:64, :, :], in_=pf[0].ap[0:64, 0:256].rearrange("p (a b) -> p a b", a=2)), reads=[pf[0].b], writes=[kT[sl].b])
        chain(pf[1], pf[1].ap[:, 0:128], [(xT.ap[:, k, :], Wv.ap[:, k, :]) for k in range(8)], [xT.b, Wv.b])
        P.add("dve", lambda e, sl=sl: e.tensor_copy(out=vx[sl].ap[:, :, 0:64], in_=pf[1].ap[:, 0:128].rearrange("p (a b) -> p a b", a=2)), reads=[pf[1].b], writes=[vx[sl].b])
        if ci == 0 or CUT2 <= 1:
            continue
        for q4 in range(4):
            bank = pf[2 + q4 % 2]
            for tt in range(4):
                j = q4 * 4 + tt
                chain(bank, bank.ap[0:64, tt * 128:(tt + 1) * 128], [(Wq.ap[:, k, j * 64:(j + 1) * 64], xT.ap[:, k, :]) for k in range(8)], [Wq.b, xT.b])
            P.add("act", lambda e, q4=q4, bank=bank: e.copy(out=qT.ap[0:64, q4 * 4:q4 * 4 + 4, :], in_=bank.ap[0:64, :].rearrange("p (a b) -> p a b", a=4)), reads=[bank.b], writes=[qT.b])
        for kvh in range(2):
            if CUT2 <= 2: break
            for hb in range(2):
                j0 = kvh * 8 + hb * 4
                for kt in range(2):
                    slk = (ci + 1 + kt) % 2
                    bank = pf[4 + kt]
                    for i in range(4):
                        j = j0 + i
                        base = (j % 2) * 64 * int(os.environ.get("KB64", "1"))
                        P.add("pe", lambda e, i=i, j=j, base=base, slk=slk, bank=bank, kvh=kvh: e.matmul(bank.ap[:, i * 128:(i + 1) * 128], kT[slk].ap[0:64, kvh, :], qT.ap[0:64, j, :], start=True, stop=True), reads=[kT[slk].b, qT.b], writes=[bank.b])
                    P.add("act", lambda e, bank=bank: e.activation(out=et.ap.rearrange("p a b -> p (a b)"), in_=bank.ap, func=AF.Exp, scale=0.125), reads=[bank.b], writes=[et.b])
                    if ci == 2 and kt == 0:
                        P.add("dve", lambda e, kt=kt, j0=j0: e.scalar_tensor_tensor(out=PT[kt].ap, in0=et.ap, scalar=flag0, in1=EB.ap[:, kt, j0:j0 + 4, :], op0=ALU.mult, op1=ALU.mult), reads=[et.b, EB.b, flg.b], writes=[PT[kt].b])
                    else:
                        P.add("dve", lambda e, kt=kt, j0=j0: e.tensor_tensor(out=PT[kt].ap, in0=et.ap, in1=EB.ap[:, kt, j0:j0 + 4, :], op=ALU.mult), reads=[et.b, EB.b], writes=[PT[kt].b])
                if CUT2 <= 3: continue
                bank = pf[hb]
                for i in range(4):
                    for kt in range(2):
                        slk = (ci + 1 + kt) % 2
                        P.add("pe", lambda e, i=i, kt=kt, slk=slk, bank=bank, kvh=kvh: e.matmul(bank.ap[:, i * 65:(i + 1) * 65], PT[kt].ap[:, i, :], vx[slk].ap[:, kvh, :], start=(kt == 0), stop=(kt == 1)), reads=[PT[kt].b, vx[slk].b], writes=[bank.b])
                if CUT2 <= 4: continue
                pv = bank.ap[:, 0:260].rearrange("p (a b) -> p a b", a=4)
                P.add("dve", lambda e, pv=pv, j0=j0: e.tensor_tensor(out=den.ap, in0=pv[:, :, 64], in1=esink[:, j0:j0 + 4], op=ALU.add), reads=[bank.b, smallp.b], writes=[den.b])
                P.add("dve", lambda e: e.reciprocal(out=den.ap, in_=den.ap), reads=[den.b], writes=[den.b])
                P.add("dve", lambda e, pv=pv, j0=j0: e.tensor_tensor(out=ya.ap[:, j0 * 64:(j0 + 4) * 64].rearrange("p (a b) -> p a b", a=4), in0=pv[:, :, 0:64], in1=bc(den.ap, 64), op=ALU.mult), reads=[bank.b, den.b], writes=[ya.b])
        dma("sp", yad[(ci - 1) * 128:ci * 128, :], ya.ap, [ya.b], [])

    if stop < 3:
        P.emit(nc); es.close(); return nc
    P.barrier()
    AB.off, AFa.off = markB, markF
    Wg = AB.get("Wg", 8, 2048); load_w(Wg, w_in[:, COL_G:COL_G + 2048], 2048)
    Wbs = AB.get("Wbs", 16, 1024); load_w(Wbs, w_bs, 1024)
    Wba = AB.get("Wba", 8, 1024); load_w(Wba, w_ba, 1024)
    Wmx = AB.get("Wmx", 8, 1024); load_w(Wmx, w_mix, 1024)
    bgb = AFa.get("bgb", 2048); dma("sp", bgb.ap, bcast_row(b_gate, 2048), [], [bgb.b])
    lng = AFa.get("lng", 2, 1024)
    dma("sp", lng.ap[:, 0, :], bcast_row(ln1g, 1024), [], [lng.b]); dma("sp", lng.ap[:, 1, :], bcast_row(ln1b, 1024), [], [lng.b])
    xb = AB.get("xb3", 1024); xT = AB.get("xT3", 8, 128)
    ynb = AB.get("ynb", 2048); yab = AB.get("yab", 1024)
    ynT = AB.get("ynT", 16, 128); yaT = AB.get("yaT", 8, 128)
    mg = AB.get("mg", 1024); mT = AB.get("mT", 8, 128)
    xf = AFa.get("xf", 1024); gt = AFa.get("gt", 2048); m1 = AFa.get("m1", 512); r = AFa.get("r", 1024)
    st = AFa.get("st", 2, 6); mv = AFa.get("mv", 2)

    def transp(src, dst, ntl):
        for bt in range(ntl // 8):
            for i in range(8):
                j = bt * 8 + i
                P.add("pe", lambda e, i=i, j=j: e.transpose(pb[1].ap[:, i * 128:(i + 1) * 128], src.ap[:, j * 128:(j + 1) * 128], identb), reads=[src.b, cb16.b], writes=[pb[1].b])
            P.add("act", lambda e, bt=bt: e.copy(out=dst.ap[:, bt * 8:bt * 8 + 8, :].rearrange("p a b -> p (a b)"), in_=pb[1].ap), reads=[pb[1].b], writes=[dst.b])

    def layer_norm(r, g_ap, b_ap, gb, st, mv):
        for i in range(2):
            P.add("dve", lambda e, i=i: e.bn_stats(out=st.ap[:, i, :], in_=r.ap[:, i * 512:(i + 1) * 512]), reads=[r.b], writes=[st.b])
        P.add("dve", lambda e: e.bn_aggr(out=mv.ap, in_=st.ap.rearrange("p a b -> p (a b)")), reads=[st.b], writes=[mv.b])
        P.add("dve", lambda e: e.tensor_scalar(out=mv.ap[:, 1:2], in0=mv.ap[:, 1:2], scalar1=1e-5, scalar2=None, op0=ALU.add), reads=[mv.b], writes=[mv.b])
        P.add("act", lambda e: e.activation(out=mv.ap[:, 1:2], in_=mv.ap[:, 1:2], func=AF.Sqrt), reads=[mv.b], writes=[mv.b])
        P.add("dve", lambda e: e.reciprocal(out=mv.ap[:, 1:2], in_=mv.ap[:, 1:2]), reads=[mv.b], writes=[mv.b])
        P.add("dve", lambda e: e.tensor_scalar(out=r.ap, in0=r.ap, scalar1=mv.ap[:, 0:1], scalar2=mv.ap[:, 1:2], op0=ALU.subtract, op1=ALU.mult), reads=[r.b, mv.b], writes=[r.b])
        P.add("dve", lambda e: e.tensor_tensor(out=r.ap, in0=r.ap, in1=g_ap, op=ALU.mult), reads=[r.b, gb], writes=[r.b])
        P.add("dve", lambda e: e.tensor_tensor(out=r.ap, in0=r.ap, in1=b_ap, op=ALU.add), reads=[r.b, gb], writes=[r.b])

    for ci in range(NM1):
        xrow = xmain[(ci + 1) * 128:(ci + 2) * 128, :]
        load_xT(xrow, xb, xT)
        dma("sp", xf.ap, xrow, [], [xf.b])
        dma("sp", ynb.ap, ynd[ci * 128:(ci + 1) * 128, :], [], [ynb.b])
        dma("sp", yab.ap, yad[ci * 128:(ci + 1) * 128, :], [], [yab.b])
        transp(ynb, ynT, 16)
        transp(yab, yaT, 8)
        for s4 in range(4):
            bank = pf[s4 % 2]
            chain(bank, bank.ap, [(xT.ap[:, k, :], Wg.ap[:, k, s4 * 512:(s4 + 1) * 512]) for k in range(8)], [xT.b, Wg.b])
            P.add("dve", lambda e, s4=s4, bank=bank: e.tensor_tensor(out=gt.ap[:, s4 * 512:(s4 + 1) * 512], in0=bank.ap, in1=bgb.ap[:, s4 * 512:(s4 + 1) * 512], op=ALU.add), reads=[bank.b, bgb.b], writes=[gt.b])
        P.add("act", lambda e: e.activation(out=gt.ap, in_=gt.ap, func=AF.Sigmoid), reads=[gt.b], writes=[gt.b])
        for hf in range(2):
            chain(pf[2], pf[2].ap, [(ynT.ap[:, i, :], Wbs.ap[:, i, hf * 512:(hf + 1) * 512]) for i in range(16)], [ynT.b, Wbs.b])
            chain(pf[3], pf[3].ap, [(yaT.ap[:, i, :], Wba.ap[:, i, hf * 512:(hf + 1) * 512]) for i in range(8)], [yaT.b, Wba.b])
            P.add("dve", lambda e, hf=hf: e.tensor_tensor(out=m1.ap, in0=pf[2].ap, in1=gt.ap[:, hf * 512:(hf + 1) * 512], op=ALU.mult), reads=[pf[2].b, gt.b], writes=[m1.b])
            P.add("dve", lambda e, hf=hf: e.tensor_tensor(out=r.ap[:, hf * 512:(hf + 1) * 512], in0=pf[3].ap, in1=gt.ap[:, 1024 + hf * 512:1024 + (hf + 1) * 512], op=ALU.mult), reads=[pf[3].b, gt.b], writes=[r.b])
            P.add("dve", lambda e, hf=hf: e.tensor_tensor(out=mg.ap[:, hf * 512:(hf + 1) * 512], in0=m1.ap, in1=r.ap[:, hf * 512:(hf + 1) * 512], op=ALU.add), reads=[m1.b, r.b], writes=[mg.b])
        transp(mg, mT, 8)
        for hf in range(2):
            bank = pf[4 + hf]
            chain(bank, bank.ap, [(mT.ap[:, i, :], Wmx.ap[:, i, hf * 512:(hf + 1) * 512]) for i in range(8)], [mT.b, Wmx.b])
            P.add("dve", lambda e, hf=hf, bank=bank: e.scalar_tensor_tensor(out=r.ap[:, hf * 512:(hf + 1) * 512], in0=xf.ap[:, hf * 512:(hf + 1) * 512], scalar=ALPHA, in1=bank.ap, op0=ALU.mult, op1=ALU.add), reads=[xf.b, bank.b], writes=[r.b])
        layer_norm(r, lng.ap[:, 0, :], lng.ap[:, 1, :], lng.b, st, mv)
        if ci == 0:
            P.add("dve", lambda e: e.tensor_scalar(out=r.ap, in0=r.ap, scalar1=flag0, scalar2=None, op0=ALU.mult), reads=[r.b, flg.b], writes=[r.b])
        dma("sp", h1d[ci * 128:(ci + 1) * 128, :], r.ap, [r.b], [])

    if stop < 4:
        P.emit(nc); es.close(); return nc
    P.barrier()
    AB.off, AFa.off = markB, markF
    Wup = AB.get("Wup", 8, 5632); load_w(Wup, w_up, 5632)
    Wdn = AB.get("Wdn", 22, 1024); load_w(Wdn, w_dn, 1024)
    fw = AFa.get("fw", 132); fb = AFa.get("fb", 44)
    per_channel(fw.ap[:, 0:88], fcw[0:88, :], 88); fw.b.writer = P.ops["dve"][-1]
    per_channel(fw.ap[:, 88:132], fcw[88:132, :], 44); fw.b.writer = P.ops["dve"][-1]
    per_channel(fb.ap, fcb, 44); fb.b.writer = P.ops["dve"][-1]
    lng2 = AFa.get("lng2", 2, 1024)
    dma("sp", lng2.ap[:, 0, :], bcast_row(ln2g, 1024), [], [lng2.b]); dma("sp", lng2.ap[:, 1, :], bcast_row(ln2b, 1024), [], [lng2.b])
    SC = 256
    hA = AB.get("hA", 1024); hB = AB.get("hB", 1024); h1T = AB.get("h1T", 8, SC + 2)
    aT = AB.get("aT", 22, SC)
    cvs = [AFa.get("cvs%d" % i, SC) for i in range(4)]
    t0s = [AFa.get("t0s%d" % i, SC) for i in range(2)]
    sg = AFa.get("sg", SC)
    hr = AFa.get("hr", 1024); r4 = AFa.get("r4", 1024)
    st4 = AFa.get("st4", 2, 6); mv4 = AFa.get("mv4", 2)
    NT = SC // 128
    tcount = 0
    for sc in range(TOK // SC):
        r0 = 128 + sc * SC
        for i in range(NT):
            dma("pool", hA.ap, h1d[r0 - 2 + i * 128:r0 + 126 + i * 128, :], [], [hA.b])
            for k in range(8):
                P.add("pe", lambda e, k=k: e.transpose(pb[0].ap[:, k * 128:(k + 1) * 128], hA.ap[:, k * 128:(k + 1) * 128], identb), reads=[hA.b, cb16.b], writes=[pb[0].b])
            P.add("act", lambda e, i=i: e.copy(out=h1T.ap[:, :, i * 128:(i + 1) * 128], in_=pb[0].ap.rearrange("p (a b) -> p a b", a=8)), reads=[pb[0].b], writes=[h1T.b])
        dma("pool", hB.ap[0:2, :], h1d[r0 + SC - 2:r0 + SC, :], [], [hB.b])
        for k in range(8):
            P.add("pe", lambda e, k=k: e.transpose(pb[1].ap[:, k * 2:(k + 1) * 2], hB.ap[0:2, k * 128:(k + 1) * 128], identb[0:2, 0:2]), reads=[hB.b, cb16.b], writes=[pb[1].b])
        P.add("act", lambda e: e.copy(out=h1T.ap[:, :, SC:SC + 2], in_=pb[1].ap[:, 0:16].rearrange("p (a b) -> p a b", a=8)), reads=[pb[1].b], writes=[h1T.b])
        for jp in range(22):
            for gv in range(2):
                j = jp + 22 * gv
                X = pf[tcount % 4]; t0 = t0s[tcount % 2]
                cv = cvs[(jp % 2) * 2 + gv]
                tcount += 1
                chain(X, X.ap[:, 0:SC + 2], [(Wup.ap[:, k, j * 128:(j + 1) * 128], h1T.ap[:, k, 0:SC + 2]) for k in range(8)], [Wup.b, h1T.b])
                w0, w1, w2, bb = fw.ap[:, j:j + 1], fw.ap[:, 44 + j:45 + j], fw.ap[:, 88 + j:89 + j], fb.ap[:, j:j + 1]
                P.add("act", lambda e, X=X, t0=t0, w2=w2, bb=bb: e.activation(out=t0.ap, in_=X.ap[:, 2:SC + 2], func=AF.Identity, bias=bb, scale=w2), reads=[X.b, fw.b, fb.b], writes=[t0.b])
                P.add("dve", lambda e, X=X, t0=t0, w1=w1: e.scalar_tensor_tensor(out=t0.ap, in0=X.ap[:, 1:SC + 1], scalar=w1, in1=t0.ap, op0=ALU.mult, op1=ALU.add), reads=[X.b, fw.b, t0.b], writes=[t0.b])
                P.add("dve", lambda e, X=X, t0=t0, w0=w0, cv=cv: e.scalar_tensor_tensor(out=cv.ap, in0=X.ap[:, 0:SC], scalar=w0, in1=t0.ap, op0=ALU.mult, op1=ALU.add), reads=[X.b, fw.b, t0.b], writes=[cv.b])
            cg, cvv = cvs[(jp % 2) * 2], cvs[(jp % 2) * 2 + 1]
            P.add("act", lambda e, cg=cg: e.activation(out=sg.ap, in_=cg.ap, func=AF.Silu), reads=[cg.b], writes=[sg.b])
            P.add("dve", lambda e, jp=jp, cvv=cvv: e.tensor_tensor(out=aT.ap[:, jp, :], in0=sg.ap, in1=cvv.ap, op=ALU.mult), reads=[sg.b, cvv.b], writes=[aT.b])
        for tt in range(NT):
            rr = r0 + tt * 128
            dma("sp", hr.ap, h1d[rr:rr + 128, :], [], [hr.b])
            for hf in range(2):
                bank = pf[4 + hf]
                chain(bank, bank.ap, [(aT.ap[:, jj, tt * 128:(tt + 1) * 128], Wdn.ap[:, jj, hf * 512:(hf + 1) * 512]) for jj in range(22)], [aT.b, Wdn.b])
                P.add("dve", lambda e, hf=hf, bank=bank: e.scalar_tensor_tensor(out=r4.ap[:, hf * 512:(hf + 1) * 512], in0=hr.ap[:, hf * 512:(hf + 1) * 512], scalar=ALPHA, in1=bank.ap, op0=ALU.mult, op1=ALU.add), reads=[hr.b, bank.b], writes=[r4.b])
            layer_norm(r4, lng2.ap[:, 0, :], lng2.ap[:, 1, :], lng2.b, st4, mv4)
            dma("sp", out[rr - 128:rr, :], r4.ap, [r4.b], [])

    P.emit(nc)
    es.close()
    return nc


def rel_bucket_np(rel):
    n = np.maximum(rel, 0)
    nf = np.maximum(n, 1).astype(np.float32)
    large = 16 + (np.log(nf / np.float32(16)) / np.float32(np.log(128 / 16)) * np.float32(16)).astype(np.int32)
    large = np.minimum(large, 31)
    return np.where(n < 16, n, large)


_NC = None


def kernel(_dbg=None, **inp):
    global _NC
    x = np.asarray(inp["x"], np.float32)[0]
    f = lambda k: np.ascontiguousarray(np.asarray(inp[k], np.float32)[0])
    common = {
        "w_in": f("w_in"), "b_gate": f("b_gate")[None], "dtb": f("ssm_dt_bias")[None], "alog": f("ssm_a_log")[None],
        "dsk": f("ssm_d")[None], "normw": f("ssm_norm_w")[None], "sinks": f("attn_sinks")[None],
        "w_bs": f("w_branch_ssm"), "w_ba": f("w_branch_attn"), "w_mix": f("w_mix_out"),
        "ln1g": f("ln1_g")[None], "ln1b": f("ln1_b")[None], "ln2g": f("ln2_g")[None], "ln2b": f("ln2_b")[None],
        "w_up": f("w_up"), "w_dn": f("w_down"),
    }
    scw = f("ssm_conv_w")
    common["scw"] = np.ascontiguousarray(scw.reshape(4 * 24, 128))
    common["scb"] = np.ascontiguousarray(f("ssm_conv_b").reshape(24, 128))
    common["fcw"] = np.ascontiguousarray(f("ffn_conv_w").reshape(3 * 44, 128))
    common["fcb"] = np.ascontiguousarray(f("ffn_conv_b").reshape(44, 128))
    s = np.arange(128)
    ident = np.eye(128, dtype=np.float32)
    triU = (s[:, None] <= s[None, :]).astype(np.float32)
    ustr = (s[:, None] > s[None, :]).astype(np.float32)
    common["cst"] = np.ascontiguousarray(np.concatenate([ident, triU, ustr, np.ones((128, 128), np.float32)], axis=1))
    rb = np.asarray(inp["rel_bias"], np.float32)
    bg = np.zeros((128, 2, 16, 128), np.float32); mk = np.zeros((128, 2, 16, 128), np.float32)
    for kt in range(2):
        rel = (s[None, :] + 128) - (s[:, None] + 128 * kt)
        valid = (rel >= 0) & (rel < 128)
        bidx = rel_bucket_np(rel)
        g = rb[bidx]
        bg[:, kt] = np.transpose(g, (0, 2, 1))
        mk[:, kt] = np.broadcast_to(valid[:, None, :], (128, 16, 128))
    common["biasg"] = np.ascontiguousarray(bg.reshape(128, -1)); common["maskg"] = np.ascontiguousarray(mk.reshape(128, -1))
    in_maps = []
    for c in range(NCORE):
        S = c * TOK
        lo = S - 128 - NPRE * 128
        xp = np.zeros((NPRE * 128, 1024), np.float32)
        if S - 128 > 0:
            src_lo = max(lo, 0)
            xp[src_lo - lo:] = x[src_lo:S - 128]
        xm = np.zeros((NM2 * 128, 1024), np.float32)
        lo2 = S - 256
        src_lo = max(lo2, 0)
        xm[src_lo - lo2:] = x[src_lo:S + TOK]
        fl = np.zeros((128, NPRE + NM1), np.float32)
        for i in range(NPRE):
            fl[:, i] = 1.0 if lo + i * 128 >= 0 else 0.0
        fl[:, NPRE] = 1.0 if c > 0 else 0.0
        fl[:, NPRE + 1:] = 1.0
        m = dict(common); m["xpre"] = xp; m["xmain"] = xm; m["pflag"] = fl
        in_maps.append(m)
    if _dbg is not None:
        return in_maps
    if _NC is None:
        _NC = build()
    res = run_bass_kernel_spmd(_NC, in_maps, core_ids=list(range(NCORE)))
    o = np.concatenate([res.results[c]["out"] for c in range(NCORE)], axis=0)
    return o[None].astype(np.float32)
```

```python
import numpy as np
import concourse.bass as bass
import concourse.mybir as mybir

ENGS = ["pe", "act", "dve", "pool", "sp"]
N_DMA_SEM = 16
import os as _os
SAME_ENGINE_SYNC = _os.environ.get("KSES", "1") == "1"


class Buf:
    __slots__ = ("name", "writer", "readers", "dma_readers", "excl")

    def __init__(self, name, excl=False):
        self.name = name
        self.excl = excl
        self.writer = None
        self.readers = {}
        self.dma_readers = []


class Op:
    __slots__ = ("eng", "fn", "idx", "waits", "signal", "is_dma", "dsem", "dtarget", "clock", "sigcount", "uid")


class Prog:
    def __init__(self):
        self.ops = {e: [] for e in ENGS}
        self.clock = {e: {f: 0 for f in ENGS} for e in ENGS}
        self.known_dma = {e: set() for e in ENGS}
        self.dma_sem_count = [0] * N_DMA_SEM
        self.dma_sem_last = [None] * N_DMA_SEM
        self.dma_rr = 0
        self.n_dma = 0
        self.uid = 0
        self.bar = {e: [] for e in ENGS}

    def barrier(self):
        lasts = [self.ops[e][-1] for e in ENGS if self.ops[e] and not self.ops[e][-1].is_dma]
        for e in ENGS:
            pass
        lasts = []
        for e in ENGS:
            for op in reversed(self.ops[e]):
                if not op.is_dma:
                    lasts.append(op)
                    break
        dmas = [op for op in self.dma_sem_last if op is not None]
        for e in ENGS:
            self.bar[e] = lasts + dmas

    def add(self, eng, fn, reads=(), writes=(), dma=False):
        op = Op()
        op.eng = eng
        op.fn = fn
        op.idx = len(self.ops[eng])
        op.waits = []
        op.signal = False
        op.is_dma = dma
        op.uid = self.uid
        self.uid += 1
        deps = []
        for b in reads:
            if b.writer is not None:
                deps.append(b.writer)
            if b.excl:
                for e2, r in b.readers.items():
                    if e2 != eng:
                        deps.append(r)
        for b in writes:
            if b.writer is not None:
                deps.append(b.writer)
            deps.extend(b.readers.values())
            deps.extend(b.dma_readers)
        if self.bar[eng]:
            deps.extend(self.bar[eng])
            self.bar[eng] = []
        clk = self.clock[eng]
        seen = set()
        for d in deps:
            if d.uid in seen:
                continue
            seen.add(d.uid)
            if d.is_dma:
                if d.uid not in self.known_dma[eng]:
                    op.waits.append(("dma", d.dsem, d.dtarget))
                    self.known_dma[eng].add(d.uid)
            else:
                if d.eng == eng and (eng == "pe" or not SAME_ENGINE_SYNC):
                    continue
                if clk[d.eng] < d.idx + 1:
                    op.waits.append(("eng", d))
                    d.signal = True
                    for f in ENGS:
                        if d.clock[f] > clk[f]:
                            clk[f] = d.clock[f]
                    if clk[d.eng] < d.idx + 1:
                        clk[d.eng] = d.idx + 1
        if dma and eng == "pool":
            pd = self.__dict__.setdefault("pool_dmas", [])
            if len(pd) >= 4:
                d = pd[-4]
                if d.uid not in self.known_dma[eng]:
                    op.waits.append(("dma", d.dsem, d.dtarget))
                    self.known_dma[eng].add(d.uid)
            pd.append(op)
        if dma:
            k = self.dma_rr
            self.dma_rr = (self.dma_rr + 1) % N_DMA_SEM
            prev = self.dma_sem_last[k]
            if prev is not None and prev.uid not in self.known_dma[eng]:
                op.waits.append(("dma", k, prev.dtarget))
                self.known_dma[eng].add(prev.uid)
            self.dma_sem_count[k] += 16
            op.dsem = k
            op.dtarget = self.dma_sem_count[k]
            self.dma_sem_last[k] = op
            self.n_dma += 1
        op.clock = dict(clk)
        if not SAME_ENGINE_SYNC or eng == "pe":
            pass
        self.ops[eng].append(op)
        for b in writes:
            b.writer = op
            b.readers = {}
            b.dma_readers = []
        for b in reads:
            if dma:
                b.dma_readers.append(op)
            else:
                b.readers[eng] = op
        return op

    def emit(self, nc, final_waits=True):
        for e in ENGS:
            c = 0
            for op in self.ops[e]:
                if op.signal:
                    c += 1
                op.sigcount = c
        from contextlib import ExitStack
        with ExitStack() as es:
            esem = {e: es.enter_context(nc.semaphore("s_" + e)) for e in ENGS}
            dsem = [es.enter_context(nc.semaphore("d%d" % i)) for i in range(N_DMA_SEM)]
            block = es.enter_context(nc.Block())
            last_dma = [op for op in self.dma_sem_last if op is not None]

            def run(e, h):
                for op in self.ops[e]:
                    for w in op.waits:
                        if w[0] == "dma":
                            h.wait_ge(dsem[w[1]], w[2])
                        else:
                            h.wait_ge(esem[w[1].eng], w[1].sigcount)
                    ins = op.fn(h)
                    if op.is_dma:
                        ins.then_inc(dsem[op.dsem], 16)
                    elif op.signal:
                        ins.then_inc(esem[e], 1)
                if e == "sp" and final_waits:
                    for k in range(N_DMA_SEM):
                        if self.dma_sem_count[k] > 0:
                            h.wait_ge(dsem[k], self.dma_sem_count[k])

            @block.tensor
            def _(h):
                run("pe", h)

            @block.scalar
            def _(h):
                run("act", h)

            @block.vector
            def _(h):
                run("dve", h)

            @block.gpsimd
            def _(h):
                run("pool", h)

            @block.sync
            def _(h):
                run("sp", h)

from contextlib import ExitStack
import ml_dtypes
from concourse.bass_utils import run_bass_kernel_spmd

F32 = mybir.dt.float32
BF16 = mybir.dt.bfloat16
AF = mybir.ActivationFunctionType
ALU = mybir.AluOpType

NCORE = 8
TOK = 2048
NPRE = 112
NM1 = 17
NM2 = 18
ALPHA = 2.0 ** 0.25
COL_Z, COL_X, COL_B, COL_C, COL_DT, COL_Q, COL_K, COL_V, COL_G = 0, 2048, 4096, 4608, 5120, 5152, 6176, 6304, 6432


class TT:
    def __init__(self, ap, name, excl=False):
        self.ap = ap
        self.b = Buf(name, excl)


class Arena:
    def __init__(self, t, n):
        self.t, self.n, self.off = t, n, 0

    def get(self, name, *fs):
        n = int(np.prod(fs))
        ap = self.t[:, self.off:self.off + n]
        self.off += n
        assert self.off <= self.n, (name, self.off, self.n)
        if len(fs) == 2:
            ap = ap.rearrange("p (a b) -> p a b", a=fs[0])
        elif len(fs) == 3:
            ap = ap.rearrange("p (a b c) -> p a b c", a=fs[0], b=fs[1])
        return TT(ap, name)


def bc(ap2, n):
    return ap2.unsqueeze(2).to_broadcast([ap2.shape[0], ap2.shape[1], n])


def build(stop=4, npre=NPRE, dbg=False):
    nc = bass.Bass("TRN2", target_bir_lowering=False)
    dt_in = lambda n, s: nc.dram_tensor(n, s, F32, kind="ExternalInput").ap()
    xpre = dt_in("xpre", [NPRE * 128, 1024])
    xmain = dt_in("xmain", [NM2 * 128, 1024])
    pflag = dt_in("pflag", [128, NPRE + NM1])
    w_in = dt_in("w_in", [1024, 8480])
    b_gate = dt_in("b_gate", [1, 2048])
    scw = dt_in("scw", [96, 128])
    scb = dt_in("scb", [24, 128])
    dtb = dt_in("dtb", [1, 32])
    alog = dt_in("alog", [1, 32])
    dsk = dt_in("dsk", [1, 32])
    normw = dt_in("normw", [1, 2048])
    sinks = dt_in("sinks", [1, 16])
    w_bs = dt_in("w_bs", [2048, 1024])
    w_ba = dt_in("w_ba", [1024, 1024])
    w_mix = dt_in("w_mix", [1024, 1024])
    ln1g = dt_in("ln1g", [1, 1024]); ln1b = dt_in("ln1b", [1, 1024])
    ln2g = dt_in("ln2g", [1, 1024]); ln2b = dt_in("ln2b", [1, 1024])
    w_up = dt_in("w_up", [1024, 5632])
    fcw = dt_in("fcw", [132, 128])
    fcb = dt_in("fcb", [44, 128])
    w_dn = dt_in("w_dn", [2816, 1024])
    cst = dt_in("cst", [128, 4 * 128])
    biasg = dt_in("biasg", [128, 2 * 16 * 128])
    maskg = dt_in("maskg", [128, 2 * 16 * 128])
    out = nc.dram_tensor("out", [TOK, 1024], F32, kind="ExternalOutput").ap()
    skind = "ExternalOutput" if dbg else "Internal"
    ynd = nc.dram_tensor("ynd", [NM1 * 128, 2048], BF16, kind=skind).ap()
    yad = nc.dram_tensor("yad", [NM1 * 128, 1024], BF16, kind=skind).ap()
    h1d = nc.dram_tensor("h1d", [NM1 * 128, 1024], F32, kind=skind).ap()

    P = Prog()
    es = ExitStack()
    NB, NF = 164 * 512, 40 * 256
    ABt = es.enter_context(nc.sbuf_tensor("AB", [128, NB], BF16))
    AFt = es.enter_context(nc.sbuf_tensor("AF", [128, NF], F32))
    pf = [TT(es.enter_context(nc.psum_tensor("pf%d" % i, [128, 512], F32))[:], "pf%d" % i, True) for i in range(6)]
    pb = [TT(es.enter_context(nc.psum_tensor("pb%d" % i, [128, 1024], BF16))[:], "pb%d" % i, True) for i in range(2)]
    AB = Arena(ABt, NB)
    AFa = Arena(AFt, NF)

    def bcast_row(src, n):
        return bass.AP(src.tensor, 0, [[0, 128], [1, n]])

    def dma(eng, o, i, reads, writes):
        P.add(eng, lambda e, o=o, i=i: e.dma_start(out=o, in_=i), reads=reads, writes=writes, dma=True)

    cf = AFa.get("cf", 4, 128)
    dma("sp", cf.ap, cst.rearrange("p (a b) -> p a b", a=4), [], [cf.b])
    identf, triU, Ustr, onesf = cf.ap[:, 0, :], cf.ap[:, 1, :], cf.ap[:, 2, :], cf.ap[:, 3, :]
    cb16 = AB.get("cb16", 4, 128)
    dma("pool", cb16.ap, cst.rearrange("p (a b) -> p a b", a=4), [], [cb16.b])
    identb, maskb = cb16.ap[:, 0, :], cb16.ap[:, 1, :]
    flg = AFa.get("flg", NPRE + NM1)
    dma("sp", flg.ap, pflag, [], [flg.b])
    smallp = AFa.get("smallp", 6, 32)
    dma("sp", smallp.ap[:, 0, :], bcast_row(dtb, 32), [], [smallp.b])
    dma("sp", smallp.ap[:, 1, :], bcast_row(alog, 32), [], [smallp.b])
    dma("sp", smallp.ap[:, 2, :], bcast_row(dsk, 32), [], [smallp.b])
    dma("sp", smallp.ap[:, 3, 0:16], bcast_row(sinks, 16), [], [smallp.b])
    P.add("act", lambda e: e.activation(out=smallp.ap[:, 1, :], in_=smallp.ap[:, 1, :], func=AF.Exp), reads=[smallp.b], writes=[smallp.b])
    P.add("dve", lambda e: e.tensor_scalar(out=smallp.ap[:, 1, :], in0=smallp.ap[:, 1, :], scalar1=-1.0, scalar2=None, op0=ALU.mult), reads=[smallp.b], writes=[smallp.b])
    P.add("act", lambda e: e.activation(out=smallp.ap[:, 3, 0:16], in_=smallp.ap[:, 3, 0:16], func=AF.Exp), reads=[smallp.b], writes=[smallp.b])
    dtb_bc, a_bc, D_bc, esink = smallp.ap[:, 0, :], smallp.ap[:, 1, :], smallp.ap[:, 2, :], smallp.ap[:, 3, 0:16]
    onecol = onesf[:, 0:1]
    rawh = AB.get("rawh", 24, 4)
    markB, markF = AB.off, AFa.off
    H = AFa.get("H", 2048)

    def load_w(dst, src, ncols, nk=8):
        nk = src.shape[0] // 128
        for c0 in range(0, ncols, 512):
            c1 = min(c0 + 512, ncols)
            for k in range(nk):
                dma("pool", dst.ap[:, k, c0:c1], src[k * 128:(k + 1) * 128, c0:c1], [], [dst.b])

    def load_xT(xrow_ap, xb, xT):
        dma("pool", xb.ap, xrow_ap, [], [xb.b])
        for k in range(8):
            P.add("pe", lambda e, k=k: e.transpose(pb[0].ap[:, k * 128:(k + 1) * 128], xb.ap[:, k * 128:(k + 1) * 128], identb), reads=[xb.b, cb16.b], writes=[pb[0].b])
        P.add("act", lambda e: e.copy(out=xT.ap.rearrange("p a b -> p (a b)"), in_=pb[0].ap), reads=[pb[0].b], writes=[xT.b])

    def chain(bank, out_ap, pairs, reads):
        n = len(pairs)
        for i, (l, r) in enumerate(pairs):
            P.add("pe", lambda e, l=l, r=r, i=i: e.matmul(out_ap, l, r, start=(i == 0), stop=(i == n - 1)), reads=reads, writes=[bank.b])

    def per_channel(dst, src_dram, rows):
        tmp = AFa.get("pc_tmp", 128)
        dma("sp", tmp.ap[0:rows, :], src_dram, [], [tmp.b])
        P.add("pe", lambda e: e.transpose(pf[5].ap[:, 0:rows], tmp.ap[0:rows, :], identf[0:rows, 0:rows]), reads=[tmp.b, cf.b], writes=[pf[5].b])
        P.add("dve", lambda e: e.tensor_copy(out=dst, in_=pf[5].ap[:, 0:rows]), reads=[pf[5].b], writes=[])

    import os
    SKIP1 = int(os.environ.get('KSKIP1', '0'))
    Wx = AB.get("Wx", 8, 3072); load_w(Wx, w_in[:, COL_X:COL_X + 3072], 3072)
    Wdt = AB.get("Wdt", 8, 32); load_w(Wdt, w_in[:, COL_DT:COL_DT + 32], 32)
    cwt = AFa.get("cwt", 96); cbt = AFa.get("cbt", 24)
    per_channel(cwt.ap, scw, 96)
    per_channel(cbt.ap, scb, 24)
    cwt.b.writer = P.ops["dve"][-2]; cbt.b.writer = P.ops["dve"][-1]
    diag = AB.get("diag", 24, 4, 128)
    for j in range(24):
        for tp in range(4):
            P.add("dve", lambda e, j=j, tp=tp: e.tensor_scalar(out=diag.ap[:, j, tp, :], in0=identf, scalar1=cwt.ap[:, tp * 24 + j:tp * 24 + j + 1], scalar2=None, op0=ALU.mult), reads=[cf.b, cwt.b], writes=[diag.b])
    P.add("dve", lambda e: e.memset(H.ap, 0.0), writes=[H.b])
    mark1B, mark1F = AB.off, AFa.off
    xbA = [AB.get("xbA%d" % i, 1024) for i in range(2)]
    xT4 = [AB.get("xT4_%d" % i, 8, 512) for i in range(2)]
    raw4 = AB.get("raw4", 24, 516); xc4 = AB.get("xc4", 24, 512)
    rawb = [Buf("rawb%d" % j) for j in range(24)]; xcb = [Buf("xcb%d" % j) for j in range(24)]
    xtk = [AB.get("xtk%d" % i, 2560) for i in range(2)]
    xdtsA4 = [AB.get("xdtsA%d" % i, 512) for i in range(4)]
    P.add("pool", lambda e: e.memset(raw4.ap, 0.0), writes=rawb)

    def load_group(gi, buf):
        for q in range(4):
            c = gi * 4 + q
            xb_ = xbA[q % 2]
            dma("pool", xb_.ap, xpre[c * 128:(c + 1) * 128, :], [], [xb_.b])
            for k in range(8):
                P.add("pe", lambda e, k=k, xb_=xb_: e.transpose(pb[0].ap[:, k * 128:(k + 1) * 128], xb_.ap[:, k * 128:(k + 1) * 128], identb), reads=[xb_.b, cb16.b], writes=[pb[0].b])
            P.add("act", lambda e, q=q, buf=buf: e.copy(out=xT4[buf].ap[:, :, q * 128:(q + 1) * 128], in_=pb[0].ap.rearrange("p (a b) -> p a b", a=8)), reads=[pb[0].b], writes=[xT4[buf].b])

    def proj_in(gi, buf, last, j0, j1):
        ntile = 24 if last else 20
        for j in range(j0, min(j1, ntile)):
            bank = pf[j % 2]
            chain(bank, bank.ap, [(Wx.ap[:, k, j * 128:(j + 1) * 128], xT4[buf].ap[:, k, :]) for k in range(8)], [Wx.b, xT4[buf].b])
            P.add("act", lambda e, j=j, bank=bank: e.copy(out=raw4.ap[:, j, 3:515], in_=bank.ap), reads=[bank.b], writes=[rawb[j]])

    def proj_conv(gi, last):
        ntile = 24 if last else 20
        for j in range(ntile):
            bank = pf[2 + j % 2]
            chain(bank, bank.ap, [(diag.ap[:, j, tp, :], raw4.ap[:, j, tp:tp + 512]) for tp in range(4)], [diag.b, rawb[j]])
            P.add("act", lambda e, j=j, bank=bank: e.activation(out=xc4.ap[:, j, :], in_=bank.ap, func=AF.Silu, bias=cbt.ap[:, j:j + 1], scale=1.0), reads=[bank.b, cbt.b], writes=[xcb[j]])
        P.add("pool", lambda e: e.tensor_copy(out=raw4.ap[:, :, 0:3], in_=raw4.ap[:, :, 512:515]), reads=rawb, writes=rawb)

    smGs = [AFa.get("smG%d" % i, 8, 128) for i in range(2)]

    def group_chunks(gi, buf, hooks):
        c0 = gi * 4
        smG = smGs[gi % 2]
        S = lambda i: smG.ap[:, i, :]
        S3 = lambda i: smG.ap[:, i, :].rearrange("p (q h) -> p q h", q=4)
        b4 = lambda ap: ap.unsqueeze(1).to_broadcast([128, 4, 32])
        v3 = lambda ap: ap.rearrange("p (a b) -> p a b", a=8)
        for q in range(4):
            chain(pf[4], pf[4].ap[:, q * 32:(q + 1) * 32], [(xT4[buf].ap[:, k, q * 128:(q + 1) * 128], Wdt.ap[:, k, :]) for k in range(8)], [xT4[buf].b, Wdt.b])
        P.add("dve", lambda e: e.tensor_tensor(out=S3(0), in0=pf[4].ap[:, 0:128].rearrange("p (q h) -> p q h", q=4), in1=b4(dtb_bc), op=ALU.add), reads=[pf[4].b, smallp.b], writes=[smG.b])
        P.add("act", lambda e: e.activation(out=S(0), in_=S(0), func=AF.Exp), reads=[smG.b], writes=[smG.b])
        P.add("act", lambda e: e.activation(out=S(1), in_=S(0), func=AF.Ln, bias=onecol, scale=1.0), reads=[smG.b, cf.b], writes=[smG.b])
        P.add("dve", lambda e: e.tensor_tensor(out=S3(1), in0=S3(1), in1=bc(flg.ap[:, c0:c0 + 4], 32), op=ALU.mult), reads=[smG.b, flg.b], writes=[smG.b])
        P.add("dve", lambda e: e.tensor_tensor(out=S3(2), in0=S3(1), in1=b4(a_bc), op=ALU.mult), reads=[smG.b, smallp.b], writes=[smG.b])

        def tr(q):
            xt = xtk[q % 2]
            for bt in range(3):
                nt = 8 if bt < 2 else 4
                pbk = pb[bt % 2]
                for i in range(nt):
                    jj = bt * 8 + i
                    P.add("pe", lambda e, i=i, jj=jj, q=q, pbk=pbk: e.transpose(pbk.ap[:, i * 128:(i + 1) * 128], xc4.ap[:, jj, q * 128:(q + 1) * 128], identb), reads=[xcb[jj], cb16.b], writes=[pbk.b])
                if bt == 1:
                    P.add("act", lambda e, bt=bt, nt=nt, xt=xt, pbk=pbk: e.copy(out=xt.ap[:, bt * 1024:bt * 1024 + nt * 128], in_=pbk.ap[:, 0:nt * 128]), reads=[pbk.b], writes=[xt.b])
                else:
                    P.add("dve", lambda e, bt=bt, nt=nt, xt=xt, pbk=pbk: e.tensor_copy(out=xt.ap[:, bt * 1024:bt * 1024 + nt * 128], in_=pbk.ap[:, 0:nt * 128]), reads=[pbk.b], writes=[xt.b])

        SBK = [pf[2], pf[3], pf[5], pf[4]]

        def state(q):
            xt = xtk[q % 2]
            for g in range(4):
                xg = xt.ap[:, g * 512:(g + 1) * 512].rearrange("p (a b) -> p a b", a=8)
                o = q * 32 + 8 * g
                xd = xdtsA4[g]
                P.add("dve", lambda e, xg=xg, o=o, xd=xd: e.tensor_tensor(out=v3(xd.ap), in0=xg, in1=bc(smG.ap[:, 6, o:o + 8], 64), op=ALU.mult), reads=[xt.b, smG.b], writes=[xd.b])
                P.add("pe", lambda e, g=g, xt=xt, xd=xd: e.matmul(SBK[g].ap, xt.ap[:, 2048 + g * 128:2048 + (g + 1) * 128], xd.ap, start=True, stop=True), reads=[xt.b, xd.b], writes=[SBK[g].b])
            P.add("dve", lambda e, q=q: e.tensor_tensor(out=H.ap.rearrange("p (a b) -> p a b", a=32), in0=H.ap.rearrange("p (a b) -> p a b", a=32), in1=bc(smG.ap[:, 7, q * 32:(q + 1) * 32], 64), op=ALU.mult), reads=[H.b, smG.b], writes=[H.b])
            for g in range(4):
                Hg = H.ap[:, g * 512:(g + 1) * 512]
                P.add("dve", lambda e, Hg=Hg, g=g: e.tensor_tensor(out=Hg, in0=Hg, in1=SBK[g].ap, op=ALU.add), reads=[H.b, SBK[g].b], writes=[H.b])

        tr(0); tr(1)
        hooks[0]()
        P.add("pe", lambda e: e.matmul(pf[4].ap[:, 128:256], triU, S(2), start=True, stop=True), reads=[cf.b, smG.b], writes=[pf[4].b])
        P.add("pe", lambda e: e.matmul(pf[4].ap[:, 256:384], onesf, S(2), start=True, stop=True), reads=[cf.b, smG.b], writes=[pf[4].b])
        P.add("dve", lambda e: e.tensor_copy(out=smG.ap[:, 3:5, :], in_=pf[4].ap[:, 128:384].rearrange("p (a b) -> p a b", a=2)), reads=[pf[4].b], writes=[smG.b])
        P.add("dve", lambda e: e.tensor_tensor(out=S(6), in0=S(4), in1=S(3), op=ALU.subtract), reads=[smG.b], writes=[smG.b])
        P.add("act", lambda e: e.activation(out=S(6), in_=S(6), func=AF.Exp), reads=[smG.b], writes=[smG.b])
        P.add("act", lambda e: e.activation(out=S(7), in_=S(4), func=AF.Exp), reads=[smG.b], writes=[smG.b])
        P.add("dve", lambda e: e.tensor_tensor(out=S(6), in0=S(6), in1=S(1), op=ALU.mult), reads=[smG.b], writes=[smG.b])
        state(0); state(1)
        hooks[1]()
        tr(2); tr(3)
        hooks[2]()
        state(2); state(3)
        hooks[3]()

    NG = NPRE // 4
    g0 = NG - (npre + 3) // 4
    if not SKIP1 and g0 < NG:
        load_group(g0, g0 % 2)
        proj_in(g0, g0 % 2, g0 == NG - 1, 0, 24)
        for gi in range(g0, NG):
            proj_conv(gi, gi == NG - 1)
            if gi + 1 < NG:
                load_group(gi + 1, (gi + 1) % 2)
                hk = [lambda a=a, gi=gi: proj_in(gi + 1, (gi + 1) % 2, gi + 1 == NG - 1, a, a + 6) for a in (0, 6, 12, 18)]
            else:
                hk = [lambda: None] * 4
            group_chunks(gi, gi % 2, hk)
        P.add("pool", lambda e: e.tensor_copy(out=rawh.ap[:, :, 0:3], in_=raw4.ap[:, :, 0:3]), reads=rawb, writes=[rawh.b])
    else:
        P.add("pool", lambda e: e.memset(rawh.ap, 0.0), writes=[rawh.b])

    P.barrier()
    AB.off, AFa.off = mark1B, mark1F
    Wz = AB.get("Wz", 8, 2048); load_w(Wz, w_in[:, COL_Z:COL_Z + 2048], 2048)
    nwb = AFa.get("nwb", 2048)
    dma("sp", nwb.ap, bcast_row(normw, 2048), [], [nwb.b])
    xbs = [AB.get("xb_%d" % i, 1024) for i in range(2)]; xTs = [AB.get("xT_%d" % i, 8, 128) for i in range(2)]
    raws = [AB.get("raw_%d" % i, 24, 132) for i in range(2)]; xcs = [AB.get("xc_%d" % i, 24, 128) for i in range(2)]
    xtok = AB.get("xtok", 2560)
    xdt = AB.get("xdt", 512); xdts = AB.get("xdts", 512)
    Hb = AB.get("Hb", 2048); cbm = AB.get("cbm", 4, 128)
    LT = AB.get("LT", 4, 128); MT = AB.get("MT", 8, 128); yn = AB.get("yn", 2048)
    adtU = [AFa.get("adtU%d" % i, 128) for i in range(2)]
    sm = AFa.get("sm", 12, 32)
    yacc = AFa.get("yacc", 512); ytmp = AFa.get("ytmp", 512); sz = AFa.get("sz", 512)
    ssq = AFa.get("ssq", 4)
    P.add("pool", lambda e: e.memset(raws[0].ap, 0.0), writes=[raws[0].b])
    P.add("pool", lambda e: e.memset(raws[1].ap, 0.0), writes=[raws[1].b])
    P.add("pool", lambda e: e.tensor_copy(out=raws[0].ap[:, :, 0:3], in_=rawh.ap[:, :, 0:3]), reads=[rawh.b], writes=[raws[0].b])

    CUT = int(os.environ.get('KCUT', '99')); NCH1 = int(os.environ.get('KNCH', str(NM1)))
    def front_pieces(ci, xrow, xb, xT, raw, xc, raw_next):
        def inproj(g):
            bank = pf[g % 2]
            for jj in range(4):
                j = 4 * g + jj
                chain(bank, bank.ap[:, jj * 128:(jj + 1) * 128], [(Wx.ap[:, k, j * 128:(j + 1) * 128], xT.ap[:, k, :]) for k in range(8)], [Wx.b, xT.b])
            P.add("act", lambda e, g=g, bank=bank: e.copy(out=raw.ap[:, 4 * g:4 * g + 4, 3:131], in_=bank.ap.rearrange("p (a b) -> p a b", a=4)), reads=[bank.b], writes=[raw.b])

        def conv(g):
            bank = pf[2 + g % 2]
            for jj in range(4):
                j = 4 * g + jj
                chain(bank, bank.ap[:, jj * 128:(jj + 1) * 128], [(diag.ap[:, j, tp, :], raw.ap[:, j, tp:tp + 128]) for tp in range(4)], [diag.b, raw.b])
                P.add("act", lambda e, j=j, jj=jj, bank=bank: e.activation(out=xc.ap[:, j, :], in_=bank.ap[:, jj * 128:(jj + 1) * 128], func=AF.Silu, bias=cbt.ap[:, j:j + 1], scale=1.0), reads=[bank.b, cbt.b], writes=[xc.b])

        def p0():
            load_xT(xrow, xb, xT); inproj(0); inproj(1)

        def p1():
            inproj(2); inproj(3)

        def p2():
            inproj(4); inproj(5)
            P.add("pool", lambda e: e.tensor_copy(out=raw_next.ap[:, :, 0:3], in_=raw.ap[:, :, 128:131]), reads=[raw.b], writes=[raw_next.b])
            conv(0); conv(1)

        def p3():
            conv(2); conv(3); conv(4); conv(5)
        return [p0, p1, p2, p3]

    def ssd_back(fcol, main, ci, xT, xc, hooks):
        for bt in range(3):
            nt = 8 if bt < 2 else 4
            for i in range(nt):
                j = bt * 8 + i
                P.add("pe", lambda e, i=i, j=j: e.transpose(pb[1].ap[:, i * 128:(i + 1) * 128], xc.ap[:, j, :], identb), reads=[xc.b, cb16.b], writes=[pb[1].b])
            P.add("dve", lambda e, bt=bt, nt=nt: e.tensor_copy(out=xtok.ap[:, bt * 1024:bt * 1024 + nt * 128], in_=pb[1].ap[:, 0:nt * 128]), reads=[pb[1].b], writes=[xtok.b])
        chain(pf[4], pf[4].ap[:, 0:32], [(xT.ap[:, k, :], Wdt.ap[:, k, :]) for k in range(8)], [xT.b, Wdt.b])
        S = lambda i: sm.ap[:, i, :]
        P.add("dve", lambda e: e.tensor_tensor(out=S(0), in0=pf[4].ap[:, 0:32], in1=dtb_bc, op=ALU.add), reads=[pf[4].b, smallp.b], writes=[sm.b])
        P.add("act", lambda e: e.activation(out=S(0), in_=S(0), func=AF.Exp), reads=[sm.b], writes=[sm.b])
        P.add("act", lambda e: e.activation(out=S(1), in_=S(0), func=AF.Ln, bias=onecol, scale=1.0), reads=[sm.b, cf.b], writes=[sm.b])
        P.add("dve", lambda e: e.tensor_scalar(out=S(1), in0=S(1), scalar1=fcol, scalar2=None, op0=ALU.mult), reads=[sm.b, flg.b], writes=[sm.b])
        P.add("dve", lambda e: e.tensor_tensor(out=S(2), in0=S(1), in1=a_bc, op=ALU.mult), reads=[sm.b, smallp.b], writes=[sm.b])
        P.add("pe", lambda e: e.matmul(pf[4].ap[:, 32:64], triU, S(2), start=True, stop=True), reads=[cf.b, sm.b], writes=[pf[4].b])
        P.add("pe", lambda e: e.matmul(pf[4].ap[:, 64:96], onesf, S(2), start=True, stop=True), reads=[cf.b, sm.b], writes=[pf[4].b])
        P.add("dve", lambda e: e.tensor_copy(out=sm.ap[:, 3:5, :], in_=pf[4].ap[:, 32:96].rearrange("p (a b) -> p a b", a=2)), reads=[pf[4].b], writes=[sm.b])
        if main:
            for g in range(4):
                P.add("pe", lambda e, g=g: e.matmul(pf[5].ap[:, g * 128:(g + 1) * 128], xc.ap[:, 16 + g, :], xc.ap[:, 20 + g, :], start=True, stop=True), reads=[xc.b], writes=[pf[5].b])
            P.add("dve", lambda e: e.tensor_tensor(out=cbm.ap, in0=pf[5].ap.rearrange("p (a b) -> p a b", a=4), in1=maskb.unsqueeze(1).to_broadcast([128, 4, 128]), op=ALU.mult), reads=[pf[5].b, cb16.b], writes=[cbm.b])
            P.add("pool", lambda e: e.tensor_copy(out=Hb.ap, in_=H.ap), reads=[H.b], writes=[Hb.b])
            P.add("act", lambda e: e.activation(out=S(5), in_=S(3), func=AF.Exp), reads=[sm.b], writes=[sm.b])
            for g in range(4):
                xg = xtok.ap[:, g * 512:(g + 1) * 512].rearrange("p (a b) -> p a b", a=8)
                P.add("dve", lambda e, g=g, xg=xg: e.tensor_tensor(out=xdt.ap.rearrange("p (a b) -> p a b", a=8), in0=xg, in1=bc(sm.ap[:, 1, 8 * g:8 * g + 8], 64), op=ALU.mult), reads=[xtok.b, sm.b], writes=[xdt.b])
                for hh in range(2):
                    bank = pf[hh]
                    for h4 in range(4):
                        h = 8 * g + 4 * hh + h4
                        au = adtU[h4 % 2]
                        P.add("dve", lambda e, h=h, au=au: e.tensor_scalar(out=au.ap, in0=Ustr, scalar1=sm.ap[:, 2, h:h + 1], scalar2=None, op0=ALU.mult), reads=[cf.b, sm.b], writes=[au.b])
                        P.add("pe", lambda e, h4=h4, au=au, bank=bank: e.matmul(bank.ap[:, h4 * 128:(h4 + 1) * 128], au.ap, triU, start=True, stop=True), reads=[au.b, cf.b], writes=[bank.b])
                    P.add("act", lambda e, bank=bank: e.activation(out=LT.ap.rearrange("p a b -> p (a b)"), in_=bank.ap, func=AF.Exp), reads=[bank.b], writes=[LT.b])
                    P.add("dve", lambda e, g=g, hh=hh: e.tensor_tensor(out=MT.ap[:, 4 * hh:4 * hh + 4, :], in0=LT.ap, in1=cbm.ap[:, g, :].unsqueeze(1).to_broadcast([128, 4, 128]), op=ALU.mult), reads=[LT.b, cbm.b], writes=[MT.b])
                for h8 in range(8):
                    P.add("pe", lambda e, h8=h8: e.matmul(pf[2].ap[:, h8 * 64:(h8 + 1) * 64], MT.ap[:, h8, :], xdt.ap[:, h8 * 64:(h8 + 1) * 64], start=True, stop=True), reads=[MT.b, xdt.b], writes=[pf[2].b])
                P.add("pe", lambda e, g=g: e.matmul(pf[3].ap, xc.ap[:, 20 + g, :], Hb.ap[:, g * 512:(g + 1) * 512], start=True, stop=True), reads=[xc.b, Hb.b], writes=[pf[3].b])
                v3 = lambda ap: ap.rearrange("p (a b) -> p a b", a=8)
                P.add("dve", lambda e, g=g: e.tensor_tensor(out=v3(yacc.ap), in0=v3(pf[3].ap), in1=bc(sm.ap[:, 5, 8 * g:8 * g + 8], 64), op=ALU.mult), reads=[pf[3].b, sm.b], writes=[yacc.b])
                P.add("dve", lambda e: e.tensor_tensor(out=yacc.ap, in0=yacc.ap, in1=pf[2].ap, op=ALU.add), reads=[pf[2].b, yacc.b], writes=[yacc.b])
                P.add("dve", lambda e, g=g, xg=xg: e.tensor_tensor(out=v3(ytmp.ap), in0=xg, in1=bc(D_bc[:, 8 * g:8 * g + 8], 64), op=ALU.mult), reads=[xtok.b, smallp.b], writes=[ytmp.b])
                P.add("dve", lambda e: e.tensor_tensor(out=yacc.ap, in0=yacc.ap, in1=ytmp.ap, op=ALU.add), reads=[ytmp.b, yacc.b], writes=[yacc.b])
                chain(pf[5], pf[5].ap, [(xT.ap[:, k, :], Wz.ap[:, k, g * 512:(g + 1) * 512]) for k in range(8)], [xT.b, Wz.b])
                P.add("act", lambda e: e.activation(out=sz.ap, in_=pf[5].ap, func=AF.Silu), reads=[pf[5].b], writes=[sz.b])
                P.add("dve", lambda e: e.tensor_tensor(out=yacc.ap, in0=yacc.ap, in1=sz.ap, op=ALU.mult), reads=[sz.b, yacc.b], writes=[yacc.b])
                P.add("act", lambda e, g=g: e.activation(out=ytmp.ap, in_=yacc.ap, func=AF.Square, accum_out=ssq.ap[:, g:g + 1]), reads=[yacc.b], writes=[ytmp.b, ssq.b])
                P.add("dve", lambda e, g=g: e.tensor_scalar(out=ssq.ap[:, g:g + 1], in0=ssq.ap[:, g:g + 1], scalar1=1.0 / 512, scalar2=1e-5, op0=ALU.mult, op1=ALU.add), reads=[ssq.b], writes=[ssq.b])
                P.add("act", lambda e, g=g: e.activation(out=ssq.ap[:, g:g + 1], in_=ssq.ap[:, g:g + 1], func=AF.Sqrt), reads=[ssq.b], writes=[ssq.b])
                P.add("dve", lambda e, g=g: e.reciprocal(out=ssq.ap[:, g:g + 1], in_=ssq.ap[:, g:g + 1]), reads=[ssq.b], writes=[ssq.b])
                P.add("dve", lambda e, g=g: e.scalar_tensor_tensor(out=yn.ap[:, g * 512:(g + 1) * 512], in0=yacc.ap, scalar=ssq.ap[:, g:g + 1], in1=nwb.ap[:, g * 512:(g + 1) * 512], op0=ALU.mult, op1=ALU.mult), reads=[yacc.b, ssq.b, nwb.b], writes=[yn.b])
                hooks[g]()
            dma("sp", ynd[ci * 128:(ci + 1) * 128, :], yn.ap, [yn.b], [])
        P.add("dve", lambda e: e.tensor_tensor(out=S(6), in0=S(4), in1=S(3), op=ALU.subtract), reads=[sm.b], writes=[sm.b])
        P.add("act", lambda e: e.activation(out=S(6), in_=S(6), func=AF.Exp), reads=[sm.b], writes=[sm.b])
        P.add("act", lambda e: e.activation(out=S(7), in_=S(4), func=AF.Exp), reads=[sm.b], writes=[sm.b])
        P.add("dve", lambda e: e.tensor_tensor(out=S(6), in0=S(6), in1=S(1), op=ALU.mult), reads=[sm.b], writes=[sm.b])
        for g in range(4):
            xg = xtok.ap[:, g * 512:(g + 1) * 512].rearrange("p (a b) -> p a b", a=8)
            v3 = lambda ap: ap.rearrange("p (a b) -> p a b", a=8)
            P.add("dve", lambda e, g=g, xg=xg: e.tensor_tensor(out=v3(xdts.ap), in0=xg, in1=bc(sm.ap[:, 6, 8 * g:8 * g + 8], 64), op=ALU.mult), reads=[xtok.b, sm.b], writes=[xdts.b])
            bank = pf[2 + g % 2]
            P.add("pe", lambda e, g=g, bank=bank: e.matmul(bank.ap, xtok.ap[:, 2048 + g * 128:2048 + (g + 1) * 128], xdts.ap, start=True, stop=True), reads=[xtok.b, xdts.b], writes=[bank.b])
            Hg = H.ap[:, g * 512:(g + 1) * 512]
            P.add("dve", lambda e, g=g, Hg=Hg: e.tensor_tensor(out=v3(Hg), in0=v3(Hg), in1=bc(sm.ap[:, 7, 8 * g:8 * g + 8], 64), op=ALU.mult), reads=[H.b, sm.b], writes=[H.b])
            P.add("dve", lambda e, Hg=Hg, bank=bank: e.tensor_tensor(out=Hg, in0=Hg, in1=bank.ap, op=ALU.add), reads=[H.b, bank.b], writes=[H.b])

    def fp(ci):
        b = ci % 2
        return front_pieces(ci, xmain[(ci + 1) * 128:(ci + 2) * 128, :], xbs[b], xTs[b], raws[b], xcs[b], raws[1 - b])
    nch = 0 if SKIP1 else NCH1
    if nch:
        for p_ in fp(0):
            p_()
    for ci in range(nch):
        nxt = fp(ci + 1) if ci + 1 < nch else [lambda: None] * 4
        ssd_back(flg.ap[:, NPRE + ci:NPRE + ci + 1], True, ci, xTs[ci % 2], xcs[ci % 2], nxt)

    if stop < 2:
        P.emit(nc); es.close(); return nc
    P.barrier()
    AB.off, AFa.off = markB, markF
    Wq = AB.get("Wq", 8, 1024); load_w(Wq, w_in[:, COL_Q:COL_Q + 1024], 1024)
    Wk2 = AB.get("Wk2", 8, 128); load_w(Wk2, w_in[:, COL_K:COL_K + 128], 128)
    Wv = AB.get("Wv", 8, 128); load_w(Wv, w_in[:, COL_V:COL_V + 128], 128)
    EB = AB.get("EB", 2, 16, 128)
    ebf = AFa.get("ebf", 2048); mkf = AFa.get("mkf", 2048)
    for kt in range(2):
        dma("sp", ebf.ap, biasg[:, kt * 2048:(kt + 1) * 2048], [], [ebf.b])
        dma("sp", mkf.ap, maskg[:, kt * 2048:(kt + 1) * 2048], [], [mkf.b])
        P.add("act", lambda e: e.activation(out=ebf.ap, in_=ebf.ap, func=AF.Exp), reads=[ebf.b], writes=[ebf.b])
        P.add("dve", lambda e, kt=kt: e.tensor_tensor(out=EB.ap[:, kt, :, :].rearrange("p a b -> p (a b)"), in0=ebf.ap, in1=mkf.ap, op=ALU.mult), reads=[ebf.b, mkf.b], writes=[EB.b])
    xb = AB.get("xb2", 1024); xT = AB.get("xT2", 8, 128)
    kT = [AB.get("kT%d" % i, 2, 128) for i in range(2)]
    vx = [AB.get("vx%d" % i, 2, 65) for i in range(2)]
    for i in range(2):
        P.add("pool", lambda e, i=i: e.memset(vx[i].ap, 1.0), writes=[vx[i].b])
    qT = AB.get("qT", 16, 128); et = AB.get("et", 4, 128)
    PT = [AB.get("PT%d" % i, 4, 128) for i in range(2)]
    ya = AB.get("ya", 1024)
    den = AFa.get("den", 4)
    flag0 = flg.ap[:, NPRE:NPRE + 1]
    CUT2 = int(os.environ.get('KCUT2', '99')); NCH2 = int(os.environ.get('KNCH2', str(NM2)))
    for ci in range(NCH2):
        sl = ci % 2
        load_xT(xmain[ci * 128:(ci + 1) * 128, :], xb, xT)
        for kv in range(2):
            chain(pf[0], pf[0].ap[0:64, kv * 128:(kv + 1) * 128], [(Wk2.ap[:, k, kv * 64:(kv + 1) * 64], xT.ap[:, k, :]) for k in range(8)], [Wk2.b, xT.b])
        P.add("act", lambda e, sl=sl: e.copy(out=kT[sl].ap[0:64, :, :], in_=pf[0].ap[0:64, 0:256].rearrange("p (a b) -> p a b", a=2)), reads=[pf[0].b], writes=[kT[sl].b])
        chain(pf[1], pf[1].ap[:, 0:128], [(xT.ap[:, k, :], Wv.ap[:, k, :]) for k in range(8)], [xT.b, Wv.b])
        P.add("dve", lambda e, sl=sl: e.tensor_copy(out=vx[sl].ap[:, :, 0:64], in_=pf[1].ap[:, 0:128].rearrange("p (a b) -> p a b", a=2)), reads=[pf[1].b], writes=[vx[sl].b])
        if ci == 0 or CUT2 <= 1:
            continue
        for q4 in range(4):
            bank = pf[2 + q4 % 2]
            for tt in range(4):
                j = q4 * 4 + tt
                chain(bank, bank.ap[0:64, tt * 128:(tt + 1) * 128], [(Wq.ap[:, k, j * 64:(j + 1) * 64], xT.ap[:, k, :]) for k in range(8)], [Wq.b, xT.b])
            P.add("act", lambda e, q4=q4, bank=bank: e.copy(out=qT.ap[0:64, q4 * 4:q4 * 4 + 4, :], in_=bank.ap[0:64, :].rearrange("p (a b) -> p a b", a=4)), reads=[bank.b], writes=[qT.b])
        for kvh in range(2):
            if CUT2 <= 2: break
            for hb in range(2):
                j0 = kvh * 8 + hb * 4
                for kt in range(2):
                    slk = (ci + 1 + kt) % 2
                    bank = pf[4 + kt]
                    for i in range(4):
                        j = j0 + i
                        base = (j % 2) * 64 * int(os.environ.get("KB64", "1"))
                        P.add("pe", lambda e, i=i, j=j, base=base, slk=slk, bank=bank, kvh=kvh: e.matmul(bank.ap[:, i * 128:(i + 1) * 128], kT[slk].ap[0:64, kvh, :], qT.ap[0:64, j, :], start=True, stop=True), reads=[kT[slk].b, qT.b], writes=[bank.b])
                    P.add("act", lambda e, bank=bank: e.activation(out=et.ap.rearrange("p a b -> p (a b)"), in_=bank.ap, func=AF.Exp, scale=0.125), reads=[bank.b], writes=[et.b])
                    if ci == 2 and kt == 0:
                        P.add("dve", lambda e, kt=kt, j0=j0: e.scalar_tensor_tensor(out=PT[kt].ap, in0=et.ap, scalar=flag0, in1=EB.ap[:, kt, j0:j0 + 4, :], op0=ALU.mult, op1=ALU.mult), reads=[et.b, EB.b, flg.b], writes=[PT[kt].b])
                    else:
                        P.add("dve", lambda e, kt=kt, j0=j0: e.tensor_tensor(out=PT[kt].ap, in0=et.ap, in1=EB.ap[:, kt, j0:j0 + 4, :], op=ALU.mult), reads=[et.b, EB.b], writes=[PT[kt].b])
                if CUT2 <= 3: continue
                bank = pf[hb]
                for i in range(4):
                    for kt in range(2):
                        slk = (ci + 1 + kt) % 2
                        P.add("pe", lambda e, i=i, kt=kt, slk=slk, bank=bank, kvh=kvh: e.matmul(bank.ap[:, i * 65:(i + 1) * 65], PT[kt].ap[:, i, :], vx[slk].ap[:, kvh, :], start=(kt == 0), stop=(kt == 1)), reads=[PT[kt].b, vx[slk].b], writes=[bank.b])
                if CUT2 <= 4: continue
                pv = bank.ap[:, 0:260].rearrange("p (a b) -> p a b", a=4)
                P.add("dve", lambda e, pv=pv, j0=j0: e.tensor_tensor(out=den.ap, in0=pv[:, :, 64], in1=esink[:, j0:j0 + 4], op=ALU.add), reads=[bank.b, smallp.b], writes=[den.b])
                P.add("dve", lambda e: e.reciprocal(out=den.ap, in_=den.ap), reads=[den.b], writes=[den.b])
                P.add("dve", lambda e, pv=pv, j0=j0: e.tensor_tensor(out=ya.ap[:, j0 * 64:(j0 + 4) * 64].rearrange("p (a b) -> p a b", a=4), in0=pv[:, :, 0:64], in1=bc(den.ap, 64), op=ALU.mult), reads=[bank.b, den.b], writes=[ya.b])
        dma("sp", yad[(ci - 1) * 128:ci * 128, :], ya.ap, [ya.b], [])

    if stop < 3:
        P.emit(nc); es.close(); return nc
    P.barrier()
    AB.off, AFa.off = markB, markF
    Wg = AB.get("Wg", 8, 2048); load_w(Wg, w_in[:, COL_G:COL_G + 2048], 2048)
    Wbs = AB.get("Wbs", 16, 1024); load_w(Wbs, w_bs, 1024)
    Wba = AB.get("Wba", 8, 1024); load_w(Wba, w_ba, 1024)
    Wmx = AB.get("Wmx", 8, 1024); load_w(Wmx, w_mix, 1024)
    bgb = AFa.get("bgb", 2048); dma("sp", bgb.ap, bcast_row(b_gate, 2048), [], [bgb.b])
    lng = AFa.get("lng", 2, 1024)
    dma("sp", lng.ap[:, 0, :], bcast_row(ln1g, 1024), [], [lng.b]); dma("sp", lng.ap[:, 1, :], bcast_row(ln1b, 1024), [], [lng.b])
    xb = AB.get("xb3", 1024); xT = AB.get("xT3", 8, 128)
    ynb = AB.get("ynb", 2048); yab = AB.get("yab", 1024)
    ynT = AB.get("ynT", 16, 128); yaT = AB.get("yaT", 8, 128)
    mg = AB.get("mg", 1024); mT = AB.get("mT", 8, 128)
    xf = AFa.get("xf", 1024); gt = AFa.get("gt", 2048); m1 = AFa.get("m1", 512); r = AFa.get("r", 1024)
    st = AFa.get("st", 2, 6); mv = AFa.get("mv", 2)

    def transp(src, dst, ntl):
        for bt in range(ntl // 8):
            for i in range(8):
                j = bt * 8 + i
                P.add("pe", lambda e, i=i, j=j: e.transpose(pb[1].ap[:, i * 128:(i + 1) * 128], src.ap[:, j * 128:(j + 1) * 128], identb), reads=[src.b, cb16.b], writes=[pb[1].b])
            P.add("act", lambda e, bt=bt: e.copy(out=dst.ap[:, bt * 8:bt * 8 + 8, :].rearrange("p a b -> p (a b)"), in_=pb[1].ap), reads=[pb[1].b], writes=[dst.b])

    def layer_norm(r, g_ap, b_ap, gb, st, mv):
        for i in range(2):
            P.add("dve", lambda e, i=i: e.bn_stats(out=st.ap[:, i, :], in_=r.ap[:, i * 512:(i + 1) * 512]), reads=[r.b], writes=[st.b])
        P.add("dve", lambda e: e.bn_aggr(out=mv.ap, in_=st.ap.rearrange("p a b -> p (a b)")), reads=[st.b], writes=[mv.b])
        P.add("dve", lambda e: e.tensor_scalar(out=mv.ap[:, 1:2], in0=mv.ap[:, 1:2], scalar1=1e-5, scalar2=None, op0=ALU.add), reads=[mv.b], writes=[mv.b])
        P.add("act", lambda e: e.activation(out=mv.ap[:, 1:2], in_=mv.ap[:, 1:2], func=AF.Sqrt), reads=[mv.b], writes=[mv.b])
        P.add("dve", lambda e: e.reciprocal(out=mv.ap[:, 1:2], in_=mv.ap[:, 1:2]), reads=[mv.b], writes=[mv.b])
        P.add("dve", lambda e: e.tensor_scalar(out=r.ap, in0=r.ap, scalar1=mv.ap[:, 0:1], scalar2=mv.ap[:, 1:2], op0=ALU.subtract, op1=ALU.mult), reads=[r.b, mv.b], writes=[r.b])
        P.add("dve", lambda e: e.tensor_tensor(out=r.ap, in0=r.ap, in1=g_ap, op=ALU.mult), reads=[r.b, gb], writes=[r.b])
        P.add("dve", lambda e: e.tensor_tensor(out=r.ap, in0=r.ap, in1=b_ap, op=ALU.add), reads=[r.b, gb], writes=[r.b])

    for ci in range(NM1):
        xrow = xmain[(ci + 1) * 128:(ci + 2) * 128, :]
        load_xT(xrow, xb, xT)
        dma("sp", xf.ap, xrow, [], [xf.b])
        dma("sp", ynb.ap, ynd[ci * 128:(ci + 1) * 128, :], [], [ynb.b])
        dma("sp", yab.ap, yad[ci * 128:(ci + 1) * 128, :], [], [yab.b])
        transp(ynb, ynT, 16)
        transp(yab, yaT, 8)
        for s4 in range(4):
            bank = pf[s4 % 2]
            chain(bank, bank.ap, [(xT.ap[:, k, :], Wg.ap[:, k, s4 * 512:(s4 + 1) * 512]) for k in range(8)], [xT.b, Wg.b])
            P.add("dve", lambda e, s4=s4, bank=bank: e.tensor_tensor(out=gt.ap[:, s4 * 512:(s4 + 1) * 512], in0=bank.ap, in1=bgb.ap[:, s4 * 512:(s4 + 1) * 512], op=ALU.add), reads=[bank.b, bgb.b], writes=[gt.b])
        P.add("act", lambda e: e.activation(out=gt.ap, in_=gt.ap, func=AF.Sigmoid), reads=[gt.b], writes=[gt.b])
        for hf in range(2):
            chain(pf[2], pf[2].ap, [(ynT.ap[:, i, :], Wbs.ap[:, i, hf * 512:(hf + 1) * 512]) for i in range(16)], [ynT.b, Wbs.b])
            chain(pf[3], pf[3].ap, [(yaT.ap[:, i, :], Wba.ap[:, i, hf * 512:(hf + 1) * 512]) for i in range(8)], [yaT.b, Wba.b])
            P.add("dve", lambda e, hf=hf: e.tensor_tensor(out=m1.ap, in0=pf[2].ap, in1=gt.ap[:, hf * 512:(hf + 1) * 512], op=ALU.mult), reads=[pf[2].b, gt.b], writes=[m1.b])
            P.add("dve", lambda e, hf=hf: e.tensor_tensor(out=r.ap[:, hf * 512:(hf + 1) * 512], in0=pf[3].ap, in1=gt.ap[:, 1024 + hf * 512:1024 + (hf + 1) * 512], op=ALU.mult), reads=[pf[3].b, gt.b], writes=[r.b])
            P.add("dve", lambda e, hf=hf: e.tensor_tensor(out=mg.ap[:, hf * 512:(hf + 1) * 512], in0=m1.ap, in1=r.ap[:, hf * 512:(hf + 1) * 512], op=ALU.add), reads=[m1.b, r.b], writes=[mg.b])
        transp(mg, mT, 8)
        for hf in range(2):
            bank = pf[4 + hf]
            chain(bank, bank.ap, [(mT.ap[:, i, :], Wmx.ap[:, i, hf * 512:(hf + 1) * 512]) for i in range(8)], [mT.b, Wmx.b])
            P.add("dve", lambda e, hf=hf, bank=bank: e.scalar_tensor_tensor(out=r.ap[:, hf * 512:(hf + 1) * 512], in0=xf.ap[:, hf * 512:(hf + 1) * 512], scalar=ALPHA, in1=bank.ap, op0=ALU.mult, op1=ALU.add), reads=[xf.b, bank.b], writes=[r.b])
        layer_norm(r, lng.ap[:, 0, :], lng.ap[:, 1, :], lng.b, st, mv)
        if ci == 0:
            P.add("dve", lambda e: e.tensor_scalar(out=r.ap, in0=r.ap, scalar1=flag0, scalar2=None, op0=ALU.mult), reads=[r.b, flg.b], writes=[r.b])
        dma("sp", h1d[ci * 128:(ci + 1) * 128, :], r.ap, [r.b], [])

    if stop < 4:
        P.emit(nc); es.close(); return nc
    P.barrier()
    AB.off, AFa.off = markB, markF
    Wup = AB.get("Wup", 8, 5632); load_w(Wup, w_up, 5632)
    Wdn = AB.get("Wdn", 22, 1024); load_w(Wdn, w_dn, 1024)
    fw = AFa.get("fw", 132); fb = AFa.get("fb", 44)
    per_channel(fw.ap[:, 0:88], fcw[0:88, :], 88); fw.b.writer = P.ops["dve"][-1]
    per_channel(fw.ap[:, 88:132], fcw[88:132, :], 44); fw.b.writer = P.ops["dve"][-1]
    per_channel(fb.ap, fcb, 44); fb.b.writer = P.ops["dve"][-1]
    lng2 = AFa.get("lng2", 2, 1024)
    dma("sp", lng2.ap[:, 0, :], bcast_row(ln2g, 1024), [], [lng2.b]); dma("sp", lng2.ap[:, 1, :], bcast_row(ln2b, 1024), [], [lng2.b])
    SC = 256
    hA = AB.get("hA", 1024); hB = AB.get("hB", 1024); h1T = AB.get("h1T", 8, SC + 2)
    aT = AB.get("aT", 22, SC)
    aTb = [Buf("aTb%d" % i) for i in range(22)]
    cvs = [AFa.get("cvs%d" % i, SC) for i in range(4)]
    t0s = [AFa.get("t0s%d" % i, SC) for i in range(2)]
    sg = AFa.get("sg", SC)
    hr = AFa.get("hr", 1024); r4 = AFa.get("r4", 1024)
    st4 = AFa.get("st4", 2, 6); mv4 = AFa.get("mv4", 2)
    NT = SC // 128
    tcount = 0
    for sc in range(TOK // SC):
        r0 = 128 + sc * SC
        for i in range(NT):
            dma("pool", hA.ap, h1d[r0 - 2 + i * 128:r0 + 126 + i * 128, :], [], [hA.b])
            for k in range(8):
                P.add("pe", lambda e, k=k: e.transpose(pb[0].ap[:, k * 128:(k + 1) * 128], hA.ap[:, k * 128:(k + 1) * 128], identb), reads=[hA.b, cb16.b], writes=[pb[0].b])
            P.add("act", lambda e, i=i: e.copy(out=h1T.ap[:, :, i * 128:(i + 1) * 128], in_=pb[0].ap.rearrange("p (a b) -> p a b", a=8)), reads=[pb[0].b], writes=[h1T.b])
        dma("pool", hB.ap[0:2, :], h1d[r0 + SC - 2:r0 + SC, :], [], [hB.b])
        for k in range(8):
            P.add("pe", lambda e, k=k: e.transpose(pb[1].ap[:, k * 2:(k + 1) * 2], hB.ap[0:2, k * 128:(k + 1) * 128], identb[0:2, 0:2]), reads=[hB.b, cb16.b], writes=[pb[1].b])
        P.add("act", lambda e: e.copy(out=h1T.ap[:, :, SC:SC + 2], in_=pb[1].ap[:, 0:16].rearrange("p (a b) -> p a b", a=8)), reads=[pb[1].b], writes=[h1T.b])
        def down(jp):
            for tt in range(NT):
                for hf in range(2):
                    bank = pf[2 + tt * 2 + hf]
                    P.add("pe", lambda e, jp=jp, tt=tt, hf=hf, bank=bank: e.matmul(bank.ap, aT.ap[:, jp, tt * 128:(tt + 1) * 128], Wdn.ap[:, jp, hf * 512:(hf + 1) * 512], start=(jp == 0), stop=(jp == 21)), reads=[aTb[jp], Wdn.b], writes=[bank.b])

        for jp in range(22):
            for gv in range(2):
                j = jp + 22 * gv
                X = pf[tcount % 2]; t0 = t0s[tcount % 2]
                cv = cvs[(jp % 2) * 2 + gv]
                tcount += 1
                chain(X, X.ap[:, 0:SC + 2], [(Wup.ap[:, k, j * 128:(j + 1) * 128], h1T.ap[:, k, 0:SC + 2]) for k in range(8)], [Wup.b, h1T.b])
                w0, w1, w2, bb = fw.ap[:, j:j + 1], fw.ap[:, 44 + j:45 + j], fw.ap[:, 88 + j:89 + j], fb.ap[:, j:j + 1]
                P.add("act", lambda e, X=X, t0=t0, w2=w2, bb=bb: e.activation(out=t0.ap, in_=X.ap[:, 2:SC + 2], func=AF.Identity, bias=bb, scale=w2), reads=[X.b, fw.b, fb.b], writes=[t0.b])
                P.add("dve", lambda e, X=X, t0=t0, w1=w1: e.scalar_tensor_tensor(out=t0.ap, in0=X.ap[:, 1:SC + 1], scalar=w1, in1=t0.ap, op0=ALU.mult, op1=ALU.add), reads=[X.b, fw.b, t0.b], writes=[t0.b])
                P.add("dve", lambda e, X=X, t0=t0, w0=w0, cv=cv: e.scalar_tensor_tensor(out=cv.ap, in0=X.ap[:, 0:SC], scalar=w0, in1=t0.ap, op0=ALU.mult, op1=ALU.add), reads=[X.b, fw.b, t0.b], writes=[cv.b])
            cg, cvv = cvs[(jp % 2) * 2], cvs[(jp % 2) * 2 + 1]
            P.add("act", lambda e, cg=cg: e.activation(out=sg.ap, in_=cg.ap, func=AF.Silu), reads=[cg.b], writes=[sg.b])
            P.add("dve", lambda e, jp=jp, cvv=cvv: e.tensor_tensor(out=aT.ap[:, jp, :], in0=sg.ap, in1=cvv.ap, op=ALU.mult), reads=[sg.b, cvv.b], writes=[aTb[jp]])
            if jp >= 1:
                down(jp - 1)
        down(21)
        for tt in range(NT):
            rr = r0 + tt * 128
            dma("sp", hr.ap, h1d[rr:rr + 128, :], [], [hr.b])
            for hf in range(2):
                bank = pf[2 + tt * 2 + hf]
                P.add("dve", lambda e, hf=hf, bank=bank: e.scalar_tensor_tensor(out=r4.ap[:, hf * 512:(hf + 1) * 512], in0=hr.ap[:, hf * 512:(hf + 1) * 512], scalar=ALPHA, in1=bank.ap, op0=ALU.mult, op1=ALU.add), reads=[hr.b, bank.b], writes=[r4.b])
            layer_norm(r4, lng2.ap[:, 0, :], lng2.ap[:, 1, :], lng2.b, st4, mv4)
            dma("sp", out[rr - 128:rr, :], r4.ap, [r4.b], [])

    P.emit(nc)
    es.close()
    return nc


def rel_bucket_np(rel):
    n = np.maximum(rel, 0)
    nf = np.maximum(n, 1).astype(np.float32)
    large = 16 + (np.log(nf / np.float32(16)) / np.float32(np.log(128 / 16)) * np.float32(16)).astype(np.int32)
    large = np.minimum(large, 31)
    return np.where(n < 16, n, large)


_NC = None


def kernel(_dbg=None, **inp):
    global _NC
    x = np.asarray(inp["x"], np.float32)[0]
    f = lambda k: np.ascontiguousarray(np.asarray(inp[k], np.float32)[0])
    common = {
        "w_in": f("w_in"), "b_gate": f("b_gate")[None], "dtb": f("ssm_dt_bias")[None], "alog": f("ssm_a_log")[None],
        "dsk": f("ssm_d")[None], "normw": f("ssm_norm_w")[None], "sinks": f("attn_sinks")[None],
        "w_bs": f("w_branch_ssm"), "w_ba": f("w_branch_attn"), "w_mix": f("w_mix_out"),
        "ln1g": f("ln1_g")[None], "ln1b": f("ln1_b")[None], "ln2g": f("ln2_g")[None], "ln2b": f("ln2_b")[None],
        "w_up": f("w_up"), "w_dn": f("w_down"),
    }
    scw = f("ssm_conv_w")
    common["scw"] = np.ascontiguousarray(scw.reshape(4 * 24, 128))
    common["scb"] = np.ascontiguousarray(f("ssm_conv_b").reshape(24, 128))
    common["fcw"] = np.ascontiguousarray(f("ffn_conv_w").reshape(3 * 44, 128))
    common["fcb"] = np.ascontiguousarray(f("ffn_conv_b").reshape(44, 128))
    s = np.arange(128)
    ident = np.eye(128, dtype=np.float32)
    triU = (s[:, None] <= s[None, :]).astype(np.float32)
    ustr = (s[:, None] > s[None, :]).astype(np.float32)
    common["cst"] = np.ascontiguousarray(np.concatenate([ident, triU, ustr, np.ones((128, 128), np.float32)], axis=1))
    rb = np.asarray(inp["rel_bias"], np.float32)
    bg = np.zeros((128, 2, 16, 128), np.float32); mk = np.zeros((128, 2, 16, 128), np.float32)
    for kt in range(2):
        rel = (s[None, :] + 128) - (s[:, None] + 128 * kt)
        valid = (rel >= 0) & (rel < 128)
        bidx = rel_bucket_np(rel)
        g = rb[bidx]
        bg[:, kt] = np.transpose(g, (0, 2, 1))
        mk[:, kt] = np.broadcast_to(valid[:, None, :], (128, 16, 128))
    common["biasg"] = np.ascontiguousarray(bg.reshape(128, -1)); common["maskg"] = np.ascontiguousarray(mk.reshape(128, -1))
    in_maps = []
    for c in range(NCORE):
        S = c * TOK
        lo = S - 128 - NPRE * 128
        xp = np.zeros((NPRE * 128, 1024), np.float32)
        if S - 128 > 0:
            src_lo = max(lo, 0)
            xp[src_lo - lo:] = x[src_lo:S - 128]
        xm = np.zeros((NM2 * 128, 1024), np.float32)
        lo2 = S - 256
        src_lo = max(lo2, 0)
        xm[src_lo - lo2:] = x[src_lo:S + TOK]
        fl = np.zeros((128, NPRE + NM1), np.float32)
        for i in range(NPRE):
            fl[:, i] = 1.0 if lo + i * 128 >= 0 else 0.0
        fl[:, NPRE] = 1.0 if c > 0 else 0.0
        fl[:, NPRE + 1:] = 1.0
        m = dict(common); m["xpre"] = xp; m["xmain"] = xm; m["pflag"] = fl
        in_maps.append(m)
    if _dbg is not None:
        return in_maps
    if _NC is None:
        _NC = build()
    res = run_bass_kernel_spmd(_NC, in_maps, core_ids=list(range(NCORE)))
    o = np.concatenate([res.results[c]["out"] for c in range(NCORE)], axis=0)
    return o[None].astype(np.float32)
```

```python
import numpy as np
import concourse.bass as bass
import concourse.mybir as mybir

ENGS = ["pe", "act", "dve", "pool", "sp"]
N_DMA_SEM = 16
import os as _os
SAME_ENGINE_SYNC = _os.environ.get("KSES", "1") == "1"


class Buf:
    __slots__ = ("name", "writer", "readers", "dma_readers", "excl")

    def __init__(self, name, excl=False):
        self.name = name
        self.excl = excl
        self.writer = None
        self.readers = {}
        self.dma_readers = []


class Op:
    __slots__ = ("eng", "fn", "idx", "waits", "signal", "is_dma", "dsem", "dtarget", "clock", "sigcount", "uid")


class Prog:
    def __init__(self):
        self.ops = {e: [] for e in ENGS}
        self.clock = {e: {f: 0 for f in ENGS} for e in ENGS}
        self.known_dma = {e: set() for e in ENGS}
        self.dma_sem_count = [0] * N_DMA_SEM
        self.dma_sem_last = [None] * N_DMA_SEM
        self.dma_rr = 0
        self.n_dma = 0
        self.uid = 0
        self.bar = {e: [] for e in ENGS}

    def barrier(self):
        lasts = [self.ops[e][-1] for e in ENGS if self.ops[e] and not self.ops[e][-1].is_dma]
        for e in ENGS:
            pass
        lasts = []
        for e in ENGS:
            for op in reversed(self.ops[e]):
                if not op.is_dma:
                    lasts.append(op)
                    break
        dmas = [op for op in self.dma_sem_last if op is not None]
        for e in ENGS:
            self.bar[e] = lasts + dmas

    def add(self, eng, fn, reads=(), writes=(), dma=False):
        op = Op()
        op.eng = eng
        op.fn = fn
        op.idx = len(self.ops[eng])
        op.waits = []
        op.signal = False
        op.is_dma = dma
        op.uid = self.uid
        self.uid += 1
        def _flat(bs):
            o = []
            for b in bs:
                if isinstance(b, (list, tuple)):
                    o.extend(b)
                else:
                    o.append(b)
            return o
        reads = _flat(reads)
        writes = _flat(writes)
        deps = []
        for b in reads:
            if b.writer is not None:
                deps.append(b.writer)
            if b.excl:
                for e2, r in b.readers.items():
                    if e2 != eng:
                        deps.append(r)
        for b in writes:
            if b.writer is not None:
                deps.append(b.writer)
            deps.extend(b.readers.values())
            deps.extend(b.dma_readers)
        if self.bar[eng]:
            deps.extend(self.bar[eng])
            self.bar[eng] = []
        clk = self.clock[eng]
        seen = set()
        for d in deps:
            if d.uid in seen:
                continue
            seen.add(d.uid)
            if d.is_dma:
                if d.uid not in self.known_dma[eng]:
                    op.waits.append(("dma", d.dsem, d.dtarget))
                    self.known_dma[eng].add(d.uid)
            else:
                if d.eng == eng and (eng == "pe" or not SAME_ENGINE_SYNC):
                    continue
                if clk[d.eng] < d.idx + 1:
                    op.waits.append(("eng", d))
                    d.signal = True
                    for f in ENGS:
                        if d.clock[f] > clk[f]:
                            clk[f] = d.clock[f]
                    if clk[d.eng] < d.idx + 1:
                        clk[d.eng] = d.idx + 1
        if dma and eng == "pool":
            pd = self.__dict__.setdefault("pool_dmas", [])
            if len(pd) >= 4:
                d = pd[-4]
                if d.uid not in self.known_dma[eng]:
                    op.waits.append(("dma", d.dsem, d.dtarget))
                    self.known_dma[eng].add(d.uid)
            pd.append(op)
        if dma:
            k = self.dma_rr
            self.dma_rr = (self.dma_rr + 1) % N_DMA_SEM
            prev = self.dma_sem_last[k]
            if prev is not None and prev.uid not in self.known_dma[eng]:
                op.waits.append(("dma", k, prev.dtarget))
                self.known_dma[eng].add(prev.uid)
            self.dma_sem_count[k] += 16
            op.dsem = k
            op.dtarget = self.dma_sem_count[k]
            self.dma_sem_last[k] = op
            self.n_dma += 1
        op.clock = dict(clk)
        if not SAME_ENGINE_SYNC or eng == "pe":
            pass
        self.ops[eng].append(op)
        for b in writes:
            b.writer = op
            b.readers = {}
            b.dma_readers = []
        for b in reads:
            if dma:
                b.dma_readers.append(op)
            else:
                b.readers[eng] = op
        return op

    def emit(self, nc, final_waits=True):
        for e in ENGS:
            c = 0
            for op in self.ops[e]:
                if op.signal:
                    c += 1
                op.sigcount = c
        from contextlib import ExitStack
        with ExitStack() as es:
            esem = {e: es.enter_context(nc.semaphore("s_" + e)) for e in ENGS}
            dsem = [es.enter_context(nc.semaphore("d%d" % i)) for i in range(N_DMA_SEM)]
            block = es.enter_context(nc.Block())
            last_dma = [op for op in self.dma_sem_last if op is not None]

            def run(e, h):
                for op in self.ops[e]:
                    for w in op.waits:
                        if w[0] == "dma":
                            h.wait_ge(dsem[w[1]], w[2])
                        else:
                            h.wait_ge(esem[w[1].eng], w[1].sigcount)
                    ins = op.fn(h)
                    if op.is_dma:
                        ins.then_inc(dsem[op.dsem], 16)
                    elif op.signal:
                        ins.then_inc(esem[e], 1)
                if e == "sp" and final_waits:
                    for k in range(N_DMA_SEM):
                        if self.dma_sem_count[k] > 0:
                            h.wait_ge(dsem[k], self.dma_sem_count[k])

            @block.tensor
            def _(h):
                run("pe", h)

            @block.scalar
            def _(h):
                run("act", h)

            @block.vector
            def _(h):
                run("dve", h)

            @block.gpsimd
            def _(h):
                run("pool", h)

            @block.sync
            def _(h):
                run("sp", h)

from contextlib import ExitStack
import ml_dtypes
from concourse.bass_utils import run_bass_kernel_spmd

F32 = mybir.dt.float32
BF16 = mybir.dt.bfloat16
AF = mybir.ActivationFunctionType
ALU = mybir.AluOpType

NCORE = 8
TOK = 2048
NPRE = 112
NM1 = 17
NM2 = 18
ALPHA = 2.0 ** 0.25
COL_Z, COL_X, COL_B, COL_C, COL_DT, COL_Q, COL_K, COL_V, COL_G = 0, 2048, 4096, 4608, 5120, 5152, 6176, 6304, 6432


class TT:
    def __init__(self, ap, name, excl=False):
        self.ap = ap
        self.b = Buf(name, excl)


class Arena:
    def __init__(self, t, n):
        self.t, self.n, self.off = t, n, 0

    def get(self, name, *fs):
        n = int(np.prod(fs))
        ap = self.t[:, self.off:self.off + n]
        self.off += n
        assert self.off <= self.n, (name, self.off, self.n)
        if len(fs) == 2:
            ap = ap.rearrange("p (a b) -> p a b", a=fs[0])
        elif len(fs) == 3:
            ap = ap.rearrange("p (a b c) -> p a b c", a=fs[0], b=fs[1])
        return TT(ap, name)


def bc(ap2, n):
    return ap2.unsqueeze(2).to_broadcast([ap2.shape[0], ap2.shape[1], n])


def build(stop=4, npre=NPRE, dbg=False):
    nc = bass.Bass("TRN2", target_bir_lowering=False)
    dt_in = lambda n, s: nc.dram_tensor(n, s, F32, kind="ExternalInput").ap()
    xpre = dt_in("xpre", [NPRE * 128, 1024])
    xmain = dt_in("xmain", [NM2 * 128, 1024])
    pflag = dt_in("pflag", [128, NPRE + NM1])
    w_in = dt_in("w_in", [1024, 8480])
    b_gate = dt_in("b_gate", [1, 2048])
    scw = dt_in("scw", [96, 128])
    scb = dt_in("scb", [24, 128])
    dtb = dt_in("dtb", [1, 32])
    alog = dt_in("alog", [1, 32])
    dsk = dt_in("dsk", [1, 32])
    normw = dt_in("normw", [1, 2048])
    sinks = dt_in("sinks", [1, 16])
    w_bs = dt_in("w_bs", [2048, 1024])
    w_ba = dt_in("w_ba", [1024, 1024])
    w_mix = dt_in("w_mix", [1024, 1024])
    ln1g = dt_in("ln1g", [1, 1024]); ln1b = dt_in("ln1b", [1, 1024])
    ln2g = dt_in("ln2g", [1, 1024]); ln2b = dt_in("ln2b", [1, 1024])
    w_up = dt_in("w_up", [1024, 5632])
    fcw = dt_in("fcw", [132, 128])
    fcb = dt_in("fcb", [44, 128])
    w_dn = dt_in("w_dn", [2816, 1024])
    cst = dt_in("cst", [128, 4 * 128])
    biasg = dt_in("biasg", [128, 2 * 16 * 128])
    maskg = dt_in("maskg", [128, 2 * 16 * 128])
    out = nc.dram_tensor("out", [TOK, 1024], F32, kind="ExternalOutput").ap()
    skind = "ExternalOutput" if dbg else "Internal"
    ynd = nc.dram_tensor("ynd", [NM1 * 128, 2048], BF16, kind=skind).ap()
    yad = nc.dram_tensor("yad", [NM1 * 128, 1024], BF16, kind=skind).ap()
    h1d = nc.dram_tensor("h1d", [NM1 * 128, 1024], F32, kind=skind).ap()

    P = Prog()
    es = ExitStack()
    NB, NF = 164 * 512, 40 * 256
    ABt = es.enter_context(nc.sbuf_tensor("AB", [128, NB], BF16))
    AFt = es.enter_context(nc.sbuf_tensor("AF", [128, NF], F32))
    pf = [TT(es.enter_context(nc.psum_tensor("pf%d" % i, [128, 512], F32))[:], "pf%d" % i, True) for i in range(6)]
    for t_ in pf:
        t_.b = [Buf(t_.b.name + "q%d" % q_, True) for q_ in range(4)]
    pb = [TT(es.enter_context(nc.psum_tensor("pb%d" % i, [128, 1024], BF16))[:], "pb%d" % i, True) for i in range(2)]
    AB = Arena(ABt, NB)
    AFa = Arena(AFt, NF)

    def bcast_row(src, n):
        return bass.AP(src.tensor, 0, [[0, 128], [1, n]])

    def dma(eng, o, i, reads, writes):
        P.add(eng, lambda e, o=o, i=i: e.dma_start(out=o, in_=i), reads=reads, writes=writes, dma=True)

    cf = AFa.get("cf", 4, 128)
    dma("sp", cf.ap, cst.rearrange("p (a b) -> p a b", a=4), [], [cf.b])
    identf, triU, Ustr, onesf = cf.ap[:, 0, :], cf.ap[:, 1, :], cf.ap[:, 2, :], cf.ap[:, 3, :]
    cb16 = AB.get("cb16", 4, 128)
    dma("pool", cb16.ap, cst.rearrange("p (a b) -> p a b", a=4), [], [cb16.b])
    identb, maskb = cb16.ap[:, 0, :], cb16.ap[:, 1, :]
    flg = AFa.get("flg", NPRE + NM1)
    dma("sp", flg.ap, pflag, [], [flg.b])
    smallp = AFa.get("smallp", 6, 32)
    dma("sp", smallp.ap[:, 0, :], bcast_row(dtb, 32), [], [smallp.b])
    dma("sp", smallp.ap[:, 1, :], bcast_row(alog, 32), [], [smallp.b])
    dma("sp", smallp.ap[:, 2, :], bcast_row(dsk, 32), [], [smallp.b])
    dma("sp", smallp.ap[:, 3, 0:16], bcast_row(sinks, 16), [], [smallp.b])
    P.add("act", lambda e: e.activation(out=smallp.ap[:, 1, :], in_=smallp.ap[:, 1, :], func=AF.Exp), reads=[smallp.b], writes=[smallp.b])
    P.add("dve", lambda e: e.tensor_scalar(out=smallp.ap[:, 1, :], in0=smallp.ap[:, 1, :], scalar1=-1.0, scalar2=None, op0=ALU.mult), reads=[smallp.b], writes=[smallp.b])
    P.add("act", lambda e: e.activation(out=smallp.ap[:, 3, 0:16], in_=smallp.ap[:, 3, 0:16], func=AF.Exp), reads=[smallp.b], writes=[smallp.b])
    dtb_bc, a_bc, D_bc, esink = smallp.ap[:, 0, :], smallp.ap[:, 1, :], smallp.ap[:, 2, :], smallp.ap[:, 3, 0:16]
    onecol = onesf[:, 0:1]
    rawh = AB.get("rawh", 24, 4)
    markB, markF = AB.off, AFa.off
    H = AFa.get("H", 2048)

    def load_w(dst, src, ncols, nk=8):
        nk = src.shape[0] // 128
        for c0 in range(0, ncols, 512):
            c1 = min(c0 + 512, ncols)
            for k in range(nk):
                dma("pool", dst.ap[:, k, c0:c1], src[k * 128:(k + 1) * 128, c0:c1], [], [dst.b])

    def load_xT(xrow_ap, xb, xT):
        dma("pool", xb.ap, xrow_ap, [], [xb.b])
        for k in range(8):
            P.add("pe", lambda e, k=k: e.transpose(pb[0].ap[:, k * 128:(k + 1) * 128], xb.ap[:, k * 128:(k + 1) * 128], identb), reads=[xb.b, cb16.b], writes=[pb[0].b])
        P.add("act", lambda e: e.copy(out=xT.ap.rearrange("p a b -> p (a b)"), in_=pb[0].ap), reads=[pb[0].b], writes=[xT.b])

    def chain(bank, out_ap, pairs, reads, wb=None):
        n = len(pairs)
        wb = [bank.b] if wb is None else wb
        for i, (l, r) in enumerate(pairs):
            P.add("pe", lambda e, l=l, r=r, i=i: e.matmul(out_ap, l, r, start=(i == 0), stop=(i == n - 1)), reads=reads, writes=wb)

    def per_channel(dst, src_dram, rows):
        tmp = AFa.get("pc_tmp", 128)
        dma("sp", tmp.ap[0:rows, :], src_dram, [], [tmp.b])
        P.add("pe", lambda e: e.transpose(pf[5].ap[:, 0:rows], tmp.ap[0:rows, :], identf[0:rows, 0:rows]), reads=[tmp.b, cf.b], writes=[pf[5].b])
        P.add("dve", lambda e: e.tensor_copy(out=dst, in_=pf[5].ap[:, 0:rows]), reads=[pf[5].b], writes=[])

    import os
    SKIP1 = int(os.environ.get('KSKIP1', '0'))
    Wx = AB.get("Wx", 8, 3072); load_w(Wx, w_in[:, COL_X:COL_X + 3072], 3072)
    Wdt = AB.get("Wdt", 8, 32); load_w(Wdt, w_in[:, COL_DT:COL_DT + 32], 32)
    cwt = AFa.get("cwt", 96); cbt = AFa.get("cbt", 24)
    per_channel(cwt.ap, scw, 96)
    per_channel(cbt.ap, scb, 24)
    cwt.b.writer = P.ops["dve"][-2]; cbt.b.writer = P.ops["dve"][-1]
    diag = AB.get("diag", 24, 4, 128)
    for j in range(24):
        for tp in range(4):
            P.add("dve", lambda e, j=j, tp=tp: e.tensor_scalar(out=diag.ap[:, j, tp, :], in0=identf, scalar1=cwt.ap[:, tp * 24 + j:tp * 24 + j + 1], scalar2=None, op0=ALU.mult), reads=[cf.b, cwt.b], writes=[diag.b])
    P.add("dve", lambda e: e.memset(H.ap, 0.0), writes=[H.b])
    mark1B, mark1F = AB.off, AFa.off
    xbA = [AB.get("xbA%d" % i, 1024) for i in range(2)]
    xT4 = [AB.get("xT4_%d" % i, 8, 512) for i in range(2)]
    raw4 = AB.get("raw4", 24, 516); xc4 = AB.get("xc4", 24, 512)
    rawb = [Buf("rawb%d" % j) for j in range(24)]; xcb = [Buf("xcb%d" % j) for j in range(24)]
    xtk = [AB.get("xtk%d" % i, 2560) for i in range(2)]
    xdtsA4 = [AB.get("xdtsA%d" % i, 512) for i in range(4)]
    P.add("pool", lambda e: e.memset(raw4.ap, 0.0), writes=rawb)

    def load_group(gi, buf):
        for q in range(4):
            c = gi * 4 + q
            xb_ = xbA[q % 2]
            dma("pool", xb_.ap, xpre[c * 128:(c + 1) * 128, :], [], [xb_.b])
            for k in range(8):
                P.add("pe", lambda e, k=k, xb_=xb_: e.transpose(pb[0].ap[:, k * 128:(k + 1) * 128], xb_.ap[:, k * 128:(k + 1) * 128], identb), reads=[xb_.b, cb16.b], writes=[pb[0].b])
            P.add("act", lambda e, q=q, buf=buf: e.copy(out=xT4[buf].ap[:, :, q * 128:(q + 1) * 128], in_=pb[0].ap.rearrange("p (a b) -> p a b", a=8)), reads=[pb[0].b], writes=[xT4[buf].b])

    def proj_in(gi, buf, last, j0, j1):
        ntile = 24 if last else 20
        for j in range(j0, min(j1, ntile)):
            bank = pf[j % 2]
            chain(bank, bank.ap, [(Wx.ap[:, k, j * 128:(j + 1) * 128], xT4[buf].ap[:, k, :]) for k in range(8)], [Wx.b, xT4[buf].b])
            P.add("act", lambda e, j=j, bank=bank: e.copy(out=raw4.ap[:, j, 3:515], in_=bank.ap), reads=[bank.b], writes=[rawb[j]])

    def proj_conv(gi, last):
        ntile = 24 if last else 20
        for j in range(ntile):
            bank = pf[2 + j % 2]
            chain(bank, bank.ap, [(diag.ap[:, j, tp, :], raw4.ap[:, j, tp:tp + 512]) for tp in range(4)], [diag.b, rawb[j]])
            P.add("act", lambda e, j=j, bank=bank: e.activation(out=xc4.ap[:, j, :], in_=bank.ap, func=AF.Silu, bias=cbt.ap[:, j:j + 1], scale=1.0), reads=[bank.b, cbt.b], writes=[xcb[j]])
        P.add("pool", lambda e: e.tensor_copy(out=raw4.ap[:, :, 0:3], in_=raw4.ap[:, :, 512:515]), reads=rawb, writes=rawb)

    smGs = [AFa.get("smG%d" % i, 8, 128) for i in range(2)]

    def group_chunks(gi, buf, hooks):
        c0 = gi * 4
        smG = smGs[gi % 2]
        S = lambda i: smG.ap[:, i, :]
        S3 = lambda i: smG.ap[:, i, :].rearrange("p (q h) -> p q h", q=4)
        b4 = lambda ap: ap.unsqueeze(1).to_broadcast([128, 4, 32])
        v3 = lambda ap: ap.rearrange("p (a b) -> p a b", a=8)
        for q in range(4):
            chain(pf[4], pf[4].ap[:, q * 32:(q + 1) * 32], [(xT4[buf].ap[:, k, q * 128:(q + 1) * 128], Wdt.ap[:, k, :]) for k in range(8)], [xT4[buf].b, Wdt.b])
        P.add("dve", lambda e: e.tensor_tensor(out=S3(0), in0=pf[4].ap[:, 0:128].rearrange("p (q h) -> p q h", q=4), in1=b4(dtb_bc), op=ALU.add), reads=[pf[4].b, smallp.b], writes=[smG.b])
        P.add("act", lambda e: e.activation(out=S(0), in_=S(0), func=AF.Exp), reads=[smG.b], writes=[smG.b])
        P.add("act", lambda e: e.activation(out=S(1), in_=S(0), func=AF.Ln, bias=onecol, scale=1.0), reads=[smG.b, cf.b], writes=[smG.b])
        P.add("dve", lambda e: e.tensor_tensor(out=S3(1), in0=S3(1), in1=bc(flg.ap[:, c0:c0 + 4], 32), op=ALU.mult), reads=[smG.b, flg.b], writes=[smG.b])
        P.add("dve", lambda e: e.tensor_tensor(out=S3(2), in0=S3(1), in1=b4(a_bc), op=ALU.mult), reads=[smG.b, smallp.b], writes=[smG.b])

        def tr(q):
            xt = xtk[q % 2]
            for bt in range(3):
                nt = 8 if bt < 2 else 4
                pbk = pb[bt % 2]
                for i in range(nt):
                    jj = bt * 8 + i
                    P.add("pe", lambda e, i=i, jj=jj, q=q, pbk=pbk: e.transpose(pbk.ap[:, i * 128:(i + 1) * 128], xc4.ap[:, jj, q * 128:(q + 1) * 128], identb), reads=[xcb[jj], cb16.b], writes=[pbk.b])
                if bt == 1:
                    P.add("act", lambda e, bt=bt, nt=nt, xt=xt, pbk=pbk: e.copy(out=xt.ap[:, bt * 1024:bt * 1024 + nt * 128], in_=pbk.ap[:, 0:nt * 128]), reads=[pbk.b], writes=[xt.b])
                else:
                    P.add("dve", lambda e, bt=bt, nt=nt, xt=xt, pbk=pbk: e.tensor_copy(out=xt.ap[:, bt * 1024:bt * 1024 + nt * 128], in_=pbk.ap[:, 0:nt * 128]), reads=[pbk.b], writes=[xt.b])

        SBK = [pf[2], pf[3], pf[5], pf[4]]

        def state(q):
            xt = xtk[q % 2]
            for g in range(4):
                xg = xt.ap[:, g * 512:(g + 1) * 512].rearrange("p (a b) -> p a b", a=8)
                o = q * 32 + 8 * g
                xd = xdtsA4[g]
                P.add("dve", lambda e, xg=xg, o=o, xd=xd: e.tensor_tensor(out=v3(xd.ap), in0=xg, in1=bc(smG.ap[:, 6, o:o + 8], 64), op=ALU.mult), reads=[xt.b, smG.b], writes=[xd.b])
                P.add("pe", lambda e, g=g, xt=xt, xd=xd: e.matmul(SBK[g].ap, xt.ap[:, 2048 + g * 128:2048 + (g + 1) * 128], xd.ap, start=True, stop=True), reads=[xt.b, xd.b], writes=[SBK[g].b])
            P.add("dve", lambda e, q=q: e.tensor_tensor(out=H.ap.rearrange("p (a b) -> p a b", a=32), in0=H.ap.rearrange("p (a b) -> p a b", a=32), in1=bc(smG.ap[:, 7, q * 32:(q + 1) * 32], 64), op=ALU.mult), reads=[H.b, smG.b], writes=[H.b])
            for g in range(4):
                Hg = H.ap[:, g * 512:(g + 1) * 512]
                P.add("dve", lambda e, Hg=Hg, g=g: e.tensor_tensor(out=Hg, in0=Hg, in1=SBK[g].ap, op=ALU.add), reads=[H.b, SBK[g].b], writes=[H.b])

        tr(0); tr(1)
        hooks[0]()
        P.add("pe", lambda e: e.matmul(pf[4].ap[:, 128:256], triU, S(2), start=True, stop=True), reads=[cf.b, smG.b], writes=[pf[4].b])
        P.add("pe", lambda e: e.matmul(pf[4].ap[:, 256:384], onesf, S(2), start=True, stop=True), reads=[cf.b, smG.b], writes=[pf[4].b])
        P.add("dve", lambda e: e.tensor_copy(out=smG.ap[:, 3:5, :], in_=pf[4].ap[:, 128:384].rearrange("p (a b) -> p a b", a=2)), reads=[pf[4].b], writes=[smG.b])
        P.add("dve", lambda e: e.tensor_tensor(out=S(6), in0=S(4), in1=S(3), op=ALU.subtract), reads=[smG.b], writes=[smG.b])
        P.add("act", lambda e: e.activation(out=S(6), in_=S(6), func=AF.Exp), reads=[smG.b], writes=[smG.b])
        P.add("act", lambda e: e.activation(out=S(7), in_=S(4), func=AF.Exp), reads=[smG.b], writes=[smG.b])
        P.add("dve", lambda e: e.tensor_tensor(out=S(6), in0=S(6), in1=S(1), op=ALU.mult), reads=[smG.b], writes=[smG.b])
        state(0); state(1)
        hooks[1]()
        tr(2); tr(3)
        hooks[2]()
        state(2); state(3)
        hooks[3]()

    NG = NPRE // 4
    g0 = NG - (npre + 3) // 4
    if not SKIP1 and g0 < NG:
        load_group(g0, g0 % 2)
        proj_in(g0, g0 % 2, g0 == NG - 1, 0, 24)
        for gi in range(g0, NG):
            proj_conv(gi, gi == NG - 1)
            if gi + 1 < NG:
                load_group(gi + 1, (gi + 1) % 2)
                hk = [lambda a=a, gi=gi: proj_in(gi + 1, (gi + 1) % 2, gi + 1 == NG - 1, a, a + 6) for a in (0, 6, 12, 18)]
            else:
                hk = [lambda: None] * 4
            group_chunks(gi, gi % 2, hk)
        P.add("pool", lambda e: e.tensor_copy(out=rawh.ap[:, :, 0:3], in_=raw4.ap[:, :, 0:3]), reads=rawb, writes=[rawh.b])
    else:
        P.add("pool", lambda e: e.memset(rawh.ap, 0.0), writes=[rawh.b])

    P.barrier()
    AB.off, AFa.off = mark1B, mark1F
    Wz = AB.get("Wz", 8, 2048); load_w(Wz, w_in[:, COL_Z:COL_Z + 2048], 2048)
    nwb = AFa.get("nwb", 2048)
    dma("sp", nwb.ap, bcast_row(normw, 2048), [], [nwb.b])
    xbs = [AB.get("xb_%d" % i, 1024) for i in range(2)]; xTs = [AB.get("xT_%d" % i, 8, 128) for i in range(2)]
    raws = [AB.get("raw_%d" % i, 24, 132) for i in range(2)]; xcs = [AB.get("xc_%d" % i, 24, 128) for i in range(2)]
    xtok = AB.get("xtok", 2560)
    xdt = AB.get("xdt", 512); xdts = AB.get("xdts", 512)
    Hb = AB.get("Hb", 2048); cbm = AB.get("cbm", 4, 128)
    LT = AB.get("LT", 4, 128); MT = AB.get("MT", 8, 128); yn = AB.get("yn", 2048)
    adtU = [AFa.get("adtU%d" % i, 128) for i in range(8)]
    sm = AFa.get("sm", 12, 32)
    yacc = AFa.get("yacc", 512); ytmp = AFa.get("ytmp", 512); sz = AFa.get("sz", 512)
    ssq = AFa.get("ssq", 4)
    P.add("pool", lambda e: e.memset(raws[0].ap, 0.0), writes=[raws[0].b])
    P.add("pool", lambda e: e.memset(raws[1].ap, 0.0), writes=[raws[1].b])
    P.add("pool", lambda e: e.tensor_copy(out=raws[0].ap[:, :, 0:3], in_=rawh.ap[:, :, 0:3]), reads=[rawh.b], writes=[raws[0].b])

    CUT = int(os.environ.get('KCUT', '99')); NCH1 = int(os.environ.get('KNCH', str(NM1)))
    def front_pieces(ci, xrow, xb, xT, raw, xc, raw_next):
        def inproj(g):
            bank = pf[g % 2]
            for jj in range(4):
                j = 4 * g + jj
                chain(bank, bank.ap[:, jj * 128:(jj + 1) * 128], [(Wx.ap[:, k, j * 128:(j + 1) * 128], xT.ap[:, k, :]) for k in range(8)], [Wx.b, xT.b])
            P.add("act", lambda e, g=g, bank=bank: e.copy(out=raw.ap[:, 4 * g:4 * g + 4, 3:131], in_=bank.ap.rearrange("p (a b) -> p a b", a=4)), reads=[bank.b], writes=[raw.b])

        def conv(g):
            for jj in range(4):
                j = 4 * g + jj
                bank = pf[2 + j % 2]
                chain(bank, bank.ap[:, 0:128], [(diag.ap[:, j, tp, :], raw.ap[:, j, tp:tp + 128]) for tp in range(4)], [diag.b, raw.b])
                P.add("act", lambda e, j=j, bank=bank: e.activation(out=xc.ap[:, j, :], in_=bank.ap[:, 0:128], func=AF.Silu, bias=cbt.ap[:, j:j + 1], scale=1.0), reads=[bank.b, cbt.b], writes=[xc.b])

        def p0():
            load_xT(xrow, xb, xT); inproj(0); inproj(1)

        def p1():
            inproj(2); inproj(3)

        def p2():
            inproj(4); inproj(5)
            P.add("pool", lambda e: e.tensor_copy(out=raw_next.ap[:, :, 0:3], in_=raw.ap[:, :, 128:131]), reads=[raw.b], writes=[raw_next.b])
            conv(0); conv(1)

        def p3():
            conv(2); conv(3); conv(4); conv(5)
        return [p0, p1, p2, p3]

    def ssd_back(fcol, main, ci, xT, xc, hooks):
        for bt in range(3):
            nt = 8 if bt < 2 else 4
            for i in range(nt):
                j = bt * 8 + i
                P.add("pe", lambda e, i=i, j=j: e.transpose(pb[1].ap[:, i * 128:(i + 1) * 128], xc.ap[:, j, :], identb), reads=[xc.b, cb16.b], writes=[pb[1].b])
            P.add("dve", lambda e, bt=bt, nt=nt: e.tensor_copy(out=xtok.ap[:, bt * 1024:bt * 1024 + nt * 128], in_=pb[1].ap[:, 0:nt * 128]), reads=[pb[1].b], writes=[xtok.b])
        chain(pf[4], pf[4].ap[:, 0:32], [(xT.ap[:, k, :], Wdt.ap[:, k, :]) for k in range(8)], [xT.b, Wdt.b])
        S = lambda i: sm.ap[:, i, :]
        P.add("dve", lambda e: e.tensor_tensor(out=S(0), in0=pf[4].ap[:, 0:32], in1=dtb_bc, op=ALU.add), reads=[pf[4].b, smallp.b], writes=[sm.b])
        P.add("act", lambda e: e.activation(out=S(0), in_=S(0), func=AF.Exp), reads=[sm.b], writes=[sm.b])
        P.add("act", lambda e: e.activation(out=S(1), in_=S(0), func=AF.Ln, bias=onecol, scale=1.0), reads=[sm.b, cf.b], writes=[sm.b])
        P.add("dve", lambda e: e.tensor_scalar(out=S(1), in0=S(1), scalar1=fcol, scalar2=None, op0=ALU.mult), reads=[sm.b, flg.b], writes=[sm.b])
        P.add("dve", lambda e: e.tensor_tensor(out=S(2), in0=S(1), in1=a_bc, op=ALU.mult), reads=[sm.b, smallp.b], writes=[sm.b])
        P.add("pe", lambda e: e.matmul(pf[4].ap[:, 32:64], triU, S(2), start=True, stop=True), reads=[cf.b, sm.b], writes=[pf[4].b])
        P.add("pe", lambda e: e.matmul(pf[4].ap[:, 64:96], onesf, S(2), start=True, stop=True), reads=[cf.b, sm.b], writes=[pf[4].b])
        P.add("dve", lambda e: e.tensor_copy(out=sm.ap[:, 3:5, :], in_=pf[4].ap[:, 32:96].rearrange("p (a b) -> p a b", a=2)), reads=[pf[4].b], writes=[sm.b])
        if main:
            for g in range(4):
                P.add("pe", lambda e, g=g: e.matmul(pf[5].ap[:, g * 128:(g + 1) * 128], xc.ap[:, 16 + g, :], xc.ap[:, 20 + g, :], start=True, stop=True), reads=[xc.b], writes=[pf[5].b])
            P.add("dve", lambda e: e.tensor_tensor(out=cbm.ap, in0=pf[5].ap.rearrange("p (a b) -> p a b", a=4), in1=maskb.unsqueeze(1).to_broadcast([128, 4, 128]), op=ALU.mult), reads=[pf[5].b, cb16.b], writes=[cbm.b])
            P.add("pool", lambda e: e.tensor_copy(out=Hb.ap, in_=H.ap), reads=[H.b], writes=[Hb.b])
            P.add("act", lambda e: e.activation(out=S(5), in_=S(3), func=AF.Exp), reads=[sm.b], writes=[sm.b])
            for g in range(4):
                xg = xtok.ap[:, g * 512:(g + 1) * 512].rearrange("p (a b) -> p a b", a=8)
                P.add("dve", lambda e, g=g, xg=xg: e.tensor_tensor(out=xdt.ap.rearrange("p (a b) -> p a b", a=8), in0=xg, in1=bc(sm.ap[:, 1, 8 * g:8 * g + 8], 64), op=ALU.mult), reads=[xtok.b, sm.b], writes=[xdt.b])
                for hh in range(2):
                    bank = pf[hh]
                    for h4 in range(4):
                        h = 8 * g + 4 * hh + h4
                        au = adtU[h % 8]
                        P.add("dve", lambda e, h=h, au=au: e.tensor_scalar(out=au.ap, in0=Ustr, scalar1=sm.ap[:, 2, h:h + 1], scalar2=None, op0=ALU.mult), reads=[cf.b, sm.b], writes=[au.b])
                        P.add("pe", lambda e, h4=h4, au=au, bank=bank: e.matmul(bank.ap[:, h4 * 128:(h4 + 1) * 128], au.ap, triU, start=True, stop=True), reads=[au.b, cf.b], writes=[bank.b])
                    P.add("act", lambda e, bank=bank: e.activation(out=LT.ap.rearrange("p a b -> p (a b)"), in_=bank.ap, func=AF.Exp), reads=[bank.b], writes=[LT.b])
                    P.add("dve", lambda e, g=g, hh=hh: e.tensor_tensor(out=MT.ap[:, 4 * hh:4 * hh + 4, :], in0=LT.ap, in1=cbm.ap[:, g, :].unsqueeze(1).to_broadcast([128, 4, 128]), op=ALU.mult), reads=[LT.b, cbm.b], writes=[MT.b])
                for h8 in range(8):
                    P.add("pe", lambda e, h8=h8: e.matmul(pf[2].ap[:, h8 * 64:(h8 + 1) * 64], MT.ap[:, h8, :], xdt.ap[:, h8 * 64:(h8 + 1) * 64], start=True, stop=True), reads=[MT.b, xdt.b], writes=[pf[2].b])
                P.add("pe", lambda e, g=g: e.matmul(pf[3].ap, xc.ap[:, 20 + g, :], Hb.ap[:, g * 512:(g + 1) * 512], start=True, stop=True), reads=[xc.b, Hb.b], writes=[pf[3].b])
                v3 = lambda ap: ap.rearrange("p (a b) -> p a b", a=8)
                P.add("dve", lambda e, g=g: e.tensor_tensor(out=v3(yacc.ap), in0=v3(pf[3].ap), in1=bc(sm.ap[:, 5, 8 * g:8 * g + 8], 64), op=ALU.mult), reads=[pf[3].b, sm.b], writes=[yacc.b])
                P.add("dve", lambda e: e.tensor_tensor(out=yacc.ap, in0=yacc.ap, in1=pf[2].ap, op=ALU.add), reads=[pf[2].b, yacc.b], writes=[yacc.b])
                P.add("dve", lambda e, g=g, xg=xg: e.tensor_tensor(out=v3(ytmp.ap), in0=xg, in1=bc(D_bc[:, 8 * g:8 * g + 8], 64), op=ALU.mult), reads=[xtok.b, smallp.b], writes=[ytmp.b])
                P.add("dve", lambda e: e.tensor_tensor(out=yacc.ap, in0=yacc.ap, in1=ytmp.ap, op=ALU.add), reads=[ytmp.b, yacc.b], writes=[yacc.b])
                chain(pf[5], pf[5].ap, [(xT.ap[:, k, :], Wz.ap[:, k, g * 512:(g + 1) * 512]) for k in range(8)], [xT.b, Wz.b])
                P.add("act", lambda e: e.activation(out=sz.ap, in_=pf[5].ap, func=AF.Silu), reads=[pf[5].b], writes=[sz.b])
                P.add("dve", lambda e: e.tensor_tensor(out=yacc.ap, in0=yacc.ap, in1=sz.ap, op=ALU.mult), reads=[sz.b, yacc.b], writes=[yacc.b])
                P.add("act", lambda e, g=g: e.activation(out=ytmp.ap, in_=yacc.ap, func=AF.Square, accum_out=ssq.ap[:, g:g + 1]), reads=[yacc.b], writes=[ytmp.b, ssq.b])
                P.add("dve", lambda e, g=g: e.tensor_scalar(out=ssq.ap[:, g:g + 1], in0=ssq.ap[:, g:g + 1], scalar1=1.0 / 512, scalar2=1e-5, op0=ALU.mult, op1=ALU.add), reads=[ssq.b], writes=[ssq.b])
                P.add("act", lambda e, g=g: e.activation(out=ssq.ap[:, g:g + 1], in_=ssq.ap[:, g:g + 1], func=AF.Sqrt), reads=[ssq.b], writes=[ssq.b])
                P.add("dve", lambda e, g=g: e.reciprocal(out=ssq.ap[:, g:g + 1], in_=ssq.ap[:, g:g + 1]), reads=[ssq.b], writes=[ssq.b])
                P.add("dve", lambda e, g=g: e.scalar_tensor_tensor(out=yn.ap[:, g * 512:(g + 1) * 512], in0=yacc.ap, scalar=ssq.ap[:, g:g + 1], in1=nwb.ap[:, g * 512:(g + 1) * 512], op0=ALU.mult, op1=ALU.mult), reads=[yacc.b, ssq.b, nwb.b], writes=[yn.b])
                hooks[g]()
            dma("sp", ynd[ci * 128:(ci + 1) * 128, :], yn.ap, [yn.b], [])
        P.add("dve", lambda e: e.tensor_tensor(out=S(6), in0=S(4), in1=S(3), op=ALU.subtract), reads=[sm.b], writes=[sm.b])
        P.add("act", lambda e: e.activation(out=S(6), in_=S(6), func=AF.Exp), reads=[sm.b], writes=[sm.b])
        P.add("act", lambda e: e.activation(out=S(7), in_=S(4), func=AF.Exp), reads=[sm.b], writes=[sm.b])
        P.add("dve", lambda e: e.tensor_tensor(out=S(6), in0=S(6), in1=S(1), op=ALU.mult), reads=[sm.b], writes=[sm.b])
        for g in range(4):
            xg = xtok.ap[:, g * 512:(g + 1) * 512].rearrange("p (a b) -> p a b", a=8)
            v3 = lambda ap: ap.rearrange("p (a b) -> p a b", a=8)
            P.add("dve", lambda e, g=g, xg=xg: e.tensor_tensor(out=v3(xdts.ap), in0=xg, in1=bc(sm.ap[:, 6, 8 * g:8 * g + 8], 64), op=ALU.mult), reads=[xtok.b, sm.b], writes=[xdts.b])
            bank = pf[2 + g % 2]
            P.add("pe", lambda e, g=g, bank=bank: e.matmul(bank.ap, xtok.ap[:, 2048 + g * 128:2048 + (g + 1) * 128], xdts.ap, start=True, stop=True), reads=[xtok.b, xdts.b], writes=[bank.b])
            Hg = H.ap[:, g * 512:(g + 1) * 512]
            P.add("dve", lambda e, g=g, Hg=Hg: e.tensor_tensor(out=v3(Hg), in0=v3(Hg), in1=bc(sm.ap[:, 7, 8 * g:8 * g + 8], 64), op=ALU.mult), reads=[H.b, sm.b], writes=[H.b])
            P.add("dve", lambda e, Hg=Hg, bank=bank: e.tensor_tensor(out=Hg, in0=Hg, in1=bank.ap, op=ALU.add), reads=[H.b, bank.b], writes=[H.b])

    def fp(ci):
        b = ci % 2
        return front_pieces(ci, xmain[(ci + 1) * 128:(ci + 2) * 128, :], xbs[b], xTs[b], raws[b], xcs[b], raws[1 - b])
    nch = 0 if SKIP1 else NCH1
    if nch:
        for p_ in fp(0):
            p_()
    for ci in range(nch):
        nxt = fp(ci + 1) if ci + 1 < nch else [lambda: None] * 4
        ssd_back(flg.ap[:, NPRE + ci:NPRE + ci + 1], True, ci, xTs[ci % 2], xcs[ci % 2], nxt)

    if stop < 2:
        P.emit(nc); es.close(); return nc
    P.barrier()
    AB.off, AFa.off = markB, markF
    Wq = AB.get("Wq", 8, 1024); load_w(Wq, w_in[:, COL_Q:COL_Q + 1024], 1024)
    Wk2 = AB.get("Wk2", 8, 128); load_w(Wk2, w_in[:, COL_K:COL_K + 128], 128)
    Wv = AB.get("Wv", 8, 128); load_w(Wv, w_in[:, COL_V:COL_V + 128], 128)
    EB = AB.get("EB", 2, 16, 128)
    ebf = AFa.get("ebf", 2048); mkf = AFa.get("mkf", 2048)
    for kt in range(2):
        dma("sp", ebf.ap, biasg[:, kt * 2048:(kt + 1) * 2048], [], [ebf.b])
        dma("sp", mkf.ap, maskg[:, kt * 2048:(kt + 1) * 2048], [], [mkf.b])
        P.add("act", lambda e: e.activation(out=ebf.ap, in_=ebf.ap, func=AF.Exp), reads=[ebf.b], writes=[ebf.b])
        P.add("dve", lambda e, kt=kt: e.tensor_tensor(out=EB.ap[:, kt, :, :].rearrange("p a b -> p (a b)"), in0=ebf.ap, in1=mkf.ap, op=ALU.mult), reads=[ebf.b, mkf.b], writes=[EB.b])
    xb = AB.get("xb2", 1024); xT = AB.get("xT2", 8, 128)
    kT = [AB.get("kT%d" % i, 2, 128) for i in range(2)]
    vx = [AB.get("vx%d" % i, 2, 65) for i in range(2)]
    for i in range(2):
        P.add("pool", lambda e, i=i: e.memset(vx[i].ap, 1.0), writes=[vx[i].b])
    qT = AB.get("qT", 16, 128); et = AB.get("et", 4, 128)
    PT = [AB.get("PT%d" % i, 4, 128) for i in range(2)]
    ya = AB.get("ya", 1024)
    den = AFa.get("den", 4)
    flag0 = flg.ap[:, NPRE:NPRE + 1]
    CUT2 = int(os.environ.get('KCUT2', '99')); NCH2 = int(os.environ.get('KNCH2', str(NM2)))
    for ci in range(NCH2):
        sl = ci % 2
        load_xT(xmain[ci * 128:(ci + 1) * 128, :], xb, xT)
        for kv in range(2):
            chain(pf[0], pf[0].ap[0:64, kv * 128:(kv + 1) * 128], [(Wk2.ap[:, k, kv * 64:(kv + 1) * 64], xT.ap[:, k, :]) for k in range(8)], [Wk2.b, xT.b])
        P.add("act", lambda e, sl=sl: e.copy(out=kT[sl].ap[0:64, :, :], in_=pf[0].ap[0:64, 0:256].rearrange("p (a b) -> p a b", a=2)), reads=[pf[0].b], writes=[kT[sl].b])
        chain(pf[1], pf[1].ap[:, 0:128], [(xT.ap[:, k, :], Wv.ap[:, k, :]) for k in range(8)], [xT.b, Wv.b])
        P.add("dve", lambda e, sl=sl: e.tensor_copy(out=vx[sl].ap[:, :, 0:64], in_=pf[1].ap[:, 0:128].rearrange("p (a b) -> p a b", a=2)), reads=[pf[1].b], writes=[vx[sl].b])
        if ci == 0 or CUT2 <= 1:
            continue
        for q4 in range(4):
            bank = pf[2 + q4 % 2]
            for tt in range(4):
                j = q4 * 4 + tt
                chain(bank, bank.ap[0:64, tt * 128:(tt + 1) * 128], [(Wq.ap[:, k, j * 64:(j + 1) * 64], xT.ap[:, k, :]) for k in range(8)], [Wq.b, xT.b])
            P.add("act", lambda e, q4=q4, bank=bank: e.copy(out=qT.ap[0:64, q4 * 4:q4 * 4 + 4, :], in_=bank.ap[0:64, :].rearrange("p (a b) -> p a b", a=4)), reads=[bank.b], writes=[qT.b])
        for kvh in range(2):
            if CUT2 <= 2: break
            for hb in range(2):
                j0 = kvh * 8 + hb * 4
                for kt in range(2):
                    slk = (ci + 1 + kt) % 2
                    bank = pf[4 + kt]
                    for i in range(4):
                        j = j0 + i
                        base = (j % 2) * 64 * int(os.environ.get("KB64", "1"))
                        P.add("pe", lambda e, i=i, j=j, base=base, slk=slk, bank=bank, kvh=kvh: e.matmul(bank.ap[:, i * 128:(i + 1) * 128], kT[slk].ap[0:64, kvh, :], qT.ap[0:64, j, :], start=True, stop=True), reads=[kT[slk].b, qT.b], writes=[bank.b])
                    P.add("act", lambda e, bank=bank: e.activation(out=et.ap.rearrange("p a b -> p (a b)"), in_=bank.ap, func=AF.Exp, scale=0.125), reads=[bank.b], writes=[et.b])
                    if ci == 2 and kt == 0:
                        P.add("dve", lambda e, kt=kt, j0=j0: e.scalar_tensor_tensor(out=PT[kt].ap, in0=et.ap, scalar=flag0, in1=EB.ap[:, kt, j0:j0 + 4, :], op0=ALU.mult, op1=ALU.mult), reads=[et.b, EB.b, flg.b], writes=[PT[kt].b])
                    else:
                        P.add("dve", lambda e, kt=kt, j0=j0: e.tensor_tensor(out=PT[kt].ap, in0=et.ap, in1=EB.ap[:, kt, j0:j0 + 4, :], op=ALU.mult), reads=[et.b, EB.b], writes=[PT[kt].b])
                if CUT2 <= 3: continue
                bank = pf[hb]
                for i in range(4):
                    for kt in range(2):
                        slk = (ci + 1 + kt) % 2
                        P.add("pe", lambda e, i=i, kt=kt, slk=slk, bank=bank, kvh=kvh: e.matmul(bank.ap[:, i * 65:(i + 1) * 65], PT[kt].ap[:, i, :], vx[slk].ap[:, kvh, :], start=(kt == 0), stop=(kt == 1)), reads=[PT[kt].b, vx[slk].b], writes=[bank.b])
                if CUT2 <= 4: continue
                pv = bank.ap[:, 0:260].rearrange("p (a b) -> p a b", a=4)
                P.add("dve", lambda e, pv=pv, j0=j0: e.tensor_tensor(out=den.ap, in0=pv[:, :, 64], in1=esink[:, j0:j0 + 4], op=ALU.add), reads=[bank.b, smallp.b], writes=[den.b])
                P.add("dve", lambda e: e.reciprocal(out=den.ap, in_=den.ap), reads=[den.b], writes=[den.b])
                P.add("dve", lambda e, pv=pv, j0=j0: e.tensor_tensor(out=ya.ap[:, j0 * 64:(j0 + 4) * 64].rearrange("p (a b) -> p a b", a=4), in0=pv[:, :, 0:64], in1=bc(den.ap, 64), op=ALU.mult), reads=[bank.b, den.b], writes=[ya.b])
        dma("sp", yad[(ci - 1) * 128:ci * 128, :], ya.ap, [ya.b], [])

    if stop < 3:
        P.emit(nc); es.close(); return nc
    P.barrier()
    AB.off, AFa.off = markB, markF
    Wg = AB.get("Wg", 8, 2048); load_w(Wg, w_in[:, COL_G:COL_G + 2048], 2048)
    Wbs = AB.get("Wbs", 16, 1024); load_w(Wbs, w_bs, 1024)
    Wba = AB.get("Wba", 8, 1024); load_w(Wba, w_ba, 1024)
    Wmx = AB.get("Wmx", 8, 1024); load_w(Wmx, w_mix, 1024)
    bgb = AFa.get("bgb", 2048); dma("sp", bgb.ap, bcast_row(b_gate, 2048), [], [bgb.b])
    lng = AFa.get("lng", 2, 1024)
    dma("sp", lng.ap[:, 0, :], bcast_row(ln1g, 1024), [], [lng.b]); dma("sp", lng.ap[:, 1, :], bcast_row(ln1b, 1024), [], [lng.b])
    xb = AB.get("xb3", 1024); xT = AB.get("xT3", 8, 128)
    ynb = AB.get("ynb", 2048); yab = AB.get("yab", 1024)
    ynT = AB.get("ynT", 16, 128); yaT = AB.get("yaT", 8, 128)
    mg = AB.get("mg", 1024); mT = AB.get("mT", 8, 128)
    xf = AFa.get("xf", 1024); gt = AFa.get("gt", 2048); m1 = AFa.get("m1", 512); r = AFa.get("r", 1024)
    st = AFa.get("st", 2, 6); mv = AFa.get("mv", 2)

    def transp(src, dst, ntl):
        for bt in range(ntl // 8):
            for i in range(8):
                j = bt * 8 + i
                P.add("pe", lambda e, i=i, j=j: e.transpose(pb[1].ap[:, i * 128:(i + 1) * 128], src.ap[:, j * 128:(j + 1) * 128], identb), reads=[src.b, cb16.b], writes=[pb[1].b])
            P.add("act", lambda e, bt=bt: e.copy(out=dst.ap[:, bt * 8:bt * 8 + 8, :].rearrange("p a b -> p (a b)"), in_=pb[1].ap), reads=[pb[1].b], writes=[dst.b])

    def layer_norm(r, g_ap, b_ap, gb, st, mv):
        for i in range(2):
            P.add("dve", lambda e, i=i: e.bn_stats(out=st.ap[:, i, :], in_=r.ap[:, i * 512:(i + 1) * 512]), reads=[r.b], writes=[st.b])
        P.add("dve", lambda e: e.bn_aggr(out=mv.ap, in_=st.ap.rearrange("p a b -> p (a b)")), reads=[st.b], writes=[mv.b])
        P.add("dve", lambda e: e.tensor_scalar(out=mv.ap[:, 1:2], in0=mv.ap[:, 1:2], scalar1=1e-5, scalar2=None, op0=ALU.add), reads=[mv.b], writes=[mv.b])
        P.add("act", lambda e: e.activation(out=mv.ap[:, 1:2], in_=mv.ap[:, 1:2], func=AF.Sqrt), reads=[mv.b], writes=[mv.b])
        P.add("dve", lambda e: e.reciprocal(out=mv.ap[:, 1:2], in_=mv.ap[:, 1:2]), reads=[mv.b], writes=[mv.b])
        P.add("dve", lambda e: e.tensor_scalar(out=r.ap, in0=r.ap, scalar1=mv.ap[:, 0:1], scalar2=mv.ap[:, 1:2], op0=ALU.subtract, op1=ALU.mult), reads=[r.b, mv.b], writes=[r.b])
        P.add("dve", lambda e: e.tensor_tensor(out=r.ap, in0=r.ap, in1=g_ap, op=ALU.mult), reads=[r.b, gb], writes=[r.b])
        P.add("dve", lambda e: e.tensor_tensor(out=r.ap, in0=r.ap, in1=b_ap, op=ALU.add), reads=[r.b, gb], writes=[r.b])

    for ci in range(NM1):
        xrow = xmain[(ci + 1) * 128:(ci + 2) * 128, :]
        load_xT(xrow, xb, xT)
        dma("sp", xf.ap, xrow, [], [xf.b])
        dma("sp", ynb.ap, ynd[ci * 128:(ci + 1) * 128, :], [], [ynb.b])
        dma("sp", yab.ap, yad[ci * 128:(ci + 1) * 128, :], [], [yab.b])
        transp(ynb, ynT, 16)
        transp(yab, yaT, 8)
        for s4 in range(4):
            bank = pf[s4 % 2]
            chain(bank, bank.ap, [(xT.ap[:, k, :], Wg.ap[:, k, s4 * 512:(s4 + 1) * 512]) for k in range(8)], [xT.b, Wg.b])
            P.add("dve", lambda e, s4=s4, bank=bank: e.tensor_tensor(out=gt.ap[:, s4 * 512:(s4 + 1) * 512], in0=bank.ap, in1=bgb.ap[:, s4 * 512:(s4 + 1) * 512], op=ALU.add), reads=[bank.b, bgb.b], writes=[gt.b])
        P.add("act", lambda e: e.activation(out=gt.ap, in_=gt.ap, func=AF.Sigmoid), reads=[gt.b], writes=[gt.b])
        for hf in range(2):
            chain(pf[2], pf[2].ap, [(ynT.ap[:, i, :], Wbs.ap[:, i, hf * 512:(hf + 1) * 512]) for i in range(16)], [ynT.b, Wbs.b])
            chain(pf[3], pf[3].ap, [(yaT.ap[:, i, :], Wba.ap[:, i, hf * 512:(hf + 1) * 512]) for i in range(8)], [yaT.b, Wba.b])
            P.add("dve", lambda e, hf=hf: e.tensor_tensor(out=m1.ap, in0=pf[2].ap, in1=gt.ap[:, hf * 512:(hf + 1) * 512], op=ALU.mult), reads=[pf[2].b, gt.b], writes=[m1.b])
            P.add("dve", lambda e, hf=hf: e.tensor_tensor(out=r.ap[:, hf * 512:(hf + 1) * 512], in0=pf[3].ap, in1=gt.ap[:, 1024 + hf * 512:1024 + (hf + 1) * 512], op=ALU.mult), reads=[pf[3].b, gt.b], writes=[r.b])
            P.add("dve", lambda e, hf=hf: e.tensor_tensor(out=mg.ap[:, hf * 512:(hf + 1) * 512], in0=m1.ap, in1=r.ap[:, hf * 512:(hf + 1) * 512], op=ALU.add), reads=[m1.b, r.b], writes=[mg.b])
        transp(mg, mT, 8)
        for hf in range(2):
            bank = pf[4 + hf]
            chain(bank, bank.ap, [(mT.ap[:, i, :], Wmx.ap[:, i, hf * 512:(hf + 1) * 512]) for i in range(8)], [mT.b, Wmx.b])
            P.add("dve", lambda e, hf=hf, bank=bank: e.scalar_tensor_tensor(out=r.ap[:, hf * 512:(hf + 1) * 512], in0=xf.ap[:, hf * 512:(hf + 1) * 512], scalar=ALPHA, in1=bank.ap, op0=ALU.mult, op1=ALU.add), reads=[xf.b, bank.b], writes=[r.b])
        layer_norm(r, lng.ap[:, 0, :], lng.ap[:, 1, :], lng.b, st, mv)
        if ci == 0:
            P.add("dve", lambda e: e.tensor_scalar(out=r.ap, in0=r.ap, scalar1=flag0, scalar2=None, op0=ALU.mult), reads=[r.b, flg.b], writes=[r.b])
        dma("sp", h1d[ci * 128:(ci + 1) * 128, :], r.ap, [r.b], [])

    if stop < 4:
        P.emit(nc); es.close(); return nc
    P.barrier()
    AB.off, AFa.off = markB, markF
    Wup = AB.get("Wup", 8, 5632); load_w(Wup, w_up, 5632)
    Wdn = AB.get("Wdn", 22, 1024); load_w(Wdn, w_dn, 1024)
    fw = AFa.get("fw", 132); fb = AFa.get("fb", 44)
    per_channel(fw.ap[:, 0:88], fcw[0:88, :], 88); fw.b.writer = P.ops["dve"][-1]
    per_channel(fw.ap[:, 88:132], fcw[88:132, :], 44); fw.b.writer = P.ops["dve"][-1]
    per_channel(fb.ap, fcb, 44); fb.b.writer = P.ops["dve"][-1]
    lng2 = AFa.get("lng2", 2, 1024)
    dma("sp", lng2.ap[:, 0, :], bcast_row(ln2g, 1024), [], [lng2.b]); dma("sp", lng2.ap[:, 1, :], bcast_row(ln2b, 1024), [], [lng2.b])
    SC = 256
    hA = AB.get("hA", 1024); hB = AB.get("hB", 1024); h1T = AB.get("h1T", 8, SC + 2)
    aT = AB.get("aT", 22, SC)
    aTb = [Buf("aTb%d" % i) for i in range(22)]
    cvs = [AFa.get("cvs%d" % i, SC) for i in range(4)]
    t0s = [AFa.get("t0s%d" % i, SC) for i in range(2)]
    sg = AFa.get("sg", SC)
    hr = AFa.get("hr", 1024); r4 = AFa.get("r4", 1024)
    st4 = AFa.get("st4", 2, 6); mv4 = AFa.get("mv4", 2)
    NT = SC // 128
    tcount = 0
    for sc in range(TOK // SC):
        r0 = 128 + sc * SC
        for i in range(NT):
            dma("pool", hA.ap, h1d[r0 - 2 + i * 128:r0 + 126 + i * 128, :], [], [hA.b])
            for k in range(8):
                P.add("pe", lambda e, k=k: e.transpose(pb[0].ap[:, k * 128:(k + 1) * 128], hA.ap[:, k * 128:(k + 1) * 128], identb), reads=[hA.b, cb16.b], writes=[pb[0].b])
            P.add("act", lambda e, i=i: e.copy(out=h1T.ap[:, :, i * 128:(i + 1) * 128], in_=pb[0].ap.rearrange("p (a b) -> p a b", a=8)), reads=[pb[0].b], writes=[h1T.b])
        dma("pool", hB.ap[0:2, :], h1d[r0 + SC - 2:r0 + SC, :], [], [hB.b])
        for k in range(8):
            P.add("pe", lambda e, k=k: e.transpose(pb[1].ap[:, k * 2:(k + 1) * 2], hB.ap[0:2, k * 128:(k + 1) * 128], identb[0:2, 0:2]), reads=[hB.b, cb16.b], writes=[pb[1].b])
        P.add("act", lambda e: e.copy(out=h1T.ap[:, :, SC:SC + 2], in_=pb[1].ap[:, 0:16].rearrange("p (a b) -> p a b", a=8)), reads=[pb[1].b], writes=[h1T.b])
        def down(jp):
            for tt in range(NT):
                for hf in range(2):
                    bank = pf[2 + tt * 2 + hf]
                    P.add("pe", lambda e, jp=jp, tt=tt, hf=hf, bank=bank: e.matmul(bank.ap, aT.ap[:, jp, tt * 128:(tt + 1) * 128], Wdn.ap[:, jp, hf * 512:(hf + 1) * 512], start=(jp == 0), stop=(jp == 21)), reads=[aTb[jp], Wdn.b], writes=[bank.b])

        for jp in range(22):
            for gv in range(2):
                j = jp + 22 * gv
                X = pf[tcount % 2]; t0 = t0s[tcount % 2]
                cv = cvs[(jp % 2) * 2 + gv]
                tcount += 1
                chain(X, X.ap[:, 0:SC + 2], [(Wup.ap[:, k, j * 128:(j + 1) * 128], h1T.ap[:, k, 0:SC + 2]) for k in range(8)], [Wup.b, h1T.b])
                w0, w1, w2, bb = fw.ap[:, j:j + 1], fw.ap[:, 44 + j:45 + j], fw.ap[:, 88 + j:89 + j], fb.ap[:, j:j + 1]
                P.add("act", lambda e, X=X, t0=t0, w2=w2, bb=bb: e.activation(out=t0.ap, in_=X.ap[:, 2:SC + 2], func=AF.Identity, bias=bb, scale=w2), reads=[X.b, fw.b, fb.b], writes=[t0.b])
                P.add("dve", lambda e, X=X, t0=t0, w1=w1: e.scalar_tensor_tensor(out=t0.ap, in0=X.ap[:, 1:SC + 1], scalar=w1, in1=t0.ap, op0=ALU.mult, op1=ALU.add), reads=[X.b, fw.b, t0.b], writes=[t0.b])
                P.add("dve", lambda e, X=X, t0=t0, w0=w0, cv=cv: e.scalar_tensor_tensor(out=cv.ap, in0=X.ap[:, 0:SC], scalar=w0, in1=t0.ap, op0=ALU.mult, op1=ALU.add), reads=[X.b, fw.b, t0.b], writes=[cv.b])
            cg, cvv = cvs[(jp % 2) * 2], cvs[(jp % 2) * 2 + 1]
            P.add("act", lambda e, cg=cg: e.activation(out=sg.ap, in_=cg.ap, func=AF.Silu), reads=[cg.b], writes=[sg.b])
            P.add("dve", lambda e, jp=jp, cvv=cvv: e.tensor_tensor(out=aT.ap[:, jp, :], in0=sg.ap, in1=cvv.ap, op=ALU.mult), reads=[sg.b, cvv.b], writes=[aTb[jp]])
            if jp >= 1:
                down(jp - 1)
        down(21)
        for tt in range(NT):
            rr = r0 + tt * 128
            dma("sp", hr.ap, h1d[rr:rr + 128, :], [], [hr.b])
            for hf in range(2):
                bank = pf[2 + tt * 2 + hf]
                P.add("dve", lambda e, hf=hf, bank=bank: e.scalar_tensor_tensor(out=r4.ap[:, hf * 512:(hf + 1) * 512], in0=hr.ap[:, hf * 512:(hf + 1) * 512], scalar=ALPHA, in1=bank.ap, op0=ALU.mult, op1=ALU.add), reads=[hr.b, bank.b], writes=[r4.b])
            layer_norm(r4, lng2.ap[:, 0, :], lng2.ap[:, 1, :], lng2.b, st4, mv4)
            dma("sp", out[rr - 128:rr, :], r4.ap, [r4.b], [])

    P.emit(nc)
    es.close()
    return nc


def rel_bucket_np(rel):
    n = np.maximum(rel, 0)
    nf = np.maximum(n, 1).astype(np.float32)
    large = 16 + (np.log(nf / np.float32(16)) / np.float32(np.log(128 / 16)) * np.float32(16)).astype(np.int32)
    large = np.minimum(large, 31)
    return np.where(n < 16, n, large)


_NC = None


def kernel(_dbg=None, **inp):
    global _NC
    x = np.asarray(inp["x"], np.float32)[0]
    f = lambda k: np.ascontiguousarray(np.asarray(inp[k], np.float32)[0])
    common = {
        "w_in": f("w_in"), "b_gate": f("b_gate")[None], "dtb": f("ssm_dt_bias")[None], "alog": f("ssm_a_log")[None],
        "dsk": f("ssm_d")[None], "normw": f("ssm_norm_w")[None], "sinks": f("attn_sinks")[None],
        "w_bs": f("w_branch_ssm"), "w_ba": f("w_branch_attn"), "w_mix": f("w_mix_out"),
        "ln1g": f("ln1_g")[None], "ln1b": f("ln1_b")[None], "ln2g": f("ln2_g")[None], "ln2b": f("ln2_b")[None],
        "w_up": f("w_up"), "w_dn": f("w_down"),
    }
    scw = f("ssm_conv_w")
    common["scw"] = np.ascontiguousarray(scw.reshape(4 * 24, 128))
    common["scb"] = np.ascontiguousarray(f("ssm_conv_b").reshape(24, 128))
    common["fcw"] = np.ascontiguousarray(f("ffn_conv_w").reshape(3 * 44, 128))
    common["fcb"] = np.ascontiguousarray(f("ffn_conv_b").reshape(44, 128))
    s = np.arange(128)
    ident = np.eye(128, dtype=np.float32)
    triU = (s[:, None] <= s[None, :]).astype(np.float32)
    ustr = (s[:, None] > s[None, :]).astype(np.float32)
    common["cst"] = np.ascontiguousarray(np.concatenate([ident, triU, ustr, np.ones((128, 128), np.float32)], axis=1))
    rb = np.asarray(inp["rel_bias"], np.float32)
    bg = np.zeros((128, 2, 16, 128), np.float32); mk = np.zeros((128, 2, 16, 128), np.float32)
    for kt in range(2):
        rel = (s[None, :] + 128) - (s[:, None] + 128 * kt)
        valid = (rel >= 0) & (rel < 128)
        bidx = rel_bucket_np(rel)
        g = rb[bidx]
        bg[:, kt] = np.transpose(g, (0, 2, 1))
        mk[:, kt] = np.broadcast_to(valid[:, None, :], (128, 16, 128))
    common["biasg"] = np.ascontiguousarray(bg.reshape(128, -1)); common["maskg"] = np.ascontiguousarray(mk.reshape(128, -1))
    in_maps = []
    for c in range(NCORE):
        S = c * TOK
        lo = S - 128 - NPRE * 128
        xp = np.zeros((NPRE * 128, 1024), np.float32)
        if S - 128 > 0:
            src_lo = max(lo, 0)
            xp[src_lo - lo:] = x[src_lo:S - 128]
        xm = np.zeros((NM2 * 128, 1024), np.float32)
        lo2 = S - 256
        src_lo = max(lo2, 0)
        xm[src_lo - lo2:] = x[src_lo:S + TOK]
        fl = np.zeros((128, NPRE + NM1), np.float32)
        for i in range(NPRE):
            fl[:, i] = 1.0 if lo + i * 128 >= 0 else 0.0
        fl[:, NPRE] = 1.0 if c > 0 else 0.0
        fl[:, NPRE + 1:] = 1.0
        m = dict(common); m["xpre"] = xp; m["xmain"] = xm; m["pflag"] = fl
        in_maps.append(m)
    if _dbg is not None:
        return in_maps
    if _NC is None:
        _NC = build()
    res = run_bass_kernel_spmd(_NC, in_maps, core_ids=list(range(NCORE)))
    o = np.concatenate([res.results[c]["out"] for c in range(NCORE)], axis=0)
    return o[None].astype(np.float32)
```

```python
import numpy as np
import concourse.bass as bass
import concourse.mybir as mybir

ENGS = ["pe", "act", "dve", "pool", "sp"]
N_DMA_SEM = 16
import os as _os
SAME_ENGINE_SYNC = _os.environ.get("KSES", "1") == "1"


class Buf:
    __slots__ = ("name", "writer", "readers", "dma_readers", "excl")

    def __init__(self, name, excl=False):
        self.name = name
        self.excl = excl
        self.writer = None
        self.readers = {}
        self.dma_readers = []


class Op:
    __slots__ = ("eng", "fn", "idx", "waits", "signal", "is_dma", "dsem", "dtarget", "clock", "sigcount", "uid")


class Prog:
    def __init__(self):
        self.ops = {e: [] for e in ENGS}
        self.clock = {e: {f: 0 for f in ENGS} for e in ENGS}
        self.known_dma = {e: set() for e in ENGS}
        self.dma_sem_count = [0] * N_DMA_SEM
        self.dma_sem_last = [None] * N_DMA_SEM
        self.dma_rr = 0
        self.n_dma = 0
        self.uid = 0
        self.bar = {e: [] for e in ENGS}

    def barrier(self):
        lasts = [self.ops[e][-1] for e in ENGS if self.ops[e] and not self.ops[e][-1].is_dma]
        for e in ENGS:
            pass
        lasts = []
        for e in ENGS:
            for op in reversed(self.ops[e]):
                if not op.is_dma:
                    lasts.append(op)
                    break
        dmas = [op for op in self.dma_sem_last if op is not None]
        for e in ENGS:
            self.bar[e] = lasts + dmas

    def add(self, eng, fn, reads=(), writes=(), dma=False):
        op = Op()
        op.eng = eng
        op.fn = fn
        op.idx = len(self.ops[eng])
        op.waits = []
        op.signal = False
        op.is_dma = dma
        op.uid = self.uid
        self.uid += 1
        def _flat(bs):
            o = []
            for b in bs:
                if isinstance(b, (list, tuple)):
                    o.extend(b)
                else:
                    o.append(b)
            return o
        reads = _flat(reads)
        writes = _flat(writes)
        deps = []
        for b in reads:
            if b.writer is not None:
                deps.append(b.writer)
            if b.excl:
                for e2, r in b.readers.items():
                    if e2 != eng:
                        deps.append(r)
        for b in writes:
            if b.writer is not None:
                deps.append(b.writer)
            deps.extend(b.readers.values())
            deps.extend(b.dma_readers)
        if self.bar[eng]:
            deps.extend(self.bar[eng])
            self.bar[eng] = []
        clk = self.clock[eng]
        seen = set()
        for d in deps:
            if d.uid in seen:
                continue
            seen.add(d.uid)
            if d.is_dma:
                if d.uid not in self.known_dma[eng]:
                    op.waits.append(("dma", d.dsem, d.dtarget))
                    self.known_dma[eng].add(d.uid)
            else:
                if d.eng == eng and (eng == "pe" or not SAME_ENGINE_SYNC):
                    continue
                if clk[d.eng] < d.idx + 1:
                    op.waits.append(("eng", d))
                    d.signal = True
                    for f in ENGS:
                        if d.clock[f] > clk[f]:
                            clk[f] = d.clock[f]
                    if clk[d.eng] < d.idx + 1:
                        clk[d.eng] = d.idx + 1
        if dma and eng == "pool":
            pd = self.__dict__.setdefault("pool_dmas", [])
            if len(pd) >= 4:
                d = pd[-4]
                if d.uid not in self.known_dma[eng]:
                    op.waits.append(("dma", d.dsem, d.dtarget))
                    self.known_dma[eng].add(d.uid)
            pd.append(op)
        if dma:
            k = self.dma_rr
            self.dma_rr = (self.dma_rr + 1) % N_DMA_SEM
            prev = self.dma_sem_last[k]
            if prev is not None and prev.uid not in self.known_dma[eng]:
                op.waits.append(("dma", k, prev.dtarget))
                self.known_dma[eng].add(prev.uid)
            self.dma_sem_count[k] += 16
            op.dsem = k
            op.dtarget = self.dma_sem_count[k]
            self.dma_sem_last[k] = op
            self.n_dma += 1
        op.clock = dict(clk)
        if not SAME_ENGINE_SYNC or eng == "pe":
            pass
        self.ops[eng].append(op)
        for b in writes:
            b.writer = op
            b.readers = {}
            b.dma_readers = []
        for b in reads:
            if dma:
                b.dma_readers.append(op)
            else:
                b.readers[eng] = op
        return op

    def emit(self, nc, final_waits=True):
        for e in ENGS:
            c = 0
            for op in self.ops[e]:
                if op.signal:
                    c += 1
                op.sigcount = c
        from contextlib import ExitStack
        with ExitStack() as es:
            esem = {e: es.enter_context(nc.semaphore("s_" + e)) for e in ENGS}
            dsem = [es.enter_context(nc.semaphore("d%d" % i)) for i in range(N_DMA_SEM)]
            block = es.enter_context(nc.Block())
            last_dma = [op for op in self.dma_sem_last if op is not None]

            def run(e, h):
                for op in self.ops[e]:
                    for w in op.waits:
                        if w[0] == "dma":
                            h.wait_ge(dsem[w[1]], w[2])
                        else:
                            h.wait_ge(esem[w[1].eng], w[1].sigcount)
                    ins = op.fn(h)
                    if op.is_dma:
                        ins.then_inc(dsem[op.dsem], 16)
                    elif op.signal:
                        ins.then_inc(esem[e], 1)
                if e == "sp" and final_waits:
                    for k in range(N_DMA_SEM):
                        if self.dma_sem_count[k] > 0:
                            h.wait_ge(dsem[k], self.dma_sem_count[k])

            @block.tensor
            def _(h):
                run("pe", h)

            @block.scalar
            def _(h):
                run("act", h)

            @block.vector
            def _(h):
                run("dve", h)

            @block.gpsimd
            def _(h):
                run("pool", h)

            @block.sync
            def _(h):
                run("sp", h)

from contextlib import ExitStack
import ml_dtypes
from concourse.bass_utils import run_bass_kernel_spmd

F32 = mybir.dt.float32
BF16 = mybir.dt.bfloat16
AF = mybir.ActivationFunctionType
ALU = mybir.AluOpType

NCORE = 8
TOK = 2048
NPRE = 112
NM1 = 17
NM2 = 18
ALPHA = 2.0 ** 0.25
COL_Z, COL_X, COL_B, COL_C, COL_DT, COL_Q, COL_K, COL_V, COL_G = 0, 2048, 4096, 4608, 5120, 5152, 6176, 6304, 6432


class TT:
    def __init__(self, ap, name, excl=False):
        self.ap = ap
        self.b = Buf(name, excl)


class Arena:
    def __init__(self, t, n):
        self.t, self.n, self.off = t, n, 0

    def get(self, name, *fs):
        n = int(np.prod(fs))
        ap = self.t[:, self.off:self.off + n]
        self.off += n
        assert self.off <= self.n, (name, self.off, self.n)
        if len(fs) == 2:
            ap = ap.rearrange("p (a b) -> p a b", a=fs[0])
        elif len(fs) == 3:
            ap = ap.rearrange("p (a b c) -> p a b c", a=fs[0], b=fs[1])
        return TT(ap, name)


def bc(ap2, n):
    return ap2.unsqueeze(2).to_broadcast([ap2.shape[0], ap2.shape[1], n])


def build(stop=4, npre=NPRE, dbg=False):
    nc = bass.Bass("TRN2", target_bir_lowering=False)
    dt_in = lambda n, s: nc.dram_tensor(n, s, F32, kind="ExternalInput").ap()
    xpre = dt_in("xpre", [NPRE * 128, 1024])
    xmain = dt_in("xmain", [NM2 * 128, 1024])
    pflag = dt_in("pflag", [128, NPRE + NM1])
    w_in = dt_in("w_in", [1024, 8480])
    b_gate = dt_in("b_gate", [1, 2048])
    scw = dt_in("scw", [96, 128])
    scb = dt_in("scb", [24, 128])
    dtb = dt_in("dtb", [1, 32])
    alog = dt_in("alog", [1, 32])
    dsk = dt_in("dsk", [1, 32])
    normw = dt_in("normw", [1, 2048])
    sinks = dt_in("sinks", [1, 16])
    w_bs = dt_in("w_bs", [2048, 1024])
    w_ba = dt_in("w_ba", [1024, 1024])
    w_mix = dt_in("w_mix", [1024, 1024])
    ln1g = dt_in("ln1g", [1, 1024]); ln1b = dt_in("ln1b", [1, 1024])
    ln2g = dt_in("ln2g", [1, 1024]); ln2b = dt_in("ln2b", [1, 1024])
    w_up = dt_in("w_up", [1024, 5632])
    fcw = dt_in("fcw", [132, 128])
    fcb = dt_in("fcb", [44, 128])
    w_dn = dt_in("w_dn", [2816, 1024])
    cst = dt_in("cst", [128, 4 * 128])
    biasg = dt_in("biasg", [128, 2 * 16 * 128])
    maskg = dt_in("maskg", [128, 2 * 16 * 128])
    out = nc.dram_tensor("out", [TOK, 1024], F32, kind="ExternalOutput").ap()
    skind = "ExternalOutput" if dbg else "Internal"
    ynd = nc.dram_tensor("ynd", [NM1 * 128, 2048], BF16, kind=skind).ap()
    yad = nc.dram_tensor("yad", [NM1 * 128, 1024], BF16, kind=skind).ap()
    h1d = nc.dram_tensor("h1d", [NM1 * 128, 1024], F32, kind=skind).ap()

    P = Prog()
    es = ExitStack()
    NB, NF = 164 * 512, 40 * 256
    ABt = es.enter_context(nc.sbuf_tensor("AB", [128, NB], BF16))
    AFt = es.enter_context(nc.sbuf_tensor("AF", [128, NF], F32))
    pf = [TT(es.enter_context(nc.psum_tensor("pf%d" % i, [128, 512], F32))[:], "pf%d" % i, True) for i in range(6)]
    for t_ in pf:
        t_.b = [Buf(t_.b.name + "q%d" % q_, True) for q_ in range(4)]
    pb = [TT(es.enter_context(nc.psum_tensor("pb%d" % i, [128, 1024], BF16))[:], "pb%d" % i, True) for i in range(2)]
    AB = Arena(ABt, NB)
    AFa = Arena(AFt, NF)

    def bcast_row(src, n):
        return bass.AP(src.tensor, 0, [[0, 128], [1, n]])

    def dma(eng, o, i, reads, writes):
        P.add(eng, lambda e, o=o, i=i: e.dma_start(out=o, in_=i), reads=reads, writes=writes, dma=True)

    cf = AFa.get("cf", 4, 128)
    dma("sp", cf.ap, cst.rearrange("p (a b) -> p a b", a=4), [], [cf.b])
    identf, triU, Ustr, onesf = cf.ap[:, 0, :], cf.ap[:, 1, :], cf.ap[:, 2, :], cf.ap[:, 3, :]
    cb16 = AB.get("cb16", 4, 128)
    dma("pool", cb16.ap, cst.rearrange("p (a b) -> p a b", a=4), [], [cb16.b])
    identb, maskb = cb16.ap[:, 0, :], cb16.ap[:, 1, :]
    flg = AFa.get("flg", NPRE + NM1)
    dma("sp", flg.ap, pflag, [], [flg.b])
    smallp = AFa.get("smallp", 6, 32)
    dma("sp", smallp.ap[:, 0, :], bcast_row(dtb, 32), [], [smallp.b])
    dma("sp", smallp.ap[:, 1, :], bcast_row(alog, 32), [], [smallp.b])
    dma("sp", smallp.ap[:, 2, :], bcast_row(dsk, 32), [], [smallp.b])
    dma("sp", smallp.ap[:, 3, 0:16], bcast_row(sinks, 16), [], [smallp.b])
    P.add("act", lambda e: e.activation(out=smallp.ap[:, 1, :], in_=smallp.ap[:, 1, :], func=AF.Exp), reads=[smallp.b], writes=[smallp.b])
    P.add("dve", lambda e: e.tensor_scalar(out=smallp.ap[:, 1, :], in0=smallp.ap[:, 1, :], scalar1=-1.0, scalar2=None, op0=ALU.mult), reads=[smallp.b], writes=[smallp.b])
    P.add("act", lambda e: e.activation(out=smallp.ap[:, 3, 0:16], in_=smallp.ap[:, 3, 0:16], func=AF.Exp), reads=[smallp.b], writes=[smallp.b])
    dtb_bc, a_bc, D_bc, esink = smallp.ap[:, 0, :], smallp.ap[:, 1, :], smallp.ap[:, 2, :], smallp.ap[:, 3, 0:16]
    onecol = onesf[:, 0:1]
    rawh = AB.get("rawh", 24, 4)
    markB, markF = AB.off, AFa.off
    H = AFa.get("H", 2048)

    def load_w(dst, src, ncols, nk=8):
        nk = src.shape[0] // 128
        for c0 in range(0, ncols, 512):
            c1 = min(c0 + 512, ncols)
            for k in range(nk):
                dma("pool", dst.ap[:, k, c0:c1], src[k * 128:(k + 1) * 128, c0:c1], [], [dst.b])

    def load_xT(xrow_ap, xb, xT):
        dma("pool", xb.ap, xrow_ap, [], [xb.b])
        for k in range(8):
            P.add("pe", lambda e, k=k: e.transpose(pb[0].ap[:, k * 128:(k + 1) * 128], xb.ap[:, k * 128:(k + 1) * 128], identb), reads=[xb.b, cb16.b], writes=[pb[0].b])
        P.add("act", lambda e: e.copy(out=xT.ap.rearrange("p a b -> p (a b)"), in_=pb[0].ap), reads=[pb[0].b], writes=[xT.b])

    def chain(bank, out_ap, pairs, reads, wb=None):
        n = len(pairs)
        wb = [bank.b] if wb is None else wb
        for i, (l, r) in enumerate(pairs):
            P.add("pe", lambda e, l=l, r=r, i=i: e.matmul(out_ap, l, r, start=(i == 0), stop=(i == n - 1)), reads=reads, writes=wb)

    def per_channel(dst, src_dram, rows):
        tmp = AFa.get("pc_tmp", 128)
        dma("sp", tmp.ap[0:rows, :], src_dram, [], [tmp.b])
        P.add("pe", lambda e: e.transpose(pf[5].ap[:, 0:rows], tmp.ap[0:rows, :], identf[0:rows, 0:rows]), reads=[tmp.b, cf.b], writes=[pf[5].b])
        P.add("dve", lambda e: e.tensor_copy(out=dst, in_=pf[5].ap[:, 0:rows]), reads=[pf[5].b], writes=[])

    import os
    SKIP1 = int(os.environ.get('KSKIP1', '0'))
    Wx = AB.get("Wx", 8, 3072); load_w(Wx, w_in[:, COL_X:COL_X + 3072], 3072)
    Wdt = AB.get("Wdt", 8, 32); load_w(Wdt, w_in[:, COL_DT:COL_DT + 32], 32)
    cwt = AFa.get("cwt", 96); cbt = AFa.get("cbt", 24)
    per_channel(cwt.ap, scw, 96)
    per_channel(cbt.ap, scb, 24)
    cwt.b.writer = P.ops["dve"][-2]; cbt.b.writer = P.ops["dve"][-1]
    diag = AB.get("diag", 24, 4, 128)
    for j in range(24):
        for tp in range(4):
            P.add("dve", lambda e, j=j, tp=tp: e.tensor_scalar(out=diag.ap[:, j, tp, :], in0=identf, scalar1=cwt.ap[:, tp * 24 + j:tp * 24 + j + 1], scalar2=None, op0=ALU.mult), reads=[cf.b, cwt.b], writes=[diag.b])
    P.add("dve", lambda e: e.memset(H.ap, 0.0), writes=[H.b])
    mark1B, mark1F = AB.off, AFa.off
    xbA = [AB.get("xbA%d" % i, 1024) for i in range(2)]
    xT4 = [AB.get("xT4_%d" % i, 8, 512) for i in range(2)]
    raw4 = AB.get("raw4", 24, 516); xc4 = AB.get("xc4", 24, 512)
    rawb = [Buf("rawb%d" % j) for j in range(24)]; xcb = [Buf("xcb%d" % j) for j in range(24)]
    xtk = [AB.get("xtk%d" % i, 2560) for i in range(2)]
    xdtsA4 = [AB.get("xdtsA%d" % i, 512) for i in range(8)]
    P.add("pool", lambda e: e.memset(raw4.ap, 0.0), writes=rawb)

    def load_group(gi, buf):
        for q in range(4):
            c = gi * 4 + q
            xb_ = xbA[q % 2]
            dma("pool", xb_.ap, xpre[c * 128:(c + 1) * 128, :], [], [xb_.b])
            for k in range(8):
                P.add("pe", lambda e, k=k, xb_=xb_: e.transpose(pb[0].ap[:, k * 128:(k + 1) * 128], xb_.ap[:, k * 128:(k + 1) * 128], identb), reads=[xb_.b, cb16.b], writes=[pb[0].b])
            P.add("act", lambda e, q=q, buf=buf: e.copy(out=xT4[buf].ap[:, :, q * 128:(q + 1) * 128], in_=pb[0].ap.rearrange("p (a b) -> p a b", a=8)), reads=[pb[0].b], writes=[xT4[buf].b])

    def proj_in(gi, buf, last, j0, j1):
        ntile = 24 if last else 20
        for j in range(j0, min(j1, ntile)):
            bank = pf[j % 2]
            chain(bank, bank.ap, [(Wx.ap[:, k, j * 128:(j + 1) * 128], xT4[buf].ap[:, k, :]) for k in range(8)], [Wx.b, xT4[buf].b])
            P.add("act", lambda e, j=j, bank=bank: e.copy(out=raw4.ap[:, j, 3:515], in_=bank.ap), reads=[bank.b], writes=[rawb[j]])

    def proj_conv(gi, last):
        ntile = 24 if last else 20
        for j in range(ntile):
            bank = pf[2 + j % 2]
            chain(bank, bank.ap, [(diag.ap[:, j, tp, :], raw4.ap[:, j, tp:tp + 512]) for tp in range(4)], [diag.b, rawb[j]])
            P.add("act", lambda e, j=j, bank=bank: e.activation(out=xc4.ap[:, j, :], in_=bank.ap, func=AF.Silu, bias=cbt.ap[:, j:j + 1], scale=1.0), reads=[bank.b, cbt.b], writes=[xcb[j]])
        P.add("pool", lambda e: e.tensor_copy(out=raw4.ap[:, :, 0:3], in_=raw4.ap[:, :, 512:515]), reads=rawb, writes=rawb)

    smGs = [AFa.get("smG%d" % i, 8, 128) for i in range(2)]

    def group_chunks(gi, buf, hooks):
        c0 = gi * 4
        smG = smGs[gi % 2]
        S = lambda i: smG.ap[:, i, :]
        S3 = lambda i: smG.ap[:, i, :].rearrange("p (q h) -> p q h", q=4)
        b4 = lambda ap: ap.unsqueeze(1).to_broadcast([128, 4, 32])
        v3 = lambda ap: ap.rearrange("p (a b) -> p a b", a=8)
        for q in range(4):
            chain(pf[4], pf[4].ap[:, q * 32:(q + 1) * 32], [(xT4[buf].ap[:, k, q * 128:(q + 1) * 128], Wdt.ap[:, k, :]) for k in range(8)], [xT4[buf].b, Wdt.b])
        P.add("dve", lambda e: e.tensor_tensor(out=S3(0), in0=pf[4].ap[:, 0:128].rearrange("p (q h) -> p q h", q=4), in1=b4(dtb_bc), op=ALU.add), reads=[pf[4].b, smallp.b], writes=[smG.b])
        P.add("act", lambda e: e.activation(out=S(0), in_=S(0), func=AF.Exp), reads=[smG.b], writes=[smG.b])
        P.add("act", lambda e: e.activation(out=S(1), in_=S(0), func=AF.Ln, bias=onecol, scale=1.0), reads=[smG.b, cf.b], writes=[smG.b])
        P.add("dve", lambda e: e.tensor_tensor(out=S3(1), in0=S3(1), in1=bc(flg.ap[:, c0:c0 + 4], 32), op=ALU.mult), reads=[smG.b, flg.b], writes=[smG.b])
        P.add("dve", lambda e: e.tensor_tensor(out=S3(2), in0=S3(1), in1=b4(a_bc), op=ALU.mult), reads=[smG.b, smallp.b], writes=[smG.b])

        def tr(q):
            xt = xtk[q % 2]
            for bt in range(3):
                nt = 8 if bt < 2 else 4
                pbk = pb[bt % 2]
                for i in range(nt):
                    jj = bt * 8 + i
                    P.add("pe", lambda e, i=i, jj=jj, q=q, pbk=pbk: e.transpose(pbk.ap[:, i * 128:(i + 1) * 128], xc4.ap[:, jj, q * 128:(q + 1) * 128], identb), reads=[xcb[jj], cb16.b], writes=[pbk.b])
                if bt == 1:
                    P.add("act", lambda e, bt=bt, nt=nt, xt=xt, pbk=pbk: e.copy(out=xt.ap[:, bt * 1024:bt * 1024 + nt * 128], in_=pbk.ap[:, 0:nt * 128]), reads=[pbk.b], writes=[xt.b])
                else:
                    P.add("dve", lambda e, bt=bt, nt=nt, xt=xt, pbk=pbk: e.tensor_copy(out=xt.ap[:, bt * 1024:bt * 1024 + nt * 128], in_=pbk.ap[:, 0:nt * 128]), reads=[pbk.b], writes=[xt.b])

        SBK = [pf[2], pf[3], pf[5], pf[4]]

        def state(q):
            xt = xtk[q % 2]
            for g in range(4):
                xg = xt.ap[:, g * 512:(g + 1) * 512].rearrange("p (a b) -> p a b", a=8)
                o = q * 32 + 8 * g
                xd = xdtsA4[(q % 2) * 4 + g]
                P.add("dve", lambda e, xg=xg, o=o, xd=xd: e.tensor_tensor(out=v3(xd.ap), in0=xg, in1=bc(smG.ap[:, 6, o:o + 8], 64), op=ALU.mult), reads=[xt.b, smG.b], writes=[xd.b])
                P.add("pe", lambda e, g=g, xt=xt, xd=xd, q=q: e.matmul(SBK[g].ap, xt.ap[:, 2048 + g * 128:2048 + (g + 1) * 128], xd.ap, start=(q == 0), stop=(q == 3)), reads=[xt.b, xd.b], writes=[SBK[g].b])
            if q == 3:
                P.add("dve", lambda e: e.tensor_tensor(out=H.ap.rearrange("p (a b) -> p a b", a=32), in0=H.ap.rearrange("p (a b) -> p a b", a=32), in1=bc(smG.ap[:, 7, 0:32], 64), op=ALU.mult), reads=[H.b, smG.b], writes=[H.b])
                for g in range(4):
                    Hg = H.ap[:, g * 512:(g + 1) * 512]
                    P.add("dve", lambda e, Hg=Hg, g=g: e.tensor_tensor(out=Hg, in0=Hg, in1=SBK[g].ap, op=ALU.add), reads=[H.b, SBK[g].b], writes=[H.b])

        tr(0); tr(1)
        hooks[0]()
        P.add("pe", lambda e: e.matmul(pf[4].ap[:, 128:256], triU, S(2), start=True, stop=True), reads=[cf.b, smG.b], writes=[pf[4].b])
        P.add("pe", lambda e: e.matmul(pf[4].ap[:, 256:384], onesf, S(2), start=True, stop=True), reads=[cf.b, smG.b], writes=[pf[4].b])
        P.add("dve", lambda e: e.tensor_copy(out=smG.ap[:, 3:5, :], in_=pf[4].ap[:, 128:384].rearrange("p (a b) -> p a b", a=2)), reads=[pf[4].b], writes=[smG.b])
        P.add("dve", lambda e: e.memset(S3(5)[:, 3, :], 0.0), writes=[smG.b])
        P.add("dve", lambda e: e.tensor_copy(out=S3(5)[:, 2, :], in_=S3(4)[:, 3, :]), reads=[smG.b], writes=[smG.b])
        P.add("dve", lambda e: e.tensor_tensor(out=S3(5)[:, 1, :], in0=S3(5)[:, 2, :], in1=S3(4)[:, 2, :], op=ALU.add), reads=[smG.b], writes=[smG.b])
        P.add("dve", lambda e: e.tensor_tensor(out=S3(5)[:, 0, :], in0=S3(5)[:, 1, :], in1=S3(4)[:, 1, :], op=ALU.add), reads=[smG.b], writes=[smG.b])
        P.add("dve", lambda e: e.tensor_tensor(out=S3(7)[:, 0, :], in0=S3(5)[:, 0, :], in1=S3(4)[:, 0, :], op=ALU.add), reads=[smG.b], writes=[smG.b])
        P.add("dve", lambda e: e.tensor_tensor(out=S(6), in0=S(4), in1=S(3), op=ALU.subtract), reads=[smG.b], writes=[smG.b])
        P.add("dve", lambda e: e.tensor_tensor(out=S(6), in0=S(6), in1=S(5), op=ALU.add), reads=[smG.b], writes=[smG.b])
        P.add("act", lambda e: e.activation(out=S(6), in_=S(6), func=AF.Exp), reads=[smG.b], writes=[smG.b])
        P.add("act", lambda e: e.activation(out=S3(7)[:, 0, :], in_=S3(7)[:, 0, :], func=AF.Exp), reads=[smG.b], writes=[smG.b])
        P.add("dve", lambda e: e.tensor_tensor(out=S(6), in0=S(6), in1=S(1), op=ALU.mult), reads=[smG.b], writes=[smG.b])
        state(0); state(1)
        hooks[1]()
        tr(2); tr(3)
        hooks[2]()
        state(2); state(3)
        hooks[3]()

    NG = NPRE // 4
    g0 = NG - (npre + 3) // 4
    if not SKIP1 and g0 < NG:
        load_group(g0, g0 % 2)
        proj_in(g0, g0 % 2, g0 == NG - 1, 0, 24)
        for gi in range(g0, NG):
            proj_conv(gi, gi == NG - 1)
            if gi + 1 < NG:
                load_group(gi + 1, (gi + 1) % 2)
                hk = [lambda a=a, gi=gi: proj_in(gi + 1, (gi + 1) % 2, gi + 1 == NG - 1, a, a + 6) for a in (0, 6, 12, 18)]
            else:
                hk = [lambda: None] * 4
            group_chunks(gi, gi % 2, hk)
        P.add("pool", lambda e: e.tensor_copy(out=rawh.ap[:, :, 0:3], in_=raw4.ap[:, :, 0:3]), reads=rawb, writes=[rawh.b])
    else:
        P.add("pool", lambda e: e.memset(rawh.ap, 0.0), writes=[rawh.b])

    P.barrier()
    AB.off, AFa.off = mark1B, mark1F
    Wz = AB.get("Wz", 8, 2048); load_w(Wz, w_in[:, COL_Z:COL_Z + 2048], 2048)
    nwb = AFa.get("nwb", 2048)
    dma("sp", nwb.ap, bcast_row(normw, 2048), [], [nwb.b])
    xbs = [AB.get("xb_%d" % i, 1024) for i in range(2)]; xTs = [AB.get("xT_%d" % i, 8, 128) for i in range(2)]
    raws = [AB.get("raw_%d" % i, 24, 132) for i in range(2)]; xcs = [AB.get("xc_%d" % i, 24, 128) for i in range(2)]
    xtok = AB.get("xtok", 2560)
    xdt = AB.get("xdt", 512); xdts = AB.get("xdts", 512)
    Hb = AB.get("Hb", 2048); cbm = AB.get("cbm", 4, 128)
    LT = AB.get("LT", 4, 128); MT = AB.get("MT", 8, 128); yn = AB.get("yn", 2048)
    adtU = [AFa.get("adtU%d" % i, 128) for i in range(8)]
    sm = AFa.get("sm", 12, 32)
    yacc = AFa.get("yacc", 512); ytmp = AFa.get("ytmp", 512); sz = AFa.get("sz", 512)
    ssq = AFa.get("ssq", 4)
    P.add("pool", lambda e: e.memset(raws[0].ap, 0.0), writes=[raws[0].b])
    P.add("pool", lambda e: e.memset(raws[1].ap, 0.0), writes=[raws[1].b])
    P.add("pool", lambda e: e.tensor_copy(out=raws[0].ap[:, :, 0:3], in_=rawh.ap[:, :, 0:3]), reads=[rawh.b], writes=[raws[0].b])

    CUT = int(os.environ.get('KCUT', '99')); NCH1 = int(os.environ.get('KNCH', str(NM1)))
    def front_pieces(ci, xrow, xb, xT, raw, xc, raw_next):
        def inproj(g):
            bank = pf[g % 2]
            for jj in range(4):
                j = 4 * g + jj
                chain(bank, bank.ap[:, jj * 128:(jj + 1) * 128], [(Wx.ap[:, k, j * 128:(j + 1) * 128], xT.ap[:, k, :]) for k in range(8)], [Wx.b, xT.b])
            P.add("act", lambda e, g=g, bank=bank: e.copy(out=raw.ap[:, 4 * g:4 * g + 4, 3:131], in_=bank.ap.rearrange("p (a b) -> p a b", a=4)), reads=[bank.b], writes=[raw.b])

        def conv(g):
            for jj in range(4):
                j = 4 * g + jj
                bank = pf[2 + j % 2]
                chain(bank, bank.ap[:, 0:128], [(diag.ap[:, j, tp, :], raw.ap[:, j, tp:tp + 128]) for tp in range(4)], [diag.b, raw.b])
                P.add("act", lambda e, j=j, bank=bank: e.activation(out=xc.ap[:, j, :], in_=bank.ap[:, 0:128], func=AF.Silu, bias=cbt.ap[:, j:j + 1], scale=1.0), reads=[bank.b, cbt.b], writes=[xc.b])

        def p0():
            load_xT(xrow, xb, xT); inproj(0); inproj(1)

        def p1():
            inproj(2); inproj(3)

        def p2():
            inproj(4); inproj(5)
            P.add("pool", lambda e: e.tensor_copy(out=raw_next.ap[:, :, 0:3], in_=raw.ap[:, :, 128:131]), reads=[raw.b], writes=[raw_next.b])
            conv(0); conv(1)

        def p3():
            conv(2); conv(3); conv(4); conv(5)
        return [p0, p1, p2, p3]

    def ssd_back(fcol, main, ci, xT, xc, hooks):
        for bt in range(3):
            nt = 8 if bt < 2 else 4
            for i in range(nt):
                j = bt * 8 + i
                P.add("pe", lambda e, i=i, j=j: e.transpose(pb[1].ap[:, i * 128:(i + 1) * 128], xc.ap[:, j, :], identb), reads=[xc.b, cb16.b], writes=[pb[1].b])
            P.add("dve", lambda e, bt=bt, nt=nt: e.tensor_copy(out=xtok.ap[:, bt * 1024:bt * 1024 + nt * 128], in_=pb[1].ap[:, 0:nt * 128]), reads=[pb[1].b], writes=[xtok.b])
        chain(pf[4], pf[4].ap[:, 0:32], [(xT.ap[:, k, :], Wdt.ap[:, k, :]) for k in range(8)], [xT.b, Wdt.b])
        S = lambda i: sm.ap[:, i, :]
        P.add("dve", lambda e: e.tensor_tensor(out=S(0), in0=pf[4].ap[:, 0:32], in1=dtb_bc, op=ALU.add), reads=[pf[4].b, smallp.b], writes=[sm.b])
        P.add("act", lambda e: e.activation(out=S(0), in_=S(0), func=AF.Exp), reads=[sm.b], writes=[sm.b])
        P.add("act", lambda e: e.activation(out=S(1), in_=S(0), func=AF.Ln, bias=onecol, scale=1.0), reads=[sm.b, cf.b], writes=[sm.b])
        P.add("dve", lambda e: e.tensor_scalar(out=S(1), in0=S(1), scalar1=fcol, scalar2=None, op0=ALU.mult), reads=[sm.b, flg.b], writes=[sm.b])
        P.add("dve", lambda e: e.tensor_tensor(out=S(2), in0=S(1), in1=a_bc, op=ALU.mult), reads=[sm.b, smallp.b], writes=[sm.b])
        P.add("pe", lambda e: e.matmul(pf[4].ap[:, 32:64], triU, S(2), start=True, stop=True), reads=[cf.b, sm.b], writes=[pf[4].b])
        P.add("pe", lambda e: e.matmul(pf[4].ap[:, 64:96], onesf, S(2), start=True, stop=True), reads=[cf.b, sm.b], writes=[pf[4].b])
        P.add("dve", lambda e: e.tensor_copy(out=sm.ap[:, 3:5, :], in_=pf[4].ap[:, 32:96].rearrange("p (a b) -> p a b", a=2)), reads=[pf[4].b], writes=[sm.b])
        if main:
            for g in range(4):
                P.add("pe", lambda e, g=g: e.matmul(pf[5].ap[:, g * 128:(g + 1) * 128], xc.ap[:, 16 + g, :], xc.ap[:, 20 + g, :], start=True, stop=True), reads=[xc.b], writes=[pf[5].b])
            P.add("dve", lambda e: e.tensor_tensor(out=cbm.ap, in0=pf[5].ap.rearrange("p (a b) -> p a b", a=4), in1=maskb.unsqueeze(1).to_broadcast([128, 4, 128]), op=ALU.mult), reads=[pf[5].b, cb16.b], writes=[cbm.b])
            P.add("pool", lambda e: e.tensor_copy(out=Hb.ap, in_=H.ap), reads=[H.b], writes=[Hb.b])
            P.add("act", lambda e: e.activation(out=S(5), in_=S(3), func=AF.Exp), reads=[sm.b], writes=[sm.b])
            for g in range(4):
                xg = xtok.ap[:, g * 512:(g + 1) * 512].rearrange("p (a b) -> p a b", a=8)
                P.add("dve", lambda e, g=g, xg=xg: e.tensor_tensor(out=xdt.ap.rearrange("p (a b) -> p a b", a=8), in0=xg, in1=bc(sm.ap[:, 1, 8 * g:8 * g + 8], 64), op=ALU.mult), reads=[xtok.b, sm.b], writes=[xdt.b])
                for hh in range(2):
                    bank = pf[hh]
                    for h4 in range(4):
                        h = 8 * g + 4 * hh + h4
                        au = adtU[h % 8]
                        P.add("dve", lambda e, h=h, au=au: e.tensor_scalar(out=au.ap, in0=Ustr, scalar1=sm.ap[:, 2, h:h + 1], scalar2=None, op0=ALU.mult), reads=[cf.b, sm.b], writes=[au.b])
                        P.add("pe", lambda e, h4=h4, au=au, bank=bank: e.matmul(bank.ap[:, h4 * 128:(h4 + 1) * 128], au.ap, triU, start=True, stop=True), reads=[au.b, cf.b], writes=[bank.b])
                    P.add("act", lambda e, bank=bank: e.activation(out=LT.ap.rearrange("p a b -> p (a b)"), in_=bank.ap, func=AF.Exp), reads=[bank.b], writes=[LT.b])
                    P.add("dve", lambda e, g=g, hh=hh: e.tensor_tensor(out=MT.ap[:, 4 * hh:4 * hh + 4, :], in0=LT.ap, in1=cbm.ap[:, g, :].unsqueeze(1).to_broadcast([128, 4, 128]), op=ALU.mult), reads=[LT.b, cbm.b], writes=[MT.b])
                for h8 in range(8):
                    P.add("pe", lambda e, h8=h8: e.matmul(pf[2].ap[:, h8 * 64:(h8 + 1) * 64], MT.ap[:, h8, :], xdt.ap[:, h8 * 64:(h8 + 1) * 64], start=True, stop=True), reads=[MT.b, xdt.b], writes=[pf[2].b])
                P.add("pe", lambda e, g=g: e.matmul(pf[3].ap, xc.ap[:, 20 + g, :], Hb.ap[:, g * 512:(g + 1) * 512], start=True, stop=True), reads=[xc.b, Hb.b], writes=[pf[3].b])
                v3 = lambda ap: ap.rearrange("p (a b) -> p a b", a=8)
                P.add("dve", lambda e, g=g: e.tensor_tensor(out=v3(yacc.ap), in0=v3(pf[3].ap), in1=bc(sm.ap[:, 5, 8 * g:8 * g + 8], 64), op=ALU.mult), reads=[pf[3].b, sm.b], writes=[yacc.b])
                P.add("dve", lambda e: e.tensor_tensor(out=yacc.ap, in0=yacc.ap, in1=pf[2].ap, op=ALU.add), reads=[pf[2].b, yacc.b], writes=[yacc.b])
                P.add("dve", lambda e, g=g, xg=xg: e.tensor_tensor(out=v3(ytmp.ap), in0=xg, in1=bc(D_bc[:, 8 * g:8 * g + 8], 64), op=ALU.mult), reads=[xtok.b, smallp.b], writes=[ytmp.b])
                P.add("dve", lambda e: e.tensor_tensor(out=yacc.ap, in0=yacc.ap, in1=ytmp.ap, op=ALU.add), reads=[ytmp.b, yacc.b], writes=[yacc.b])
                chain(pf[5], pf[5].ap, [(xT.ap[:, k, :], Wz.ap[:, k, g * 512:(g + 1) * 512]) for k in range(8)], [xT.b, Wz.b])
                P.add("act", lambda e: e.activation(out=sz.ap, in_=pf[5].ap, func=AF.Silu), reads=[pf[5].b], writes=[sz.b])
                P.add("dve", lambda e: e.tensor_tensor(out=yacc.ap, in0=yacc.ap, in1=sz.ap, op=ALU.mult), reads=[sz.b, yacc.b], writes=[yacc.b])
                P.add("act", lambda e, g=g: e.activation(out=ytmp.ap, in_=yacc.ap, func=AF.Square, accum_out=ssq.ap[:, g:g + 1]), reads=[yacc.b], writes=[ytmp.b, ssq.b])
                P.add("dve", lambda e, g=g: e.tensor_scalar(out=ssq.ap[:, g:g + 1], in0=ssq.ap[:, g:g + 1], scalar1=1.0 / 512, scalar2=1e-5, op0=ALU.mult, op1=ALU.add), reads=[ssq.b], writes=[ssq.b])
                P.add("act", lambda e, g=g: e.activation(out=ssq.ap[:, g:g + 1], in_=ssq.ap[:, g:g + 1], func=AF.Sqrt), reads=[ssq.b], writes=[ssq.b])
                P.add("dve", lambda e, g=g: e.reciprocal(out=ssq.ap[:, g:g + 1], in_=ssq.ap[:, g:g + 1]), reads=[ssq.b], writes=[ssq.b])
                P.add("dve", lambda e, g=g: e.scalar_tensor_tensor(out=yn.ap[:, g * 512:(g + 1) * 512], in0=yacc.ap, scalar=ssq.ap[:, g:g + 1], in1=nwb.ap[:, g * 512:(g + 1) * 512], op0=ALU.mult, op1=ALU.mult), reads=[yacc.b, ssq.b, nwb.b], writes=[yn.b])
                hooks[g]()
            dma("sp", ynd[ci * 128:(ci + 1) * 128, :], yn.ap, [yn.b], [])
        P.add("dve", lambda e: e.tensor_tensor(out=S(6), in0=S(4), in1=S(3), op=ALU.subtract), reads=[sm.b], writes=[sm.b])
        P.add("act", lambda e: e.activation(out=S(6), in_=S(6), func=AF.Exp), reads=[sm.b], writes=[sm.b])
        P.add("act", lambda e: e.activation(out=S(7), in_=S(4), func=AF.Exp), reads=[sm.b], writes=[sm.b])
        P.add("dve", lambda e: e.tensor_tensor(out=S(6), in0=S(6), in1=S(1), op=ALU.mult), reads=[sm.b], writes=[sm.b])
        for g in range(4):
            xg = xtok.ap[:, g * 512:(g + 1) * 512].rearrange("p (a b) -> p a b", a=8)
            v3 = lambda ap: ap.rearrange("p (a b) -> p a b", a=8)
            P.add("dve", lambda e, g=g, xg=xg: e.tensor_tensor(out=v3(xdts.ap), in0=xg, in1=bc(sm.ap[:, 6, 8 * g:8 * g + 8], 64), op=ALU.mult), reads=[xtok.b, sm.b], writes=[xdts.b])
            bank = pf[2 + g % 2]
            P.add("pe", lambda e, g=g, bank=bank: e.matmul(bank.ap, xtok.ap[:, 2048 + g * 128:2048 + (g + 1) * 128], xdts.ap, start=True, stop=True), reads=[xtok.b, xdts.b], writes=[bank.b])
            Hg = H.ap[:, g * 512:(g + 1) * 512]
            P.add("dve", lambda e, g=g, Hg=Hg: e.tensor_tensor(out=v3(Hg), in0=v3(Hg), in1=bc(sm.ap[:, 7, 8 * g:8 * g + 8], 64), op=ALU.mult), reads=[H.b, sm.b], writes=[H.b])
            P.add("dve", lambda e, Hg=Hg, bank=bank: e.tensor_tensor(out=Hg, in0=Hg, in1=bank.ap, op=ALU.add), reads=[H.b, bank.b], writes=[H.b])

    def fp(ci):
        b = ci % 2
        return front_pieces(ci, xmain[(ci + 1) * 128:(ci + 2) * 128, :], xbs[b], xTs[b], raws[b], xcs[b], raws[1 - b])
    nch = 0 if SKIP1 else NCH1
    if nch:
        for p_ in fp(0):
            p_()
    for ci in range(nch):
        nxt = fp(ci + 1) if ci + 1 < nch else [lambda: None] * 4
        ssd_back(flg.ap[:, NPRE + ci:NPRE + ci + 1], True, ci, xTs[ci % 2], xcs[ci % 2], nxt)

    if stop < 2:
        P.emit(nc); es.close(); return nc
    P.barrier()
    AB.off, AFa.off = markB, markF
    Wq = AB.get("Wq", 8, 1024); load_w(Wq, w_in[:, COL_Q:COL_Q + 1024], 1024)
    Wk2 = AB.get("Wk2", 8, 128); load_w(Wk2, w_in[:, COL_K:COL_K + 128], 128)
    Wv = AB.get("Wv", 8, 128); load_w(Wv, w_in[:, COL_V:COL_V + 128], 128)
    EB = AB.get("EB", 2, 16, 128)
    ebf = AFa.get("ebf", 2048); mkf = AFa.get("mkf", 2048)
    for kt in range(2):
        dma("sp", ebf.ap, biasg[:, kt * 2048:(kt + 1) * 2048], [], [ebf.b])
        dma("sp", mkf.ap, maskg[:, kt * 2048:(kt + 1) * 2048], [], [mkf.b])
        P.add("act", lambda e: e.activation(out=ebf.ap, in_=ebf.ap, func=AF.Exp), reads=[ebf.b], writes=[ebf.b])
        P.add("dve", lambda e, kt=kt: e.tensor_tensor(out=EB.ap[:, kt, :, :].rearrange("p a b -> p (a b)"), in0=ebf.ap, in1=mkf.ap, op=ALU.mult), reads=[ebf.b, mkf.b], writes=[EB.b])
    xb = AB.get("xb2", 1024); xT = AB.get("xT2", 8, 128)
    kT = [AB.get("kT%d" % i, 2, 128) for i in range(2)]
    vx = [AB.get("vx%d" % i, 2, 65) for i in range(2)]
    for i in range(2):
        P.add("pool", lambda e, i=i: e.memset(vx[i].ap, 1.0), writes=[vx[i].b])
    qT = AB.get("qT", 16, 128); et = AB.get("et", 4, 128)
    PT = [AB.get("PT%d" % i, 4, 128) for i in range(2)]
    ya = AB.get("ya", 1024)
    den = AFa.get("den", 4)
    flag0 = flg.ap[:, NPRE:NPRE + 1]
    CUT2 = int(os.environ.get('KCUT2', '99')); NCH2 = int(os.environ.get('KNCH2', str(NM2)))
    for ci in range(NCH2):
        sl = ci % 2
        load_xT(xmain[ci * 128:(ci + 1) * 128, :], xb, xT)
        for kv in range(2):
            chain(pf[0], pf[0].ap[0:64, kv * 128:(kv + 1) * 128], [(Wk2.ap[:, k, kv * 64:(kv + 1) * 64], xT.ap[:, k, :]) for k in range(8)], [Wk2.b, xT.b])
        P.add("act", lambda e, sl=sl: e.copy(out=kT[sl].ap[0:64, :, :], in_=pf[0].ap[0:64, 0:256].rearrange("p (a b) -> p a b", a=2)), reads=[pf[0].b], writes=[kT[sl].b])
        chain(pf[1], pf[1].ap[:, 0:128], [(xT.ap[:, k, :], Wv.ap[:, k, :]) for k in range(8)], [xT.b, Wv.b])
        P.add("dve", lambda e, sl=sl: e.tensor_copy(out=vx[sl].ap[:, :, 0:64], in_=pf[1].ap[:, 0:128].rearrange("p (a b) -> p a b", a=2)), reads=[pf[1].b], writes=[vx[sl].b])
        if ci == 0 or CUT2 <= 1:
            continue
        for q4 in range(4):
            bank = pf[2 + q4 % 2]
            for tt in range(4):
                j = q4 * 4 + tt
                chain(bank, bank.ap[0:64, tt * 128:(tt + 1) * 128], [(Wq.ap[:, k, j * 64:(j + 1) * 64], xT.ap[:, k, :]) for k in range(8)], [Wq.b, xT.b])
            P.add("act", lambda e, q4=q4, bank=bank: e.copy(out=qT.ap[0:64, q4 * 4:q4 * 4 + 4, :], in_=bank.ap[0:64, :].rearrange("p (a b) -> p a b", a=4)), reads=[bank.b], writes=[qT.b])
        for kvh in range(2):
            if CUT2 <= 2: break
            for hb in range(2):
                j0 = kvh * 8 + hb * 4
                for kt in range(2):
                    slk = (ci + 1 + kt) % 2
                    bank = pf[4 + kt]
                    for i in range(4):
                        j = j0 + i
                        base = (j % 2) * 64 * int(os.environ.get("KB64", "1"))
                        P.add("pe", lambda e, i=i, j=j, base=base, slk=slk, bank=bank, kvh=kvh: e.matmul(bank.ap[:, i * 128:(i + 1) * 128], kT[slk].ap[0:64, kvh, :], qT.ap[0:64, j, :], start=True, stop=True), reads=[kT[slk].b, qT.b], writes=[bank.b])
                    P.add("act", lambda e, bank=bank: e.activation(out=et.ap.rearrange("p a b -> p (a b)"), in_=bank.ap, func=AF.Exp, scale=0.125), reads=[bank.b], writes=[et.b])
                    if ci == 2 and kt == 0:
                        P.add("dve", lambda e, kt=kt, j0=j0: e.scalar_tensor_tensor(out=PT[kt].ap, in0=et.ap, scalar=flag0, in1=EB.ap[:, kt, j0:j0 + 4, :], op0=ALU.mult, op1=ALU.mult), reads=[et.b, EB.b, flg.b], writes=[PT[kt].b])
                    else:
                        P.add("dve", lambda e, kt=kt, j0=j0: e.tensor_tensor(out=PT[kt].ap, in0=et.ap, in1=EB.ap[:, kt, j0:j0 + 4, :], op=ALU.mult), reads=[et.b, EB.b], writes=[PT[kt].b])
                if CUT2 <= 3: continue
                bank = pf[hb]
                for i in range(4):
                    for kt in range(2):
                        slk = (ci + 1 + kt) % 2
                        P.add("pe", lambda e, i=i, kt=kt, slk=slk, bank=bank, kvh=kvh: e.matmul(bank.ap[:, i * 65:(i + 1) * 65], PT[kt].ap[:, i, :], vx[slk].ap[:, kvh, :], start=(kt == 0), stop=(kt == 1)), reads=[PT[kt].b, vx[slk].b], writes=[bank.b])
                if CUT2 <= 4: continue
                pv = bank.ap[:, 0:260].rearrange("p (a b) -> p a b", a=4)
                P.add("dve", lambda e, pv=pv, j0=j0: e.tensor_tensor(out=den.ap, in0=pv[:, :, 64], in1=esink[:, j0:j0 + 4], op=ALU.add), reads=[bank.b, smallp.b], writes=[den.b])
                P.add("dve", lambda e: e.reciprocal(out=den.ap, in_=den.ap), reads=[den.b], writes=[den.b])
                P.add("dve", lambda e, pv=pv, j0=j0: e.tensor_tensor(out=ya.ap[:, j0 * 64:(j0 + 4) * 64].rearrange("p (a b) -> p a b", a=4), in0=pv[:, :, 0:64], in1=bc(den.ap, 64), op=ALU.mult), reads=[bank.b, den.b], writes=[ya.b])
        dma("sp", yad[(ci - 1) * 128:ci * 128, :], ya.ap, [ya.b], [])

    if stop < 3:
        P.emit(nc); es.close(); return nc
    P.barrier()
    AB.off, AFa.off = markB, markF
    Wg = AB.get("Wg", 8, 2048); load_w(Wg, w_in[:, COL_G:COL_G + 2048], 2048)
    Wbs = AB.get("Wbs", 16, 1024); load_w(Wbs, w_bs, 1024)
    Wba = AB.get("Wba", 8, 1024); load_w(Wba, w_ba, 1024)
    Wmx = AB.get("Wmx", 8, 1024); load_w(Wmx, w_mix, 1024)
    bgb = AFa.get("bgb", 2048); dma("sp", bgb.ap, bcast_row(b_gate, 2048), [], [bgb.b])
    lng = AFa.get("lng", 2, 1024)
    dma("sp", lng.ap[:, 0, :], bcast_row(ln1g, 1024), [], [lng.b]); dma("sp", lng.ap[:, 1, :], bcast_row(ln1b, 1024), [], [lng.b])
    xb = AB.get("xb3", 1024); xT = AB.get("xT3", 8, 128)
    ynb = AB.get("ynb", 2048); yab = AB.get("yab", 1024)
    ynT = AB.get("ynT", 16, 128); yaT = AB.get("yaT", 8, 128)
    mg = AB.get("mg", 1024); mT = AB.get("mT", 8, 128)
    xf = AFa.get("xf", 1024); gt = AFa.get("gt", 2048); m1 = AFa.get("m1", 512); r = AFa.get("r", 1024)
    st = AFa.get("st", 2, 6); mv = AFa.get("mv", 2)

    def transp(src, dst, ntl):
        for bt in range(ntl // 8):
            for i in range(8):
                j = bt * 8 + i
                P.add("pe", lambda e, i=i, j=j: e.transpose(pb[1].ap[:, i * 128:(i + 1) * 128], src.ap[:, j * 128:(j + 1) * 128], identb), reads=[src.b, cb16.b], writes=[pb[1].b])
            P.add("act", lambda e, bt=bt: e.copy(out=dst.ap[:, bt * 8:bt * 8 + 8, :].rearrange("p a b -> p (a b)"), in_=pb[1].ap), reads=[pb[1].b], writes=[dst.b])

    def layer_norm(r, g_ap, b_ap, gb, st, mv):
        for i in range(2):
            P.add("dve", lambda e, i=i: e.bn_stats(out=st.ap[:, i, :], in_=r.ap[:, i * 512:(i + 1) * 512]), reads=[r.b], writes=[st.b])
        P.add("dve", lambda e: e.bn_aggr(out=mv.ap, in_=st.ap.rearrange("p a b -> p (a b)")), reads=[st.b], writes=[mv.b])
        P.add("dve", lambda e: e.tensor_scalar(out=mv.ap[:, 1:2], in0=mv.ap[:, 1:2], scalar1=1e-5, scalar2=None, op0=ALU.add), reads=[mv.b], writes=[mv.b])
        P.add("act", lambda e: e.activation(out=mv.ap[:, 1:2], in_=mv.ap[:, 1:2], func=AF.Sqrt), reads=[mv.b], writes=[mv.b])
        P.add("dve", lambda e: e.reciprocal(out=mv.ap[:, 1:2], in_=mv.ap[:, 1:2]), reads=[mv.b], writes=[mv.b])
        P.add("dve", lambda e: e.tensor_scalar(out=r.ap, in0=r.ap, scalar1=mv.ap[:, 0:1], scalar2=mv.ap[:, 1:2], op0=ALU.subtract, op1=ALU.mult), reads=[r.b, mv.b], writes=[r.b])
        P.add("dve", lambda e: e.tensor_tensor(out=r.ap, in0=r.ap, in1=g_ap, op=ALU.mult), reads=[r.b, gb], writes=[r.b])
        P.add("dve", lambda e: e.tensor_tensor(out=r.ap, in0=r.ap, in1=b_ap, op=ALU.add), reads=[r.b, gb], writes=[r.b])

    for ci in range(NM1):
        xrow = xmain[(ci + 1) * 128:(ci + 2) * 128, :]
        load_xT(xrow, xb, xT)
        dma("sp", xf.ap, xrow, [], [xf.b])
        dma("sp", ynb.ap, ynd[ci * 128:(ci + 1) * 128, :], [], [ynb.b])
        dma("sp", yab.ap, yad[ci * 128:(ci + 1) * 128, :], [], [yab.b])
        transp(ynb, ynT, 16)
        transp(yab, yaT, 8)
        for s4 in range(4):
            bank = pf[s4 % 2]
            chain(bank, bank.ap, [(xT.ap[:, k, :], Wg.ap[:, k, s4 * 512:(s4 + 1) * 512]) for k in range(8)], [xT.b, Wg.b])
            P.add("dve", lambda e, s4=s4, bank=bank: e.tensor_tensor(out=gt.ap[:, s4 * 512:(s4 + 1) * 512], in0=bank.ap, in1=bgb.ap[:, s4 * 512:(s4 + 1) * 512], op=ALU.add), reads=[bank.b, bgb.b], writes=[gt.b])
        P.add("act", lambda e: e.activation(out=gt.ap, in_=gt.ap, func=AF.Sigmoid), reads=[gt.b], writes=[gt.b])
        for hf in range(2):
            chain(pf[2], pf[2].ap, [(ynT.ap[:, i, :], Wbs.ap[:, i, hf * 512:(hf + 1) * 512]) for i in range(16)], [ynT.b, Wbs.b])
            chain(pf[3], pf[3].ap, [(yaT.ap[:, i, :], Wba.ap[:, i, hf * 512:(hf + 1) * 512]) for i in range(8)], [yaT.b, Wba.b])
            P.add("dve", lambda e, hf=hf: e.tensor_tensor(out=m1.ap, in0=pf[2].ap, in1=gt.ap[:, hf * 512:(hf + 1) * 512], op=ALU.mult), reads=[pf[2].b, gt.b], writes=[m1.b])
            P.add("dve", lambda e, hf=hf: e.tensor_tensor(out=r.ap[:, hf * 512:(hf + 1) * 512], in0=pf[3].ap, in1=gt.ap[:, 1024 + hf * 512:1024 + (hf + 1) * 512], op=ALU.mult), reads=[pf[3].b, gt.b], writes=[r.b])
            P.add("dve", lambda e, hf=hf: e.tensor_tensor(out=mg.ap[:, hf * 512:(hf + 1) * 512], in0=m1.ap, in1=r.ap[:, hf * 512:(hf + 1) * 512], op=ALU.add), reads=[m1.b, r.b], writes=[mg.b])
        transp(mg, mT, 8)
        for hf in range(2):
            bank = pf[4 + hf]
            chain(bank, bank.ap, [(mT.ap[:, i, :], Wmx.ap[:, i, hf * 512:(hf + 1) * 512]) for i in range(8)], [mT.b, Wmx.b])
            P.add("dve", lambda e, hf=hf, bank=bank: e.scalar_tensor_tensor(out=r.ap[:, hf * 512:(hf + 1) * 512], in0=xf.ap[:, hf * 512:(hf + 1) * 512], scalar=ALPHA, in1=bank.ap, op0=ALU.mult, op1=ALU.add), reads=[xf.b, bank.b], writes=[r.b])
        layer_norm(r, lng.ap[:, 0, :], lng.ap[:, 1, :], lng.b, st, mv)
        if ci == 0:
            P.add("dve", lambda e: e.tensor_scalar(out=r.ap, in0=r.ap, scalar1=flag0, scalar2=None, op0=ALU.mult), reads=[r.b, flg.b], writes=[r.b])
        dma("sp", h1d[ci * 128:(ci + 1) * 128, :], r.ap, [r.b], [])

    if stop < 4:
        P.emit(nc); es.close(); return nc
    P.barrier()
    AB.off, AFa.off = markB, markF
    Wup = AB.get("Wup", 8, 5632); load_w(Wup, w_up, 5632)
    Wdn = AB.get("Wdn", 22, 1024); load_w(Wdn, w_dn, 1024)
    fw = AFa.get("fw", 132); fb = AFa.get("fb", 44)
    per_channel(fw.ap[:, 0:88], fcw[0:88, :], 88); fw.b.writer = P.ops["dve"][-1]
    per_channel(fw.ap[:, 88:132], fcw[88:132, :], 44); fw.b.writer = P.ops["dve"][-1]
    per_channel(fb.ap, fcb, 44); fb.b.writer = P.ops["dve"][-1]
    lng2 = AFa.get("lng2", 2, 1024)
    dma("sp", lng2.ap[:, 0, :], bcast_row(ln2g, 1024), [], [lng2.b]); dma("sp", lng2.ap[:, 1, :], bcast_row(ln2b, 1024), [], [lng2.b])
    SC = 256
    hA = AB.get("hA", 1024); hB = AB.get("hB", 1024); h1T = AB.get("h1T", 8, SC + 2)
    aT = AB.get("aT", 22, SC)
    aTb = [Buf("aTb%d" % i) for i in range(22)]
    cvs = [AFa.get("cvs%d" % i, SC) for i in range(4)]
    t0s = [AFa.get("t0s%d" % i, SC) for i in range(2)]
    sg = AFa.get("sg", SC)
    hr = AFa.get("hr", 1024); r4 = AFa.get("r4", 1024)
    st4 = AFa.get("st4", 2, 6); mv4 = AFa.get("mv4", 2)
    NT = SC // 128
    tcount = 0
    for sc in range(TOK // SC):
        r0 = 128 + sc * SC
        for i in range(NT):
            dma("pool", hA.ap, h1d[r0 - 2 + i * 128:r0 + 126 + i * 128, :], [], [hA.b])
            for k in range(8):
                P.add("pe", lambda e, k=k: e.transpose(pb[0].ap[:, k * 128:(k + 1) * 128], hA.ap[:, k * 128:(k + 1) * 128], identb), reads=[hA.b, cb16.b], writes=[pb[0].b])
            P.add("act", lambda e, i=i: e.copy(out=h1T.ap[:, :, i * 128:(i + 1) * 128], in_=pb[0].ap.rearrange("p (a b) -> p a b", a=8)), reads=[pb[0].b], writes=[h1T.b])
        dma("pool", hB.ap[0:2, :], h1d[r0 + SC - 2:r0 + SC, :], [], [hB.b])
        for k in range(8):
            P.add("pe", lambda e, k=k: e.transpose(pb[1].ap[:, k * 2:(k + 1) * 2], hB.ap[0:2, k * 128:(k + 1) * 128], identb[0:2, 0:2]), reads=[hB.b, cb16.b], writes=[pb[1].b])
        P.add("act", lambda e: e.copy(out=h1T.ap[:, :, SC:SC + 2], in_=pb[1].ap[:, 0:16].rearrange("p (a b) -> p a b", a=8)), reads=[pb[1].b], writes=[h1T.b])
        def down(jp):
            for tt in range(NT):
                for hf in range(2):
                    bank = pf[2 + tt * 2 + hf]
                    P.add("pe", lambda e, jp=jp, tt=tt, hf=hf, bank=bank: e.matmul(bank.ap, aT.ap[:, jp, tt * 128:(tt + 1) * 128], Wdn.ap[:, jp, hf * 512:(hf + 1) * 512], start=(jp == 0), stop=(jp == 21)), reads=[aTb[jp], Wdn.b], writes=[bank.b])

        for jp in range(22):
            for gv in range(2):
                j = jp + 22 * gv
                X = pf[tcount % 2]; t0 = t0s[tcount % 2]
                cv = cvs[(jp % 2) * 2 + gv]
                tcount += 1
                chain(X, X.ap[:, 0:SC + 2], [(Wup.ap[:, k, j * 128:(j + 1) * 128], h1T.ap[:, k, 0:SC + 2]) for k in range(8)], [Wup.b, h1T.b])
                w0, w1, w2, bb = fw.ap[:, j:j + 1], fw.ap[:, 44 + j:45 + j], fw.ap[:, 88 + j:89 + j], fb.ap[:, j:j + 1]
                P.add("act", lambda e, X=X, t0=t0, w2=w2, bb=bb: e.activation(out=t0.ap, in_=X.ap[:, 2:SC + 2], func=AF.Identity, bias=bb, scale=w2), reads=[X.b, fw.b, fb.b], writes=[t0.b])
                P.add("dve", lambda e, X=X, t0=t0, w1=w1: e.scalar_tensor_tensor(out=t0.ap, in0=X.ap[:, 1:SC + 1], scalar=w1, in1=t0.ap, op0=ALU.mult, op1=ALU.add), reads=[X.b, fw.b, t0.b], writes=[t0.b])
                P.add("dve", lambda e, X=X, t0=t0, w0=w0, cv=cv: e.scalar_tensor_tensor(out=cv.ap, in0=X.ap[:, 0:SC], scalar=w0, in1=t0.ap, op0=ALU.mult, op1=ALU.add), reads=[X.b, fw.b, t0.b], writes=[cv.b])
            cg, cvv = cvs[(jp % 2) * 2], cvs[(jp % 2) * 2 + 1]
            P.add("act", lambda e, cg=cg: e.activation(out=sg.ap, in_=cg.ap, func=AF.Silu), reads=[cg.b], writes=[sg.b])
            P.add("dve", lambda e, jp=jp, cvv=cvv: e.tensor_tensor(out=aT.ap[:, jp, :], in0=sg.ap, in1=cvv.ap, op=ALU.mult), reads=[sg.b, cvv.b], writes=[aTb[jp]])
            if jp >= 1:
                down(jp - 1)
        down(21)
        for tt in range(NT):
            rr = r0 + tt * 128
            dma("sp", hr.ap, h1d[rr:rr + 128, :], [], [hr.b])
            for hf in range(2):
                bank = pf[2 + tt * 2 + hf]
                P.add("dve", lambda e, hf=hf, bank=bank: e.scalar_tensor_tensor(out=r4.ap[:, hf * 512:(hf + 1) * 512], in0=hr.ap[:, hf * 512:(hf + 1) * 512], scalar=ALPHA, in1=bank.ap, op0=ALU.mult, op1=ALU.add), reads=[hr.b, bank.b], writes=[r4.b])
            layer_norm(r4, lng2.ap[:, 0, :], lng2.ap[:, 1, :], lng2.b, st4, mv4)
            dma("sp", out[rr - 128:rr, :], r4.ap, [r4.b], [])

    P.emit(nc)
    es.close()
    return nc


def rel_bucket_np(rel):
    n = np.maximum(rel, 0)
    nf = np.maximum(n, 1).astype(np.float32)
    large = 16 + (np.log(nf / np.float32(16)) / np.float32(np.log(128 / 16)) * np.float32(16)).astype(np.int32)
    large = np.minimum(large, 31)
    return np.where(n < 16, n, large)


_NC = None


def kernel(_dbg=None, **inp):
    global _NC
    x = np.asarray(inp["x"], np.float32)[0]
    f = lambda k: np.ascontiguousarray(np.asarray(inp[k], np.float32)[0])
    common = {
        "w_in": f("w_in"), "b_gate": f("b_gate")[None], "dtb": f("ssm_dt_bias")[None], "alog": f("ssm_a_log")[None],
        "dsk": f("ssm_d")[None], "normw": f("ssm_norm_w")[None], "sinks": f("attn_sinks")[None],
        "w_bs": f("w_branch_ssm"), "w_ba": f("w_branch_attn"), "w_mix": f("w_mix_out"),
        "ln1g": f("ln1_g")[None], "ln1b": f("ln1_b")[None], "ln2g": f("ln2_g")[None], "ln2b": f("ln2_b")[None],
        "w_up": f("w_up"), "w_dn": f("w_down"),
    }
    scw = f("ssm_conv_w")
    common["scw"] = np.ascontiguousarray(scw.reshape(4 * 24, 128))
    common["scb"] = np.ascontiguousarray(f("ssm_conv_b").reshape(24, 128))
    common["fcw"] = np.ascontiguousarray(f("ffn_conv_w").reshape(3 * 44, 128))
    common["fcb"] = np.ascontiguousarray(f("ffn_conv_b").reshape(44, 128))
    s = np.arange(128)
    ident = np.eye(128, dtype=np.float32)
    triU = (s[:, None] <= s[None, :]).astype(np.float32)
    ustr = (s[:, None] > s[None, :]).astype(np.float32)
    common["cst"] = np.ascontiguousarray(np.concatenate([ident, triU, ustr, np.ones((128, 128), np.float32)], axis=1))
    rb = np.asarray(inp["rel_bias"], np.float32)
    bg = np.zeros((128, 2, 16, 128), np.float32); mk = np.zeros((128, 2, 16, 128), np.float32)
    for kt in range(2):
        rel = (s[None, :] + 128) - (s[:, None] + 128 * kt)
        valid = (rel >= 0) & (rel < 128)
        bidx = rel_bucket_np(rel)
        g = rb[bidx]
        bg[:, kt] = np.transpose(g, (0, 2, 1))
        mk[:, kt] = np.broadcast_to(valid[:, None, :], (128, 16, 128))
    common["biasg"] = np.ascontiguousarray(bg.reshape(128, -1)); common["maskg"] = np.ascontiguousarray(mk.reshape(128, -1))
    in_maps = []
    for c in range(NCORE):
        S = c * TOK
        lo = S - 128 - NPRE * 128
        xp = np.zeros((NPRE * 128, 1024), np.float32)
        if S - 128 > 0:
            src_lo = max(lo, 0)
            xp[src_lo - lo:] = x[src_lo:S - 128]
        xm = np.zeros((NM2 * 128, 1024), np.float32)
        lo2 = S - 256
        src_lo = max(lo2, 0)
        xm[src_lo - lo2:] = x[src_lo:S + TOK]
        fl = np.zeros((128, NPRE + NM1), np.float32)
        for i in range(NPRE):
            fl[:, i] = 1.0 if lo + i * 128 >= 0 else 0.0
        fl[:, NPRE] = 1.0 if c > 0 else 0.0
        fl[:, NPRE + 1:] = 1.0
        m = dict(common); m["xpre"] = xp; m["xmain"] = xm; m["pflag"] = fl
        in_maps.append(m)
    if _dbg is not None:
        return in_maps
    if _NC is None:
        _NC = build()
    res = run_bass_kernel_spmd(_NC, in_maps, core_ids=list(range(NCORE)))
    o = np.concatenate([res.results[c]["out"] for c in range(NCORE)], axis=0)
    return o[None].astype(np.float32)
```

```python
import numpy as np
import concourse.bass as bass
import concourse.mybir as mybir

ENGS = ["pe", "act", "dve", "pool", "sp"]
N_DMA_SEM = 16
import os as _os
SAME_ENGINE_SYNC = _os.environ.get("KSES", "1") == "1"


class Buf:
    __slots__ = ("name", "writer", "readers", "dma_readers", "excl")

    def __init__(self, name, excl=False):
        self.name = name
        self.excl = excl
        self.writer = None
        self.readers = {}
        self.dma_readers = []


class Op:
    __slots__ = ("eng", "fn", "idx", "waits", "signal", "is_dma", "dsem", "dtarget", "clock", "sigcount", "uid")


class Prog:
    def __init__(self):
        self.ops = {e: [] for e in ENGS}
        self.clock = {e: {f: 0 for f in ENGS} for e in ENGS}
        self.known_dma = {e: set() for e in ENGS}
        self.dma_sem_count = [0] * N_DMA_SEM
        self.dma_sem_last = [None] * N_DMA_SEM
        self.dma_rr = 0
        self.n_dma = 0
        self.uid = 0
        self.bar = {e: [] for e in ENGS}

    def barrier(self):
        lasts = [self.ops[e][-1] for e in ENGS if self.ops[e] and not self.ops[e][-1].is_dma]
        for e in ENGS:
            pass
        lasts = []
        for e in ENGS:
            for op in reversed(self.ops[e]):
                if not op.is_dma:
                    lasts.append(op)
                    break
        dmas = [op for op in self.dma_sem_last if op is not None]
        for e in ENGS:
            self.bar[e] = lasts + dmas

    def add(self, eng, fn, reads=(), writes=(), dma=False):
        op = Op()
        op.eng = eng
        op.fn = fn
        op.idx = len(self.ops[eng])
        op.waits = []
        op.signal = False
        op.is_dma = dma
        op.uid = self.uid
        self.uid += 1
        def _flat(bs):
            o = []
            for b in bs:
                if isinstance(b, (list, tuple)):
                    o.extend(b)
                else:
                    o.append(b)
            return o
        reads = _flat(reads)
        writes = _flat(writes)
        deps = []
        for b in reads:
            if b.writer is not None:
                deps.append(b.writer)
            if b.excl:
                for e2, r in b.readers.items():
                    if e2 != eng:
                        deps.append(r)
        for b in writes:
            if b.writer is not None:
                deps.append(b.writer)
            deps.extend(b.readers.values())
            deps.extend(b.dma_readers)
        if self.bar[eng]:
            deps.extend(self.bar[eng])
            self.bar[eng] = []
        clk = self.clock[eng]
        seen = set()
        for d in deps:
            if d.uid in seen:
                continue
            seen.add(d.uid)
            if d.is_dma:
                if d.uid not in self.known_dma[eng]:
                    op.waits.append(("dma", d.dsem, d.dtarget))
                    self.known_dma[eng].add(d.uid)
            else:
                if d.eng == eng and (eng == "pe" or not SAME_ENGINE_SYNC):
                    continue
                if clk[d.eng] < d.idx + 1:
                    op.waits.append(("eng", d))
                    d.signal = True
                    for f in ENGS:
                        if d.clock[f] > clk[f]:
                            clk[f] = d.clock[f]
                    if clk[d.eng] < d.idx + 1:
                        clk[d.eng] = d.idx + 1
        if dma and eng == "pool":
            pd = self.__dict__.setdefault("pool_dmas", [])
            if len(pd) >= 4:
                d = pd[-4]
                if d.uid not in self.known_dma[eng]:
                    op.waits.append(("dma", d.dsem, d.dtarget))
                    self.known_dma[eng].add(d.uid)
            pd.append(op)
        if dma:
            k = self.dma_rr
            self.dma_rr = (self.dma_rr + 1) % N_DMA_SEM
            prev = self.dma_sem_last[k]
            if prev is not None and prev.uid not in self.known_dma[eng]:
                op.waits.append(("dma", k, prev.dtarget))
                self.known_dma[eng].add(prev.uid)
            self.dma_sem_count[k] += 16
            op.dsem = k
            op.dtarget = self.dma_sem_count[k]
            self.dma_sem_last[k] = op
            self.n_dma += 1
        op.clock = dict(clk)
        if not SAME_ENGINE_SYNC or eng == "pe":
            pass
        self.ops[eng].append(op)
        for b in writes:
            b.writer = op
            b.readers = {}
            b.dma_readers = []
        for b in reads:
            if dma:
                b.dma_readers.append(op)
            else:
                b.readers[eng] = op
        return op

    def emit(self, nc, final_waits=True):
        for e in ENGS:
            c = 0
            for op in self.ops[e]:
                if op.signal:
                    c += 1
                op.sigcount = c
        from contextlib import ExitStack
        with ExitStack() as es:
            esem = {e: es.enter_context(nc.semaphore("s_" + e)) for e in ENGS}
            dsem = [es.enter_context(nc.semaphore("d%d" % i)) for i in range(N_DMA_SEM)]
            block = es.enter_context(nc.Block())
            last_dma = [op for op in self.dma_sem_last if op is not None]

            def run(e, h):
                for op in self.ops[e]:
                    for w in op.waits:
                        if w[0] == "dma":
                            h.wait_ge(dsem[w[1]], w[2])
                        else:
                            h.wait_ge(esem[w[1].eng], w[1].sigcount)
                    ins = op.fn(h)
                    if op.is_dma:
                        ins.then_inc(dsem[op.dsem], 16)
                    elif op.signal:
                        ins.then_inc(esem[e], 1)
                if e == "sp" and final_waits:
                    for k in range(N_DMA_SEM):
                        if self.dma_sem_count[k] > 0:
                            h.wait_ge(dsem[k], self.dma_sem_count[k])

            @block.tensor
            def _(h):
                run("pe", h)

            @block.scalar
            def _(h):
                run("act", h)

            @block.vector
            def _(h):
                run("dve", h)

            @block.gpsimd
            def _(h):
                run("pool", h)

            @block.sync
            def _(h):
                run("sp", h)

from contextlib import ExitStack
import ml_dtypes
from concourse.bass_utils import run_bass_kernel_spmd

F32 = mybir.dt.float32
BF16 = mybir.dt.bfloat16
AF = mybir.ActivationFunctionType
ALU = mybir.AluOpType

NCORE = 8
TOK = 2048
NPRE = 112
NM1 = 17
NM2 = 18
ALPHA = 2.0 ** 0.25
COL_Z, COL_X, COL_B, COL_C, COL_DT, COL_Q, COL_K, COL_V, COL_G = 0, 2048, 4096, 4608, 5120, 5152, 6176, 6304, 6432


class TT:
    def __init__(self, ap, name, excl=False):
        self.ap = ap
        self.b = Buf(name, excl)


class Arena:
    def __init__(self, t, n):
        self.t, self.n, self.off = t, n, 0

    def get(self, name, *fs):
        n = int(np.prod(fs))
        ap = self.t[:, self.off:self.off + n]
        self.off += n
        assert self.off <= self.n, (name, self.off, self.n)
        if len(fs) == 2:
            ap = ap.rearrange("p (a b) -> p a b", a=fs[0])
        elif len(fs) == 3:
            ap = ap.rearrange("p (a b c) -> p a b c", a=fs[0], b=fs[1])
        return TT(ap, name)


def bc(ap2, n):
    return ap2.unsqueeze(2).to_broadcast([ap2.shape[0], ap2.shape[1], n])


def build(stop=4, npre=NPRE, dbg=False):
    nc = bass.Bass("TRN2", target_bir_lowering=False)
    dt_in = lambda n, s: nc.dram_tensor(n, s, F32, kind="ExternalInput").ap()
    xpre = dt_in("xpre", [NPRE * 128, 1024])
    xmain = dt_in("xmain", [NM2 * 128, 1024])
    pflag = dt_in("pflag", [128, NPRE + NM1])
    w_in = dt_in("w_in", [1024, 8480])
    b_gate = dt_in("b_gate", [1, 2048])
    scw = dt_in("scw", [96, 128])
    scb = dt_in("scb", [24, 128])
    dtb = dt_in("dtb", [1, 32])
    alog = dt_in("alog", [1, 32])
    dsk = dt_in("dsk", [1, 32])
    normw = dt_in("normw", [1, 2048])
    sinks = dt_in("sinks", [1, 16])
    w_bs = dt_in("w_bs", [2048, 1024])
    w_ba = dt_in("w_ba", [1024, 1024])
    w_mix = dt_in("w_mix", [1024, 1024])
    ln1g = dt_in("ln1g", [1, 1024]); ln1b = dt_in("ln1b", [1, 1024])
    ln2g = dt_in("ln2g", [1, 1024]); ln2b = dt_in("ln2b", [1, 1024])
    w_up = dt_in("w_up", [1024, 5632])
    fcw = dt_in("fcw", [132, 128])
    fcb = dt_in("fcb", [44, 128])
    w_dn = dt_in("w_dn", [2816, 1024])
    cst = dt_in("cst", [128, 4 * 128])
    biasg = dt_in("biasg", [128, 2 * 16 * 128])
    maskg = dt_in("maskg", [128, 2 * 16 * 128])
    out = nc.dram_tensor("out", [TOK, 1024], F32, kind="ExternalOutput").ap()
    skind = "ExternalOutput" if dbg else "Internal"
    ynd = nc.dram_tensor("ynd", [NM1 * 128, 2048], BF16, kind=skind).ap()
    yad = nc.dram_tensor("yad", [NM1 * 128, 1024], BF16, kind=skind).ap()
    h1d = nc.dram_tensor("h1d", [NM1 * 128, 1024], F32, kind=skind).ap()

    P = Prog()
    es = ExitStack()
    NB, NF = 164 * 512, 40 * 256
    ABt = es.enter_context(nc.sbuf_tensor("AB", [128, NB], BF16))
    AFt = es.enter_context(nc.sbuf_tensor("AF", [128, NF], F32))
    pf = [TT(es.enter_context(nc.psum_tensor("pf%d" % i, [128, 512], F32))[:], "pf%d" % i, True) for i in range(6)]
    for t_ in pf:
        t_.b = [Buf(t_.b.name + "q%d" % q_, True) for q_ in range(4)]
    pb = [TT(es.enter_context(nc.psum_tensor("pb%d" % i, [128, 1024], BF16))[:], "pb%d" % i, True) for i in range(2)]
    AB = Arena(ABt, NB)
    AFa = Arena(AFt, NF)

    def bcast_row(src, n):
        return bass.AP(src.tensor, 0, [[0, 128], [1, n]])

    def dma(eng, o, i, reads, writes):
        P.add(eng, lambda e, o=o, i=i: e.dma_start(out=o, in_=i), reads=reads, writes=writes, dma=True)

    cf = AFa.get("cf", 4, 128)
    dma("sp", cf.ap, cst.rearrange("p (a b) -> p a b", a=4), [], [cf.b])
    identf, triU, Ustr, onesf = cf.ap[:, 0, :], cf.ap[:, 1, :], cf.ap[:, 2, :], cf.ap[:, 3, :]
    cb16 = AB.get("cb16", 4, 128)
    dma("pool", cb16.ap, cst.rearrange("p (a b) -> p a b", a=4), [], [cb16.b])
    identb, maskb = cb16.ap[:, 0, :], cb16.ap[:, 1, :]
    flg = AFa.get("flg", NPRE + NM1)
    dma("sp", flg.ap, pflag, [], [flg.b])
    smallp = AFa.get("smallp", 6, 32)
    dma("sp", smallp.ap[:, 0, :], bcast_row(dtb, 32), [], [smallp.b])
    dma("sp", smallp.ap[:, 1, :], bcast_row(alog, 32), [], [smallp.b])
    dma("sp", smallp.ap[:, 2, :], bcast_row(dsk, 32), [], [smallp.b])
    dma("sp", smallp.ap[:, 3, 0:16], bcast_row(sinks, 16), [], [smallp.b])
    P.add("act", lambda e: e.activation(out=smallp.ap[:, 1, :], in_=smallp.ap[:, 1, :], func=AF.Exp), reads=[smallp.b], writes=[smallp.b])
    P.add("dve", lambda e: e.tensor_scalar(out=smallp.ap[:, 1, :], in0=smallp.ap[:, 1, :], scalar1=-1.0, scalar2=None, op0=ALU.mult), reads=[smallp.b], writes=[smallp.b])
    P.add("act", lambda e: e.activation(out=smallp.ap[:, 3, 0:16], in_=smallp.ap[:, 3, 0:16], func=AF.Exp), reads=[smallp.b], writes=[smallp.b])
    dtb_bc, a_bc, D_bc, esink = smallp.ap[:, 0, :], smallp.ap[:, 1, :], smallp.ap[:, 2, :], smallp.ap[:, 3, 0:16]
    onecol = onesf[:, 0:1]
    rawh = AB.get("rawh", 24, 4)
    markB, markF = AB.off, AFa.off
    H = AFa.get("H", 2048)

    def load_w(dst, src, ncols, nk=8):
        nk = src.shape[0] // 128
        for c0 in range(0, ncols, 512):
            c1 = min(c0 + 512, ncols)
            for k in range(nk):
                dma("pool", dst.ap[:, k, c0:c1], src[k * 128:(k + 1) * 128, c0:c1], [], [dst.b])

    def load_xT(xrow_ap, xb, xT):
        dma("pool", xb.ap, xrow_ap, [], [xb.b])
        for k in range(8):
            P.add("pe", lambda e, k=k: e.transpose(pb[0].ap[:, k * 128:(k + 1) * 128], xb.ap[:, k * 128:(k + 1) * 128], identb), reads=[xb.b, cb16.b], writes=[pb[0].b])
        P.add("act", lambda e: e.copy(out=xT.ap.rearrange("p a b -> p (a b)"), in_=pb[0].ap), reads=[pb[0].b], writes=[xT.b])

    def chain(bank, out_ap, pairs, reads, wb=None):
        n = len(pairs)
        wb = [bank.b] if wb is None else wb
        for i, (l, r) in enumerate(pairs):
            P.add("pe", lambda e, l=l, r=r, i=i: e.matmul(out_ap, l, r, start=(i == 0), stop=(i == n - 1)), reads=reads, writes=wb)

    def per_channel(dst, src_dram, rows):
        tmp = AFa.get("pc_tmp", 128)
        dma("sp", tmp.ap[0:rows, :], src_dram, [], [tmp.b])
        P.add("pe", lambda e: e.transpose(pf[5].ap[:, 0:rows], tmp.ap[0:rows, :], identf[0:rows, 0:rows]), reads=[tmp.b, cf.b], writes=[pf[5].b])
        P.add("dve", lambda e: e.tensor_copy(out=dst, in_=pf[5].ap[:, 0:rows]), reads=[pf[5].b], writes=[])

    import os
    SKIP1 = int(os.environ.get('KSKIP1', '0'))
    Wx = AB.get("Wx", 8, 3072); load_w(Wx, w_in[:, COL_X:COL_X + 3072], 3072)
    Wdt = AB.get("Wdt", 8, 32); load_w(Wdt, w_in[:, COL_DT:COL_DT + 32], 32)
    cwt = AFa.get("cwt", 96); cbt = AFa.get("cbt", 24)
    per_channel(cwt.ap, scw, 96)
    per_channel(cbt.ap, scb, 24)
    cwt.b.writer = P.ops["dve"][-2]; cbt.b.writer = P.ops["dve"][-1]
    diag = AB.get("diag", 24, 4, 128)
    for j in range(24):
        for tp in range(4):
            P.add("dve", lambda e, j=j, tp=tp: e.tensor_scalar(out=diag.ap[:, j, tp, :], in0=identf, scalar1=cwt.ap[:, tp * 24 + j:tp * 24 + j + 1], scalar2=None, op0=ALU.mult), reads=[cf.b, cwt.b], writes=[diag.b])
    P.add("dve", lambda e: e.memset(H.ap, 0.0), writes=[H.b])
    mark1B, mark1F = AB.off, AFa.off
    xbA = [AB.get("xbA%d" % i, 1024) for i in range(2)]
    xT4 = [AB.get("xT4_%d" % i, 8, 512) for i in range(2)]
    raw4 = AB.get("raw4", 24, 516); xc4 = AB.get("xc4", 24, 512)
    rawb = [Buf("rawb%d" % j) for j in range(24)]; xcb = [Buf("xcb%d" % j) for j in range(24)]
    xtk = [AB.get("xtk%d" % i, 2560) for i in range(2)]
    xdtsA4 = [AB.get("xdtsA%d" % i, 512) for i in range(8)]
    P.add("pool", lambda e: e.memset(raw4.ap, 0.0), writes=rawb)

    def load_group(gi, buf):
        for q in range(4):
            c = gi * 4 + q
            xb_ = xbA[q % 2]
            dma("pool", xb_.ap, xpre[c * 128:(c + 1) * 128, :], [], [xb_.b])
            for k in range(8):
                P.add("pe", lambda e, k=k, xb_=xb_: e.transpose(pb[0].ap[:, k * 128:(k + 1) * 128], xb_.ap[:, k * 128:(k + 1) * 128], identb), reads=[xb_.b, cb16.b], writes=[pb[0].b])
            P.add("act", lambda e, q=q, buf=buf: e.copy(out=xT4[buf].ap[:, :, q * 128:(q + 1) * 128], in_=pb[0].ap.rearrange("p (a b) -> p a b", a=8)), reads=[pb[0].b], writes=[xT4[buf].b])

    def proj_in(gi, buf, last, j0, j1):
        ntile = 24 if last else 20
        for j in range(j0, min(j1, ntile)):
            bank = pf[j % 2]
            chain(bank, bank.ap, [(Wx.ap[:, k, j * 128:(j + 1) * 128], xT4[buf].ap[:, k, :]) for k in range(8)], [Wx.b, xT4[buf].b])
            P.add("act", lambda e, j=j, bank=bank: e.copy(out=raw4.ap[:, j, 3:515], in_=bank.ap), reads=[bank.b], writes=[rawb[j]])

    def proj_conv(gi, last):
        ntile = 24 if last else 20
        for j in range(ntile):
            bank = pf[2 + j % 2]
            chain(bank, bank.ap, [(diag.ap[:, j, tp, :], raw4.ap[:, j, tp:tp + 512]) for tp in range(4)], [diag.b, rawb[j]])
            P.add("act", lambda e, j=j, bank=bank: e.activation(out=xc4.ap[:, j, :], in_=bank.ap, func=AF.Silu, bias=cbt.ap[:, j:j + 1], scale=1.0), reads=[bank.b, cbt.b], writes=[xcb[j]])
        P.add("pool", lambda e: e.tensor_copy(out=raw4.ap[:, :, 0:3], in_=raw4.ap[:, :, 512:515]), reads=rawb, writes=rawb)

    smGs = [AFa.get("smG%d" % i, 8, 128) for i in range(2)]

    def group_chunks(gi, buf, hooks):
        c0 = gi * 4
        smG = smGs[gi % 2]
        S = lambda i: smG.ap[:, i, :]
        S3 = lambda i: smG.ap[:, i, :].rearrange("p (q h) -> p q h", q=4)
        b4 = lambda ap: ap.unsqueeze(1).to_broadcast([128, 4, 32])
        v3 = lambda ap: ap.rearrange("p (a b) -> p a b", a=8)
        for q in range(4):
            chain(pf[4], pf[4].ap[:, q * 32:(q + 1) * 32], [(xT4[buf].ap[:, k, q * 128:(q + 1) * 128], Wdt.ap[:, k, :]) for k in range(8)], [xT4[buf].b, Wdt.b])
        P.add("dve", lambda e: e.tensor_tensor(out=S3(0), in0=pf[4].ap[:, 0:128].rearrange("p (q h) -> p q h", q=4), in1=b4(dtb_bc), op=ALU.add), reads=[pf[4].b, smallp.b], writes=[smG.b])
        P.add("act", lambda e: e.activation(out=S(0), in_=S(0), func=AF.Exp), reads=[smG.b], writes=[smG.b])
        P.add("act", lambda e: e.activation(out=S(1), in_=S(0), func=AF.Ln, bias=onecol, scale=1.0), reads=[smG.b, cf.b], writes=[smG.b])
        P.add("dve", lambda e: e.tensor_tensor(out=S3(1), in0=S3(1), in1=bc(flg.ap[:, c0:c0 + 4], 32), op=ALU.mult), reads=[smG.b, flg.b], writes=[smG.b])
        P.add("dve", lambda e: e.tensor_tensor(out=S3(2), in0=S3(1), in1=b4(a_bc), op=ALU.mult), reads=[smG.b, smallp.b], writes=[smG.b])

        def tr(q):
            xt = xtk[q % 2]
            for bt in range(3):
                nt = 8 if bt < 2 else 4
                pbk = pb[bt % 2]
                for i in range(nt):
                    jj = bt * 8 + i
                    P.add("pe", lambda e, i=i, jj=jj, q=q, pbk=pbk: e.transpose(pbk.ap[:, i * 128:(i + 1) * 128], xc4.ap[:, jj, q * 128:(q + 1) * 128], identb), reads=[xcb[jj], cb16.b], writes=[pbk.b])
                if bt == 1:
                    P.add("act", lambda e, bt=bt, nt=nt, xt=xt, pbk=pbk: e.copy(out=xt.ap[:, bt * 1024:bt * 1024 + nt * 128], in_=pbk.ap[:, 0:nt * 128]), reads=[pbk.b], writes=[xt.b])
                else:
                    P.add("dve", lambda e, bt=bt, nt=nt, xt=xt, pbk=pbk: e.tensor_copy(out=xt.ap[:, bt * 1024:bt * 1024 + nt * 128], in_=pbk.ap[:, 0:nt * 128]), reads=[pbk.b], writes=[xt.b])

        SBK = [pf[2], pf[3], pf[5], pf[4]]

        def state(q):
            xt = xtk[q % 2]
            for g in range(4):
                xg = xt.ap[:, g * 512:(g + 1) * 512].rearrange("p (a b) -> p a b", a=8)
                o = q * 32 + 8 * g
                xd = xdtsA4[(q % 2) * 4 + g]
                P.add("dve", lambda e, xg=xg, o=o, xd=xd: e.tensor_tensor(out=v3(xd.ap), in0=xg, in1=bc(smG.ap[:, 6, o:o + 8], 64), op=ALU.mult), reads=[xt.b, smG.b], writes=[xd.b])
                P.add("pe", lambda e, g=g, xt=xt, xd=xd, q=q: e.matmul(SBK[g].ap, xt.ap[:, 2048 + g * 128:2048 + (g + 1) * 128], xd.ap, start=(q == 0), stop=(q == 3)), reads=[xt.b, xd.b], writes=[SBK[g].b])
            if q == 3:
                P.add("dve", lambda e: e.tensor_tensor(out=H.ap.rearrange("p (a b) -> p a b", a=32), in0=H.ap.rearrange("p (a b) -> p a b", a=32), in1=bc(smG.ap[:, 7, 0:32], 64), op=ALU.mult), reads=[H.b, smG.b], writes=[H.b])
                for g in range(4):
                    Hg = H.ap[:, g * 512:(g + 1) * 512]
                    P.add("dve", lambda e, Hg=Hg, g=g: e.tensor_tensor(out=Hg, in0=Hg, in1=SBK[g].ap, op=ALU.add), reads=[H.b, SBK[g].b], writes=[H.b])

        tr(0); tr(1)
        hooks[0]()
        P.add("pe", lambda e: e.matmul(pf[4].ap[:, 128:256], triU, S(2), start=True, stop=True), reads=[cf.b, smG.b], writes=[pf[4].b])
        P.add("pe", lambda e: e.matmul(pf[4].ap[:, 256:384], onesf, S(2), start=True, stop=True), reads=[cf.b, smG.b], writes=[pf[4].b])
        P.add("dve", lambda e: e.tensor_copy(out=smG.ap[:, 3:5, :], in_=pf[4].ap[:, 128:384].rearrange("p (a b) -> p a b", a=2)), reads=[pf[4].b], writes=[smG.b])
        P.add("dve", lambda e: e.memset(S3(5)[:, 3, :], 0.0), writes=[smG.b])
        P.add("dve", lambda e: e.tensor_copy(out=S3(5)[:, 2, :], in_=S3(4)[:, 3, :]), reads=[smG.b], writes=[smG.b])
        P.add("dve", lambda e: e.tensor_tensor(out=S3(5)[:, 1, :], in0=S3(5)[:, 2, :], in1=S3(4)[:, 2, :], op=ALU.add), reads=[smG.b], writes=[smG.b])
        P.add("dve", lambda e: e.tensor_tensor(out=S3(5)[:, 0, :], in0=S3(5)[:, 1, :], in1=S3(4)[:, 1, :], op=ALU.add), reads=[smG.b], writes=[smG.b])
        P.add("dve", lambda e: e.tensor_tensor(out=S3(7)[:, 0, :], in0=S3(5)[:, 0, :], in1=S3(4)[:, 0, :], op=ALU.add), reads=[smG.b], writes=[smG.b])
        P.add("dve", lambda e: e.tensor_tensor(out=S(6), in0=S(4), in1=S(3), op=ALU.subtract), reads=[smG.b], writes=[smG.b])
        P.add("dve", lambda e: e.tensor_tensor(out=S(6), in0=S(6), in1=S(5), op=ALU.add), reads=[smG.b], writes=[smG.b])
        P.add("act", lambda e: e.activation(out=S(6), in_=S(6), func=AF.Exp), reads=[smG.b], writes=[smG.b])
        P.add("act", lambda e: e.activation(out=S3(7)[:, 0, :], in_=S3(7)[:, 0, :], func=AF.Exp), reads=[smG.b], writes=[smG.b])
        P.add("dve", lambda e: e.tensor_tensor(out=S(6), in0=S(6), in1=S(1), op=ALU.mult), reads=[smG.b], writes=[smG.b])
        state(0); state(1)
        hooks[1]()
        tr(2); tr(3)
        hooks[2]()
        state(2); state(3)
        hooks[3]()

    NG = NPRE // 4
    g0 = NG - (npre + 3) // 4
    if not SKIP1 and g0 < NG:
        load_group(g0, g0 % 2)
        proj_in(g0, g0 % 2, g0 == NG - 1, 0, 24)
        for gi in range(g0, NG):
            proj_conv(gi, gi == NG - 1)
            if gi + 1 < NG:
                load_group(gi + 1, (gi + 1) % 2)
                hk = [lambda a=a, gi=gi: proj_in(gi + 1, (gi + 1) % 2, gi + 1 == NG - 1, a, a + 6) for a in (0, 6, 12, 18)]
            else:
                hk = [lambda: None] * 4
            group_chunks(gi, gi % 2, hk)
        P.add("pool", lambda e: e.tensor_copy(out=rawh.ap[:, :, 0:3], in_=raw4.ap[:, :, 0:3]), reads=rawb, writes=[rawh.b])
    else:
        P.add("pool", lambda e: e.memset(rawh.ap, 0.0), writes=[rawh.b])

    P.barrier()
    AB.off, AFa.off = mark1B, mark1F
    Wz = AB.get("Wz", 8, 2048); load_w(Wz, w_in[:, COL_Z:COL_Z + 2048], 2048)
    nwb = AFa.get("nwb", 2048)
    dma("sp", nwb.ap, bcast_row(normw, 2048), [], [nwb.b])
    xbs = [AB.get("xb_%d" % i, 1024) for i in range(2)]; xTs = [AB.get("xT_%d" % i, 8, 128) for i in range(2)]
    raws = [AB.get("raw_%d" % i, 24, 132) for i in range(2)]; xcs = [AB.get("xc_%d" % i, 24, 128) for i in range(2)]
    xtok = AB.get("xtok", 2560)
    xdt = AB.get("xdt", 512); xdts = AB.get("xdts", 512)
    Hb = AB.get("Hb", 2048); cbm = AB.get("cbm", 4, 128)
    LT = AB.get("LT", 4, 128); MT = AB.get("MT", 8, 128); yn = AB.get("yn", 2048)
    szb = AB.get("szb", 4, 512)
    adtU = [AFa.get("adtU%d" % i, 128) for i in range(8)]
    sm = AFa.get("sm", 12, 32)
    yacc = AFa.get("yacc", 512); ytmp = AFa.get("ytmp", 512); sz = AFa.get("sz", 512)
    ssq = AFa.get("ssq", 4)
    P.add("pool", lambda e: e.memset(raws[0].ap, 0.0), writes=[raws[0].b])
    P.add("pool", lambda e: e.memset(raws[1].ap, 0.0), writes=[raws[1].b])
    P.add("pool", lambda e: e.tensor_copy(out=raws[0].ap[:, :, 0:3], in_=rawh.ap[:, :, 0:3]), reads=[rawh.b], writes=[raws[0].b])

    CUT = int(os.environ.get('KCUT', '99')); NCH1 = int(os.environ.get('KNCH', str(NM1)))
    def front_pieces(ci, xrow, xb, xT, raw, xc, raw_next):
        def inproj(g):
            bank = pf[g % 2]
            for jj in range(4):
                j = 4 * g + jj
                chain(bank, bank.ap[:, jj * 128:(jj + 1) * 128], [(Wx.ap[:, k, j * 128:(j + 1) * 128], xT.ap[:, k, :]) for k in range(8)], [Wx.b, xT.b])
            P.add("act", lambda e, g=g, bank=bank: e.copy(out=raw.ap[:, 4 * g:4 * g + 4, 3:131], in_=bank.ap.rearrange("p (a b) -> p a b", a=4)), reads=[bank.b], writes=[raw.b])

        def conv(g):
            for jj in range(4):
                j = 4 * g + jj
                bank = pf[2 + j % 2]
                chain(bank, bank.ap[:, 0:128], [(diag.ap[:, j, tp, :], raw.ap[:, j, tp:tp + 128]) for tp in range(4)], [diag.b, raw.b])
                P.add("act", lambda e, j=j, bank=bank: e.activation(out=xc.ap[:, j, :], in_=bank.ap[:, 0:128], func=AF.Silu, bias=cbt.ap[:, j:j + 1], scale=1.0), reads=[bank.b, cbt.b], writes=[xc.b])

        def p0():
            load_xT(xrow, xb, xT); inproj(0); inproj(1)

        def p1():
            inproj(2); inproj(3)

        def p2():
            inproj(4); inproj(5)
            P.add("pool", lambda e: e.tensor_copy(out=raw_next.ap[:, :, 0:3], in_=raw.ap[:, :, 128:131]), reads=[raw.b], writes=[raw_next.b])

        def p3():
            conv(0); conv(1); conv(2); conv(3); conv(4); conv(5)
        return [p0, p1, p2, p3]

    def ssd_back(fcol, main, ci, xT, xc, hooks):
        for bt in range(3):
            nt = 8 if bt < 2 else 4
            for i in range(nt):
                j = bt * 8 + i
                P.add("pe", lambda e, i=i, j=j: e.transpose(pb[1].ap[:, i * 128:(i + 1) * 128], xc.ap[:, j, :], identb), reads=[xc.b, cb16.b], writes=[pb[1].b])
            P.add("dve", lambda e, bt=bt, nt=nt: e.tensor_copy(out=xtok.ap[:, bt * 1024:bt * 1024 + nt * 128], in_=pb[1].ap[:, 0:nt * 128]), reads=[pb[1].b], writes=[xtok.b])
        chain(pf[4], pf[4].ap[:, 0:32], [(xT.ap[:, k, :], Wdt.ap[:, k, :]) for k in range(8)], [xT.b, Wdt.b])
        S = lambda i: sm.ap[:, i, :]
        P.add("dve", lambda e: e.tensor_tensor(out=S(0), in0=pf[4].ap[:, 0:32], in1=dtb_bc, op=ALU.add), reads=[pf[4].b, smallp.b], writes=[sm.b])
        P.add("act", lambda e: e.activation(out=S(0), in_=S(0), func=AF.Exp), reads=[sm.b], writes=[sm.b])
        P.add("act", lambda e: e.activation(out=S(1), in_=S(0), func=AF.Ln, bias=onecol, scale=1.0), reads=[sm.b, cf.b], writes=[sm.b])
        P.add("dve", lambda e: e.tensor_scalar(out=S(1), in0=S(1), scalar1=fcol, scalar2=None, op0=ALU.mult), reads=[sm.b, flg.b], writes=[sm.b])
        P.add("dve", lambda e: e.tensor_tensor(out=S(2), in0=S(1), in1=a_bc, op=ALU.mult), reads=[sm.b, smallp.b], writes=[sm.b])
        P.add("pe", lambda e: e.matmul(pf[4].ap[:, 32:64], triU, S(2), start=True, stop=True), reads=[cf.b, sm.b], writes=[pf[4].b])
        P.add("pe", lambda e: e.matmul(pf[4].ap[:, 64:96], onesf, S(2), start=True, stop=True), reads=[cf.b, sm.b], writes=[pf[4].b])
        P.add("dve", lambda e: e.tensor_copy(out=sm.ap[:, 3:5, :], in_=pf[4].ap[:, 32:96].rearrange("p (a b) -> p a b", a=2)), reads=[pf[4].b], writes=[sm.b])
        if main:
            for g in range(4):
                P.add("pe", lambda e, g=g: e.matmul(pf[5].ap[:, g * 128:(g + 1) * 128], xc.ap[:, 16 + g, :], xc.ap[:, 20 + g, :], start=True, stop=True), reads=[xc.b], writes=[pf[5].b])
            P.add("dve", lambda e: e.tensor_tensor(out=cbm.ap, in0=pf[5].ap.rearrange("p (a b) -> p a b", a=4), in1=maskb.unsqueeze(1).to_broadcast([128, 4, 128]), op=ALU.mult), reads=[pf[5].b, cb16.b], writes=[cbm.b])
            P.add("pool", lambda e: e.tensor_copy(out=Hb.ap, in_=H.ap), reads=[H.b], writes=[Hb.b])
            P.add("act", lambda e: e.activation(out=S(5), in_=S(3), func=AF.Exp), reads=[sm.b], writes=[sm.b])
            for g in range(4):
                zb = pf[5 - g % 2]
                chain(zb, zb.ap, [(xT.ap[:, k, :], Wz.ap[:, k, g * 512:(g + 1) * 512]) for k in range(8)], [xT.b, Wz.b])
                P.add("act", lambda e, g=g, zb=zb: e.activation(out=szb.ap[:, g, :], in_=zb.ap, func=AF.Silu), reads=[zb.b], writes=[szb.b])
            for g in range(4):
                xg = xtok.ap[:, g * 512:(g + 1) * 512].rearrange("p (a b) -> p a b", a=8)
                P.add("dve", lambda e, g=g, xg=xg: e.tensor_tensor(out=xdt.ap.rearrange("p (a b) -> p a b", a=8), in0=xg, in1=bc(sm.ap[:, 1, 8 * g:8 * g + 8], 64), op=ALU.mult), reads=[xtok.b, sm.b], writes=[xdt.b])
                for hh in range(2):
                    bank = pf[hh]
                    for h4 in range(4):
                        h = 8 * g + 4 * hh + h4
                        au = adtU[h % 8]
                        P.add("dve", lambda e, h=h, au=au: e.tensor_scalar(out=au.ap, in0=Ustr, scalar1=sm.ap[:, 2, h:h + 1], scalar2=None, op0=ALU.mult), reads=[cf.b, sm.b], writes=[au.b])
                        P.add("pe", lambda e, h4=h4, au=au, bank=bank: e.matmul(bank.ap[:, h4 * 128:(h4 + 1) * 128], au.ap, triU, start=True, stop=True), reads=[au.b, cf.b], writes=[bank.b])
                    P.add("act", lambda e, bank=bank: e.activation(out=LT.ap.rearrange("p a b -> p (a b)"), in_=bank.ap, func=AF.Exp), reads=[bank.b], writes=[LT.b])
                    P.add("dve", lambda e, g=g, hh=hh: e.tensor_tensor(out=MT.ap[:, 4 * hh:4 * hh + 4, :], in0=LT.ap, in1=cbm.ap[:, g, :].unsqueeze(1).to_broadcast([128, 4, 128]), op=ALU.mult), reads=[LT.b, cbm.b], writes=[MT.b])
                for h8 in range(8):
                    P.add("pe", lambda e, h8=h8: e.matmul(pf[2].ap[:, h8 * 64:(h8 + 1) * 64], MT.ap[:, h8, :], xdt.ap[:, h8 * 64:(h8 + 1) * 64], start=True, stop=True), reads=[MT.b, xdt.b], writes=[pf[2].b])
                P.add("pe", lambda e, g=g: e.matmul(pf[3].ap, xc.ap[:, 20 + g, :], Hb.ap[:, g * 512:(g + 1) * 512], start=True, stop=True), reads=[xc.b, Hb.b], writes=[pf[3].b])
                v3 = lambda ap: ap.rearrange("p (a b) -> p a b", a=8)
                P.add("dve", lambda e, g=g: e.tensor_tensor(out=v3(yacc.ap), in0=v3(pf[3].ap), in1=bc(sm.ap[:, 5, 8 * g:8 * g + 8], 64), op=ALU.mult), reads=[pf[3].b, sm.b], writes=[yacc.b])
                P.add("dve", lambda e: e.tensor_tensor(out=yacc.ap, in0=yacc.ap, in1=pf[2].ap, op=ALU.add), reads=[pf[2].b, yacc.b], writes=[yacc.b])
                P.add("dve", lambda e, g=g, xg=xg: e.tensor_tensor(out=v3(ytmp.ap), in0=xg, in1=bc(D_bc[:, 8 * g:8 * g + 8], 64), op=ALU.mult), reads=[xtok.b, smallp.b], writes=[ytmp.b])
                P.add("dve", lambda e: e.tensor_tensor(out=yacc.ap, in0=yacc.ap, in1=ytmp.ap, op=ALU.add), reads=[ytmp.b, yacc.b], writes=[yacc.b])
                P.add("dve", lambda e, g=g: e.tensor_tensor(out=yacc.ap, in0=yacc.ap, in1=szb.ap[:, g, :], op=ALU.mult), reads=[szb.b, yacc.b], writes=[yacc.b])
                P.add("dve", lambda e: e.tensor_tensor(out=ytmp.ap, in0=yacc.ap, in1=yacc.ap, op=ALU.mult), reads=[yacc.b], writes=[ytmp.b])
                P.add("dve", lambda e, g=g: e.reduce_sum(out=ssq.ap[:, g:g + 1], in_=ytmp.ap, axis=mybir.AxisListType.X), reads=[ytmp.b], writes=[ssq.b])
                P.add("dve", lambda e, g=g: e.tensor_scalar(out=ssq.ap[:, g:g + 1], in0=ssq.ap[:, g:g + 1], scalar1=1.0 / 512, scalar2=1e-5, op0=ALU.mult, op1=ALU.add), reads=[ssq.b], writes=[ssq.b])
                P.add("act", lambda e, g=g: e.activation(out=ssq.ap[:, g:g + 1], in_=ssq.ap[:, g:g + 1], func=AF.Ln), reads=[ssq.b], writes=[ssq.b])
                P.add("act", lambda e, g=g: e.activation(out=ssq.ap[:, g:g + 1], in_=ssq.ap[:, g:g + 1], func=AF.Exp, scale=-0.5), reads=[ssq.b], writes=[ssq.b])
                P.add("dve", lambda e, g=g: e.scalar_tensor_tensor(out=yn.ap[:, g * 512:(g + 1) * 512], in0=yacc.ap, scalar=ssq.ap[:, g:g + 1], in1=nwb.ap[:, g * 512:(g + 1) * 512], op0=ALU.mult, op1=ALU.mult), reads=[yacc.b, ssq.b, nwb.b], writes=[yn.b])
                hooks[g]()
            dma("sp", ynd[ci * 128:(ci + 1) * 128, :], yn.ap, [yn.b], [])
        P.add("dve", lambda e: e.tensor_tensor(out=S(6), in0=S(4), in1=S(3), op=ALU.subtract), reads=[sm.b], writes=[sm.b])
        P.add("act", lambda e: e.activation(out=S(6), in_=S(6), func=AF.Exp), reads=[sm.b], writes=[sm.b])
        P.add("act", lambda e: e.activation(out=S(7), in_=S(4), func=AF.Exp), reads=[sm.b], writes=[sm.b])
        P.add("dve", lambda e: e.tensor_tensor(out=S(6), in0=S(6), in1=S(1), op=ALU.mult), reads=[sm.b], writes=[sm.b])
        for g in range(4):
            xg = xtok.ap[:, g * 512:(g + 1) * 512].rearrange("p (a b) -> p a b", a=8)
            v3 = lambda ap: ap.rearrange("p (a b) -> p a b", a=8)
            P.add("dve", lambda e, g=g, xg=xg: e.tensor_tensor(out=v3(xdts.ap), in0=xg, in1=bc(sm.ap[:, 6, 8 * g:8 * g + 8], 64), op=ALU.mult), reads=[xtok.b, sm.b], writes=[xdts.b])
            bank = pf[2 + g % 2]
            P.add("pe", lambda e, g=g, bank=bank: e.matmul(bank.ap, xtok.ap[:, 2048 + g * 128:2048 + (g + 1) * 128], xdts.ap, start=True, stop=True), reads=[xtok.b, xdts.b], writes=[bank.b])
            Hg = H.ap[:, g * 512:(g + 1) * 512]
            P.add("dve", lambda e, g=g, Hg=Hg: e.tensor_tensor(out=v3(Hg), in0=v3(Hg), in1=bc(sm.ap[:, 7, 8 * g:8 * g + 8], 64), op=ALU.mult), reads=[H.b, sm.b], writes=[H.b])
            P.add("dve", lambda e, Hg=Hg, bank=bank: e.tensor_tensor(out=Hg, in0=Hg, in1=bank.ap, op=ALU.add), reads=[H.b, bank.b], writes=[H.b])

    def fp(ci):
        b = ci % 2
        return front_pieces(ci, xmain[(ci + 1) * 128:(ci + 2) * 128, :], xbs[b], xTs[b], raws[b], xcs[b], raws[1 - b])
    nch = 0 if SKIP1 else NCH1
    if nch:
        for p_ in fp(0):
            p_()
    for ci in range(nch):
        nxt = fp(ci + 1) if ci + 1 < nch else [lambda: None] * 4
        ssd_back(flg.ap[:, NPRE + ci:NPRE + ci + 1], True, ci, xTs[ci % 2], xcs[ci % 2], nxt)

    if stop < 2:
        P.emit(nc); es.close(); return nc
    P.barrier()
    AB.off, AFa.off = markB, markF
    Wq = AB.get("Wq", 8, 1024); load_w(Wq, w_in[:, COL_Q:COL_Q + 1024], 1024)
    Wk2 = AB.get("Wk2", 8, 128); load_w(Wk2, w_in[:, COL_K:COL_K + 128], 128)
    Wv = AB.get("Wv", 8, 128); load_w(Wv, w_in[:, COL_V:COL_V + 128], 128)
    EB = AB.get("EB", 2, 16, 128)
    ebf = AFa.get("ebf", 2048); mkf = AFa.get("mkf", 2048)
    for kt in range(2):
        dma("sp", ebf.ap, biasg[:, kt * 2048:(kt + 1) * 2048], [], [ebf.b])
        dma("sp", mkf.ap, maskg[:, kt * 2048:(kt + 1) * 2048], [], [mkf.b])
        P.add("act", lambda e: e.activation(out=ebf.ap, in_=ebf.ap, func=AF.Exp), reads=[ebf.b], writes=[ebf.b])
        P.add("dve", lambda e, kt=kt: e.tensor_tensor(out=EB.ap[:, kt, :, :].rearrange("p a b -> p (a b)"), in0=ebf.ap, in1=mkf.ap, op=ALU.mult), reads=[ebf.b, mkf.b], writes=[EB.b])
    xb = AB.get("xb2", 1024); xT = AB.get("xT2", 8, 128)
    kT = [AB.get("kT%d" % i, 2, 128) for i in range(2)]
    vx = [AB.get("vx%d" % i, 2, 65) for i in range(2)]
    for i in range(2):
        P.add("pool", lambda e, i=i: e.memset(vx[i].ap, 1.0), writes=[vx[i].b])
    qT = AB.get("qT", 16, 128); et = AB.get("et", 4, 128)
    PT = [AB.get("PT%d" % i, 4, 128) for i in range(2)]
    ya = AB.get("ya", 1024)
    den = AFa.get("den", 4)
    flag0 = flg.ap[:, NPRE:NPRE + 1]
    CUT2 = int(os.environ.get('KCUT2', '99')); NCH2 = int(os.environ.get('KNCH2', str(NM2)))
    for ci in range(NCH2):
        sl = ci % 2
        load_xT(xmain[ci * 128:(ci + 1) * 128, :], xb, xT)
        for kv in range(2):
            chain(pf[0], pf[0].ap[0:64, kv * 128:(kv + 1) * 128], [(Wk2.ap[:, k, kv * 64:(kv + 1) * 64], xT.ap[:, k, :]) for k in range(8)], [Wk2.b, xT.b])
        P.add("act", lambda e, sl=sl: e.copy(out=kT[sl].ap[0:64, :, :], in_=pf[0].ap[0:64, 0:256].rearrange("p (a b) -> p a b", a=2)), reads=[pf[0].b], writes=[kT[sl].b])
        chain(pf[1], pf[1].ap[:, 0:128], [(xT.ap[:, k, :], Wv.ap[:, k, :]) for k in range(8)], [xT.b, Wv.b])
        P.add("dve", lambda e, sl=sl: e.tensor_copy(out=vx[sl].ap[:, :, 0:64], in_=pf[1].ap[:, 0:128].rearrange("p (a b) -> p a b", a=2)), reads=[pf[1].b], writes=[vx[sl].b])
        if ci == 0 or CUT2 <= 1:
            continue
        for q4 in range(4):
            bank = pf[2 + q4 % 2]
            for tt in range(4):
                j = q4 * 4 + tt
                chain(bank, bank.ap[0:64, tt * 128:(tt + 1) * 128], [(Wq.ap[:, k, j * 64:(j + 1) * 64], xT.ap[:, k, :]) for k in range(8)], [Wq.b, xT.b])
            P.add("act", lambda e, q4=q4, bank=bank: e.copy(out=qT.ap[0:64, q4 * 4:q4 * 4 + 4, :], in_=bank.ap[0:64, :].rearrange("p (a b) -> p a b", a=4)), reads=[bank.b], writes=[qT.b])
        for kvh in range(2):
            if CUT2 <= 2: break
            for hb in range(2):
                j0 = kvh * 8 + hb * 4
                for kt in range(2):
                    slk = (ci + 1 + kt) % 2
                    bank = pf[4 + kt]
                    for i in range(4):
                        j = j0 + i
                        base = (j % 2) * 64 * int(os.environ.get("KB64", "1"))
                        P.add("pe", lambda e, i=i, j=j, base=base, slk=slk, bank=bank, kvh=kvh: e.matmul(bank.ap[:, i * 128:(i + 1) * 128], kT[slk].ap[0:64, kvh, :], qT.ap[0:64, j, :], start=True, stop=True), reads=[kT[slk].b, qT.b], writes=[bank.b])
                    P.add("act", lambda e, bank=bank: e.activation(out=et.ap.rearrange("p a b -> p (a b)"), in_=bank.ap, func=AF.Exp, scale=0.125), reads=[bank.b], writes=[et.b])
                    if ci == 2 and kt == 0:
                        P.add("dve", lambda e, kt=kt, j0=j0: e.scalar_tensor_tensor(out=PT[kt].ap, in0=et.ap, scalar=flag0, in1=EB.ap[:, kt, j0:j0 + 4, :], op0=ALU.mult, op1=ALU.mult), reads=[et.b, EB.b, flg.b], writes=[PT[kt].b])
                    else:
                        P.add("dve", lambda e, kt=kt, j0=j0: e.tensor_tensor(out=PT[kt].ap, in0=et.ap, in1=EB.ap[:, kt, j0:j0 + 4, :], op=ALU.mult), reads=[et.b, EB.b], writes=[PT[kt].b])
                if CUT2 <= 3: continue
                bank = pf[hb]
                for i in range(4):
                    for kt in range(2):
                        slk = (ci + 1 + kt) % 2
                        P.add("pe", lambda e, i=i, kt=kt, slk=slk, bank=bank, kvh=kvh: e.matmul(bank.ap[:, i * 65:(i + 1) * 65], PT[kt].ap[:, i, :], vx[slk].ap[:, kvh, :], start=(kt == 0), stop=(kt == 1)), reads=[PT[kt].b, vx[slk].b], writes=[bank.b])
                if CUT2 <= 4: continue
                pv = bank.ap[:, 0:260].rearrange("p (a b) -> p a b", a=4)
                P.add("dve", lambda e, pv=pv, j0=j0: e.tensor_tensor(out=den.ap, in0=pv[:, :, 64], in1=esink[:, j0:j0 + 4], op=ALU.add), reads=[bank.b, smallp.b], writes=[den.b])
                P.add("dve", lambda e: e.reciprocal(out=den.ap, in_=den.ap), reads=[den.b], writes=[den.b])
                P.add("dve", lambda e, pv=pv, j0=j0: e.tensor_tensor(out=ya.ap[:, j0 * 64:(j0 + 4) * 64].rearrange("p (a b) -> p a b", a=4), in0=pv[:, :, 0:64], in1=bc(den.ap, 64), op=ALU.mult), reads=[bank.b, den.b], writes=[ya.b])
        dma("sp", yad[(ci - 1) * 128:ci * 128, :], ya.ap, [ya.b], [])

    if stop < 3:
        P.emit(nc); es.close(); return nc
    P.barrier()
    AB.off, AFa.off = markB, markF
    Wg = AB.get("Wg", 8, 2048); load_w(Wg, w_in[:, COL_G:COL_G + 2048], 2048)
    Wbs = AB.get("Wbs", 16, 1024); load_w(Wbs, w_bs, 1024)
    Wba = AB.get("Wba", 8, 1024); load_w(Wba, w_ba, 1024)
    Wmx = AB.get("Wmx", 8, 1024); load_w(Wmx, w_mix, 1024)
    bgb = AFa.get("bgb", 2048); dma("sp", bgb.ap, bcast_row(b_gate, 2048), [], [bgb.b])
    lng = AFa.get("lng", 2, 1024)
    dma("sp", lng.ap[:, 0, :], bcast_row(ln1g, 1024), [], [lng.b]); dma("sp", lng.ap[:, 1, :], bcast_row(ln1b, 1024), [], [lng.b])
    xb = AB.get("xb3", 1024); xT = AB.get("xT3", 8, 128)
    ynb = AB.get("ynb", 2048); yab = AB.get("yab", 1024)
    ynT = AB.get("ynT", 16, 128); yaT = AB.get("yaT", 8, 128)
    mg = AB.get("mg", 1024); mT = AB.get("mT", 8, 128)
    xf = AFa.get("xf", 1024); gt = AFa.get("gt", 2048); m1 = AFa.get("m1", 512); r = AFa.get("r", 1024)
    st = AFa.get("st", 2, 6); mv = AFa.get("mv", 2)

    def transp(src, dst, ntl):
        for bt in range(ntl // 8):
            for i in range(8):
                j = bt * 8 + i
                P.add("pe", lambda e, i=i, j=j: e.transpose(pb[1].ap[:, i * 128:(i + 1) * 128], src.ap[:, j * 128:(j + 1) * 128], identb), reads=[src.b, cb16.b], writes=[pb[1].b])
            P.add("act", lambda e, bt=bt: e.copy(out=dst.ap[:, bt * 8:bt * 8 + 8, :].rearrange("p a b -> p (a b)"), in_=pb[1].ap), reads=[pb[1].b], writes=[dst.b])

    def layer_norm(r, g_ap, b_ap, gb, st, mv):
        for i in range(2):
            P.add("dve", lambda e, i=i: e.bn_stats(out=st.ap[:, i, :], in_=r.ap[:, i * 512:(i + 1) * 512]), reads=[r.b], writes=[st.b])
        P.add("dve", lambda e: e.bn_aggr(out=mv.ap, in_=st.ap.rearrange("p a b -> p (a b)")), reads=[st.b], writes=[mv.b])
        P.add("dve", lambda e: e.tensor_scalar(out=mv.ap[:, 1:2], in0=mv.ap[:, 1:2], scalar1=1e-5, scalar2=None, op0=ALU.add), reads=[mv.b], writes=[mv.b])
        P.add("act", lambda e: e.activation(out=mv.ap[:, 1:2], in_=mv.ap[:, 1:2], func=AF.Sqrt), reads=[mv.b], writes=[mv.b])
        P.add("dve", lambda e: e.reciprocal(out=mv.ap[:, 1:2], in_=mv.ap[:, 1:2]), reads=[mv.b], writes=[mv.b])
        P.add("dve", lambda e: e.tensor_scalar(out=r.ap, in0=r.ap, scalar1=mv.ap[:, 0:1], scalar2=mv.ap[:, 1:2], op0=ALU.subtract, op1=ALU.mult), reads=[r.b, mv.b], writes=[r.b])
        P.add("dve", lambda e: e.tensor_tensor(out=r.ap, in0=r.ap, in1=g_ap, op=ALU.mult), reads=[r.b, gb], writes=[r.b])
        P.add("dve", lambda e: e.tensor_tensor(out=r.ap, in0=r.ap, in1=b_ap, op=ALU.add), reads=[r.b, gb], writes=[r.b])

    for ci in range(NM1):
        xrow = xmain[(ci + 1) * 128:(ci + 2) * 128, :]
        load_xT(xrow, xb, xT)
        dma("sp", xf.ap, xrow, [], [xf.b])
        dma("sp", ynb.ap, ynd[ci * 128:(ci + 1) * 128, :], [], [ynb.b])
        dma("sp", yab.ap, yad[ci * 128:(ci + 1) * 128, :], [], [yab.b])
        transp(ynb, ynT, 16)
        transp(yab, yaT, 8)
        for s4 in range(4):
            bank = pf[s4 % 2]
            chain(bank, bank.ap, [(xT.ap[:, k, :], Wg.ap[:, k, s4 * 512:(s4 + 1) * 512]) for k in range(8)], [xT.b, Wg.b])
            P.add("dve", lambda e, s4=s4, bank=bank: e.tensor_tensor(out=gt.ap[:, s4 * 512:(s4 + 1) * 512], in0=bank.ap, in1=bgb.ap[:, s4 * 512:(s4 + 1) * 512], op=ALU.add), reads=[bank.b, bgb.b], writes=[gt.b])
        P.add("act", lambda e: e.activation(out=gt.ap, in_=gt.ap, func=AF.Sigmoid), reads=[gt.b], writes=[gt.b])
        for hf in range(2):
            chain(pf[2], pf[2].ap, [(ynT.ap[:, i, :], Wbs.ap[:, i, hf * 512:(hf + 1) * 512]) for i in range(16)], [ynT.b, Wbs.b])
            chain(pf[3], pf[3].ap, [(yaT.ap[:, i, :], Wba.ap[:, i, hf * 512:(hf + 1) * 512]) for i in range(8)], [yaT.b, Wba.b])
            P.add("dve", lambda e, hf=hf: e.tensor_tensor(out=m1.ap, in0=pf[2].ap, in1=gt.ap[:, hf * 512:(hf + 1) * 512], op=ALU.mult), reads=[pf[2].b, gt.b], writes=[m1.b])
            P.add("dve", lambda e, hf=hf: e.tensor_tensor(out=r.ap[:, hf * 512:(hf + 1) * 512], in0=pf[3].ap, in1=gt.ap[:, 1024 + hf * 512:1024 + (hf + 1) * 512], op=ALU.mult), reads=[pf[3].b, gt.b], writes=[r.b])
            P.add("dve", lambda e, hf=hf: e.tensor_tensor(out=mg.ap[:, hf * 512:(hf + 1) * 512], in0=m1.ap, in1=r.ap[:, hf * 512:(hf + 1) * 512], op=ALU.add), reads=[m1.b, r.b], writes=[mg.b])
        transp(mg, mT, 8)
        for hf in range(2):
            bank = pf[4 + hf]
            chain(bank, bank.ap, [(mT.ap[:, i, :], Wmx.ap[:, i, hf * 512:(hf + 1) * 512]) for i in range(8)], [mT.b, Wmx.b])
            P.add("dve", lambda e, hf=hf, bank=bank: e.scalar_tensor_tensor(out=r.ap[:, hf * 512:(hf + 1) * 512], in0=xf.ap[:, hf * 512:(hf + 1) * 512], scalar=ALPHA, in1=bank.ap, op0=ALU.mult, op1=ALU.add), reads=[xf.b, bank.b], writes=[r.b])
        layer_norm(r, lng.ap[:, 0, :], lng.ap[:, 1, :], lng.b, st, mv)
        if ci == 0:
            P.add("dve", lambda e: e.tensor_scalar(out=r.ap, in0=r.ap, scalar1=flag0, scalar2=None, op0=ALU.mult), reads=[r.b, flg.b], writes=[r.b])
        dma("sp", h1d[ci * 128:(ci + 1) * 128, :], r.ap, [r.b], [])

    if stop < 4:
        P.emit(nc); es.close(); return nc
    P.barrier()
    AB.off, AFa.off = markB, markF
    Wup = AB.get("Wup", 8, 5632); load_w(Wup, w_up, 5632)
    Wdn = AB.get("Wdn", 22, 1024); load_w(Wdn, w_dn, 1024)
    fw = AFa.get("fw", 132); fb = AFa.get("fb", 44)
    per_channel(fw.ap[:, 0:88], fcw[0:88, :], 88); fw.b.writer = P.ops["dve"][-1]
    per_channel(fw.ap[:, 88:132], fcw[88:132, :], 44); fw.b.writer = P.ops["dve"][-1]
    per_channel(fb.ap, fcb, 44); fb.b.writer = P.ops["dve"][-1]
    lng2 = AFa.get("lng2", 2, 1024)
    dma("sp", lng2.ap[:, 0, :], bcast_row(ln2g, 1024), [], [lng2.b]); dma("sp", lng2.ap[:, 1, :], bcast_row(ln2b, 1024), [], [lng2.b])
    SC = 256
    hA = AB.get("hA", 1024); hB = AB.get("hB", 1024); h1T = AB.get("h1T", 8, SC + 2)
    aT = AB.get("aT", 22, SC)
    aTb = [Buf("aTb%d" % i) for i in range(22)]
    cvs = [AFa.get("cvs%d" % i, SC) for i in range(4)]
    t0s = [AFa.get("t0s%d" % i, SC) for i in range(2)]
    sg = AFa.get("sg", SC)
    hr = AFa.get("hr", 1024); r4 = AFa.get("r4", 1024)
    st4 = AFa.get("st4", 2, 6); mv4 = AFa.get("mv4", 2)
    NT = SC // 128
    tcount = 0
    for sc in range(TOK // SC):
        r0 = 128 + sc * SC
        for i in range(NT):
            dma("pool", hA.ap, h1d[r0 - 2 + i * 128:r0 + 126 + i * 128, :], [], [hA.b])
            for k in range(8):
                P.add("pe", lambda e, k=k: e.transpose(pb[0].ap[:, k * 128:(k + 1) * 128], hA.ap[:, k * 128:(k + 1) * 128], identb), reads=[hA.b, cb16.b], writes=[pb[0].b])
            P.add("act", lambda e, i=i: e.copy(out=h1T.ap[:, :, i * 128:(i + 1) * 128], in_=pb[0].ap.rearrange("p (a b) -> p a b", a=8)), reads=[pb[0].b], writes=[h1T.b])
        dma("pool", hB.ap[0:2, :], h1d[r0 + SC - 2:r0 + SC, :], [], [hB.b])
        for k in range(8):
            P.add("pe", lambda e, k=k: e.transpose(pb[1].ap[:, k * 2:(k + 1) * 2], hB.ap[0:2, k * 128:(k + 1) * 128], identb[0:2, 0:2]), reads=[hB.b, cb16.b], writes=[pb[1].b])
        P.add("act", lambda e: e.copy(out=h1T.ap[:, :, SC:SC + 2], in_=pb[1].ap[:, 0:16].rearrange("p (a b) -> p a b", a=8)), reads=[pb[1].b], writes=[h1T.b])
        def down(jp):
            for tt in range(NT):
                for hf in range(2):
                    bank = pf[2 + tt * 2 + hf]
                    P.add("pe", lambda e, jp=jp, tt=tt, hf=hf, bank=bank: e.matmul(bank.ap, aT.ap[:, jp, tt * 128:(tt + 1) * 128], Wdn.ap[:, jp, hf * 512:(hf + 1) * 512], start=(jp == 0), stop=(jp == 21)), reads=[aTb[jp], Wdn.b], writes=[bank.b])

        for jp in range(22):
            for gv in range(2):
                j = jp + 22 * gv
                X = pf[tcount % 2]; t0 = t0s[tcount % 2]
                cv = cvs[(jp % 2) * 2 + gv]
                tcount += 1
                chain(X, X.ap[:, 0:SC + 2], [(Wup.ap[:, k, j * 128:(j + 1) * 128], h1T.ap[:, k, 0:SC + 2]) for k in range(8)], [Wup.b, h1T.b])
                w0, w1, w2, bb = fw.ap[:, j:j + 1], fw.ap[:, 44 + j:45 + j], fw.ap[:, 88 + j:89 + j], fb.ap[:, j:j + 1]
                P.add("act", lambda e, X=X, t0=t0, w2=w2, bb=bb: e.activation(out=t0.ap, in_=X.ap[:, 2:SC + 2], func=AF.Identity, bias=bb, scale=w2), reads=[X.b, fw.b, fb.b], writes=[t0.b])
                P.add("dve", lambda e, X=X, t0=t0, w1=w1: e.scalar_tensor_tensor(out=t0.ap, in0=X.ap[:, 1:SC + 1], scalar=w1, in1=t0.ap, op0=ALU.mult, op1=ALU.add), reads=[X.b, fw.b, t0.b], writes=[t0.b])
                P.add("dve", lambda e, X=X, t0=t0, w0=w0, cv=cv: e.scalar_tensor_tensor(out=cv.ap, in0=X.ap[:, 0:SC], scalar=w0, in1=t0.ap, op0=ALU.mult, op1=ALU.add), reads=[X.b, fw.b, t0.b], writes=[cv.b])
            cg, cvv = cvs[(jp % 2) * 2], cvs[(jp % 2) * 2 + 1]
            P.add("act", lambda e, cg=cg: e.activation(out=sg.ap, in_=cg.ap, func=AF.Silu), reads=[cg.b], writes=[sg.b])
            P.add("dve", lambda e, jp=jp, cvv=cvv: e.tensor_tensor(out=aT.ap[:, jp, :], in0=sg.ap, in1=cvv.ap, op=ALU.mult), reads=[sg.b, cvv.b], writes=[aTb[jp]])
            if jp >= 1:
                down(jp - 1)
        down(21)
        for tt in range(NT):
            rr = r0 + tt * 128
            dma("sp", hr.ap, h1d[rr:rr + 128, :], [], [hr.b])
            for hf in range(2):
                bank = pf[2 + tt * 2 + hf]
                P.add("dve", lambda e, hf=hf, bank=bank: e.scalar_tensor_tensor(out=r4.ap[:, hf * 512:(hf + 1) * 512], in0=hr.ap[:, hf * 512:(hf + 1) * 512], scalar=ALPHA, in1=bank.ap, op0=ALU.mult, op1=ALU.add), reads=[hr.b, bank.b], writes=[r4.b])
            layer_norm(r4, lng2.ap[:, 0, :], lng2.ap[:, 1, :], lng2.b, st4, mv4)
            dma("sp", out[rr - 128:rr, :], r4.ap, [r4.b], [])

    P.emit(nc)
    es.close()
    return nc


def rel_bucket_np(rel):
    n = np.maximum(rel, 0)
    nf = np.maximum(n, 1).astype(np.float32)
    large = 16 + (np.log(nf / np.float32(16)) / np.float32(np.log(128 / 16)) * np.float32(16)).astype(np.int32)
    large = np.minimum(large, 31)
    return np.where(n < 16, n, large)


_NC = None


def kernel(_dbg=None, **inp):
    global _NC
    x = np.asarray(inp["x"], np.float32)[0]
    f = lambda k: np.ascontiguousarray(np.asarray(inp[k], np.float32)[0])
    common = {
        "w_in": f("w_in"), "b_gate": f("b_gate")[None], "dtb": f("ssm_dt_bias")[None], "alog": f("ssm_a_log")[None],
        "dsk": f("ssm_d")[None], "normw": f("ssm_norm_w")[None], "sinks": f("attn_sinks")[None],
        "w_bs": f("w_branch_ssm"), "w_ba": f("w_branch_attn"), "w_mix": f("w_mix_out"),
        "ln1g": f("ln1_g")[None], "ln1b": f("ln1_b")[None], "ln2g": f("ln2_g")[None], "ln2b": f("ln2_b")[None],
        "w_up": f("w_up"), "w_dn": f("w_down"),
    }
    scw = f("ssm_conv_w")
    common["scw"] = np.ascontiguousarray(scw.reshape(4 * 24, 128))
    common["scb"] = np.ascontiguousarray(f("ssm_conv_b").reshape(24, 128))
    common["fcw"] = np.ascontiguousarray(f("ffn_conv_w").reshape(3 * 44, 128))
    common["fcb"] = np.ascontiguousarray(f("ffn_conv_b").reshape(44, 128))
    s = np.arange(128)
    ident = np.eye(128, dtype=np.float32)
    triU = (s[:, None] <= s[None, :]).astype(np.float32)
    ustr = (s[:, None] > s[None, :]).astype(np.float32)
    common["cst"] = np.ascontiguousarray(np.concatenate([ident, triU, ustr, np.ones((128, 128), np.float32)], axis=1))
    rb = np.asarray(inp["rel_bias"], np.float32)
    bg = np.zeros((128, 2, 16, 128), np.float32); mk = np.zeros((128, 2, 16, 128), np.float32)
    for kt in range(2):
        rel = (s[None, :] + 128) - (s[:, None] + 128 * kt)
        valid = (rel >= 0) & (rel < 128)
        bidx = rel_bucket_np(rel)
        g = rb[bidx]
        bg[:, kt] = np.transpose(g, (0, 2, 1))
        mk[:, kt] = np.broadcast_to(valid[:, None, :], (128, 16, 128))
    common["biasg"] = np.ascontiguousarray(bg.reshape(128, -1)); common["maskg"] = np.ascontiguousarray(mk.reshape(128, -1))
    in_maps = []
    for c in range(NCORE):
        S = c * TOK
        lo = S - 128 - NPRE * 128
        xp = np.zeros((NPRE * 128, 1024), np.float32)
        if S - 128 > 0:
            src_lo = max(lo, 0)
            xp[src_lo - lo:] = x[src_lo:S - 128]
        xm = np.zeros((NM2 * 128, 1024), np.float32)
        lo2 = S - 256
        src_lo = max(lo2, 0)
        xm[src_lo - lo2:] = x[src_lo:S + TOK]
        fl = np.zeros((128, NPRE + NM1), np.float32)
        for i in range(NPRE):
            fl[:, i] = 1.0 if lo + i * 128 >= 0 else 0.0
        fl[:, NPRE] = 1.0 if c > 0 else 0.0
        fl[:, NPRE + 1:] = 1.0
        m = dict(common); m["xpre"] = xp; m["xmain"] = xm; m["pflag"] = fl
        in_maps.append(m)
    if _dbg is not None:
        return in_maps
    if _NC is None:
        _NC = build()
    res = run_bass_kernel_spmd(_NC, in_maps, core_ids=list(range(NCORE)))
    o = np.concatenate([res.results[c]["out"] for c in range(NCORE)], axis=0)
    return o[None].astype(np.float32)
```

```python
import numpy as np
import concourse.bass as bass
import concourse.mybir as mybir

ENGS = ["pe", "act", "dve", "pool", "sp"]
N_DMA_SEM = 16
import os as _os
SAME_ENGINE_SYNC = _os.environ.get("KSES", "1") == "1"


class Buf:
    __slots__ = ("name", "writer", "readers", "dma_readers", "excl")

    def __init__(self, name, excl=False):
        self.name = name
        self.excl = excl
        self.writer = None
        self.readers = {}
        self.dma_readers = []


class Op:
    __slots__ = ("eng", "fn", "idx", "waits", "signal", "is_dma", "dsem", "dtarget", "clock", "sigcount", "uid")


class Prog:
    def __init__(self):
        self.ops = {e: [] for e in ENGS}
        self.clock = {e: {f: 0 for f in ENGS} for e in ENGS}
        self.known_dma = {e: set() for e in ENGS}
        self.dma_sem_count = [0] * N_DMA_SEM
        self.dma_sem_last = [None] * N_DMA_SEM
        self.dma_rr = 0
        self.n_dma = 0
        self.uid = 0
        self.bar = {e: [] for e in ENGS}

    def barrier(self):
        lasts = [self.ops[e][-1] for e in ENGS if self.ops[e] and not self.ops[e][-1].is_dma]
        for e in ENGS:
            pass
        lasts = []
        for e in ENGS:
            for op in reversed(self.ops[e]):
                if not op.is_dma:
                    lasts.append(op)
                    break
        dmas = [op for op in self.dma_sem_last if op is not None]
        for e in ENGS:
            self.bar[e] = lasts + dmas

    def add(self, eng, fn, reads=(), writes=(), dma=False):
        op = Op()
        op.eng = eng
        op.fn = fn
        op.idx = len(self.ops[eng])
        op.waits = []
        op.signal = False
        op.is_dma = dma
        op.uid = self.uid
        self.uid += 1
        def _flat(bs):
            o = []
            for b in bs:
                if isinstance(b, (list, tuple)):
                    o.extend(b)
                else:
                    o.append(b)
            return o
        reads = _flat(reads)
        writes = _flat(writes)
        deps = []
        for b in reads:
            if b.writer is not None:
                deps.append(b.writer)
            if b.excl:
                for e2, r in b.readers.items():
                    if e2 != eng:
                        deps.append(r)
        for b in writes:
            if b.writer is not None:
                deps.append(b.writer)
            deps.extend(b.readers.values())
            deps.extend(b.dma_readers)
        if self.bar[eng]:
            deps.extend(self.bar[eng])
            self.bar[eng] = []
        clk = self.clock[eng]
        seen = set()
        for d in deps:
            if d.uid in seen:
                continue
            seen.add(d.uid)
            if d.is_dma:
                if d.uid not in self.known_dma[eng]:
                    op.waits.append(("dma", d.dsem, d.dtarget))
                    self.known_dma[eng].add(d.uid)
            else:
                if d.eng == eng and (eng == "pe" or not SAME_ENGINE_SYNC):
                    continue
                if clk[d.eng] < d.idx + 1:
                    op.waits.append(("eng", d))
                    d.signal = True
                    for f in ENGS:
                        if d.clock[f] > clk[f]:
                            clk[f] = d.clock[f]
                    if clk[d.eng] < d.idx + 1:
                        clk[d.eng] = d.idx + 1
        if dma and eng == "pool":
            pd = self.__dict__.setdefault("pool_dmas", [])
            if len(pd) >= 4:
                d = pd[-4]
                if d.uid not in self.known_dma[eng]:
                    op.waits.append(("dma", d.dsem, d.dtarget))
                    self.known_dma[eng].add(d.uid)
            pd.append(op)
        if dma:
            k = self.dma_rr
            self.dma_rr = (self.dma_rr + 1) % N_DMA_SEM
            prev = self.dma_sem_last[k]
            if prev is not None and prev.uid not in self.known_dma[eng]:
                op.waits.append(("dma", k, prev.dtarget))
                self.known_dma[eng].add(prev.uid)
            self.dma_sem_count[k] += 16
            op.dsem = k
            op.dtarget = self.dma_sem_count[k]
            self.dma_sem_last[k] = op
            self.n_dma += 1
        op.clock = dict(clk)
        if not SAME_ENGINE_SYNC or eng == "pe":
            pass
        self.ops[eng].append(op)
        for b in writes:
            b.writer = op
            b.readers = {}
            b.dma_readers = []
        for b in reads:
            if dma:
                b.dma_readers.append(op)
            else:
                b.readers[eng] = op
        return op

    def emit(self, nc, final_waits=True):
        for e in ENGS:
            c = 0
            for op in self.ops[e]:
                if op.signal:
                    c += 1
                op.sigcount = c
        from contextlib import ExitStack
        with ExitStack() as es:
            esem = {e: es.enter_context(nc.semaphore("s_" + e)) for e in ENGS}
            dsem = [es.enter_context(nc.semaphore("d%d" % i)) for i in range(N_DMA_SEM)]
            block = es.enter_context(nc.Block())
            last_dma = [op for op in self.dma_sem_last if op is not None]

            def run(e, h):
                for op in self.ops[e]:
                    for w in op.waits:
                        if w[0] == "dma":
                            h.wait_ge(dsem[w[1]], w[2])
                        else:
                            h.wait_ge(esem[w[1].eng], w[1].sigcount)
                    ins = op.fn(h)
                    if op.is_dma:
                        ins.then_inc(dsem[op.dsem], 16)
                    elif op.signal:
                        ins.then_inc(esem[e], 1)
                if e == "sp" and final_waits:
                    for k in range(N_DMA_SEM):
                        if self.dma_sem_count[k] > 0:
                            h.wait_ge(dsem[k], self.dma_sem_count[k])

            @block.tensor
            def _(h):
                run("pe", h)

            @block.scalar
            def _(h):
                run("act", h)

            @block.vector
            def _(h):
                run("dve", h)

            @block.gpsimd
            def _(h):
                run("pool", h)

            @block.sync
            def _(h):
                run("sp", h)

from contextlib import ExitStack
import ml_dtypes
from concourse.bass_utils import run_bass_kernel_spmd

F32 = mybir.dt.float32
BF16 = mybir.dt.bfloat16
AF = mybir.ActivationFunctionType
ALU = mybir.AluOpType

NCORE = 8
TOK = 2048
NPRE = 112
NM1 = 17
NM2 = 18
ALPHA = 2.0 ** 0.25
COL_Z, COL_X, COL_B, COL_C, COL_DT, COL_Q, COL_K, COL_V, COL_G = 0, 2048, 4096, 4608, 5120, 5152, 6176, 6304, 6432


class TT:
    def __init__(self, ap, name, excl=False):
        self.ap = ap
        self.b = Buf(name, excl)


class Arena:
    def __init__(self, t, n):
        self.t, self.n, self.off = t, n, 0

    def get(self, name, *fs):
        n = int(np.prod(fs))
        ap = self.t[:, self.off:self.off + n]
        self.off += n
        assert self.off <= self.n, (name, self.off, self.n)
        if len(fs) == 2:
            ap = ap.rearrange("p (a b) -> p a b", a=fs[0])
        elif len(fs) == 3:
            ap = ap.rearrange("p (a b c) -> p a b c", a=fs[0], b=fs[1])
        return TT(ap, name)


def bc(ap2, n):
    return ap2.unsqueeze(2).to_broadcast([ap2.shape[0], ap2.shape[1], n])


def build(stop=4, npre=NPRE, dbg=False):
    nc = bass.Bass("TRN2", target_bir_lowering=False)
    dt_in = lambda n, s: nc.dram_tensor(n, s, F32, kind="ExternalInput").ap()
    xpre = dt_in("xpre", [NPRE * 128, 1024])
    xmain = dt_in("xmain", [NM2 * 128, 1024])
    pflag = dt_in("pflag", [128, NPRE + NM1])
    w_in = dt_in("w_in", [1024, 8480])
    b_gate = dt_in("b_gate", [1, 2048])
    scw = dt_in("scw", [96, 128])
    scb = dt_in("scb", [24, 128])
    dtb = dt_in("dtb", [1, 32])
    alog = dt_in("alog", [1, 32])
    dsk = dt_in("dsk", [1, 32])
    normw = dt_in("normw", [1, 2048])
    sinks = dt_in("sinks", [1, 16])
    w_bs = dt_in("w_bs", [2048, 1024])
    w_ba = dt_in("w_ba", [1024, 1024])
    w_mix = dt_in("w_mix", [1024, 1024])
    ln1g = dt_in("ln1g", [1, 1024]); ln1b = dt_in("ln1b", [1, 1024])
    ln2g = dt_in("ln2g", [1, 1024]); ln2b = dt_in("ln2b", [1, 1024])
    w_up = dt_in("w_up", [1024, 5632])
    fcw = dt_in("fcw", [132, 128])
    fcb = dt_in("fcb", [44, 128])
    w_dn = dt_in("w_dn", [2816, 1024])
    cst = dt_in("cst", [128, 4 * 128])
    biasg = dt_in("biasg", [128, 2 * 16 * 128])
    maskg = dt_in("maskg", [128, 2 * 16 * 128])
    out = nc.dram_tensor("out", [TOK, 1024], F32, kind="ExternalOutput").ap()
    skind = "ExternalOutput" if dbg else "Internal"
    ynd = nc.dram_tensor("ynd", [NM1 * 128, 2048], BF16, kind=skind).ap()
    yad = nc.dram_tensor("yad", [NM1 * 128, 1024], BF16, kind=skind).ap()
    h1d = nc.dram_tensor("h1d", [NM1 * 128, 1024], F32, kind=skind).ap()

    P = Prog()
    es = ExitStack()
    NB, NF = 164 * 512, 40 * 256
    ABt = es.enter_context(nc.sbuf_tensor("AB", [128, NB], BF16))
    AFt = es.enter_context(nc.sbuf_tensor("AF", [128, NF], F32))
    pf = [TT(es.enter_context(nc.psum_tensor("pf%d" % i, [128, 512], F32))[:], "pf%d" % i, True) for i in range(6)]
    for t_ in pf:
        t_.b = [Buf(t_.b.name + "q%d" % q_, True) for q_ in range(4)]
    pb = [TT(es.enter_context(nc.psum_tensor("pb%d" % i, [128, 1024], BF16))[:], "pb%d" % i, True) for i in range(2)]
    AB = Arena(ABt, NB)
    AFa = Arena(AFt, NF)

    def bcast_row(src, n):
        return bass.AP(src.tensor, 0, [[0, 128], [1, n]])

    def dma(eng, o, i, reads, writes):
        P.add(eng, lambda e, o=o, i=i: e.dma_start(out=o, in_=i), reads=reads, writes=writes, dma=True)

    cf = AFa.get("cf", 4, 128)
    dma("sp", cf.ap, cst.rearrange("p (a b) -> p a b", a=4), [], [cf.b])
    identf, triU, Ustr, onesf = cf.ap[:, 0, :], cf.ap[:, 1, :], cf.ap[:, 2, :], cf.ap[:, 3, :]
    cb16 = AB.get("cb16", 4, 128)
    dma("pool", cb16.ap, cst.rearrange("p (a b) -> p a b", a=4), [], [cb16.b])
    identb, maskb = cb16.ap[:, 0, :], cb16.ap[:, 1, :]
    flg = AFa.get("flg", NPRE + NM1)
    dma("sp", flg.ap, pflag, [], [flg.b])
    smallp = AFa.get("smallp", 6, 32)
    dma("sp", smallp.ap[:, 0, :], bcast_row(dtb, 32), [], [smallp.b])
    dma("sp", smallp.ap[:, 1, :], bcast_row(alog, 32), [], [smallp.b])
    dma("sp", smallp.ap[:, 2, :], bcast_row(dsk, 32), [], [smallp.b])
    dma("sp", smallp.ap[:, 3, 0:16], bcast_row(sinks, 16), [], [smallp.b])
    P.add("act", lambda e: e.activation(out=smallp.ap[:, 1, :], in_=smallp.ap[:, 1, :], func=AF.Exp), reads=[smallp.b], writes=[smallp.b])
    P.add("dve", lambda e: e.tensor_scalar(out=smallp.ap[:, 1, :], in0=smallp.ap[:, 1, :], scalar1=-1.0, scalar2=None, op0=ALU.mult), reads=[smallp.b], writes=[smallp.b])
    P.add("act", lambda e: e.activation(out=smallp.ap[:, 3, 0:16], in_=smallp.ap[:, 3, 0:16], func=AF.Exp), reads=[smallp.b], writes=[smallp.b])
    dtb_bc, a_bc, D_bc, esink = smallp.ap[:, 0, :], smallp.ap[:, 1, :], smallp.ap[:, 2, :], smallp.ap[:, 3, 0:16]
    onecol = onesf[:, 0:1]
    rawh = AB.get("rawh", 24, 4)
    markB, markF = AB.off, AFa.off
    H = AFa.get("H", 2048)

    def load_w(dst, src, ncols, nk=8):
        nk = src.shape[0] // 128
        for c0 in range(0, ncols, 512):
            c1 = min(c0 + 512, ncols)
            for k in range(nk):
                dma("pool", dst.ap[:, k, c0:c1], src[k * 128:(k + 1) * 128, c0:c1], [], [dst.b])

    def load_xT(xrow_ap, xb, xT):
        dma("pool", xb.ap, xrow_ap, [], [xb.b])
        for k in range(8):
            P.add("pe", lambda e, k=k: e.transpose(pb[0].ap[:, k * 128:(k + 1) * 128], xb.ap[:, k * 128:(k + 1) * 128], identb), reads=[xb.b, cb16.b], writes=[pb[0].b])
        P.add("act", lambda e: e.copy(out=xT.ap.rearrange("p a b -> p (a b)"), in_=pb[0].ap), reads=[pb[0].b], writes=[xT.b])

    def chain(bank, out_ap, pairs, reads, wb=None):
        n = len(pairs)
        wb = [bank.b] if wb is None else wb
        for i, (l, r) in enumerate(pairs):
            P.add("pe", lambda e, l=l, r=r, i=i: e.matmul(out_ap, l, r, start=(i == 0), stop=(i == n - 1)), reads=reads, writes=wb)

    def per_channel(dst, src_dram, rows):
        tmp = AFa.get("pc_tmp", 128)
        dma("sp", tmp.ap[0:rows, :], src_dram, [], [tmp.b])
        P.add("pe", lambda e: e.transpose(pf[5].ap[:, 0:rows], tmp.ap[0:rows, :], identf[0:rows, 0:rows]), reads=[tmp.b, cf.b], writes=[pf[5].b])
        P.add("dve", lambda e: e.tensor_copy(out=dst, in_=pf[5].ap[:, 0:rows]), reads=[pf[5].b], writes=[])

    import os
    SKIP1 = int(os.environ.get('KSKIP1', '0'))
    Wx = AB.get("Wx", 8, 3072); load_w(Wx, w_in[:, COL_X:COL_X + 3072], 3072)
    Wdt = AB.get("Wdt", 8, 32); load_w(Wdt, w_in[:, COL_DT:COL_DT + 32], 32)
    cwt = AFa.get("cwt", 96); cbt = AFa.get("cbt", 24)
    per_channel(cwt.ap, scw, 96)
    per_channel(cbt.ap, scb, 24)
    cwt.b.writer = P.ops["dve"][-2]; cbt.b.writer = P.ops["dve"][-1]
    diag = AB.get("diag", 24, 4, 128)
    for j in range(24):
        for tp in range(4):
            P.add("dve", lambda e, j=j, tp=tp: e.tensor_scalar(out=diag.ap[:, j, tp, :], in0=identf, scalar1=cwt.ap[:, tp * 24 + j:tp * 24 + j + 1], scalar2=None, op0=ALU.mult), reads=[cf.b, cwt.b], writes=[diag.b])
    P.add("dve", lambda e: e.memset(H.ap, 0.0), writes=[H.b])
    mark1B, mark1F = AB.off, AFa.off
    xbA = [AB.get("xbA%d" % i, 1024) for i in range(2)]
    xT4 = [AB.get("xT4_%d" % i, 8, 512) for i in range(2)]
    raw4 = AB.get("raw4", 24, 516); xc4 = AB.get("xc4", 24, 512)
    rawb = [Buf("rawb%d" % j) for j in range(24)]; xcb = [Buf("xcb%d" % j) for j in range(24)]
    xtk = [AB.get("xtk%d" % i, 2560) for i in range(2)]
    xdtsA4 = [AB.get("xdtsA%d" % i, 512) for i in range(8)]
    P.add("pool", lambda e: e.memset(raw4.ap, 0.0), writes=rawb)

    def load_group(gi, buf):
        for q in range(4):
            c = gi * 4 + q
            xb_ = xbA[q % 2]
            dma("pool", xb_.ap, xpre[c * 128:(c + 1) * 128, :], [], [xb_.b])
            for k in range(8):
                P.add("pe", lambda e, k=k, xb_=xb_: e.transpose(pb[0].ap[:, k * 128:(k + 1) * 128], xb_.ap[:, k * 128:(k + 1) * 128], identb), reads=[xb_.b, cb16.b], writes=[pb[0].b])
            P.add("act", lambda e, q=q, buf=buf: e.copy(out=xT4[buf].ap[:, :, q * 128:(q + 1) * 128], in_=pb[0].ap.rearrange("p (a b) -> p a b", a=8)), reads=[pb[0].b], writes=[xT4[buf].b])

    def proj_in(gi, buf, last, j0, j1):
        ntile = 24 if last else 20
        for j in range(j0, min(j1, ntile)):
            bank = pf[j % 2]
            chain(bank, bank.ap, [(Wx.ap[:, k, j * 128:(j + 1) * 128], xT4[buf].ap[:, k, :]) for k in range(8)], [Wx.b, xT4[buf].b])
            P.add("act", lambda e, j=j, bank=bank: e.copy(out=raw4.ap[:, j, 3:515], in_=bank.ap), reads=[bank.b], writes=[rawb[j]])

    def proj_conv(gi, last):
        ntile = 24 if last else 20
        for j in range(ntile):
            bank = pf[2 + j % 2]
            chain(bank, bank.ap, [(diag.ap[:, j, tp, :], raw4.ap[:, j, tp:tp + 512]) for tp in range(4)], [diag.b, rawb[j]])
            P.add("act", lambda e, j=j, bank=bank: e.activation(out=xc4.ap[:, j, :], in_=bank.ap, func=AF.Silu, bias=cbt.ap[:, j:j + 1], scale=1.0), reads=[bank.b, cbt.b], writes=[xcb[j]])
        P.add("pool", lambda e: e.tensor_copy(out=raw4.ap[:, :, 0:3], in_=raw4.ap[:, :, 512:515]), reads=rawb, writes=rawb)

    smGs = [AFa.get("smG%d" % i, 8, 128) for i in range(2)]

    def group_chunks(gi, buf, hooks):
        c0 = gi * 4
        smG = smGs[gi % 2]
        S = lambda i: smG.ap[:, i, :]
        S3 = lambda i: smG.ap[:, i, :].rearrange("p (q h) -> p q h", q=4)
        b4 = lambda ap: ap.unsqueeze(1).to_broadcast([128, 4, 32])
        v3 = lambda ap: ap.rearrange("p (a b) -> p a b", a=8)
        for q in range(4):
            chain(pf[4], pf[4].ap[:, q * 32:(q + 1) * 32], [(xT4[buf].ap[:, k, q * 128:(q + 1) * 128], Wdt.ap[:, k, :]) for k in range(8)], [xT4[buf].b, Wdt.b])
        P.add("dve", lambda e: e.tensor_tensor(out=S3(0), in0=pf[4].ap[:, 0:128].rearrange("p (q h) -> p q h", q=4), in1=b4(dtb_bc), op=ALU.add), reads=[pf[4].b, smallp.b], writes=[smG.b])
        P.add("act", lambda e: e.activation(out=S(0), in_=S(0), func=AF.Exp), reads=[smG.b], writes=[smG.b])
        P.add("act", lambda e: e.activation(out=S(1), in_=S(0), func=AF.Ln, bias=onecol, scale=1.0), reads=[smG.b, cf.b], writes=[smG.b])
        P.add("dve", lambda e: e.tensor_tensor(out=S3(1), in0=S3(1), in1=bc(flg.ap[:, c0:c0 + 4], 32), op=ALU.mult), reads=[smG.b, flg.b], writes=[smG.b])
        P.add("dve", lambda e: e.tensor_tensor(out=S3(2), in0=S3(1), in1=b4(a_bc), op=ALU.mult), reads=[smG.b, smallp.b], writes=[smG.b])

        def tr(q):
            xt = xtk[q % 2]
            for bt in range(3):
                nt = 8 if bt < 2 else 4
                pbk = pb[bt % 2]
                for i in range(nt):
                    jj = bt * 8 + i
                    P.add("pe", lambda e, i=i, jj=jj, q=q, pbk=pbk: e.transpose(pbk.ap[:, i * 128:(i + 1) * 128], xc4.ap[:, jj, q * 128:(q + 1) * 128], identb), reads=[xcb[jj], cb16.b], writes=[pbk.b])
                if bt == 1:
                    P.add("act", lambda e, bt=bt, nt=nt, xt=xt, pbk=pbk: e.copy(out=xt.ap[:, bt * 1024:bt * 1024 + nt * 128], in_=pbk.ap[:, 0:nt * 128]), reads=[pbk.b], writes=[xt.b])
                else:
                    P.add("dve", lambda e, bt=bt, nt=nt, xt=xt, pbk=pbk: e.tensor_copy(out=xt.ap[:, bt * 1024:bt * 1024 + nt * 128], in_=pbk.ap[:, 0:nt * 128]), reads=[pbk.b], writes=[xt.b])

        SBK = [pf[2], pf[3], pf[5], pf[4]]

        def state(q):
            xt = xtk[q % 2]
            for g in range(4):
                xg = xt.ap[:, g * 512:(g + 1) * 512].rearrange("p (a b) -> p a b", a=8)
                o = q * 32 + 8 * g
                xd = xdtsA4[(q % 2) * 4 + g]
                P.add("dve", lambda e, xg=xg, o=o, xd=xd: e.tensor_tensor(out=v3(xd.ap), in0=xg, in1=bc(smG.ap[:, 6, o:o + 8], 64), op=ALU.mult), reads=[xt.b, smG.b], writes=[xd.b])
                P.add("pe", lambda e, g=g, xt=xt, xd=xd, q=q: e.matmul(SBK[g].ap, xt.ap[:, 2048 + g * 128:2048 + (g + 1) * 128], xd.ap, start=(q == 0), stop=(q == 3)), reads=[xt.b, xd.b], writes=[SBK[g].b])
            if q == 3:
                P.add("dve", lambda e: e.tensor_tensor(out=H.ap.rearrange("p (a b) -> p a b", a=32), in0=H.ap.rearrange("p (a b) -> p a b", a=32), in1=bc(smG.ap[:, 7, 0:32], 64), op=ALU.mult), reads=[H.b, smG.b], writes=[H.b])
                for g in range(4):
                    Hg = H.ap[:, g * 512:(g + 1) * 512]
                    P.add("dve", lambda e, Hg=Hg, g=g: e.tensor_tensor(out=Hg, in0=Hg, in1=SBK[g].ap, op=ALU.add), reads=[H.b, SBK[g].b], writes=[H.b])

        tr(0); tr(1)
        hooks[0]()
        P.add("pe", lambda e: e.matmul(pf[4].ap[:, 128:256], triU, S(2), start=True, stop=True), reads=[cf.b, smG.b], writes=[pf[4].b])
        P.add("pe", lambda e: e.matmul(pf[4].ap[:, 256:384], onesf, S(2), start=True, stop=True), reads=[cf.b, smG.b], writes=[pf[4].b])
        P.add("dve", lambda e: e.tensor_copy(out=smG.ap[:, 3:5, :], in_=pf[4].ap[:, 128:384].rearrange("p (a b) -> p a b", a=2)), reads=[pf[4].b], writes=[smG.b])
        P.add("dve", lambda e: e.memset(S3(5)[:, 3, :], 0.0), writes=[smG.b])
        P.add("dve", lambda e: e.tensor_copy(out=S3(5)[:, 2, :], in_=S3(4)[:, 3, :]), reads=[smG.b], writes=[smG.b])
        P.add("dve", lambda e: e.tensor_tensor(out=S3(5)[:, 1, :], in0=S3(5)[:, 2, :], in1=S3(4)[:, 2, :], op=ALU.add), reads=[smG.b], writes=[smG.b])
        P.add("dve", lambda e: e.tensor_tensor(out=S3(5)[:, 0, :], in0=S3(5)[:, 1, :], in1=S3(4)[:, 1, :], op=ALU.add), reads=[smG.b], writes=[smG.b])
        P.add("dve", lambda e: e.tensor_tensor(out=S3(7)[:, 0, :], in0=S3(5)[:, 0, :], in1=S3(4)[:, 0, :], op=ALU.add), reads=[smG.b], writes=[smG.b])
        P.add("dve", lambda e: e.tensor_tensor(out=S(6), in0=S(4), in1=S(3), op=ALU.subtract), reads=[smG.b], writes=[smG.b])
        P.add("dve", lambda e: e.tensor_tensor(out=S(6), in0=S(6), in1=S(5), op=ALU.add), reads=[smG.b], writes=[smG.b])
        P.add("act", lambda e: e.activation(out=S(6), in_=S(6), func=AF.Exp), reads=[smG.b], writes=[smG.b])
        P.add("act", lambda e: e.activation(out=S3(7)[:, 0, :], in_=S3(7)[:, 0, :], func=AF.Exp), reads=[smG.b], writes=[smG.b])
        P.add("dve", lambda e: e.tensor_tensor(out=S(6), in0=S(6), in1=S(1), op=ALU.mult), reads=[smG.b], writes=[smG.b])
        state(0); state(1)
        hooks[1]()
        tr(2); tr(3)
        hooks[2]()
        state(2); state(3)
        hooks[3]()

    NG = NPRE // 4
    g0 = NG - (npre + 3) // 4
    if not SKIP1 and g0 < NG:
        load_group(g0, g0 % 2)
        proj_in(g0, g0 % 2, g0 == NG - 1, 0, 24)
        for gi in range(g0, NG):
            proj_conv(gi, gi == NG - 1)
            if gi + 1 < NG:
                load_group(gi + 1, (gi + 1) % 2)
                hk = [lambda a=a, gi=gi: proj_in(gi + 1, (gi + 1) % 2, gi + 1 == NG - 1, a, a + 6) for a in (0, 6, 12, 18)]
            else:
                hk = [lambda: None] * 4
            group_chunks(gi, gi % 2, hk)
        P.add("pool", lambda e: e.tensor_copy(out=rawh.ap[:, :, 0:3], in_=raw4.ap[:, :, 0:3]), reads=rawb, writes=[rawh.b])
    else:
        P.add("pool", lambda e: e.memset(rawh.ap, 0.0), writes=[rawh.b])

    P.barrier()
    AB.off, AFa.off = mark1B, mark1F
    Wz = AB.get("Wz", 8, 2048); load_w(Wz, w_in[:, COL_Z:COL_Z + 2048], 2048)
    nwb = AFa.get("nwb", 2048)
    dma("sp", nwb.ap, bcast_row(normw, 2048), [], [nwb.b])
    xbs = [AB.get("xb_%d" % i, 1024) for i in range(2)]; xTs = [AB.get("xT_%d" % i, 8, 128) for i in range(2)]
    raws = [AB.get("raw_%d" % i, 24, 132) for i in range(2)]; xcs = [AB.get("xc_%d" % i, 24, 128) for i in range(2)]
    xtok = AB.get("xtok", 2560)
    xdt = AB.get("xdt", 512); xdts = AB.get("xdts", 512)
    Hb = AB.get("Hb", 2048); cbm = AB.get("cbm", 4, 128)
    LT = AB.get("LT", 4, 128); MT = AB.get("MT", 8, 128); yn = AB.get("yn", 2048)
    szb = AB.get("szb", 4, 512)
    adtU = [AFa.get("adtU%d" % i, 128) for i in range(8)]
    sm = AFa.get("sm", 12, 32)
    yacc = AFa.get("yacc", 512); ytmp = AFa.get("ytmp", 512); sz = AFa.get("sz", 512)
    ssq = AFa.get("ssq", 4)
    P.add("pool", lambda e: e.memset(raws[0].ap, 0.0), writes=[raws[0].b])
    P.add("pool", lambda e: e.memset(raws[1].ap, 0.0), writes=[raws[1].b])
    P.add("pool", lambda e: e.tensor_copy(out=raws[0].ap[:, :, 0:3], in_=rawh.ap[:, :, 0:3]), reads=[rawh.b], writes=[raws[0].b])

    CUT = int(os.environ.get('KCUT', '99')); NCH1 = int(os.environ.get('KNCH', str(NM1)))
    def front_pieces(ci, xrow, xb, xT, raw, xc, raw_next):
        def inproj(g):
            bank = pf[g % 2]
            for jj in range(4):
                j = 4 * g + jj
                chain(bank, bank.ap[:, jj * 128:(jj + 1) * 128], [(Wx.ap[:, k, j * 128:(j + 1) * 128], xT.ap[:, k, :]) for k in range(8)], [Wx.b, xT.b])
            P.add("act", lambda e, g=g, bank=bank: e.copy(out=raw.ap[:, 4 * g:4 * g + 4, 3:131], in_=bank.ap.rearrange("p (a b) -> p a b", a=4)), reads=[bank.b], writes=[raw.b])

        def conv(g):
            for jj in range(4):
                j = 4 * g + jj
                bank = pf[2 + j % 2]
                chain(bank, bank.ap[:, 0:128], [(diag.ap[:, j, tp, :], raw.ap[:, j, tp:tp + 128]) for tp in range(4)], [diag.b, raw.b])
                P.add("act", lambda e, j=j, bank=bank: e.activation(out=xc.ap[:, j, :], in_=bank.ap[:, 0:128], func=AF.Silu, bias=cbt.ap[:, j:j + 1], scale=1.0), reads=[bank.b, cbt.b], writes=[xc.b])

        def p0():
            load_xT(xrow, xb, xT); inproj(0); inproj(1)

        def p1():
            inproj(2); inproj(3)

        def p2():
            inproj(4); inproj(5)
            P.add("pool", lambda e: e.tensor_copy(out=raw_next.ap[:, :, 0:3], in_=raw.ap[:, :, 128:131]), reads=[raw.b], writes=[raw_next.b])

        def p3():
            conv(0); conv(1); conv(2); conv(3); conv(4); conv(5)
        return [p0, p1, p2, p3]

    def ssd_back(fcol, main, ci, xT, xc, hooks):
        for bt in range(3):
            nt = 8 if bt < 2 else 4
            for i in range(nt):
                j = bt * 8 + i
                P.add("pe", lambda e, i=i, j=j: e.transpose(pb[1].ap[:, i * 128:(i + 1) * 128], xc.ap[:, j, :], identb), reads=[xc.b, cb16.b], writes=[pb[1].b])
            P.add("dve", lambda e, bt=bt, nt=nt: e.tensor_copy(out=xtok.ap[:, bt * 1024:bt * 1024 + nt * 128], in_=pb[1].ap[:, 0:nt * 128]), reads=[pb[1].b], writes=[xtok.b])
        chain(pf[4], pf[4].ap[:, 0:32], [(xT.ap[:, k, :], Wdt.ap[:, k, :]) for k in range(8)], [xT.b, Wdt.b])
        S = lambda i: sm.ap[:, i, :]
        P.add("dve", lambda e: e.tensor_tensor(out=S(0), in0=pf[4].ap[:, 0:32], in1=dtb_bc, op=ALU.add), reads=[pf[4].b, smallp.b], writes=[sm.b])
        P.add("act", lambda e: e.activation(out=S(0), in_=S(0), func=AF.Exp), reads=[sm.b], writes=[sm.b])
        P.add("act", lambda e: e.activation(out=S(1), in_=S(0), func=AF.Ln, bias=onecol, scale=1.0), reads=[sm.b, cf.b], writes=[sm.b])
        P.add("dve", lambda e: e.tensor_scalar(out=S(1), in0=S(1), scalar1=fcol, scalar2=None, op0=ALU.mult), reads=[sm.b, flg.b], writes=[sm.b])
        P.add("dve", lambda e: e.tensor_tensor(out=S(2), in0=S(1), in1=a_bc, op=ALU.mult), reads=[sm.b, smallp.b], writes=[sm.b])
        P.add("pe", lambda e: e.matmul(pf[4].ap[:, 32:64], triU, S(2), start=True, stop=True), reads=[cf.b, sm.b], writes=[pf[4].b])
        P.add("pe", lambda e: e.matmul(pf[4].ap[:, 64:96], onesf, S(2), start=True, stop=True), reads=[cf.b, sm.b], writes=[pf[4].b])
        P.add("dve", lambda e: e.tensor_copy(out=sm.ap[:, 3:5, :], in_=pf[4].ap[:, 32:96].rearrange("p (a b) -> p a b", a=2)), reads=[pf[4].b], writes=[sm.b])
        if main:
            for g in range(4):
                P.add("pe", lambda e, g=g: e.matmul(pf[5].ap[:, g * 128:(g + 1) * 128], xc.ap[:, 16 + g, :], xc.ap[:, 20 + g, :], start=True, stop=True), reads=[xc.b], writes=[pf[5].b])
            P.add("dve", lambda e: e.tensor_tensor(out=cbm.ap, in0=pf[5].ap.rearrange("p (a b) -> p a b", a=4), in1=maskb.unsqueeze(1).to_broadcast([128, 4, 128]), op=ALU.mult), reads=[pf[5].b, cb16.b], writes=[cbm.b])
            P.add("pool", lambda e: e.tensor_copy(out=Hb.ap, in_=H.ap), reads=[H.b], writes=[Hb.b])
            P.add("act", lambda e: e.activation(out=S(5), in_=S(3), func=AF.Exp), reads=[sm.b], writes=[sm.b])
            for g in range(4):
                zb = pf[5 - g % 2]
                chain(zb, zb.ap, [(xT.ap[:, k, :], Wz.ap[:, k, g * 512:(g + 1) * 512]) for k in range(8)], [xT.b, Wz.b])
                P.add("act", lambda e, g=g, zb=zb: e.activation(out=szb.ap[:, g, :], in_=zb.ap, func=AF.Silu), reads=[zb.b], writes=[szb.b])
            for g in range(4):
                xg = xtok.ap[:, g * 512:(g + 1) * 512].rearrange("p (a b) -> p a b", a=8)
                P.add("dve", lambda e, g=g, xg=xg: e.tensor_tensor(out=xdt.ap.rearrange("p (a b) -> p a b", a=8), in0=xg, in1=bc(sm.ap[:, 1, 8 * g:8 * g + 8], 64), op=ALU.mult), reads=[xtok.b, sm.b], writes=[xdt.b])
                for hh in range(2):
                    bank = pf[hh]
                    for h4 in range(4):
                        h = 8 * g + 4 * hh + h4
                        au = adtU[h % 8]
                        P.add("dve", lambda e, h=h, au=au: e.tensor_scalar(out=au.ap, in0=Ustr, scalar1=sm.ap[:, 2, h:h + 1], scalar2=None, op0=ALU.mult), reads=[cf.b, sm.b], writes=[au.b])
                        P.add("pe", lambda e, h4=h4, au=au, bank=bank: e.matmul(bank.ap[:, h4 * 128:(h4 + 1) * 128], au.ap, triU, start=True, stop=True), reads=[au.b, cf.b], writes=[bank.b])
                    P.add("act", lambda e, bank=bank: e.activation(out=LT.ap.rearrange("p a b -> p (a b)"), in_=bank.ap, func=AF.Exp), reads=[bank.b], writes=[LT.b])
                    P.add("dve", lambda e, g=g, hh=hh: e.tensor_tensor(out=MT.ap[:, 4 * hh:4 * hh + 4, :], in0=LT.ap, in1=cbm.ap[:, g, :].unsqueeze(1).to_broadcast([128, 4, 128]), op=ALU.mult), reads=[LT.b, cbm.b], writes=[MT.b])
                for h8 in range(8):
                    P.add("pe", lambda e, h8=h8: e.matmul(pf[2].ap[:, h8 * 64:(h8 + 1) * 64], MT.ap[:, h8, :], xdt.ap[:, h8 * 64:(h8 + 1) * 64], start=True, stop=True), reads=[MT.b, xdt.b], writes=[pf[2].b])
                P.add("pe", lambda e, g=g: e.matmul(pf[3].ap, xc.ap[:, 20 + g, :], Hb.ap[:, g * 512:(g + 1) * 512], start=True, stop=True), reads=[xc.b, Hb.b], writes=[pf[3].b])
                v3 = lambda ap: ap.rearrange("p (a b) -> p a b", a=8)
                P.add("dve", lambda e, g=g: e.tensor_tensor(out=v3(yacc.ap), in0=v3(pf[3].ap), in1=bc(sm.ap[:, 5, 8 * g:8 * g + 8], 64), op=ALU.mult), reads=[pf[3].b, sm.b], writes=[yacc.b])
                P.add("dve", lambda e: e.tensor_tensor(out=yacc.ap, in0=yacc.ap, in1=pf[2].ap, op=ALU.add), reads=[pf[2].b, yacc.b], writes=[yacc.b])
                P.add("dve", lambda e, g=g, xg=xg: e.tensor_tensor(out=v3(ytmp.ap), in0=xg, in1=bc(D_bc[:, 8 * g:8 * g + 8], 64), op=ALU.mult), reads=[xtok.b, smallp.b], writes=[ytmp.b])
                P.add("dve", lambda e: e.tensor_tensor(out=yacc.ap, in0=yacc.ap, in1=ytmp.ap, op=ALU.add), reads=[ytmp.b, yacc.b], writes=[yacc.b])
                P.add("dve", lambda e, g=g: e.tensor_tensor(out=yacc.ap, in0=yacc.ap, in1=szb.ap[:, g, :], op=ALU.mult), reads=[szb.b, yacc.b], writes=[yacc.b])
                P.add("dve", lambda e: e.tensor_tensor(out=ytmp.ap, in0=yacc.ap, in1=yacc.ap, op=ALU.mult), reads=[yacc.b], writes=[ytmp.b])
                P.add("dve", lambda e, g=g: e.reduce_sum(out=ssq.ap[:, g:g + 1], in_=ytmp.ap, axis=mybir.AxisListType.X), reads=[ytmp.b], writes=[ssq.b])
                P.add("dve", lambda e, g=g: e.tensor_scalar(out=ssq.ap[:, g:g + 1], in0=ssq.ap[:, g:g + 1], scalar1=1.0 / 512, scalar2=1e-5, op0=ALU.mult, op1=ALU.add), reads=[ssq.b], writes=[ssq.b])
                P.add("act", lambda e, g=g: e.activation(out=ssq.ap[:, g:g + 1], in_=ssq.ap[:, g:g + 1], func=AF.Ln), reads=[ssq.b], writes=[ssq.b])
                P.add("act", lambda e, g=g: e.activation(out=ssq.ap[:, g:g + 1], in_=ssq.ap[:, g:g + 1], func=AF.Exp, scale=-0.5), reads=[ssq.b], writes=[ssq.b])
                P.add("dve", lambda e, g=g: e.scalar_tensor_tensor(out=yn.ap[:, g * 512:(g + 1) * 512], in0=yacc.ap, scalar=ssq.ap[:, g:g + 1], in1=nwb.ap[:, g * 512:(g + 1) * 512], op0=ALU.mult, op1=ALU.mult), reads=[yacc.b, ssq.b, nwb.b], writes=[yn.b])
                hooks[g]()
            dma("sp", ynd[ci * 128:(ci + 1) * 128, :], yn.ap, [yn.b], [])
        P.add("dve", lambda e: e.tensor_tensor(out=S(6), in0=S(4), in1=S(3), op=ALU.subtract), reads=[sm.b], writes=[sm.b])
        P.add("act", lambda e: e.activation(out=S(6), in_=S(6), func=AF.Exp), reads=[sm.b], writes=[sm.b])
        P.add("act", lambda e: e.activation(out=S(7), in_=S(4), func=AF.Exp), reads=[sm.b], writes=[sm.b])
        P.add("dve", lambda e: e.tensor_tensor(out=S(6), in0=S(6), in1=S(1), op=ALU.mult), reads=[sm.b], writes=[sm.b])
        for g in range(4):
            xg = xtok.ap[:, g * 512:(g + 1) * 512].rearrange("p (a b) -> p a b", a=8)
            v3 = lambda ap: ap.rearrange("p (a b) -> p a b", a=8)
            P.add("dve", lambda e, g=g, xg=xg: e.tensor_tensor(out=v3(xdts.ap), in0=xg, in1=bc(sm.ap[:, 6, 8 * g:8 * g + 8], 64), op=ALU.mult), reads=[xtok.b, sm.b], writes=[xdts.b])
            bank = pf[2 + g % 2]
            P.add("pe", lambda e, g=g, bank=bank: e.matmul(bank.ap, xtok.ap[:, 2048 + g * 128:2048 + (g + 1) * 128], xdts.ap, start=True, stop=True), reads=[xtok.b, xdts.b], writes=[bank.b])
            Hg = H.ap[:, g * 512:(g + 1) * 512]
            P.add("dve", lambda e, g=g, Hg=Hg: e.tensor_tensor(out=v3(Hg), in0=v3(Hg), in1=bc(sm.ap[:, 7, 8 * g:8 * g + 8], 64), op=ALU.mult), reads=[H.b, sm.b], writes=[H.b])
            P.add("dve", lambda e, Hg=Hg, bank=bank: e.tensor_tensor(out=Hg, in0=Hg, in1=bank.ap, op=ALU.add), reads=[H.b, bank.b], writes=[H.b])

    def fp(ci):
        b = ci % 2
        return front_pieces(ci, xmain[(ci + 1) * 128:(ci + 2) * 128, :], xbs[b], xTs[b], raws[b], xcs[b], raws[1 - b])
    nch = 0 if SKIP1 else NCH1
    if nch:
        for p_ in fp(0):
            p_()
    for ci in range(nch):
        nxt = fp(ci + 1) if ci + 1 < nch else [lambda: None] * 4
        ssd_back(flg.ap[:, NPRE + ci:NPRE + ci + 1], True, ci, xTs[ci % 2], xcs[ci % 2], nxt)

    if stop < 2:
        P.emit(nc); es.close(); return nc
    P.barrier()
    AB.off, AFa.off = markB, markF
    Wq = AB.get("Wq", 8, 1024); load_w(Wq, w_in[:, COL_Q:COL_Q + 1024], 1024)
    Wk2 = AB.get("Wk2", 8, 128); load_w(Wk2, w_in[:, COL_K:COL_K + 128], 128)
    Wv = AB.get("Wv", 8, 128); load_w(Wv, w_in[:, COL_V:COL_V + 128], 128)
    EB = AB.get("EB", 2, 16, 128)
    ebf = AFa.get("ebf", 2048); mkf = AFa.get("mkf", 2048)
    for kt in range(2):
        dma("sp", ebf.ap, biasg[:, kt * 2048:(kt + 1) * 2048], [], [ebf.b])
        dma("sp", mkf.ap, maskg[:, kt * 2048:(kt + 1) * 2048], [], [mkf.b])
        P.add("act", lambda e: e.activation(out=ebf.ap, in_=ebf.ap, func=AF.Exp), reads=[ebf.b], writes=[ebf.b])
        P.add("dve", lambda e, kt=kt: e.tensor_tensor(out=EB.ap[:, kt, :, :].rearrange("p a b -> p (a b)"), in0=ebf.ap, in1=mkf.ap, op=ALU.mult), reads=[ebf.b, mkf.b], writes=[EB.b])
    xb = AB.get("xb2", 1024); xT = AB.get("xT2", 8, 128)
    kT = [AB.get("kT%d" % i, 2, 128) for i in range(2)]
    vx = [AB.get("vx%d" % i, 2, 65) for i in range(2)]
    for i in range(2):
        P.add("pool", lambda e, i=i: e.memset(vx[i].ap, 1.0), writes=[vx[i].b])
    qT = AB.get("qT", 16, 128); et = AB.get("et", 4, 128)
    PT = [AB.get("PT%d" % i, 4, 128) for i in range(2)]
    ya = AB.get("ya", 1024)
    den = AFa.get("den", 4)
    flag0 = flg.ap[:, NPRE:NPRE + 1]
    CUT2 = int(os.environ.get('KCUT2', '99')); NCH2 = int(os.environ.get('KNCH2', str(NM2)))
    for ci in range(NCH2):
        sl = ci % 2
        load_xT(xmain[ci * 128:(ci + 1) * 128, :], xb, xT)
        for kv in range(2):
            chain(pf[0], pf[0].ap[0:64, kv * 128:(kv + 1) * 128], [(Wk2.ap[:, k, kv * 64:(kv + 1) * 64], xT.ap[:, k, :]) for k in range(8)], [Wk2.b, xT.b])
        P.add("act", lambda e, sl=sl: e.copy(out=kT[sl].ap[0:64, :, :], in_=pf[0].ap[0:64, 0:256].rearrange("p (a b) -> p a b", a=2)), reads=[pf[0].b], writes=[kT[sl].b])
        chain(pf[1], pf[1].ap[:, 0:128], [(xT.ap[:, k, :], Wv.ap[:, k, :]) for k in range(8)], [xT.b, Wv.b])
        P.add("dve", lambda e, sl=sl: e.tensor_copy(out=vx[sl].ap[:, :, 0:64], in_=pf[1].ap[:, 0:128].rearrange("p (a b) -> p a b", a=2)), reads=[pf[1].b], writes=[vx[sl].b])
        if ci == 0 or CUT2 <= 1:
            continue
        for q4 in range(4):
            bank = pf[2 + q4 % 2]
            for tt in range(4):
                j = q4 * 4 + tt
                chain(bank, bank.ap[0:64, tt * 128:(tt + 1) * 128], [(Wq.ap[:, k, j * 64:(j + 1) * 64], xT.ap[:, k, :]) for k in range(8)], [Wq.b, xT.b])
            P.add("act", lambda e, q4=q4, bank=bank: e.copy(out=qT.ap[0:64, q4 * 4:q4 * 4 + 4, :], in_=bank.ap[0:64, :].rearrange("p (a b) -> p a b", a=4)), reads=[bank.b], writes=[qT.b])
        for kvh in range(2):
            if CUT2 <= 2: break
            for hb in range(2):
                j0 = kvh * 8 + hb * 4
                for kt in range(2):
                    slk = (ci + 1 + kt) % 2
                    bank = pf[4 + kt]
                    for i in range(4):
                        j = j0 + i
                        base = (j % 2) * 64 * int(os.environ.get("KB64", "1"))
                        P.add("pe", lambda e, i=i, j=j, base=base, slk=slk, bank=bank, kvh=kvh: e.matmul(bank.ap[:, i * 128:(i + 1) * 128], kT[slk].ap[0:64, kvh, :], qT.ap[0:64, j, :], start=True, stop=True), reads=[kT[slk].b, qT.b], writes=[bank.b])
                    P.add("act", lambda e, bank=bank: e.activation(out=et.ap.rearrange("p a b -> p (a b)"), in_=bank.ap, func=AF.Exp, scale=0.125), reads=[bank.b], writes=[et.b])
                    if ci == 2 and kt == 0:
                        P.add("dve", lambda e, kt=kt, j0=j0: e.scalar_tensor_tensor(out=PT[kt].ap, in0=et.ap, scalar=flag0, in1=EB.ap[:, kt, j0:j0 + 4, :], op0=ALU.mult, op1=ALU.mult), reads=[et.b, EB.b, flg.b], writes=[PT[kt].b])
                    else:
                        P.add("dve", lambda e, kt=kt, j0=j0: e.tensor_tensor(out=PT[kt].ap, in0=et.ap, in1=EB.ap[:, kt, j0:j0 + 4, :], op=ALU.mult), reads=[et.b, EB.b], writes=[PT[kt].b])
                if CUT2 <= 3: continue
                bank = pf[hb]
                for i in range(4):
                    for kt in range(2):
                        slk = (ci + 1 + kt) % 2
                        P.add("pe", lambda e, i=i, kt=kt, slk=slk, bank=bank, kvh=kvh: e.matmul(bank.ap[:, i * 65:(i + 1) * 65], PT[kt].ap[:, i, :], vx[slk].ap[:, kvh, :], start=(kt == 0), stop=(kt == 1)), reads=[PT[kt].b, vx[slk].b], writes=[bank.b])
                if CUT2 <= 4: continue
                pv = bank.ap[:, 0:260].rearrange("p (a b) -> p a b", a=4)
                P.add("dve", lambda e, pv=pv, j0=j0: e.tensor_tensor(out=den.ap, in0=pv[:, :, 64], in1=esink[:, j0:j0 + 4], op=ALU.add), reads=[bank.b, smallp.b], writes=[den.b])
                P.add("dve", lambda e: e.reciprocal(out=den.ap, in_=den.ap), reads=[den.b], writes=[den.b])
                P.add("dve", lambda e, pv=pv, j0=j0: e.tensor_tensor(out=ya.ap[:, j0 * 64:(j0 + 4) * 64].rearrange("p (a b) -> p a b", a=4), in0=pv[:, :, 0:64], in1=bc(den.ap, 64), op=ALU.mult), reads=[bank.b, den.b], writes=[ya.b])
        dma("sp", yad[(ci - 1) * 128:ci * 128, :], ya.ap, [ya.b], [])

    if stop < 3:
        P.emit(nc); es.close(); return nc
    P.barrier()
    AB.off, AFa.off = markB, markF
    Wg = AB.get("Wg", 8, 2048); load_w(Wg, w_in[:, COL_G:COL_G + 2048], 2048)
    Wbs = AB.get("Wbs", 16, 1024); load_w(Wbs, w_bs, 1024)
    Wba = AB.get("Wba", 8, 1024); load_w(Wba, w_ba, 1024)
    Wmx = AB.get("Wmx", 8, 1024); load_w(Wmx, w_mix, 1024)
    bgb = AFa.get("bgb", 2048); dma("sp", bgb.ap, bcast_row(b_gate, 2048), [], [bgb.b])
    lng = AFa.get("lng", 2, 1024)
    dma("sp", lng.ap[:, 0, :], bcast_row(ln1g, 1024), [], [lng.b]); dma("sp", lng.ap[:, 1, :], bcast_row(ln1b, 1024), [], [lng.b])
    xb = AB.get("xb3", 1024); xT = AB.get("xT3", 8, 128)
    ynb = AB.get("ynb", 2048); yab = AB.get("yab", 1024)
    ynT = AB.get("ynT", 16, 128); yaT = AB.get("yaT", 8, 128)
    mg = AB.get("mg", 1024); mT = AB.get("mT", 8, 128)
    xf = AFa.get("xf", 1024); gt = AB.get("gt", 2048); m1 = AFa.get("m1", 512); r = AFa.get("r", 1024)
    st = AFa.get("st", 2, 6); mv = AFa.get("mv", 2)

    def transp(src, dst, ntl):
        for bt in range(ntl // 8):
            for i in range(8):
                j = bt * 8 + i
                P.add("pe", lambda e, i=i, j=j: e.transpose(pb[1].ap[:, i * 128:(i + 1) * 128], src.ap[:, j * 128:(j + 1) * 128], identb), reads=[src.b, cb16.b], writes=[pb[1].b])
            P.add("act", lambda e, bt=bt: e.copy(out=dst.ap[:, bt * 8:bt * 8 + 8, :].rearrange("p a b -> p (a b)"), in_=pb[1].ap), reads=[pb[1].b], writes=[dst.b])

    def layer_norm(r, g_ap, b_ap, gb, st, mv):
        for i in range(2):
            P.add("dve", lambda e, i=i: e.bn_stats(out=st.ap[:, i, :], in_=r.ap[:, i * 512:(i + 1) * 512]), reads=[r.b], writes=[st.b])
        P.add("dve", lambda e: e.bn_aggr(out=mv.ap, in_=st.ap.rearrange("p a b -> p (a b)")), reads=[st.b], writes=[mv.b])
        P.add("dve", lambda e: e.tensor_scalar(out=mv.ap[:, 1:2], in0=mv.ap[:, 1:2], scalar1=1e-5, scalar2=None, op0=ALU.add), reads=[mv.b], writes=[mv.b])
        P.add("act", lambda e: e.activation(out=mv.ap[:, 1:2], in_=mv.ap[:, 1:2], func=AF.Sqrt), reads=[mv.b], writes=[mv.b])
        P.add("dve", lambda e: e.reciprocal(out=mv.ap[:, 1:2], in_=mv.ap[:, 1:2]), reads=[mv.b], writes=[mv.b])
        P.add("dve", lambda e: e.tensor_scalar(out=r.ap, in0=r.ap, scalar1=mv.ap[:, 0:1], scalar2=mv.ap[:, 1:2], op0=ALU.subtract, op1=ALU.mult), reads=[r.b, mv.b], writes=[r.b])
        P.add("dve", lambda e: e.tensor_tensor(out=r.ap, in0=r.ap, in1=g_ap, op=ALU.mult), reads=[r.b, gb], writes=[r.b])
        P.add("dve", lambda e: e.tensor_tensor(out=r.ap, in0=r.ap, in1=b_ap, op=ALU.add), reads=[r.b, gb], writes=[r.b])

    def p3_loads(ci, xb, xf, ynb, yab):
        xrow = xmain[(ci + 1) * 128:(ci + 2) * 128, :]
        dma("pool", xb.ap, xrow, [], [xb.b])
        dma("sp", xf.ap, xrow, [], [xf.b])
        dma("sp", ynb.ap, ynd[ci * 128:(ci + 1) * 128, :], [], [ynb.b])
        dma("sp", yab.ap, yad[ci * 128:(ci + 1) * 128, :], [], [yab.b])

    def p3_chunk(ci, xb, xf, ynb, yab, r):
        for k in range(8):
            P.add("pe", lambda e, k=k: e.transpose(pb[0].ap[:, k * 128:(k + 1) * 128], xb.ap[:, k * 128:(k + 1) * 128], identb), reads=[xb.b, cb16.b], writes=[pb[0].b])
        P.add("act", lambda e: e.copy(out=xT.ap.rearrange("p a b -> p (a b)"), in_=pb[0].ap), reads=[pb[0].b], writes=[xT.b])
        transp(ynb, ynT, 16)
        transp(yab, yaT, 8)
        for s4 in range(4):
            bank = pf[s4 % 2]
            chain(bank, bank.ap, [(xT.ap[:, k, :], Wg.ap[:, k, s4 * 512:(s4 + 1) * 512]) for k in range(8)], [xT.b, Wg.b])
            P.add("dve", lambda e, s4=s4, bank=bank: e.tensor_tensor(out=gt.ap[:, s4 * 512:(s4 + 1) * 512], in0=bank.ap, in1=bgb.ap[:, s4 * 512:(s4 + 1) * 512], op=ALU.add), reads=[bank.b, bgb.b], writes=[gt.b])
        P.add("act", lambda e: e.activation(out=gt.ap, in_=gt.ap, func=AF.Sigmoid), reads=[gt.b], writes=[gt.b])
        for hf in range(2):
            chain(pf[2], pf[2].ap, [(ynT.ap[:, i, :], Wbs.ap[:, i, hf * 512:(hf + 1) * 512]) for i in range(16)], [ynT.b, Wbs.b])
            chain(pf[3], pf[3].ap, [(yaT.ap[:, i, :], Wba.ap[:, i, hf * 512:(hf + 1) * 512]) for i in range(8)], [yaT.b, Wba.b])
            P.add("dve", lambda e, hf=hf: e.tensor_tensor(out=m1.ap, in0=pf[2].ap, in1=gt.ap[:, hf * 512:(hf + 1) * 512], op=ALU.mult), reads=[pf[2].b, gt.b], writes=[m1.b])
            P.add("dve", lambda e, hf=hf: e.tensor_tensor(out=r.ap[:, hf * 512:(hf + 1) * 512], in0=pf[3].ap, in1=gt.ap[:, 1024 + hf * 512:1024 + (hf + 1) * 512], op=ALU.mult), reads=[pf[3].b, gt.b], writes=[r.b])
            P.add("dve", lambda e, hf=hf: e.tensor_tensor(out=mg.ap[:, hf * 512:(hf + 1) * 512], in0=m1.ap, in1=r.ap[:, hf * 512:(hf + 1) * 512], op=ALU.add), reads=[m1.b, r.b], writes=[mg.b])
        transp(mg, mT, 8)
        for hf in range(2):
            bank = pf[4 + hf]
            chain(bank, bank.ap, [(mT.ap[:, i, :], Wmx.ap[:, i, hf * 512:(hf + 1) * 512]) for i in range(8)], [mT.b, Wmx.b])
            P.add("dve", lambda e, hf=hf, bank=bank: e.scalar_tensor_tensor(out=r.ap[:, hf * 512:(hf + 1) * 512], in0=xf.ap[:, hf * 512:(hf + 1) * 512], scalar=ALPHA, in1=bank.ap, op0=ALU.mult, op1=ALU.add), reads=[xf.b, bank.b], writes=[r.b])
        layer_norm(r, lng.ap[:, 0, :], lng.ap[:, 1, :], lng.b, st, mv)
        if ci == 0:
            P.add("dve", lambda e: e.tensor_scalar(out=r.ap, in0=r.ap, scalar1=flag0, scalar2=None, op0=ALU.mult), reads=[r.b, flg.b], writes=[r.b])
        dma("sp", h1d[ci * 128:(ci + 1) * 128, :], r.ap, [r.b], [])


    xb3 = [xb, AB.get("xb3b", 1024)]; xf3 = [xf, AFa.get("xf3b", 1024)]
    ynb3 = [ynb, AB.get("ynb3b", 2048)]; yab3 = [yab, AB.get("yab3b", 1024)]; r3 = [r, AFa.get("r3b", 1024)]
    p3_loads(0, xb3[0], xf3[0], ynb3[0], yab3[0])
    for ci in range(NM1):
        if ci + 1 < NM1:
            b_ = (ci + 1) % 2
            p3_loads(ci + 1, xb3[b_], xf3[b_], ynb3[b_], yab3[b_])
        b_ = ci % 2
        p3_chunk(ci, xb3[b_], xf3[b_], ynb3[b_], yab3[b_], r3[b_])

    if stop < 4:
        P.emit(nc); es.close(); return nc
    P.barrier()
    AB.off, AFa.off = markB, markF
    Wup = AB.get("Wup", 8, 5632); load_w(Wup, w_up, 5632)
    Wdn = AB.get("Wdn", 22, 1024); load_w(Wdn, w_dn, 1024)
    fw = AFa.get("fw", 132); fb = AFa.get("fb", 44)
    per_channel(fw.ap[:, 0:88], fcw[0:88, :], 88); fw.b.writer = P.ops["dve"][-1]
    per_channel(fw.ap[:, 88:132], fcw[88:132, :], 44); fw.b.writer = P.ops["dve"][-1]
    per_channel(fb.ap, fcb, 44); fb.b.writer = P.ops["dve"][-1]
    lng2 = AFa.get("lng2", 2, 1024)
    dma("sp", lng2.ap[:, 0, :], bcast_row(ln2g, 1024), [], [lng2.b]); dma("sp", lng2.ap[:, 1, :], bcast_row(ln2b, 1024), [], [lng2.b])
    SC = 256
    hA = AB.get("hA", 1024); hB = AB.get("hB", 1024); h1T = AB.get("h1T", 8, SC + 2)
    aT = AB.get("aT", 22, SC)
    aTb = [Buf("aTb%d" % i) for i in range(22)]
    cvs = [AFa.get("cvs%d" % i, SC) for i in range(4)]
    t0s = [AFa.get("t0s%d" % i, SC) for i in range(2)]
    sg = AFa.get("sg", SC)
    hr = AFa.get("hr", 1024); r4 = AFa.get("r4", 1024)
    st4 = AFa.get("st4", 2, 6); mv4 = AFa.get("mv4", 2)
    NT = SC // 128
    tcount = 0
    for sc in range(TOK // SC):
        r0 = 128 + sc * SC
        for i in range(NT):
            dma("pool", hA.ap, h1d[r0 - 2 + i * 128:r0 + 126 + i * 128, :], [], [hA.b])
            for k in range(8):
                P.add("pe", lambda e, k=k: e.transpose(pb[0].ap[:, k * 128:(k + 1) * 128], hA.ap[:, k * 128:(k + 1) * 128], identb), reads=[hA.b, cb16.b], writes=[pb[0].b])
            P.add("act", lambda e, i=i: e.copy(out=h1T.ap[:, :, i * 128:(i + 1) * 128], in_=pb[0].ap.rearrange("p (a b) -> p a b", a=8)), reads=[pb[0].b], writes=[h1T.b])
        dma("pool", hB.ap[0:2, :], h1d[r0 + SC - 2:r0 + SC, :], [], [hB.b])
        for k in range(8):
            P.add("pe", lambda e, k=k: e.transpose(pb[1].ap[:, k * 2:(k + 1) * 2], hB.ap[0:2, k * 128:(k + 1) * 128], identb[0:2, 0:2]), reads=[hB.b, cb16.b], writes=[pb[1].b])
        P.add("act", lambda e: e.copy(out=h1T.ap[:, :, SC:SC + 2], in_=pb[1].ap[:, 0:16].rearrange("p (a b) -> p a b", a=8)), reads=[pb[1].b], writes=[h1T.b])
        def down(jp):
            for tt in range(NT):
                for hf in range(2):
                    bank = pf[2 + tt * 2 + hf]
                    P.add("pe", lambda e, jp=jp, tt=tt, hf=hf, bank=bank: e.matmul(bank.ap, aT.ap[:, jp, tt * 128:(tt + 1) * 128], Wdn.ap[:, jp, hf * 512:(hf + 1) * 512], start=(jp == 0), stop=(jp == 21)), reads=[aTb[jp], Wdn.b], writes=[bank.b])

        for jp in range(22):
            for gv in range(2):
                j = jp + 22 * gv
                X = pf[tcount % 2]; t0 = t0s[tcount % 2]
                cv = cvs[(jp % 2) * 2 + gv]
                tcount += 1
                chain(X, X.ap[:, 0:SC + 2], [(Wup.ap[:, k, j * 128:(j + 1) * 128], h1T.ap[:, k, 0:SC + 2]) for k in range(8)], [Wup.b, h1T.b])
                w0, w1, w2, bb = fw.ap[:, j:j + 1], fw.ap[:, 44 + j:45 + j], fw.ap[:, 88 + j:89 + j], fb.ap[:, j:j + 1]
                P.add("act", lambda e, X=X, t0=t0, w2=w2, bb=bb: e.activation(out=t0.ap, in_=X.ap[:, 2:SC + 2], func=AF.Identity, bias=bb, scale=w2), reads=[X.b, fw.b, fb.b], writes=[t0.b])
                P.add("dve", lambda e, X=X, t0=t0, w1=w1: e.scalar_tensor_tensor(out=t0.ap, in0=X.ap[:, 1:SC + 1], scalar=w1, in1=t0.ap, op0=ALU.mult, op1=ALU.add), reads=[X.b, fw.b, t0.b], writes=[t0.b])
                P.add("dve", lambda e, X=X, t0=t0, w0=w0, cv=cv: e.scalar_tensor_tensor(out=cv.ap, in0=X.ap[:, 0:SC], scalar=w0, in1=t0.ap, op0=ALU.mult, op1=ALU.add), reads=[X.b, fw.b, t0.b], writes=[cv.b])
            cg, cvv = cvs[(jp % 2) * 2], cvs[(jp % 2) * 2 + 1]
            P.add("act", lambda e, cg=cg: e.activation(out=sg.ap, in_=cg.ap, func=AF.Silu), reads=[cg.b], writes=[sg.b])
            P.add("dve", lambda e, jp=jp, cvv=cvv: e.tensor_tensor(out=aT.ap[:, jp, :], in0=sg.ap, in1=cvv.ap, op=ALU.mult), reads=[sg.b, cvv.b], writes=[aTb[jp]])
            if jp >= 1:
                down(jp - 1)
        down(21)
        for tt in range(NT):
            rr = r0 + tt * 128
            dma("sp", hr.ap, h1d[rr:rr + 128, :], [], [hr.b])
            for hf in range(2):
                bank = pf[2 + tt * 2 + hf]
                P.add("dve", lambda e, hf=hf, bank=bank: e.scalar_tensor_tensor(out=r4.ap[:, hf * 512:(hf + 1) * 512], in0=hr.ap[:, hf * 512:(hf + 1) * 512], scalar=ALPHA, in1=bank.ap, op0=ALU.mult, op1=ALU.add), reads=[hr.b, bank.b], writes=[r4.b])
            layer_norm(r4, lng2.ap[:, 0, :], lng2.ap[:, 1, :], lng2.b, st4, mv4)
            dma("sp", out[rr - 128:rr, :], r4.ap, [r4.b], [])

    P.emit(nc)
    es.close()
    return nc


def rel_bucket_np(rel):
    n = np.maximum(rel, 0)
    nf = np.maximum(n, 1).astype(np.float32)
    large = 16 + (np.log(nf / np.float32(16)) / np.float32(np.log(128 / 16)) * np.float32(16)).astype(np.int32)
    large = np.minimum(large, 31)
    return np.where(n < 16, n, large)


_NC = None


def kernel(_dbg=None, **inp):
    global _NC
    x = np.asarray(inp["x"], np.float32)[0]
    f = lambda k: np.ascontiguousarray(np.asarray(inp[k], np.float32)[0])
    common = {
        "w_in": f("w_in"), "b_gate": f("b_gate")[None], "dtb": f("ssm_dt_bias")[None], "alog": f("ssm_a_log")[None],
        "dsk": f("ssm_d")[None], "normw": f("ssm_norm_w")[None], "sinks": f("attn_sinks")[None],
        "w_bs": f("w_branch_ssm"), "w_ba": f("w_branch_attn"), "w_mix": f("w_mix_out"),
        "ln1g": f("ln1_g")[None], "ln1b": f("ln1_b")[None], "ln2g": f("ln2_g")[None], "ln2b": f("ln2_b")[None],
        "w_up": f("w_up"), "w_dn": f("w_down"),
    }
    scw = f("ssm_conv_w")
    common["scw"] = np.ascontiguousarray(scw.reshape(4 * 24, 128))
    common["scb"] = np.ascontiguousarray(f("ssm_conv_b").reshape(24, 128))
    common["fcw"] = np.ascontiguousarray(f("ffn_conv_w").reshape(3 * 44, 128))
    common["fcb"] = np.ascontiguousarray(f("ffn_conv_b").reshape(44, 128))
    s = np.arange(128)
    ident = np.eye(128, dtype=np.float32)
    triU = (s[:, None] <= s[None, :]).astype(np.float32)
    ustr = (s[:, None] > s[None, :]).astype(np.float32)
    common["cst"] = np.ascontiguousarray(np.concatenate([ident, triU, ustr, np.ones((128, 128), np.float32)], axis=1))
    rb = np.asarray(inp["rel_bias"], np.float32)
    bg = np.zeros((128, 2, 16, 128), np.float32); mk = np.zeros((128, 2, 16, 128), np.float32)
    for kt in range(2):
        rel = (s[None, :] + 128) - (s[:, None] + 128 * kt)
        valid = (rel >= 0) & (rel < 128)
        bidx = rel_bucket_np(rel)
        g = rb[bidx]
        bg[:, kt] = np.transpose(g, (0, 2, 1))
        mk[:, kt] = np.broadcast_to(valid[:, None, :], (128, 16, 128))
    common["biasg"] = np.ascontiguousarray(bg.reshape(128, -1)); common["maskg"] = np.ascontiguousarray(mk.reshape(128, -1))
    in_maps = []
    for c in range(NCORE):
        S = c * TOK
        lo = S - 128 - NPRE * 128
        xp = np.zeros((NPRE * 128, 1024), np.float32)
        if S - 128 > 0:
            src_lo = max(lo, 0)
            xp[src_lo - lo:] = x[src_lo:S - 128]
        xm = np.zeros((NM2 * 128, 1024), np.float32)
        lo2 = S - 256
        src_lo = max(lo2, 0)
        xm[src_lo - lo2:] = x[src_lo:S + TOK]
        fl = np.zeros((128, NPRE + NM1), np.float32)
        for i in range(NPRE):
            fl[:, i] = 1.0 if lo + i * 128 >= 0 else 0.0
        fl[:, NPRE] = 1.0 if c > 0 else 0.0
        fl[:, NPRE + 1:] = 1.0
        m = dict(common); m["xpre"] = xp; m["xmain"] = xm; m["pflag"] = fl
        in_maps.append(m)
    if _dbg is not None:
        return in_maps
    if _NC is None:
        _NC = build()
    res = run_bass_kernel_spmd(_NC, in_maps, core_ids=list(range(NCORE)))
    o = np.concatenate([res.results[c]["out"] for c in range(NCORE)], axis=0)
    return o[None].astype(np.float32)
```

```python
import numpy as np
import concourse.bass as bass
import concourse.mybir as mybir

ENGS = ["pe", "act", "dve", "pool", "sp"]
N_DMA_SEM = 16
import os as _os
SAME_ENGINE_SYNC = _os.environ.get("KSES", "1") == "1"


class Buf:
    __slots__ = ("name", "writer", "readers", "dma_readers", "excl")

    def __init__(self, name, excl=False):
        self.name = name
        self.excl = excl
        self.writer = None
        self.readers = {}
        self.dma_readers = []


class Op:
    __slots__ = ("eng", "fn", "idx", "waits", "signal", "is_dma", "dsem", "dtarget", "clock", "sigcount", "uid")


class Prog:
    def __init__(self):
        self.ops = {e: [] for e in ENGS}
        self.clock = {e: {f: 0 for f in ENGS} for e in ENGS}
        self.known_dma = {e: set() for e in ENGS}
        self.dma_sem_count = [0] * N_DMA_SEM
        self.dma_sem_last = [None] * N_DMA_SEM
        self.dma_rr = 0
        self.n_dma = 0
        self.uid = 0
        self.bar = {e: [] for e in ENGS}

    def barrier(self):
        lasts = [self.ops[e][-1] for e in ENGS if self.ops[e] and not self.ops[e][-1].is_dma]
        for e in ENGS:
            pass
        lasts = []
        for e in ENGS:
            for op in reversed(self.ops[e]):
                if not op.is_dma:
                    lasts.append(op)
                    break
        dmas = [op for op in self.dma_sem_last if op is not None]
        for e in ENGS:
            self.bar[e] = lasts + dmas

    def add(self, eng, fn, reads=(), writes=(), dma=False):
        op = Op()
        op.eng = eng
        op.fn = fn
        op.idx = len(self.ops[eng])
        op.waits = []
        op.signal = False
        op.is_dma = dma
        op.uid = self.uid
        self.uid += 1
        def _flat(bs):
            o = []
            for b in bs:
                if isinstance(b, (list, tuple)):
                    o.extend(b)
                else:
                    o.append(b)
            return o
        reads = _flat(reads)
        writes = _flat(writes)
        deps = []
        for b in reads:
            if b.writer is not None:
                deps.append(b.writer)
            if b.excl:
                for e2, r in b.readers.items():
                    if e2 != eng:
                        deps.append(r)
        for b in writes:
            if b.writer is not None:
                deps.append(b.writer)
            deps.extend(b.readers.values())
            deps.extend(b.dma_readers)
        if self.bar[eng]:
            deps.extend(self.bar[eng])
            self.bar[eng] = []
        clk = self.clock[eng]
        seen = set()
        for d in deps:
            if d.uid in seen:
                continue
            seen.add(d.uid)
            if d.is_dma:
                if d.uid not in self.known_dma[eng]:
                    op.waits.append(("dma", d.dsem, d.dtarget))
                    self.known_dma[eng].add(d.uid)
            else:
                if d.eng == eng and (eng == "pe" or not SAME_ENGINE_SYNC):
                    continue
                if clk[d.eng] < d.idx + 1:
                    op.waits.append(("eng", d))
                    d.signal = True
                    for f in ENGS:
                        if d.clock[f] > clk[f]:
                            clk[f] = d.clock[f]
                    if clk[d.eng] < d.idx + 1:
                        clk[d.eng] = d.idx + 1
        if dma and eng == "pool":
            pd = self.__dict__.setdefault("pool_dmas", [])
            if len(pd) >= 4:
                d = pd[-4]
                if d.uid not in self.known_dma[eng]:
                    op.waits.append(("dma", d.dsem, d.dtarget))
                    self.known_dma[eng].add(d.uid)
            pd.append(op)
        if dma:
            k = self.dma_rr
            self.dma_rr = (self.dma_rr + 1) % N_DMA_SEM
            prev = self.dma_sem_last[k]
            if prev is not None and prev.uid not in self.known_dma[eng]:
                op.waits.append(("dma", k, prev.dtarget))
                self.known_dma[eng].add(prev.uid)
            self.dma_sem_count[k] += 16
            op.dsem = k
            op.dtarget = self.dma_sem_count[k]
            self.dma_sem_last[k] = op
            self.n_dma += 1
        op.clock = dict(clk)
        if not SAME_ENGINE_SYNC or eng == "pe":
            pass
        self.ops[eng].append(op)
        for b in writes:
            b.writer = op
            b.readers = {}
            b.dma_readers = []
        for b in reads:
            if dma:
                b.dma_readers.append(op)
            else:
                b.readers[eng] = op
        return op

    def emit(self, nc, final_waits=True):
        for e in ENGS:
            c = 0
            for op in self.ops[e]:
                if op.signal:
                    c += 1
                op.sigcount = c
        from contextlib import ExitStack
        with ExitStack() as es:
            esem = {e: es.enter_context(nc.semaphore("s_" + e)) for e in ENGS}
            dsem = [es.enter_context(nc.semaphore("d%d" % i)) for i in range(N_DMA_SEM)]
            block = es.enter_context(nc.Block())
            last_dma = [op for op in self.dma_sem_last if op is not None]

            def run(e, h):
                for op in self.ops[e]:
                    for w in op.waits:
                        if w[0] == "dma":
                            h.wait_ge(dsem[w[1]], w[2])
                        else:
                            h.wait_ge(esem[w[1].eng], w[1].sigcount)
                    ins = op.fn(h)
                    if op.is_dma:
                        ins.then_inc(dsem[op.dsem], 16)
                    elif op.signal:
                        ins.then_inc(esem[e], 1)
                if e == "sp" and final_waits:
                    for k in range(N_DMA_SEM):
                        if self.dma_sem_count[k] > 0:
                            h.wait_ge(dsem[k], self.dma_sem_count[k])

            @block.tensor
            def _(h):
                run("pe", h)

            @block.scalar
            def _(h):
                run("act", h)

            @block.vector
            def _(h):
                run("dve", h)

            @block.gpsimd
            def _(h):
                run("pool", h)

            @block.sync
            def _(h):
                run("sp", h)

from contextlib import ExitStack
import ml_dtypes
from concourse.bass_utils import run_bass_kernel_spmd

F32 = mybir.dt.float32
BF16 = mybir.dt.bfloat16
AF = mybir.ActivationFunctionType
ALU = mybir.AluOpType

NCORE = 8
TOK = 2048
NPRE = 112
NM1 = 17
NM2 = 18
ALPHA = 2.0 ** 0.25
COL_Z, COL_X, COL_B, COL_C, COL_DT, COL_Q, COL_K, COL_V, COL_G = 0, 2048, 4096, 4608, 5120, 5152, 6176, 6304, 6432


class TT:
    def __init__(self, ap, name, excl=False):
        self.ap = ap
        self.b = Buf(name, excl)


class Arena:
    def __init__(self, t, n):
        self.t, self.n, self.off = t, n, 0

    def get(self, name, *fs):
        n = int(np.prod(fs))
        ap = self.t[:, self.off:self.off + n]
        self.off += n
        assert self.off <= self.n, (name, self.off, self.n)
        if len(fs) == 2:
            ap = ap.rearrange("p (a b) -> p a b", a=fs[0])
        elif len(fs) == 3:
            ap = ap.rearrange("p (a b c) -> p a b c", a=fs[0], b=fs[1])
        return TT(ap, name)


def bc(ap2, n):
    return ap2.unsqueeze(2).to_broadcast([ap2.shape[0], ap2.shape[1], n])


def build(stop=4, npre=NPRE, dbg=False):
    nc = bass.Bass("TRN2", target_bir_lowering=False)
    dt_in = lambda n, s: nc.dram_tensor(n, s, F32, kind="ExternalInput").ap()
    xpre = dt_in("xpre", [NPRE * 128, 1024])
    xmain = dt_in("xmain", [NM2 * 128, 1024])
    pflag = dt_in("pflag", [128, NPRE + NM1])
    w_in = dt_in("w_in", [1024, 8480])
    b_gate = dt_in("b_gate", [1, 2048])
    scw = dt_in("scw", [96, 128])
    scb = dt_in("scb", [24, 128])
    dtb = dt_in("dtb", [1, 32])
    alog = dt_in("alog", [1, 32])
    dsk = dt_in("dsk", [1, 32])
    normw = dt_in("normw", [1, 2048])
    sinks = dt_in("sinks", [1, 16])
    w_bs = dt_in("w_bs", [2048, 1024])
    w_ba = dt_in("w_ba", [1024, 1024])
    w_mix = dt_in("w_mix", [1024, 1024])
    ln1g = dt_in("ln1g", [1, 1024]); ln1b = dt_in("ln1b", [1, 1024])
    ln2g = dt_in("ln2g", [1, 1024]); ln2b = dt_in("ln2b", [1, 1024])
    w_up = dt_in("w_up", [1024, 5632])
    fcw = dt_in("fcw", [132, 128])
    fcb = dt_in("fcb", [44, 128])
    w_dn = dt_in("w_dn", [2816, 1024])
    cst = dt_in("cst", [128, 4 * 128])
    biasg = dt_in("biasg", [128, 2 * 16 * 128])
    maskg = dt_in("maskg", [128, 2 * 16 * 128])
    out = nc.dram_tensor("out", [TOK, 1024], F32, kind="ExternalOutput").ap()
    skind = "ExternalOutput" if dbg else "Internal"
    ynd = nc.dram_tensor("ynd", [NM1 * 128, 2048], BF16, kind=skind).ap()
    yad = nc.dram_tensor("yad", [NM1 * 128, 1024], BF16, kind=skind).ap()
    h1d = nc.dram_tensor("h1d", [NM1 * 128, 1024], F32, kind=skind).ap()

    P = Prog()
    es = ExitStack()
    NB, NF = 164 * 512, 40 * 256
    ABt = es.enter_context(nc.sbuf_tensor("AB", [128, NB], BF16))
    AFt = es.enter_context(nc.sbuf_tensor("AF", [128, NF], F32))
    pf = [TT(es.enter_context(nc.psum_tensor("pf%d" % i, [128, 512], F32))[:], "pf%d" % i, True) for i in range(6)]
    for t_ in pf:
        t_.b = [Buf(t_.b.name + "q%d" % q_, True) for q_ in range(4)]
    pb = [TT(es.enter_context(nc.psum_tensor("pb%d" % i, [128, 1024], BF16))[:], "pb%d" % i, True) for i in range(2)]
    AB = Arena(ABt, NB)
    AFa = Arena(AFt, NF)

    def bcast_row(src, n):
        return bass.AP(src.tensor, 0, [[0, 128], [1, n]])

    def dma(eng, o, i, reads, writes):
        P.add(eng, lambda e, o=o, i=i: e.dma_start(out=o, in_=i), reads=reads, writes=writes, dma=True)

    cf = AFa.get("cf", 4, 128)
    dma("sp", cf.ap, cst.rearrange("p (a b) -> p a b", a=4), [], [cf.b])
    identf, triU, Ustr, onesf = cf.ap[:, 0, :], cf.ap[:, 1, :], cf.ap[:, 2, :], cf.ap[:, 3, :]
    cb16 = AB.get("cb16", 4, 128)
    dma("pool", cb16.ap, cst.rearrange("p (a b) -> p a b", a=4), [], [cb16.b])
    identb, maskb = cb16.ap[:, 0, :], cb16.ap[:, 1, :]
    flg = AFa.get("flg", NPRE + NM1)
    dma("sp", flg.ap, pflag, [], [flg.b])
    smallp = AFa.get("smallp", 6, 32)
    dma("sp", smallp.ap[:, 0, :], bcast_row(dtb, 32), [], [smallp.b])
    dma("sp", smallp.ap[:, 1, :], bcast_row(alog, 32), [], [smallp.b])
    dma("sp", smallp.ap[:, 2, :], bcast_row(dsk, 32), [], [smallp.b])
    dma("sp", smallp.ap[:, 3, 0:16], bcast_row(sinks, 16), [], [smallp.b])
    P.add("act", lambda e: e.activation(out=smallp.ap[:, 1, :], in_=smallp.ap[:, 1, :], func=AF.Exp), reads=[smallp.b], writes=[smallp.b])
    P.add("dve", lambda e: e.tensor_scalar(out=smallp.ap[:, 1, :], in0=smallp.ap[:, 1, :], scalar1=-1.0, scalar2=None, op0=ALU.mult), reads=[smallp.b], writes=[smallp.b])
    P.add("act", lambda e: e.activation(out=smallp.ap[:, 3, 0:16], in_=smallp.ap[:, 3, 0:16], func=AF.Exp), reads=[smallp.b], writes=[smallp.b])
    dtb_bc, a_bc, D_bc, esink = smallp.ap[:, 0, :], smallp.ap[:, 1, :], smallp.ap[:, 2, :], smallp.ap[:, 3, 0:16]
    onecol = onesf[:, 0:1]
    rawh = AB.get("rawh", 24, 4)
    markB, markF = AB.off, AFa.off
    H = AFa.get("H", 2048)

    def load_w(dst, src, ncols, nk=8):
        nk = src.shape[0] // 128
        for c0 in range(0, ncols, 512):
            c1 = min(c0 + 512, ncols)
            for k in range(nk):
                dma("pool", dst.ap[:, k, c0:c1], src[k * 128:(k + 1) * 128, c0:c1], [], [dst.b])

    def load_xT(xrow_ap, xb, xT):
        dma("pool", xb.ap, xrow_ap, [], [xb.b])
        for k in range(8):
            P.add("pe", lambda e, k=k: e.transpose(pb[0].ap[:, k * 128:(k + 1) * 128], xb.ap[:, k * 128:(k + 1) * 128], identb), reads=[xb.b, cb16.b], writes=[pb[0].b])
        P.add("act", lambda e: e.copy(out=xT.ap.rearrange("p a b -> p (a b)"), in_=pb[0].ap), reads=[pb[0].b], writes=[xT.b])

    def chain(bank, out_ap, pairs, reads, wb=None):
        n = len(pairs)
        wb = [bank.b] if wb is None else wb
        for i, (l, r) in enumerate(pairs):
            P.add("pe", lambda e, l=l, r=r, i=i: e.matmul(out_ap, l, r, start=(i == 0), stop=(i == n - 1)), reads=reads, writes=wb)

    def per_channel(dst, src_dram, rows):
        tmp = AFa.get("pc_tmp", 128)
        dma("sp", tmp.ap[0:rows, :], src_dram, [], [tmp.b])
        P.add("pe", lambda e: e.transpose(pf[5].ap[:, 0:rows], tmp.ap[0:rows, :], identf[0:rows, 0:rows]), reads=[tmp.b, cf.b], writes=[pf[5].b])
        P.add("dve", lambda e: e.tensor_copy(out=dst, in_=pf[5].ap[:, 0:rows]), reads=[pf[5].b], writes=[])

    import os
    SKIP1 = int(os.environ.get('KSKIP1', '0'))
    Wx = AB.get("Wx", 8, 3072); load_w(Wx, w_in[:, COL_X:COL_X + 3072], 3072)
    Wdt = AB.get("Wdt", 8, 32); load_w(Wdt, w_in[:, COL_DT:COL_DT + 32], 32)
    cwt = AFa.get("cwt", 96); cbt = AFa.get("cbt", 24)
    per_channel(cwt.ap, scw, 96)
    per_channel(cbt.ap, scb, 24)
    cwt.b.writer = P.ops["dve"][-2]; cbt.b.writer = P.ops["dve"][-1]
    diag = AB.get("diag", 24, 4, 128)
    for j in range(24):
        for tp in range(4):
            P.add("dve", lambda e, j=j, tp=tp: e.tensor_scalar(out=diag.ap[:, j, tp, :], in0=identf, scalar1=cwt.ap[:, tp * 24 + j:tp * 24 + j + 1], scalar2=None, op0=ALU.mult), reads=[cf.b, cwt.b], writes=[diag.b])
    P.add("dve", lambda e: e.memset(H.ap, 0.0), writes=[H.b])
    mark1B, mark1F = AB.off, AFa.off
    xbA = [AB.get("xbA%d" % i, 1024) for i in range(2)]
    xT4 = [AB.get("xT4_%d" % i, 8, 512) for i in range(2)]
    raw4 = AB.get("raw4", 24, 516); xc4 = AB.get("xc4", 24, 512)
    rawb = [Buf("rawb%d" % j) for j in range(24)]; xcb = [Buf("xcb%d" % j) for j in range(24)]
    xtk = [AB.get("xtk%d" % i, 2560) for i in range(2)]
    xdtsA4 = [AB.get("xdtsA%d" % i, 512) for i in range(8)]
    P.add("pool", lambda e: e.memset(raw4.ap, 0.0), writes=rawb)

    def load_group(gi, buf):
        for q in range(4):
            c = gi * 4 + q
            xb_ = xbA[q % 2]
            dma("pool", xb_.ap, xpre[c * 128:(c + 1) * 128, :], [], [xb_.b])
            for k in range(8):
                P.add("pe", lambda e, k=k, xb_=xb_: e.transpose(pb[0].ap[:, k * 128:(k + 1) * 128], xb_.ap[:, k * 128:(k + 1) * 128], identb), reads=[xb_.b, cb16.b], writes=[pb[0].b])
            P.add("act", lambda e, q=q, buf=buf: e.copy(out=xT4[buf].ap[:, :, q * 128:(q + 1) * 128], in_=pb[0].ap.rearrange("p (a b) -> p a b", a=8)), reads=[pb[0].b], writes=[xT4[buf].b])

    def proj_in(gi, buf, last, j0, j1):
        ntile = 24 if last else 20
        for j in range(j0, min(j1, ntile)):
            bank = pf[j % 2]
            chain(bank, bank.ap, [(Wx.ap[:, k, j * 128:(j + 1) * 128], xT4[buf].ap[:, k, :]) for k in range(8)], [Wx.b, xT4[buf].b])
            P.add("act", lambda e, j=j, bank=bank: e.copy(out=raw4.ap[:, j, 3:515], in_=bank.ap), reads=[bank.b], writes=[rawb[j]])

    def proj_conv(gi, last):
        ntile = 24 if last else 20
        for j in range(ntile):
            bank = (pf[2], pf[3], pf[5], pf[4])[j % 4]
            chain(bank, bank.ap, [(diag.ap[:, j, tp, :], raw4.ap[:, j, tp:tp + 512]) for tp in range(4)], [diag.b, rawb[j]])
            P.add("act", lambda e, j=j, bank=bank: e.activation(out=xc4.ap[:, j, :], in_=bank.ap, func=AF.Silu, bias=cbt.ap[:, j:j + 1], scale=1.0), reads=[bank.b, cbt.b], writes=[xcb[j]])
        P.add("pool", lambda e: e.tensor_copy(out=raw4.ap[:, :, 0:3], in_=raw4.ap[:, :, 512:515]), reads=rawb, writes=rawb)

    smGs = [AFa.get("smG%d" % i, 8, 128) for i in range(2)]

    def group_chunks(gi, buf, hooks):
        c0 = gi * 4
        smG = smGs[gi % 2]
        S = lambda i: smG.ap[:, i, :]
        S3 = lambda i: smG.ap[:, i, :].rearrange("p (q h) -> p q h", q=4)
        b4 = lambda ap: ap.unsqueeze(1).to_broadcast([128, 4, 32])
        v3 = lambda ap: ap.rearrange("p (a b) -> p a b", a=8)
        for q in range(4):
            chain(pf[4], pf[4].ap[:, q * 32:(q + 1) * 32], [(xT4[buf].ap[:, k, q * 128:(q + 1) * 128], Wdt.ap[:, k, :]) for k in range(8)], [xT4[buf].b, Wdt.b])
        P.add("dve", lambda e: e.tensor_tensor(out=S3(0), in0=pf[4].ap[:, 0:128].rearrange("p (q h) -> p q h", q=4), in1=b4(dtb_bc), op=ALU.add), reads=[pf[4].b, smallp.b], writes=[smG.b])
        P.add("act", lambda e: e.activation(out=S(0), in_=S(0), func=AF.Exp), reads=[smG.b], writes=[smG.b])
        P.add("act", lambda e: e.activation(out=S(1), in_=S(0), func=AF.Ln, bias=onecol, scale=1.0), reads=[smG.b, cf.b], writes=[smG.b])
        P.add("dve", lambda e: e.tensor_tensor(out=S3(1), in0=S3(1), in1=bc(flg.ap[:, c0:c0 + 4], 32), op=ALU.mult), reads=[smG.b, flg.b], writes=[smG.b])
        P.add("dve", lambda e: e.tensor_tensor(out=S3(2), in0=S3(1), in1=b4(a_bc), op=ALU.mult), reads=[smG.b, smallp.b], writes=[smG.b])

        def tr(q):
            xt = xtk[q % 2]
            for bt in range(3):
                nt = 8 if bt < 2 else 4
                pbk = pb[bt % 2]
                for i in range(nt):
                    jj = bt * 8 + i
                    P.add("pe", lambda e, i=i, jj=jj, q=q, pbk=pbk: e.transpose(pbk.ap[:, i * 128:(i + 1) * 128], xc4.ap[:, jj, q * 128:(q + 1) * 128], identb), reads=[xcb[jj], cb16.b], writes=[pbk.b])
                if bt == 1:
                    P.add("act", lambda e, bt=bt, nt=nt, xt=xt, pbk=pbk: e.copy(out=xt.ap[:, bt * 1024:bt * 1024 + nt * 128], in_=pbk.ap[:, 0:nt * 128]), reads=[pbk.b], writes=[xt.b])
                else:
                    P.add("dve", lambda e, bt=bt, nt=nt, xt=xt, pbk=pbk: e.tensor_copy(out=xt.ap[:, bt * 1024:bt * 1024 + nt * 128], in_=pbk.ap[:, 0:nt * 128]), reads=[pbk.b], writes=[xt.b])

        SBK = [pf[2], pf[3], pf[5], pf[4]]

        def state(q):
            xt = xtk[q % 2]
            for g in range(4):
                xg = xt.ap[:, g * 512:(g + 1) * 512].rearrange("p (a b) -> p a b", a=8)
                o = q * 32 + 8 * g
                xd = xdtsA4[(q % 2) * 4 + g]
                P.add("dve", lambda e, xg=xg, o=o, xd=xd: e.tensor_tensor(out=v3(xd.ap), in0=xg, in1=bc(smG.ap[:, 6, o:o + 8], 64), op=ALU.mult), reads=[xt.b, smG.b], writes=[xd.b])
                P.add("pe", lambda e, g=g, xt=xt, xd=xd, q=q: e.matmul(SBK[g].ap, xt.ap[:, 2048 + g * 128:2048 + (g + 1) * 128], xd.ap, start=(q == 0), stop=(q == 3)), reads=[xt.b, xd.b], writes=[SBK[g].b])
            if q == 3:
                P.add("dve", lambda e: e.tensor_tensor(out=H.ap.rearrange("p (a b) -> p a b", a=32), in0=H.ap.rearrange("p (a b) -> p a b", a=32), in1=bc(smG.ap[:, 7, 0:32], 64), op=ALU.mult), reads=[H.b, smG.b], writes=[H.b])
                for g in range(4):
                    Hg = H.ap[:, g * 512:(g + 1) * 512]
                    P.add("dve", lambda e, Hg=Hg, g=g: e.tensor_tensor(out=Hg, in0=Hg, in1=SBK[g].ap, op=ALU.add), reads=[H.b, SBK[g].b], writes=[H.b])

        tr(0); tr(1)
        hooks[0]()
        P.add("pe", lambda e: e.matmul(pf[4].ap[:, 128:256], triU, S(2), start=True, stop=True), reads=[cf.b, smG.b], writes=[pf[4].b])
        P.add("pe", lambda e: e.matmul(pf[4].ap[:, 256:384], onesf, S(2), start=True, stop=True), reads=[cf.b, smG.b], writes=[pf[4].b])
        P.add("dve", lambda e: e.tensor_copy(out=smG.ap[:, 3:5, :], in_=pf[4].ap[:, 128:384].rearrange("p (a b) -> p a b", a=2)), reads=[pf[4].b], writes=[smG.b])
        P.add("dve", lambda e: e.memset(S3(5)[:, 3, :], 0.0), writes=[smG.b])
        P.add("dve", lambda e: e.tensor_copy(out=S3(5)[:, 2, :], in_=S3(4)[:, 3, :]), reads=[smG.b], writes=[smG.b])
        P.add("dve", lambda e: e.tensor_tensor(out=S3(5)[:, 1, :], in0=S3(5)[:, 2, :], in1=S3(4)[:, 2, :], op=ALU.add), reads=[smG.b], writes=[smG.b])
        P.add("dve", lambda e: e.tensor_tensor(out=S3(5)[:, 0, :], in0=S3(5)[:, 1, :], in1=S3(4)[:, 1, :], op=ALU.add), reads=[smG.b], writes=[smG.b])
        P.add("dve", lambda e: e.tensor_tensor(out=S3(7)[:, 0, :], in0=S3(5)[:, 0, :], in1=S3(4)[:, 0, :], op=ALU.add), reads=[smG.b], writes=[smG.b])
        P.add("dve", lambda e: e.tensor_tensor(out=S(6), in0=S(4), in1=S(3), op=ALU.subtract), reads=[smG.b], writes=[smG.b])
        P.add("dve", lambda e: e.tensor_tensor(out=S(6), in0=S(6), in1=S(5), op=ALU.add), reads=[smG.b], writes=[smG.b])
        P.add("act", lambda e: e.activation(out=S(6), in_=S(6), func=AF.Exp), reads=[smG.b], writes=[smG.b])
        P.add("act", lambda e: e.activation(out=S3(7)[:, 0, :], in_=S3(7)[:, 0, :], func=AF.Exp), reads=[smG.b], writes=[smG.b])
        P.add("dve", lambda e: e.tensor_tensor(out=S(6), in0=S(6), in1=S(1), op=ALU.mult), reads=[smG.b], writes=[smG.b])
        state(0); state(1)
        hooks[1]()
        tr(2); tr(3)
        hooks[2]()
        state(2); state(3)
        hooks[3]()

    NG = NPRE // 4
    g0 = NG - (npre + 3) // 4
    if not SKIP1 and g0 < NG:
        load_group(g0, g0 % 2)
        proj_in(g0, g0 % 2, g0 == NG - 1, 0, 24)
        for gi in range(g0, NG):
            proj_conv(gi, gi == NG - 1)
            if gi + 1 < NG:
                load_group(gi + 1, (gi + 1) % 2)
                hk = [lambda a=a, gi=gi: proj_in(gi + 1, (gi + 1) % 2, gi + 1 == NG - 1, a, a + 6) for a in (0, 6, 12, 18)]
            else:
                hk = [lambda: None] * 4
            group_chunks(gi, gi % 2, hk)
        P.add("pool", lambda e: e.tensor_copy(out=rawh.ap[:, :, 0:3], in_=raw4.ap[:, :, 0:3]), reads=rawb, writes=[rawh.b])
    else:
        P.add("pool", lambda e: e.memset(rawh.ap, 0.0), writes=[rawh.b])

    P.barrier()
    AB.off, AFa.off = mark1B, mark1F
    Wz = AB.get("Wz", 8, 2048); load_w(Wz, w_in[:, COL_Z:COL_Z + 2048], 2048)
    nwb = AFa.get("nwb", 2048)
    dma("sp", nwb.ap, bcast_row(normw, 2048), [], [nwb.b])
    xbs = [AB.get("xb_%d" % i, 1024) for i in range(2)]; xTs = [AB.get("xT_%d" % i, 8, 128) for i in range(2)]
    raws = [AB.get("raw_%d" % i, 24, 132) for i in range(2)]; xcs = [AB.get("xc_%d" % i, 24, 128) for i in range(2)]
    xtok = AB.get("xtok", 2560)
    xdt = AB.get("xdt", 512); xdts = AB.get("xdts", 512)
    Hb = AB.get("Hb", 2048); cbm = AB.get("cbm", 4, 128)
    LT = AB.get("LT", 4, 128); MT = AB.get("MT", 8, 128); yn = AB.get("yn", 2048)
    szb = AB.get("szb", 4, 512)
    adtU = [AFa.get("adtU%d" % i, 128) for i in range(8)]
    sm = AFa.get("sm", 12, 32)
    yacc = AFa.get("yacc", 512); ytmp = AFa.get("ytmp", 512); sz = AFa.get("sz", 512)
    ssq = AFa.get("ssq", 4)
    P.add("pool", lambda e: e.memset(raws[0].ap, 0.0), writes=[raws[0].b])
    P.add("pool", lambda e: e.memset(raws[1].ap, 0.0), writes=[raws[1].b])
    P.add("pool", lambda e: e.tensor_copy(out=raws[0].ap[:, :, 0:3], in_=rawh.ap[:, :, 0:3]), reads=[rawh.b], writes=[raws[0].b])

    CUT = int(os.environ.get('KCUT', '99')); NCH1 = int(os.environ.get('KNCH', str(NM1)))
    def front_pieces(ci, xrow, xb, xT, raw, xc, raw_next):
        def inproj(g):
            bank = pf[g % 2]
            for jj in range(4):
                j = 4 * g + jj
                chain(bank, bank.ap[:, jj * 128:(jj + 1) * 128], [(Wx.ap[:, k, j * 128:(j + 1) * 128], xT.ap[:, k, :]) for k in range(8)], [Wx.b, xT.b])
            P.add("act", lambda e, g=g, bank=bank: e.copy(out=raw.ap[:, 4 * g:4 * g + 4, 3:131], in_=bank.ap.rearrange("p (a b) -> p a b", a=4)), reads=[bank.b], writes=[raw.b])

        def conv(g):
            for jj in range(4):
                j = 4 * g + jj
                bank = pf[2 + j % 2]
                chain(bank, bank.ap[:, 0:128], [(diag.ap[:, j, tp, :], raw.ap[:, j, tp:tp + 128]) for tp in range(4)], [diag.b, raw.b])
                P.add("act", lambda e, j=j, bank=bank: e.activation(out=xc.ap[:, j, :], in_=bank.ap[:, 0:128], func=AF.Silu, bias=cbt.ap[:, j:j + 1], scale=1.0), reads=[bank.b, cbt.b], writes=[xc.b])

        def p0():
            load_xT(xrow, xb, xT); inproj(0); inproj(1)

        def p1():
            inproj(2); inproj(3)

        def p2():
            inproj(4); inproj(5)
            P.add("pool", lambda e: e.tensor_copy(out=raw_next.ap[:, :, 0:3], in_=raw.ap[:, :, 128:131]), reads=[raw.b], writes=[raw_next.b])

        def p3():
            conv(0); conv(1); conv(2); conv(3); conv(4); conv(5)
        return [p0, p1, p2, p3]

    def ssd_back(fcol, main, ci, xT, xc, hooks):
        for bt in range(3):
            nt = 8 if bt < 2 else 4
            for i in range(nt):
                j = bt * 8 + i
                P.add("pe", lambda e, i=i, j=j: e.transpose(pb[1].ap[:, i * 128:(i + 1) * 128], xc.ap[:, j, :], identb), reads=[xc.b, cb16.b], writes=[pb[1].b])
            P.add("dve", lambda e, bt=bt, nt=nt: e.tensor_copy(out=xtok.ap[:, bt * 1024:bt * 1024 + nt * 128], in_=pb[1].ap[:, 0:nt * 128]), reads=[pb[1].b], writes=[xtok.b])
        chain(pf[4], pf[4].ap[:, 0:32], [(xT.ap[:, k, :], Wdt.ap[:, k, :]) for k in range(8)], [xT.b, Wdt.b])
        S = lambda i: sm.ap[:, i, :]
        P.add("dve", lambda e: e.tensor_tensor(out=S(0), in0=pf[4].ap[:, 0:32], in1=dtb_bc, op=ALU.add), reads=[pf[4].b, smallp.b], writes=[sm.b])
        P.add("act", lambda e: e.activation(out=S(0), in_=S(0), func=AF.Exp), reads=[sm.b], writes=[sm.b])
        P.add("act", lambda e: e.activation(out=S(1), in_=S(0), func=AF.Ln, bias=onecol, scale=1.0), reads=[sm.b, cf.b], writes=[sm.b])
        P.add("dve", lambda e: e.tensor_scalar(out=S(1), in0=S(1), scalar1=fcol, scalar2=None, op0=ALU.mult), reads=[sm.b, flg.b], writes=[sm.b])
        P.add("dve", lambda e: e.tensor_tensor(out=S(2), in0=S(1), in1=a_bc, op=ALU.mult), reads=[sm.b, smallp.b], writes=[sm.b])
        P.add("pe", lambda e: e.matmul(pf[4].ap[:, 32:64], triU, S(2), start=True, stop=True), reads=[cf.b, sm.b], writes=[pf[4].b])
        P.add("pe", lambda e: e.matmul(pf[4].ap[:, 64:96], onesf, S(2), start=True, stop=True), reads=[cf.b, sm.b], writes=[pf[4].b])
        P.add("dve", lambda e: e.tensor_copy(out=sm.ap[:, 3:5, :], in_=pf[4].ap[:, 32:96].rearrange("p (a b) -> p a b", a=2)), reads=[pf[4].b], writes=[sm.b])
        if main:
            for g in range(4):
                P.add("pe", lambda e, g=g: e.matmul(pf[5].ap[:, g * 128:(g + 1) * 128], xc.ap[:, 16 + g, :], xc.ap[:, 20 + g, :], start=True, stop=True), reads=[xc.b], writes=[pf[5].b])
            P.add("dve", lambda e: e.tensor_tensor(out=cbm.ap, in0=pf[5].ap.rearrange("p (a b) -> p a b", a=4), in1=maskb.unsqueeze(1).to_broadcast([128, 4, 128]), op=ALU.mult), reads=[pf[5].b, cb16.b], writes=[cbm.b])
            P.add("pool", lambda e: e.tensor_copy(out=Hb.ap, in_=H.ap), reads=[H.b], writes=[Hb.b])
            P.add("act", lambda e: e.activation(out=S(5), in_=S(3), func=AF.Exp), reads=[sm.b], writes=[sm.b])
            for g in range(4):
                zb = pf[5 - g % 2]
                chain(zb, zb.ap, [(xT.ap[:, k, :], Wz.ap[:, k, g * 512:(g + 1) * 512]) for k in range(8)], [xT.b, Wz.b])
                P.add("act", lambda e, g=g, zb=zb: e.activation(out=szb.ap[:, g, :], in_=zb.ap, func=AF.Silu), reads=[zb.b], writes=[szb.b])
            for g in range(4):
                xg = xtok.ap[:, g * 512:(g + 1) * 512].rearrange("p (a b) -> p a b", a=8)
                P.add("dve", lambda e, g=g, xg=xg: e.tensor_tensor(out=xdt.ap.rearrange("p (a b) -> p a b", a=8), in0=xg, in1=bc(sm.ap[:, 1, 8 * g:8 * g + 8], 64), op=ALU.mult), reads=[xtok.b, sm.b], writes=[xdt.b])
                for hh in range(2):
                    bank = pf[hh]
                    for h4 in range(4):
                        h = 8 * g + 4 * hh + h4
                        au = adtU[h % 8]
                        P.add("dve", lambda e, h=h, au=au: e.tensor_scalar(out=au.ap, in0=Ustr, scalar1=sm.ap[:, 2, h:h + 1], scalar2=None, op0=ALU.mult), reads=[cf.b, sm.b], writes=[au.b])
                        P.add("pe", lambda e, h4=h4, au=au, bank=bank: e.matmul(bank.ap[:, h4 * 128:(h4 + 1) * 128], au.ap, triU, start=True, stop=True), reads=[au.b, cf.b], writes=[bank.b])
                    P.add("act", lambda e, bank=bank: e.activation(out=LT.ap.rearrange("p a b -> p (a b)"), in_=bank.ap, func=AF.Exp), reads=[bank.b], writes=[LT.b])
                    P.add("dve", lambda e, g=g, hh=hh: e.tensor_tensor(out=MT.ap[:, 4 * hh:4 * hh + 4, :], in0=LT.ap, in1=cbm.ap[:, g, :].unsqueeze(1).to_broadcast([128, 4, 128]), op=ALU.mult), reads=[LT.b, cbm.b], writes=[MT.b])
                for h8 in range(8):
                    P.add("pe", lambda e, h8=h8: e.matmul(pf[2].ap[:, h8 * 64:(h8 + 1) * 64], MT.ap[:, h8, :], xdt.ap[:, h8 * 64:(h8 + 1) * 64], start=True, stop=True), reads=[MT.b, xdt.b], writes=[pf[2].b])
                P.add("pe", lambda e, g=g: e.matmul(pf[3].ap, xc.ap[:, 20 + g, :], Hb.ap[:, g * 512:(g + 1) * 512], start=True, stop=True), reads=[xc.b, Hb.b], writes=[pf[3].b])
                v3 = lambda ap: ap.rearrange("p (a b) -> p a b", a=8)
                P.add("dve", lambda e, g=g: e.tensor_tensor(out=v3(yacc.ap), in0=v3(pf[3].ap), in1=bc(sm.ap[:, 5, 8 * g:8 * g + 8], 64), op=ALU.mult), reads=[pf[3].b, sm.b], writes=[yacc.b])
                P.add("dve", lambda e: e.tensor_tensor(out=yacc.ap, in0=yacc.ap, in1=pf[2].ap, op=ALU.add), reads=[pf[2].b, yacc.b], writes=[yacc.b])
                P.add("dve", lambda e, g=g, xg=xg: e.tensor_tensor(out=v3(ytmp.ap), in0=xg, in1=bc(D_bc[:, 8 * g:8 * g + 8], 64), op=ALU.mult), reads=[xtok.b, smallp.b], writes=[ytmp.b])
                P.add("dve", lambda e: e.tensor_tensor(out=yacc.ap, in0=yacc.ap, in1=ytmp.ap, op=ALU.add), reads=[ytmp.b, yacc.b], writes=[yacc.b])
                P.add("dve", lambda e, g=g: e.tensor_tensor(out=yacc.ap, in0=yacc.ap, in1=szb.ap[:, g, :], op=ALU.mult), reads=[szb.b, yacc.b], writes=[yacc.b])
                P.add("dve", lambda e: e.tensor_tensor(out=ytmp.ap, in0=yacc.ap, in1=yacc.ap, op=ALU.mult), reads=[yacc.b], writes=[ytmp.b])
                P.add("dve", lambda e, g=g: e.reduce_sum(out=ssq.ap[:, g:g + 1], in_=ytmp.ap, axis=mybir.AxisListType.X), reads=[ytmp.b], writes=[ssq.b])
                P.add("dve", lambda e, g=g: e.tensor_scalar(out=ssq.ap[:, g:g + 1], in0=ssq.ap[:, g:g + 1], scalar1=1.0 / 512, scalar2=1e-5, op0=ALU.mult, op1=ALU.add), reads=[ssq.b], writes=[ssq.b])
                P.add("act", lambda e, g=g: e.activation(out=ssq.ap[:, g:g + 1], in_=ssq.ap[:, g:g + 1], func=AF.Ln), reads=[ssq.b], writes=[ssq.b])
                P.add("act", lambda e, g=g: e.activation(out=ssq.ap[:, g:g + 1], in_=ssq.ap[:, g:g + 1], func=AF.Exp, scale=-0.5), reads=[ssq.b], writes=[ssq.b])
                P.add("dve", lambda e, g=g: e.scalar_tensor_tensor(out=yn.ap[:, g * 512:(g + 1) * 512], in0=yacc.ap, scalar=ssq.ap[:, g:g + 1], in1=nwb.ap[:, g * 512:(g + 1) * 512], op0=ALU.mult, op1=ALU.mult), reads=[yacc.b, ssq.b, nwb.b], writes=[yn.b])
                hooks[g]()
            dma("sp", ynd[ci * 128:(ci + 1) * 128, :], yn.ap, [yn.b], [])
        P.add("dve", lambda e: e.tensor_tensor(out=S(6), in0=S(4), in1=S(3), op=ALU.subtract), reads=[sm.b], writes=[sm.b])
        P.add("act", lambda e: e.activation(out=S(6), in_=S(6), func=AF.Exp), reads=[sm.b], writes=[sm.b])
        P.add("act", lambda e: e.activation(out=S(7), in_=S(4), func=AF.Exp), reads=[sm.b], writes=[sm.b])
        P.add("dve", lambda e: e.tensor_tensor(out=S(6), in0=S(6), in1=S(1), op=ALU.mult), reads=[sm.b], writes=[sm.b])
        for g in range(4):
            xg = xtok.ap[:, g * 512:(g + 1) * 512].rearrange("p (a b) -> p a b", a=8)
            v3 = lambda ap: ap.rearrange("p (a b) -> p a b", a=8)
            P.add("dve", lambda e, g=g, xg=xg: e.tensor_tensor(out=v3(xdts.ap), in0=xg, in1=bc(sm.ap[:, 6, 8 * g:8 * g + 8], 64), op=ALU.mult), reads=[xtok.b, sm.b], writes=[xdts.b])
            bank = pf[2 + g % 2]
            P.add("pe", lambda e, g=g, bank=bank: e.matmul(bank.ap, xtok.ap[:, 2048 + g * 128:2048 + (g + 1) * 128], xdts.ap, start=True, stop=True), reads=[xtok.b, xdts.b], writes=[bank.b])
            Hg = H.ap[:, g * 512:(g + 1) * 512]
            P.add("dve", lambda e, g=g, Hg=Hg: e.tensor_tensor(out=v3(Hg), in0=v3(Hg), in1=bc(sm.ap[:, 7, 8 * g:8 * g + 8], 64), op=ALU.mult), reads=[H.b, sm.b], writes=[H.b])
            P.add("dve", lambda e, Hg=Hg, bank=bank: e.tensor_tensor(out=Hg, in0=Hg, in1=bank.ap, op=ALU.add), reads=[H.b, bank.b], writes=[H.b])

    def fp(ci):
        b = ci % 2
        return front_pieces(ci, xmain[(ci + 1) * 128:(ci + 2) * 128, :], xbs[b], xTs[b], raws[b], xcs[b], raws[1 - b])
    nch = 0 if SKIP1 else NCH1
    if nch:
        for p_ in fp(0):
            p_()
    for ci in range(nch):
        nxt = fp(ci + 1) if ci + 1 < nch else [lambda: None] * 4
        ssd_back(flg.ap[:, NPRE + ci:NPRE + ci + 1], True, ci, xTs[ci % 2], xcs[ci % 2], nxt)

    if stop < 2:
        P.emit(nc); es.close(); return nc
    P.barrier()
    AB.off, AFa.off = markB, markF
    Wq = AB.get("Wq", 8, 1024); load_w(Wq, w_in[:, COL_Q:COL_Q + 1024], 1024)
    Wk2 = AB.get("Wk2", 8, 128); load_w(Wk2, w_in[:, COL_K:COL_K + 128], 128)
    Wv = AB.get("Wv", 8, 128); load_w(Wv, w_in[:, COL_V:COL_V + 128], 128)
    EB = AB.get("EB", 2, 16, 128)
    ebf = AFa.get("ebf", 2048); mkf = AFa.get("mkf", 2048)
    for kt in range(2):
        dma("sp", ebf.ap, biasg[:, kt * 2048:(kt + 1) * 2048], [], [ebf.b])
        dma("sp", mkf.ap, maskg[:, kt * 2048:(kt + 1) * 2048], [], [mkf.b])
        P.add("act", lambda e: e.activation(out=ebf.ap, in_=ebf.ap, func=AF.Exp), reads=[ebf.b], writes=[ebf.b])
        P.add("dve", lambda e, kt=kt: e.tensor_tensor(out=EB.ap[:, kt, :, :].rearrange("p a b -> p (a b)"), in0=ebf.ap, in1=mkf.ap, op=ALU.mult), reads=[ebf.b, mkf.b], writes=[EB.b])
    xb2s = [AB.get("xb2_%d" % i, 1024) for i in range(2)]; xT = AB.get("xT2", 8, 128)
    kT = [AB.get("kT%d" % i, 2, 128) for i in range(2)]
    vx = [AB.get("vx%d" % i, 2, 65) for i in range(2)]
    for i in range(2):
        P.add("pool", lambda e, i=i: e.memset(vx[i].ap, 1.0), writes=[vx[i].b])
    qT = AB.get("qT", 16, 128); et = AB.get("et", 4, 128)
    PT = [AB.get("PT%d" % i, 4, 128) for i in range(2)]
    ya = AB.get("ya", 1024)
    den = AFa.get("den", 4)
    flag0 = flg.ap[:, NPRE:NPRE + 1]
    CUT2 = int(os.environ.get('KCUT2', '99')); NCH2 = int(os.environ.get('KNCH2', str(NM2)))
    for ci in range(NCH2):
        sl = ci % 2
        load_xT(xmain[ci * 128:(ci + 1) * 128, :], xb2s[ci % 2], xT)
        for kv in range(2):
            chain(pf[0], pf[0].ap[0:64, kv * 128:(kv + 1) * 128], [(Wk2.ap[:, k, kv * 64:(kv + 1) * 64], xT.ap[:, k, :]) for k in range(8)], [Wk2.b, xT.b])
        P.add("act", lambda e, sl=sl: e.copy(out=kT[sl].ap[0:64, :, :], in_=pf[0].ap[0:64, 0:256].rearrange("p (a b) -> p a b", a=2)), reads=[pf[0].b], writes=[kT[sl].b])
        chain(pf[1], pf[1].ap[:, 0:128], [(xT.ap[:, k, :], Wv.ap[:, k, :]) for k in range(8)], [xT.b, Wv.b])
        P.add("dve", lambda e, sl=sl: e.tensor_copy(out=vx[sl].ap[:, :, 0:64], in_=pf[1].ap[:, 0:128].rearrange("p (a b) -> p a b", a=2)), reads=[pf[1].b], writes=[vx[sl].b])
        if ci == 0 or CUT2 <= 1:
            continue
        for q4 in range(4):
            bank = pf[2 + q4 % 2]
            for tt in range(4):
                j = q4 * 4 + tt
                chain(bank, bank.ap[0:64, tt * 128:(tt + 1) * 128], [(Wq.ap[:, k, j * 64:(j + 1) * 64], xT.ap[:, k, :]) for k in range(8)], [Wq.b, xT.b])
            P.add("act", lambda e, q4=q4, bank=bank: e.copy(out=qT.ap[0:64, q4 * 4:q4 * 4 + 4, :], in_=bank.ap[0:64, :].rearrange("p (a b) -> p a b", a=4)), reads=[bank.b], writes=[qT.b])
        for kvh in range(2):
            if CUT2 <= 2: break
            for hb in range(2):
                j0 = kvh * 8 + hb * 4
                for kt in range(2):
                    slk = (ci + 1 + kt) % 2
                    bank = pf[4 + kt]
                    for i in range(4):
                        j = j0 + i
                        base = (j % 2) * 64 * int(os.environ.get("KB64", "1"))
                        P.add("pe", lambda e, i=i, j=j, base=base, slk=slk, bank=bank, kvh=kvh: e.matmul(bank.ap[:, i * 128:(i + 1) * 128], kT[slk].ap[0:64, kvh, :], qT.ap[0:64, j, :], start=True, stop=True), reads=[kT[slk].b, qT.b], writes=[bank.b])
                    P.add("act", lambda e, bank=bank: e.activation(out=et.ap.rearrange("p a b -> p (a b)"), in_=bank.ap, func=AF.Exp, scale=0.125), reads=[bank.b], writes=[et.b])
                    if ci == 2 and kt == 0:
                        P.add("dve", lambda e, kt=kt, j0=j0: e.scalar_tensor_tensor(out=PT[kt].ap, in0=et.ap, scalar=flag0, in1=EB.ap[:, kt, j0:j0 + 4, :], op0=ALU.mult, op1=ALU.mult), reads=[et.b, EB.b, flg.b], writes=[PT[kt].b])
                    else:
                        P.add("dve", lambda e, kt=kt, j0=j0: e.tensor_tensor(out=PT[kt].ap, in0=et.ap, in1=EB.ap[:, kt, j0:j0 + 4, :], op=ALU.mult), reads=[et.b, EB.b], writes=[PT[kt].b])
                if CUT2 <= 3: continue
                bank = pf[hb]
                for i in range(4):
                    for kt in range(2):
                        slk = (ci + 1 + kt) % 2
                        P.add("pe", lambda e, i=i, kt=kt, slk=slk, bank=bank, kvh=kvh: e.matmul(bank.ap[:, i * 65:(i + 1) * 65], PT[kt].ap[:, i, :], vx[slk].ap[:, kvh, :], start=(kt == 0), stop=(kt == 1)), reads=[PT[kt].b, vx[slk].b], writes=[bank.b])
                if CUT2 <= 4: continue
                pv = bank.ap[:, 0:260].rearrange("p (a b) -> p a b", a=4)
                P.add("dve", lambda e, pv=pv, j0=j0: e.tensor_tensor(out=den.ap, in0=pv[:, :, 64], in1=esink[:, j0:j0 + 4], op=ALU.add), reads=[bank.b, smallp.b], writes=[den.b])
                P.add("dve", lambda e: e.reciprocal(out=den.ap, in_=den.ap), reads=[den.b], writes=[den.b])
                P.add("dve", lambda e, pv=pv, j0=j0: e.tensor_tensor(out=ya.ap[:, j0 * 64:(j0 + 4) * 64].rearrange("p (a b) -> p a b", a=4), in0=pv[:, :, 0:64], in1=bc(den.ap, 64), op=ALU.mult), reads=[bank.b, den.b], writes=[ya.b])
        dma("sp", yad[(ci - 1) * 128:ci * 128, :], ya.ap, [ya.b], [])

    if stop < 3:
        P.emit(nc); es.close(); return nc
    P.barrier()
    AB.off, AFa.off = markB, markF
    Wg = AB.get("Wg", 8, 2048); load_w(Wg, w_in[:, COL_G:COL_G + 2048], 2048)
    Wbs = AB.get("Wbs", 16, 1024); load_w(Wbs, w_bs, 1024)
    Wba = AB.get("Wba", 8, 1024); load_w(Wba, w_ba, 1024)
    Wmx = AB.get("Wmx", 8, 1024); load_w(Wmx, w_mix, 1024)
    bgb = AFa.get("bgb", 2048); dma("sp", bgb.ap, bcast_row(b_gate, 2048), [], [bgb.b])
    lng = AFa.get("lng", 2, 1024)
    dma("sp", lng.ap[:, 0, :], bcast_row(ln1g, 1024), [], [lng.b]); dma("sp", lng.ap[:, 1, :], bcast_row(ln1b, 1024), [], [lng.b])
    xb = AB.get("xb3", 1024); xT = AB.get("xT3", 8, 128)
    ynb = AB.get("ynb", 2048); yab = AB.get("yab", 1024)
    ynT = AB.get("ynT", 16, 128); yaT = AB.get("yaT", 8, 128)
    mg = AB.get("mg", 1024); mT = AB.get("mT", 8, 128)
    xf = AFa.get("xf", 1024); gt = AB.get("gt", 2048); m1 = AFa.get("m1", 512); r = AFa.get("r", 1024)
    st = AFa.get("st", 2, 6); mv = AFa.get("mv", 2)

    def transp(src, dst, ntl):
        for bt in range(ntl // 8):
            for i in range(8):
                j = bt * 8 + i
                P.add("pe", lambda e, i=i, j=j: e.transpose(pb[1].ap[:, i * 128:(i + 1) * 128], src.ap[:, j * 128:(j + 1) * 128], identb), reads=[src.b, cb16.b], writes=[pb[1].b])
            P.add("act", lambda e, bt=bt: e.copy(out=dst.ap[:, bt * 8:bt * 8 + 8, :].rearrange("p a b -> p (a b)"), in_=pb[1].ap), reads=[pb[1].b], writes=[dst.b])

    def layer_norm(r, g_ap, b_ap, gb, st, mv):
        for i in range(2):
            P.add("dve", lambda e, i=i: e.bn_stats(out=st.ap[:, i, :], in_=r.ap[:, i * 512:(i + 1) * 512]), reads=[r.b], writes=[st.b])
        P.add("dve", lambda e: e.bn_aggr(out=mv.ap, in_=st.ap.rearrange("p a b -> p (a b)")), reads=[st.b], writes=[mv.b])
        P.add("dve", lambda e: e.tensor_scalar(out=mv.ap[:, 1:2], in0=mv.ap[:, 1:2], scalar1=1e-5, scalar2=None, op0=ALU.add), reads=[mv.b], writes=[mv.b])
        P.add("act", lambda e: e.activation(out=mv.ap[:, 1:2], in_=mv.ap[:, 1:2], func=AF.Sqrt), reads=[mv.b], writes=[mv.b])
        P.add("dve", lambda e: e.reciprocal(out=mv.ap[:, 1:2], in_=mv.ap[:, 1:2]), reads=[mv.b], writes=[mv.b])
        P.add("dve", lambda e: e.tensor_scalar(out=r.ap, in0=r.ap, scalar1=mv.ap[:, 0:1], scalar2=mv.ap[:, 1:2], op0=ALU.subtract, op1=ALU.mult), reads=[r.b, mv.b], writes=[r.b])
        P.add("dve", lambda e: e.tensor_tensor(out=r.ap, in0=r.ap, in1=g_ap, op=ALU.mult), reads=[r.b, gb], writes=[r.b])
        P.add("dve", lambda e: e.tensor_tensor(out=r.ap, in0=r.ap, in1=b_ap, op=ALU.add), reads=[r.b, gb], writes=[r.b])

    def p3_loads(ci, xb, xf, ynb, yab):
        xrow = xmain[(ci + 1) * 128:(ci + 2) * 128, :]
        dma("pool", xb.ap, xrow, [], [xb.b])
        dma("sp", xf.ap, xrow, [], [xf.b])
        dma("sp", ynb.ap, ynd[ci * 128:(ci + 1) * 128, :], [], [ynb.b])
        dma("sp", yab.ap, yad[ci * 128:(ci + 1) * 128, :], [], [yab.b])

    def p3_chunk(ci, xb, xf, ynb, yab, r):
        for k in range(8):
            P.add("pe", lambda e, k=k: e.transpose(pb[0].ap[:, k * 128:(k + 1) * 128], xb.ap[:, k * 128:(k + 1) * 128], identb), reads=[xb.b, cb16.b], writes=[pb[0].b])
        P.add("act", lambda e: e.copy(out=xT.ap.rearrange("p a b -> p (a b)"), in_=pb[0].ap), reads=[pb[0].b], writes=[xT.b])
        transp(ynb, ynT, 16)
        transp(yab, yaT, 8)
        for s4 in range(4):
            bank = pf[s4 % 2]
            chain(bank, bank.ap, [(xT.ap[:, k, :], Wg.ap[:, k, s4 * 512:(s4 + 1) * 512]) for k in range(8)], [xT.b, Wg.b])
            P.add("dve", lambda e, s4=s4, bank=bank: e.tensor_tensor(out=gt.ap[:, s4 * 512:(s4 + 1) * 512], in0=bank.ap, in1=bgb.ap[:, s4 * 512:(s4 + 1) * 512], op=ALU.add), reads=[bank.b, bgb.b], writes=[gt.b])
        P.add("act", lambda e: e.activation(out=gt.ap, in_=gt.ap, func=AF.Sigmoid), reads=[gt.b], writes=[gt.b])
        for hf in range(2):
            chain(pf[2], pf[2].ap, [(ynT.ap[:, i, :], Wbs.ap[:, i, hf * 512:(hf + 1) * 512]) for i in range(16)], [ynT.b, Wbs.b])
            chain(pf[3], pf[3].ap, [(yaT.ap[:, i, :], Wba.ap[:, i, hf * 512:(hf + 1) * 512]) for i in range(8)], [yaT.b, Wba.b])
            P.add("dve", lambda e, hf=hf: e.tensor_tensor(out=m1.ap, in0=pf[2].ap, in1=gt.ap[:, hf * 512:(hf + 1) * 512], op=ALU.mult), reads=[pf[2].b, gt.b], writes=[m1.b])
            P.add("dve", lambda e, hf=hf: e.tensor_tensor(out=r.ap[:, hf * 512:(hf + 1) * 512], in0=pf[3].ap, in1=gt.ap[:, 1024 + hf * 512:1024 + (hf + 1) * 512], op=ALU.mult), reads=[pf[3].b, gt.b], writes=[r.b])
            P.add("dve", lambda e, hf=hf: e.tensor_tensor(out=mg.ap[:, hf * 512:(hf + 1) * 512], in0=m1.ap, in1=r.ap[:, hf * 512:(hf + 1) * 512], op=ALU.add), reads=[m1.b, r.b], writes=[mg.b])
        transp(mg, mT, 8)
        for hf in range(2):
            bank = pf[4 + hf]
            chain(bank, bank.ap, [(mT.ap[:, i, :], Wmx.ap[:, i, hf * 512:(hf + 1) * 512]) for i in range(8)], [mT.b, Wmx.b])
            P.add("dve", lambda e, hf=hf, bank=bank: e.scalar_tensor_tensor(out=r.ap[:, hf * 512:(hf + 1) * 512], in0=xf.ap[:, hf * 512:(hf + 1) * 512], scalar=ALPHA, in1=bank.ap, op0=ALU.mult, op1=ALU.add), reads=[xf.b, bank.b], writes=[r.b])
        layer_norm(r, lng.ap[:, 0, :], lng.ap[:, 1, :], lng.b, st, mv)
        if ci == 0:
            P.add("dve", lambda e: e.tensor_scalar(out=r.ap, in0=r.ap, scalar1=flag0, scalar2=None, op0=ALU.mult), reads=[r.b, flg.b], writes=[r.b])
        dma("sp", h1d[ci * 128:(ci + 1) * 128, :], r.ap, [r.b], [])


    xb3 = [xb, AB.get("xb3b", 1024)]; xf3 = [xf, AFa.get("xf3b", 1024)]
    ynb3 = [ynb, AB.get("ynb3b", 2048)]; yab3 = [yab, AB.get("yab3b", 1024)]; r3 = [r, AFa.get("r3b", 1024)]
    p3_loads(0, xb3[0], xf3[0], ynb3[0], yab3[0])
    for ci in range(NM1):
        if ci + 1 < NM1:
            b_ = (ci + 1) % 2
            p3_loads(ci + 1, xb3[b_], xf3[b_], ynb3[b_], yab3[b_])
        b_ = ci % 2
        p3_chunk(ci, xb3[b_], xf3[b_], ynb3[b_], yab3[b_], r3[b_])

    if stop < 4:
        P.emit(nc); es.close(); return nc
    P.barrier()
    AB.off, AFa.off = markB, markF
    Wup = AB.get("Wup", 8, 5632); load_w(Wup, w_up, 5632)
    Wdn = AB.get("Wdn", 22, 1024); load_w(Wdn, w_dn, 1024)
    fw = AFa.get("fw", 132); fb = AFa.get("fb", 44)
    per_channel(fw.ap[:, 0:88], fcw[0:88, :], 88); fw.b.writer = P.ops["dve"][-1]
    per_channel(fw.ap[:, 88:132], fcw[88:132, :], 44); fw.b.writer = P.ops["dve"][-1]
    per_channel(fb.ap, fcb, 44); fb.b.writer = P.ops["dve"][-1]
    lng2 = AFa.get("lng2", 2, 1024)
    dma("sp", lng2.ap[:, 0, :], bcast_row(ln2g, 1024), [], [lng2.b]); dma("sp", lng2.ap[:, 1, :], bcast_row(ln2b, 1024), [], [lng2.b])
    SC = 256
    hAs = [AB.get("hA%d" % i, 1024) for i in range(2)]; hBs = [AB.get("hB%d" % i, 1024) for i in range(2)]; h1T = AB.get("h1T", 8, SC + 2)
    aT = AB.get("aT", 22, SC)
    aTb = [Buf("aTb%d" % i) for i in range(22)]
    cvs = [AFa.get("cvs%d" % i, SC) for i in range(4)]
    t0s = [AFa.get("t0s%d" % i, SC) for i in range(2)]
    sgs = [AFa.get("sg%d" % i, SC) for i in range(2)]
    hrs = [AFa.get("hr%d" % i, 1024) for i in range(2)]; r4s = [AFa.get("r4_%d" % i, 1024) for i in range(2)]
    st4 = AFa.get("st4", 2, 6); mv4 = AFa.get("mv4", 2)
    NT = SC // 128
    tcount = 0
    for sc in range(TOK // SC):
        r0 = 128 + sc * SC
        for i in range(NT):
            hA = hAs[i % 2]
            dma("pool", hA.ap, h1d[r0 - 2 + i * 128:r0 + 126 + i * 128, :], [], [hA.b])
            for k in range(8):
                P.add("pe", lambda e, k=k, hA=hA: e.transpose(pb[0].ap[:, k * 128:(k + 1) * 128], hA.ap[:, k * 128:(k + 1) * 128], identb), reads=[hA.b, cb16.b], writes=[pb[0].b])
            P.add("act", lambda e, i=i: e.copy(out=h1T.ap[:, :, i * 128:(i + 1) * 128], in_=pb[0].ap.rearrange("p (a b) -> p a b", a=8)), reads=[pb[0].b], writes=[h1T.b])
        hB = hBs[sc % 2]
        dma("pool", hB.ap[0:2, :], h1d[r0 + SC - 2:r0 + SC, :], [], [hB.b])
        for k in range(8):
            P.add("pe", lambda e, k=k, hB=hB: e.transpose(pb[1].ap[:, k * 2:(k + 1) * 2], hB.ap[0:2, k * 128:(k + 1) * 128], identb[0:2, 0:2]), reads=[hB.b, cb16.b], writes=[pb[1].b])
        P.add("act", lambda e: e.copy(out=h1T.ap[:, :, SC:SC + 2], in_=pb[1].ap[:, 0:16].rearrange("p (a b) -> p a b", a=8)), reads=[pb[1].b], writes=[h1T.b])
        def down(jp):
            for tt in range(NT):
                for hf in range(2):
                    bank = pf[2 + tt * 2 + hf]
                    P.add("pe", lambda e, jp=jp, tt=tt, hf=hf, bank=bank: e.matmul(bank.ap, aT.ap[:, jp, tt * 128:(tt + 1) * 128], Wdn.ap[:, jp, hf * 512:(hf + 1) * 512], start=(jp == 0), stop=(jp == 21)), reads=[aTb[jp], Wdn.b], writes=[bank.b])

        for jp in range(22):
            for gv in range(2):
                j = jp + 22 * gv
                X = pf[tcount % 2]; t0 = t0s[tcount % 2]
                cv = cvs[(jp % 2) * 2 + gv]
                tcount += 1
                chain(X, X.ap[:, 0:SC + 2], [(Wup.ap[:, k, j * 128:(j + 1) * 128], h1T.ap[:, k, 0:SC + 2]) for k in range(8)], [Wup.b, h1T.b])
                w0, w1, w2, bb = fw.ap[:, j:j + 1], fw.ap[:, 44 + j:45 + j], fw.ap[:, 88 + j:89 + j], fb.ap[:, j:j + 1]
                P.add("act", lambda e, X=X, t0=t0, w2=w2, bb=bb: e.activation(out=t0.ap, in_=X.ap[:, 2:SC + 2], func=AF.Identity, bias=bb, scale=w2), reads=[X.b, fw.b, fb.b], writes=[t0.b])
                P.add("dve", lambda e, X=X, t0=t0, w1=w1: e.scalar_tensor_tensor(out=t0.ap, in0=X.ap[:, 1:SC + 1], scalar=w1, in1=t0.ap, op0=ALU.mult, op1=ALU.add), reads=[X.b, fw.b, t0.b], writes=[t0.b])
                P.add("dve", lambda e, X=X, t0=t0, w0=w0, cv=cv: e.scalar_tensor_tensor(out=cv.ap, in0=X.ap[:, 0:SC], scalar=w0, in1=t0.ap, op0=ALU.mult, op1=ALU.add), reads=[X.b, fw.b, t0.b], writes=[cv.b])
            cg, cvv = cvs[(jp % 2) * 2], cvs[(jp % 2) * 2 + 1]
            sgp = sgs[jp % 2]
            P.add("act", lambda e, cg=cg, sgp=sgp: e.activation(out=sgp.ap, in_=cg.ap, func=AF.Silu), reads=[cg.b], writes=[sgp.b])
            P.add("dve", lambda e, jp=jp, cvv=cvv, sgp=sgp: e.tensor_tensor(out=aT.ap[:, jp, :], in0=sgp.ap, in1=cvv.ap, op=ALU.mult), reads=[sgp.b, cvv.b], writes=[aTb[jp]])
            if jp >= 1:
                down(jp - 1)
        down(21)
        for tt in range(NT):
            rr = r0 + tt * 128
            hr = hrs[tt % 2]; r4 = r4s[tt % 2]
            dma("sp", hr.ap, h1d[rr:rr + 128, :], [], [hr.b])
            for hf in range(2):
                bank = pf[2 + tt * 2 + hf]
                P.add("dve", lambda e, hf=hf, bank=bank, hr=hr, r4=r4: e.scalar_tensor_tensor(out=r4.ap[:, hf * 512:(hf + 1) * 512], in0=hr.ap[:, hf * 512:(hf + 1) * 512], scalar=ALPHA, in1=bank.ap, op0=ALU.mult, op1=ALU.add), reads=[hr.b, bank.b], writes=[r4.b])
            layer_norm(r4, lng2.ap[:, 0, :], lng2.ap[:, 1, :], lng2.b, st4, mv4)
            dma("sp", out[rr - 128:rr, :], r4.ap, [r4.b], [])

    P.emit(nc)
    es.close()
    return nc


def rel_bucket_np(rel):
    n = np.maximum(rel, 0)
    nf = np.maximum(n, 1).astype(np.float32)
    large = 16 + (np.log(nf / np.float32(16)) / np.float32(np.log(128 / 16)) * np.float32(16)).astype(np.int32)
    large = np.minimum(large, 31)
    return np.where(n < 16, n, large)


_NC = None


def kernel(_dbg=None, **inp):
    global _NC
    x = np.asarray(inp["x"], np.float32)[0]
    f = lambda k: np.ascontiguousarray(np.asarray(inp[k], np.float32)[0])
    common = {
        "w_in": f("w_in"), "b_gate": f("b_gate")[None], "dtb": f("ssm_dt_bias")[None], "alog": f("ssm_a_log")[None],
        "dsk": f("ssm_d")[None], "normw": f("ssm_norm_w")[None], "sinks": f("attn_sinks")[None],
        "w_bs": f("w_branch_ssm"), "w_ba": f("w_branch_attn"), "w_mix": f("w_mix_out"),
        "ln1g": f("ln1_g")[None], "ln1b": f("ln1_b")[None], "ln2g": f("ln2_g")[None], "ln2b": f("ln2_b")[None],
        "w_up": f("w_up"), "w_dn": f("w_down"),
    }
    scw = f("ssm_conv_w")
    common["scw"] = np.ascontiguousarray(scw.reshape(4 * 24, 128))
    common["scb"] = np.ascontiguousarray(f("ssm_conv_b").reshape(24, 128))
    common["fcw"] = np.ascontiguousarray(f("ffn_conv_w").reshape(3 * 44, 128))
    common["fcb"] = np.ascontiguousarray(f("ffn_conv_b").reshape(44, 128))
    s = np.arange(128)
    ident = np.eye(128, dtype=np.float32)
    triU = (s[:, None] <= s[None, :]).astype(np.float32)
    ustr = (s[:, None] > s[None, :]).astype(np.float32)
    common["cst"] = np.ascontiguousarray(np.concatenate([ident, triU, ustr, np.ones((128, 128), np.float32)], axis=1))
    rb = np.asarray(inp["rel_bias"], np.float32)
    bg = np.zeros((128, 2, 16, 128), np.float32); mk = np.zeros((128, 2, 16, 128), np.float32)
    for kt in range(2):
        rel = (s[None, :] + 128) - (s[:, None] + 128 * kt)
        valid = (rel >= 0) & (rel < 128)
        bidx = rel_bucket_np(rel)
        g = rb[bidx]
        bg[:, kt] = np.transpose(g, (0, 2, 1))
        mk[:, kt] = np.broadcast_to(valid[:, None, :], (128, 16, 128))
    common["biasg"] = np.ascontiguousarray(bg.reshape(128, -1)); common["maskg"] = np.ascontiguousarray(mk.reshape(128, -1))
    in_maps = []
    for c in range(NCORE):
        S = c * TOK
        lo = S - 128 - NPRE * 128
        xp = np.zeros((NPRE * 128, 1024), np.float32)
        if S - 128 > 0:
            src_lo = max(lo, 0)
            xp[src_lo - lo:] = x[src_lo:S - 128]
        xm = np.zeros((NM2 * 128, 1024), np.float32)
        lo2 = S - 256
        src_lo = max(lo2, 0)
        xm[src_lo - lo2:] = x[src_lo:S + TOK]
        fl = np.zeros((128, NPRE + NM1), np.float32)
        for i in range(NPRE):
            fl[:, i] = 1.0 if lo + i * 128 >= 0 else 0.0
        fl[:, NPRE] = 1.0 if c > 0 else 0.0
        fl[:, NPRE + 1:] = 1.0
        m = dict(common); m["xpre"] = xp; m["xmain"] = xm; m["pflag"] = fl
        in_maps.append(m)
    if _dbg is not None:
        return in_maps
    if _NC is None:
        _NC = build()
    res = run_bass_kernel_spmd(_NC, in_maps, core_ids=list(range(NCORE)))
    o = np.concatenate([res.results[c]["out"] for c in range(NCORE)], axis=0)
    return o[None].astype(np.float32)
```

```python
import numpy as np
import concourse.bass as bass
import concourse.mybir as mybir

ENGS = ["pe", "act", "dve", "pool", "sp"]
N_DMA_SEM = 16
import os as _os
SAME_ENGINE_SYNC = _os.environ.get("KSES", "1") == "1"


class Buf:
    __slots__ = ("name", "writer", "readers", "dma_readers", "excl")

    def __init__(self, name, excl=False):
        self.name = name
        self.excl = excl
        self.writer = None
        self.readers = {}
        self.dma_readers = []


class Op:
    __slots__ = ("eng", "fn", "idx", "waits", "signal", "is_dma", "dsem", "dtarget", "clock", "sigcount", "uid")


class Prog:
    def __init__(self):
        self.ops = {e: [] for e in ENGS}
        self.clock = {e: {f: 0 for f in ENGS} for e in ENGS}
        self.known_dma = {e: set() for e in ENGS}
        self.dma_sem_count = [0] * N_DMA_SEM
        self.dma_sem_last = [None] * N_DMA_SEM
        self.dma_rr = 0
        self.n_dma = 0
        self.uid = 0
        self.bar = {e: [] for e in ENGS}

    def barrier(self):
        lasts = [self.ops[e][-1] for e in ENGS if self.ops[e] and not self.ops[e][-1].is_dma]
        for e in ENGS:
            pass
        lasts = []
        for e in ENGS:
            for op in reversed(self.ops[e]):
                if not op.is_dma:
                    lasts.append(op)
                    break
        dmas = [op for op in self.dma_sem_last if op is not None]
        for e in ENGS:
            self.bar[e] = lasts + dmas

    def add(self, eng, fn, reads=(), writes=(), dma=False):
        op = Op()
        op.eng = eng
        op.fn = fn
        op.idx = len(self.ops[eng])
        op.waits = []
        op.signal = False
        op.is_dma = dma
        op.uid = self.uid
        self.uid += 1
        def _flat(bs):
            o = []
            for b in bs:
                if isinstance(b, (list, tuple)):
                    o.extend(b)
                else:
                    o.append(b)
            return o
        reads = _flat(reads)
        writes = _flat(writes)
        deps = []
        for b in reads:
            if b.writer is not None:
                deps.append(b.writer)
            if b.excl:
                for e2, r in b.readers.items():
                    if e2 != eng:
                        deps.append(r)
        for b in writes:
            if b.writer is not None:
                deps.append(b.writer)
            deps.extend(b.readers.values())
            deps.extend(b.dma_readers)
        if self.bar[eng]:
            deps.extend(self.bar[eng])
            self.bar[eng] = []
        clk = self.clock[eng]
        seen = set()
        for d in deps:
            if d.uid in seen:
                continue
            seen.add(d.uid)
            if d.is_dma:
                if d.uid not in self.known_dma[eng]:
                    op.waits.append(("dma", d.dsem, d.dtarget))
                    self.known_dma[eng].add(d.uid)
            else:
                if d.eng == eng and (eng == "pe" or not SAME_ENGINE_SYNC):
                    continue
                if clk[d.eng] < d.idx + 1:
                    op.waits.append(("eng", d))
                    d.signal = True
                    for f in ENGS:
                        if d.clock[f] > clk[f]:
                            clk[f] = d.clock[f]
                    if clk[d.eng] < d.idx + 1:
                        clk[d.eng] = d.idx + 1
        if dma and eng == "pool":
            pd = self.__dict__.setdefault("pool_dmas", [])
            if len(pd) >= 4:
                d = pd[-4]
                if d.uid not in self.known_dma[eng]:
                    op.waits.append(("dma", d.dsem, d.dtarget))
                    self.known_dma[eng].add(d.uid)
            pd.append(op)
        if dma:
            k = self.dma_rr
            self.dma_rr = (self.dma_rr + 1) % N_DMA_SEM
            prev = self.dma_sem_last[k]
            if prev is not None and prev.uid not in self.known_dma[eng]:
                op.waits.append(("dma", k, prev.dtarget))
                self.known_dma[eng].add(prev.uid)
            self.dma_sem_count[k] += 16
            op.dsem = k
            op.dtarget = self.dma_sem_count[k]
            self.dma_sem_last[k] = op
            self.n_dma += 1
        op.clock = dict(clk)
        if not SAME_ENGINE_SYNC or eng == "pe":
            pass
        self.ops[eng].append(op)
        for b in writes:
            b.writer = op
            b.readers = {}
            b.dma_readers = []
        for b in reads:
            if dma:
                b.dma_readers.append(op)
            else:
                b.readers[eng] = op
        return op

    def emit(self, nc, final_waits=True):
        for e in ENGS:
            c = 0
            for op in self.ops[e]:
                if op.signal:
                    c += 1
                op.sigcount = c
        from contextlib import ExitStack
        with ExitStack() as es:
            esem = {e: es.enter_context(nc.semaphore("s_" + e)) for e in ENGS}
            dsem = [es.enter_context(nc.semaphore("d%d" % i)) for i in range(N_DMA_SEM)]
            block = es.enter_context(nc.Block())
            last_dma = [op for op in self.dma_sem_last if op is not None]

            def run(e, h):
                for op in self.ops[e]:
                    for w in op.waits:
                        if w[0] == "dma":
                            h.wait_ge(dsem[w[1]], w[2])
                        else:
                            h.wait_ge(esem[w[1].eng], w[1].sigcount)
                    ins = op.fn(h)
                    if op.is_dma:
                        ins.then_inc(dsem[op.dsem], 16)
                    elif op.signal:
                        ins.then_inc(esem[e], 1)
                if e == "sp" and final_waits:
                    for k in range(N_DMA_SEM):
                        if self.dma_sem_count[k] > 0:
                            h.wait_ge(dsem[k], self.dma_sem_count[k])

            @block.tensor
            def _(h):
                run("pe", h)

            @block.scalar
            def _(h):
                run("act", h)

            @block.vector
            def _(h):
                run("dve", h)

            @block.gpsimd
            def _(h):
                run("pool", h)

            @block.sync
            def _(h):
                run("sp", h)

from contextlib import ExitStack
import ml_dtypes
from concourse.bass_utils import run_bass_kernel_spmd

F32 = mybir.dt.float32
BF16 = mybir.dt.bfloat16
AF = mybir.ActivationFunctionType
ALU = mybir.AluOpType

NCORE = 8
TOK = 2048
NPRE = 112
NM1 = 17
NM2 = 18
ALPHA = 2.0 ** 0.25
COL_Z, COL_X, COL_B, COL_C, COL_DT, COL_Q, COL_K, COL_V, COL_G = 0, 2048, 4096, 4608, 5120, 5152, 6176, 6304, 6432


class TT:
    def __init__(self, ap, name, excl=False):
        self.ap = ap
        self.b = Buf(name, excl)


class Arena:
    def __init__(self, t, n):
        self.t, self.n, self.off = t, n, 0

    def get(self, name, *fs):
        n = int(np.prod(fs))
        ap = self.t[:, self.off:self.off + n]
        self.off += n
        assert self.off <= self.n, (name, self.off, self.n)
        if len(fs) == 2:
            ap = ap.rearrange("p (a b) -> p a b", a=fs[0])
        elif len(fs) == 3:
            ap = ap.rearrange("p (a b c) -> p a b c", a=fs[0], b=fs[1])
        return TT(ap, name)


def bc(ap2, n):
    return ap2.unsqueeze(2).to_broadcast([ap2.shape[0], ap2.shape[1], n])


def build(stop=4, npre=NPRE, dbg=False):
    nc = bass.Bass("TRN2", target_bir_lowering=False)
    dt_in = lambda n, s: nc.dram_tensor(n, s, F32, kind="ExternalInput").ap()
    xpre = dt_in("xpre", [NPRE * 128, 1024])
    xmain = dt_in("xmain", [NM2 * 128, 1024])
    pflag = dt_in("pflag", [128, NPRE + NM1])
    w_in = dt_in("w_in", [1024, 8480])
    b_gate = dt_in("b_gate", [1, 2048])
    scw = dt_in("scw", [96, 128])
    scb = dt_in("scb", [24, 128])
    dtb = dt_in("dtb", [1, 32])
    alog = dt_in("alog", [1, 32])
    dsk = dt_in("dsk", [1, 32])
    normw = dt_in("normw", [1, 2048])
    sinks = dt_in("sinks", [1, 16])
    w_bs = dt_in("w_bs", [2048, 1024])
    w_ba = dt_in("w_ba", [1024, 1024])
    w_mix = dt_in("w_mix", [1024, 1024])
    ln1g = dt_in("ln1g", [1, 1024]); ln1b = dt_in("ln1b", [1, 1024])
    ln2g = dt_in("ln2g", [1, 1024]); ln2b = dt_in("ln2b", [1, 1024])
    w_up = dt_in("w_up", [1024, 5632])
    fcw = dt_in("fcw", [132, 128])
    fcb = dt_in("fcb", [44, 128])
    w_dn = dt_in("w_dn", [2816, 1024])
    cst = dt_in("cst", [128, 4 * 128])
    biasg = dt_in("biasg", [128, 2 * 16 * 128])
    maskg = dt_in("maskg", [128, 2 * 16 * 128])
    out = nc.dram_tensor("out", [TOK, 1024], F32, kind="ExternalOutput").ap()
    skind = "ExternalOutput" if dbg else "Internal"
    ynd = nc.dram_tensor("ynd", [NM1 * 128, 2048], BF16, kind=skind).ap()
    yad = nc.dram_tensor("yad", [NM1 * 128, 1024], BF16, kind=skind).ap()
    h1d = nc.dram_tensor("h1d", [NM1 * 128, 1024], F32, kind=skind).ap()

    P = Prog()
    es = ExitStack()
    NB, NF = 164 * 512, 40 * 256
    ABt = es.enter_context(nc.sbuf_tensor("AB", [128, NB], BF16))
    AFt = es.enter_context(nc.sbuf_tensor("AF", [128, NF], F32))
    pf = [TT(es.enter_context(nc.psum_tensor("pf%d" % i, [128, 512], F32))[:], "pf%d" % i, True) for i in range(6)]
    for t_ in pf:
        t_.b = [Buf(t_.b.name + "q%d" % q_, True) for q_ in range(4)]
    pb = [TT(es.enter_context(nc.psum_tensor("pb%d" % i, [128, 1024], BF16))[:], "pb%d" % i, True) for i in range(2)]
    AB = Arena(ABt, NB)
    AFa = Arena(AFt, NF)

    def bcast_row(src, n):
        return bass.AP(src.tensor, 0, [[0, 128], [1, n]])

    def dma(eng, o, i, reads, writes):
        P.add(eng, lambda e, o=o, i=i: e.dma_start(out=o, in_=i), reads=reads, writes=writes, dma=True)

    cf = AFa.get("cf", 4, 128)
    dma("sp", cf.ap, cst.rearrange("p (a b) -> p a b", a=4), [], [cf.b])
    identf, triU, Ustr, onesf = cf.ap[:, 0, :], cf.ap[:, 1, :], cf.ap[:, 2, :], cf.ap[:, 3, :]
    cb16 = AB.get("cb16", 4, 128)
    dma("pool", cb16.ap, cst.rearrange("p (a b) -> p a b", a=4), [], [cb16.b])
    identb, maskb = cb16.ap[:, 0, :], cb16.ap[:, 1, :]
    flg = AFa.get("flg", NPRE + NM1)
    dma("sp", flg.ap, pflag, [], [flg.b])
    smallp = AFa.get("smallp", 6, 32)
    dma("sp", smallp.ap[:, 0, :], bcast_row(dtb, 32), [], [smallp.b])
    dma("sp", smallp.ap[:, 1, :], bcast_row(alog, 32), [], [smallp.b])
    dma("sp", smallp.ap[:, 2, :], bcast_row(dsk, 32), [], [smallp.b])
    dma("sp", smallp.ap[:, 3, 0:16], bcast_row(sinks, 16), [], [smallp.b])
    P.add("act", lambda e: e.activation(out=smallp.ap[:, 1, :], in_=smallp.ap[:, 1, :], func=AF.Exp), reads=[smallp.b], writes=[smallp.b])
    P.add("dve", lambda e: e.tensor_scalar(out=smallp.ap[:, 1, :], in0=smallp.ap[:, 1, :], scalar1=-1.0, scalar2=None, op0=ALU.mult), reads=[smallp.b], writes=[smallp.b])
    P.add("act", lambda e: e.activation(out=smallp.ap[:, 3, 0:16], in_=smallp.ap[:, 3, 0:16], func=AF.Exp), reads=[smallp.b], writes=[smallp.b])
    dtb_bc, a_bc, D_bc, esink = smallp.ap[:, 0, :], smallp.ap[:, 1, :], smallp.ap[:, 2, :], smallp.ap[:, 3, 0:16]
    onecol = onesf[:, 0:1]
    rawh = AB.get("rawh", 24, 4)
    markB, markF = AB.off, AFa.off
    H = AFa.get("H", 2048)

    def load_w(dst, src, ncols, nk=8):
        nk = src.shape[0] // 128
        for c0 in range(0, ncols, 512):
            c1 = min(c0 + 512, ncols)
            for k in range(nk):
                dma("pool", dst.ap[:, k, c0:c1], src[k * 128:(k + 1) * 128, c0:c1], [], [dst.b])

    def load_xT(xrow_ap, xb, xT):
        dma("pool", xb.ap, xrow_ap, [], [xb.b])
        for k in range(8):
            P.add("pe", lambda e, k=k: e.transpose(pb[0].ap[:, k * 128:(k + 1) * 128], xb.ap[:, k * 128:(k + 1) * 128], identb), reads=[xb.b, cb16.b], writes=[pb[0].b])
        P.add("act", lambda e: e.copy(out=xT.ap.rearrange("p a b -> p (a b)"), in_=pb[0].ap), reads=[pb[0].b], writes=[xT.b])

    def chain(bank, out_ap, pairs, reads, wb=None):
        n = len(pairs)
        wb = [bank.b] if wb is None else wb
        for i, (l, r) in enumerate(pairs):
            P.add("pe", lambda e, l=l, r=r, i=i: e.matmul(out_ap, l, r, start=(i == 0), stop=(i == n - 1)), reads=reads, writes=wb)

    def per_channel(dst, src_dram, rows):
        tmp = AFa.get("pc_tmp", 128)
        dma("sp", tmp.ap[0:rows, :], src_dram, [], [tmp.b])
        P.add("pe", lambda e: e.transpose(pf[5].ap[:, 0:rows], tmp.ap[0:rows, :], identf[0:rows, 0:rows]), reads=[tmp.b, cf.b], writes=[pf[5].b])
        P.add("dve", lambda e: e.tensor_copy(out=dst, in_=pf[5].ap[:, 0:rows]), reads=[pf[5].b], writes=[])

    import os
    SKIP1 = int(os.environ.get('KSKIP1', '0'))
    Wx = AB.get("Wx", 8, 3072); load_w(Wx, w_in[:, COL_X:COL_X + 3072], 3072)
    Wdt = AB.get("Wdt", 8, 32); load_w(Wdt, w_in[:, COL_DT:COL_DT + 32], 32)
    cwt = AFa.get("cwt", 96); cbt = AFa.get("cbt", 24)
    per_channel(cwt.ap, scw, 96)
    per_channel(cbt.ap, scb, 24)
    cwt.b.writer = P.ops["dve"][-2]; cbt.b.writer = P.ops["dve"][-1]
    diag = AB.get("diag", 24, 4, 128)
    for j in range(24):
        for tp in range(4):
            P.add("dve", lambda e, j=j, tp=tp: e.tensor_scalar(out=diag.ap[:, j, tp, :], in0=identf, scalar1=cwt.ap[:, tp * 24 + j:tp * 24 + j + 1], scalar2=None, op0=ALU.mult), reads=[cf.b, cwt.b], writes=[diag.b])
    P.add("dve", lambda e: e.memset(H.ap, 0.0), writes=[H.b])
    mark1B, mark1F = AB.off, AFa.off
    xbA = [AB.get("xbA%d" % i, 1024) for i in range(2)]
    xT4 = [AB.get("xT4_%d" % i, 8, 512) for i in range(2)]
    raw4 = AB.get("raw4", 24, 516); xc4 = AB.get("xc4", 24, 512)
    rawb = [Buf("rawb%d" % j) for j in range(24)]; xcb = [Buf("xcb%d" % j) for j in range(24)]
    xtk = [AB.get("xtk%d" % i, 2560) for i in range(2)]
    xdtsA4 = [AB.get("xdtsA%d" % i, 512) for i in range(8)]
    P.add("pool", lambda e: e.memset(raw4.ap, 0.0), writes=rawb)

    def load_group(gi, buf):
        for q in range(4):
            c = gi * 4 + q
            xb_ = xbA[q % 2]
            dma("pool", xb_.ap, xpre[c * 128:(c + 1) * 128, :], [], [xb_.b])
            for k in range(8):
                P.add("pe", lambda e, k=k, xb_=xb_: e.transpose(pb[0].ap[:, k * 128:(k + 1) * 128], xb_.ap[:, k * 128:(k + 1) * 128], identb), reads=[xb_.b, cb16.b], writes=[pb[0].b])
            P.add("act", lambda e, q=q, buf=buf: e.copy(out=xT4[buf].ap[:, :, q * 128:(q + 1) * 128], in_=pb[0].ap.rearrange("p (a b) -> p a b", a=8)), reads=[pb[0].b], writes=[xT4[buf].b])

    def proj_in(gi, buf, last, j0, j1):
        ntile = 24 if last else 20
        for j in range(j0, min(j1, ntile)):
            bank = pf[j % 2]
            chain(bank, bank.ap, [(Wx.ap[:, k, j * 128:(j + 1) * 128], xT4[buf].ap[:, k, :]) for k in range(8)], [Wx.b, xT4[buf].b])
            P.add("act", lambda e, j=j, bank=bank: e.copy(out=raw4.ap[:, j, 3:515], in_=bank.ap), reads=[bank.b], writes=[rawb[j]])

    def proj_conv(gi, last):
        ntile = 24 if last else 20
        for j in range(ntile):
            bank = (pf[2], pf[3], pf[5], pf[4])[j % 4]
            chain(bank, bank.ap, [(diag.ap[:, j, tp, :], raw4.ap[:, j, tp:tp + 512]) for tp in range(4)], [diag.b, rawb[j]])
            P.add("act", lambda e, j=j, bank=bank: e.activation(out=xc4.ap[:, j, :], in_=bank.ap, func=AF.Silu, bias=cbt.ap[:, j:j + 1], scale=1.0), reads=[bank.b, cbt.b], writes=[xcb[j]])
        P.add("pool", lambda e: e.tensor_copy(out=raw4.ap[:, :, 0:3], in_=raw4.ap[:, :, 512:515]), reads=rawb, writes=rawb)

    smGs = [AFa.get("smG%d" % i, 8, 128) for i in range(2)]

    def group_chunks(gi, buf, hooks):
        c0 = gi * 4
        smG = smGs[gi % 2]
        S = lambda i: smG.ap[:, i, :]
        S3 = lambda i: smG.ap[:, i, :].rearrange("p (q h) -> p q h", q=4)
        b4 = lambda ap: ap.unsqueeze(1).to_broadcast([128, 4, 32])
        v3 = lambda ap: ap.rearrange("p (a b) -> p a b", a=8)
        for q in range(4):
            chain(pf[4], pf[4].ap[:, q * 32:(q + 1) * 32], [(xT4[buf].ap[:, k, q * 128:(q + 1) * 128], Wdt.ap[:, k, :]) for k in range(8)], [xT4[buf].b, Wdt.b])
        P.add("dve", lambda e: e.tensor_tensor(out=S3(0), in0=pf[4].ap[:, 0:128].rearrange("p (q h) -> p q h", q=4), in1=b4(dtb_bc), op=ALU.add), reads=[pf[4].b, smallp.b], writes=[smG.b])
        P.add("act", lambda e: e.activation(out=S(0), in_=S(0), func=AF.Exp), reads=[smG.b], writes=[smG.b])
        P.add("act", lambda e: e.activation(out=S(1), in_=S(0), func=AF.Ln, bias=onecol, scale=1.0), reads=[smG.b, cf.b], writes=[smG.b])
        P.add("dve", lambda e: e.tensor_tensor(out=S3(1), in0=S3(1), in1=bc(flg.ap[:, c0:c0 + 4], 32), op=ALU.mult), reads=[smG.b, flg.b], writes=[smG.b])
        P.add("dve", lambda e: e.tensor_tensor(out=S3(2), in0=S3(1), in1=b4(a_bc), op=ALU.mult), reads=[smG.b, smallp.b], writes=[smG.b])

        def tr(q):
            xt = xtk[q % 2]
            for bt in range(3):
                nt = 8 if bt < 2 else 4
                pbk = pb[bt % 2]
                for i in range(nt):
                    jj = bt * 8 + i
                    P.add("pe", lambda e, i=i, jj=jj, q=q, pbk=pbk: e.transpose(pbk.ap[:, i * 128:(i + 1) * 128], xc4.ap[:, jj, q * 128:(q + 1) * 128], identb), reads=[xcb[jj], cb16.b], writes=[pbk.b])
                if bt == 1:
                    P.add("act", lambda e, bt=bt, nt=nt, xt=xt, pbk=pbk: e.copy(out=xt.ap[:, bt * 1024:bt * 1024 + nt * 128], in_=pbk.ap[:, 0:nt * 128]), reads=[pbk.b], writes=[xt.b])
                else:
                    P.add("dve", lambda e, bt=bt, nt=nt, xt=xt, pbk=pbk: e.tensor_copy(out=xt.ap[:, bt * 1024:bt * 1024 + nt * 128], in_=pbk.ap[:, 0:nt * 128]), reads=[pbk.b], writes=[xt.b])

        SBK = [pf[2], pf[3], pf[5], pf[4]]

        def state(q):
            xt = xtk[q % 2]
            for g in range(4):
                xg = xt.ap[:, g * 512:(g + 1) * 512].rearrange("p (a b) -> p a b", a=8)
                o = q * 32 + 8 * g
                xd = xdtsA4[(q % 2) * 4 + g]
                P.add("dve", lambda e, xg=xg, o=o, xd=xd: e.tensor_tensor(out=v3(xd.ap), in0=xg, in1=bc(smG.ap[:, 6, o:o + 8], 64), op=ALU.mult), reads=[xt.b, smG.b], writes=[xd.b])
                P.add("pe", lambda e, g=g, xt=xt, xd=xd, q=q: e.matmul(SBK[g].ap, xt.ap[:, 2048 + g * 128:2048 + (g + 1) * 128], xd.ap, start=(q == 0), stop=(q == 3)), reads=[xt.b, xd.b], writes=[SBK[g].b])
            if q == 3:
                P.add("dve", lambda e: e.tensor_tensor(out=H.ap.rearrange("p (a b) -> p a b", a=32), in0=H.ap.rearrange("p (a b) -> p a b", a=32), in1=bc(smG.ap[:, 7, 0:32], 64), op=ALU.mult), reads=[H.b, smG.b], writes=[H.b])
                for g in range(4):
                    Hg = H.ap[:, g * 512:(g + 1) * 512]
                    P.add("dve", lambda e, Hg=Hg, g=g: e.tensor_tensor(out=Hg, in0=Hg, in1=SBK[g].ap, op=ALU.add), reads=[H.b, SBK[g].b], writes=[H.b])

        tr(0); tr(1)
        hooks[0]()
        P.add("pe", lambda e: e.matmul(pf[4].ap[:, 128:256], triU, S(2), start=True, stop=True), reads=[cf.b, smG.b], writes=[pf[4].b])
        P.add("pe", lambda e: e.matmul(pf[4].ap[:, 256:384], onesf, S(2), start=True, stop=True), reads=[cf.b, smG.b], writes=[pf[4].b])
        P.add("dve", lambda e: e.tensor_copy(out=smG.ap[:, 3:5, :], in_=pf[4].ap[:, 128:384].rearrange("p (a b) -> p a b", a=2)), reads=[pf[4].b], writes=[smG.b])
        P.add("dve", lambda e: e.memset(S3(5)[:, 3, :], 0.0), writes=[smG.b])
        P.add("dve", lambda e: e.tensor_copy(out=S3(5)[:, 2, :], in_=S3(4)[:, 3, :]), reads=[smG.b], writes=[smG.b])
        P.add("dve", lambda e: e.tensor_tensor(out=S3(5)[:, 1, :], in0=S3(5)[:, 2, :], in1=S3(4)[:, 2, :], op=ALU.add), reads=[smG.b], writes=[smG.b])
        P.add("dve", lambda e: e.tensor_tensor(out=S3(5)[:, 0, :], in0=S3(5)[:, 1, :], in1=S3(4)[:, 1, :], op=ALU.add), reads=[smG.b], writes=[smG.b])
        P.add("dve", lambda e: e.tensor_tensor(out=S3(7)[:, 0, :], in0=S3(5)[:, 0, :], in1=S3(4)[:, 0, :], op=ALU.add), reads=[smG.b], writes=[smG.b])
        P.add("dve", lambda e: e.tensor_tensor(out=S(6), in0=S(4), in1=S(3), op=ALU.subtract), reads=[smG.b], writes=[smG.b])
        P.add("dve", lambda e: e.tensor_tensor(out=S(6), in0=S(6), in1=S(5), op=ALU.add), reads=[smG.b], writes=[smG.b])
        P.add("act", lambda e: e.activation(out=S(6), in_=S(6), func=AF.Exp), reads=[smG.b], writes=[smG.b])
        P.add("act", lambda e: e.activation(out=S3(7)[:, 0, :], in_=S3(7)[:, 0, :], func=AF.Exp), reads=[smG.b], writes=[smG.b])
        P.add("dve", lambda e: e.tensor_tensor(out=S(6), in0=S(6), in1=S(1), op=ALU.mult), reads=[smG.b], writes=[smG.b])
        state(0); state(1)
        hooks[1]()
        tr(2); tr(3)
        hooks[2]()
        state(2); state(3)
        hooks[3]()

    NG = NPRE // 4
    g0 = NG - (npre + 3) // 4
    if not SKIP1 and g0 < NG:
        load_group(g0, g0 % 2)
        proj_in(g0, g0 % 2, g0 == NG - 1, 0, 24)
        for gi in range(g0, NG):
            proj_conv(gi, gi == NG - 1)
            if gi + 1 < NG:
                load_group(gi + 1, (gi + 1) % 2)
                hk = [lambda a=a, gi=gi: proj_in(gi + 1, (gi + 1) % 2, gi + 1 == NG - 1, a, a + 6) for a in (0, 6, 12, 18)]
            else:
                hk = [lambda: None] * 4
            group_chunks(gi, gi % 2, hk)
        P.add("pool", lambda e: e.tensor_copy(out=rawh.ap[:, :, 0:3], in_=raw4.ap[:, :, 0:3]), reads=rawb, writes=[rawh.b])
    else:
        P.add("pool", lambda e: e.memset(rawh.ap, 0.0), writes=[rawh.b])

    P.barrier()
    AB.off, AFa.off = mark1B, mark1F
    Wz = AB.get("Wz", 8, 2048); load_w(Wz, w_in[:, COL_Z:COL_Z + 2048], 2048)
    nwb = AFa.get("nwb", 2048)
    dma("sp", nwb.ap, bcast_row(normw, 2048), [], [nwb.b])
    xbs = [AB.get("xb_%d" % i, 1024) for i in range(2)]; xTs = [AB.get("xT_%d" % i, 8, 128) for i in range(2)]
    raws = [AB.get("raw_%d" % i, 24, 132) for i in range(2)]; xcs = [AB.get("xc_%d" % i, 24, 128) for i in range(2)]
    xtok = AB.get("xtok", 2560)
    xdt = AB.get("xdt", 512); xdts = AB.get("xdts", 512)
    Hb = AB.get("Hb", 2048); cbm = AB.get("cbm", 4, 128)
    LT = AB.get("LT", 4, 128); MT = AB.get("MT", 8, 128); yn = AB.get("yn", 2048)
    szb = AB.get("szb", 4, 512)
    adtU = [AFa.get("adtU%d" % i, 128) for i in range(8)]
    sm = AFa.get("sm", 12, 32)
    yacc = AFa.get("yacc", 512); ytmp = AFa.get("ytmp", 512); sz = AFa.get("sz", 512)
    ssq = AFa.get("ssq", 4)
    P.add("pool", lambda e: e.memset(raws[0].ap, 0.0), writes=[raws[0].b])
    P.add("pool", lambda e: e.memset(raws[1].ap, 0.0), writes=[raws[1].b])
    P.add("pool", lambda e: e.tensor_copy(out=raws[0].ap[:, :, 0:3], in_=rawh.ap[:, :, 0:3]), reads=[rawh.b], writes=[raws[0].b])

    CUT = int(os.environ.get('KCUT', '99')); NCH1 = int(os.environ.get('KNCH', str(NM1)))
    def front_pieces(ci, xrow, xb, xT, raw, xc, raw_next):
        def inproj(g):
            bank = pf[g % 2]
            for jj in range(4):
                j = 4 * g + jj
                chain(bank, bank.ap[:, jj * 128:(jj + 1) * 128], [(Wx.ap[:, k, j * 128:(j + 1) * 128], xT.ap[:, k, :]) for k in range(8)], [Wx.b, xT.b])
            P.add("act", lambda e, g=g, bank=bank: e.copy(out=raw.ap[:, 4 * g:4 * g + 4, 3:131], in_=bank.ap.rearrange("p (a b) -> p a b", a=4)), reads=[bank.b], writes=[raw.b])

        def conv(g):
            for jj in range(4):
                j = 4 * g + jj
                bank = pf[2 + j % 2]
                chain(bank, bank.ap[:, 0:128], [(diag.ap[:, j, tp, :], raw.ap[:, j, tp:tp + 128]) for tp in range(4)], [diag.b, raw.b])
                P.add("act", lambda e, j=j, bank=bank: e.activation(out=xc.ap[:, j, :], in_=bank.ap[:, 0:128], func=AF.Silu, bias=cbt.ap[:, j:j + 1], scale=1.0), reads=[bank.b, cbt.b], writes=[xc.b])

        def p0():
            load_xT(xrow, xb, xT); inproj(0); inproj(1)

        def p1():
            inproj(2); inproj(3)

        def p2():
            inproj(4); inproj(5)
            P.add("pool", lambda e: e.tensor_copy(out=raw_next.ap[:, :, 0:3], in_=raw.ap[:, :, 128:131]), reads=[raw.b], writes=[raw_next.b])

        def p3():
            conv(0); conv(1); conv(2); conv(3); conv(4); conv(5)
        return [p0, p1, p2, p3]

    def ssd_back(fcol, main, ci, xT, xc, hooks):
        for bt in range(3):
            nt = 8 if bt < 2 else 4
            for i in range(nt):
                j = bt * 8 + i
                P.add("pe", lambda e, i=i, j=j: e.transpose(pb[1].ap[:, i * 128:(i + 1) * 128], xc.ap[:, j, :], identb), reads=[xc.b, cb16.b], writes=[pb[1].b])
            P.add("dve", lambda e, bt=bt, nt=nt: e.tensor_copy(out=xtok.ap[:, bt * 1024:bt * 1024 + nt * 128], in_=pb[1].ap[:, 0:nt * 128]), reads=[pb[1].b], writes=[xtok.b])
        chain(pf[4], pf[4].ap[:, 0:32], [(xT.ap[:, k, :], Wdt.ap[:, k, :]) for k in range(8)], [xT.b, Wdt.b])
        S = lambda i: sm.ap[:, i, :]
        P.add("dve", lambda e: e.tensor_tensor(out=S(0), in0=pf[4].ap[:, 0:32], in1=dtb_bc, op=ALU.add), reads=[pf[4].b, smallp.b], writes=[sm.b])
        P.add("act", lambda e: e.activation(out=S(0), in_=S(0), func=AF.Exp), reads=[sm.b], writes=[sm.b])
        P.add("act", lambda e: e.activation(out=S(1), in_=S(0), func=AF.Ln, bias=onecol, scale=1.0), reads=[sm.b, cf.b], writes=[sm.b])
        P.add("dve", lambda e: e.tensor_scalar(out=S(1), in0=S(1), scalar1=fcol, scalar2=None, op0=ALU.mult), reads=[sm.b, flg.b], writes=[sm.b])
        P.add("dve", lambda e: e.tensor_tensor(out=S(2), in0=S(1), in1=a_bc, op=ALU.mult), reads=[sm.b, smallp.b], writes=[sm.b])
        P.add("pe", lambda e: e.matmul(pf[4].ap[:, 32:64], triU, S(2), start=True, stop=True), reads=[cf.b, sm.b], writes=[pf[4].b])
        P.add("pe", lambda e: e.matmul(pf[4].ap[:, 64:96], onesf, S(2), start=True, stop=True), reads=[cf.b, sm.b], writes=[pf[4].b])
        P.add("dve", lambda e: e.tensor_copy(out=sm.ap[:, 3:5, :], in_=pf[4].ap[:, 32:96].rearrange("p (a b) -> p a b", a=2)), reads=[pf[4].b], writes=[sm.b])
        if main:
            for g in range(4):
                P.add("pe", lambda e, g=g: e.matmul(pf[5].ap[:, g * 128:(g + 1) * 128], xc.ap[:, 16 + g, :], xc.ap[:, 20 + g, :], start=True, stop=True), reads=[xc.b], writes=[pf[5].b])
            P.add("dve", lambda e: e.tensor_tensor(out=cbm.ap, in0=pf[5].ap.rearrange("p (a b) -> p a b", a=4), in1=maskb.unsqueeze(1).to_broadcast([128, 4, 128]), op=ALU.mult), reads=[pf[5].b, cb16.b], writes=[cbm.b])
            P.add("pool", lambda e: e.tensor_copy(out=Hb.ap, in_=H.ap), reads=[H.b], writes=[Hb.b])
            P.add("act", lambda e: e.activation(out=S(5), in_=S(3), func=AF.Exp), reads=[sm.b], writes=[sm.b])
            for g in range(4):
                zb = pf[5 - g % 2]
                chain(zb, zb.ap, [(xT.ap[:, k, :], Wz.ap[:, k, g * 512:(g + 1) * 512]) for k in range(8)], [xT.b, Wz.b])
                P.add("act", lambda e, g=g, zb=zb: e.activation(out=szb.ap[:, g, :], in_=zb.ap, func=AF.Silu), reads=[zb.b], writes=[szb.b])
            for g in range(4):
                xg = xtok.ap[:, g * 512:(g + 1) * 512].rearrange("p (a b) -> p a b", a=8)
                P.add("dve", lambda e, g=g, xg=xg: e.tensor_tensor(out=xdt.ap.rearrange("p (a b) -> p a b", a=8), in0=xg, in1=bc(sm.ap[:, 1, 8 * g:8 * g + 8], 64), op=ALU.mult), reads=[xtok.b, sm.b], writes=[xdt.b])
                for hh in range(2):
                    bank = pf[hh]
                    for h4 in range(4):
                        h = 8 * g + 4 * hh + h4
                        au = adtU[h % 8]
                        P.add("dve", lambda e, h=h, au=au: e.tensor_scalar(out=au.ap, in0=Ustr, scalar1=sm.ap[:, 2, h:h + 1], scalar2=None, op0=ALU.mult), reads=[cf.b, sm.b], writes=[au.b])
                        P.add("pe", lambda e, h4=h4, au=au, bank=bank: e.matmul(bank.ap[:, h4 * 128:(h4 + 1) * 128], au.ap, triU, start=True, stop=True), reads=[au.b, cf.b], writes=[bank.b])
                    P.add("act", lambda e, bank=bank: e.activation(out=LT.ap.rearrange("p a b -> p (a b)"), in_=bank.ap, func=AF.Exp), reads=[bank.b], writes=[LT.b])
                    P.add("dve", lambda e, g=g, hh=hh: e.tensor_tensor(out=MT.ap[:, 4 * hh:4 * hh + 4, :], in0=LT.ap, in1=cbm.ap[:, g, :].unsqueeze(1).to_broadcast([128, 4, 128]), op=ALU.mult), reads=[LT.b, cbm.b], writes=[MT.b])
                for h8 in range(8):
                    P.add("pe", lambda e, h8=h8: e.matmul(pf[2].ap[:, h8 * 64:(h8 + 1) * 64], MT.ap[:, h8, :], xdt.ap[:, h8 * 64:(h8 + 1) * 64], start=True, stop=True), reads=[MT.b, xdt.b], writes=[pf[2].b])
                P.add("pe", lambda e, g=g: e.matmul(pf[3].ap, xc.ap[:, 20 + g, :], Hb.ap[:, g * 512:(g + 1) * 512], start=True, stop=True), reads=[xc.b, Hb.b], writes=[pf[3].b])
                v3 = lambda ap: ap.rearrange("p (a b) -> p a b", a=8)
                P.add("dve", lambda e, g=g: e.tensor_tensor(out=v3(yacc.ap), in0=v3(pf[3].ap), in1=bc(sm.ap[:, 5, 8 * g:8 * g + 8], 64), op=ALU.mult), reads=[pf[3].b, sm.b], writes=[yacc.b])
                P.add("dve", lambda e: e.tensor_tensor(out=yacc.ap, in0=yacc.ap, in1=pf[2].ap, op=ALU.add), reads=[pf[2].b, yacc.b], writes=[yacc.b])
                P.add("dve", lambda e, g=g, xg=xg: e.tensor_tensor(out=v3(ytmp.ap), in0=xg, in1=bc(D_bc[:, 8 * g:8 * g + 8], 64), op=ALU.mult), reads=[xtok.b, smallp.b], writes=[ytmp.b])
                P.add("dve", lambda e: e.tensor_tensor(out=yacc.ap, in0=yacc.ap, in1=ytmp.ap, op=ALU.add), reads=[ytmp.b, yacc.b], writes=[yacc.b])
                P.add("dve", lambda e, g=g: e.tensor_tensor(out=yacc.ap, in0=yacc.ap, in1=szb.ap[:, g, :], op=ALU.mult), reads=[szb.b, yacc.b], writes=[yacc.b])
                P.add("dve", lambda e: e.tensor_tensor(out=ytmp.ap, in0=yacc.ap, in1=yacc.ap, op=ALU.mult), reads=[yacc.b], writes=[ytmp.b])
                P.add("dve", lambda e, g=g: e.reduce_sum(out=ssq.ap[:, g:g + 1], in_=ytmp.ap, axis=mybir.AxisListType.X), reads=[ytmp.b], writes=[ssq.b])
                P.add("dve", lambda e, g=g: e.tensor_scalar(out=ssq.ap[:, g:g + 1], in0=ssq.ap[:, g:g + 1], scalar1=1.0 / 512, scalar2=1e-5, op0=ALU.mult, op1=ALU.add), reads=[ssq.b], writes=[ssq.b])
                P.add("act", lambda e, g=g: e.activation(out=ssq.ap[:, g:g + 1], in_=ssq.ap[:, g:g + 1], func=AF.Ln), reads=[ssq.b], writes=[ssq.b])
                P.add("act", lambda e, g=g: e.activation(out=ssq.ap[:, g:g + 1], in_=ssq.ap[:, g:g + 1], func=AF.Exp, scale=-0.5), reads=[ssq.b], writes=[ssq.b])
                P.add("dve", lambda e, g=g: e.scalar_tensor_tensor(out=yn.ap[:, g * 512:(g + 1) * 512], in0=yacc.ap, scalar=ssq.ap[:, g:g + 1], in1=nwb.ap[:, g * 512:(g + 1) * 512], op0=ALU.mult, op1=ALU.mult), reads=[yacc.b, ssq.b, nwb.b], writes=[yn.b])
                hooks[g]()
            dma("sp", ynd[ci * 128:(ci + 1) * 128, :], yn.ap, [yn.b], [])
        P.add("dve", lambda e: e.tensor_tensor(out=S(6), in0=S(4), in1=S(3), op=ALU.subtract), reads=[sm.b], writes=[sm.b])
        P.add("act", lambda e: e.activation(out=S(6), in_=S(6), func=AF.Exp), reads=[sm.b], writes=[sm.b])
        P.add("act", lambda e: e.activation(out=S(7), in_=S(4), func=AF.Exp), reads=[sm.b], writes=[sm.b])
        P.add("dve", lambda e: e.tensor_tensor(out=S(6), in0=S(6), in1=S(1), op=ALU.mult), reads=[sm.b], writes=[sm.b])
        for g in range(4):
            xg = xtok.ap[:, g * 512:(g + 1) * 512].rearrange("p (a b) -> p a b", a=8)
            v3 = lambda ap: ap.rearrange("p (a b) -> p a b", a=8)
            P.add("dve", lambda e, g=g, xg=xg: e.tensor_tensor(out=v3(xdts.ap), in0=xg, in1=bc(sm.ap[:, 6, 8 * g:8 * g + 8], 64), op=ALU.mult), reads=[xtok.b, sm.b], writes=[xdts.b])
            bank = pf[2 + g % 2]
            P.add("pe", lambda e, g=g, bank=bank: e.matmul(bank.ap, xtok.ap[:, 2048 + g * 128:2048 + (g + 1) * 128], xdts.ap, start=True, stop=True), reads=[xtok.b, xdts.b], writes=[bank.b])
            Hg = H.ap[:, g * 512:(g + 1) * 512]
            P.add("dve", lambda e, g=g, Hg=Hg: e.tensor_tensor(out=v3(Hg), in0=v3(Hg), in1=bc(sm.ap[:, 7, 8 * g:8 * g + 8], 64), op=ALU.mult), reads=[H.b, sm.b], writes=[H.b])
            P.add("dve", lambda e, Hg=Hg, bank=bank: e.tensor_tensor(out=Hg, in0=Hg, in1=bank.ap, op=ALU.add), reads=[H.b, bank.b], writes=[H.b])

    def fp(ci):
        b = ci % 2
        return front_pieces(ci, xmain[(ci + 1) * 128:(ci + 2) * 128, :], xbs[b], xTs[b], raws[b], xcs[b], raws[1 - b])
    nch = 0 if SKIP1 else NCH1
    if nch:
        for p_ in fp(0):
            p_()
    for ci in range(nch):
        nxt = fp(ci + 1) if ci + 1 < nch else [lambda: None] * 4
        ssd_back(flg.ap[:, NPRE + ci:NPRE + ci + 1], True, ci, xTs[ci % 2], xcs[ci % 2], nxt)

    if stop < 2:
        P.emit(nc); es.close(); return nc
    P.barrier()
    AB.off, AFa.off = markB, markF
    Wq = AB.get("Wq", 8, 1024); load_w(Wq, w_in[:, COL_Q:COL_Q + 1024], 1024)
    Wk2 = AB.get("Wk2", 8, 128); load_w(Wk2, w_in[:, COL_K:COL_K + 128], 128)
    Wv = AB.get("Wv", 8, 128); load_w(Wv, w_in[:, COL_V:COL_V + 128], 128)
    EB = AB.get("EB", 2, 16, 128)
    AB3 = Arena(ABt, NB); AB3.off = NB - 49152
    Wg = AB3.get("Wg", 8, 2048); Wbs = AB3.get("Wbs", 16, 1024); Wba = AB3.get("Wba", 8, 1024); Wmx = AB3.get("Wmx", 8, 1024)
    w3_list = []
    for dst_, src_, nc_ in ((Wg, w_in[:, COL_G:COL_G + 2048], 2048), (Wbs, w_bs, 1024), (Wba, w_ba, 1024), (Wmx, w_mix, 1024)):
        for c0 in range(0, nc_, 512):
            for k in range(src_.shape[0] // 128):
                w3_list.append((dst_, dst_.ap[:, k, c0:c0 + 512], src_[k * 128:(k + 1) * 128, c0:c0 + 512]))
    w3_pos = [0]

    def w3_issue(n):
        for _ in range(n):
            if w3_pos[0] < len(w3_list):
                d_, o_, i_ = w3_list[w3_pos[0]]
                dma("pool", o_, i_, [], [d_.b])
                w3_pos[0] += 1
    ebf = AFa.get("ebf", 2048); mkf = AFa.get("mkf", 2048)
    for kt in range(2):
        dma("sp", ebf.ap, biasg[:, kt * 2048:(kt + 1) * 2048], [], [ebf.b])
        dma("sp", mkf.ap, maskg[:, kt * 2048:(kt + 1) * 2048], [], [mkf.b])
        P.add("act", lambda e: e.activation(out=ebf.ap, in_=ebf.ap, func=AF.Exp), reads=[ebf.b], writes=[ebf.b])
        P.add("dve", lambda e, kt=kt: e.tensor_tensor(out=EB.ap[:, kt, :, :].rearrange("p a b -> p (a b)"), in0=ebf.ap, in1=mkf.ap, op=ALU.mult), reads=[ebf.b, mkf.b], writes=[EB.b])
    xb2s = [AB.get("xb2_%d" % i, 1024) for i in range(2)]; xT = AB.get("xT2", 8, 128)
    kT = [AB.get("kT%d" % i, 2, 128) for i in range(2)]
    vx = [AB.get("vx%d" % i, 2, 65) for i in range(2)]
    for i in range(2):
        P.add("pool", lambda e, i=i: e.memset(vx[i].ap, 1.0), writes=[vx[i].b])
    qT = AB.get("qT", 16, 128); et = AB.get("et", 4, 128)
    PT = [AB.get("PT%d" % i, 4, 128) for i in range(2)]
    ya = AB.get("ya", 1024)
    den = AFa.get("den", 4)
    flag0 = flg.ap[:, NPRE:NPRE + 1]
    CUT2 = int(os.environ.get('KCUT2', '99')); NCH2 = int(os.environ.get('KNCH2', str(NM2)))
    for ci in range(NCH2):
        sl = ci % 2
        load_xT(xmain[ci * 128:(ci + 1) * 128, :], xb2s[ci % 2], xT)
        w3_issue(6)
        for kv in range(2):
            chain(pf[0], pf[0].ap[0:64, kv * 128:(kv + 1) * 128], [(Wk2.ap[:, k, kv * 64:(kv + 1) * 64], xT.ap[:, k, :]) for k in range(8)], [Wk2.b, xT.b])
        P.add("act", lambda e, sl=sl: e.copy(out=kT[sl].ap[0:64, :, :], in_=pf[0].ap[0:64, 0:256].rearrange("p (a b) -> p a b", a=2)), reads=[pf[0].b], writes=[kT[sl].b])
        chain(pf[1], pf[1].ap[:, 0:128], [(xT.ap[:, k, :], Wv.ap[:, k, :]) for k in range(8)], [xT.b, Wv.b])
        P.add("dve", lambda e, sl=sl: e.tensor_copy(out=vx[sl].ap[:, :, 0:64], in_=pf[1].ap[:, 0:128].rearrange("p (a b) -> p a b", a=2)), reads=[pf[1].b], writes=[vx[sl].b])
        if ci == 0 or CUT2 <= 1:
            continue
        for q4 in range(4):
            bank = pf[2 + q4 % 2]
            for tt in range(4):
                j = q4 * 4 + tt
                chain(bank, bank.ap[0:64, tt * 128:(tt + 1) * 128], [(Wq.ap[:, k, j * 64:(j + 1) * 64], xT.ap[:, k, :]) for k in range(8)], [Wq.b, xT.b])
            P.add("act", lambda e, q4=q4, bank=bank: e.copy(out=qT.ap[0:64, q4 * 4:q4 * 4 + 4, :], in_=bank.ap[0:64, :].rearrange("p (a b) -> p a b", a=4)), reads=[bank.b], writes=[qT.b])
        for kvh in range(2):
            if CUT2 <= 2: break
            for hb in range(2):
                j0 = kvh * 8 + hb * 4
                for kt in range(2):
                    slk = (ci + 1 + kt) % 2
                    bank = pf[4 + kt]
                    for i in range(4):
                        j = j0 + i
                        base = (j % 2) * 64 * int(os.environ.get("KB64", "1"))
                        P.add("pe", lambda e, i=i, j=j, base=base, slk=slk, bank=bank, kvh=kvh: e.matmul(bank.ap[:, i * 128:(i + 1) * 128], kT[slk].ap[0:64, kvh, :], qT.ap[0:64, j, :], start=True, stop=True), reads=[kT[slk].b, qT.b], writes=[bank.b])
                    P.add("act", lambda e, bank=bank: e.activation(out=et.ap.rearrange("p a b -> p (a b)"), in_=bank.ap, func=AF.Exp, scale=0.125), reads=[bank.b], writes=[et.b])
                    if ci == 2 and kt == 0:
                        P.add("dve", lambda e, kt=kt, j0=j0: e.scalar_tensor_tensor(out=PT[kt].ap, in0=et.ap, scalar=flag0, in1=EB.ap[:, kt, j0:j0 + 4, :], op0=ALU.mult, op1=ALU.mult), reads=[et.b, EB.b, flg.b], writes=[PT[kt].b])
                    else:
                        P.add("dve", lambda e, kt=kt, j0=j0: e.tensor_tensor(out=PT[kt].ap, in0=et.ap, in1=EB.ap[:, kt, j0:j0 + 4, :], op=ALU.mult), reads=[et.b, EB.b], writes=[PT[kt].b])
                if CUT2 <= 3: continue
                bank = pf[hb]
                for i in range(4):
                    for kt in range(2):
                        slk = (ci + 1 + kt) % 2
                        P.add("pe", lambda e, i=i, kt=kt, slk=slk, bank=bank, kvh=kvh: e.matmul(bank.ap[:, i * 65:(i + 1) * 65], PT[kt].ap[:, i, :], vx[slk].ap[:, kvh, :], start=(kt == 0), stop=(kt == 1)), reads=[PT[kt].b, vx[slk].b], writes=[bank.b])
                if CUT2 <= 4: continue
                pv = bank.ap[:, 0:260].rearrange("p (a b) -> p a b", a=4)
                P.add("dve", lambda e, pv=pv, j0=j0: e.tensor_tensor(out=den.ap, in0=pv[:, :, 64], in1=esink[:, j0:j0 + 4], op=ALU.add), reads=[bank.b, smallp.b], writes=[den.b])
                P.add("dve", lambda e: e.reciprocal(out=den.ap, in_=den.ap), reads=[den.b], writes=[den.b])
                P.add("dve", lambda e, pv=pv, j0=j0: e.tensor_tensor(out=ya.ap[:, j0 * 64:(j0 + 4) * 64].rearrange("p (a b) -> p a b", a=4), in0=pv[:, :, 0:64], in1=bc(den.ap, 64), op=ALU.mult), reads=[bank.b, den.b], writes=[ya.b])
        dma("sp", yad[(ci - 1) * 128:ci * 128, :], ya.ap, [ya.b], [])

    if stop < 3:
        P.emit(nc); es.close(); return nc
    w3_issue(len(w3_list))
    P.barrier()
    AB.off, AFa.off = markB, markF
    bgb = AFa.get("bgb", 2048); dma("sp", bgb.ap, bcast_row(b_gate, 2048), [], [bgb.b])
    lng = AFa.get("lng", 2, 1024)
    dma("sp", lng.ap[:, 0, :], bcast_row(ln1g, 1024), [], [lng.b]); dma("sp", lng.ap[:, 1, :], bcast_row(ln1b, 1024), [], [lng.b])
    xb = AB.get("xb3", 1024); xT = AB.get("xT3", 8, 128)
    ynb = AB.get("ynb", 2048); yab = AB.get("yab", 1024)
    ynT = AB.get("ynT", 16, 128); yaT = AB.get("yaT", 8, 128)
    mg = AB.get("mg", 1024); mT = AB.get("mT", 8, 128)
    xf = AFa.get("xf", 1024); gt = AB.get("gt", 2048); m1 = AFa.get("m1", 512); r = AFa.get("r", 1024)
    st = AFa.get("st", 2, 6); mv = AFa.get("mv", 2)

    def transp(src, dst, ntl):
        for bt in range(ntl // 8):
            for i in range(8):
                j = bt * 8 + i
                P.add("pe", lambda e, i=i, j=j: e.transpose(pb[1].ap[:, i * 128:(i + 1) * 128], src.ap[:, j * 128:(j + 1) * 128], identb), reads=[src.b, cb16.b], writes=[pb[1].b])
            P.add("act", lambda e, bt=bt: e.copy(out=dst.ap[:, bt * 8:bt * 8 + 8, :].rearrange("p a b -> p (a b)"), in_=pb[1].ap), reads=[pb[1].b], writes=[dst.b])

    def layer_norm(r, g_ap, b_ap, gb, st, mv):
        for i in range(2):
            P.add("dve", lambda e, i=i: e.bn_stats(out=st.ap[:, i, :], in_=r.ap[:, i * 512:(i + 1) * 512]), reads=[r.b], writes=[st.b])
        P.add("dve", lambda e: e.bn_aggr(out=mv.ap, in_=st.ap.rearrange("p a b -> p (a b)")), reads=[st.b], writes=[mv.b])
        P.add("dve", lambda e: e.tensor_scalar(out=mv.ap[:, 1:2], in0=mv.ap[:, 1:2], scalar1=1e-5, scalar2=None, op0=ALU.add), reads=[mv.b], writes=[mv.b])
        P.add("act", lambda e: e.activation(out=mv.ap[:, 1:2], in_=mv.ap[:, 1:2], func=AF.Sqrt), reads=[mv.b], writes=[mv.b])
        P.add("dve", lambda e: e.reciprocal(out=mv.ap[:, 1:2], in_=mv.ap[:, 1:2]), reads=[mv.b], writes=[mv.b])
        P.add("dve", lambda e: e.tensor_scalar(out=r.ap, in0=r.ap, scalar1=mv.ap[:, 0:1], scalar2=mv.ap[:, 1:2], op0=ALU.subtract, op1=ALU.mult), reads=[r.b, mv.b], writes=[r.b])
        P.add("dve", lambda e: e.tensor_tensor(out=r.ap, in0=r.ap, in1=g_ap, op=ALU.mult), reads=[r.b, gb], writes=[r.b])
        P.add("dve", lambda e: e.tensor_tensor(out=r.ap, in0=r.ap, in1=b_ap, op=ALU.add), reads=[r.b, gb], writes=[r.b])

    def p3_loads(ci, xb, xf, ynb, yab):
        xrow = xmain[(ci + 1) * 128:(ci + 2) * 128, :]
        dma("pool", xb.ap, xrow, [], [xb.b])
        dma("sp", xf.ap, xrow, [], [xf.b])
        dma("sp", ynb.ap, ynd[ci * 128:(ci + 1) * 128, :], [], [ynb.b])
        dma("sp", yab.ap, yad[ci * 128:(ci + 1) * 128, :], [], [yab.b])

    def p3_chunk(ci, xb, xf, ynb, yab, r):
        for k in range(8):
            P.add("pe", lambda e, k=k: e.transpose(pb[0].ap[:, k * 128:(k + 1) * 128], xb.ap[:, k * 128:(k + 1) * 128], identb), reads=[xb.b, cb16.b], writes=[pb[0].b])
        P.add("act", lambda e: e.copy(out=xT.ap.rearrange("p a b -> p (a b)"), in_=pb[0].ap), reads=[pb[0].b], writes=[xT.b])
        transp(ynb, ynT, 16)
        transp(yab, yaT, 8)
        for s4 in range(4):
            bank = pf[s4 % 2]
            chain(bank, bank.ap, [(xT.ap[:, k, :], Wg.ap[:, k, s4 * 512:(s4 + 1) * 512]) for k in range(8)], [xT.b, Wg.b])
            P.add("dve", lambda e, s4=s4, bank=bank: e.tensor_tensor(out=gt.ap[:, s4 * 512:(s4 + 1) * 512], in0=bank.ap, in1=bgb.ap[:, s4 * 512:(s4 + 1) * 512], op=ALU.add), reads=[bank.b, bgb.b], writes=[gt.b])
        P.add("act", lambda e: e.activation(out=gt.ap, in_=gt.ap, func=AF.Sigmoid), reads=[gt.b], writes=[gt.b])
        for hf in range(2):
            chain(pf[2], pf[2].ap, [(ynT.ap[:, i, :], Wbs.ap[:, i, hf * 512:(hf + 1) * 512]) for i in range(16)], [ynT.b, Wbs.b])
            chain(pf[3], pf[3].ap, [(yaT.ap[:, i, :], Wba.ap[:, i, hf * 512:(hf + 1) * 512]) for i in range(8)], [yaT.b, Wba.b])
            P.add("dve", lambda e, hf=hf: e.tensor_tensor(out=m1.ap, in0=pf[2].ap, in1=gt.ap[:, hf * 512:(hf + 1) * 512], op=ALU.mult), reads=[pf[2].b, gt.b], writes=[m1.b])
            P.add("dve", lambda e, hf=hf: e.tensor_tensor(out=r.ap[:, hf * 512:(hf + 1) * 512], in0=pf[3].ap, in1=gt.ap[:, 1024 + hf * 512:1024 + (hf + 1) * 512], op=ALU.mult), reads=[pf[3].b, gt.b], writes=[r.b])
            P.add("dve", lambda e, hf=hf: e.tensor_tensor(out=mg.ap[:, hf * 512:(hf + 1) * 512], in0=m1.ap, in1=r.ap[:, hf * 512:(hf + 1) * 512], op=ALU.add), reads=[m1.b, r.b], writes=[mg.b])
        transp(mg, mT, 8)
        for hf in range(2):
            bank = pf[4 + hf]
            chain(bank, bank.ap, [(mT.ap[:, i, :], Wmx.ap[:, i, hf * 512:(hf + 1) * 512]) for i in range(8)], [mT.b, Wmx.b])
            P.add("dve", lambda e, hf=hf, bank=bank: e.scalar_tensor_tensor(out=r.ap[:, hf * 512:(hf + 1) * 512], in0=xf.ap[:, hf * 512:(hf + 1) * 512], scalar=ALPHA, in1=bank.ap, op0=ALU.mult, op1=ALU.add), reads=[xf.b, bank.b], writes=[r.b])
        layer_norm(r, lng.ap[:, 0, :], lng.ap[:, 1, :], lng.b, st, mv)
        if ci == 0:
            P.add("dve", lambda e: e.tensor_scalar(out=r.ap, in0=r.ap, scalar1=flag0, scalar2=None, op0=ALU.mult), reads=[r.b, flg.b], writes=[r.b])
        dma("sp", h1d[ci * 128:(ci + 1) * 128, :], r.ap, [r.b], [])


    xb3 = [xb, AB.get("xb3b", 1024)]; xf3 = [xf, AFa.get("xf3b", 1024)]
    ABp = Arena(ABt, NB); ABp.off = markB
    Wup_pre = ABp.get("Wup_pre", 8, 5632)
    wp_list = [(k, c0) for k in (3, 4, 5) for c0 in range(0, 5632, 512)]
    wp_pos = [0]

    def wp_issue(n):
        for _ in range(n):
            if wp_pos[0] < len(wp_list):
                k, c0 = wp_list[wp_pos[0]]
                dma("pool", Wup_pre.ap[:, k, c0:c0 + 512], w_up[k * 128:(k + 1) * 128, c0:c0 + 512], [], [Wup_pre.b])
                wp_pos[0] += 1
    ynb3 = [ynb, AB.get("ynb3b", 2048)]; yab3 = [yab, AB.get("yab3b", 1024)]; r3 = [r, AFa.get("r3b", 1024)]
    assert AB.off <= markB + 3 * 5632, AB.off
    p3_loads(0, xb3[0], xf3[0], ynb3[0], yab3[0])
    for ci in range(NM1):
        if ci + 1 < NM1:
            b_ = (ci + 1) % 2
            p3_loads(ci + 1, xb3[b_], xf3[b_], ynb3[b_], yab3[b_])
        b_ = ci % 2
        wp_issue(2)
        p3_chunk(ci, xb3[b_], xf3[b_], ynb3[b_], yab3[b_], r3[b_])

    if stop < 4:
        P.emit(nc); es.close(); return nc
    wp_issue(len(wp_list))
    P.barrier()
    AB.off, AFa.off = markB, markF
    Wup = AB.get("Wup", 8, 5632)
    for c0 in range(0, 5632, 512):
        for k in (0, 1, 2, 6, 7):
            dma("pool", Wup.ap[:, k, c0:c0 + 512], w_up[k * 128:(k + 1) * 128, c0:c0 + 512], [], [Wup.b])
    Wdn = AB.get("Wdn", 22, 1024); load_w(Wdn, w_dn, 1024)
    fw = AFa.get("fw", 132); fb = AFa.get("fb", 44)
    per_channel(fw.ap[:, 0:88], fcw[0:88, :], 88); fw.b.writer = P.ops["dve"][-1]
    per_channel(fw.ap[:, 88:132], fcw[88:132, :], 44); fw.b.writer = P.ops["dve"][-1]
    per_channel(fb.ap, fcb, 44); fb.b.writer = P.ops["dve"][-1]
    lng2 = AFa.get("lng2", 2, 1024)
    dma("sp", lng2.ap[:, 0, :], bcast_row(ln2g, 1024), [], [lng2.b]); dma("sp", lng2.ap[:, 1, :], bcast_row(ln2b, 1024), [], [lng2.b])
    SC = 256
    hAs = [AB.get("hA%d" % i, 1024) for i in range(2)]; hBs = [AB.get("hB%d" % i, 1024) for i in range(2)]; h1T = AB.get("h1T", 8, SC + 2)
    aT = AB.get("aT", 22, SC)
    aTb = [Buf("aTb%d" % i) for i in range(22)]
    cvs = [AFa.get("cvs%d" % i, SC) for i in range(4)]
    t0s = [AFa.get("t0s%d" % i, SC) for i in range(2)]
    sgs = [AFa.get("sg%d" % i, SC) for i in range(2)]
    hrs = [AFa.get("hr%d" % i, 1024) for i in range(2)]; r4s = [AFa.get("r4_%d" % i, 1024) for i in range(2)]
    st4 = AFa.get("st4", 2, 6); mv4 = AFa.get("mv4", 2)
    NT = SC // 128
    tcount = 0
    for sc in range(TOK // SC):
        r0 = 128 + sc * SC
        for i in range(NT):
            hA = hAs[i % 2]
            dma("pool", hA.ap, h1d[r0 - 2 + i * 128:r0 + 126 + i * 128, :], [], [hA.b])
            for k in range(8):
                P.add("pe", lambda e, k=k, hA=hA: e.transpose(pb[0].ap[:, k * 128:(k + 1) * 128], hA.ap[:, k * 128:(k + 1) * 128], identb), reads=[hA.b, cb16.b], writes=[pb[0].b])
            P.add("act", lambda e, i=i: e.copy(out=h1T.ap[:, :, i * 128:(i + 1) * 128], in_=pb[0].ap.rearrange("p (a b) -> p a b", a=8)), reads=[pb[0].b], writes=[h1T.b])
        hB = hBs[sc % 2]
        dma("pool", hB.ap[0:2, :], h1d[r0 + SC - 2:r0 + SC, :], [], [hB.b])
        for k in range(8):
            P.add("pe", lambda e, k=k, hB=hB: e.transpose(pb[1].ap[:, k * 2:(k + 1) * 2], hB.ap[0:2, k * 128:(k + 1) * 128], identb[0:2, 0:2]), reads=[hB.b, cb16.b], writes=[pb[1].b])
        P.add("act", lambda e: e.copy(out=h1T.ap[:, :, SC:SC + 2], in_=pb[1].ap[:, 0:16].rearrange("p (a b) -> p a b", a=8)), reads=[pb[1].b], writes=[h1T.b])
        def down(jp):
            for tt in range(NT):
                for hf in range(2):
                    bank = pf[2 + tt * 2 + hf]
                    P.add("pe", lambda e, jp=jp, tt=tt, hf=hf, bank=bank: e.matmul(bank.ap, aT.ap[:, jp, tt * 128:(tt + 1) * 128], Wdn.ap[:, jp, hf * 512:(hf + 1) * 512], start=(jp == 0), stop=(jp == 21)), reads=[aTb[jp], Wdn.b], writes=[bank.b])

        for jp in range(22):
            for gv in range(2):
                j = jp + 22 * gv
                X = pf[tcount % 2]; t0 = t0s[tcount % 2]
                cv = cvs[(jp % 2) * 2 + gv]
                tcount += 1
                chain(X, X.ap[:, 0:SC + 2], [(Wup.ap[:, k, j * 128:(j + 1) * 128], h1T.ap[:, k, 0:SC + 2]) for k in range(8)], [Wup.b, h1T.b])
                w0, w1, w2, bb = fw.ap[:, j:j + 1], fw.ap[:, 44 + j:45 + j], fw.ap[:, 88 + j:89 + j], fb.ap[:, j:j + 1]
                P.add("act", lambda e, X=X, t0=t0, w2=w2, bb=bb: e.activation(out=t0.ap, in_=X.ap[:, 2:SC + 2], func=AF.Identity, bias=bb, scale=w2), reads=[X.b, fw.b, fb.b], writes=[t0.b])
                P.add("dve", lambda e, X=X, t0=t0, w1=w1: e.scalar_tensor_tensor(out=t0.ap, in0=X.ap[:, 1:SC + 1], scalar=w1, in1=t0.ap, op0=ALU.mult, op1=ALU.add), reads=[X.b, fw.b, t0.b], writes=[t0.b])
                P.add("dve", lambda e, X=X, t0=t0, w0=w0, cv=cv: e.scalar_tensor_tensor(out=cv.ap, in0=X.ap[:, 0:SC], scalar=w0, in1=t0.ap, op0=ALU.mult, op1=ALU.add), reads=[X.b, fw.b, t0.b], writes=[cv.b])
            cg, cvv = cvs[(jp % 2) * 2], cvs[(jp % 2) * 2 + 1]
            sgp = sgs[jp % 2]
            P.add("act", lambda e, cg=cg, sgp=sgp: e.activation(out=sgp.ap, in_=cg.ap, func=AF.Silu), reads=[cg.b], writes=[sgp.b])
            P.add("dve", lambda e, jp=jp, cvv=cvv, sgp=sgp: e.tensor_tensor(out=aT.ap[:, jp, :], in0=sgp.ap, in1=cvv.ap, op=ALU.mult), reads=[sgp.b, cvv.b], writes=[aTb[jp]])
            if jp >= 1:
                down(jp - 1)
        down(21)
        for tt in range(NT):
            rr = r0 + tt * 128
            hr = hrs[tt % 2]; r4 = r4s[tt % 2]
            dma("sp", hr.ap, h1d[rr:rr + 128, :], [], [hr.b])
            for hf in range(2):
                bank = pf[2 + tt * 2 + hf]
                P.add("dve", lambda e, hf=hf, bank=bank, hr=hr, r4=r4: e.scalar_tensor_tensor(out=r4.ap[:, hf * 512:(hf + 1) * 512], in0=hr.ap[:, hf * 512:(hf + 1) * 512], scalar=ALPHA, in1=bank.ap, op0=ALU.mult, op1=ALU.add), reads=[hr.b, bank.b], writes=[r4.b])
            layer_norm(r4, lng2.ap[:, 0, :], lng2.ap[:, 1, :], lng2.b, st4, mv4)
            dma("sp", out[rr - 128:rr, :], r4.ap, [r4.b], [])

    P.emit(nc)
    es.close()
    return nc


def rel_bucket_np(rel):
    n = np.maximum(rel, 0)
    nf = np.maximum(n, 1).astype(np.float32)
    large = 16 + (np.log(nf / np.float32(16)) / np.float32(np.log(128 / 16)) * np.float32(16)).astype(np.int32)
    large = np.minimum(large, 31)
    return np.where(n < 16, n, large)


_NC = None


def kernel(_dbg=None, **inp):
    global _NC
    x = np.asarray(inp["x"], np.float32)[0]
    f = lambda k: np.ascontiguousarray(np.asarray(inp[k], np.float32)[0])
    common = {
        "w_in": f("w_in"), "b_gate": f("b_gate")[None], "dtb": f("ssm_dt_bias")[None], "alog": f("ssm_a_log")[None],
        "dsk": f("ssm_d")[None], "normw": f("ssm_norm_w")[None], "sinks": f("attn_sinks")[None],
        "w_bs": f("w_branch_ssm"), "w_ba": f("w_branch_attn"), "w_mix": f("w_mix_out"),
        "ln1g": f("ln1_g")[None], "ln1b": f("ln1_b")[None], "ln2g": f("ln2_g")[None], "ln2b": f("ln2_b")[None],
        "w_up": f("w_up"), "w_dn": f("w_down"),
    }
    scw = f("ssm_conv_w")
    common["scw"] = np.ascontiguousarray(scw.reshape(4 * 24, 128))
    common["scb"] = np.ascontiguousarray(f("ssm_conv_b").reshape(24, 128))
    common["fcw"] = np.ascontiguousarray(f("ffn_conv_w").reshape(3 * 44, 128))
    common["fcb"] = np.ascontiguousarray(f("ffn_conv_b").reshape(44, 128))
    s = np.arange(128)
    ident = np.eye(128, dtype=np.float32)
    triU = (s[:, None] <= s[None, :]).astype(np.float32)
    ustr = (s[:, None] > s[None, :]).astype(np.float32)
    common["cst"] = np.ascontiguousarray(np.concatenate([ident, triU, ustr, np.ones((128, 128), np.float32)], axis=1))
    rb = np.asarray(inp["rel_bias"], np.float32)
    bg = np.zeros((128, 2, 16, 128), np.float32); mk = np.zeros((128, 2, 16, 128), np.float32)
    for kt in range(2):
        rel = (s[None, :] + 128) - (s[:, None] + 128 * kt)
        valid = (rel >= 0) & (rel < 128)
        bidx = rel_bucket_np(rel)
        g = rb[bidx]
        bg[:, kt] = np.transpose(g, (0, 2, 1))
        mk[:, kt] = np.broadcast_to(valid[:, None, :], (128, 16, 128))
    common["biasg"] = np.ascontiguousarray(bg.reshape(128, -1)); common["maskg"] = np.ascontiguousarray(mk.reshape(128, -1))
    in_maps = []
    for c in range(NCORE):
        S = c * TOK
        lo = S - 128 - NPRE * 128
        xp = np.zeros((NPRE * 128, 1024), np.float32)
        if S - 128 > 0:
            src_lo = max(lo, 0)
            xp[src_lo - lo:] = x[src_lo:S - 128]
        xm = np.zeros((NM2 * 128, 1024), np.float32)
        lo2 = S - 256
        src_lo = max(lo2, 0)
        xm[src_lo - lo2:] = x[src_lo:S + TOK]
        fl = np.zeros((128, NPRE + NM1), np.float32)
        for i in range(NPRE):
            fl[:, i] = 1.0 if lo + i * 128 >= 0 else 0.0
        fl[:, NPRE] = 1.0 if c > 0 else 0.0
        fl[:, NPRE + 1:] = 1.0
        m = dict(common); m["xpre"] = xp; m["xmain"] = xm; m["pflag"] = fl
        in_maps.append(m)
    if _dbg is not None:
        return in_maps
    if _NC is None:
        _NC = build()
    res = run_bass_kernel_spmd(_NC, in_maps, core_ids=list(range(NCORE)))
    o = np.concatenate([res.results[c]["out"] for c in range(NCORE)], axis=0)
    return o[None].astype(np.float32)
```

```python
import numpy as np
import concourse.bass as bass
import concourse.mybir as mybir

ENGS = ["pe", "act", "dve", "pool", "sp"]
N_DMA_SEM = 16
import os as _os
SAME_ENGINE_SYNC = _os.environ.get("KSES", "1") == "1"


class Buf:
    __slots__ = ("name", "writer", "readers", "dma_readers", "excl")

    def __init__(self, name, excl=False):
        self.name = name
        self.excl = excl
        self.writer = None
        self.readers = {}
        self.dma_readers = []


class Op:
    __slots__ = ("eng", "fn", "idx", "waits", "signal", "is_dma", "dsem", "dtarget", "clock", "sigcount", "uid")


class Prog:
    def __init__(self):
        self.ops = {e: [] for e in ENGS}
        self.clock = {e: {f: 0 for f in ENGS} for e in ENGS}
        self.known_dma = {e: set() for e in ENGS}
        self.dma_sem_count = [0] * N_DMA_SEM
        self.dma_sem_last = [None] * N_DMA_SEM
        self.dma_rr = 0
        self.n_dma = 0
        self.uid = 0
        self.bar = {e: [] for e in ENGS}

    def barrier(self):
        lasts = [self.ops[e][-1] for e in ENGS if self.ops[e] and not self.ops[e][-1].is_dma]
        for e in ENGS:
            pass
        lasts = []
        for e in ENGS:
            for op in reversed(self.ops[e]):
                if not op.is_dma:
                    lasts.append(op)
                    break
        dmas = [op for op in self.dma_sem_last if op is not None]
        for e in ENGS:
            self.bar[e] = lasts + dmas

    def add(self, eng, fn, reads=(), writes=(), dma=False):
        op = Op()
        op.eng = eng
        op.fn = fn
        op.idx = len(self.ops[eng])
        op.waits = []
        op.signal = False
        op.is_dma = dma
        op.uid = self.uid
        self.uid += 1
        def _flat(bs):
            o = []
            for b in bs:
                if isinstance(b, (list, tuple)):
                    o.extend(b)
                else:
                    o.append(b)
            return o
        reads = _flat(reads)
        writes = _flat(writes)
        deps = []
        for b in reads:
            if b.writer is not None:
                deps.append(b.writer)
            if b.excl:
                for e2, r in b.readers.items():
                    if e2 != eng:
                        deps.append(r)
        for b in writes:
            if b.writer is not None:
                deps.append(b.writer)
            deps.extend(b.readers.values())
            deps.extend(b.dma_readers)
        if self.bar[eng]:
            deps.extend(self.bar[eng])
            self.bar[eng] = []
        clk = self.clock[eng]
        seen = set()
        for d in deps:
            if d.uid in seen:
                continue
            seen.add(d.uid)
            if d.is_dma:
                if d.uid not in self.known_dma[eng]:
                    op.waits.append(("dma", d.dsem, d.dtarget))
                    self.known_dma[eng].add(d.uid)
            else:
                if d.eng == eng and (eng == "pe" or not SAME_ENGINE_SYNC):
                    continue
                if clk[d.eng] < d.idx + 1:
                    op.waits.append(("eng", d))
                    d.signal = True
                    for f in ENGS:
                        if d.clock[f] > clk[f]:
                            clk[f] = d.clock[f]
                    if clk[d.eng] < d.idx + 1:
                        clk[d.eng] = d.idx + 1
        if dma and eng == "pool":
            pd = self.__dict__.setdefault("pool_dmas", [])
            if len(pd) >= 4:
                d = pd[-4]
                if d.uid not in self.known_dma[eng]:
                    op.waits.append(("dma", d.dsem, d.dtarget))
                    self.known_dma[eng].add(d.uid)
            pd.append(op)
        if dma:
            k = self.dma_rr
            self.dma_rr = (self.dma_rr + 1) % N_DMA_SEM
            prev = self.dma_sem_last[k]
            if prev is not None and prev.uid not in self.known_dma[eng]:
                op.waits.append(("dma", k, prev.dtarget))
                self.known_dma[eng].add(prev.uid)
            self.dma_sem_count[k] += 16
            op.dsem = k
            op.dtarget = self.dma_sem_count[k]
            self.dma_sem_last[k] = op
            self.n_dma += 1
        op.clock = dict(clk)
        if not SAME_ENGINE_SYNC or eng == "pe":
            pass
        self.ops[eng].append(op)
        for b in writes:
            b.writer = op
            b.readers = {}
            b.dma_readers = []
        for b in reads:
            if dma:
                b.dma_readers.append(op)
            else:
                b.readers[eng] = op
        return op

    def emit(self, nc, final_waits=True):
        for e in ENGS:
            c = 0
            for op in self.ops[e]:
                if op.signal:
                    c += 1
                op.sigcount = c
        from contextlib import ExitStack
        with ExitStack() as es:
            esem = {e: es.enter_context(nc.semaphore("s_" + e)) for e in ENGS}
            dsem = [es.enter_context(nc.semaphore("d%d" % i)) for i in range(N_DMA_SEM)]
            block = es.enter_context(nc.Block())
            last_dma = [op for op in self.dma_sem_last if op is not None]

            def run(e, h):
                for op in self.ops[e]:
                    for w in op.waits:
                        if w[0] == "dma":
                            h.wait_ge(dsem[w[1]], w[2])
                        else:
                            h.wait_ge(esem[w[1].eng], w[1].sigcount)
                    ins = op.fn(h)
                    if op.is_dma:
                        ins.then_inc(dsem[op.dsem], 16)
                    elif op.signal:
                        ins.then_inc(esem[e], 1)
                if e == "sp" and final_waits:
                    for k in range(N_DMA_SEM):
                        if self.dma_sem_count[k] > 0:
                            h.wait_ge(dsem[k], self.dma_sem_count[k])

            @block.tensor
            def _(h):
                run("pe", h)

            @block.scalar
            def _(h):
                run("act", h)

            @block.vector
            def _(h):
                run("dve", h)

            @block.gpsimd
            def _(h):
                run("pool", h)

            @block.sync
            def _(h):
                run("sp", h)

from contextlib import ExitStack
import ml_dtypes
from concourse.bass_utils import run_bass_kernel_spmd

F32 = mybir.dt.float32
BF16 = mybir.dt.bfloat16
AF = mybir.ActivationFunctionType
ALU = mybir.AluOpType

NCORE = 8
TOK = 2048
NPRE = 112
NM1 = 17
NM2 = 18
ALPHA = 2.0 ** 0.25
COL_Z, COL_X, COL_B, COL_C, COL_DT, COL_Q, COL_K, COL_V, COL_G = 0, 2048, 4096, 4608, 5120, 5152, 6176, 6304, 6432


class TT:
    def __init__(self, ap, name, excl=False):
        self.ap = ap
        self.b = Buf(name, excl)


class Arena:
    def __init__(self, t, n):
        self.t, self.n, self.off = t, n, 0

    def get(self, name, *fs):
        n = int(np.prod(fs))
        ap = self.t[:, self.off:self.off + n]
        self.off += n
        assert self.off <= self.n, (name, self.off, self.n)
        if len(fs) == 2:
            ap = ap.rearrange("p (a b) -> p a b", a=fs[0])
        elif len(fs) == 3:
            ap = ap.rearrange("p (a b c) -> p a b c", a=fs[0], b=fs[1])
        return TT(ap, name)


def bc(ap2, n):
    return ap2.unsqueeze(2).to_broadcast([ap2.shape[0], ap2.shape[1], n])


def build(stop=4, npre=NPRE, dbg=False):
    nc = bass.Bass("TRN2", target_bir_lowering=False)
    dt_in = lambda n, s: nc.dram_tensor(n, s, F32, kind="ExternalInput").ap()
    xpre = dt_in("xpre", [NPRE * 128, 1024])
    xmain = dt_in("xmain", [NM2 * 128, 1024])
    pflag = dt_in("pflag", [128, NPRE + NM1])
    w_in = dt_in("w_in", [1024, 8480])
    b_gate = dt_in("b_gate", [1, 2048])
    scw = dt_in("scw", [96, 128])
    scb = dt_in("scb", [24, 128])
    dtb = dt_in("dtb", [1, 32])
    alog = dt_in("alog", [1, 32])
    dsk = dt_in("dsk", [1, 32])
    normw = dt_in("normw", [1, 2048])
    sinks = dt_in("sinks", [1, 16])
    w_bs = dt_in("w_bs", [2048, 1024])
    w_ba = dt_in("w_ba", [1024, 1024])
    w_mix = dt_in("w_mix", [1024, 1024])
    ln1g = dt_in("ln1g", [1, 1024]); ln1b = dt_in("ln1b", [1, 1024])
    ln2g = dt_in("ln2g", [1, 1024]); ln2b = dt_in("ln2b", [1, 1024])
    w_up = dt_in("w_up", [1024, 5632])
    fcw = dt_in("fcw", [132, 128])
    fcb = dt_in("fcb", [44, 128])
    w_dn = dt_in("w_dn", [2816, 1024])
    cst = dt_in("cst", [128, 4 * 128])
    biasg = dt_in("biasg", [128, 2 * 16 * 128])
    maskg = dt_in("maskg", [128, 2 * 16 * 128])
    out = nc.dram_tensor("out", [TOK, 1024], F32, kind="ExternalOutput").ap()
    skind = "ExternalOutput" if dbg else "Internal"
    ynd = nc.dram_tensor("ynd", [NM1 * 128, 2048], BF16, kind=skind).ap()
    yad = nc.dram_tensor("yad", [NM1 * 128, 1024], BF16, kind=skind).ap()
    h1d = nc.dram_tensor("h1d", [NM1 * 128, 1024], F32, kind=skind).ap()

    P = Prog()
    es = ExitStack()
    NB, NF = 164 * 512, 40 * 256
    ABt = es.enter_context(nc.sbuf_tensor("AB", [128, NB], BF16))
    AFt = es.enter_context(nc.sbuf_tensor("AF", [128, NF], F32))
    pf = [TT(es.enter_context(nc.psum_tensor("pf%d" % i, [128, 512], F32))[:], "pf%d" % i, True) for i in range(6)]
    for t_ in pf:
        t_.b = [Buf(t_.b.name + "q%d" % q_, True) for q_ in range(4)]
    pb = [TT(es.enter_context(nc.psum_tensor("pb%d" % i, [128, 1024], BF16))[:], "pb%d" % i, True) for i in range(2)]
    AB = Arena(ABt, NB)
    AFa = Arena(AFt, NF)

    def bcast_row(src, n):
        return bass.AP(src.tensor, 0, [[0, 128], [1, n]])

    def dma(eng, o, i, reads, writes):
        P.add(eng, lambda e, o=o, i=i: e.dma_start(out=o, in_=i), reads=reads, writes=writes, dma=True)

    cf = AFa.get("cf", 4, 128)
    dma("sp", cf.ap, cst.rearrange("p (a b) -> p a b", a=4), [], [cf.b])
    identf, triU, Ustr, onesf = cf.ap[:, 0, :], cf.ap[:, 1, :], cf.ap[:, 2, :], cf.ap[:, 3, :]
    cb16 = AB.get("cb16", 4, 128)
    dma("pool", cb16.ap, cst.rearrange("p (a b) -> p a b", a=4), [], [cb16.b])
    identb, maskb = cb16.ap[:, 0, :], cb16.ap[:, 1, :]
    flg = AFa.get("flg", NPRE + NM1)
    dma("sp", flg.ap, pflag, [], [flg.b])
    smallp = AFa.get("smallp", 6, 32)
    dma("sp", smallp.ap[:, 0, :], bcast_row(dtb, 32), [], [smallp.b])
    dma("sp", smallp.ap[:, 1, :], bcast_row(alog, 32), [], [smallp.b])
    dma("sp", smallp.ap[:, 2, :], bcast_row(dsk, 32), [], [smallp.b])
    dma("sp", smallp.ap[:, 3, 0:16], bcast_row(sinks, 16), [], [smallp.b])
    P.add("act", lambda e: e.activation(out=smallp.ap[:, 1, :], in_=smallp.ap[:, 1, :], func=AF.Exp), reads=[smallp.b], writes=[smallp.b])
    P.add("dve", lambda e: e.tensor_scalar(out=smallp.ap[:, 1, :], in0=smallp.ap[:, 1, :], scalar1=-1.0, scalar2=None, op0=ALU.mult), reads=[smallp.b], writes=[smallp.b])
    P.add("act", lambda e: e.activation(out=smallp.ap[:, 3, 0:16], in_=smallp.ap[:, 3, 0:16], func=AF.Exp), reads=[smallp.b], writes=[smallp.b])
    dtb_bc, a_bc, D_bc, esink = smallp.ap[:, 0, :], smallp.ap[:, 1, :], smallp.ap[:, 2, :], smallp.ap[:, 3, 0:16]
    onecol = onesf[:, 0:1]
    rawh = AB.get("rawh", 24, 4)
    markB, markF = AB.off, AFa.off
    H = AFa.get("H", 2048)

    def load_w(dst, src, ncols, nk=8):
        nk = src.shape[0] // 128
        for c0 in range(0, ncols, 512):
            c1 = min(c0 + 512, ncols)
            for k in range(nk):
                dma("pool", dst.ap[:, k, c0:c1], src[k * 128:(k + 1) * 128, c0:c1], [], [dst.b])

    def load_xT(xrow_ap, xb, xT):
        dma("pool", xb.ap, xrow_ap, [], [xb.b])
        for k in range(8):
            P.add("pe", lambda e, k=k: e.transpose(pb[0].ap[:, k * 128:(k + 1) * 128], xb.ap[:, k * 128:(k + 1) * 128], identb), reads=[xb.b, cb16.b], writes=[pb[0].b])
        P.add("act", lambda e: e.copy(out=xT.ap.rearrange("p a b -> p (a b)"), in_=pb[0].ap), reads=[pb[0].b], writes=[xT.b])

    def chain(bank, out_ap, pairs, reads, wb=None):
        n = len(pairs)
        wb = [bank.b] if wb is None else wb
        for i, (l, r) in enumerate(pairs):
            P.add("pe", lambda e, l=l, r=r, i=i: e.matmul(out_ap, l, r, start=(i == 0), stop=(i == n - 1)), reads=reads, writes=wb)

    def per_channel(dst, src_dram, rows):
        tmp = AFa.get("pc_tmp", 128)
        dma("sp", tmp.ap[0:rows, :], src_dram, [], [tmp.b])
        P.add("pe", lambda e: e.transpose(pf[5].ap[:, 0:rows], tmp.ap[0:rows, :], identf[0:rows, 0:rows]), reads=[tmp.b, cf.b], writes=[pf[5].b])
        P.add("dve", lambda e: e.tensor_copy(out=dst, in_=pf[5].ap[:, 0:rows]), reads=[pf[5].b], writes=[])

    import os
    SKIP1 = int(os.environ.get('KSKIP1', '0'))
    Wx = AB.get("Wx", 8, 3072)
    wxb = [Buf("wxb%d" % i) for i in range(6)]
    for c0 in range(0, 3072, 512):
        for k in range(8):
            dma("pool", Wx.ap[:, k, c0:c0 + 512], w_in[k * 128:(k + 1) * 128, COL_X + c0:COL_X + c0 + 512], [], [wxb[c0 // 512]])
    Wx.b = wxb
    Wdt = AB.get("Wdt", 8, 32); load_w(Wdt, w_in[:, COL_DT:COL_DT + 32], 32)
    cwt = AFa.get("cwt", 96); cbt = AFa.get("cbt", 24)
    per_channel(cwt.ap, scw, 96)
    per_channel(cbt.ap, scb, 24)
    cwt.b.writer = P.ops["dve"][-2]; cbt.b.writer = P.ops["dve"][-1]
    diag = AB.get("diag", 24, 4, 128)
    for j in range(24):
        for tp in range(4):
            P.add("dve", lambda e, j=j, tp=tp: e.tensor_scalar(out=diag.ap[:, j, tp, :], in0=identf, scalar1=cwt.ap[:, tp * 24 + j:tp * 24 + j + 1], scalar2=None, op0=ALU.mult), reads=[cf.b, cwt.b], writes=[diag.b])
    P.add("dve", lambda e: e.memset(H.ap, 0.0), writes=[H.b])
    mark1B, mark1F = AB.off, AFa.off
    xbA = [AB.get("xbA%d" % i, 1024) for i in range(2)]
    xT4 = [AB.get("xT4_%d" % i, 8, 512) for i in range(2)]
    raw4 = AB.get("raw4", 24, 516); xc4 = AB.get("xc4", 24, 512)
    rawb = [Buf("rawb%d" % j) for j in range(24)]; xcb = [Buf("xcb%d" % j) for j in range(24)]
    xtk = [AB.get("xtk%d" % i, 2560) for i in range(2)]
    xdtsA4 = [AB.get("xdtsA%d" % i, 512) for i in range(8)]
    P.add("pool", lambda e: e.memset(raw4.ap, 0.0), writes=rawb)

    def load_group(gi, buf):
        for q in range(4):
            c = gi * 4 + q
            xb_ = xbA[q % 2]
            dma("pool", xb_.ap, xpre[c * 128:(c + 1) * 128, :], [], [xb_.b])
            for k in range(8):
                P.add("pe", lambda e, k=k, xb_=xb_: e.transpose(pb[0].ap[:, k * 128:(k + 1) * 128], xb_.ap[:, k * 128:(k + 1) * 128], identb), reads=[xb_.b, cb16.b], writes=[pb[0].b])
            P.add("act", lambda e, q=q, buf=buf: e.copy(out=xT4[buf].ap[:, :, q * 128:(q + 1) * 128], in_=pb[0].ap.rearrange("p (a b) -> p a b", a=8)), reads=[pb[0].b], writes=[xT4[buf].b])

    def proj_in(gi, buf, last, j0, j1):
        ntile = 24 if last else 20
        for j in range(j0, min(j1, ntile)):
            bank = pf[j % 2]
            chain(bank, bank.ap, [(Wx.ap[:, k, j * 128:(j + 1) * 128], xT4[buf].ap[:, k, :]) for k in range(8)], [wxb[j // 4], xT4[buf].b])
            P.add("act", lambda e, j=j, bank=bank: e.copy(out=raw4.ap[:, j, 3:515], in_=bank.ap), reads=[bank.b], writes=[rawb[j]])

    def proj_conv(gi, last):
        ntile = 24 if last else 20
        for j in range(ntile):
            bank = (pf[2], pf[3], pf[5], pf[4])[j % 4]
            chain(bank, bank.ap, [(diag.ap[:, j, tp, :], raw4.ap[:, j, tp:tp + 512]) for tp in range(4)], [diag.b, rawb[j]])
            P.add("act", lambda e, j=j, bank=bank: e.activation(out=xc4.ap[:, j, :], in_=bank.ap, func=AF.Silu, bias=cbt.ap[:, j:j + 1], scale=1.0), reads=[bank.b, cbt.b], writes=[xcb[j]])
        P.add("pool", lambda e: e.tensor_copy(out=raw4.ap[:, :, 0:3], in_=raw4.ap[:, :, 512:515]), reads=rawb, writes=rawb)

    smGs = [AFa.get("smG%d" % i, 8, 128) for i in range(2)]

    def group_chunks(gi, buf, hooks):
        c0 = gi * 4
        smG = smGs[gi % 2]
        S = lambda i: smG.ap[:, i, :]
        S3 = lambda i: smG.ap[:, i, :].rearrange("p (q h) -> p q h", q=4)
        b4 = lambda ap: ap.unsqueeze(1).to_broadcast([128, 4, 32])
        v3 = lambda ap: ap.rearrange("p (a b) -> p a b", a=8)
        for q in range(4):
            chain(pf[4], pf[4].ap[:, q * 32:(q + 1) * 32], [(xT4[buf].ap[:, k, q * 128:(q + 1) * 128], Wdt.ap[:, k, :]) for k in range(8)], [xT4[buf].b, Wdt.b])
        P.add("dve", lambda e: e.tensor_tensor(out=S3(0), in0=pf[4].ap[:, 0:128].rearrange("p (q h) -> p q h", q=4), in1=b4(dtb_bc), op=ALU.add), reads=[pf[4].b, smallp.b], writes=[smG.b])
        P.add("act", lambda e: e.activation(out=S(0), in_=S(0), func=AF.Exp), reads=[smG.b], writes=[smG.b])
        P.add("act", lambda e: e.activation(out=S(1), in_=S(0), func=AF.Ln, bias=onecol, scale=1.0), reads=[smG.b, cf.b], writes=[smG.b])
        P.add("dve", lambda e: e.tensor_tensor(out=S3(1), in0=S3(1), in1=bc(flg.ap[:, c0:c0 + 4], 32), op=ALU.mult), reads=[smG.b, flg.b], writes=[smG.b])
        P.add("dve", lambda e: e.tensor_tensor(out=S3(2), in0=S3(1), in1=b4(a_bc), op=ALU.mult), reads=[smG.b, smallp.b], writes=[smG.b])

        def tr(q):
            xt = xtk[q % 2]
            for bt in range(3):
                nt = 8 if bt < 2 else 4
                pbk = pb[bt % 2]
                for i in range(nt):
                    jj = bt * 8 + i
                    P.add("pe", lambda e, i=i, jj=jj, q=q, pbk=pbk: e.transpose(pbk.ap[:, i * 128:(i + 1) * 128], xc4.ap[:, jj, q * 128:(q + 1) * 128], identb), reads=[xcb[jj], cb16.b], writes=[pbk.b])
                if bt == 1:
                    P.add("act", lambda e, bt=bt, nt=nt, xt=xt, pbk=pbk: e.copy(out=xt.ap[:, bt * 1024:bt * 1024 + nt * 128], in_=pbk.ap[:, 0:nt * 128]), reads=[pbk.b], writes=[xt.b])
                else:
                    P.add("dve", lambda e, bt=bt, nt=nt, xt=xt, pbk=pbk: e.tensor_copy(out=xt.ap[:, bt * 1024:bt * 1024 + nt * 128], in_=pbk.ap[:, 0:nt * 128]), reads=[pbk.b], writes=[xt.b])

        SBK = [pf[2], pf[3], pf[5], pf[4]]

        def state(q):
            xt = xtk[q % 2]
            for g in range(4):
                xg = xt.ap[:, g * 512:(g + 1) * 512].rearrange("p (a b) -> p a b", a=8)
                o = q * 32 + 8 * g
                xd = xdtsA4[(q % 2) * 4 + g]
                P.add("dve", lambda e, xg=xg, o=o, xd=xd: e.tensor_tensor(out=v3(xd.ap), in0=xg, in1=bc(smG.ap[:, 6, o:o + 8], 64), op=ALU.mult), reads=[xt.b, smG.b], writes=[xd.b])
                P.add("pe", lambda e, g=g, xt=xt, xd=xd, q=q: e.matmul(SBK[g].ap, xt.ap[:, 2048 + g * 128:2048 + (g + 1) * 128], xd.ap, start=(q == 0), stop=(q == 3)), reads=[xt.b, xd.b], writes=[SBK[g].b])
            if q == 3:
                P.add("dve", lambda e: e.tensor_tensor(out=H.ap.rearrange("p (a b) -> p a b", a=32), in0=H.ap.rearrange("p (a b) -> p a b", a=32), in1=bc(smG.ap[:, 7, 0:32], 64), op=ALU.mult), reads=[H.b, smG.b], writes=[H.b])
                for g in range(4):
                    Hg = H.ap[:, g * 512:(g + 1) * 512]
                    P.add("dve", lambda e, Hg=Hg, g=g: e.tensor_tensor(out=Hg, in0=Hg, in1=SBK[g].ap, op=ALU.add), reads=[H.b, SBK[g].b], writes=[H.b])

        tr(0); tr(1)
        hooks[0]()
        P.add("pe", lambda e: e.matmul(pf[4].ap[:, 128:256], triU, S(2), start=True, stop=True), reads=[cf.b, smG.b], writes=[pf[4].b])
        P.add("pe", lambda e: e.matmul(pf[4].ap[:, 256:384], onesf, S(2), start=True, stop=True), reads=[cf.b, smG.b], writes=[pf[4].b])
        P.add("dve", lambda e: e.tensor_copy(out=smG.ap[:, 3:5, :], in_=pf[4].ap[:, 128:384].rearrange("p (a b) -> p a b", a=2)), reads=[pf[4].b], writes=[smG.b])
        P.add("dve", lambda e: e.memset(S3(5)[:, 3, :], 0.0), writes=[smG.b])
        P.add("dve", lambda e: e.tensor_copy(out=S3(5)[:, 2, :], in_=S3(4)[:, 3, :]), reads=[smG.b], writes=[smG.b])
        P.add("dve", lambda e: e.tensor_tensor(out=S3(5)[:, 1, :], in0=S3(5)[:, 2, :], in1=S3(4)[:, 2, :], op=ALU.add), reads=[smG.b], writes=[smG.b])
        P.add("dve", lambda e: e.tensor_tensor(out=S3(5)[:, 0, :], in0=S3(5)[:, 1, :], in1=S3(4)[:, 1, :], op=ALU.add), reads=[smG.b], writes=[smG.b])
        P.add("dve", lambda e: e.tensor_tensor(out=S3(7)[:, 0, :], in0=S3(5)[:, 0, :], in1=S3(4)[:, 0, :], op=ALU.add), reads=[smG.b], writes=[smG.b])
        P.add("dve", lambda e: e.tensor_tensor(out=S(6), in0=S(4), in1=S(3), op=ALU.subtract), reads=[smG.b], writes=[smG.b])
        P.add("dve", lambda e: e.tensor_tensor(out=S(6), in0=S(6), in1=S(5), op=ALU.add), reads=[smG.b], writes=[smG.b])
        P.add("act", lambda e: e.activation(out=S(6), in_=S(6), func=AF.Exp), reads=[smG.b], writes=[smG.b])
        P.add("act", lambda e: e.activation(out=S3(7)[:, 0, :], in_=S3(7)[:, 0, :], func=AF.Exp), reads=[smG.b], writes=[smG.b])
        P.add("dve", lambda e: e.tensor_tensor(out=S(6), in0=S(6), in1=S(1), op=ALU.mult), reads=[smG.b], writes=[smG.b])
        state(0); state(1)
        hooks[1]()
        tr(2); tr(3)
        hooks[2]()
        state(2); state(3)
        hooks[3]()

    NG = NPRE // 4
    g0 = NG - (npre + 3) // 4
    if not SKIP1 and g0 < NG:
        load_group(g0, g0 % 2)
        proj_in(g0, g0 % 2, g0 == NG - 1, 0, 24)
        for gi in range(g0, NG):
            proj_conv(gi, gi == NG - 1)
            if gi + 1 < NG:
                load_group(gi + 1, (gi + 1) % 2)
                hk = [lambda a=a, gi=gi: proj_in(gi + 1, (gi + 1) % 2, gi + 1 == NG - 1, a, a + 6) for a in (0, 6, 12, 18)]
            else:
                hk = [lambda: None] * 4
            group_chunks(gi, gi % 2, hk)
        P.add("pool", lambda e: e.tensor_copy(out=rawh.ap[:, :, 0:3], in_=raw4.ap[:, :, 0:3]), reads=rawb, writes=[rawh.b])
    else:
        P.add("pool", lambda e: e.memset(rawh.ap, 0.0), writes=[rawh.b])

    P.barrier()
    AB.off, AFa.off = mark1B, mark1F
    Wz = AB.get("Wz", 8, 2048); load_w(Wz, w_in[:, COL_Z:COL_Z + 2048], 2048)
    nwb = AFa.get("nwb", 2048)
    dma("sp", nwb.ap, bcast_row(normw, 2048), [], [nwb.b])
    xbs = [AB.get("xb_%d" % i, 1024) for i in range(2)]; xTs = [AB.get("xT_%d" % i, 8, 128) for i in range(2)]
    raws = [AB.get("raw_%d" % i, 24, 132) for i in range(2)]; xcs = [AB.get("xc_%d" % i, 24, 128) for i in range(2)]
    xtok = AB.get("xtok", 2560)
    xdt = AB.get("xdt", 512); xdts = AB.get("xdts", 512)
    Hb = AB.get("Hb", 2048); cbm = AB.get("cbm", 4, 128)
    LT = AB.get("LT", 4, 128); MT = AB.get("MT", 8, 128); yn = AB.get("yn", 2048)
    szb = AB.get("szb", 4, 512)
    adtU = [AFa.get("adtU%d" % i, 128) for i in range(8)]
    sm = AFa.get("sm", 12, 32)
    yacc = AFa.get("yacc", 512); ytmp = AFa.get("ytmp", 512); sz = AFa.get("sz", 512)
    ssq = AFa.get("ssq", 4)
    P.add("pool", lambda e: e.memset(raws[0].ap, 0.0), writes=[raws[0].b])
    P.add("pool", lambda e: e.memset(raws[1].ap, 0.0), writes=[raws[1].b])
    P.add("pool", lambda e: e.tensor_copy(out=raws[0].ap[:, :, 0:3], in_=rawh.ap[:, :, 0:3]), reads=[rawh.b], writes=[raws[0].b])

    CUT = int(os.environ.get('KCUT', '99')); NCH1 = int(os.environ.get('KNCH', str(NM1)))
    def front_pieces(ci, xrow, xb, xT, raw, xc, raw_next):
        def inproj(g):
            bank = pf[g % 2]
            for jj in range(4):
                j = 4 * g + jj
                chain(bank, bank.ap[:, jj * 128:(jj + 1) * 128], [(Wx.ap[:, k, j * 128:(j + 1) * 128], xT.ap[:, k, :]) for k in range(8)], [Wx.b, xT.b])
            P.add("act", lambda e, g=g, bank=bank: e.copy(out=raw.ap[:, 4 * g:4 * g + 4, 3:131], in_=bank.ap.rearrange("p (a b) -> p a b", a=4)), reads=[bank.b], writes=[raw.b])

        def conv(g):
            for jj in range(4):
                j = 4 * g + jj
                bank = pf[2 + j % 2]
                chain(bank, bank.ap[:, 0:128], [(diag.ap[:, j, tp, :], raw.ap[:, j, tp:tp + 128]) for tp in range(4)], [diag.b, raw.b])
                P.add("act", lambda e, j=j, bank=bank: e.activation(out=xc.ap[:, j, :], in_=bank.ap[:, 0:128], func=AF.Silu, bias=cbt.ap[:, j:j + 1], scale=1.0), reads=[bank.b, cbt.b], writes=[xc.b])

        def p0():
            load_xT(xrow, xb, xT); inproj(0); inproj(1)

        def p1():
            inproj(2); inproj(3)

        def p2():
            inproj(4); inproj(5)
            P.add("pool", lambda e: e.tensor_copy(out=raw_next.ap[:, :, 0:3], in_=raw.ap[:, :, 128:131]), reads=[raw.b], writes=[raw_next.b])

        def p3():
            conv(0); conv(1); conv(2); conv(3); conv(4); conv(5)
        return [p0, p1, p2, p3]

    def ssd_back(fcol, main, ci, xT, xc, hooks):
        for bt in range(3):
            nt = 8 if bt < 2 else 4
            for i in range(nt):
                j = bt * 8 + i
                P.add("pe", lambda e, i=i, j=j: e.transpose(pb[1].ap[:, i * 128:(i + 1) * 128], xc.ap[:, j, :], identb), reads=[xc.b, cb16.b], writes=[pb[1].b])
            P.add("dve", lambda e, bt=bt, nt=nt: e.tensor_copy(out=xtok.ap[:, bt * 1024:bt * 1024 + nt * 128], in_=pb[1].ap[:, 0:nt * 128]), reads=[pb[1].b], writes=[xtok.b])
        chain(pf[4], pf[4].ap[:, 0:32], [(xT.ap[:, k, :], Wdt.ap[:, k, :]) for k in range(8)], [xT.b, Wdt.b])
        S = lambda i: sm.ap[:, i, :]
        P.add("dve", lambda e: e.tensor_tensor(out=S(0), in0=pf[4].ap[:, 0:32], in1=dtb_bc, op=ALU.add), reads=[pf[4].b, smallp.b], writes=[sm.b])
        P.add("act", lambda e: e.activation(out=S(0), in_=S(0), func=AF.Exp), reads=[sm.b], writes=[sm.b])
        P.add("act", lambda e: e.activation(out=S(1), in_=S(0), func=AF.Ln, bias=onecol, scale=1.0), reads=[sm.b, cf.b], writes=[sm.b])
        P.add("dve", lambda e: e.tensor_scalar(out=S(1), in0=S(1), scalar1=fcol, scalar2=None, op0=ALU.mult), reads=[sm.b, flg.b], writes=[sm.b])
        P.add("dve", lambda e: e.tensor_tensor(out=S(2), in0=S(1), in1=a_bc, op=ALU.mult), reads=[sm.b, smallp.b], writes=[sm.b])
        P.add("pe", lambda e: e.matmul(pf[4].ap[:, 32:64], triU, S(2), start=True, stop=True), reads=[cf.b, sm.b], writes=[pf[4].b])
        P.add("pe", lambda e: e.matmul(pf[4].ap[:, 64:96], onesf, S(2), start=True, stop=True), reads=[cf.b, sm.b], writes=[pf[4].b])
        P.add("dve", lambda e: e.tensor_copy(out=sm.ap[:, 3:5, :], in_=pf[4].ap[:, 32:96].rearrange("p (a b) -> p a b", a=2)), reads=[pf[4].b], writes=[sm.b])
        if main:
            for g in range(4):
                P.add("pe", lambda e, g=g: e.matmul(pf[5].ap[:, g * 128:(g + 1) * 128], xc.ap[:, 16 + g, :], xc.ap[:, 20 + g, :], start=True, stop=True), reads=[xc.b], writes=[pf[5].b])
            P.add("dve", lambda e: e.tensor_tensor(out=cbm.ap, in0=pf[5].ap.rearrange("p (a b) -> p a b", a=4), in1=maskb.unsqueeze(1).to_broadcast([128, 4, 128]), op=ALU.mult), reads=[pf[5].b, cb16.b], writes=[cbm.b])
            P.add("pool", lambda e: e.tensor_copy(out=Hb.ap, in_=H.ap), reads=[H.b], writes=[Hb.b])
            P.add("act", lambda e: e.activation(out=S(5), in_=S(3), func=AF.Exp), reads=[sm.b], writes=[sm.b])
            for g in range(4):
                zb = pf[5 - g % 2]
                chain(zb, zb.ap, [(xT.ap[:, k, :], Wz.ap[:, k, g * 512:(g + 1) * 512]) for k in range(8)], [xT.b, Wz.b])
                P.add("act", lambda e, g=g, zb=zb: e.activation(out=szb.ap[:, g, :], in_=zb.ap, func=AF.Silu), reads=[zb.b], writes=[szb.b])
            for g in range(4):
                xg = xtok.ap[:, g * 512:(g + 1) * 512].rearrange("p (a b) -> p a b", a=8)
                P.add("dve", lambda e, g=g, xg=xg: e.tensor_tensor(out=xdt.ap.rearrange("p (a b) -> p a b", a=8), in0=xg, in1=bc(sm.ap[:, 1, 8 * g:8 * g + 8], 64), op=ALU.mult), reads=[xtok.b, sm.b], writes=[xdt.b])
                for hh in range(2):
                    bank = pf[hh]
                    for h4 in range(4):
                        h = 8 * g + 4 * hh + h4
                        au = adtU[h % 8]
                        P.add("dve", lambda e, h=h, au=au: e.tensor_scalar(out=au.ap, in0=Ustr, scalar1=sm.ap[:, 2, h:h + 1], scalar2=None, op0=ALU.mult), reads=[cf.b, sm.b], writes=[au.b])
                        P.add("pe", lambda e, h4=h4, au=au, bank=bank: e.matmul(bank.ap[:, h4 * 128:(h4 + 1) * 128], au.ap, triU, start=True, stop=True), reads=[au.b, cf.b], writes=[bank.b])
                    P.add("act", lambda e, bank=bank: e.activation(out=LT.ap.rearrange("p a b -> p (a b)"), in_=bank.ap, func=AF.Exp), reads=[bank.b], writes=[LT.b])
                    P.add("dve", lambda e, g=g, hh=hh: e.tensor_tensor(out=MT.ap[:, 4 * hh:4 * hh + 4, :], in0=LT.ap, in1=cbm.ap[:, g, :].unsqueeze(1).to_broadcast([128, 4, 128]), op=ALU.mult), reads=[LT.b, cbm.b], writes=[MT.b])
                for h8 in range(8):
                    P.add("pe", lambda e, h8=h8: e.matmul(pf[2].ap[:, h8 * 64:(h8 + 1) * 64], MT.ap[:, h8, :], xdt.ap[:, h8 * 64:(h8 + 1) * 64], start=True, stop=True), reads=[MT.b, xdt.b], writes=[pf[2].b])
                P.add("pe", lambda e, g=g: e.matmul(pf[3].ap, xc.ap[:, 20 + g, :], Hb.ap[:, g * 512:(g + 1) * 512], start=True, stop=True), reads=[xc.b, Hb.b], writes=[pf[3].b])
                v3 = lambda ap: ap.rearrange("p (a b) -> p a b", a=8)
                P.add("dve", lambda e, g=g: e.tensor_tensor(out=v3(yacc.ap), in0=v3(pf[3].ap), in1=bc(sm.ap[:, 5, 8 * g:8 * g + 8], 64), op=ALU.mult), reads=[pf[3].b, sm.b], writes=[yacc.b])
                P.add("dve", lambda e: e.tensor_tensor(out=yacc.ap, in0=yacc.ap, in1=pf[2].ap, op=ALU.add), reads=[pf[2].b, yacc.b], writes=[yacc.b])
                P.add("dve", lambda e, g=g, xg=xg: e.tensor_tensor(out=v3(ytmp.ap), in0=xg, in1=bc(D_bc[:, 8 * g:8 * g + 8], 64), op=ALU.mult), reads=[xtok.b, smallp.b], writes=[ytmp.b])
                P.add("dve", lambda e: e.tensor_tensor(out=yacc.ap, in0=yacc.ap, in1=ytmp.ap, op=ALU.add), reads=[ytmp.b, yacc.b], writes=[yacc.b])
                P.add("dve", lambda e, g=g: e.tensor_tensor(out=yacc.ap, in0=yacc.ap, in1=szb.ap[:, g, :], op=ALU.mult), reads=[szb.b, yacc.b], writes=[yacc.b])
                P.add("dve", lambda e: e.tensor_tensor(out=ytmp.ap, in0=yacc.ap, in1=yacc.ap, op=ALU.mult), reads=[yacc.b], writes=[ytmp.b])
                P.add("dve", lambda e, g=g: e.reduce_sum(out=ssq.ap[:, g:g + 1], in_=ytmp.ap, axis=mybir.AxisListType.X), reads=[ytmp.b], writes=[ssq.b])
                P.add("dve", lambda e, g=g: e.tensor_scalar(out=ssq.ap[:, g:g + 1], in0=ssq.ap[:, g:g + 1], scalar1=1.0 / 512, scalar2=1e-5, op0=ALU.mult, op1=ALU.add), reads=[ssq.b], writes=[ssq.b])
                P.add("act", lambda e, g=g: e.activation(out=ssq.ap[:, g:g + 1], in_=ssq.ap[:, g:g + 1], func=AF.Ln), reads=[ssq.b], writes=[ssq.b])
                P.add("act", lambda e, g=g: e.activation(out=ssq.ap[:, g:g + 1], in_=ssq.ap[:, g:g + 1], func=AF.Exp, scale=-0.5), reads=[ssq.b], writes=[ssq.b])
                P.add("dve", lambda e, g=g: e.scalar_tensor_tensor(out=yn.ap[:, g * 512:(g + 1) * 512], in0=yacc.ap, scalar=ssq.ap[:, g:g + 1], in1=nwb.ap[:, g * 512:(g + 1) * 512], op0=ALU.mult, op1=ALU.mult), reads=[yacc.b, ssq.b, nwb.b], writes=[yn.b])
                hooks[g]()
            dma("sp", ynd[ci * 128:(ci + 1) * 128, :], yn.ap, [yn.b], [])
        P.add("dve", lambda e: e.tensor_tensor(out=S(6), in0=S(4), in1=S(3), op=ALU.subtract), reads=[sm.b], writes=[sm.b])
        P.add("act", lambda e: e.activation(out=S(6), in_=S(6), func=AF.Exp), reads=[sm.b], writes=[sm.b])
        P.add("act", lambda e: e.activation(out=S(7), in_=S(4), func=AF.Exp), reads=[sm.b], writes=[sm.b])
        P.add("dve", lambda e: e.tensor_tensor(out=S(6), in0=S(6), in1=S(1), op=ALU.mult), reads=[sm.b], writes=[sm.b])
        for g in range(4):
            xg = xtok.ap[:, g * 512:(g + 1) * 512].rearrange("p (a b) -> p a b", a=8)
            v3 = lambda ap: ap.rearrange("p (a b) -> p a b", a=8)
            P.add("dve", lambda e, g=g, xg=xg: e.tensor_tensor(out=v3(xdts.ap), in0=xg, in1=bc(sm.ap[:, 6, 8 * g:8 * g + 8], 64), op=ALU.mult), reads=[xtok.b, sm.b], writes=[xdts.b])
            bank = pf[2 + g % 2]
            P.add("pe", lambda e, g=g, bank=bank: e.matmul(bank.ap, xtok.ap[:, 2048 + g * 128:2048 + (g + 1) * 128], xdts.ap, start=True, stop=True), reads=[xtok.b, xdts.b], writes=[bank.b])
            Hg = H.ap[:, g * 512:(g + 1) * 512]
            P.add("dve", lambda e, g=g, Hg=Hg: e.tensor_tensor(out=v3(Hg), in0=v3(Hg), in1=bc(sm.ap[:, 7, 8 * g:8 * g + 8], 64), op=ALU.mult), reads=[H.b, sm.b], writes=[H.b])
            P.add("dve", lambda e, Hg=Hg, bank=bank: e.tensor_tensor(out=Hg, in0=Hg, in1=bank.ap, op=ALU.add), reads=[H.b, bank.b], writes=[H.b])

    def fp(ci):
        b = ci % 2
        return front_pieces(ci, xmain[(ci + 1) * 128:(ci + 2) * 128, :], xbs[b], xTs[b], raws[b], xcs[b], raws[1 - b])
    nch = 0 if SKIP1 else NCH1
    if nch:
        for p_ in fp(0):
            p_()
    for ci in range(nch):
        nxt = fp(ci + 1) if ci + 1 < nch else [lambda: None] * 4
        ssd_back(flg.ap[:, NPRE + ci:NPRE + ci + 1], True, ci, xTs[ci % 2], xcs[ci % 2], nxt)

    if stop < 2:
        P.emit(nc); es.close(); return nc
    P.barrier()
    AB.off, AFa.off = markB, markF
    Wq = AB.get("Wq", 8, 1024); load_w(Wq, w_in[:, COL_Q:COL_Q + 1024], 1024)
    Wk2 = AB.get("Wk2", 8, 128); load_w(Wk2, w_in[:, COL_K:COL_K + 128], 128)
    Wv = AB.get("Wv", 8, 128); load_w(Wv, w_in[:, COL_V:COL_V + 128], 128)
    EB = AB.get("EB", 2, 16, 128)
    AB3 = Arena(ABt, NB); AB3.off = NB - 49152
    Wg = AB3.get("Wg", 8, 2048); Wbs = AB3.get("Wbs", 16, 1024); Wba = AB3.get("Wba", 8, 1024); Wmx = AB3.get("Wmx", 8, 1024)
    w3_list = []
    for dst_, src_, nc_ in ((Wg, w_in[:, COL_G:COL_G + 2048], 2048), (Wbs, w_bs, 1024), (Wba, w_ba, 1024), (Wmx, w_mix, 1024)):
        for c0 in range(0, nc_, 512):
            for k in range(src_.shape[0] // 128):
                w3_list.append((dst_, dst_.ap[:, k, c0:c0 + 512], src_[k * 128:(k + 1) * 128, c0:c0 + 512]))
    w3_pos = [0]

    def w3_issue(n):
        for _ in range(n):
            if w3_pos[0] < len(w3_list):
                d_, o_, i_ = w3_list[w3_pos[0]]
                dma("pool", o_, i_, [], [d_.b])
                w3_pos[0] += 1
    ebf = AFa.get("ebf", 2048); mkf = AFa.get("mkf", 2048)
    for kt in range(2):
        dma("sp", ebf.ap, biasg[:, kt * 2048:(kt + 1) * 2048], [], [ebf.b])
        dma("sp", mkf.ap, maskg[:, kt * 2048:(kt + 1) * 2048], [], [mkf.b])
        P.add("act", lambda e: e.activation(out=ebf.ap, in_=ebf.ap, func=AF.Exp), reads=[ebf.b], writes=[ebf.b])
        P.add("dve", lambda e, kt=kt: e.tensor_tensor(out=EB.ap[:, kt, :, :].rearrange("p a b -> p (a b)"), in0=ebf.ap, in1=mkf.ap, op=ALU.mult), reads=[ebf.b, mkf.b], writes=[EB.b])
    xb2s = [AB.get("xb2_%d" % i, 1024) for i in range(2)]; xT = AB.get("xT2", 8, 128)
    kT = [AB.get("kT%d" % i, 2, 128) for i in range(2)]
    vx = [AB.get("vx%d" % i, 2, 65) for i in range(2)]
    for i in range(2):
        P.add("pool", lambda e, i=i: e.memset(vx[i].ap, 1.0), writes=[vx[i].b])
    qT = AB.get("qT", 16, 128); et = AB.get("et", 4, 128)
    PT = [AB.get("PT%d" % i, 4, 128) for i in range(2)]
    ya = AB.get("ya", 1024)
    den = AFa.get("den", 4)
    flag0 = flg.ap[:, NPRE:NPRE + 1]
    CUT2 = int(os.environ.get('KCUT2', '99')); NCH2 = int(os.environ.get('KNCH2', str(NM2)))
    for ci in range(NCH2):
        sl = ci % 2
        load_xT(xmain[ci * 128:(ci + 1) * 128, :], xb2s[ci % 2], xT)
        w3_issue(6)
        for kv in range(2):
            chain(pf[0], pf[0].ap[0:64, kv * 128:(kv + 1) * 128], [(Wk2.ap[:, k, kv * 64:(kv + 1) * 64], xT.ap[:, k, :]) for k in range(8)], [Wk2.b, xT.b])
        P.add("act", lambda e, sl=sl: e.copy(out=kT[sl].ap[0:64, :, :], in_=pf[0].ap[0:64, 0:256].rearrange("p (a b) -> p a b", a=2)), reads=[pf[0].b], writes=[kT[sl].b])
        chain(pf[1], pf[1].ap[:, 0:128], [(xT.ap[:, k, :], Wv.ap[:, k, :]) for k in range(8)], [xT.b, Wv.b])
        P.add("dve", lambda e, sl=sl: e.tensor_copy(out=vx[sl].ap[:, :, 0:64], in_=pf[1].ap[:, 0:128].rearrange("p (a b) -> p a b", a=2)), reads=[pf[1].b], writes=[vx[sl].b])
        if ci == 0 or CUT2 <= 1:
            continue
        for q4 in range(4):
            bank = pf[2 + q4 % 2]
            for tt in range(4):
                j = q4 * 4 + tt
                chain(bank, bank.ap[0:64, tt * 128:(tt + 1) * 128], [(Wq.ap[:, k, j * 64:(j + 1) * 64], xT.ap[:, k, :]) for k in range(8)], [Wq.b, xT.b])
            P.add("act", lambda e, q4=q4, bank=bank: e.copy(out=qT.ap[0:64, q4 * 4:q4 * 4 + 4, :], in_=bank.ap[0:64, :].rearrange("p (a b) -> p a b", a=4)), reads=[bank.b], writes=[qT.b])
        for kvh in range(2):
            if CUT2 <= 2: break
            for hb in range(2):
                j0 = kvh * 8 + hb * 4
                for kt in range(2):
                    slk = (ci + 1 + kt) % 2
                    bank = pf[4 + kt]
                    for i in range(4):
                        j = j0 + i
                        base = (j % 2) * 64 * int(os.environ.get("KB64", "1"))
                        P.add("pe", lambda e, i=i, j=j, base=base, slk=slk, bank=bank, kvh=kvh: e.matmul(bank.ap[:, i * 128:(i + 1) * 128], kT[slk].ap[0:64, kvh, :], qT.ap[0:64, j, :], start=True, stop=True), reads=[kT[slk].b, qT.b], writes=[bank.b])
                    P.add("act", lambda e, bank=bank: e.activation(out=et.ap.rearrange("p a b -> p (a b)"), in_=bank.ap, func=AF.Exp, scale=0.125), reads=[bank.b], writes=[et.b])
                    if ci == 2 and kt == 0:
                        P.add("dve", lambda e, kt=kt, j0=j0: e.scalar_tensor_tensor(out=PT[kt].ap, in0=et.ap, scalar=flag0, in1=EB.ap[:, kt, j0:j0 + 4, :], op0=ALU.mult, op1=ALU.mult), reads=[et.b, EB.b, flg.b], writes=[PT[kt].b])
                    else:
                        P.add("dve", lambda e, kt=kt, j0=j0: e.tensor_tensor(out=PT[kt].ap, in0=et.ap, in1=EB.ap[:, kt, j0:j0 + 4, :], op=ALU.mult), reads=[et.b, EB.b], writes=[PT[kt].b])
                if CUT2 <= 3: continue
                bank = pf[hb]
                for i in range(4):
                    for kt in range(2):
                        slk = (ci + 1 + kt) % 2
                        P.add("pe", lambda e, i=i, kt=kt, slk=slk, bank=bank, kvh=kvh: e.matmul(bank.ap[:, i * 65:(i + 1) * 65], PT[kt].ap[:, i, :], vx[slk].ap[:, kvh, :], start=(kt == 0), stop=(kt == 1)), reads=[PT[kt].b, vx[slk].b], writes=[bank.b])
                if CUT2 <= 4: continue
                pv = bank.ap[:, 0:260].rearrange("p (a b) -> p a b", a=4)
                P.add("dve", lambda e, pv=pv, j0=j0: e.tensor_tensor(out=den.ap, in0=pv[:, :, 64], in1=esink[:, j0:j0 + 4], op=ALU.add), reads=[bank.b, smallp.b], writes=[den.b])
                P.add("dve", lambda e: e.reciprocal(out=den.ap, in_=den.ap), reads=[den.b], writes=[den.b])
                P.add("dve", lambda e, pv=pv, j0=j0: e.tensor_tensor(out=ya.ap[:, j0 * 64:(j0 + 4) * 64].rearrange("p (a b) -> p a b", a=4), in0=pv[:, :, 0:64], in1=bc(den.ap, 64), op=ALU.mult), reads=[bank.b, den.b], writes=[ya.b])
        dma("sp", yad[(ci - 1) * 128:ci * 128, :], ya.ap, [ya.b], [])

    if stop < 3:
        P.emit(nc); es.close(); return nc
    w3_issue(len(w3_list))
    P.barrier()
    AB.off, AFa.off = markB, markF
    bgb = AFa.get("bgb", 2048); dma("sp", bgb.ap, bcast_row(b_gate, 2048), [], [bgb.b])
    lng = AFa.get("lng", 2, 1024)
    dma("sp", lng.ap[:, 0, :], bcast_row(ln1g, 1024), [], [lng.b]); dma("sp", lng.ap[:, 1, :], bcast_row(ln1b, 1024), [], [lng.b])
    xb = AB.get("xb3", 1024); xT = AB.get("xT3", 8, 128)
    ynb = AB.get("ynb", 2048); yab = AB.get("yab", 1024)
    ynT = AB.get("ynT", 16, 128); yaT = AB.get("yaT", 8, 128)
    mg = AB.get("mg", 1024); mT = AB.get("mT", 8, 128)
    xf = AFa.get("xf", 1024); gt = AB.get("gt", 2048); m1 = AFa.get("m1", 512); r = AFa.get("r", 1024)
    st = AFa.get("st", 2, 6); mv = AFa.get("mv", 2)

    def transp(src, dst, ntl):
        for bt in range(ntl // 8):
            for i in range(8):
                j = bt * 8 + i
                P.add("pe", lambda e, i=i, j=j: e.transpose(pb[1].ap[:, i * 128:(i + 1) * 128], src.ap[:, j * 128:(j + 1) * 128], identb), reads=[src.b, cb16.b], writes=[pb[1].b])
            P.add("act", lambda e, bt=bt: e.copy(out=dst.ap[:, bt * 8:bt * 8 + 8, :].rearrange("p a b -> p (a b)"), in_=pb[1].ap), reads=[pb[1].b], writes=[dst.b])

    def layer_norm(r, g_ap, b_ap, gb, st, mv):
        for i in range(2):
            P.add("dve", lambda e, i=i: e.bn_stats(out=st.ap[:, i, :], in_=r.ap[:, i * 512:(i + 1) * 512]), reads=[r.b], writes=[st.b])
        P.add("dve", lambda e: e.bn_aggr(out=mv.ap, in_=st.ap.rearrange("p a b -> p (a b)")), reads=[st.b], writes=[mv.b])
        P.add("dve", lambda e: e.tensor_scalar(out=mv.ap[:, 1:2], in0=mv.ap[:, 1:2], scalar1=1e-5, scalar2=None, op0=ALU.add), reads=[mv.b], writes=[mv.b])
        P.add("act", lambda e: e.activation(out=mv.ap[:, 1:2], in_=mv.ap[:, 1:2], func=AF.Sqrt), reads=[mv.b], writes=[mv.b])
        P.add("dve", lambda e: e.reciprocal(out=mv.ap[:, 1:2], in_=mv.ap[:, 1:2]), reads=[mv.b], writes=[mv.b])
        P.add("dve", lambda e: e.tensor_scalar(out=r.ap, in0=r.ap, scalar1=mv.ap[:, 0:1], scalar2=mv.ap[:, 1:2], op0=ALU.subtract, op1=ALU.mult), reads=[r.b, mv.b], writes=[r.b])
        P.add("dve", lambda e: e.tensor_tensor(out=r.ap, in0=r.ap, in1=g_ap, op=ALU.mult), reads=[r.b, gb], writes=[r.b])
        P.add("dve", lambda e: e.tensor_tensor(out=r.ap, in0=r.ap, in1=b_ap, op=ALU.add), reads=[r.b, gb], writes=[r.b])

    def p3_loads(ci, xb, xf, ynb, yab):
        xrow = xmain[(ci + 1) * 128:(ci + 2) * 128, :]
        dma("pool", xb.ap, xrow, [], [xb.b])
        dma("sp", xf.ap, xrow, [], [xf.b])
        dma("sp", ynb.ap, ynd[ci * 128:(ci + 1) * 128, :], [], [ynb.b])
        dma("sp", yab.ap, yad[ci * 128:(ci + 1) * 128, :], [], [yab.b])

    def p3_chunk(ci, xb, xf, ynb, yab, r):
        for k in range(8):
            P.add("pe", lambda e, k=k: e.transpose(pb[0].ap[:, k * 128:(k + 1) * 128], xb.ap[:, k * 128:(k + 1) * 128], identb), reads=[xb.b, cb16.b], writes=[pb[0].b])
        P.add("act", lambda e: e.copy(out=xT.ap.rearrange("p a b -> p (a b)"), in_=pb[0].ap), reads=[pb[0].b], writes=[xT.b])
        transp(ynb, ynT, 16)
        transp(yab, yaT, 8)
        for s4 in range(4):
            bank = pf[s4 % 2]
            chain(bank, bank.ap, [(xT.ap[:, k, :], Wg.ap[:, k, s4 * 512:(s4 + 1) * 512]) for k in range(8)], [xT.b, Wg.b])
            P.add("dve", lambda e, s4=s4, bank=bank: e.tensor_tensor(out=gt.ap[:, s4 * 512:(s4 + 1) * 512], in0=bank.ap, in1=bgb.ap[:, s4 * 512:(s4 + 1) * 512], op=ALU.add), reads=[bank.b, bgb.b], writes=[gt.b])
        P.add("act", lambda e: e.activation(out=gt.ap, in_=gt.ap, func=AF.Sigmoid), reads=[gt.b], writes=[gt.b])
        for hf in range(2):
            chain(pf[2], pf[2].ap, [(ynT.ap[:, i, :], Wbs.ap[:, i, hf * 512:(hf + 1) * 512]) for i in range(16)], [ynT.b, Wbs.b])
            chain(pf[3], pf[3].ap, [(yaT.ap[:, i, :], Wba.ap[:, i, hf * 512:(hf + 1) * 512]) for i in range(8)], [yaT.b, Wba.b])
            P.add("dve", lambda e, hf=hf: e.tensor_tensor(out=m1.ap, in0=pf[2].ap, in1=gt.ap[:, hf * 512:(hf + 1) * 512], op=ALU.mult), reads=[pf[2].b, gt.b], writes=[m1.b])
            P.add("dve", lambda e, hf=hf: e.tensor_tensor(out=r.ap[:, hf * 512:(hf + 1) * 512], in0=pf[3].ap, in1=gt.ap[:, 1024 + hf * 512:1024 + (hf + 1) * 512], op=ALU.mult), reads=[pf[3].b, gt.b], writes=[r.b])
            P.add("dve", lambda e, hf=hf: e.tensor_tensor(out=mg.ap[:, hf * 512:(hf + 1) * 512], in0=m1.ap, in1=r.ap[:, hf * 512:(hf + 1) * 512], op=ALU.add), reads=[m1.b, r.b], writes=[mg.b])
        transp(mg, mT, 8)
        for hf in range(2):
            bank = pf[4 + hf]
            chain(bank, bank.ap, [(mT.ap[:, i, :], Wmx.ap[:, i, hf * 512:(hf + 1) * 512]) for i in range(8)], [mT.b, Wmx.b])
            P.add("dve", lambda e, hf=hf, bank=bank: e.scalar_tensor_tensor(out=r.ap[:, hf * 512:(hf + 1) * 512], in0=xf.ap[:, hf * 512:(hf + 1) * 512], scalar=ALPHA, in1=bank.ap, op0=ALU.mult, op1=ALU.add), reads=[xf.b, bank.b], writes=[r.b])
        layer_norm(r, lng.ap[:, 0, :], lng.ap[:, 1, :], lng.b, st, mv)
        if ci == 0:
            P.add("dve", lambda e: e.tensor_scalar(out=r.ap, in0=r.ap, scalar1=flag0, scalar2=None, op0=ALU.mult), reads=[r.b, flg.b], writes=[r.b])
        dma("sp", h1d[ci * 128:(ci + 1) * 128, :], r.ap, [r.b], [])


    xb3 = [xb, AB.get("xb3b", 1024)]; xf3 = [xf, AFa.get("xf3b", 1024)]
    ABp = Arena(ABt, NB); ABp.off = markB
    Wup_pre = ABp.get("Wup_pre", 8, 5632)
    wp_list = [(k, c0) for k in (3, 4, 5) for c0 in range(0, 5632, 512)]
    wp_pos = [0]

    def wp_issue(n):
        for _ in range(n):
            if wp_pos[0] < len(wp_list):
                k, c0 = wp_list[wp_pos[0]]
                dma("pool", Wup_pre.ap[:, k, c0:c0 + 512], w_up[k * 128:(k + 1) * 128, c0:c0 + 512], [], [Wup_pre.b])
                wp_pos[0] += 1
    ynb3 = [ynb, AB.get("ynb3b", 2048)]; yab3 = [yab, AB.get("yab3b", 1024)]; r3 = [r, AFa.get("r3b", 1024)]
    assert AB.off <= markB + 3 * 5632, AB.off
    p3_loads(0, xb3[0], xf3[0], ynb3[0], yab3[0])
    for ci in range(NM1):
        if ci + 1 < NM1:
            b_ = (ci + 1) % 2
            p3_loads(ci + 1, xb3[b_], xf3[b_], ynb3[b_], yab3[b_])
        b_ = ci % 2
        wp_issue(2)
        p3_chunk(ci, xb3[b_], xf3[b_], ynb3[b_], yab3[b_], r3[b_])

    if stop < 4:
        P.emit(nc); es.close(); return nc
    wp_issue(len(wp_list))
    P.barrier()
    AB.off, AFa.off = markB, markF
    Wup = AB.get("Wup", 8, 5632)
    wub = [Buf("wub%d" % i) for i in range(11)]
    for bi in (0, 5, 6, 1, 7, 2, 8, 3, 9, 4, 10):
        c0 = bi * 512
        for k in (0, 1, 2, 6, 7):
            dma("pool", Wup.ap[:, k, c0:c0 + 512], w_up[k * 128:(k + 1) * 128, c0:c0 + 512], [], [wub[bi]])
    Wdn = AB.get("Wdn", 22, 1024)
    wdb = [Buf("wdb%d" % i) for i in range(22)]
    for k in range(22):
        for c0 in (0, 512):
            dma("pool", Wdn.ap[:, k, c0:c0 + 512], w_dn[k * 128:(k + 1) * 128, c0:c0 + 512], [], [wdb[k]])
    fw = AFa.get("fw", 132); fb = AFa.get("fb", 44)
    per_channel(fw.ap[:, 0:88], fcw[0:88, :], 88); fw.b.writer = P.ops["dve"][-1]
    per_channel(fw.ap[:, 88:132], fcw[88:132, :], 44); fw.b.writer = P.ops["dve"][-1]
    per_channel(fb.ap, fcb, 44); fb.b.writer = P.ops["dve"][-1]
    lng2 = AFa.get("lng2", 2, 1024)
    dma("sp", lng2.ap[:, 0, :], bcast_row(ln2g, 1024), [], [lng2.b]); dma("sp", lng2.ap[:, 1, :], bcast_row(ln2b, 1024), [], [lng2.b])
    SC = 256
    hAs = [AB.get("hA%d" % i, 1024) for i in range(2)]; hBs = [AB.get("hB%d" % i, 1024) for i in range(2)]; h1T = AB.get("h1T", 8, SC + 2)
    aT = AB.get("aT", 22, SC)
    aTb = [Buf("aTb%d" % i) for i in range(22)]
    cvs = [AFa.get("cvs%d" % i, SC) for i in range(4)]
    t0s = [AFa.get("t0s%d" % i, SC) for i in range(2)]
    sgs = [AFa.get("sg%d" % i, SC) for i in range(2)]
    hrs = [AFa.get("hr%d" % i, 1024) for i in range(2)]; r4s = [AFa.get("r4_%d" % i, 1024) for i in range(2)]
    st4 = AFa.get("st4", 2, 6); mv4 = AFa.get("mv4", 2)
    NT = SC // 128
    tcount = 0
    for sc in range(TOK // SC):
        r0 = 128 + sc * SC
        for i in range(NT):
            hA = hAs[i % 2]
            dma("pool", hA.ap, h1d[r0 - 2 + i * 128:r0 + 126 + i * 128, :], [], [hA.b])
            for k in range(8):
                P.add("pe", lambda e, k=k, hA=hA: e.transpose(pb[0].ap[:, k * 128:(k + 1) * 128], hA.ap[:, k * 128:(k + 1) * 128], identb), reads=[hA.b, cb16.b], writes=[pb[0].b])
            P.add("act", lambda e, i=i: e.copy(out=h1T.ap[:, :, i * 128:(i + 1) * 128], in_=pb[0].ap.rearrange("p (a b) -> p a b", a=8)), reads=[pb[0].b], writes=[h1T.b])
        hB = hBs[sc % 2]
        dma("pool", hB.ap[0:2, :], h1d[r0 + SC - 2:r0 + SC, :], [], [hB.b])
        for k in range(8):
            P.add("pe", lambda e, k=k, hB=hB: e.transpose(pb[1].ap[:, k * 2:(k + 1) * 2], hB.ap[0:2, k * 128:(k + 1) * 128], identb[0:2, 0:2]), reads=[hB.b, cb16.b], writes=[pb[1].b])
        P.add("act", lambda e: e.copy(out=h1T.ap[:, :, SC:SC + 2], in_=pb[1].ap[:, 0:16].rearrange("p (a b) -> p a b", a=8)), reads=[pb[1].b], writes=[h1T.b])
        def down(jp):
            for tt in range(NT):
                for hf in range(2):
                    bank = pf[2 + tt * 2 + hf]
                    P.add("pe", lambda e, jp=jp, tt=tt, hf=hf, bank=bank: e.matmul(bank.ap, aT.ap[:, jp, tt * 128:(tt + 1) * 128], Wdn.ap[:, jp, hf * 512:(hf + 1) * 512], start=(jp == 0), stop=(jp == 21)), reads=[aTb[jp], wdb[jp]], writes=[bank.b])

        for jp in range(22):
            for gv in range(2):
                j = jp + 22 * gv
                X = pf[tcount % 2]; t0 = t0s[tcount % 2]
                cv = cvs[(jp % 2) * 2 + gv]
                tcount += 1
                chain(X, X.ap[:, 0:SC + 2], [(Wup.ap[:, k, j * 128:(j + 1) * 128], h1T.ap[:, k, 0:SC + 2]) for k in range(8)], [wub[j // 4], h1T.b])
                w0, w1, w2, bb = fw.ap[:, j:j + 1], fw.ap[:, 44 + j:45 + j], fw.ap[:, 88 + j:89 + j], fb.ap[:, j:j + 1]
                P.add("act", lambda e, X=X, t0=t0, w2=w2, bb=bb: e.activation(out=t0.ap, in_=X.ap[:, 2:SC + 2], func=AF.Identity, bias=bb, scale=w2), reads=[X.b, fw.b, fb.b], writes=[t0.b])
                P.add("dve", lambda e, X=X, t0=t0, w1=w1: e.scalar_tensor_tensor(out=t0.ap, in0=X.ap[:, 1:SC + 1], scalar=w1, in1=t0.ap, op0=ALU.mult, op1=ALU.add), reads=[X.b, fw.b, t0.b], writes=[t0.b])
                P.add("dve", lambda e, X=X, t0=t0, w0=w0, cv=cv: e.scalar_tensor_tensor(out=cv.ap, in0=X.ap[:, 0:SC], scalar=w0, in1=t0.ap, op0=ALU.mult, op1=ALU.add), reads=[X.b, fw.b, t0.b], writes=[cv.b])
            cg, cvv = cvs[(jp % 2) * 2], cvs[(jp % 2) * 2 + 1]
            sgp = sgs[jp % 2]
            P.add("act", lambda e, cg=cg, sgp=sgp: e.activation(out=sgp.ap, in_=cg.ap, func=AF.Silu), reads=[cg.b], writes=[sgp.b])
            P.add("dve", lambda e, jp=jp, cvv=cvv, sgp=sgp: e.tensor_tensor(out=aT.ap[:, jp, :], in0=sgp.ap, in1=cvv.ap, op=ALU.mult), reads=[sgp.b, cvv.b], writes=[aTb[jp]])
            if jp >= 1:
                down(jp - 1)
        down(21)
        for tt in range(NT):
            rr = r0 + tt * 128
            hr = hrs[tt % 2]; r4 = r4s[tt % 2]
            dma("sp", hr.ap, h1d[rr:rr + 128, :], [], [hr.b])
            for hf in range(2):
                bank = pf[2 + tt * 2 + hf]
                P.add("dve", lambda e, hf=hf, bank=bank, hr=hr, r4=r4: e.scalar_tensor_tensor(out=r4.ap[:, hf * 512:(hf + 1) * 512], in0=hr.ap[:, hf * 512:(hf + 1) * 512], scalar=ALPHA, in1=bank.ap, op0=ALU.mult, op1=ALU.add), reads=[hr.b, bank.b], writes=[r4.b])
            layer_norm(r4, lng2.ap[:, 0, :], lng2.ap[:, 1, :], lng2.b, st4, mv4)
            dma("sp", out[rr - 128:rr, :], r4.ap, [r4.b], [])

    P.emit(nc)
    es.close()
    return nc


def rel_bucket_np(rel):
    n = np.maximum(rel, 0)
    nf = np.maximum(n, 1).astype(np.float32)
    large = 16 + (np.log(nf / np.float32(16)) / np.float32(np.log(128 / 16)) * np.float32(16)).astype(np.int32)
    large = np.minimum(large, 31)
    return np.where(n < 16, n, large)


_NC = None


def kernel(_dbg=None, **inp):
    global _NC
    x = np.asarray(inp["x"], np.float32)[0]
    f = lambda k: np.ascontiguousarray(np.asarray(inp[k], np.float32)[0])
    common = {
        "w_in": f("w_in"), "b_gate": f("b_gate")[None], "dtb": f("ssm_dt_bias")[None], "alog": f("ssm_a_log")[None],
        "dsk": f("ssm_d")[None], "normw": f("ssm_norm_w")[None], "sinks": f("attn_sinks")[None],
        "w_bs": f("w_branch_ssm"), "w_ba": f("w_branch_attn"), "w_mix": f("w_mix_out"),
        "ln1g": f("ln1_g")[None], "ln1b": f("ln1_b")[None], "ln2g": f("ln2_g")[None], "ln2b": f("ln2_b")[None],
        "w_up": f("w_up"), "w_dn": f("w_down"),
    }
    scw = f("ssm_conv_w")
    common["scw"] = np.ascontiguousarray(scw.reshape(4 * 24, 128))
    common["scb"] = np.ascontiguousarray(f("ssm_conv_b").reshape(24, 128))
    common["fcw"] = np.ascontiguousarray(f("ffn_conv_w").reshape(3 * 44, 128))
    common["fcb"] = np.ascontiguousarray(f("ffn_conv_b").reshape(44, 128))
    s = np.arange(128)
    ident = np.eye(128, dtype=np.float32)
    triU = (s[:, None] <= s[None, :]).astype(np.float32)
    ustr = (s[:, None] > s[None, :]).astype(np.float32)
    common["cst"] = np.ascontiguousarray(np.concatenate([ident, triU, ustr, np.ones((128, 128), np.float32)], axis=1))
    rb = np.asarray(inp["rel_bias"], np.float32)
    bg = np.zeros((128, 2, 16, 128), np.float32); mk = np.zeros((128, 2, 16, 128), np.float32)
    for kt in range(2):
        rel = (s[None, :] + 128) - (s[:, None] + 128 * kt)
        valid = (rel >= 0) & (rel < 128)
        bidx = rel_bucket_np(rel)
        g = rb[bidx]
        bg[:, kt] = np.transpose(g, (0, 2, 1))
        mk[:, kt] = np.broadcast_to(valid[:, None, :], (128, 16, 128))
    common["biasg"] = np.ascontiguousarray(bg.reshape(128, -1)); common["maskg"] = np.ascontiguousarray(mk.reshape(128, -1))
    in_maps = []
    for c in range(NCORE):
        S = c * TOK
        lo = S - 128 - NPRE * 128
        xp = np.zeros((NPRE * 128, 1024), np.float32)
        if S - 128 > 0:
            src_lo = max(lo, 0)
            xp[src_lo - lo:] = x[src_lo:S - 128]
        xm = np.zeros((NM2 * 128, 1024), np.float32)
        lo2 = S - 256
        src_lo = max(lo2, 0)
        xm[src_lo - lo2:] = x[src_lo:S + TOK]
        fl = np.zeros((128, NPRE + NM1), np.float32)
        for i in range(NPRE):
            fl[:, i] = 1.0 if lo + i * 128 >= 0 else 0.0
        fl[:, NPRE] = 1.0 if c > 0 else 0.0
        fl[:, NPRE + 1:] = 1.0
        m = dict(common); m["xpre"] = xp; m["xmain"] = xm; m["pflag"] = fl
        in_maps.append(m)
    if _dbg is not None:
        return in_maps
    if _NC is None:
        _NC = build()
    res = run_bass_kernel_spmd(_NC, in_maps, core_ids=list(range(NCORE)))
    o = np.concatenate([res.results[c]["out"] for c in range(NCORE)], axis=0)
    return o[None].astype(np.float32)
```

```python
import numpy as np
import concourse.bass as bass
import concourse.mybir as mybir

ENGS = ["pe", "act", "dve", "pool", "sp"]
N_DMA_SEM = 16
import os as _os
SAME_ENGINE_SYNC = _os.environ.get("KSES", "1") == "1"


class Buf:
    __slots__ = ("name", "writer", "readers", "dma_readers", "excl")

    def __init__(self, name, excl=False):
        self.name = name
        self.excl = excl
        self.writer = None
        self.readers = {}
        self.dma_readers = []


class Op:
    __slots__ = ("eng", "fn", "idx", "waits", "signal", "is_dma", "dsem", "dtarget", "clock", "sigcount", "uid")


class Prog:
    def __init__(self):
        self.ops = {e: [] for e in ENGS}
        self.clock = {e: {f: 0 for f in ENGS} for e in ENGS}
        self.known_dma = {e: set() for e in ENGS}
        self.dma_sem_count = [0] * N_DMA_SEM
        self.dma_sem_last = [None] * N_DMA_SEM
        self.dma_rr = 0
        self.n_dma = 0
        self.uid = 0
        self.bar = {e: [] for e in ENGS}

    def barrier(self):
        lasts = [self.ops[e][-1] for e in ENGS if self.ops[e] and not self.ops[e][-1].is_dma]
        for e in ENGS:
            pass
        lasts = []
        for e in ENGS:
            for op in reversed(self.ops[e]):
                if not op.is_dma:
                    lasts.append(op)
                    break
        dmas = [op for op in self.dma_sem_last if op is not None]
        for e in ENGS:
            self.bar[e] = lasts + dmas

    def add(self, eng, fn, reads=(), writes=(), dma=False):
        op = Op()
        op.eng = eng
        op.fn = fn
        op.idx = len(self.ops[eng])
        op.waits = []
        op.signal = False
        op.is_dma = dma
        op.uid = self.uid
        self.uid += 1
        def _flat(bs):
            o = []
            for b in bs:
                if isinstance(b, (list, tuple)):
                    o.extend(b)
                else:
                    o.append(b)
            return o
        reads = _flat(reads)
        writes = _flat(writes)
        deps = []
        for b in reads:
            if b.writer is not None:
                deps.append(b.writer)
            if b.excl:
                for e2, r in b.readers.items():
                    if e2 != eng:
                        deps.append(r)
        for b in writes:
            if b.writer is not None:
                deps.append(b.writer)
            deps.extend(b.readers.values())
            deps.extend(b.dma_readers)
        if self.bar[eng]:
            deps.extend(self.bar[eng])
            self.bar[eng] = []
        clk = self.clock[eng]
        seen = set()
        for d in deps:
            if d.uid in seen:
                continue
            seen.add(d.uid)
            if d.is_dma:
                if d.uid not in self.known_dma[eng]:
                    op.waits.append(("dma", d.dsem, d.dtarget))
                    self.known_dma[eng].add(d.uid)
            else:
                if d.eng == eng and (eng == "pe" or not SAME_ENGINE_SYNC):
                    continue
                if clk[d.eng] < d.idx + 1:
                    op.waits.append(("eng", d))
                    d.signal = True
                    for f in ENGS:
                        if d.clock[f] > clk[f]:
                            clk[f] = d.clock[f]
                    if clk[d.eng] < d.idx + 1:
                        clk[d.eng] = d.idx + 1
        if dma and eng == "pool":
            pd = self.__dict__.setdefault("pool_dmas", [])
            if len(pd) >= 4:
                d = pd[-4]
                if d.uid not in self.known_dma[eng]:
                    op.waits.append(("dma", d.dsem, d.dtarget))
                    self.known_dma[eng].add(d.uid)
            pd.append(op)
        if dma:
            half = N_DMA_SEM // 2
            rr = self.__dict__.setdefault("dma_rr2", {"pool": 0, "hw": 0})
            if eng == "pool":
                k = half + rr["pool"]
                rr["pool"] = (rr["pool"] + 1) % (N_DMA_SEM - half)
            else:
                k = rr["hw"]
                rr["hw"] = (rr["hw"] + 1) % half
            prev = self.dma_sem_last[k]
            if prev is not None and prev.uid not in self.known_dma[eng]:
                op.waits.append(("dma", k, prev.dtarget))
                self.known_dma[eng].add(prev.uid)
            self.dma_sem_count[k] += 16
            op.dsem = k
            op.dtarget = self.dma_sem_count[k]
            self.dma_sem_last[k] = op
            self.n_dma += 1
        op.clock = dict(clk)
        if not SAME_ENGINE_SYNC or eng == "pe":
            pass
        self.ops[eng].append(op)
        for b in writes:
            b.writer = op
            b.readers = {}
            b.dma_readers = []
        for b in reads:
            if dma:
                b.dma_readers.append(op)
            else:
                b.readers[eng] = op
        return op

    def emit(self, nc, final_waits=True):
        for e in ENGS:
            c = 0
            for op in self.ops[e]:
                if op.signal:
                    c += 1
                op.sigcount = c
        from contextlib import ExitStack
        with ExitStack() as es:
            esem = {e: es.enter_context(nc.semaphore("s_" + e)) for e in ENGS}
            dsem = [es.enter_context(nc.semaphore("d%d" % i)) for i in range(N_DMA_SEM)]
            block = es.enter_context(nc.Block())
            last_dma = [op for op in self.dma_sem_last if op is not None]

            def run(e, h):
                for op in self.ops[e]:
                    for w in op.waits:
                        if w[0] == "dma":
                            h.wait_ge(dsem[w[1]], w[2])
                        else:
                            h.wait_ge(esem[w[1].eng], w[1].sigcount)
                    ins = op.fn(h)
                    if op.is_dma:
                        ins.then_inc(dsem[op.dsem], 16)
                    elif op.signal:
                        ins.then_inc(esem[e], 1)
                if e == "sp" and final_waits:
                    for k in range(N_DMA_SEM):
                        if self.dma_sem_count[k] > 0:
                            h.wait_ge(dsem[k], self.dma_sem_count[k])

            @block.tensor
            def _(h):
                run("pe", h)

            @block.scalar
            def _(h):
                run("act", h)

            @block.vector
            def _(h):
                run("dve", h)

            @block.gpsimd
            def _(h):
                run("pool", h)

            @block.sync
            def _(h):
                run("sp", h)

from contextlib import ExitStack
import ml_dtypes
from concourse.bass_utils import run_bass_kernel_spmd

F32 = mybir.dt.float32
BF16 = mybir.dt.bfloat16
AF = mybir.ActivationFunctionType
ALU = mybir.AluOpType

NCORE = 8
TOK = 2048
NPRE = 112
NM1 = 17
NM2 = 18
ALPHA = 2.0 ** 0.25
COL_Z, COL_X, COL_B, COL_C, COL_DT, COL_Q, COL_K, COL_V, COL_G = 0, 2048, 4096, 4608, 5120, 5152, 6176, 6304, 6432


class TT:
    def __init__(self, ap, name, excl=False):
        self.ap = ap
        self.b = Buf(name, excl)


class Arena:
    def __init__(self, t, n):
        self.t, self.n, self.off = t, n, 0

    def get(self, name, *fs):
        n = int(np.prod(fs))
        ap = self.t[:, self.off:self.off + n]
        self.off += n
        assert self.off <= self.n, (name, self.off, self.n)
        if len(fs) == 2:
            ap = ap.rearrange("p (a b) -> p a b", a=fs[0])
        elif len(fs) == 3:
            ap = ap.rearrange("p (a b c) -> p a b c", a=fs[0], b=fs[1])
        return TT(ap, name)


def bc(ap2, n):
    return ap2.unsqueeze(2).to_broadcast([ap2.shape[0], ap2.shape[1], n])


def build(stop=4, npre=NPRE, dbg=False):
    nc = bass.Bass("TRN2", target_bir_lowering=False)
    dt_in = lambda n, s: nc.dram_tensor(n, s, F32, kind="ExternalInput").ap()
    xpre = dt_in("xpre", [NPRE * 128, 1024])
    xmain = dt_in("xmain", [NM2 * 128, 1024])
    pflag = dt_in("pflag", [128, NPRE + NM1])
    w_in = dt_in("w_in", [1024, 8480])
    b_gate = dt_in("b_gate", [1, 2048])
    scw = dt_in("scw", [96, 128])
    scb = dt_in("scb", [24, 128])
    dtb = dt_in("dtb", [1, 32])
    alog = dt_in("alog", [1, 32])
    dsk = dt_in("dsk", [1, 32])
    normw = dt_in("normw", [1, 2048])
    sinks = dt_in("sinks", [1, 16])
    w_bs = dt_in("w_bs", [2048, 1024])
    w_ba = dt_in("w_ba", [1024, 1024])
    w_mix = dt_in("w_mix", [1024, 1024])
    ln1g = dt_in("ln1g", [1, 1024]); ln1b = dt_in("ln1b", [1, 1024])
    ln2g = dt_in("ln2g", [1, 1024]); ln2b = dt_in("ln2b", [1, 1024])
    w_up = dt_in("w_up", [1024, 5632])
    fcw = dt_in("fcw", [132, 128])
    fcb = dt_in("fcb", [44, 128])
    w_dn = dt_in("w_dn", [2816, 1024])
    cst = dt_in("cst", [128, 4 * 128])
    biasg = dt_in("biasg", [128, 2 * 16 * 128])
    maskg = dt_in("maskg", [128, 2 * 16 * 128])
    out = nc.dram_tensor("out", [TOK, 1024], F32, kind="ExternalOutput").ap()
    skind = "ExternalOutput" if dbg else "Internal"
    ynd = nc.dram_tensor("ynd", [NM1 * 128, 2048], BF16, kind=skind).ap()
    yad = nc.dram_tensor("yad", [NM1 * 128, 1024], BF16, kind=skind).ap()
    h1d = nc.dram_tensor("h1d", [NM1 * 128, 1024], F32, kind=skind).ap()

    P = Prog()
    es = ExitStack()
    NB, NF = 164 * 512, 40 * 256
    ABt = es.enter_context(nc.sbuf_tensor("AB", [128, NB], BF16))
    AFt = es.enter_context(nc.sbuf_tensor("AF", [128, NF], F32))
    pf = [TT(es.enter_context(nc.psum_tensor("pf%d" % i, [128, 512], F32))[:], "pf%d" % i, True) for i in range(6)]
    for t_ in pf:
        t_.b = [Buf(t_.b.name + "q%d" % q_, True) for q_ in range(4)]
    pb = [TT(es.enter_context(nc.psum_tensor("pb%d" % i, [128, 1024], BF16))[:], "pb%d" % i, True) for i in range(2)]
    AB = Arena(ABt, NB)
    AFa = Arena(AFt, NF)

    def bcast_row(src, n):
        return bass.AP(src.tensor, 0, [[0, 128], [1, n]])

    def dma(eng, o, i, reads, writes):
        P.add(eng, lambda e, o=o, i=i: e.dma_start(out=o, in_=i), reads=reads, writes=writes, dma=True)

    cf = AFa.get("cf", 4, 128)
    dma("sp", cf.ap, cst.rearrange("p (a b) -> p a b", a=4), [], [cf.b])
    identf, triU, Ustr, onesf = cf.ap[:, 0, :], cf.ap[:, 1, :], cf.ap[:, 2, :], cf.ap[:, 3, :]
    cb16 = AB.get("cb16", 4, 128)
    dma("pool", cb16.ap, cst.rearrange("p (a b) -> p a b", a=4), [], [cb16.b])
    identb, maskb = cb16.ap[:, 0, :], cb16.ap[:, 1, :]
    flg = AFa.get("flg", NPRE + NM1)
    dma("sp", flg.ap, pflag, [], [flg.b])
    smallp = AFa.get("smallp", 6, 32)
    dma("sp", smallp.ap[:, 0, :], bcast_row(dtb, 32), [], [smallp.b])
    dma("sp", smallp.ap[:, 1, :], bcast_row(alog, 32), [], [smallp.b])
    dma("sp", smallp.ap[:, 2, :], bcast_row(dsk, 32), [], [smallp.b])
    dma("sp", smallp.ap[:, 3, 0:16], bcast_row(sinks, 16), [], [smallp.b])
    P.add("act", lambda e: e.activation(out=smallp.ap[:, 1, :], in_=smallp.ap[:, 1, :], func=AF.Exp), reads=[smallp.b], writes=[smallp.b])
    P.add("dve", lambda e: e.tensor_scalar(out=smallp.ap[:, 1, :], in0=smallp.ap[:, 1, :], scalar1=-1.0, scalar2=None, op0=ALU.mult), reads=[smallp.b], writes=[smallp.b])
    P.add("act", lambda e: e.activation(out=smallp.ap[:, 3, 0:16], in_=smallp.ap[:, 3, 0:16], func=AF.Exp), reads=[smallp.b], writes=[smallp.b])
    dtb_bc, a_bc, D_bc, esink = smallp.ap[:, 0, :], smallp.ap[:, 1, :], smallp.ap[:, 2, :], smallp.ap[:, 3, 0:16]
    onecol = onesf[:, 0:1]
    rawh = AB.get("rawh", 24, 4)
    markB, markF = AB.off, AFa.off
    H = AFa.get("H", 2048)

    def load_w(dst, src, ncols, nk=8):
        nk = src.shape[0] // 128
        for c0 in range(0, ncols, 512):
            c1 = min(c0 + 512, ncols)
            for k in range(nk):
                dma("pool", dst.ap[:, k, c0:c1], src[k * 128:(k + 1) * 128, c0:c1], [], [dst.b])

    def load_xT(xrow_ap, xb, xT):
        dma("pool", xb.ap, xrow_ap, [], [xb.b])
        for k in range(8):
            P.add("pe", lambda e, k=k: e.transpose(pb[0].ap[:, k * 128:(k + 1) * 128], xb.ap[:, k * 128:(k + 1) * 128], identb), reads=[xb.b, cb16.b], writes=[pb[0].b])
        P.add("act", lambda e: e.copy(out=xT.ap.rearrange("p a b -> p (a b)"), in_=pb[0].ap), reads=[pb[0].b], writes=[xT.b])

    def chain(bank, out_ap, pairs, reads, wb=None):
        n = len(pairs)
        wb = [bank.b] if wb is None else wb
        for i, (l, r) in enumerate(pairs):
            P.add("pe", lambda e, l=l, r=r, i=i: e.matmul(out_ap, l, r, start=(i == 0), stop=(i == n - 1)), reads=reads, writes=wb)

    def per_channel(dst, src_dram, rows):
        tmp = AFa.get("pc_tmp", 128)
        dma("sp", tmp.ap[0:rows, :], src_dram, [], [tmp.b])
        P.add("pe", lambda e: e.transpose(pf[5].ap[:, 0:rows], tmp.ap[0:rows, :], identf[0:rows, 0:rows]), reads=[tmp.b, cf.b], writes=[pf[5].b])
        P.add("dve", lambda e: e.tensor_copy(out=dst, in_=pf[5].ap[:, 0:rows]), reads=[pf[5].b], writes=[])

    import os
    SKIP1 = int(os.environ.get('KSKIP1', '0'))
    Wx = AB.get("Wx", 8, 3072)
    wxb = [Buf("wxb%d" % i) for i in range(6)]
    for c0 in range(0, 3072, 512):
        for k in range(8):
            dma("pool", Wx.ap[:, k, c0:c0 + 512], w_in[k * 128:(k + 1) * 128, COL_X + c0:COL_X + c0 + 512], [], [wxb[c0 // 512]])
    Wx.b = wxb
    Wdt = AB.get("Wdt", 8, 32); load_w(Wdt, w_in[:, COL_DT:COL_DT + 32], 32)
    cwt = AFa.get("cwt", 96); cbt = AFa.get("cbt", 24)
    per_channel(cwt.ap, scw, 96)
    per_channel(cbt.ap, scb, 24)
    cwt.b.writer = P.ops["dve"][-2]; cbt.b.writer = P.ops["dve"][-1]
    diag = AB.get("diag", 24, 4, 128)
    for j in range(24):
        for tp in range(4):
            P.add("dve", lambda e, j=j, tp=tp: e.tensor_scalar(out=diag.ap[:, j, tp, :], in0=identf, scalar1=cwt.ap[:, tp * 24 + j:tp * 24 + j + 1], scalar2=None, op0=ALU.mult), reads=[cf.b, cwt.b], writes=[diag.b])
    P.add("dve", lambda e: e.memset(H.ap, 0.0), writes=[H.b])
    mark1B, mark1F = AB.off, AFa.off
    xbA = [AB.get("xbA%d" % i, 1024) for i in range(2)]
    xT4 = [AB.get("xT4_%d" % i, 8, 512) for i in range(2)]
    raw4 = AB.get("raw4", 24, 516); xc4 = AB.get("xc4", 24, 512)
    rawb = [Buf("rawb%d" % j) for j in range(24)]; xcb = [Buf("xcb%d" % j) for j in range(24)]
    xtk = [AB.get("xtk%d" % i, 2560) for i in range(2)]
    xdtsA4 = [AB.get("xdtsA%d" % i, 512) for i in range(8)]
    P.add("pool", lambda e: e.memset(raw4.ap, 0.0), writes=rawb)

    def load_group(gi, buf):
        for q in range(4):
            c = gi * 4 + q
            xb_ = xbA[q % 2]
            dma("pool", xb_.ap, xpre[c * 128:(c + 1) * 128, :], [], [xb_.b])
            for k in range(8):
                P.add("pe", lambda e, k=k, xb_=xb_: e.transpose(pb[0].ap[:, k * 128:(k + 1) * 128], xb_.ap[:, k * 128:(k + 1) * 128], identb), reads=[xb_.b, cb16.b], writes=[pb[0].b])
            P.add("act", lambda e, q=q, buf=buf: e.copy(out=xT4[buf].ap[:, :, q * 128:(q + 1) * 128], in_=pb[0].ap.rearrange("p (a b) -> p a b", a=8)), reads=[pb[0].b], writes=[xT4[buf].b])

    def proj_in(gi, buf, last, j0, j1):
        ntile = 24 if last else 20
        for j in range(j0, min(j1, ntile)):
            bank = pf[j % 2]
            chain(bank, bank.ap, [(Wx.ap[:, k, j * 128:(j + 1) * 128], xT4[buf].ap[:, k, :]) for k in range(8)], [wxb[j // 4], xT4[buf].b])
            P.add("act", lambda e, j=j, bank=bank: e.copy(out=raw4.ap[:, j, 3:515], in_=bank.ap), reads=[bank.b], writes=[rawb[j]])

    def proj_conv(gi, last):
        ntile = 24 if last else 20
        for j in range(ntile):
            bank = (pf[2], pf[3], pf[5], pf[4])[j % 4]
            chain(bank, bank.ap, [(diag.ap[:, j, tp, :], raw4.ap[:, j, tp:tp + 512]) for tp in range(4)], [diag.b, rawb[j]])
            P.add("act", lambda e, j=j, bank=bank: e.activation(out=xc4.ap[:, j, :], in_=bank.ap, func=AF.Silu, bias=cbt.ap[:, j:j + 1], scale=1.0), reads=[bank.b, cbt.b], writes=[xcb[j]])
        P.add("pool", lambda e: e.tensor_copy(out=raw4.ap[:, :, 0:3], in_=raw4.ap[:, :, 512:515]), reads=rawb, writes=rawb)

    smGs = [AFa.get("smG%d" % i, 8, 128) for i in range(2)]

    def group_chunks(gi, buf, hooks):
        c0 = gi * 4
        smG = smGs[gi % 2]
        S = lambda i: smG.ap[:, i, :]
        S3 = lambda i: smG.ap[:, i, :].rearrange("p (q h) -> p q h", q=4)
        b4 = lambda ap: ap.unsqueeze(1).to_broadcast([128, 4, 32])
        v3 = lambda ap: ap.rearrange("p (a b) -> p a b", a=8)
        for q in range(4):
            chain(pf[4], pf[4].ap[:, q * 32:(q + 1) * 32], [(xT4[buf].ap[:, k, q * 128:(q + 1) * 128], Wdt.ap[:, k, :]) for k in range(8)], [xT4[buf].b, Wdt.b])
        P.add("dve", lambda e: e.tensor_tensor(out=S3(0), in0=pf[4].ap[:, 0:128].rearrange("p (q h) -> p q h", q=4), in1=b4(dtb_bc), op=ALU.add), reads=[pf[4].b, smallp.b], writes=[smG.b])
        P.add("act", lambda e: e.activation(out=S(0), in_=S(0), func=AF.Exp), reads=[smG.b], writes=[smG.b])
        P.add("act", lambda e: e.activation(out=S(1), in_=S(0), func=AF.Ln, bias=onecol, scale=1.0), reads=[smG.b, cf.b], writes=[smG.b])
        P.add("dve", lambda e: e.tensor_tensor(out=S3(1), in0=S3(1), in1=bc(flg.ap[:, c0:c0 + 4], 32), op=ALU.mult), reads=[smG.b, flg.b], writes=[smG.b])
        P.add("dve", lambda e: e.tensor_tensor(out=S3(2), in0=S3(1), in1=b4(a_bc), op=ALU.mult), reads=[smG.b, smallp.b], writes=[smG.b])

        def tr(q):
            xt = xtk[q % 2]
            for bt in range(3):
                nt = 8 if bt < 2 else 4
                pbk = pb[bt % 2]
                for i in range(nt):
                    jj = bt * 8 + i
                    P.add("pe", lambda e, i=i, jj=jj, q=q, pbk=pbk: e.transpose(pbk.ap[:, i * 128:(i + 1) * 128], xc4.ap[:, jj, q * 128:(q + 1) * 128], identb), reads=[xcb[jj], cb16.b], writes=[pbk.b])
                if bt == 1:
                    P.add("act", lambda e, bt=bt, nt=nt, xt=xt, pbk=pbk: e.copy(out=xt.ap[:, bt * 1024:bt * 1024 + nt * 128], in_=pbk.ap[:, 0:nt * 128]), reads=[pbk.b], writes=[xt.b])
                else:
                    P.add("dve", lambda e, bt=bt, nt=nt, xt=xt, pbk=pbk: e.tensor_copy(out=xt.ap[:, bt * 1024:bt * 1024 + nt * 128], in_=pbk.ap[:, 0:nt * 128]), reads=[pbk.b], writes=[xt.b])

        SBK = [pf[2], pf[3], pf[5], pf[4]]

        def state(q):
            xt = xtk[q % 2]
            for g in range(4):
                xg = xt.ap[:, g * 512:(g + 1) * 512].rearrange("p (a b) -> p a b", a=8)
                o = q * 32 + 8 * g
                xd = xdtsA4[(q % 2) * 4 + g]
                P.add("dve", lambda e, xg=xg, o=o, xd=xd: e.tensor_tensor(out=v3(xd.ap), in0=xg, in1=bc(smG.ap[:, 6, o:o + 8], 64), op=ALU.mult), reads=[xt.b, smG.b], writes=[xd.b])
                P.add("pe", lambda e, g=g, xt=xt, xd=xd, q=q: e.matmul(SBK[g].ap, xt.ap[:, 2048 + g * 128:2048 + (g + 1) * 128], xd.ap, start=(q == 0), stop=(q == 3)), reads=[xt.b, xd.b], writes=[SBK[g].b])
            if q == 3:
                P.add("dve", lambda e: e.tensor_tensor(out=H.ap.rearrange("p (a b) -> p a b", a=32), in0=H.ap.rearrange("p (a b) -> p a b", a=32), in1=bc(smG.ap[:, 7, 0:32], 64), op=ALU.mult), reads=[H.b, smG.b], writes=[H.b])
                for g in range(4):
                    Hg = H.ap[:, g * 512:(g + 1) * 512]
                    P.add("dve", lambda e, Hg=Hg, g=g: e.tensor_tensor(out=Hg, in0=Hg, in1=SBK[g].ap, op=ALU.add), reads=[H.b, SBK[g].b], writes=[H.b])

        tr(0); tr(1)
        hooks[0]()
        P.add("pe", lambda e: e.matmul(pf[4].ap[:, 128:256], triU, S(2), start=True, stop=True), reads=[cf.b, smG.b], writes=[pf[4].b])
        P.add("pe", lambda e: e.matmul(pf[4].ap[:, 256:384], onesf, S(2), start=True, stop=True), reads=[cf.b, smG.b], writes=[pf[4].b])
        P.add("dve", lambda e: e.tensor_copy(out=smG.ap[:, 3:5, :], in_=pf[4].ap[:, 128:384].rearrange("p (a b) -> p a b", a=2)), reads=[pf[4].b], writes=[smG.b])
        P.add("dve", lambda e: e.memset(S3(5)[:, 3, :], 0.0), writes=[smG.b])
        P.add("dve", lambda e: e.tensor_copy(out=S3(5)[:, 2, :], in_=S3(4)[:, 3, :]), reads=[smG.b], writes=[smG.b])
        P.add("dve", lambda e: e.tensor_tensor(out=S3(5)[:, 1, :], in0=S3(5)[:, 2, :], in1=S3(4)[:, 2, :], op=ALU.add), reads=[smG.b], writes=[smG.b])
        P.add("dve", lambda e: e.tensor_tensor(out=S3(5)[:, 0, :], in0=S3(5)[:, 1, :], in1=S3(4)[:, 1, :], op=ALU.add), reads=[smG.b], writes=[smG.b])
        P.add("dve", lambda e: e.tensor_tensor(out=S3(7)[:, 0, :], in0=S3(5)[:, 0, :], in1=S3(4)[:, 0, :], op=ALU.add), reads=[smG.b], writes=[smG.b])
        P.add("dve", lambda e: e.tensor_tensor(out=S(6), in0=S(4), in1=S(3), op=ALU.subtract), reads=[smG.b], writes=[smG.b])
        P.add("dve", lambda e: e.tensor_tensor(out=S(6), in0=S(6), in1=S(5), op=ALU.add), reads=[smG.b], writes=[smG.b])
        P.add("act", lambda e: e.activation(out=S(6), in_=S(6), func=AF.Exp), reads=[smG.b], writes=[smG.b])
        P.add("act", lambda e: e.activation(out=S3(7)[:, 0, :], in_=S3(7)[:, 0, :], func=AF.Exp), reads=[smG.b], writes=[smG.b])
        P.add("dve", lambda e: e.tensor_tensor(out=S(6), in0=S(6), in1=S(1), op=ALU.mult), reads=[smG.b], writes=[smG.b])
        state(0); state(1)
        hooks[1]()
        tr(2); tr(3)
        hooks[2]()
        state(2); state(3)
        hooks[3]()

    NG = NPRE // 4
    g0 = NG - (npre + 3) // 4
    if not SKIP1 and g0 < NG:
        load_group(g0, g0 % 2)
        proj_in(g0, g0 % 2, g0 == NG - 1, 0, 24)
        for gi in range(g0, NG):
            proj_conv(gi, gi == NG - 1)
            if gi + 1 < NG:
                load_group(gi + 1, (gi + 1) % 2)
                hk = [lambda a=a, gi=gi: proj_in(gi + 1, (gi + 1) % 2, gi + 1 == NG - 1, a, a + 6) for a in (0, 6, 12, 18)]
            else:
                hk = [lambda: None] * 4
            group_chunks(gi, gi % 2, hk)
        P.add("pool", lambda e: e.tensor_copy(out=rawh.ap[:, :, 0:3], in_=raw4.ap[:, :, 0:3]), reads=rawb, writes=[rawh.b])
    else:
        P.add("pool", lambda e: e.memset(rawh.ap, 0.0), writes=[rawh.b])

    P.barrier()
    AB.off, AFa.off = mark1B, mark1F
    Wz = AB.get("Wz", 8, 2048); load_w(Wz, w_in[:, COL_Z:COL_Z + 2048], 2048)
    nwb = AFa.get("nwb", 2048)
    dma("sp", nwb.ap, bcast_row(normw, 2048), [], [nwb.b])
    xbs = [AB.get("xb_%d" % i, 1024) for i in range(2)]; xTs = [AB.get("xT_%d" % i, 8, 128) for i in range(2)]
    raws = [AB.get("raw_%d" % i, 24, 132) for i in range(2)]; xcs = [AB.get("xc_%d" % i, 24, 128) for i in range(2)]
    xtok = AB.get("xtok", 2560)
    xdt = AB.get("xdt", 512); xdts = AB.get("xdts", 512)
    Hb = AB.get("Hb", 2048); cbm = AB.get("cbm", 4, 128)
    LT = AB.get("LT", 4, 128); MT = AB.get("MT", 8, 128); yn = AB.get("yn", 2048)
    szb = AB.get("szb", 4, 512)
    adtU = [AFa.get("adtU%d" % i, 128) for i in range(8)]
    sm = AFa.get("sm", 12, 32)
    yacc = AFa.get("yacc", 512); ytmp = AFa.get("ytmp", 512); sz = AFa.get("sz", 512)
    ssq = AFa.get("ssq", 4)
    P.add("pool", lambda e: e.memset(raws[0].ap, 0.0), writes=[raws[0].b])
    P.add("pool", lambda e: e.memset(raws[1].ap, 0.0), writes=[raws[1].b])
    P.add("pool", lambda e: e.tensor_copy(out=raws[0].ap[:, :, 0:3], in_=rawh.ap[:, :, 0:3]), reads=[rawh.b], writes=[raws[0].b])

    CUT = int(os.environ.get('KCUT', '99')); NCH1 = int(os.environ.get('KNCH', str(NM1)))
    def front_pieces(ci, xrow, xb, xT, raw, xc, raw_next):
        def inproj(g):
            bank = pf[g % 2]
            for jj in range(4):
                j = 4 * g + jj
                chain(bank, bank.ap[:, jj * 128:(jj + 1) * 128], [(Wx.ap[:, k, j * 128:(j + 1) * 128], xT.ap[:, k, :]) for k in range(8)], [Wx.b, xT.b])
            P.add("act", lambda e, g=g, bank=bank: e.copy(out=raw.ap[:, 4 * g:4 * g + 4, 3:131], in_=bank.ap.rearrange("p (a b) -> p a b", a=4)), reads=[bank.b], writes=[raw.b])

        def conv(g):
            for jj in range(4):
                j = 4 * g + jj
                bank = pf[2 + j % 2]
                chain(bank, bank.ap[:, 0:128], [(diag.ap[:, j, tp, :], raw.ap[:, j, tp:tp + 128]) for tp in range(4)], [diag.b, raw.b])
                P.add("act", lambda e, j=j, bank=bank: e.activation(out=xc.ap[:, j, :], in_=bank.ap[:, 0:128], func=AF.Silu, bias=cbt.ap[:, j:j + 1], scale=1.0), reads=[bank.b, cbt.b], writes=[xc.b])

        def p0():
            load_xT(xrow, xb, xT); inproj(0); inproj(1)

        def p1():
            inproj(2); inproj(3)

        def p2():
            inproj(4); inproj(5)
            P.add("pool", lambda e: e.tensor_copy(out=raw_next.ap[:, :, 0:3], in_=raw.ap[:, :, 128:131]), reads=[raw.b], writes=[raw_next.b])

        def p3():
            conv(0); conv(1); conv(2); conv(3); conv(4); conv(5)
        return [p0, p1, p2, p3]

    def ssd_back(fcol, main, ci, xT, xc, hooks):
        for bt in range(3):
            nt = 8 if bt < 2 else 4
            for i in range(nt):
                j = bt * 8 + i
                P.add("pe", lambda e, i=i, j=j: e.transpose(pb[1].ap[:, i * 128:(i + 1) * 128], xc.ap[:, j, :], identb), reads=[xc.b, cb16.b], writes=[pb[1].b])
            P.add("dve", lambda e, bt=bt, nt=nt: e.tensor_copy(out=xtok.ap[:, bt * 1024:bt * 1024 + nt * 128], in_=pb[1].ap[:, 0:nt * 128]), reads=[pb[1].b], writes=[xtok.b])
        chain(pf[4], pf[4].ap[:, 0:32], [(xT.ap[:, k, :], Wdt.ap[:, k, :]) for k in range(8)], [xT.b, Wdt.b])
        S = lambda i: sm.ap[:, i, :]
        P.add("dve", lambda e: e.tensor_tensor(out=S(0), in0=pf[4].ap[:, 0:32], in1=dtb_bc, op=ALU.add), reads=[pf[4].b, smallp.b], writes=[sm.b])
        P.add("act", lambda e: e.activation(out=S(0), in_=S(0), func=AF.Exp), reads=[sm.b], writes=[sm.b])
        P.add("act", lambda e: e.activation(out=S(1), in_=S(0), func=AF.Ln, bias=onecol, scale=1.0), reads=[sm.b, cf.b], writes=[sm.b])
        P.add("dve", lambda e: e.tensor_scalar(out=S(1), in0=S(1), scalar1=fcol, scalar2=None, op0=ALU.mult), reads=[sm.b, flg.b], writes=[sm.b])
        P.add("dve", lambda e: e.tensor_tensor(out=S(2), in0=S(1), in1=a_bc, op=ALU.mult), reads=[sm.b, smallp.b], writes=[sm.b])
        P.add("pe", lambda e: e.matmul(pf[4].ap[:, 32:64], triU, S(2), start=True, stop=True), reads=[cf.b, sm.b], writes=[pf[4].b])
        P.add("pe", lambda e: e.matmul(pf[4].ap[:, 64:96], onesf, S(2), start=True, stop=True), reads=[cf.b, sm.b], writes=[pf[4].b])
        P.add("dve", lambda e: e.tensor_copy(out=sm.ap[:, 3:5, :], in_=pf[4].ap[:, 32:96].rearrange("p (a b) -> p a b", a=2)), reads=[pf[4].b], writes=[sm.b])
        if main:
            for g in range(4):
                P.add("pe", lambda e, g=g: e.matmul(pf[5].ap[:, g * 128:(g + 1) * 128], xc.ap[:, 16 + g, :], xc.ap[:, 20 + g, :], start=True, stop=True), reads=[xc.b], writes=[pf[5].b])
            P.add("dve", lambda e: e.tensor_tensor(out=cbm.ap, in0=pf[5].ap.rearrange("p (a b) -> p a b", a=4), in1=maskb.unsqueeze(1).to_broadcast([128, 4, 128]), op=ALU.mult), reads=[pf[5].b, cb16.b], writes=[cbm.b])
            P.add("pool", lambda e: e.tensor_copy(out=Hb.ap, in_=H.ap), reads=[H.b], writes=[Hb.b])
            P.add("act", lambda e: e.activation(out=S(5), in_=S(3), func=AF.Exp), reads=[sm.b], writes=[sm.b])
            for g in range(4):
                zb = pf[5 - g % 2]
                chain(zb, zb.ap, [(xT.ap[:, k, :], Wz.ap[:, k, g * 512:(g + 1) * 512]) for k in range(8)], [xT.b, Wz.b])
                P.add("act", lambda e, g=g, zb=zb: e.activation(out=szb.ap[:, g, :], in_=zb.ap, func=AF.Silu), reads=[zb.b], writes=[szb.b])
            for g in range(4):
                xg = xtok.ap[:, g * 512:(g + 1) * 512].rearrange("p (a b) -> p a b", a=8)
                P.add("dve", lambda e, g=g, xg=xg: e.tensor_tensor(out=xdt.ap.rearrange("p (a b) -> p a b", a=8), in0=xg, in1=bc(sm.ap[:, 1, 8 * g:8 * g + 8], 64), op=ALU.mult), reads=[xtok.b, sm.b], writes=[xdt.b])
                for hh in range(2):
                    bank = pf[hh]
                    for h4 in range(4):
                        h = 8 * g + 4 * hh + h4
                        au = adtU[h % 8]
                        P.add("dve", lambda e, h=h, au=au: e.tensor_scalar(out=au.ap, in0=Ustr, scalar1=sm.ap[:, 2, h:h + 1], scalar2=None, op0=ALU.mult), reads=[cf.b, sm.b], writes=[au.b])
                        P.add("pe", lambda e, h4=h4, au=au, bank=bank: e.matmul(bank.ap[:, h4 * 128:(h4 + 1) * 128], au.ap, triU, start=True, stop=True), reads=[au.b, cf.b], writes=[bank.b])
                    P.add("act", lambda e, bank=bank: e.activation(out=LT.ap.rearrange("p a b -> p (a b)"), in_=bank.ap, func=AF.Exp), reads=[bank.b], writes=[LT.b])
                    P.add("dve", lambda e, g=g, hh=hh: e.tensor_tensor(out=MT.ap[:, 4 * hh:4 * hh + 4, :], in0=LT.ap, in1=cbm.ap[:, g, :].unsqueeze(1).to_broadcast([128, 4, 128]), op=ALU.mult), reads=[LT.b, cbm.b], writes=[MT.b])
                for h8 in range(8):
                    P.add("pe", lambda e, h8=h8: e.matmul(pf[2].ap[:, h8 * 64:(h8 + 1) * 64], MT.ap[:, h8, :], xdt.ap[:, h8 * 64:(h8 + 1) * 64], start=True, stop=True), reads=[MT.b, xdt.b], writes=[pf[2].b])
                P.add("pe", lambda e, g=g: e.matmul(pf[3].ap, xc.ap[:, 20 + g, :], Hb.ap[:, g * 512:(g + 1) * 512], start=True, stop=True), reads=[xc.b, Hb.b], writes=[pf[3].b])
                v3 = lambda ap: ap.rearrange("p (a b) -> p a b", a=8)
                P.add("dve", lambda e, g=g: e.tensor_tensor(out=v3(yacc.ap), in0=v3(pf[3].ap), in1=bc(sm.ap[:, 5, 8 * g:8 * g + 8], 64), op=ALU.mult), reads=[pf[3].b, sm.b], writes=[yacc.b])
                P.add("dve", lambda e: e.tensor_tensor(out=yacc.ap, in0=yacc.ap, in1=pf[2].ap, op=ALU.add), reads=[pf[2].b, yacc.b], writes=[yacc.b])
                P.add("dve", lambda e, g=g, xg=xg: e.tensor_tensor(out=v3(ytmp.ap), in0=xg, in1=bc(D_bc[:, 8 * g:8 * g + 8], 64), op=ALU.mult), reads=[xtok.b, smallp.b], writes=[ytmp.b])
                P.add("dve", lambda e: e.tensor_tensor(out=yacc.ap, in0=yacc.ap, in1=ytmp.ap, op=ALU.add), reads=[ytmp.b, yacc.b], writes=[yacc.b])
                P.add("dve", lambda e, g=g: e.tensor_tensor(out=yacc.ap, in0=yacc.ap, in1=szb.ap[:, g, :], op=ALU.mult), reads=[szb.b, yacc.b], writes=[yacc.b])
                P.add("dve", lambda e: e.tensor_tensor(out=ytmp.ap, in0=yacc.ap, in1=yacc.ap, op=ALU.mult), reads=[yacc.b], writes=[ytmp.b])
                P.add("dve", lambda e, g=g: e.reduce_sum(out=ssq.ap[:, g:g + 1], in_=ytmp.ap, axis=mybir.AxisListType.X), reads=[ytmp.b], writes=[ssq.b])
                P.add("dve", lambda e, g=g: e.tensor_scalar(out=ssq.ap[:, g:g + 1], in0=ssq.ap[:, g:g + 1], scalar1=1.0 / 512, scalar2=1e-5, op0=ALU.mult, op1=ALU.add), reads=[ssq.b], writes=[ssq.b])
                P.add("act", lambda e, g=g: e.activation(out=ssq.ap[:, g:g + 1], in_=ssq.ap[:, g:g + 1], func=AF.Ln), reads=[ssq.b], writes=[ssq.b])
                P.add("act", lambda e, g=g: e.activation(out=ssq.ap[:, g:g + 1], in_=ssq.ap[:, g:g + 1], func=AF.Exp, scale=-0.5), reads=[ssq.b], writes=[ssq.b])
                P.add("dve", lambda e, g=g: e.scalar_tensor_tensor(out=yn.ap[:, g * 512:(g + 1) * 512], in0=yacc.ap, scalar=ssq.ap[:, g:g + 1], in1=nwb.ap[:, g * 512:(g + 1) * 512], op0=ALU.mult, op1=ALU.mult), reads=[yacc.b, ssq.b, nwb.b], writes=[yn.b])
                hooks[g]()
            dma("sp", ynd[ci * 128:(ci + 1) * 128, :], yn.ap, [yn.b], [])
        P.add("dve", lambda e: e.tensor_tensor(out=S(6), in0=S(4), in1=S(3), op=ALU.subtract), reads=[sm.b], writes=[sm.b])
        P.add("act", lambda e: e.activation(out=S(6), in_=S(6), func=AF.Exp), reads=[sm.b], writes=[sm.b])
        P.add("act", lambda e: e.activation(out=S(7), in_=S(4), func=AF.Exp), reads=[sm.b], writes=[sm.b])
        P.add("dve", lambda e: e.tensor_tensor(out=S(6), in0=S(6), in1=S(1), op=ALU.mult), reads=[sm.b], writes=[sm.b])
        for g in range(4):
            xg = xtok.ap[:, g * 512:(g + 1) * 512].rearrange("p (a b) -> p a b", a=8)
            v3 = lambda ap: ap.rearrange("p (a b) -> p a b", a=8)
            P.add("dve", lambda e, g=g, xg=xg: e.tensor_tensor(out=v3(xdts.ap), in0=xg, in1=bc(sm.ap[:, 6, 8 * g:8 * g + 8], 64), op=ALU.mult), reads=[xtok.b, sm.b], writes=[xdts.b])
            bank = pf[2 + g % 2]
            P.add("pe", lambda e, g=g, bank=bank: e.matmul(bank.ap, xtok.ap[:, 2048 + g * 128:2048 + (g + 1) * 128], xdts.ap, start=True, stop=True), reads=[xtok.b, xdts.b], writes=[bank.b])
            Hg = H.ap[:, g * 512:(g + 1) * 512]
            P.add("dve", lambda e, g=g, Hg=Hg: e.tensor_tensor(out=v3(Hg), in0=v3(Hg), in1=bc(sm.ap[:, 7, 8 * g:8 * g + 8], 64), op=ALU.mult), reads=[H.b, sm.b], writes=[H.b])
            P.add("dve", lambda e, Hg=Hg, bank=bank: e.tensor_tensor(out=Hg, in0=Hg, in1=bank.ap, op=ALU.add), reads=[H.b, bank.b], writes=[H.b])

    def fp(ci):
        b = ci % 2
        return front_pieces(ci, xmain[(ci + 1) * 128:(ci + 2) * 128, :], xbs[b], xTs[b], raws[b], xcs[b], raws[1 - b])
    nch = 0 if SKIP1 else NCH1
    if nch:
        for p_ in fp(0):
            p_()
    for ci in range(nch):
        nxt = fp(ci + 1) if ci + 1 < nch else [lambda: None] * 4
        ssd_back(flg.ap[:, NPRE + ci:NPRE + ci + 1], True, ci, xTs[ci % 2], xcs[ci % 2], nxt)

    if stop < 2:
        P.emit(nc); es.close(); return nc
    P.barrier()
    AB.off, AFa.off = markB, markF
    Wq = AB.get("Wq", 8, 1024); load_w(Wq, w_in[:, COL_Q:COL_Q + 1024], 1024)
    Wk2 = AB.get("Wk2", 8, 128); load_w(Wk2, w_in[:, COL_K:COL_K + 128], 128)
    Wv = AB.get("Wv", 8, 128); load_w(Wv, w_in[:, COL_V:COL_V + 128], 128)
    EB = AB.get("EB", 2, 16, 128)
    AB3 = Arena(ABt, NB); AB3.off = NB - 49152
    Wg = AB3.get("Wg", 8, 2048); Wbs = AB3.get("Wbs", 16, 1024); Wba = AB3.get("Wba", 8, 1024); Wmx = AB3.get("Wmx", 8, 1024)
    w3_list = []
    for dst_, src_, nc_ in ((Wg, w_in[:, COL_G:COL_G + 2048], 2048), (Wbs, w_bs, 1024), (Wba, w_ba, 1024), (Wmx, w_mix, 1024)):
        for c0 in range(0, nc_, 512):
            for k in range(src_.shape[0] // 128):
                w3_list.append((dst_, dst_.ap[:, k, c0:c0 + 512], src_[k * 128:(k + 1) * 128, c0:c0 + 512]))
    w3_pos = [0]

    def w3_issue(n):
        for _ in range(n):
            if w3_pos[0] < len(w3_list):
                d_, o_, i_ = w3_list[w3_pos[0]]
                dma("pool", o_, i_, [], [d_.b])
                w3_pos[0] += 1
    ebf = AFa.get("ebf", 2048); mkf = AFa.get("mkf", 2048)
    for kt in range(2):
        dma("sp", ebf.ap, biasg[:, kt * 2048:(kt + 1) * 2048], [], [ebf.b])
        dma("sp", mkf.ap, maskg[:, kt * 2048:(kt + 1) * 2048], [], [mkf.b])
        P.add("act", lambda e: e.activation(out=ebf.ap, in_=ebf.ap, func=AF.Exp), reads=[ebf.b], writes=[ebf.b])
        P.add("dve", lambda e, kt=kt: e.tensor_tensor(out=EB.ap[:, kt, :, :].rearrange("p a b -> p (a b)"), in0=ebf.ap, in1=mkf.ap, op=ALU.mult), reads=[ebf.b, mkf.b], writes=[EB.b])
    xb2s = [AB.get("xb2_%d" % i, 1024) for i in range(2)]; xT = AB.get("xT2", 8, 128)
    kT = [AB.get("kT%d" % i, 2, 128) for i in range(2)]
    vx = [AB.get("vx%d" % i, 2, 65) for i in range(2)]
    for i in range(2):
        P.add("pool", lambda e, i=i: e.memset(vx[i].ap, 1.0), writes=[vx[i].b])
    qT = AB.get("qT", 16, 128); et = AB.get("et", 4, 128)
    PT = [AB.get("PT%d" % i, 4, 128) for i in range(2)]
    ya = AB.get("ya", 1024)
    den = AFa.get("den", 4)
    flag0 = flg.ap[:, NPRE:NPRE + 1]
    CUT2 = int(os.environ.get('KCUT2', '99')); NCH2 = int(os.environ.get('KNCH2', str(NM2)))
    for ci in range(NCH2):
        sl = ci % 2
        load_xT(xmain[ci * 128:(ci + 1) * 128, :], xb2s[ci % 2], xT)
        w3_issue(6)
        for kv in range(2):
            chain(pf[0], pf[0].ap[0:64, kv * 128:(kv + 1) * 128], [(Wk2.ap[:, k, kv * 64:(kv + 1) * 64], xT.ap[:, k, :]) for k in range(8)], [Wk2.b, xT.b])
        P.add("act", lambda e, sl=sl: e.copy(out=kT[sl].ap[0:64, :, :], in_=pf[0].ap[0:64, 0:256].rearrange("p (a b) -> p a b", a=2)), reads=[pf[0].b], writes=[kT[sl].b])
        chain(pf[1], pf[1].ap[:, 0:128], [(xT.ap[:, k, :], Wv.ap[:, k, :]) for k in range(8)], [xT.b, Wv.b])
        P.add("dve", lambda e, sl=sl: e.tensor_copy(out=vx[sl].ap[:, :, 0:64], in_=pf[1].ap[:, 0:128].rearrange("p (a b) -> p a b", a=2)), reads=[pf[1].b], writes=[vx[sl].b])
        if ci == 0 or CUT2 <= 1:
            continue
        for q4 in range(4):
            bank = pf[2 + q4 % 2]
            for tt in range(4):
                j = q4 * 4 + tt
                chain(bank, bank.ap[0:64, tt * 128:(tt + 1) * 128], [(Wq.ap[:, k, j * 64:(j + 1) * 64], xT.ap[:, k, :]) for k in range(8)], [Wq.b, xT.b])
            P.add("act", lambda e, q4=q4, bank=bank: e.copy(out=qT.ap[0:64, q4 * 4:q4 * 4 + 4, :], in_=bank.ap[0:64, :].rearrange("p (a b) -> p a b", a=4)), reads=[bank.b], writes=[qT.b])
        for kvh in range(2):
            if CUT2 <= 2: break
            for hb in range(2):
                j0 = kvh * 8 + hb * 4
                for kt in range(2):
                    slk = (ci + 1 + kt) % 2
                    bank = pf[4 + kt]
                    for i in range(4):
                        j = j0 + i
                        base = (j % 2) * 64 * int(os.environ.get("KB64", "1"))
                        P.add("pe", lambda e, i=i, j=j, base=base, slk=slk, bank=bank, kvh=kvh: e.matmul(bank.ap[:, i * 128:(i + 1) * 128], kT[slk].ap[0:64, kvh, :], qT.ap[0:64, j, :], start=True, stop=True), reads=[kT[slk].b, qT.b], writes=[bank.b])
                    P.add("act", lambda e, bank=bank: e.activation(out=et.ap.rearrange("p a b -> p (a b)"), in_=bank.ap, func=AF.Exp, scale=0.125), reads=[bank.b], writes=[et.b])
                    if ci == 2 and kt == 0:
                        P.add("dve", lambda e, kt=kt, j0=j0: e.scalar_tensor_tensor(out=PT[kt].ap, in0=et.ap, scalar=flag0, in1=EB.ap[:, kt, j0:j0 + 4, :], op0=ALU.mult, op1=ALU.mult), reads=[et.b, EB.b, flg.b], writes=[PT[kt].b])
                    else:
                        P.add("dve", lambda e, kt=kt, j0=j0: e.tensor_tensor(out=PT[kt].ap, in0=et.ap, in1=EB.ap[:, kt, j0:j0 + 4, :], op=ALU.mult), reads=[et.b, EB.b], writes=[PT[kt].b])
                if CUT2 <= 3: continue
                bank = pf[hb]
                for i in range(4):
                    for kt in range(2):
                        slk = (ci + 1 + kt) % 2
                        P.add("pe", lambda e, i=i, kt=kt, slk=slk, bank=bank, kvh=kvh: e.matmul(bank.ap[:, i * 65:(i + 1) * 65], PT[kt].ap[:, i, :], vx[slk].ap[:, kvh, :], start=(kt == 0), stop=(kt == 1)), reads=[PT[kt].b, vx[slk].b], writes=[bank.b])
                if CUT2 <= 4: continue
                pv = bank.ap[:, 0:260].rearrange("p (a b) -> p a b", a=4)
                P.add("dve", lambda e, pv=pv, j0=j0: e.tensor_tensor(out=den.ap, in0=pv[:, :, 64], in1=esink[:, j0:j0 + 4], op=ALU.add), reads=[bank.b, smallp.b], writes=[den.b])
                P.add("dve", lambda e: e.reciprocal(out=den.ap, in_=den.ap), reads=[den.b], writes=[den.b])
                P.add("dve", lambda e, pv=pv, j0=j0: e.tensor_tensor(out=ya.ap[:, j0 * 64:(j0 + 4) * 64].rearrange("p (a b) -> p a b", a=4), in0=pv[:, :, 0:64], in1=bc(den.ap, 64), op=ALU.mult), reads=[bank.b, den.b], writes=[ya.b])
        dma("sp", yad[(ci - 1) * 128:ci * 128, :], ya.ap, [ya.b], [])

    if stop < 3:
        P.emit(nc); es.close(); return nc
    w3_issue(len(w3_list))
    P.barrier()
    AB.off, AFa.off = markB, markF
    bgb = AFa.get("bgb", 2048); dma("sp", bgb.ap, bcast_row(b_gate, 2048), [], [bgb.b])
    lng = AFa.get("lng", 2, 1024)
    dma("sp", lng.ap[:, 0, :], bcast_row(ln1g, 1024), [], [lng.b]); dma("sp", lng.ap[:, 1, :], bcast_row(ln1b, 1024), [], [lng.b])
    xb = AB.get("xb3", 1024); xT = AB.get("xT3", 8, 128)
    ynb = AB.get("ynb", 2048); yab = AB.get("yab", 1024)
    ynT = AB.get("ynT", 16, 128); yaT = AB.get("yaT", 8, 128)
    mg = AB.get("mg", 1024); mT = AB.get("mT", 8, 128)
    xf = AFa.get("xf", 1024); gt = AB.get("gt", 2048); m1 = AFa.get("m1", 512); r = AFa.get("r", 1024)
    st = AFa.get("st", 2, 6); mv = AFa.get("mv", 2)

    def transp(src, dst, ntl):
        for bt in range(ntl // 8):
            for i in range(8):
                j = bt * 8 + i
                P.add("pe", lambda e, i=i, j=j: e.transpose(pb[1].ap[:, i * 128:(i + 1) * 128], src.ap[:, j * 128:(j + 1) * 128], identb), reads=[src.b, cb16.b], writes=[pb[1].b])
            P.add("act", lambda e, bt=bt: e.copy(out=dst.ap[:, bt * 8:bt * 8 + 8, :].rearrange("p a b -> p (a b)"), in_=pb[1].ap), reads=[pb[1].b], writes=[dst.b])

    def layer_norm(r, g_ap, b_ap, gb, st, mv):
        for i in range(2):
            P.add("dve", lambda e, i=i: e.bn_stats(out=st.ap[:, i, :], in_=r.ap[:, i * 512:(i + 1) * 512]), reads=[r.b], writes=[st.b])
        P.add("dve", lambda e: e.bn_aggr(out=mv.ap, in_=st.ap.rearrange("p a b -> p (a b)")), reads=[st.b], writes=[mv.b])
        P.add("dve", lambda e: e.tensor_scalar(out=mv.ap[:, 1:2], in0=mv.ap[:, 1:2], scalar1=1e-5, scalar2=None, op0=ALU.add), reads=[mv.b], writes=[mv.b])
        P.add("act", lambda e: e.activation(out=mv.ap[:, 1:2], in_=mv.ap[:, 1:2], func=AF.Sqrt), reads=[mv.b], writes=[mv.b])
        P.add("dve", lambda e: e.reciprocal(out=mv.ap[:, 1:2], in_=mv.ap[:, 1:2]), reads=[mv.b], writes=[mv.b])
        P.add("dve", lambda e: e.tensor_scalar(out=r.ap, in0=r.ap, scalar1=mv.ap[:, 0:1], scalar2=mv.ap[:, 1:2], op0=ALU.subtract, op1=ALU.mult), reads=[r.b, mv.b], writes=[r.b])
        P.add("dve", lambda e: e.tensor_tensor(out=r.ap, in0=r.ap, in1=g_ap, op=ALU.mult), reads=[r.b, gb], writes=[r.b])
        P.add("dve", lambda e: e.tensor_tensor(out=r.ap, in0=r.ap, in1=b_ap, op=ALU.add), reads=[r.b, gb], writes=[r.b])

    def p3_loads(ci, xb, xf, ynb, yab):
        xrow = xmain[(ci + 1) * 128:(ci + 2) * 128, :]
        dma("pool", xb.ap, xrow, [], [xb.b])
        dma("sp", xf.ap, xrow, [], [xf.b])
        dma("sp", ynb.ap, ynd[ci * 128:(ci + 1) * 128, :], [], [ynb.b])
        dma("sp", yab.ap, yad[ci * 128:(ci + 1) * 128, :], [], [yab.b])

    def p3_chunk(ci, xb, xf, ynb, yab, r):
        for k in range(8):
            P.add("pe", lambda e, k=k: e.transpose(pb[0].ap[:, k * 128:(k + 1) * 128], xb.ap[:, k * 128:(k + 1) * 128], identb), reads=[xb.b, cb16.b], writes=[pb[0].b])
        P.add("act", lambda e: e.copy(out=xT.ap.rearrange("p a b -> p (a b)"), in_=pb[0].ap), reads=[pb[0].b], writes=[xT.b])
        transp(ynb, ynT, 16)
        transp(yab, yaT, 8)
        for s4 in range(4):
            bank = pf[s4 % 2]
            chain(bank, bank.ap, [(xT.ap[:, k, :], Wg.ap[:, k, s4 * 512:(s4 + 1) * 512]) for k in range(8)], [xT.b, Wg.b])
            P.add("dve", lambda e, s4=s4, bank=bank: e.tensor_tensor(out=gt.ap[:, s4 * 512:(s4 + 1) * 512], in0=bank.ap, in1=bgb.ap[:, s4 * 512:(s4 + 1) * 512], op=ALU.add), reads=[bank.b, bgb.b], writes=[gt.b])
        P.add("act", lambda e: e.activation(out=gt.ap, in_=gt.ap, func=AF.Sigmoid), reads=[gt.b], writes=[gt.b])
        for hf in range(2):
            chain(pf[2], pf[2].ap, [(ynT.ap[:, i, :], Wbs.ap[:, i, hf * 512:(hf + 1) * 512]) for i in range(16)], [ynT.b, Wbs.b])
            chain(pf[3], pf[3].ap, [(yaT.ap[:, i, :], Wba.ap[:, i, hf * 512:(hf + 1) * 512]) for i in range(8)], [yaT.b, Wba.b])
            P.add("dve", lambda e, hf=hf: e.tensor_tensor(out=m1.ap, in0=pf[2].ap, in1=gt.ap[:, hf * 512:(hf + 1) * 512], op=ALU.mult), reads=[pf[2].b, gt.b], writes=[m1.b])
            P.add("dve", lambda e, hf=hf: e.tensor_tensor(out=r.ap[:, hf * 512:(hf + 1) * 512], in0=pf[3].ap, in1=gt.ap[:, 1024 + hf * 512:1024 + (hf + 1) * 512], op=ALU.mult), reads=[pf[3].b, gt.b], writes=[r.b])
            P.add("dve", lambda e, hf=hf: e.tensor_tensor(out=mg.ap[:, hf * 512:(hf + 1) * 512], in0=m1.ap, in1=r.ap[:, hf * 512:(hf + 1) * 512], op=ALU.add), reads=[m1.b, r.b], writes=[mg.b])
        transp(mg, mT, 8)
        for hf in range(2):
            bank = pf[4 + hf]
            chain(bank, bank.ap, [(mT.ap[:, i, :], Wmx.ap[:, i, hf * 512:(hf + 1) * 512]) for i in range(8)], [mT.b, Wmx.b])
            P.add("dve", lambda e, hf=hf, bank=bank: e.scalar_tensor_tensor(out=r.ap[:, hf * 512:(hf + 1) * 512], in0=xf.ap[:, hf * 512:(hf + 1) * 512], scalar=ALPHA, in1=bank.ap, op0=ALU.mult, op1=ALU.add), reads=[xf.b, bank.b], writes=[r.b])
        layer_norm(r, lng.ap[:, 0, :], lng.ap[:, 1, :], lng.b, st, mv)
        if ci == 0:
            P.add("dve", lambda e: e.tensor_scalar(out=r.ap, in0=r.ap, scalar1=flag0, scalar2=None, op0=ALU.mult), reads=[r.b, flg.b], writes=[r.b])
        dma("sp", h1d[ci * 128:(ci + 1) * 128, :], r.ap, [r.b], [])


    xb3 = [xb, AB.get("xb3b", 1024)]; xf3 = [xf, AFa.get("xf3b", 1024)]
    ABp = Arena(ABt, NB); ABp.off = markB
    Wup_pre = ABp.get("Wup_pre", 8, 5632)
    wp_list = [(k, c0) for k in (3, 4, 5) for c0 in range(0, 5632, 512)]
    wp_pos = [0]

    def wp_issue(n):
        for _ in range(n):
            if wp_pos[0] < len(wp_list):
                k, c0 = wp_list[wp_pos[0]]
                dma("pool", Wup_pre.ap[:, k, c0:c0 + 512], w_up[k * 128:(k + 1) * 128, c0:c0 + 512], [], [Wup_pre.b])
                wp_pos[0] += 1
    ynb3 = [ynb, AB.get("ynb3b", 2048)]; yab3 = [yab, AB.get("yab3b", 1024)]; r3 = [r, AFa.get("r3b", 1024)]
    assert AB.off <= markB + 3 * 5632, AB.off
    p3_loads(0, xb3[0], xf3[0], ynb3[0], yab3[0])
    for ci in range(NM1):
        if ci + 1 < NM1:
            b_ = (ci + 1) % 2
            p3_loads(ci + 1, xb3[b_], xf3[b_], ynb3[b_], yab3[b_])
        b_ = ci % 2
        wp_issue(2)
        p3_chunk(ci, xb3[b_], xf3[b_], ynb3[b_], yab3[b_], r3[b_])

    if stop < 4:
        P.emit(nc); es.close(); return nc
    wp_issue(len(wp_list))
    P.barrier()
    AB.off, AFa.off = markB, markF
    Wup = AB.get("Wup", 8, 5632)
    wub = [Buf("wub%d" % i) for i in range(11)]
    for bi in (0, 5, 6, 1, 7, 2, 8, 3, 9, 4, 10):
        c0 = bi * 512
        for k in (0, 1, 2, 6, 7):
            dma("pool", Wup.ap[:, k, c0:c0 + 512], w_up[k * 128:(k + 1) * 128, c0:c0 + 512], [], [wub[bi]])
    Wdn = AB.get("Wdn", 22, 1024)
    wdb = [Buf("wdb%d" % i) for i in range(22)]
    for k in range(22):
        for c0 in (0, 512):
            dma("pool", Wdn.ap[:, k, c0:c0 + 512], w_dn[k * 128:(k + 1) * 128, c0:c0 + 512], [], [wdb[k]])
    fw = AFa.get("fw", 132); fb = AFa.get("fb", 44)
    per_channel(fw.ap[:, 0:88], fcw[0:88, :], 88); fw.b.writer = P.ops["dve"][-1]
    per_channel(fw.ap[:, 88:132], fcw[88:132, :], 44); fw.b.writer = P.ops["dve"][-1]
    per_channel(fb.ap, fcb, 44); fb.b.writer = P.ops["dve"][-1]
    lng2 = AFa.get("lng2", 2, 1024)
    dma("sp", lng2.ap[:, 0, :], bcast_row(ln2g, 1024), [], [lng2.b]); dma("sp", lng2.ap[:, 1, :], bcast_row(ln2b, 1024), [], [lng2.b])
    SC = 256
    hAs = [AB.get("hA%d" % i, 1024) for i in range(2)]; hBs = [AB.get("hB%d" % i, 1024) for i in range(2)]; h1T = AB.get("h1T", 8, SC + 2)
    aT = AB.get("aT", 22, SC)
    aTb = [Buf("aTb%d" % i) for i in range(22)]
    cvs = [AFa.get("cvs%d" % i, SC) for i in range(4)]
    t0s = [AFa.get("t0s%d" % i, SC) for i in range(2)]
    sgs = [AFa.get("sg%d" % i, SC) for i in range(2)]
    hrs = [AFa.get("hr%d" % i, 1024) for i in range(2)]; r4s = [AFa.get("r4_%d" % i, 1024) for i in range(2)]
    st4 = AFa.get("st4", 2, 6); mv4 = AFa.get("mv4", 2)
    NT = SC // 128
    tcount = 0
    for sc in range(TOK // SC):
        r0 = 128 + sc * SC
        for i in range(NT):
            hA = hAs[i % 2]
            dma("pool", hA.ap, h1d[r0 - 2 + i * 128:r0 + 126 + i * 128, :], [], [hA.b])
            for k in range(8):
                P.add("pe", lambda e, k=k, hA=hA: e.transpose(pb[0].ap[:, k * 128:(k + 1) * 128], hA.ap[:, k * 128:(k + 1) * 128], identb), reads=[hA.b, cb16.b], writes=[pb[0].b])
            P.add("act", lambda e, i=i: e.copy(out=h1T.ap[:, :, i * 128:(i + 1) * 128], in_=pb[0].ap.rearrange("p (a b) -> p a b", a=8)), reads=[pb[0].b], writes=[h1T.b])
        hB = hBs[sc % 2]
        dma("pool", hB.ap[0:2, :], h1d[r0 + SC - 2:r0 + SC, :], [], [hB.b])
        for k in range(8):
            P.add("pe", lambda e, k=k, hB=hB: e.transpose(pb[1].ap[:, k * 2:(k + 1) * 2], hB.ap[0:2, k * 128:(k + 1) * 128], identb[0:2, 0:2]), reads=[hB.b, cb16.b], writes=[pb[1].b])
        P.add("act", lambda e: e.copy(out=h1T.ap[:, :, SC:SC + 2], in_=pb[1].ap[:, 0:16].rearrange("p (a b) -> p a b", a=8)), reads=[pb[1].b], writes=[h1T.b])
        def down(jp):
            for tt in range(NT):
                for hf in range(2):
                    bank = pf[2 + tt * 2 + hf]
                    P.add("pe", lambda e, jp=jp, tt=tt, hf=hf, bank=bank: e.matmul(bank.ap, aT.ap[:, jp, tt * 128:(tt + 1) * 128], Wdn.ap[:, jp, hf * 512:(hf + 1) * 512], start=(jp == 0), stop=(jp == 21)), reads=[aTb[jp], wdb[jp]], writes=[bank.b])

        for jp in range(22):
            for gv in range(2):
                j = jp + 22 * gv
                X = pf[tcount % 2]; t0 = t0s[tcount % 2]
                cv = cvs[(jp % 2) * 2 + gv]
                tcount += 1
                chain(X, X.ap[:, 0:SC + 2], [(Wup.ap[:, k, j * 128:(j + 1) * 128], h1T.ap[:, k, 0:SC + 2]) for k in range(8)], [wub[j // 4], h1T.b])
                w0, w1, w2, bb = fw.ap[:, j:j + 1], fw.ap[:, 44 + j:45 + j], fw.ap[:, 88 + j:89 + j], fb.ap[:, j:j + 1]
                P.add("act", lambda e, X=X, t0=t0, w2=w2, bb=bb: e.activation(out=t0.ap, in_=X.ap[:, 2:SC + 2], func=AF.Identity, bias=bb, scale=w2), reads=[X.b, fw.b, fb.b], writes=[t0.b])
                P.add("dve", lambda e, X=X, t0=t0, w1=w1: e.scalar_tensor_tensor(out=t0.ap, in0=X.ap[:, 1:SC + 1], scalar=w1, in1=t0.ap, op0=ALU.mult, op1=ALU.add), reads=[X.b, fw.b, t0.b], writes=[t0.b])
                P.add("dve", lambda e, X=X, t0=t0, w0=w0, cv=cv: e.scalar_tensor_tensor(out=cv.ap, in0=X.ap[:, 0:SC], scalar=w0, in1=t0.ap, op0=ALU.mult, op1=ALU.add), reads=[X.b, fw.b, t0.b], writes=[cv.b])
            cg, cvv = cvs[(jp % 2) * 2], cvs[(jp % 2) * 2 + 1]
            sgp = sgs[jp % 2]
            P.add("act", lambda e, cg=cg, sgp=sgp: e.activation(out=sgp.ap, in_=cg.ap, func=AF.Silu), reads=[cg.b], writes=[sgp.b])
            P.add("dve", lambda e, jp=jp, cvv=cvv, sgp=sgp: e.tensor_tensor(out=aT.ap[:, jp, :], in0=sgp.ap, in1=cvv.ap, op=ALU.mult), reads=[sgp.b, cvv.b], writes=[aTb[jp]])
            if jp >= 1:
                down(jp - 1)
        down(21)
        for tt in range(NT):
            rr = r0 + tt * 128
            hr = hrs[tt % 2]; r4 = r4s[tt % 2]
            dma("sp", hr.ap, h1d[rr:rr + 128, :], [], [hr.b])
            for hf in range(2):
                bank = pf[2 + tt * 2 + hf]
                P.add("dve", lambda e, hf=hf, bank=bank, hr=hr, r4=r4: e.scalar_tensor_tensor(out=r4.ap[:, hf * 512:(hf + 1) * 512], in0=hr.ap[:, hf * 512:(hf + 1) * 512], scalar=ALPHA, in1=bank.ap, op0=ALU.mult, op1=ALU.add), reads=[hr.b, bank.b], writes=[r4.b])
            layer_norm(r4, lng2.ap[:, 0, :], lng2.ap[:, 1, :], lng2.b, st4, mv4)
            dma("sp", out[rr - 128:rr, :], r4.ap, [r4.b], [])

    P.emit(nc)
    es.close()
    return nc


def rel_bucket_np(rel):
    n = np.maximum(rel, 0)
    nf = np.maximum(n, 1).astype(np.float32)
    large = 16 + (np.log(nf / np.float32(16)) / np.float32(np.log(128 / 16)) * np.float32(16)).astype(np.int32)
    large = np.minimum(large, 31)
    return np.where(n < 16, n, large)


_NC = None


def kernel(_dbg=None, **inp):
    global _NC
    x = np.asarray(inp["x"], np.float32)[0]
    f = lambda k: np.ascontiguousarray(np.asarray(inp[k], np.float32)[0])
    common = {
        "w_in": f("w_in"), "b_gate": f("b_gate")[None], "dtb": f("ssm_dt_bias")[None], "alog": f("ssm_a_log")[None],
        "dsk": f("ssm_d")[None], "normw": f("ssm_norm_w")[None], "sinks": f("attn_sinks")[None],
        "w_bs": f("w_branch_ssm"), "w_ba": f("w_branch_attn"), "w_mix": f("w_mix_out"),
        "ln1g": f("ln1_g")[None], "ln1b": f("ln1_b")[None], "ln2g": f("ln2_g")[None], "ln2b": f("ln2_b")[None],
        "w_up": f("w_up"), "w_dn": f("w_down"),
    }
    scw = f("ssm_conv_w")
    common["scw"] = np.ascontiguousarray(scw.reshape(4 * 24, 128))
    common["scb"] = np.ascontiguousarray(f("ssm_conv_b").reshape(24, 128))
    common["fcw"] = np.ascontiguousarray(f("ffn_conv_w").reshape(3 * 44, 128))
    common["fcb"] = np.ascontiguousarray(f("ffn_conv_b").reshape(44, 128))
    s = np.arange(128)
    ident = np.eye(128, dtype=np.float32)
    triU = (s[:, None] <= s[None, :]).astype(np.float32)
    ustr = (s[:, None] > s[None, :]).astype(np.float32)
    common["cst"] = np.ascontiguousarray(np.concatenate([ident, triU, ustr, np.ones((128, 128), np.float32)], axis=1))
    rb = np.asarray(inp["rel_bias"], np.float32)
    bg = np.zeros((128, 2, 16, 128), np.float32); mk = np.zeros((128, 2, 16, 128), np.float32)
    for kt in range(2):
        rel = (s[None, :] + 128) - (s[:, None] + 128 * kt)
        valid = (rel >= 0) & (rel < 128)
        bidx = rel_bucket_np(rel)
        g = rb[bidx]
        bg[:, kt] = np.transpose(g, (0, 2, 1))
        mk[:, kt] = np.broadcast_to(valid[:, None, :], (128, 16, 128))
    common["biasg"] = np.ascontiguousarray(bg.reshape(128, -1)); common["maskg"] = np.ascontiguousarray(mk.reshape(128, -1))
    in_maps = []
    for c in range(NCORE):
        S = c * TOK
        lo = S - 128 - NPRE * 128
        xp = np.zeros((NPRE * 128, 1024), np.float32)
        if S - 128 > 0:
            src_lo = max(lo, 0)
            xp[src_lo - lo:] = x[src_lo:S - 128]
        xm = np.zeros((NM2 * 128, 1024), np.float32)
        lo2 = S - 256
        src_lo = max(lo2, 0)
        xm[src_lo - lo2:] = x[src_lo:S + TOK]
        fl = np.zeros((128, NPRE + NM1), np.float32)
        for i in range(NPRE):
            fl[:, i] = 1.0 if lo + i * 128 >= 0 else 0.0
        fl[:, NPRE] = 1.0 if c > 0 else 0.0
        fl[:, NPRE + 1:] = 1.0
        m = dict(common); m["xpre"] = xp; m["xmain"] = xm; m["pflag"] = fl
        in_maps.append(m)
    if _dbg is not None:
        return in_maps
    if _NC is None:
        _NC = build()
    res = run_bass_kernel_spmd(_NC, in_maps, core_ids=list(range(NCORE)))
    o = np.concatenate([res.results[c]["out"] for c in range(NCORE)], axis=0)
    return o[None].astype(np.float32)
```
